# Optimizing a Trainium2 kernel written in Bass

```python
import math
import jax
import jax.numpy as jnp
from jax import lax
import numpy as np

D_MODEL = 1024
BATCH = 4
SEQ = 4096
DEPTH = 2

N_MEM = 256
BRANCH_W = 256
N_BRANCH = 4
Q_BLOCK = 128
ROPE_THETA = 500000.0
ROPE_FRAC = 4
LN_EPS = 1e-5
ALPHA = (2 * DEPTH) ** 0.25
BETA = (8 * DEPTH) ** -0.25
RET_HEADS = 4
RET_DK = 32
RET_DV = 64
RET_CHUNK = 128
RET_THETA = 10000.0
DSA_HEADS = 4
DSA_DH = 64
IDX_HEADS = 8
IDX_DH = 32
DSA_TOPK = 256
NSA_HEADS = 4
NSA_DH = 64
CMP_LEN = 32
CMP_STRIDE = 16
SEL_LEN = 64
SEL_TOPN = 16
WINDOW = 512
SSM_HEADS = 4
SSM_HEADDIM = 64
SSM_GROUPS = 2
SSM_STATE = 128
SSM_CONV = 4
SSM_CHUNK = 128
SSM_INNER = SSM_HEADS * SSM_HEADDIM
SSM_CONV_DIM = SSM_INNER + 2 * SSM_GROUPS * SSM_STATE
D_FF = 2816
X_HEADS = 4
X_DH = D_MODEL // X_HEADS

IN_SPLITS = (
    RET_HEADS * RET_DK, RET_HEADS * RET_DK, RET_HEADS * RET_DV, RET_HEADS * RET_DV,
    DSA_HEADS * DSA_DH, DSA_DH, DSA_DH, IDX_HEADS * IDX_DH, IDX_DH, IDX_HEADS,
    NSA_HEADS * NSA_DH, NSA_DH, NSA_DH, NSA_DH, NSA_DH, NSA_DH, NSA_DH, NSA_HEADS * 3,
    SSM_INNER, SSM_CONV_DIM, SSM_HEADS,
    N_BRANCH * D_MODEL,
)
D_IN = sum(IN_SPLITS)

kernel_name = 'hybrid_gated_retention_dsa_nsa_ssd_trunk'


def layer_norm(x, g, b):
    xf = x.astype(jnp.float32)
    mu = jnp.mean(xf, -1, keepdims=True)
    var = jnp.mean(jnp.square(xf - mu), -1, keepdims=True)
    return ((xf - mu) * lax.rsqrt(var + LN_EPS)).astype(x.dtype) * g + b


def rope(x, pos, rot_dim, theta):
    half = rot_dim // 2
    inv = jnp.power(jnp.float32(theta), -2.0 * jnp.arange(half, dtype=jnp.float32) / rot_dim)
    ang = pos.astype(jnp.float32)[:, None] * inv[None, :]
    cos = jnp.cos(ang)[None, :, None, :].astype(x.dtype)
    sin = jnp.sin(ang)[None, :, None, :].astype(x.dtype)
    x1, x2, rest = x[..., :half], x[..., half:rot_dim], x[..., rot_dim:]
    return jnp.concatenate([x1 * cos - x2 * sin, x2 * cos + x1 * sin, rest], axis=-1)


def masked_softmax(s, mask):
    s = jnp.where(mask, s.astype(jnp.float32), -jnp.inf)
    m = jnp.max(s, -1, keepdims=True)
    m = jnp.where(jnp.isfinite(m), m, 0.0)
    e = jnp.exp(s - m)
    return e / jnp.maximum(jnp.sum(e, -1, keepdims=True), 1e-30)


def to_blocks(a, nb):
    return jnp.moveaxis(a.reshape(a.shape[0], nb, Q_BLOCK, *a.shape[2:]), 1, 0)


def from_blocks(a):
    a = jnp.moveaxis(a, 0, 1)
    return a.reshape(a.shape[0], a.shape[1] * a.shape[2], *a.shape[3:])


def swiglu(x, w_gu, w_down):
    g, u = jnp.split(x @ w_gu, 2, axis=-1)
    return (jax.nn.silu(g) * u) @ w_down


def retention(q, k, v, g, pos):
    B, S, H, dk = q.shape
    dv = v.shape[-1]
    C = RET_CHUNK
    n = S // C
    dt = v.dtype
    q = rope(q, pos, dk, RET_THETA)
    k = rope(k, pos, dk, RET_THETA) * (dk ** -0.5)
    log_gamma = jnp.log1p(-jnp.exp2(-5.0 - jnp.arange(H, dtype=jnp.float32)))
    c = jnp.arange(C, dtype=jnp.float32)
    rel = c[:, None] - c[None, :]
    intra_decay = jnp.where(rel >= 0, jnp.exp(jnp.maximum(rel, 0.0)[None] * log_gamma[:, None, None]), 0.0).astype(dt)
    zeta = jnp.exp((C - 1 - c)[None] * log_gamma[:, None]).astype(dt)
    xi = jnp.exp((c + 1)[None] * log_gamma[:, None]).astype(dt)
    chunk_decay = jnp.exp(C * log_gamma).astype(dt)
    qc = q.reshape(B, n, C, H, dk)
    kc = k.reshape(B, n, C, H, dk)
    vc = v.reshape(B, n, C, H, dv)
    scores = jnp.einsum('bnihd,bnjhd->bnhij', qc, kc) * intra_decay
    intra = jnp.einsum('bnhij,bnjhe->bnihe', scores, vc)
    contrib = jnp.einsum('bnjhd,bnjhe,hj->nbhde', kc, vc, zeta)

    def step(state, cin):
        return state * chunk_decay[None, :, None, None] + cin, state

    _, prev = lax.scan(step, jnp.zeros((B, H, dk, dv), dt), contrib)
    cross = jnp.einsum('bnihd,nbhde,hi->bnihe', qc, prev, xi)
    o = (intra + cross).reshape(B, S, H, dv).astype(jnp.float32)
    mu = jnp.mean(o, -1, keepdims=True)
    var = jnp.mean(jnp.square(o - mu), -1, keepdims=True)
    o = ((o - mu) * lax.rsqrt(var + LN_EPS)).astype(dt)
    return jax.nn.silu(g) * o.reshape(B, S, H * dv)


def dsa_attention(q, k, v, iq, ik, iw, pos):
    B, S, H, Dh = q.shape
    n_keep = min(DSA_TOPK, S // 4)
    nb = S // Q_BLOCK
    q = rope(q, pos, Dh // ROPE_FRAC, ROPE_THETA)
    k = rope(k[:, :, None], pos, Dh // ROPE_FRAC, ROPE_THETA)[:, :, 0]
    iq = rope(iq, pos, IDX_DH // ROPE_FRAC, ROPE_THETA)
    ik = rope(ik[:, :, None], pos, IDX_DH // ROPE_FRAC, ROPE_THETA)[:, :, 0]
    iw = iw * (IDX_HEADS ** -0.5) * (IDX_DH ** -0.5)
    kpos = jnp.arange(S)
    gather = jax.vmap(lambda t, i: t[i])

    def block(args):
        qb, iqb, iwb, bi = args
        qpos = bi * Q_BLOCK + jnp.arange(Q_BLOCK)
        rel = jax.nn.relu(jnp.einsum('bqhd,bkd->bqkh', iqb, ik))
        score = jnp.einsum('bqkh,bqh->bqk', rel, iwb).astype(jnp.float32)
        causal = kpos[None, :] <= qpos[:, None]
        score = jnp.where(causal[None], score, -jnp.inf)
        _, idx = lax.top_k(score, n_keep)
        ks = gather(k, idx)
        vs = gather(v, idx)
        s = jnp.einsum('bqhd,bqnd->bhqn', qb, ks) * (Dh ** -0.5)
        valid = (idx <= qpos[None, :, None])[:, None]
        p = masked_softmax(s, valid).astype(vs.dtype)
        return jnp.einsum('bhqn,bqnd->bqhd', p, vs)

    out = lax.map(block, (to_blocks(q, nb), to_blocks(iq, nb), to_blocks(iw, nb), jnp.arange(nb)))
    return from_blocks(out).reshape(B, S, H * Dh)


def nsa_attention(q, kc, vc, ks, vs, kw, vw, gates, cmp_w1, cmp_w2, cmp_pos, pos):
    B, S, H, Dh = q.shape
    rot = Dh // ROPE_FRAC
    scale = Dh ** -0.5
    q = rope(q, pos, rot, ROPE_THETA)
    r1 = lambda t: rope(t[:, :, None], pos, rot, ROPE_THETA)[:, :, 0]
    kc, ks, kw = r1(kc), r1(ks), r1(kw)
    n_cmp = (S - CMP_LEN) // CMP_STRIDE + 1
    starts = jnp.arange(n_cmp) * CMP_STRIDE
    tok = starts[:, None] + jnp.arange(CMP_LEN)[None]

    def compress(t, i):
        blk = t[:, tok] + cmp_pos[i]
        h = jax.nn.gelu(blk.reshape(B, n_cmp, CMP_LEN * Dh) @ cmp_w1[i])
        return h @ cmp_w2[i]

    k_cmp = compress(kc, 0)
    v_cmp = compress(vc, 1)
    cmp_visible = (starts + CMP_LEN - 1)[None, :] <= pos[:, None]
    p_cmp = masked_softmax(jnp.einsum('bshd,bcd->bhsc', q, k_cmp) * scale, cmp_visible[None, None])
    o_cmp = jnp.einsum('bhsc,bcd->bshd', p_cmp.astype(v_cmp.dtype), v_cmp)
    n_blk = S // SEL_LEN
    sel_start = jnp.arange(n_blk) * SEL_LEN
    overlap = jnp.maximum(jnp.minimum(starts[:, None] + CMP_LEN, sel_start[None] + SEL_LEN)
                          - jnp.maximum(starts[:, None], sel_start[None]), 0).astype(jnp.float32) / CMP_LEN
    imp = jnp.einsum('bhsc,cj->bsj', p_cmp, overlap)
    blk = jnp.arange(n_blk)[None]
    cur = (pos // SEL_LEN)[:, None]
    forced = (blk == 0) | (blk == cur) | (blk == cur - 1)
    admissible = sel_start[None] <= pos[:, None]
    imp = jnp.where(admissible[None], jnp.where(forced[None], jnp.inf, imp), -jnp.inf)
    n_top = min(SEL_TOPN, n_blk)
    _, sel_idx = lax.top_k(imp, n_top)
    ks_blk = ks.reshape(B, n_blk, SEL_LEN, Dh)
    vs_blk = vs.reshape(B, n_blk, SEL_LEN, Dh)
    kw_pad = jnp.pad(kw, ((0, 0), (WINDOW, 0), (0, 0)))
    vw_pad = jnp.pad(vw, ((0, 0), (WINDOW, 0), (0, 0)))
    nb = S // Q_BLOCK
    gather = jax.vmap(lambda t, i: t[i])
    sel_off = jnp.arange(SEL_LEN)
    win_off = jnp.arange(WINDOW + Q_BLOCK)

    def block(args):
        qb, idxb, bi = args
        qpos = bi * Q_BLOCK + jnp.arange(Q_BLOCK)
        ksel = gather(ks_blk, idxb).reshape(B, Q_BLOCK, n_top * SEL_LEN, Dh)
        vsel = gather(vs_blk, idxb).reshape(B, Q_BLOCK, n_top * SEL_LEN, Dh)
        kpos_sel = (idxb[..., None] * SEL_LEN + sel_off).reshape(B, Q_BLOCK, n_top * SEL_LEN)
        p = masked_softmax(jnp.einsum('bqhd,bqkd->bhqk', qb, ksel) * scale,
                           (kpos_sel <= qpos[None, :, None])[:, None])
        o_sel = jnp.einsum('bhqk,bqkd->bqhd', p.astype(vsel.dtype), vsel)
        kwin = lax.dynamic_slice_in_dim(kw_pad, bi * Q_BLOCK, WINDOW + Q_BLOCK, axis=1)
        vwin = lax.dynamic_slice_in_dim(vw_pad, bi * Q_BLOCK, WINDOW + Q_BLOCK, axis=1)
        kpos_w = bi * Q_BLOCK - WINDOW + win_off
        dlt = qpos[:, None] - kpos_w[None]
        wmask = (dlt >= 0) & (dlt < WINDOW) & (kpos_w[None] >= 0)
        p = masked_softmax(jnp.einsum('bqhd,bkd->bhqk', qb, kwin) * scale, wmask[None, None])
        o_win = jnp.einsum('bhqk,bkd->bqhd', p.astype(vwin.dtype), vwin)
        return o_sel, o_win

    o_sel, o_win = lax.map(block, (to_blocks(q, nb), to_blocks(sel_idx, nb), jnp.arange(nb)))
    o_sel, o_win = from_blocks(o_sel), from_blocks(o_win)
    g = jax.nn.sigmoid(gates.reshape(B, S, H, 3))
    o = g[..., 0:1] * o_cmp + g[..., 1:2] * o_sel + g[..., 2:3] * o_win
    return o.reshape(B, S, H * Dh)


def ssd_mixer(z, xbc, dt_raw, conv_w, conv_b, dt_bias, a_log, d_skip, norm_g):
    B, S, _ = xbc.shape
    H, P, G, N = SSM_HEADS, SSM_HEADDIM, SSM_GROUPS, SSM_STATE
    Q = SSM_CHUNK
    nc = S // Q
    xbc = lax.conv_general_dilated(xbc, conv_w[:, None, :], window_strides=(1,),
                                   padding=[(SSM_CONV - 1, 0)],
                                   dimension_numbers=('NWC', 'WIO', 'NWC'),
                                   feature_group_count=SSM_CONV_DIM)
    xbc = jax.nn.silu(xbc + conv_b)
    xs, bm, cm = jnp.split(xbc, [SSM_INNER, SSM_INNER + G * N], axis=-1)
    dtx = xs.dtype
    xs = xs.reshape(B, S, H, P)
    bm = jnp.repeat(bm.reshape(B, S, G, N), H // G, axis=2)
    cm = jnp.repeat(cm.reshape(B, S, G, N), H // G, axis=2)
    dt = jax.nn.softplus((dt_raw + dt_bias).astype(jnp.float32))
    a = -jnp.exp(a_log.astype(jnp.float32))
    adt = dt * a
    X = (xs * dt[..., None].astype(dtx)).reshape(B, nc, Q, H, P)
    Bc = bm.reshape(B, nc, Q, H, N)
    Cc = cm.reshape(B, nc, Q, H, N)
    A = jnp.moveaxis(adt.reshape(B, nc, Q, H), 3, 1)
    A_cs = jnp.cumsum(A, axis=-1)
    seg = A_cs[..., :, None] - A_cs[..., None, :]
    tril = jnp.tril(jnp.ones((Q, Q), dtype=bool))
    Lm = jnp.exp(jnp.where(tril, seg, -jnp.inf)).astype(dtx)
    y_diag = jnp.einsum('bclhn,bcshn,bhcls,bcshp->bclhp', Cc, Bc, Lm, X)
    decay_states = jnp.exp(A_cs[..., -1:] - A_cs).astype(dtx)
    states = jnp.einsum('bclhn,bhcl,bclhp->cbhpn', Bc, decay_states, X)
    chunk_decay = jnp.moveaxis(jnp.exp(A_cs[..., -1]).astype(dtx), 2, 0)

    def step(h, inp):
        st, dec = inp
        return h * dec[..., None, None] + st, h

    _, prev = lax.scan(step, jnp.zeros((B, H, P, N), dtx), (states, chunk_decay))
    y_off = jnp.einsum('bclhn,cbhpn,bhcl->bclhp', Cc, prev, jnp.exp(A_cs).astype(dtx))
    y = (y_diag + y_off).reshape(B, S, H, P) + xs * d_skip[:, None]
    y = y.reshape(B, S, SSM_INNER) * jax.nn.silu(z)
    yg = y.reshape(B, S, G, SSM_INNER // G).astype(jnp.float32)
    yg = yg * lax.rsqrt(jnp.mean(yg * yg, -1, keepdims=True) + LN_EPS)
    return yg.reshape(B, S, SSM_INNER).astype(z.dtype) * norm_g


def token_mixing(x, w_in, cmp_w1, cmp_w2, cmp_pos, conv_w, conv_b, dt_bias, a_log, d_skip,
                 ssm_norm_g, w_branch, w_out):
    B, S, D = x.shape
    pos = jnp.arange(S)
    parts = jnp.split(x @ w_in, np.cumsum(IN_SPLITS)[:-1].tolist(), axis=-1)
    (r_q, r_k, r_v, r_g, d_q, d_k, d_v, i_q, i_k, i_w,
     n_q, n_kc, n_vc, n_ks, n_vs, n_kw, n_vw, n_g, s_z, s_xbc, s_dt, br_g) = parts
    y_ret = retention(r_q.reshape(B, S, RET_HEADS, RET_DK), r_k.reshape(B, S, RET_HEADS, RET_DK),
                      r_v.reshape(B, S, RET_HEADS, RET_DV), r_g, pos)
    y_dsa = dsa_attention(d_q.reshape(B, S, DSA_HEADS, DSA_DH), d_k, d_v,
                          i_q.reshape(B, S, IDX_HEADS, IDX_DH), i_k, i_w, pos)
    y_nsa = nsa_attention(n_q.reshape(B, S, NSA_HEADS, NSA_DH), n_kc, n_vc, n_ks, n_vs, n_kw, n_vw,
                          n_g, cmp_w1, cmp_w2, cmp_pos, pos)
    y_ssd = ssd_mixer(s_z, s_xbc, s_dt, conv_w, conv_b, dt_bias, a_log, d_skip, ssm_norm_g)
    ys = jnp.stack([y_ret, y_dsa, y_nsa, y_ssd], axis=2)
    gates = jax.nn.sigmoid(br_g.reshape(B, S, N_BRANCH, D))
    merged = jnp.einsum('bsnd,bsnd->bsd', gates, jnp.einsum('bsnw,nwd->bsnd', ys, w_branch))
    return merged @ w_out


def memory_cross_attention(x, mem, wq, wkv, wo):
    B, S, D = x.shape
    M = mem.shape[1]
    q = (x @ wq).reshape(B, S, X_HEADS, X_DH)
    kv = (mem @ wkv).reshape(B, M, 2, X_HEADS, X_DH)
    k, v = kv[:, :, 0], kv[:, :, 1]
    s = jnp.einsum('bshd,bmhd->bhsm', q, k) * (X_DH ** -0.5)
    p = jax.nn.softmax(s.astype(jnp.float32), axis=-1).astype(v.dtype)
    return jnp.einsum('bhsm,bmhd->bshd', p, v).reshape(B, S, D) @ wo


def setup_inputs(seed: int = 0) -> dict:
    key = jax.random.key(seed)
    ks = jax.random.split(key, 23)
    L, D = DEPTH, D_MODEL
    nrm = lambda k, shape, scale: jax.random.normal(k, shape, jnp.float32) * scale
    dt0 = jnp.exp(jax.random.uniform(ks[12], (L, SSM_HEADS), jnp.float32, math.log(1e-3), math.log(1e-1)))
    return {
        'x': nrm(ks[0], (BATCH, SEQ, D), 1.0),
        'mem': nrm(ks[1], (BATCH, N_MEM, D), 1.0),
        'ln_g': 1.0 + nrm(ks[2], (L, 4, D), 0.02),
        'ln_b': nrm(ks[3], (L, 4, D), 0.02),
        'ffn1_w_gu': nrm(ks[4], (L, D, 2 * D_FF), D ** -0.5),
        'ffn1_w_down': nrm(ks[5], (L, D_FF, D), BETA * D_FF ** -0.5),
        'w_in': nrm(ks[6], (L, D, D_IN), D ** -0.5),
        'cmp_w1': nrm(ks[7], (L, 2, CMP_LEN * NSA_DH, NSA_DH), (CMP_LEN * NSA_DH) ** -0.5),
        'cmp_w2': nrm(ks[8], (L, 2, NSA_DH, NSA_DH), NSA_DH ** -0.5),
        'cmp_pos': nrm(ks[9], (L, 2, CMP_LEN, NSA_DH), 0.1),
        'conv_w': nrm(ks[10], (L, SSM_CONV, SSM_CONV_DIM), SSM_CONV ** -0.5),
        'conv_b': nrm(ks[11], (L, SSM_CONV_DIM), 0.01),
        'dt_bias': dt0 + jnp.log(-jnp.expm1(-dt0)),
        'a_log': jnp.log(jax.random.uniform(ks[13], (L, SSM_HEADS), jnp.float32, 1.0, 16.0)),
        'd_skip': 1.0 + nrm(ks[14], (L, SSM_HEADS), 0.1),
        'ssm_norm_g': 1.0 + nrm(ks[15], (L, SSM_INNER), 0.02),
        'w_branch': nrm(ks[16], (L, N_BRANCH, BRANCH_W, D), BRANCH_W ** -0.5),
        'w_out': nrm(ks[17], (L, D, D), BETA * D ** -0.5),
        'xattn_wq': nrm(ks[18], (L, D, D), D ** -0.5),
        'xattn_wkv': nrm(ks[19], (L, D, 2 * D), D ** -0.5),
        'xattn_wo': nrm(ks[20], (L, D, D), BETA * D ** -0.5),
        'ffn2_w_gu': nrm(ks[21], (L, D, 2 * D_FF), D ** -0.5),
        'ffn2_w_down': nrm(ks[22], (L, D_FF, D), BETA * D_FF ** -0.5),
    }


def reference(x, mem, ln_g, ln_b, ffn1_w_gu, ffn1_w_down, w_in, cmp_w1, cmp_w2, cmp_pos,
              conv_w, conv_b, dt_bias, a_log, d_skip, ssm_norm_g, w_branch, w_out,
              xattn_wq, xattn_wkv, xattn_wo, ffn2_w_gu, ffn2_w_down):
    for l in range(DEPTH):
        x = layer_norm(ALPHA * x + 0.5 * swiglu(x, ffn1_w_gu[l], ffn1_w_down[l]), ln_g[l, 0], ln_b[l, 0])
        mix = token_mixing(x, w_in[l], cmp_w1[l], cmp_w2[l], cmp_pos[l], conv_w[l], conv_b[l],
                           dt_bias[l], a_log[l], d_skip[l], ssm_norm_g[l], w_branch[l], w_out[l])
        x = layer_norm(ALPHA * x + mix, ln_g[l, 1], ln_b[l, 1])
        x = layer_norm(ALPHA * x + memory_cross_attention(x, mem, xattn_wq[l], xattn_wkv[l], xattn_wo[l]),
                       ln_g[l, 2], ln_b[l, 2])
        x = layer_norm(ALPHA * x + 0.5 * swiglu(x, ffn2_w_gu[l], ffn2_w_down[l]), ln_g[l, 3], ln_b[l, 3])
    return x
```

```python
import math
import sys
import numpy as np
import concourse.bass as bass
import concourse.mybir as mybir
from concourse.bass_utils import run_bass_kernel_spmd

F32 = mybir.dt.float32
BF16 = mybir.dt.bfloat16
I32 = mybir.dt.int32
AF = mybir.ActivationFunctionType
ALU = mybir.AluOpType
AX = mybir.AxisListType

SEM_LIMIT = 30000
N_DMA_SEMS = 24


class Buf:
    __slots__ = ("name", "last_w", "readers")

    def __init__(self, name):
        self.name = name
        self.last_w = None
        self.readers = []


class Op:
    __slots__ = ("eng", "fn", "deps", "need_inc", "sem", "val", "is_dma", "idx", "tag")


class Prog:
    def __init__(self, nc):
        self.nc = nc
        self.engs = {"pe": nc.tensor, "act": nc.scalar, "dve": nc.vector, "pool": nc.gpsimd, "sp": nc.sync}
        self.ops = []
        self.bufs = {}
        self.last_on = {}
        self.dmas_since = []
        self.phase_deps = []
        self.phase_bufs = set()

    def buf(self, name):
        b = self.bufs.get(name)
        if b is None:
            b = self.bufs[name] = Buf(name)
        return b

    def add(self, eng, fn, reads=(), writes=(), dma=False, extra_deps=()):
        op = Op()
        op.eng = eng
        op.fn = fn
        op.is_dma = dma
        op.need_inc = False
        op.sem = None
        op.val = 0
        op.idx = len(self.ops)
        f_ = sys._getframe(2)
        op.tag = (f_.f_lineno, f_.f_back.f_lineno if f_.f_back else 0)
        deps = {}
        for b in reads:
            b = self.buf(b)
            w = b.last_w
            if w is not None:
                deps[w.idx] = (w, "raw")
        for b in writes:
            b = self.buf(b)
            w = b.last_w
            if w is not None and w.idx not in deps:
                deps[w.idx] = (w, "waw")
            for r in b.readers:
                if r.idx not in deps:
                    deps[r.idx] = (r, "war")
        real = []
        for d, kind in deps.values():
            if (not d.is_dma) and d.eng == eng and not dma:
                if eng == "pe" or kind != "raw":
                    continue
            real.append(d)
        for d in extra_deps:
            real.append(d)
        if self.phase_deps:
            for b in list(reads) + list(writes):
                if b not in self.phase_bufs:
                    self.phase_bufs.add(b)
                    real.extend(self.phase_deps)
        op.deps = real
        for b in writes:
            b = self.buf(b)
            b.last_w = op
            b.readers = []
        for b in reads:
            self.buf(b).readers.append(op)
        self.ops.append(op)
        self.last_on[eng] = op
        if dma:
            self.dmas_since.append(op)
        return op

    def barrier(self):
        self.phase_deps = list(self.last_on.values()) + list(self.dmas_since)
        self.dmas_since = []
        self.phase_bufs = set()

    def emit(self, final_wait_eng="sp"):
        nc = self.nc
        for op in self.ops:
            for d in op.deps:
                d.need_inc = True
            if op.is_dma:
                op.need_inc = True
        eng_sem = {}
        eng_cnt = {}
        dma_sems = [nc.alloc_semaphore("dq%d" % i) for i in range(N_DMA_SEMS)]
        dma_cnt = [0] * N_DMA_SEMS
        dma_last = [None] * N_DMA_SEMS
        ndma = 0
        for op in self.ops:
            if not op.need_inc:
                continue
            if op.is_dma:
                j = ndma % N_DMA_SEMS
                ndma += 1
                if dma_last[j] is not None:
                    op.deps.append(dma_last[j])
                dma_cnt[j] += 16
                op.sem = dma_sems[j]
                op.val = dma_cnt[j]
                dma_last[j] = op
            else:
                e = op.eng
                if e not in eng_sem or eng_cnt[e] >= SEM_LIMIT:
                    eng_sem[e] = nc.alloc_semaphore("s_%s_%d" % (e, op.idx))
                    eng_cnt[e] = 0
                eng_cnt[e] += 1
                op.sem = eng_sem[e]
                op.val = eng_cnt[e]
        waited = {}
        nwaits = 0
        for op in self.ops:
            E = self.engs[op.eng]
            need = {}
            for d in op.deps:
                k = id(d.sem)
                if k not in need or need[k][1] < d.val:
                    need[k] = (d.sem, d.val)
            for k, (sem, val) in need.items():
                wk = (op.eng, k)
                if waited.get(wk, 0) >= val:
                    continue
                E.wait_ge(sem, val)
                nwaits += 1
                waited[wk] = val
            try:
                inst = op.fn()
            except Exception:
                print('EMIT FAIL at op', op.idx, op.eng, 'lines', op.tag)
                raise
            if op.need_inc:
                inst.then_inc(op.sem, 16 if op.is_dma else 1)
        E = self.engs[final_wait_eng]
        for j in range(N_DMA_SEMS):
            if dma_cnt[j] > 0:
                E.wait_ge(dma_sems[j], dma_cnt[j])
        self.stats = dict(n_ops=len(self.ops), n_waits=nwaits, n_dma=ndma,
                          n_inc=sum(1 for o in self.ops if o.need_inc))
        return self.stats


class V:
    __slots__ = ("ap", "b")

    def __init__(self, ap, b):
        self.ap = ap
        self.b = b


class T:
    def __init__(self, h, name, dram=False):
        self.h = h
        self.name = name
        self.dram = dram

    def __getitem__(self, idx):
        if self.dram:
            return V(self.h[idx], self.name)
        return V(self.h[idx], self.name)

    def v(self, ap):
        return V(ap, self.name)


DT_SIZE = {F32: 4, BF16: 2, I32: 4}

D = 1024
DFF = 2816
NKC = D // 128
NFC = DFF // 128
LN_EPS = 1e-5
DEPTH = 2
ALPHA = (2 * DEPTH) ** 0.25
N_MEM = 256


class KB:
    def __init__(self, S, depth=DEPTH, stop_after=None, debug=()):
        self.S = S
        self.depth = depth
        self.stop_after = stop_after
        self.debug = debug
        self.nc = bass.Bass("TRN2", target_bir_lowering=False)
        self.P = Prog(self.nc)
        self.uid = 0
        self.sb_base = 0
        self.sb_cur = 0
        self.outs = {}
        self.arena = None
        self.rots = {}
        self.fill_regs = {}
        self.n_keep = 256

    def sb(self, name, shape, dtype):
        nbytes = int(np.prod(shape[1:])) * DT_SIZE[dtype]
        nbytes = (nbytes + 63) // 64 * 64
        off = self.sb_cur
        self.sb_cur += nbytes
        assert self.sb_cur <= 204 * 1024, ("SBUF overflow", name, self.sb_cur)
        self.uid += 1
        if self.arena is None:
            self.arena = self.nc.alloc_sbuf_tensor("arena", [128, 204 * 1024], mybir.dt.uint8)
        ap = self.arena[:, off:off + int(np.prod(shape[1:])) * DT_SIZE[dtype]].bitcast(dtype)
        if len(shape) == 3:
            ap = ap.rearrange("p (a b) -> p a b", a=shape[1])
        elif len(shape) == 4:
            ap = ap.rearrange("p (a b c) -> p a b c", a=shape[1], b=shape[2])
        if shape[0] < 128:
            ap = ap[0:shape[0]]
        return T(ap, "%s_%d" % (name, self.uid))

    def phase_begin(self):
        self.P.barrier()
        self.sb_cur = self.sb_base

    def dram(self, name, shape, dtype, kind="ExternalOutput"):
        h = self.nc.dram_tensor(name, list(shape), dtype, kind=kind)
        return T(h.ap(), name, dram=True)

    def _rw(self, reads, writes):
        return [r.b for r in reads if isinstance(r, V)], [w.b for w in writes]

    def dma(self, out, in_, eng="sp"):
        nc = self.nc
        E = self.P.engs[eng]
        return self.P.add(eng, lambda: E.dma_start(out=out.ap, in_=in_.ap), reads=[in_.b], writes=[out.b], dma=True)

    def mm(self, out, lhsT, rhs, start=True, stop=True):
        nc = self.nc
        return self.P.add("pe", lambda: nc.tensor.matmul(out.ap, lhsT.ap, rhs.ap, start=start, stop=stop),
                          reads=[lhsT.b, rhs.b], writes=[out.b])

    def tr(self, out, in_, ident):
        nc = self.nc
        return self.P.add("pe", lambda: nc.tensor.transpose(out.ap, in_.ap, ident.ap),
                          reads=[in_.b, ident.b], writes=[out.b])

    def act(self, out, in_, func, bias=None, scale=None, accum=None, eng="act"):
        nc = self.nc
        kw = {}
        reads = [in_.b]
        writes = [out.b]
        if bias is not None:
            if isinstance(bias, V):
                kw["bias"] = bias.ap
                reads.append(bias.b)
            else:
                kw["bias"] = bias
        if scale is not None:
            if isinstance(scale, V):
                kw["scale"] = scale.ap
                reads.append(scale.b)
            else:
                kw["scale"] = scale
        if accum is not None:
            kw["accum_out"] = accum.ap
            writes.append(accum.b)
        return self.P.add("act", lambda: nc.scalar.activation(out=out.ap, in_=in_.ap, func=func, **kw),
                          reads=reads, writes=writes)

    def ts(self, out, in0, s1, s2, op0, op1=None, accum=None, eng="dve"):
        E = self.P.engs[eng]
        reads = [in0.b]
        writes = [out.b]
        a1 = s1
        a2 = s2
        if isinstance(s1, V):
            a1 = s1.ap
            reads.append(s1.b)
        if isinstance(s2, V):
            a2 = s2.ap
            reads.append(s2.b)
        kw = {}
        if op1 is not None:
            kw["op1"] = op1
        if accum is not None:
            kw["accum_out"] = accum.ap
            writes.append(accum.b)
        return self.P.add(eng, lambda: E.tensor_scalar(out=out.ap, in0=in0.ap, scalar1=a1, scalar2=a2, op0=op0, **kw),
                          reads=reads, writes=writes)

    def tt(self, out, in0, in1, op, eng="dve"):
        E = self.P.engs[eng]
        return self.P.add(eng, lambda: E.tensor_tensor(out=out.ap, in0=in0.ap, in1=in1.ap, op=op),
                          reads=[in0.b, in1.b], writes=[out.b])

    def stt(self, out, in0, scalar, in1, op0, op1, accum=None):
        nc = self.nc
        reads = [in0.b, in1.b]
        writes = [out.b]
        a = scalar
        if isinstance(scalar, V):
            a = scalar.ap
            reads.append(scalar.b)
        kw = {}
        if accum is not None:
            kw["accum_out"] = accum.ap
            writes.append(accum.b)
        return self.P.add("dve", lambda: nc.vector.scalar_tensor_tensor(out=out.ap, in0=in0.ap, scalar=a, in1=in1.ap,
                                                                     op0=op0, op1=op1, **kw),
                          reads=reads, writes=writes)

    def copy(self, out, in_, eng="dve"):
        E = self.P.engs[eng]
        if eng == "act":
            return self.P.add(eng, lambda: E.copy(out=out.ap, in_=in_.ap), reads=[in_.b], writes=[out.b])
        return self.P.add(eng, lambda: E.tensor_copy(out=out.ap, in_=in_.ap), reads=[in_.b], writes=[out.b])

    def memset(self, out, val, eng="pool"):
        E = self.P.engs[eng]
        return self.P.add(eng, lambda: E.memset(out.ap, val), writes=[out.b])

    def red(self, out, in_, op, axis=AX.X, eng="dve"):
        E = self.P.engs[eng]
        return self.P.add(eng, lambda: E.tensor_reduce(out=out.ap, in_=in_.ap, axis=axis, op=op),
                          reads=[in_.b], writes=[out.b])

    def recip(self, out, in_):
        nc = self.nc
        return self.P.add("dve", lambda: nc.vector.reciprocal(out=out.ap, in_=in_.ap), reads=[in_.b], writes=[out.b])

    def aselect(self, out, in_, pattern, cmp, fill, base, cm):
        nc = self.nc
        regs = self.fill_regs

        def fn():
            if fill not in regs:
                regs[fill] = nc.gpsimd.to_reg(float(fill))
            return nc.gpsimd.affine_select(out=out.ap, in_=in_.ap, pattern=pattern, compare_op=cmp,
                                           fill=regs[fill], base=base, channel_multiplier=cm)
        return self.P.add("pool", fn, reads=[in_.b], writes=[out.b])

    def iota(self, out, pattern, base, cm):
        nc = self.nc
        return self.P.add("pool", lambda: nc.gpsimd.iota(out.ap, pattern=pattern, base=base, channel_multiplier=cm,
                                                         allow_small_or_imprecise_dtypes=True), writes=[out.b])

    def setup(self):
        nc = self.nc
        self.ps = []
        for i in range(5):
            h = nc.alloc_psum_tensor("ps%d" % i, [128, 512], F32)
            self.ps.append(T(h, "ps%d" % i))
        self.psb = []
        for i in range(2):
            h = nc.alloc_psum_tensor("psb%d" % i, [128, 1024], BF16)
            self.psb.append(T(h, "psb%d" % i))
        self.ps_rr = 0
        self.psb_rr = 0
        self.ident_f = self.sb("identf", [128, 128], F32)
        self.ident = self.sb("ident", [128, 128], BF16)
        self.memset(self.ident_f[:], 1.0)
        self.aselect(self.ident_f[:], self.ident_f[:], [[-1, 128]], ALU.is_equal, 0.0, 0, 1)
        self.copy(self.ident[:], self.ident_f[:], eng="pool")
        self.sb_base = self.sb_cur

    def next_ps(self):
        t = self.ps[self.ps_rr % 5]
        self.ps_rr += 1
        return t

    def next_psb(self):
        t = self.psb[self.psb_rr % 2]
        self.psb_rr += 1
        return t

    def load_w(self, name, dram_ap_fn, kchunks, ncols, eng="pool", split=4):
        w = self.sb(name, [128, kchunks, ncols], BF16)
        for k in range(kchunks):
            self.dma(w[:, k, :], dram_ap_fn(k), eng="pool")
        return w

    def layer_norm_tile(self, r, g_bc, b_bc, out_f32, scr):
        st = scr["st"]
        junk = scr["junk"]
        self.act(junk[:], r[:], AF.Identity, accum=st[:, 0:1])
        self.act(junk[:], r[:], AF.Square, accum=st[:, 1:2])
        self.ts(st[:, 2:3], st[:, 0:1], 1.0 / D, None, ALU.mult)
        self.tt(st[:, 3:4], st[:, 2:3], st[:, 2:3], ALU.mult)
        self.stt(st[:, 4:5], st[:, 1:2], 1.0 / D, st[:, 3:4], ALU.mult, ALU.subtract)
        self.ts(st[:, 4:5], st[:, 4:5], 0.0, LN_EPS, ALU.max, ALU.add)
        self.act(st[:, 5:6], st[:, 4:5], AF.Sqrt)
        self.recip(st[:, 6:7], st[:, 5:6])
        self.ts(out_f32[:], r[:], st[:, 2:3], st[:, 6:7], ALU.subtract, ALU.mult)
        self.tt(out_f32[:], out_f32[:], g_bc[:], ALU.mult)
        self.tt(out_f32[:], out_f32[:], b_bc[:], ALU.add)

    def store_xT(self, x_f32, xT_dram, t0, scr):
        xb = scr["xb"]
        xTs = scr["xTs"]
        self.copy(xb[:], x_f32[:], eng="act")
        pb = self.next_psb()
        for k in range(NKC):
            self.tr(pb[:, k * 128:(k + 1) * 128], xb[:, k * 128:(k + 1) * 128], self.ident[:])
        self.copy(xTs[:], pb[:, :], eng="dve")
        self.dma(V(xT_dram.h.rearrange("(k p) s -> p k s", p=128)[:, :, t0:t0 + 128], xT_dram.name),
                 V(xTs.h[:].rearrange("p (k t) -> p k t", k=NKC), xTs.name))

    def dma_s(self, out, in_, eng="sp"):
        E = self.P.engs[eng]
        return self.P.add(eng, lambda: E.dma_start(out=out.ap, in_=in_.ap, allow_slow_non_contiguous=True),
                          reads=[in_.b], writes=[out.b], dma=True)

    def rot(self, name, n, shape, dtype):
        key = "_rot_" + name
        lst = [self.sb(name + str(i), shape, dtype) for i in range(n)]
        self.rots[key] = [lst, 0]
        return key

    def nx(self, key):
        lst, i = self.rots[key]
        self.rots[key][1] = i + 1
        return lst[i % len(lst)]

    def vmax(self, out, in_):
        nc = self.nc
        return self.P.add("dve", lambda: nc.vector.max(out=out.ap, in_=in_.ap), reads=[in_.b], writes=[out.b])

    def match_replace(self, out, rep, vals, imm):
        nc = self.nc
        return self.P.add("dve", lambda: nc.vector.match_replace(out=out.ap, in_to_replace=rep.ap, in_values=vals.ap, imm_value=imm),
                          reads=[rep.b, vals.b], writes=[out.b])

    def redabs(self, out, in_):
        nc = self.nc
        return self.P.add("dve", lambda: nc.vector.tensor_reduce(out=out.ap, in_=in_.ap, axis=AX.X, op=ALU.max,
                                                                 apply_absolute_value=True),
                          reads=[in_.b], writes=[out.b])

    def fm_rows(self, FMS, c0, nchunk, s0, s1):
        return V(FMS.h[c0 * 128:(c0 + nchunk) * 128, s0:s1].rearrange("(c p) s -> p c s", p=128), FMS.name)

    def setup_consts(self, meta, bdm, ovl, NB, NCP):
        S = self.S
        NT = S // 128
        self.NB = NB
        self.NCP = NCP
        self.meta = self.sb("meta", [128, 32], F32)
        self.dma(self.meta[:], meta[:, :])
        self.bdm = self.sb("bdm", [128, 256], F32)
        self.dma(self.bdm[:], bdm[:, :])
        self.ovl = self.sb("ovl", [128, NCP // 128, NB], F32)
        self.dma(self.ovl[:], V(ovl.h.rearrange("(c p) j -> p c j", p=128), ovl.name))
        self.U = self.sb("U", [128, 128], F32)
        self.memset(self.U[:], 1.0)
        self.aselect(self.U[:], self.U[:], [[1, 128]], ALU.is_ge, 0.0, 0, -1)
        self.cneg30 = self.sb("cneg30", [128, 128], F32)
        self.memset(self.cneg30[:], 0.0)
        self.aselect(self.cneg30[:], self.cneg30[:], [[-1, 128]], ALU.is_ge, -1e30, 0, 1)
        self.cneg2k = self.sb("cneg2k", [128, 128], F32)
        self.memset(self.cneg2k[:], 0.0)
        self.aselect(self.cneg2k[:], self.cneg2k[:], [[-1, 128]], ALU.is_ge, -2000.0, 0, 1)
        self.band = self.sb("band", [128, 640], F32)
        self.memset(self.band[:], 0.0)
        self.aselect(self.band[:], self.band[:], [[1, 640]], ALU.is_ge, -2000.0, -1, -1)
        self.aselect(self.band[:], self.band[:], [[-1, 640]], ALU.is_ge, -2000.0, 512, 1)
        self.decayT4 = self.sb("decayT4", [128, 4, 128], F32)
        self.xi = self.sb("xi", [128, 128], F32)
        self.zeta = self.sb("zeta", [128, 128], F32)
        self.cdecay = self.sb("cdecay", [128, 1], F32)
        self.rkc = self.sb("rkc", [128, 20], F32)
        self.sb_base = self.sb_cur
        dji = self.sb("dji", [128, 128], F32)
        self.iota(dji[:], [[1, 128]], 0, -1)
        for h in range(4):
            self.act(self.decayT4[:, h, :], dji[:], AF.Exp, scale=RET_LNG[h])
        self.tt(self.decayT4[:], self.decayT4[:], V(self.U.h[:, :].unsqueeze(1).to_broadcast([128, 4, 128]), self.U.name), ALU.mult)
        ip1 = self.sb("ip1", [128, 128], F32)
        self.iota(ip1[:], [[1, 128]], 1, 0)
        self.act(self.xi[:], ip1[:], AF.Exp, scale=self.meta[:, 12:13])
        jr = self.sb("jr", [128, 128], F32)
        self.iota(jr[:], [[0, 128]], 127, -1)
        for h in range(4):
            self.act(self.zeta[:, 32 * h:32 * h + 32], jr[:, 32 * h:32 * h + 32], AF.Exp, scale=RET_LNG[h])
        c128 = self.sb("c128", [128, 1], F32)
        self.memset(c128[:], 128.0)
        self.act(self.cdecay[:], c128[:], AF.Exp, scale=self.meta[:, 12:13])
        for k in range(20):
            self.memset(self.rkc[:, k:k + 1], 2.0 ** (-k))

    def build_addmask(self):
        NT = self.S // 128
        NB = self.NB
        self.addmask = self.sb("addmask", [128, NT, NB], F32)
        self.memset(self.addmask[:], 0.0)
        for i in range(NT):
            for half in range(2):
                cur = 2 * i + half
                r0 = 64 * half
                v = self.addmask[r0:r0 + 64, i, :]
                self.aselect(v, v, [[-1, NB]], ALU.is_ge, -1e30, cur, 0)
                self.memset(self.addmask[r0:r0 + 64, i, 0:1], 1e30)
                self.memset(self.addmask[r0:r0 + 64, i, cur:cur + 1], 1e30)
                if cur >= 1:
                    self.memset(self.addmask[r0:r0 + 64, i, cur - 1:cur], 1e30)

    def phase_rope(self, ROPE):
        S = self.S
        self.phase_begin()
        pos = self.sb("pos", [128, S], F32)
        self.iota(pos[:], [[1, S]], 0, 0)
        a = self.sb("a", [128, S], F32)
        ki = self.sb("ki", [128, S], I32)
        kf = self.sb("kf", [128, S], F32)
        m = self.sb("m", [128, S], F32)
        r = self.sb("r", [128, S], F32)
        PI = math.pi
        for t in range(4):
            for which in range(2):
                self.ts(a[:], pos[:], self.meta[:, t:t + 1], (PI / 2 if which == 0 else 0.0), ALU.mult, ALU.add)
                self.ts(kf[:], a[:], 1.0 / (2 * PI), None, ALU.mult)
                self.copy(ki[:], kf[:])
                self.copy(kf[:], ki[:])
                self.stt(r[:], kf[:], -2 * PI, a[:], ALU.mult, ALU.add)
                self.ts(m[:], r[:], PI, -2 * PI, ALU.is_gt, ALU.mult)
                self.tt(r[:], r[:], m[:], ALU.add)
                self.ts(m[:], r[:], -PI, 2 * PI, ALU.is_lt, ALU.mult)
                self.tt(r[:], r[:], m[:], ALU.add)
                self.ts(r[:], r[:], PI, -PI, ALU.min, ALU.max)
                self.act(r[:], r[:], AF.Sin)
                col = 4 + 4 * which + t
                self.ts(r[:], r[:], self.meta[:, col:col + 1], None, ALU.mult)
                self.dma(ROPE[t, which], r[:])

    def finish_tile(self, rq, g_bc, b_bc, x_out, xT_out, t0, scr):
        self.layer_norm_tile(rq, g_bc, b_bc, rq, scr)
        self.dma(x_out[t0:t0 + 128, :], rq[:])
        if xT_out is not None:
            self.store_xT(rq, xT_out, t0, scr)

    def ln_setup(self, ln_g, ln_b):
        g_bc = self.sb("g_bc", [128, D], F32)
        b_bc = self.sb("b_bc", [128, D], F32)
        self.dma(g_bc[:], V(ln_g.h.partition_broadcast(128), ln_g.name))
        self.dma(b_bc[:], V(ln_b.h.partition_broadcast(128), ln_b.name))
        scr = dict(st=self.sb("st", [128, 8], F32), junk=self.sb("junk", [128, D], BF16),
                   xb=self.sb("xb", [128, D], BF16), xTs=self.sb("xTs", [128, D], BF16))
        return g_bc, b_bc, scr

    def phase_ffn(self, x_in, xT_in, w_gu, w_down, ln_g, ln_b, x_out, xT_out):
        S = self.S
        self.phase_begin()
        wgu = self.load_w("wgu", lambda k: V(w_gu.h[k * 128:(k + 1) * 128, :], w_gu.name), NKC, 2 * DFF)
        wdn = self.load_w("wdn", lambda k: V(w_down.h[k * 128:(k + 1) * 128, :], w_down.name), NFC, D)
        g_bc, b_bc, scr = self.ln_setup(ln_g, ln_b)
        xt = self.sb("xT", [128, NKC, 512], BF16)
        hT = self.sb("hT", [128, NFC, 512], BF16)
        sg = [self.sb("sg%d" % i, [128, 512], F32) for i in range(2)]
        xr = [self.sb("xr%d" % i, [128, D], F32) for i in range(1)]
        rq = self.sb("r", [128, D], F32)
        for t in range(S // 512):
            self.dma(xt[:], V(xT_in.h.rearrange("(k p) s -> p k s", p=128)[:, :, t * 512:(t + 1) * 512], xT_in.name))
            for j in range(NFC):
                pg = self.next_ps()
                pu = self.next_ps()
                for k in range(NKC):
                    self.mm(pg[:], wgu[:, k, j * 128:(j + 1) * 128], xt[:, k, :], start=(k == 0), stop=(k == NKC - 1))
                for k in range(NKC):
                    self.mm(pu[:], wgu[:, k, DFF + j * 128:DFF + (j + 1) * 128], xt[:, k, :], start=(k == 0), stop=(k == NKC - 1))
                s = sg[j % 2]
                self.act(s[:], pg[:], AF.Silu)
                self.tt(hT[:, j, :], s[:], pu[:], ALU.mult)
            for q in range(4):
                t0 = t * 512 + q * 128
                xq = xr[0]
                self.dma(xq[:], x_in[t0:t0 + 128, :])
                for half in range(2):
                    hs = slice(half * 512, (half + 1) * 512)
                    pd = self.next_ps()
                    for j in range(NFC):
                        self.mm(pd[:], hT[:, j, q * 128:(q + 1) * 128], wdn[:, j, hs], start=(j == 0), stop=(j == NFC - 1))
                    self.act(xq[:, hs], xq[:, hs], AF.Copy, scale=ALPHA)
                    self.stt(rq[:, hs], pd[:], 0.5, xq[:, hs], ALU.mult, ALU.add)
                self.finish_tile(rq, g_bc, b_bc, x_out, xT_out, t0, scr)

    def phase_transpose_in(self, x_in, xT_out):
        S = self.S
        self.phase_begin()
        xr = [self.sb("xr%d" % i, [128, D], F32) for i in range(2)]
        scr = dict(xb=self.sb("xb", [128, D], BF16), xTs=self.sb("xTs", [128, D], BF16))
        for i in range(S // 128):
            xq = xr[i % 2]
            self.dma(xq[:], x_in[i * 128:(i + 1) * 128, :])
            self.store_xT(xq, xT_out, i * 128, scr)

    def phase_inproj(self, xT_in, w2, ROPE, FMS, TMB, TMF):
        S = self.S
        self.phase_begin()
        w = self.load_w("win", lambda k: V(w2.h[k * 128:(k + 1) * 128, :], w2.name), NKC, NCOL2)
        xt = self.sb("xT", [128, NKC, 512], BF16)
        tab = self.sb("tab", [128, 4, 2, 512], F32)
        t1 = self.rot("t1", 2, [128, 512], F32)
        t2 = self.rot("t2", 2, [128, 512], F32)
        ob = self.rot("ob", 3, [128, 512], BF16)
        tmb = self.rot("tmb", 2, [128, 960], BF16)
        tmf = self.rot("tmf", 2, [128, 24], F32)
        for t in range(S // 512):
            ss = slice(t * 512, (t + 1) * 512)
            self.dma(xt[:], V(xT_in.h.rearrange("(k p) s -> p k s", p=128)[:, :, ss], xT_in.name))
            self.dma(tab[:], V(ROPE.h[:, :, :, ss].rearrange("t w p s -> p t w s"), ROPE.name))
            for ci, tb in enumerate(ROPED_TABLES):
                pA = self.next_ps()
                pB = self.next_ps()
                for k in range(NKC):
                    self.mm(pA[:], w[:, k, (2 * ci) * 128:(2 * ci + 1) * 128], xt[:, k, :], start=(k == 0), stop=(k == NKC - 1))
                for k in range(NKC):
                    self.mm(pB[:], w[:, k, (2 * ci + 1) * 128:(2 * ci + 2) * 128], xt[:, k, :], start=(k == 0), stop=(k == NKC - 1))
                a1 = self.nx(t1)
                a2 = self.nx(t2)
                o = self.nx(ob)
                self.tt(a1[:], pA[:], tab[:, tb, 0, :], ALU.mult)
                self.tt(a2[:], pB[:], tab[:, tb, 1, :], ALU.mult)
                self.tt(o[:], a1[:], a2[:], ALU.add, eng="pool")
                self.dma(V(FMS.h[ci * 128:(ci + 1) * 128, ss], FMS.name), o[:])
            nr = len(ROPED_TABLES)
            for j in range(7):
                wc = 2 * nr + j
                pA = self.next_ps()
                for k in range(NKC):
                    self.mm(pA[:], w[:, k, wc * 128:(wc + 1) * 128], xt[:, k, :], start=(k == 0), stop=(k == NKC - 1))
                o = self.nx(ob)
                self.copy(o[:], pA[:], eng="act")
                self.dma(V(FMS.h[(nr + j) * 128:(nr + j + 1) * 128, ss], FMS.name), o[:])
            for q in range(4):
                t0 = t * 512 + q * 128
                pA = self.next_ps()
                pB = self.next_ps()
                for k in range(NKC):
                    self.mm(pA[:], xt[:, k, q * 128:(q + 1) * 128], w[:, k, TM0:TM0 + 512], start=(k == 0), stop=(k == NKC - 1))
                for k in range(NKC):
                    self.mm(pB[:, 0:472], xt[:, k, q * 128:(q + 1) * 128], w[:, k, TM0 + 512:TM0 + 984], start=(k == 0), stop=(k == NKC - 1))
                b = self.nx(tmb)
                f = self.nx(tmf)
                self.copy(b[:, 0:512], pA[:], eng="act")
                self.copy(b[:, 512:960], pB[:, 0:448])
                self.copy(f[:], pB[:, 448:472])
                self.dma(TMB[t0:t0 + 128, :], b[:])
                self.dma(TMF[t0:t0 + 128, :], f[:])

    def store_yT(self, y, YT, br, n, yTs_key):
        pb = self.next_psb()
        self.tr(pb[:, 0:128], y[:, 0:128], self.ident[:])
        self.tr(pb[:, 128:256], y[:, 128:256], self.ident[:])
        yTs = self.nx(yTs_key)
        self.copy(yTs[:], pb[:, 0:256])
        self.dma(V(YT.h[br * 256:(br + 1) * 256, n * 128:(n + 1) * 128].rearrange("(c p) t -> p c t", p=128), YT.name),
                 V(yTs.h[:, :].rearrange("p (c t) -> p c t", c=2), yTs.name))

    def phase_ret(self, FMS, TMB, YT):
        S = self.S
        NT = S // 128
        self.phase_begin()
        rq = self.sb("rq", [128, S], BF16)
        rk = self.sb("rk", [128, S], BF16)
        self.dma(rq[:], V(FMS.h[0:128, :], FMS.name))
        self.dma(rk[:], V(FMS.h[128:256, :], FMS.name))
        Sbd = self.sb("Sbd", [128, 256], F32)
        Sbd_bf = self.sb("Sbd_bf", [128, 256], BF16)
        self.memset(Sbd[:], 0.0)
        self.memset(Sbd_bf[:], 0.0)
        vt_k = self.rot("vt", 2, [128, 512], BF16)
        qxi_k = self.rot("qxi", 2, [128, 128], BF16)
        qm_k = self.rot("qm", 2, [128, 4, 128], BF16)
        kz_k = self.rot("kz", 2, [128, 128], BF16)
        PT_k = self.rot("PT", 2, [128, 4, 128], BF16)
        cross_k = self.rot("cross", 2, [128, 256], F32)
        o_k = self.rot("o", 2, [128, 256], F32)
        tmp_k = self.rot("tmp", 2, [128, 256], F32)
        osq_k = self.rot("osq", 2, [128, 256], F32)
        sg_k = self.rot("sg", 2, [128, 256], F32)
        st_k = self.rot("st", 2, [128, 16], F32)
        y_k = self.rot("y", 2, [128, 256], BF16)
        yTs_k = self.rot("yTs", 2, [128, 256], BF16)
        hm = V(self.meta.h[:, 13:17].unsqueeze(2).to_broadcast([128, 4, 128]), self.meta.name)
        for n in range(NT):
            sl = slice(n * 128, (n + 1) * 128)
            vt = self.nx(vt_k)
            self.dma(vt[:], TMB[n * 128:(n + 1) * 128, 0:512])
            qxi = self.nx(qxi_k)
            self.tt(qxi[:], rq[:, sl], self.xi[:], ALU.mult)
            qm = self.nx(qm_k)
            self.tt(qm[:], V(rq.h[:, sl].unsqueeze(1).to_broadcast([128, 4, 128]), rq.name), hm, ALU.mult, eng="pool")
            pb = self.next_psb()
            self.tr(pb[:, 0:128], rk[:, sl], self.ident[:])
            kz = self.nx(kz_k)
            self.tt(kz[:], pb[:, 0:128], self.zeta[:], ALU.mult)
            ps1 = self.next_ps()
            self.mm(ps1[:], rk[:, sl], V(qm.h[:, :, :].rearrange("p h i -> p (h i)"), qm.name))
            PT = self.nx(PT_k)
            self.tt(PT[:], V(ps1.h[:, :].rearrange("p (h i) -> p h i", h=4), ps1.name), self.decayT4[:], ALU.mult)
            ps2 = self.next_ps()
            self.mm(ps2[:, 0:256], qxi[:], Sbd_bf[:])
            cross = self.nx(cross_k)
            self.copy(cross[:], ps2[:, 0:256], eng="act")
            ps3 = self.next_ps()
            for h in range(4):
                self.mm(ps3[:, 64 * h:64 * h + 64], PT[:, h, :], vt[:, 64 * h:64 * h + 64])
            o = self.nx(o_k)
            self.tt(o[:], ps3[:, 0:256], cross[:], ALU.add)
            ps4 = self.next_ps()
            self.mm(ps4[:, 0:256], kz[:], vt[:, 0:256])
            tmp = self.nx(tmp_k)
            self.tt(tmp[:], ps4[:, 0:256], self.bdm[:], ALU.mult)
            self.stt(Sbd[:], Sbd[:], self.cdecay[:, 0:1], tmp[:], ALU.mult, ALU.add)
            self.copy(Sbd_bf[:], Sbd[:], eng="act")
            st = self.nx(st_k)
            o3 = V(o.h[:, :].rearrange("p (h e) -> p h e", h=4), o.name)
            self.red(st[:, 0:4], o3, ALU.add)
            osq = self.nx(osq_k)
            self.tt(osq[:], o[:], o[:], ALU.mult, eng="pool")
            self.red(st[:, 4:8], V(osq.h[:, :].rearrange("p (h e) -> p h e", h=4), osq.name), ALU.add)
            self.ts(st[:, 8:12], st[:, 0:4], 1.0 / 64, None, ALU.mult)
            self.tt(st[:, 12:16], st[:, 8:12], st[:, 8:12], ALU.mult)
            self.stt(st[:, 4:8], st[:, 4:8], 1.0 / 64, st[:, 12:16], ALU.mult, ALU.subtract)
            self.ts(st[:, 4:8], st[:, 4:8], 0.0, LN_EPS, ALU.max, ALU.add)
            self.act(st[:, 4:8], st[:, 4:8], AF.Sqrt)
            self.recip(st[:, 4:8], st[:, 4:8])
            self.tt(o3, o3, V(st.h[:, 8:12].unsqueeze(2).to_broadcast([128, 4, 64]), st.name), ALU.subtract)
            self.tt(o3, o3, V(st.h[:, 4:8].unsqueeze(2).to_broadcast([128, 4, 64]), st.name), ALU.mult)
            sg = self.nx(sg_k)
            self.act(sg[:], vt[:, 256:512], AF.Silu)
            y = self.nx(y_k)
            self.tt(y[:], o[:], sg[:], ALU.mult)
            self.store_yT(y, YT, 0, n, yTs_k)

    def phase_ssd(self, FMS, TMB, TMF, conv_w, conv_b, dt_bias, a_log, d_skip, norm_g, YT):
        S = self.S
        NT = S // 128
        self.phase_begin()
        cw = self.sb("cw", [128, 6, 4], F32)
        for k_ in range(4):
            self.dma_s(cw[:, :, k_], V(conv_w.h[k_].rearrange("(c p) -> p c", p=128), conv_w.name))
        cb = self.sb("cb", [128, 6], F32)
        self.dma_s(cb[:], V(conv_b.h.rearrange("(c p) -> p c", p=128), conv_b.name))
        dtb = self.sb("dtb", [128, 4], F32)
        self.dma(dtb[:], V(dt_bias.h.partition_broadcast(128), dt_bias.name))
        a_bc = self.sb("a_bc", [128, 4], F32)
        self.dma(a_bc[:], V(a_log.h.partition_broadcast(128), a_log.name))
        self.act(a_bc[:], a_bc[:], AF.Exp)
        self.ts(a_bc[:], a_bc[:], -1.0, None, ALU.mult)
        Dbc = self.sb("Dbc", [128, 4], F32)
        self.dma(Dbc[:], V(d_skip.h.partition_broadcast(128), d_skip.name))
        ng_bc = self.sb("ng_bc", [128, 256], F32)
        self.dma(ng_bc[:], V(norm_g.h.partition_broadcast(128), norm_g.name))
        xbcs = self.sb("xbcs", [128, 6, S], BF16)
        raw_k = self.rot("raw", 2, [128, 6, 515], BF16)
        acc_k = self.rot("acc", 2, [128, 512], F32)
        for t in range(S // 512):
            raw = self.nx(raw_k)
            if t == 0:
                self.memset(raw[:, :, 0:3], 0.0)
                self.dma(raw[:, :, 3:515], self.fm_rows(FMS, 14, 6, 0, 512))
            else:
                self.dma(raw[:, :, 0:515], self.fm_rows(FMS, 14, 6, t * 512 - 3, (t + 1) * 512))
            for c in range(6):
                acc = self.nx(acc_k)
                self.ts(acc[:], raw[:, c, 3:515], cw[:, c, 3:4], None, ALU.mult)
                for k in (2, 1, 0):
                    self.stt(acc[:], raw[:, c, k:k + 512], cw[:, c, k:k + 1], acc[:], ALU.mult, ALU.add)
                self.act(xbcs[:, c, t * 512:(t + 1) * 512], acc[:], AF.Silu, bias=cb[:, c:c + 1])
        prev = self.sb("prev", [128, 256], F32)
        prev_bf = self.sb("prev_bf", [128, 256], BF16)
        self.memset(prev[:], 0.0)
        self.memset(prev_bf[:], 0.0)
        xsB_k = self.rot("xsB", 2, [128, 512], BF16)
        tmf_k = self.rot("tmf", 2, [128, 24], F32)
        zt_k = self.rot("zt", 2, [128, 256], BF16)
        st_k = self.rot("st", 2, [128, 32], F32)
        adtb_k = self.rot("adtb", 2, [128, 4, 128], F32)
        seg_k = self.rot("seg", 2, [128, 4, 128], F32)
        MT_k = self.rot("MT", 2, [128, 4, 128], BF16)
        X_k = self.rot("X", 2, [128, 256], BF16)
        Xd_k = self.rot("Xd", 2, [128, 256], BF16)
        yd_k = self.rot("yd", 2, [128, 256], F32)
        y_k = self.rot("y", 2, [128, 256], F32)
        t2_k = self.rot("t2", 2, [128, 256], F32)
        sz_k = self.rot("sz", 2, [128, 256], F32)
        yb_k = self.rot("yb", 2, [128, 256], BF16)
        yTs_k = self.rot("yTs", 2, [128, 256], BF16)
        Ubc = V(self.U.h[:, :].unsqueeze(1).to_broadcast([128, 4, 128]), self.U.name)

        def h4(t_):
            return V(t_.h[:, 0:256].rearrange("p (h e) -> p h e", h=4), t_.name)

        def bc4(v_):
            return V(v_.ap.unsqueeze(2).to_broadcast([128, 4, 64]), v_.b)

        for n in range(NT):
            sl = slice(n * 128, (n + 1) * 128)
            pb = self.next_psb()
            for c in range(4):
                self.tr(pb[:, c * 128:(c + 1) * 128], xbcs[:, c, sl], self.ident[:])
            xsB = self.nx(xsB_k)
            self.copy(xsB[:], pb[:, 0:512])
            tmf = self.nx(tmf_k)
            self.dma(tmf[:], TMF[n * 128:(n + 1) * 128, :])
            zt = self.nx(zt_k)
            self.dma(zt[:], TMB[n * 128:(n + 1) * 128, 704:960])
            st = self.nx(st_k)
            self.tt(st[:, 0:4], tmf[:, 20:24], dtb[:], ALU.add)
            self.act(st[:, 0:4], st[:, 0:4], AF.Exp)
            self.act(st[:, 0:4], st[:, 0:4], AF.Ln, bias=1.0)
            self.tt(st[:, 4:8], st[:, 0:4], a_bc[:], ALU.mult)
            adtb = self.nx(adtb_k)
            self.copy(adtb[:], V(st.h[:, 4:8].unsqueeze(2).to_broadcast([128, 4, 128]), st.name))
            psA = self.next_ps()
            self.mm(psA[:, 0:4], self.U[:], st[:, 4:8])
            self.copy(st[:, 8:12], psA[:, 0:4], eng="act")
            psB = self.next_ps()
            for h in range(4):
                self.mm(psB[:, h * 128:(h + 1) * 128], adtb[:, h, :], self.U[:])
            seg = self.nx(seg_k)
            for h in range(4):
                self.ts(seg[:, h, :], psB[:, h * 128:(h + 1) * 128], st[:, 8 + h:9 + h], 0.0, ALU.subtract, ALU.min)
            self.act(seg[:], seg[:], AF.Exp)
            self.tt(seg[:], seg[:], Ubc, ALU.mult, eng="pool")
            alast = V(psB.h[:, 127:512:128], psB.name)
            self.tt(st[:, 12:16], alast, st[:, 8:12], ALU.subtract)
            self.act(st[:, 12:16], st[:, 12:16], AF.Exp)
            self.act(st[:, 16:20], alast, AF.Exp)
            self.act(st[:, 20:24], st[:, 8:12], AF.Exp)
            psG = self.next_ps()
            for g in range(2):
                self.mm(psG[:, g * 128:(g + 1) * 128], xbcs[:, 2 + g, sl], xbcs[:, 4 + g, sl])
            MT = self.nx(MT_k)
            for g in range(2):
                self.tt(MT[:, 2 * g:2 * g + 2, :], seg[:, 2 * g:2 * g + 2, :],
                        V(psG.h[:, g * 128:(g + 1) * 128].unsqueeze(1).to_broadcast([128, 2, 128]), psG.name), ALU.mult)
            X = self.nx(X_k)
            self.tt(h4(X), h4(xsB), bc4(st[:, 0:4]), ALU.mult)
            psY = self.next_ps()
            for h in range(4):
                self.mm(psY[:, 64 * h:64 * h + 64], MT[:, h, :], X[:, 64 * h:64 * h + 64])
            psO = self.next_ps()
            for g in range(2):
                self.mm(psO[:, 128 * g:128 * g + 128], xbcs[:, 4 + g, sl], prev_bf[:, 128 * g:128 * g + 128])
            yd = self.nx(yd_k)
            self.copy(yd[:], psY[:, 0:256], eng="act")
            y = self.nx(y_k)
            self.tt(h4(y), h4(psO), bc4(st[:, 20:24]), ALU.mult)
            self.tt(y[:], y[:], yd[:], ALU.add)
            t2 = self.nx(t2_k)
            self.tt(h4(t2), h4(xsB), bc4(Dbc[:, 0:4]), ALU.mult, eng="pool")
            self.tt(y[:], y[:], t2[:], ALU.add)
            Xd = self.nx(Xd_k)
            self.tt(h4(Xd), h4(X), bc4(st[:, 12:16]), ALU.mult, eng="pool")
            psS = self.next_ps()
            for g in range(2):
                self.mm(psS[:, 128 * g:128 * g + 128], xsB[:, 256 + 128 * g:256 + 128 * g + 128], Xd[:, 128 * g:128 * g + 128])
            self.tt(h4(prev), h4(prev), bc4(st[:, 16:20]), ALU.mult)
            self.tt(prev[:], prev[:], psS[:, 0:256], ALU.add)
            self.copy(prev_bf[:], prev[:], eng="act")
            sz = self.nx(sz_k)
            self.act(sz[:], zt[:], AF.Silu)
            self.tt(y[:], y[:], sz[:], ALU.mult)
            self.tt(t2[:], y[:], y[:], ALU.mult, eng="pool")
            self.red(st[:, 24:26], V(t2.h[:, :].rearrange("p (g e) -> p g e", g=2), t2.name), ALU.add)
            self.ts(st[:, 24:26], st[:, 24:26], 1.0 / 128, LN_EPS, ALU.mult, ALU.add)
            self.act(st[:, 24:26], st[:, 24:26], AF.Sqrt)
            self.recip(st[:, 24:26], st[:, 24:26])
            y2 = V(y.h[:, :].rearrange("p (g e) -> p g e", g=2), y.name)
            self.tt(y2, y2, V(st.h[:, 24:26].unsqueeze(2).to_broadcast([128, 2, 128]), st.name), ALU.mult)
            yb = self.nx(yb_k)
            self.tt(yb[:], y[:], ng_bc[:], ALU.mult)
            self.store_yT(yb, YT, 3, n, yTs_k)

    def softmax_pv(self, Ssb, nk, Vt, kt0, out, kk, clamp=None):
        st = self.nx(kk["st"])
        self.red(st[:, 0:1], Ssb, ALU.max)
        if clamp is not None:
            self.ts(st[:, 0:1], st[:, 0:1], clamp, None, ALU.max)
        self.ts(st[:, 1:2], st[:, 0:1], -1.0, None, ALU.mult)
        P = self.nx(kk["P"])
        self.act(P[:, 0:nk], Ssb, AF.Exp, bias=st[:, 1:2], accum=st[:, 2:3])
        self.ts(st[:, 3:4], st[:, 2:3], 1e-30, None, ALU.max)
        self.recip(st[:, 4:5], st[:, 3:4])
        po = self.next_ps()
        nkt = nk // 128
        cnt = 0
        for g0 in range(0, nkt, 8):
            gn = min(8, nkt - g0)
            pb = self.next_psb()
            for j in range(gn):
                self.tr(pb[:, j * 128:(j + 1) * 128], P[:, (g0 + j) * 128:(g0 + j + 1) * 128], self.ident[:])
            PT = self.nx(kk["PT"])
            self.copy(PT[:, 0:gn * 128], pb[:, 0:gn * 128], eng=("act" if (cnt % 2) else "dve"))
            cnt += 1
            for j in range(gn):
                self.mm(po[:, 0:64], PT[:, j * 128:(j + 1) * 128], Vt[:, kt0 + g0 + j, :],
                        start=(g0 + j == 0), stop=(g0 + j == nkt - 1))
        self.ts(out, po[:, 0:64], st[:, 4:5], None, ALU.mult)

    def attn_keys(self):
        S = self.S
        return dict(st=self.rot("sst", 2, [128, 8], F32), P=self.rot("P", 2, [128, S], BF16),
                    PT=self.rot("PTa", 2, [128, 1024], BF16))

    def phase_dsa(self, FMS, TMB, TMF, YT):
        S = self.S
        NT = S // 128
        self.phase_begin()
        dq = self.sb("dq", [128, 2, S], BF16)
        self.dma(dq[:], self.fm_rows(FMS, 2, 2, 0, S))
        dk = self.sb("dk", [128, S], BF16)
        self.dma(dk[:], V(FMS.h[4 * 128:5 * 128, :], FMS.name))
        iq = self.sb("iq", [128, 2, S], BF16)
        self.dma(iq[:], self.fm_rows(FMS, 5, 2, 0, S))
        ikr = self.sb("ikr", [128, S], BF16)
        self.dma(ikr[:], V(FMS.h[7 * 128:8 * 128, :], FMS.name))
        ikm = self.sb("ikm", [128, 4, S], BF16)
        for g in range(4):
            self.ts(ikm[:, g, :], ikr[:], self.meta[:, 13 + g:14 + g], None, ALU.mult, eng=("pool" if g % 2 else "dve"))
        Vt = self.sb("Vt", [128, NT, 64], BF16)
        self.dma(Vt[:], V(TMB.h[:, 512:576].rearrange("(n p) c -> p n c", p=128), TMB.name))
        iw = self.sb("iw", [128, NT, 8], F32)
        self.dma(iw[:], V(TMF.h[:, 0:8].rearrange("(n p) c -> p n c", p=128), TMF.name))
        absw = self.sb("absw", [128, NT, 8], F32)
        self.act(absw[:], iw[:], AF.Abs, scale=1.0 / 16)
        sgn = self.sb("sgn", [128, NT, 8], F32)
        self.ts(sgn[:], iw[:], 0.0, 2.0, ALU.is_ge, ALU.mult)
        self.ts(sgn[:], sgn[:], -1.0, None, ALU.add)
        I = self.sb("I", [128, S], F32)
        Ssb = self.sb("Ssb", [128, S], F32)
        junk = self.sb("junkc", [128, S], BF16)
        kk = self.attn_keys()
        tmp_k = self.rot("tmpr", 3, [128, 512], F32)
        st_k = self.rot("dst", 2, [128, 8], F32)
        Rk_k = self.rot("Rk", 2, [128, 20], F32)
        o_k = self.rot("o", 2, [128, 256], F32)
        y_k = self.rot("y", 2, [128, 256], BF16)
        yTs_k = self.rot("yTs", 2, [128, 256], BF16)
        for i in range(NT):
            nk = 128 * (i + 1)
            qs = slice(i * 128, (i + 1) * 128)
            nkc = (nk + 511) // 512
            for kc in range(nkc):
                c0 = kc * 512
                cols = min(512, nk - c0)
                for h in range(8):
                    ps = self.next_ps()
                    self.mm(ps[:, 0:cols], iq[:, h // 4, qs], ikm[:, h % 4, c0:c0 + cols])
                    tmp = self.nx(tmp_k)
                    self.act(tmp[:, 0:cols], ps[:, 0:cols], AF.Relu, scale=absw[:, i, h:h + 1])
                    if h == 0:
                        self.ts(I[:, c0:c0 + cols], tmp[:, 0:cols], sgn[:, i, 0:1], None, ALU.mult)
                    else:
                        self.stt(I[:, c0:c0 + cols], tmp[:, 0:cols], sgn[:, i, h:h + 1], I[:, c0:c0 + cols], ALU.mult, ALU.add)
            if nk > self.n_keep:
                st = self.nx(st_k)
                self.redabs(st[:, 0:1], I[:, 0:nk])
                self.ts(st[:, 0:1], st[:, 0:1], 1e-20, None, ALU.max)
                self.tt(I[:, nk - 128:nk], I[:, nk - 128:nk], self.cneg30[:], ALU.add)
                Rk = self.nx(Rk_k)
                self.ts(Rk[:], self.rkc[:], st[:, 0:1], None, ALU.mult)
                self.ts(st[:, 1:2], st[:, 0:1], -1.0, None, ALU.mult)
                for k in range(20):
                    self.tt(st[:, 2:3], st[:, 1:2], Rk[:, k:k + 1], ALU.add)
                    self.ts(junk[:, 0:nk], I[:, 0:nk], st[:, 2:3], None, ALU.is_ge, ALU.add, accum=st[:, 3:4])
                    self.ts(st[:, 4:5], st[:, 3:4], self.n_keep - 0.5, None, ALU.is_ge)
                    self.stt(st[:, 1:2], st[:, 4:5], Rk[:, k:k + 1], st[:, 1:2], ALU.mult, ALU.add)
                self.ts(I[:, 0:nk], I[:, 0:nk], st[:, 1:2], 1000.0, ALU.is_ge, ALU.mult)
            else:
                self.ts(I[:, 0:nk], I[:, 0:nk], 0.0, 1000.0, ALU.mult, ALU.add)
                self.tt(I[:, nk - 128:nk], I[:, nk - 128:nk], self.cneg2k[:], ALU.add)
            o = self.nx(o_k)
            for h in range(4):
                base = 64 * (h % 2)
                c = h // 2
                for kc in range(nkc):
                    c0 = kc * 512
                    cols = min(512, nk - c0)
                    ps = self.next_ps()
                    self.mm(ps[:, 0:cols], dq[base:base + 64, c, qs], dk[base:base + 64, c0:c0 + cols])
                    self.stt(Ssb[:, c0:c0 + cols], ps[:, 0:cols], 0.125, I[:, c0:c0 + cols], ALU.mult, ALU.add)
                self.softmax_pv(Ssb[:, 0:nk], nk, Vt, 0, o[:, 64 * h:64 * h + 64], kk)
            y = self.nx(y_k)
            self.copy(y[:], o[:], eng="act")
            self.store_yT(y, YT, 1, i, yTs_k)

    def phase_nsa(self, FMS, TMB, TMF, cmp_w1, cmp_w2, cmp_pos, YT):
        S = self.S
        NT = S // 128
        NB = self.NB
        NCP = self.NCP
        NC = (S - 32) // 16 + 1
        NCT = NCP // 128
        self.phase_begin()
        nq = self.sb("nq", [128, 2, S], BF16)
        self.dma(nq[:], self.fm_rows(FMS, 8, 2, 0, S))
        kcT = self.sb("kcT", [128, S], BF16)
        self.dma(kcT[:], V(FMS.h[10 * 128:11 * 128, :], FMS.name))
        ksT = self.sb("ksT", [128, S], BF16)
        self.dma(ksT[:], V(FMS.h[11 * 128:12 * 128, :], FMS.name))
        kwT = self.sb("kwT", [128, S], BF16)
        self.dma(kwT[:], V(FMS.h[12 * 128:13 * 128, :], FMS.name))
        vcT = self.sb("vcT", [128, S], BF16)
        self.dma(vcT[:], V(FMS.h[13 * 128:14 * 128, :], FMS.name))
        Vs = self.sb("Vs", [128, NT, 64], BF16)
        self.dma(Vs[:], V(TMB.h[:, 576:640].rearrange("(n p) c -> p n c", p=128), TMB.name))
        Vw = self.sb("Vw", [128, NT, 64], BF16)
        self.dma(Vw[:], V(TMB.h[:, 640:704].rearrange("(n p) c -> p n c", p=128), TMB.name))
        ngt = self.sb("ngt", [128, NT, 12], F32)
        self.dma(ngt[:], V(TMF.h[:, 8:20].rearrange("(n p) c -> p n c", p=128), TMF.name))
        kcmp = self.sb("kcmp", [128, NCP], BF16)
        vcmp = self.sb("vcmp", [128, NCT, 64], BF16)
        w1 = self.sb("w1", [64, 32, 64], BF16)
        w2 = self.sb("w2", [64, 128], BF16)
        posT = self.sb("posT", [64, 32], F32)
        posb = self.sb("posb", [64, 32], BF16)
        cst = self.sb("cst", [64, 1], F32)
        u = self.sb("u", [64, NCP], F32)
        u2 = self.sb("u2", [64, NCP], F32)
        gl = self.sb("gl", [64, NCP], BF16)
        for i, src in ((0, kcT), (1, vcT)):
            self.dma(w1[:], V(cmp_w1.h[i].rearrange("(l d) f -> d l f", d=64), cmp_w1.name), eng="pool")
            self.dma(w2[:, 0:64], V(cmp_w2.h[i], cmp_w2.name), eng="pool")
            self.dma(w2[:, 64:128], V(cmp_w2.h[i], cmp_w2.name), eng="pool")
            self.dma_s(posT[:], V(cmp_pos.h[i].rearrange("l d -> d l"), cmp_pos.name))
            self.copy(posb[:], posT[:])
            psc = self.next_ps()
            for l in range(32):
                self.mm(psc[0:64, 0:1], w1[:, l, :], posb[:, l:l + 1], start=(l == 0), stop=(l == 31))
            self.copy(cst[:], psc[0:64, 0:1])
            psh = self.next_ps()
            for l in range(32):
                self.mm(psh[0:64, 0:NC], w1[:, l, :], src[0:64, l:l + 16 * (NC - 1) + 1:16], start=(l == 0), stop=(l == 31))
            self.memset(u[:], 0.0)
            self.act(u[:, 0:NC], psh[0:64, 0:NC], AF.Identity, bias=cst[:, 0:1])
            self.tt(u2[:], u[:], u[:], ALU.mult)
            self.tt(u2[:], u2[:], u[:], ALU.mult)
            self.stt(u2[:], u2[:], 0.044715, u[:], ALU.mult, ALU.add)
            self.act(u2[:], u2[:], AF.Tanh, scale=0.7978845608028654)
            self.ts(u2[:], u2[:], 1.0, 0.5, ALU.add, ALU.mult)
            self.tt(gl[:], u2[:], u[:], ALU.mult)
            if i == 0:
                pso = self.next_ps()
                self.mm(pso[:, 0:NCP], w2[:, :], gl[:, :])
                self.copy(kcmp[:], pso[:, 0:NCP])
            else:
                for ct in range(NCT):
                    pso = self.next_ps()
                    self.mm(pso[:, 0:64], gl[:, ct * 128:(ct + 1) * 128], w2[:, 0:64])
                    self.copy(vcmp[:, ct, :], pso[:, 0:64])
        self.build_addmask()
        Ssb = self.sb("Ssb", [128, S], F32)
        Sw = self.sb("Sw", [128, 640], F32)
        kk = self.attn_keys()
        vis_k = self.rot("vis", 2, [128, NCP], F32)
        pns_k = self.rot("pns", 2, [128, NCP], F32)
        pn_k = self.rot("pn", 2, [128, NCP], F32)
        Sc_k = self.rot("Sc", 2, [128, NCP], F32)
        Pc_k = self.rot("Pc", 2, [128, NCP], F32)
        pnb_k = self.rot("pnb", 2, [128, NCP], BF16)
        PTc_k = self.rot("PTc", 2, [128, NCP], BF16)
        pnT_k = self.rot("pnT", 2, [128, NCP], F32)
        cst_k = self.rot("cst2", 2, [128, 8], F32)
        imp_k = self.rot("imp", 2, [128, NB], F32)
        imp2_k = self.rot("imp2", 2, [128, NB], F32)
        m8_k = self.rot("m8", 2, [128, 16], F32)
        selm_k = self.rot("selm", 2, [128, NB], F32)
        oc_k = self.rot("oc", 2, [128, 256], F32)
        os_k = self.rot("os", 2, [128, 256], F32)
        ow_k = self.rot("ow", 2, [128, 256], F32)
        gs_k = self.rot("gs", 2, [128, 12], F32)
        o_k = self.rot("o", 2, [128, 256], F32)
        y_k = self.rot("y", 2, [128, 256], BF16)
        yTs_k = self.rot("yTs", 2, [128, 256], BF16)

        def h4(t_):
            return V(t_.h[:, 0:256].rearrange("p (h e) -> p h e", h=4), t_.name)

        for i in range(NT):
            nk = 128 * (i + 1)
            qs = slice(i * 128, (i + 1) * 128)
            nkc = (nk + 511) // 512
            vis = self.nx(vis_k)
            self.memset(vis[:], 0.0)
            self.aselect(vis[:], vis[:], [[-16, NCP]], ALU.is_ge, -1000.0, 128 * i - 31, 1)
            pns = self.nx(pns_k)
            oc = self.nx(oc_k)
            osl = self.nx(os_k)
            ow = self.nx(ow_k)
            for h in range(4):
                base = 64 * (h % 2)
                c = h // 2
                ps = self.next_ps()
                self.mm(ps[:, 0:NCP], nq[base:base + 64, c, qs], kcmp[base:base + 64, :])
                Sc = self.nx(Sc_k)
                self.stt(Sc[:], ps[:, 0:NCP], 0.125, vis[:], ALU.mult, ALU.add)
                st = self.nx(cst_k)
                self.red(st[:, 0:1], Sc[:], ALU.max)
                self.ts(st[:, 0:1], st[:, 0:1], -500.0, -1.0, ALU.max, ALU.mult)
                Pc = self.nx(Pc_k)
                self.act(Pc[:], Sc[:], AF.Exp, bias=st[:, 0:1], accum=st[:, 1:2])
                self.ts(st[:, 2:3], st[:, 1:2], 1e-30, None, ALU.max)
                self.recip(st[:, 3:4], st[:, 2:3])
                pn = pns if h == 0 else self.nx(pn_k)
                self.ts(pn[:], Pc[:], st[:, 3:4], None, ALU.mult)
                pnb = self.nx(pnb_k)
                self.copy(pnb[:], pn[:], eng="act")
                if h > 0:
                    self.tt(pns[:], pns[:], pn[:], ALU.add, eng="pool")
                pb = self.next_psb()
                for ct in range(NCT):
                    self.tr(pb[:, ct * 128:(ct + 1) * 128], pnb[:, ct * 128:(ct + 1) * 128], self.ident[:])
                PTc = self.nx(PTc_k)
                self.copy(PTc[:], pb[:, 0:NCP])
                po = self.next_ps()
                for ct in range(NCT):
                    self.mm(po[:, 0:64], PTc[:, ct * 128:(ct + 1) * 128], vcmp[:, ct, :], start=(ct == 0), stop=(ct == NCT - 1))
                self.copy(oc[:, 64 * h:64 * h + 64], po[:, 0:64], eng="act")
            selm = self.nx(selm_k)
            if NB > 16:
                pf = self.next_ps()
                for ct in range(NCT):
                    self.tr(pf[:, ct * 128:(ct + 1) * 128], pns[:, ct * 128:(ct + 1) * 128], self.ident_f[:])
                pnT = self.nx(pnT_k)
                self.copy(pnT[:], pf[:, 0:NCP])
                pi = self.next_ps()
                for ct in range(NCT):
                    self.mm(pi[:, 0:NB], pnT[:, ct * 128:(ct + 1) * 128], self.ovl[:, ct, :], start=(ct == 0), stop=(ct == NCT - 1))
                imp = self.nx(imp_k)
                self.tt(imp[:], pi[:, 0:NB], self.addmask[:, i, :], ALU.add)
                m8 = self.nx(m8_k)
                self.vmax(m8[:, 0:8], imp[:])
                imp2 = self.nx(imp2_k)
                self.match_replace(imp2[:], m8[:, 0:8], imp[:], -3.0e38)
                self.vmax(m8[:, 8:16], imp2[:])
                self.ts(selm[:], imp[:], m8[:, 15:16], 1000.0, ALU.is_ge, ALU.mult)
            else:
                self.memset(selm[:], 1000.0)
            for h in range(4):
                base = 64 * (h % 2)
                c = h // 2
                for kc in range(nkc):
                    c0 = kc * 512
                    cols = min(512, nk - c0)
                    nb_ = cols // 64
                    ps = self.next_ps()
                    self.mm(ps[:, 0:cols], nq[base:base + 64, c, qs], ksT[base:base + 64, c0:c0 + cols])
                    self.stt(V(Ssb.h[:, c0:c0 + cols].rearrange("p (b e) -> p b e", e=64), Ssb.name),
                             V(ps.h[:, 0:cols].rearrange("p (b e) -> p b e", e=64), ps.name), 0.125,
                             V(selm.h[:, c0 // 64:c0 // 64 + nb_].unsqueeze(2).to_broadcast([128, nb_, 64]), selm.name),
                             ALU.mult, ALU.add)
                self.tt(Ssb[:, nk - 128:nk], Ssb[:, nk - 128:nk], self.cneg2k[:], ALU.add)
                self.softmax_pv(Ssb[:, 0:nk], nk, Vs, 0, osl[:, 64 * h:64 * h + 64], kk)
            k0 = max(0, i * 128 - 512)
            nkw = nk - k0
            boff = 640 - nkw
            for h in range(4):
                base = 64 * (h % 2)
                c = h // 2
                for c0 in range(0, nkw, 512):
                    cols = min(512, nkw - c0)
                    ps = self.next_ps()
                    self.mm(ps[:, 0:cols], nq[base:base + 64, c, qs], kwT[base:base + 64, k0 + c0:k0 + c0 + cols])
                    self.stt(Sw[:, c0:c0 + cols], ps[:, 0:cols], 0.125, self.band[:, boff + c0:boff + c0 + cols], ALU.mult, ALU.add)
                self.softmax_pv(Sw[:, 0:nkw], nkw, Vw, k0 // 128, ow[:, 64 * h:64 * h + 64], kk)
            gs = self.nx(gs_k)
            self.act(gs[:], ngt[:, i, :], AF.Sigmoid)
            o = self.nx(o_k)

            def gbc(j):
                return V(gs.h[:, j:12:3].unsqueeze(2).to_broadcast([128, 4, 64]), gs.name)
            self.tt(h4(o), h4(oc), gbc(0), ALU.mult)
            self.tt(h4(osl), h4(osl), gbc(1), ALU.mult)
            self.tt(o[:], o[:], osl[:], ALU.add)
            self.tt(h4(ow), h4(ow), gbc(2), ALU.mult)
            self.tt(o[:], o[:], ow[:], ALU.add)
            y = self.nx(y_k)
            self.copy(y[:], o[:], eng="act")
            self.store_yT(y, YT, 2, i, yTs_k)

    def phase_merge(self, x_in, xT_in, w_in_l, w_branch, w_out, ln_g, ln_b, YT, x_out, xT_out):
        S = self.S
        self.phase_begin()
        wg = self.load_w("wg", lambda k: V(w_in_l.h[k * 128:(k + 1) * 128, 3128:7224], w_in_l.name), NKC, 4096)
        wb = self.sb("wb", [128, 4, 2, 1024], BF16)
        for n in range(4):
            for kk_ in range(2):
                self.dma(wb[:, n, kk_, :], V(w_branch.h[n, kk_ * 128:(kk_ + 1) * 128, :], w_branch.name), eng="pool")
        wo = self.load_w("wo", lambda k: V(w_out.h[k * 128:(k + 1) * 128, :], w_out.name), NKC, D)
        g_bc, b_bc, scr = self.ln_setup(ln_g, ln_b)
        xt = self.sb("xT", [128, NKC, 512], BF16)
        yt = self.sb("yT", [128, 8, 512], BF16)
        mT = self.sb("mT", [128, 8, 512], BF16)
        acc_k = self.rot("acc", 2, [128, 512], F32)
        sg_k = self.rot("sg", 2, [128, 512], F32)
        tmp_k = self.rot("tmp", 2, [128, 512], F32)
        xr = [self.sb("xr%d" % i, [128, D], F32) for i in range(2)]
        rq = self.sb("r", [128, D], F32)
        for t in range(S // 512):
            ss = slice(t * 512, (t + 1) * 512)
            self.dma(xt[:], V(xT_in.h.rearrange("(k p) s -> p k s", p=128)[:, :, ss], xT_in.name))
            self.dma(yt[:], V(YT.h[:, ss].rearrange("(c p) s -> p c s", p=128), YT.name))
            for dc in range(8):
                acc = self.nx(acc_k)
                for n in range(4):
                    pg = self.next_ps()
                    for k in range(NKC):
                        self.mm(pg[:], wg[:, k, n * 1024 + dc * 128:n * 1024 + (dc + 1) * 128], xt[:, k, :],
                                start=(k == 0), stop=(k == NKC - 1))
                    pp = self.next_ps()
                    for k2 in range(2):
                        self.mm(pp[:], wb[:, n, k2, dc * 128:(dc + 1) * 128], yt[:, 2 * n + k2, :], start=(k2 == 0), stop=(k2 == 1))
                    sg = self.nx(sg_k)
                    self.act(sg[:], pg[:], AF.Sigmoid)
                    if n == 0:
                        self.tt(acc[:], sg[:], pp[:], ALU.mult)
                    else:
                        tmp = self.nx(tmp_k)
                        self.tt(tmp[:], sg[:], pp[:], ALU.mult)
                        self.tt(acc[:], acc[:], tmp[:], ALU.add, eng="pool")
                self.copy(mT[:, dc, :], acc[:], eng="act")
            for q in range(4):
                t0 = t * 512 + q * 128
                xq = xr[q % 2]
                self.dma(xq[:], x_in[t0:t0 + 128, :])
                for half in range(2):
                    hs = slice(half * 512, (half + 1) * 512)
                    pd = self.next_ps()
                    for dc in range(8):
                        self.mm(pd[:], mT[:, dc, q * 128:(q + 1) * 128], wo[:, dc, hs], start=(dc == 0), stop=(dc == 7))
                    self.act(xq[:, hs], xq[:, hs], AF.Copy, scale=ALPHA)
                    self.stt(rq[:, hs], pd[:], 1.0, xq[:, hs], ALU.mult, ALU.add)
                self.finish_tile(rq, g_bc, b_bc, x_out, xT_out, t0, scr)

    def phase_xattn(self, x_in, xT_in, mem, wq_d, wkv_d, wo_d, ln_g, ln_b, x_out, xT_out):
        S = self.S
        self.phase_begin()
        wq = self.load_w("wq", lambda k: V(wq_d.h[k * 128:(k + 1) * 128, :], wq_d.name), NKC, D)
        wkv = self.load_w("wkv", lambda k: V(wkv_d.h[k * 128:(k + 1) * 128, :], wkv_d.name), NKC, 2 * D)
        wo = self.load_w("wo", lambda k: V(wo_d.h[k * 128:(k + 1) * 128, :], wo_d.name), NKC, D)
        g_bc, b_bc, scr = self.ln_setup(ln_g, ln_b)
        memT = self.sb("memT", [128, 8, 256], BF16)
        mr = self.sb("mr", [128, D], F32)
        mb = self.sb("mb", [128, D], BF16)
        for mt in range(2):
            self.dma(mr[:], mem[mt * 128:(mt + 1) * 128, :])
            self.copy(mb[:], mr[:], eng="act")
            pb = self.next_psb()
            for k in range(8):
                self.tr(pb[:, k * 128:(k + 1) * 128], mb[:, k * 128:(k + 1) * 128], self.ident[:])
            self.copy(memT[:, :, mt * 128:(mt + 1) * 128], V(pb.h[:, :].rearrange("p (k t) -> p k t", k=8), pb.name))
        KT = self.sb("KT", [128, 8, 256], BF16)
        for c in range(8):
            ps = self.next_ps()
            for k in range(NKC):
                self.mm(ps[:, 0:256], wkv[:, k, c * 128:(c + 1) * 128], memT[:, k, :], start=(k == 0), stop=(k == NKC - 1))
            self.copy(KT[:, c, :], ps[:, 0:256], eng=("act" if c % 2 else "dve"))
        Vm = self.sb("Vm", [128, 2, D], BF16)
        for mt in range(2):
            for half in range(2):
                ps = self.next_ps()
                for k in range(NKC):
                    self.mm(ps[:], memT[:, k, mt * 128:(mt + 1) * 128], wkv[:, k, D + half * 512:D + (half + 1) * 512],
                            start=(k == 0), stop=(k == NKC - 1))
                self.copy(Vm[:, mt, half * 512:(half + 1) * 512], ps[:], eng=("act" if half else "dve"))
        xt = self.sb("xT", [128, NKC, 512], BF16)
        qT = self.sb("qT", [128, 8, 512], BF16)
        Pf_k = self.rot("Pf", 2, [128, 4, 256], F32)
        Pb_k = self.rot("Pb", 2, [128, 4, 256], BF16)
        PT_k = self.rot("PTx", 2, [128, 8, 128], BF16)
        oT_k = self.rot("oT", 2, [128, 8, 128], BF16)
        st_k = self.rot("xst", 2, [128, 16], F32)
        xr = [self.sb("xr%d" % i, [128, D], F32) for i in range(2)]
        rq = self.sb("r", [128, D], F32)
        SC = 1.0 / 16
        for t in range(S // 512):
            ss = slice(t * 512, (t + 1) * 512)
            self.dma(xt[:], V(xT_in.h.rearrange("(k p) s -> p k s", p=128)[:, :, ss], xT_in.name))
            for c in range(8):
                ps = self.next_ps()
                for k in range(NKC):
                    self.mm(ps[:], wq[:, k, c * 128:(c + 1) * 128], xt[:, k, :], start=(k == 0), stop=(k == NKC - 1))
                self.copy(qT[:, c, :], ps[:], eng=("act" if c % 2 else "dve"))
            for q in range(4):
                t0 = t * 512 + q * 128
                tq = slice(q * 128, (q + 1) * 128)
                pss = [self.next_ps(), self.next_ps()]
                st = self.nx(st_k)
                Pf = self.nx(Pf_k)
                for h in range(4):
                    pv = pss[h // 2][:, (h % 2) * 256:(h % 2) * 256 + 256]
                    for cc in range(2):
                        self.mm(pv, qT[:, 2 * h + cc, tq], KT[:, 2 * h + cc, :], start=(cc == 0), stop=(cc == 1))
                    self.red(st[:, h:h + 1], pv, ALU.max)
                    self.ts(st[:, 4 + h:5 + h], st[:, h:h + 1], -SC, None, ALU.mult)
                    self.act(Pf[:, h, :], pv, AF.Exp, bias=st[:, 4 + h:5 + h], scale=SC, accum=st[:, 8 + h:9 + h])
                self.recip(st[:, 12:16], st[:, 8:12])
                Pb = self.nx(Pb_k)
                self.tt(Pb[:], Pf[:], V(st.h[:, 12:16].unsqueeze(2).to_broadcast([128, 4, 256]), st.name), ALU.mult)
                pb = self.next_psb()
                for h in range(4):
                    for mc in range(2):
                        j = 2 * h + mc
                        self.tr(pb[:, j * 128:(j + 1) * 128], Pb[:, h, mc * 128:(mc + 1) * 128], self.ident[:])
                PT = self.nx(PT_k)
                self.copy(PT[:], V(pb.h[:, :].rearrange("p (j t) -> p j t", j=8), pb.name))
                oT = self.nx(oT_k)
                pso = [self.next_ps(), self.next_ps()]
                for h in range(4):
                    for dc in range(2):
                        j = 2 * h + dc
                        pv = pso[j // 4][:, (j % 4) * 128:(j % 4) * 128 + 128]
                        for mc in range(2):
                            self.mm(pv, Vm[:, mc, h * 256 + dc * 128:h * 256 + (dc + 1) * 128], PT[:, 2 * h + mc, :],
                                    start=(mc == 0), stop=(mc == 1))
                for j4 in range(2):
                    self.copy(oT[:, 4 * j4:4 * j4 + 4, :], V(pso[j4].h[:, :].rearrange("p (j t) -> p j t", j=4), pso[j4].name),
                              eng=("act" if j4 else "dve"))
                xq = xr[q % 2]
                self.dma(xq[:], x_in[t0:t0 + 128, :])
                for half in range(2):
                    hs = slice(half * 512, (half + 1) * 512)
                    pd = self.next_ps()
                    for c in range(8):
                        self.mm(pd[:], oT[:, c, :], wo[:, c, hs], start=(c == 0), stop=(c == 7))
                    self.act(xq[:, hs], xq[:, hs], AF.Copy, scale=ALPHA)
                    self.stt(rq[:, hs], pd[:], 1.0, xq[:, hs], ALU.mult, ALU.add)
                self.finish_tile(rq, g_bc, b_bc, x_out, xT_out, t0, scr)


OFF = dict(r_q=0, r_k=128, r_v=256, r_g=512, d_q=768, d_k=1024, d_v=1088, i_q=1152, i_k=1408, i_w=1440,
           n_q=1448, n_kc=1704, n_vc=1768, n_ks=1832, n_vs=1896, n_kw=1960, n_vw=2024, n_g=2088,
           s_z=2100, s_xbc=2356, s_dt=3124, br_g=3128)


def _partner(i, headdim, rot):
    half = rot // 2
    j = i % headdim
    b = i - j
    if j < half:
        return b + j + half
    if j < rot:
        return b + j - half
    return i


def build_colidx():
    cols = []

    def roped(name, width, headdim, rot, lo=0, rep=1):
        loc = []
        for r in range(rep):
            loc += list(range(lo, lo + width))
        assert len(loc) == 128
        a = [OFF[name] + i for i in loc]
        b = [OFF[name] + _partner(i, headdim, rot) for i in loc]
        cols.extend(a)
        cols.extend(b)

    roped("r_q", 128, 32, 32)
    roped("r_k", 128, 32, 32)
    roped("d_q", 128, 64, 16, 0)
    roped("d_q", 128, 64, 16, 128)
    roped("d_k", 64, 64, 16, 0, 2)
    roped("i_q", 128, 32, 8, 0)
    roped("i_q", 128, 32, 8, 128)
    roped("i_k", 32, 32, 8, 0, 4)
    roped("n_q", 128, 64, 16, 0)
    roped("n_q", 128, 64, 16, 128)
    roped("n_kc", 64, 64, 16, 0, 2)
    roped("n_ks", 64, 64, 16, 0, 2)
    roped("n_kw", 64, 64, 16, 0, 2)
    cols.extend([OFF["n_vc"] + i for i in range(64)] * 2)
    cols.extend([OFF["s_xbc"] + i for i in range(768)])
    for name, w in (("r_v", 256), ("r_g", 256), ("d_v", 64), ("n_vs", 64), ("n_vw", 64), ("s_z", 256),
                    ("i_w", 8), ("n_g", 12), ("s_dt", 4)):
        cols.extend([OFF[name] + i for i in range(w)])
    return np.asarray(cols, dtype=np.int64)


ROPED_TABLES = [0, 1, 2, 2, 2, 3, 3, 3, 2, 2, 2, 2, 2]
TM0 = (2 * len(ROPED_TABLES) + 7) * 128
NCOL2 = TM0 + 984
RET_LNG = [math.log1p(-2.0 ** (-5 - h)) for h in range(4)]


def host_consts(S):
    meta = np.zeros((128, 32), np.float32)

    def fill(t, headdim, rot, theta, scale):
        half = rot // 2
        inv = np.power(np.float32(theta), (-2.0 * np.arange(half, dtype=np.float32) / np.float32(rot)).astype(np.float32)).astype(np.float32)
        for p in range(128):
            i = p % headdim
            if i < rot:
                meta[p, t] = inv[i % half]
                meta[p, 4 + t] = scale
                meta[p, 8 + t] = -scale if i < half else scale
            else:
                meta[p, t] = 0.0
                meta[p, 4 + t] = 1.0
                meta[p, 8 + t] = 0.0

    fill(0, 32, 32, 10000.0, 1.0)
    fill(1, 32, 32, 10000.0, 32.0 ** -0.5)
    fill(2, 64, 16, 500000.0, 1.0)
    fill(3, 32, 8, 500000.0, 1.0)
    for p in range(128):
        meta[p, 12] = RET_LNG[p // 32]
        meta[p, 13 + p // 32] = 1.0
    bdm = np.zeros((128, 256), np.float32)
    for p in range(128):
        bdm[p, 64 * (p // 32):64 * (p // 32) + 64] = 1.0
    NC = (S - 32) // 16 + 1
    NCP = (NC + 127) // 128 * 128
    NB = S // 64
    ovl = np.zeros((NCP, NB), np.float32)
    for c in range(NC):
        for j in range(NB):
            ovl[c, j] = max(min(16 * c + 32, 64 * j + 64) - max(16 * c, 64 * j), 0) / 32.0
    return meta, bdm, ovl, NB, NCP


STAGES = ["ffn1", "inproj", "ret", "ssd", "dsa", "nsa", "merge", "xattn", "ffn2"]


def build(S, depth=DEPTH, stop_after=None):
    kb = KB(S, depth, stop_after)
    meta_np, bdm_np, ovl_np, NB, NCP = host_consts(S)
    kb.n_keep = min(256, S // 4)
    EI = "ExternalInput"
    x = kb.dram("x", [S, D], F32, kind=EI)
    mem = kb.dram("mem", [N_MEM, D], F32, kind=EI)
    ln_g = kb.dram("ln_g", [DEPTH, 4, D], F32, kind=EI)
    ln_b = kb.dram("ln_b", [DEPTH, 4, D], F32, kind=EI)
    f1gu = kb.dram("ffn1_w_gu", [DEPTH, D, 2 * DFF], F32, kind=EI)
    f1dn = kb.dram("ffn1_w_down", [DEPTH, DFF, D], F32, kind=EI)
    w_in = kb.dram("w_in", [DEPTH, D, 7224], F32, kind=EI)
    w2 = kb.dram("w2", [DEPTH, D, NCOL2], F32, kind=EI)
    cmp_w1 = kb.dram("cmp_w1", [DEPTH, 2, 2048, 64], F32, kind=EI)
    cmp_w2 = kb.dram("cmp_w2", [DEPTH, 2, 64, 64], F32, kind=EI)
    cmp_pos = kb.dram("cmp_pos", [DEPTH, 2, 32, 64], F32, kind=EI)
    conv_w = kb.dram("conv_w", [DEPTH, 4, 768], F32, kind=EI)
    conv_b = kb.dram("conv_b", [DEPTH, 768], F32, kind=EI)
    dt_bias = kb.dram("dt_bias", [DEPTH, 4], F32, kind=EI)
    a_log = kb.dram("a_log", [DEPTH, 4], F32, kind=EI)
    d_skip = kb.dram("d_skip", [DEPTH, 4], F32, kind=EI)
    norm_g = kb.dram("ssm_norm_g", [DEPTH, 256], F32, kind=EI)
    w_branch = kb.dram("w_branch", [DEPTH, 4, 256, D], F32, kind=EI)
    w_out = kb.dram("w_out", [DEPTH, D, D], F32, kind=EI)
    xwq = kb.dram("xattn_wq", [DEPTH, D, D], F32, kind=EI)
    xwkv = kb.dram("xattn_wkv", [DEPTH, D, 2 * D], F32, kind=EI)
    xwo = kb.dram("xattn_wo", [DEPTH, D, D], F32, kind=EI)
    f2gu = kb.dram("ffn2_w_gu", [DEPTH, D, 2 * DFF], F32, kind=EI)
    f2dn = kb.dram("ffn2_w_down", [DEPTH, DFF, D], F32, kind=EI)
    meta = kb.dram("meta", [128, 32], F32, kind=EI)
    bdm = kb.dram("bdm", [128, 256], F32, kind=EI)
    ovl = kb.dram("ovl", [NCP, NB], F32, kind=EI)
    out = kb.dram("out", [S, D], F32)
    xTa = kb.dram("xTa", [D, S], BF16)
    xTb = kb.dram("xTb", [D, S], BF16)
    xa = kb.dram("xa", [S, D], F32)
    xb2 = kb.dram("xb2", [S, D], F32)
    FMS = kb.dram("FMS", [20 * 128, S], BF16)
    TMB = kb.dram("TMB", [S, 960], BF16)
    TMF = kb.dram("TMF", [S, 24], F32)
    YT = kb.dram("YT", [1024, S], BF16)
    ROPE = kb.dram("ROPE", [4, 2, 128, S], F32)
    kb.setup()
    kb.setup_consts(meta, bdm, ovl, NB, NCP)
    kb.phase_rope(ROPE)
    kb.phase_transpose_in(x, xTa)

    def L(t, *idx):
        return T(t.h[idx], t.name, True)

    done = False
    xin = x
    for l in range(depth):
        last = (l == depth - 1)

        def stop(name):
            return stop_after == (l, name)
        kb.phase_ffn(xin, xTa, L(f1gu, l), L(f1dn, l), L(ln_g, l, 0), L(ln_b, l, 0), xa, xTb)
        if stop("ffn1"):
            break
        kb.phase_inproj(xTb, L(w2, l), ROPE, FMS, TMB, TMF)
        if stop("inproj"):
            break
        kb.phase_ret(FMS, TMB, YT)
        if stop("ret"):
            break
        kb.phase_ssd(FMS, TMB, TMF, L(conv_w, l), L(conv_b, l), L(dt_bias, l), L(a_log, l), L(d_skip, l), L(norm_g, l), YT)
        if stop("ssd"):
            break
        kb.phase_dsa(FMS, TMB, TMF, YT)
        if stop("dsa"):
            break
        kb.phase_nsa(FMS, TMB, TMF, L(cmp_w1, l), L(cmp_w2, l), L(cmp_pos, l), YT)
        if stop("nsa"):
            break
        kb.phase_merge(xa, xTb, L(w_in, l), L(w_branch, l), L(w_out, l), L(ln_g, l, 1), L(ln_b, l, 1), YT, xb2, xTa)
        if stop("merge"):
            break
        kb.phase_xattn(xb2, xTa, mem, L(xwq, l), L(xwkv, l), L(xwo, l), L(ln_g, l, 2), L(ln_b, l, 2), xa, xTb)
        if stop("xattn"):
            break
        kb.phase_ffn(xa, xTb, L(f2gu, l), L(f2dn, l), L(ln_g, l, 3), L(ln_b, l, 3), out if last else xb2, None if last else xTa)
        if stop("ffn2"):
            break
        xin = xb2
    st = kb.P.emit()
    kb.stats = st
    return kb


def make_in_maps(inputs, S, ncores):
    meta_np, bdm_np, ovl_np, NB, NCP = host_consts(S)
    colidx = build_colidx()
    w_in = np.asarray(inputs["w_in"], dtype=np.float32)
    w2 = np.ascontiguousarray(w_in[:, :, colidx])
    shared = {k: np.ascontiguousarray(np.asarray(v, dtype=np.float32)) for k, v in inputs.items() if k not in ("x", "mem")}
    shared["w2"] = w2
    shared["meta"] = meta_np
    shared["bdm"] = bdm_np
    shared["ovl"] = ovl_np
    maps = []
    for b in range(ncores):
        m = dict(shared)
        m["x"] = np.ascontiguousarray(np.asarray(inputs["x"][b, :S], dtype=np.float32))
        m["mem"] = np.ascontiguousarray(np.asarray(inputs["mem"][b], dtype=np.float32))
        maps.append(m)
    return maps


def kernel(**inputs):
    S = inputs["x"].shape[1]
    B = inputs["x"].shape[0]
    kb = build(S)
    maps = make_in_maps(inputs, S, B)
    res = run_bass_kernel_spmd(kb.nc, maps, core_ids=list(range(B)))
    out = np.stack([np.asarray(r["out"], dtype=np.float32) for r in res.results], axis=0)
    return out
```

```python
import math
import sys
import numpy as np
import concourse.bass as bass
import concourse.mybir as mybir
from concourse.bass_utils import run_bass_kernel_spmd

F32 = mybir.dt.float32
BF16 = mybir.dt.bfloat16
I32 = mybir.dt.int32
AF = mybir.ActivationFunctionType
ALU = mybir.AluOpType
AX = mybir.AxisListType

SEM_LIMIT = 30000
N_DMA_SEMS = 24


class Buf:
    __slots__ = ("name", "last_w", "readers")

    def __init__(self, name):
        self.name = name
        self.last_w = None
        self.readers = []


class Op:
    __slots__ = ("eng", "fn", "deps", "need_inc", "sem", "val", "is_dma", "idx", "tag")


class Prog:
    def __init__(self, nc):
        self.nc = nc
        self.engs = {"pe": nc.tensor, "act": nc.scalar, "dve": nc.vector, "pool": nc.gpsimd, "sp": nc.sync}
        self.ops = []
        self.bufs = {}
        self.last_on = {}
        self.dmas_since = []
        self.phase_deps = []
        self.phase_bufs = set()
        self.capture = None

    def buf(self, name):
        b = self.bufs.get(name)
        if b is None:
            b = self.bufs[name] = Buf(name)
        return b

    def add(self, eng, fn, reads=(), writes=(), dma=False, extra_deps=()):
        if self.capture is not None:
            self.capture.append(((eng, fn), dict(reads=list(reads), writes=list(writes), dma=dma)))
            return None
        op = Op()
        op.eng = eng
        op.fn = fn
        op.is_dma = dma
        op.need_inc = False
        op.sem = None
        op.val = 0
        op.idx = len(self.ops)
        try:
            f_ = sys._getframe(2)
            op.tag = (f_.f_lineno, f_.f_back.f_lineno if f_.f_back else 0)
        except Exception:
            op.tag = (0, 0)
        deps = {}
        for b in reads:
            b = self.buf(b)
            w = b.last_w
            if w is not None:
                deps[w.idx] = (w, "raw")
        for b in writes:
            b = self.buf(b)
            w = b.last_w
            if w is not None and w.idx not in deps:
                deps[w.idx] = (w, "waw")
            for r in b.readers:
                if r.idx not in deps:
                    deps[r.idx] = (r, "war")
        real = []
        for d, kind in deps.values():
            if (not d.is_dma) and d.eng == eng and not dma:
                if eng == "pe" or kind != "raw":
                    continue
            real.append(d)
        for d in extra_deps:
            real.append(d)
        if self.phase_deps:
            for b in list(reads) + list(writes):
                if b not in self.phase_bufs:
                    self.phase_bufs.add(b)
                    real.extend(self.phase_deps)
        op.deps = real
        for b in writes:
            b = self.buf(b)
            b.last_w = op
            b.readers = []
        for b in reads:
            self.buf(b).readers.append(op)
        self.ops.append(op)
        self.last_on[eng] = op
        if dma:
            self.dmas_since.append(op)
        return op

    def barrier(self):
        self.phase_deps = list(self.last_on.values()) + list(self.dmas_since)
        self.dmas_since = []
        self.phase_bufs = set()

    def emit(self, final_wait_eng="sp"):
        nc = self.nc
        for op in self.ops:
            for d in op.deps:
                d.need_inc = True
            if op.is_dma:
                op.need_inc = True
        eng_sem = {}
        eng_cnt = {}
        dma_sems = [nc.alloc_semaphore("dq%d" % i) for i in range(N_DMA_SEMS)]
        dma_cnt = [0] * N_DMA_SEMS
        dma_last = [None] * N_DMA_SEMS
        ndma = 0
        for op in self.ops:
            if not op.need_inc:
                continue
            if op.is_dma:
                j = ndma % N_DMA_SEMS
                ndma += 1
                if dma_last[j] is not None:
                    op.deps.append(dma_last[j])
                dma_cnt[j] += 16
                op.sem = dma_sems[j]
                op.val = dma_cnt[j]
                dma_last[j] = op
            else:
                e = op.eng
                if e not in eng_sem or eng_cnt[e] >= SEM_LIMIT:
                    eng_sem[e] = nc.alloc_semaphore("s_%s_%d" % (e, op.idx))
                    eng_cnt[e] = 0
                eng_cnt[e] += 1
                op.sem = eng_sem[e]
                op.val = eng_cnt[e]
        waited = {}
        nwaits = 0
        for op in self.ops:
            E = self.engs[op.eng]
            need = {}
            for d in op.deps:
                k = id(d.sem)
                if k not in need or need[k][1] < d.val:
                    need[k] = (d.sem, d.val)
            for k, (sem, val) in need.items():
                wk = (op.eng, k)
                if waited.get(wk, 0) >= val:
                    continue
                E.wait_ge(sem, val)
                nwaits += 1
                waited[wk] = val
            try:
                inst = op.fn()
            except Exception:
                print('EMIT FAIL at op', op.idx, op.eng, 'lines', op.tag)
                raise
            if op.need_inc:
                inst.then_inc(op.sem, 16 if op.is_dma else 1)
        E = self.engs[final_wait_eng]
        for j in range(N_DMA_SEMS):
            if dma_cnt[j] > 0:
                E.wait_ge(dma_sems[j], dma_cnt[j])
        self.stats = dict(n_ops=len(self.ops), n_waits=nwaits, n_dma=ndma,
                          n_inc=sum(1 for o in self.ops if o.need_inc))
        return self.stats


class V:
    __slots__ = ("ap", "b")

    def __init__(self, ap, b):
        self.ap = ap
        self.b = b


class T:
    def __init__(self, h, name, dram=False):
        self.h = h
        self.name = name
        self.dram = dram

    def __getitem__(self, idx):
        if self.dram:
            return V(self.h[idx], self.name)
        return V(self.h[idx], self.name)

    def v(self, ap):
        return V(ap, self.name)


DT_SIZE = {F32: 4, BF16: 2, I32: 4}

D = 1024
DFF = 2816
NKC = D // 128
NFC = DFF // 128
LN_EPS = 1e-5
DEPTH = 2
ALPHA = (2 * DEPTH) ** 0.25
N_MEM = 256


class KB:
    def __init__(self, S, depth=DEPTH, stop_after=None, debug=()):
        self.S = S
        self.depth = depth
        self.stop_after = stop_after
        self.debug = debug
        self.nc = bass.Bass("TRN2", target_bir_lowering=False)
        self.P = Prog(self.nc)
        self.uid = 0
        self.sb_base = 0
        self.sb_cur = 0
        self.outs = {}
        self.arena = None
        self.rots = {}
        self.ps_set = (0, 5)
        self.psb_set = (0, 2)
        self.fill_regs = {}
        self.n_keep = 256

    def sb(self, name, shape, dtype):
        nbytes = int(np.prod(shape[1:])) * DT_SIZE[dtype]
        nbytes = (nbytes + 63) // 64 * 64
        off = self.sb_cur
        self.sb_cur += nbytes
        assert self.sb_cur <= 204 * 1024, ("SBUF overflow", name, self.sb_cur)
        self.uid += 1
        if self.arena is None:
            self.arena = self.nc.alloc_sbuf_tensor("arena", [128, 204 * 1024], mybir.dt.uint8)
        ap = self.arena[:, off:off + int(np.prod(shape[1:])) * DT_SIZE[dtype]].bitcast(dtype)
        if len(shape) == 3:
            ap = ap.rearrange("p (a b) -> p a b", a=shape[1])
        elif len(shape) == 4:
            ap = ap.rearrange("p (a b c) -> p a b c", a=shape[1], b=shape[2])
        if shape[0] < 128:
            ap = ap[0:shape[0]]
        return T(ap, "%s_%d" % (name, self.uid))

    def phase_begin(self):
        self.P.barrier()
        self.sb_cur = self.sb_base

    def dram(self, name, shape, dtype, kind="ExternalOutput"):
        h = self.nc.dram_tensor(name, list(shape), dtype, kind=kind)
        return T(h.ap(), name, dram=True)

    def _rw(self, reads, writes):
        return [r.b for r in reads if isinstance(r, V)], [w.b for w in writes]

    def dma(self, out, in_, eng="sp"):
        nc = self.nc
        E = self.P.engs[eng]
        return self.P.add(eng, lambda: E.dma_start(out=out.ap, in_=in_.ap), reads=[in_.b], writes=[out.b], dma=True)

    def mm(self, out, lhsT, rhs, start=True, stop=True):
        nc = self.nc
        return self.P.add("pe", lambda: nc.tensor.matmul(out.ap, lhsT.ap, rhs.ap, start=start, stop=stop),
                          reads=[lhsT.b, rhs.b], writes=[out.b])

    def tr(self, out, in_, ident):
        nc = self.nc
        return self.P.add("pe", lambda: nc.tensor.transpose(out.ap, in_.ap, ident.ap),
                          reads=[in_.b, ident.b], writes=[out.b])

    def act(self, out, in_, func, bias=None, scale=None, accum=None, eng="act"):
        nc = self.nc
        kw = {}
        reads = [in_.b]
        writes = [out.b]
        if bias is not None:
            if isinstance(bias, V):
                kw["bias"] = bias.ap
                reads.append(bias.b)
            else:
                kw["bias"] = bias
        if scale is not None:
            if isinstance(scale, V):
                kw["scale"] = scale.ap
                reads.append(scale.b)
            else:
                kw["scale"] = scale
        if accum is not None:
            kw["accum_out"] = accum.ap
            writes.append(accum.b)
        return self.P.add("act", lambda: nc.scalar.activation(out=out.ap, in_=in_.ap, func=func, **kw),
                          reads=reads, writes=writes)

    def ts(self, out, in0, s1, s2, op0, op1=None, accum=None, eng="dve"):
        E = self.P.engs[eng]
        reads = [in0.b]
        writes = [out.b]
        a1 = s1
        a2 = s2
        if isinstance(s1, V):
            a1 = s1.ap
            reads.append(s1.b)
        if isinstance(s2, V):
            a2 = s2.ap
            reads.append(s2.b)
        kw = {}
        if op1 is not None:
            kw["op1"] = op1
        if accum is not None:
            kw["accum_out"] = accum.ap
            writes.append(accum.b)
        return self.P.add(eng, lambda: E.tensor_scalar(out=out.ap, in0=in0.ap, scalar1=a1, scalar2=a2, op0=op0, **kw),
                          reads=reads, writes=writes)

    def tt(self, out, in0, in1, op, eng="dve"):
        E = self.P.engs[eng]
        return self.P.add(eng, lambda: E.tensor_tensor(out=out.ap, in0=in0.ap, in1=in1.ap, op=op),
                          reads=[in0.b, in1.b], writes=[out.b])

    def stt(self, out, in0, scalar, in1, op0, op1, accum=None):
        nc = self.nc
        reads = [in0.b, in1.b]
        writes = [out.b]
        a = scalar
        if isinstance(scalar, V):
            a = scalar.ap
            reads.append(scalar.b)
        kw = {}
        if accum is not None:
            kw["accum_out"] = accum.ap
            writes.append(accum.b)
        return self.P.add("dve", lambda: nc.vector.scalar_tensor_tensor(out=out.ap, in0=in0.ap, scalar=a, in1=in1.ap,
                                                                     op0=op0, op1=op1, **kw),
                          reads=reads, writes=writes)

    def copy(self, out, in_, eng="dve"):
        E = self.P.engs[eng]
        if eng == "act":
            return self.P.add(eng, lambda: E.copy(out=out.ap, in_=in_.ap), reads=[in_.b], writes=[out.b])
        return self.P.add(eng, lambda: E.tensor_copy(out=out.ap, in_=in_.ap), reads=[in_.b], writes=[out.b])

    def memset(self, out, val, eng="pool"):
        E = self.P.engs[eng]
        return self.P.add(eng, lambda: E.memset(out.ap, val), writes=[out.b])

    def red(self, out, in_, op, axis=AX.X, eng="dve"):
        E = self.P.engs[eng]
        return self.P.add(eng, lambda: E.tensor_reduce(out=out.ap, in_=in_.ap, axis=axis, op=op),
                          reads=[in_.b], writes=[out.b])

    def recip(self, out, in_):
        nc = self.nc
        return self.P.add("dve", lambda: nc.vector.reciprocal(out=out.ap, in_=in_.ap), reads=[in_.b], writes=[out.b])

    def aselect(self, out, in_, pattern, cmp, fill, base, cm):
        nc = self.nc
        regs = self.fill_regs

        def fn():
            if fill not in regs:
                regs[fill] = nc.gpsimd.to_reg(float(fill))
            return nc.gpsimd.affine_select(out=out.ap, in_=in_.ap, pattern=pattern, compare_op=cmp,
                                           fill=regs[fill], base=base, channel_multiplier=cm)
        return self.P.add("pool", fn, reads=[in_.b], writes=[out.b])

    def iota(self, out, pattern, base, cm):
        nc = self.nc
        return self.P.add("pool", lambda: nc.gpsimd.iota(out.ap, pattern=pattern, base=base, channel_multiplier=cm,
                                                         allow_small_or_imprecise_dtypes=True), writes=[out.b])

    def setup(self):
        nc = self.nc
        self.ps = []
        for i in range(5):
            h = nc.alloc_psum_tensor("ps%d" % i, [128, 512], F32)
            self.ps.append(T(h, "ps%d" % i))
        self.psb = []
        for i in range(2):
            h = nc.alloc_psum_tensor("psb%d" % i, [128, 1024], BF16)
            self.psb.append(T(h, "psb%d" % i))
        self.ps_rr = 0
        self.psb_rr = 0
        self.ident_f = self.sb("identf", [128, 128], F32)
        self.ident = self.sb("ident", [128, 128], BF16)
        self.memset(self.ident_f[:], 1.0)
        self.aselect(self.ident_f[:], self.ident_f[:], [[-1, 128]], ALU.is_equal, 0.0, 0, 1)
        self.copy(self.ident[:], self.ident_f[:], eng="pool")
        self.sb_base = self.sb_cur

    def next_ps(self):
        b0, n = self.ps_set
        t = self.ps[b0 + self.ps_rr % n]
        self.ps_rr += 1
        return t

    def next_psb(self):
        b0, n = self.psb_set
        t = self.psb[b0 + self.psb_rr % n]
        self.psb_rr += 1
        return t

    def load_w(self, name, dram_ap_fn, kchunks, ncols, eng="pool", split=4):
        w = self.sb(name, [128, kchunks, ncols], BF16)
        for k in range(kchunks):
            self.dma(w[:, k, :], dram_ap_fn(k), eng="pool")
        return w

    def layer_norm_tile(self, r, g_bc, b_bc, out_f32, scr):
        st = scr["st"]
        junk = scr["junk"]
        self.act(junk[:], r[:], AF.Identity, accum=st[:, 0:1])
        self.act(junk[:], r[:], AF.Square, accum=st[:, 1:2])
        self.ts(st[:, 2:3], st[:, 0:1], 1.0 / D, None, ALU.mult)
        self.tt(st[:, 3:4], st[:, 2:3], st[:, 2:3], ALU.mult)
        self.stt(st[:, 4:5], st[:, 1:2], 1.0 / D, st[:, 3:4], ALU.mult, ALU.subtract)
        self.ts(st[:, 4:5], st[:, 4:5], 0.0, LN_EPS, ALU.max, ALU.add)
        self.act(st[:, 5:6], st[:, 4:5], AF.Sqrt)
        self.recip(st[:, 6:7], st[:, 5:6])
        self.ts(out_f32[:], r[:], st[:, 2:3], st[:, 6:7], ALU.subtract, ALU.mult)
        self.tt(out_f32[:], out_f32[:], g_bc[:], ALU.mult)
        self.tt(out_f32[:], out_f32[:], b_bc[:], ALU.add)

    def store_xT(self, x_f32, xT_dram, t0, scr):
        xb = scr["xb"]
        xTs = scr["xTs"]
        self.copy(xb[:], x_f32[:], eng="act")
        pb = self.next_psb()
        for k in range(NKC):
            self.tr(pb[:, k * 128:(k + 1) * 128], xb[:, k * 128:(k + 1) * 128], self.ident[:])
        self.copy(xTs[:], pb[:, :], eng="dve")
        self.dma(V(xT_dram.h.rearrange("(k p) s -> p k s", p=128)[:, :, t0:t0 + 128], xT_dram.name),
                 V(xTs.h[:].rearrange("p (k t) -> p k t", k=NKC), xTs.name))

    def dma_s(self, out, in_, eng="sp"):
        E = self.P.engs[eng]
        return self.P.add(eng, lambda: E.dma_start(out=out.ap, in_=in_.ap, allow_slow_non_contiguous=True),
                          reads=[in_.b], writes=[out.b], dma=True)

    def rot(self, name, n, shape, dtype):
        key = "_rot_" + name
        lst = [self.sb(name + str(i), shape, dtype) for i in range(n)]
        self.rots[key] = [lst, 0]
        return key

    def nx(self, key):
        lst, i = self.rots[key]
        self.rots[key][1] = i + 1
        return lst[i % len(lst)]

    def vmax(self, out, in_):
        nc = self.nc
        return self.P.add("dve", lambda: nc.vector.max(out=out.ap, in_=in_.ap), reads=[in_.b], writes=[out.b])

    def match_replace(self, out, rep, vals, imm):
        nc = self.nc
        return self.P.add("dve", lambda: nc.vector.match_replace(out=out.ap, in_to_replace=rep.ap, in_values=vals.ap, imm_value=imm),
                          reads=[rep.b, vals.b], writes=[out.b])

    def redabs(self, out, in_):
        nc = self.nc
        return self.P.add("dve", lambda: nc.vector.tensor_reduce(out=out.ap, in_=in_.ap, axis=AX.X, op=ALU.max,
                                                                 apply_absolute_value=True),
                          reads=[in_.b], writes=[out.b])

    def fm_rows(self, FMS, c0, nchunk, s0, s1):
        return V(FMS.h[c0 * 128:(c0 + nchunk) * 128, s0:s1].rearrange("(c p) s -> p c s", p=128), FMS.name)

    def setup_consts(self, meta, bdm, ovl, NB, NCP):
        S = self.S
        NT = S // 128
        self.NB = NB
        self.NCP = NCP
        self.meta = self.sb("meta", [128, 32], F32)
        self.dma(self.meta[:], meta[:, :])
        self.bdm = self.sb("bdm", [128, 256], F32)
        self.dma(self.bdm[:], bdm[:, :])
        self.ovl = self.sb("ovl", [128, NCP // 128, NB], F32)
        self.dma(self.ovl[:], V(ovl.h.rearrange("(c p) j -> p c j", p=128), ovl.name))
        self.U = self.sb("U", [128, 128], F32)
        self.memset(self.U[:], 1.0)
        self.aselect(self.U[:], self.U[:], [[1, 128]], ALU.is_ge, 0.0, 0, -1)
        self.cneg30 = self.sb("cneg30", [128, 128], F32)
        self.memset(self.cneg30[:], 0.0)
        self.aselect(self.cneg30[:], self.cneg30[:], [[-1, 128]], ALU.is_ge, -1e30, 0, 1)
        self.cneg2k = self.sb("cneg2k", [128, 128], F32)
        self.memset(self.cneg2k[:], 0.0)
        self.aselect(self.cneg2k[:], self.cneg2k[:], [[-1, 128]], ALU.is_ge, -2000.0, 0, 1)
        self.band = self.sb("band", [128, 640], F32)
        self.memset(self.band[:], 0.0)
        self.aselect(self.band[:], self.band[:], [[1, 640]], ALU.is_ge, -2000.0, -1, -1)
        self.aselect(self.band[:], self.band[:], [[-1, 640]], ALU.is_ge, -2000.0, 512, 1)
        self.decayT4 = self.sb("decayT4", [128, 4, 128], F32)
        self.xi = self.sb("xi", [128, 128], F32)
        self.zeta = self.sb("zeta", [128, 128], F32)
        self.cdecay = self.sb("cdecay", [128, 1], F32)
        self.rkc = self.sb("rkc", [128, 20], F32)
        self.sb_base = self.sb_cur
        dji = self.sb("dji", [128, 128], F32)
        self.iota(dji[:], [[1, 128]], 0, -1)
        for h in range(4):
            self.act(self.decayT4[:, h, :], dji[:], AF.Exp, scale=RET_LNG[h])
        self.tt(self.decayT4[:], self.decayT4[:], V(self.U.h[:, :].unsqueeze(1).to_broadcast([128, 4, 128]), self.U.name), ALU.mult)
        ip1 = self.sb("ip1", [128, 128], F32)
        self.iota(ip1[:], [[1, 128]], 1, 0)
        self.act(self.xi[:], ip1[:], AF.Exp, scale=self.meta[:, 12:13])
        jr = self.sb("jr", [128, 128], F32)
        self.iota(jr[:], [[0, 128]], 127, -1)
        for h in range(4):
            self.act(self.zeta[:, 32 * h:32 * h + 32], jr[:, 32 * h:32 * h + 32], AF.Exp, scale=RET_LNG[h])
        c128 = self.sb("c128", [128, 1], F32)
        self.memset(c128[:], 128.0)
        self.act(self.cdecay[:], c128[:], AF.Exp, scale=self.meta[:, 12:13])
        for k in range(20):
            self.memset(self.rkc[:, k:k + 1], 2.0 ** (-k))

    def build_addmask(self):
        NT = self.S // 128
        NB = self.NB
        self.addmask = self.sb("addmask", [128, NT, NB], F32)
        self.memset(self.addmask[:], 0.0)
        for i in range(NT):
            for half in range(2):
                cur = 2 * i + half
                r0 = 64 * half
                v = self.addmask[r0:r0 + 64, i, :]
                self.aselect(v, v, [[-1, NB]], ALU.is_ge, -1e30, cur, 0)
                self.memset(self.addmask[r0:r0 + 64, i, 0:1], 1e30)
                self.memset(self.addmask[r0:r0 + 64, i, cur:cur + 1], 1e30)
                if cur >= 1:
                    self.memset(self.addmask[r0:r0 + 64, i, cur - 1:cur], 1e30)

    def phase_rope(self, ROPE):
        S = self.S
        self.phase_begin()
        pos = self.sb("pos", [128, S], F32)
        self.iota(pos[:], [[1, S]], 0, 0)
        a = self.sb("a", [128, S], F32)
        ki = self.sb("ki", [128, S], I32)
        kf = self.sb("kf", [128, S], F32)
        m = self.sb("m", [128, S], F32)
        r = self.sb("r", [128, S], F32)
        PI = math.pi
        for t in range(4):
            for which in range(2):
                self.ts(a[:], pos[:], self.meta[:, t:t + 1], (PI / 2 if which == 0 else 0.0), ALU.mult, ALU.add)
                self.ts(kf[:], a[:], 1.0 / (2 * PI), None, ALU.mult)
                self.copy(ki[:], kf[:])
                self.copy(kf[:], ki[:])
                self.stt(r[:], kf[:], -2 * PI, a[:], ALU.mult, ALU.add)
                self.ts(m[:], r[:], PI, -2 * PI, ALU.is_gt, ALU.mult)
                self.tt(r[:], r[:], m[:], ALU.add)
                self.ts(m[:], r[:], -PI, 2 * PI, ALU.is_lt, ALU.mult)
                self.tt(r[:], r[:], m[:], ALU.add)
                self.ts(r[:], r[:], PI, -PI, ALU.min, ALU.max)
                self.act(r[:], r[:], AF.Sin)
                col = 4 + 4 * which + t
                self.ts(r[:], r[:], self.meta[:, col:col + 1], None, ALU.mult)
                self.dma(ROPE[t, which], r[:])

    def finish_tile(self, rq, g_bc, b_bc, x_out, xT_out, t0, scr):
        self.layer_norm_tile(rq, g_bc, b_bc, rq, scr)
        self.dma(x_out[t0:t0 + 128, :], rq[:])
        if xT_out is not None:
            self.store_xT(rq, xT_out, t0, scr)

    def ln_setup(self, ln_g, ln_b):
        g_bc = self.sb("g_bc", [128, D], F32)
        b_bc = self.sb("b_bc", [128, D], F32)
        self.dma(g_bc[:], V(ln_g.h.partition_broadcast(128), ln_g.name))
        self.dma(b_bc[:], V(ln_b.h.partition_broadcast(128), ln_b.name))
        scr = dict(st=self.sb("st", [128, 8], F32), junk=self.sb("junk", [128, D], BF16),
                   xb=self.sb("xb", [128, D], BF16), xTs=self.sb("xTs", [128, D], BF16))
        return g_bc, b_bc, scr

    def phase_ffn(self, x_in, xT_in, w_gu, w_down, ln_g, ln_b, x_out, xT_out):
        S = self.S
        self.phase_begin()
        wgu = self.load_w("wgu", lambda k: V(w_gu.h[k * 128:(k + 1) * 128, :], w_gu.name), NKC, 2 * DFF)
        wdn = self.load_w("wdn", lambda k: V(w_down.h[k * 128:(k + 1) * 128, :], w_down.name), NFC, D)
        g_bc, b_bc, scr = self.ln_setup(ln_g, ln_b)
        xt = self.sb("xT", [128, NKC, 512], BF16)
        hT = self.sb("hT", [128, NFC, 512], BF16)
        sg = [self.sb("sg%d" % i, [128, 512], F32) for i in range(2)]
        xr = [self.sb("xr%d" % i, [128, D], F32) for i in range(1)]
        rq = self.sb("r", [128, D], F32)
        for t in range(S // 512):
            self.dma(xt[:], V(xT_in.h.rearrange("(k p) s -> p k s", p=128)[:, :, t * 512:(t + 1) * 512], xT_in.name))
            for j in range(NFC):
                pg = self.next_ps()
                pu = self.next_ps()
                for k in range(NKC):
                    self.mm(pg[:], wgu[:, k, j * 128:(j + 1) * 128], xt[:, k, :], start=(k == 0), stop=(k == NKC - 1))
                for k in range(NKC):
                    self.mm(pu[:], wgu[:, k, DFF + j * 128:DFF + (j + 1) * 128], xt[:, k, :], start=(k == 0), stop=(k == NKC - 1))
                s = sg[j % 2]
                self.act(s[:], pg[:], AF.Silu)
                self.tt(hT[:, j, :], s[:], pu[:], ALU.mult)
            for q in range(4):
                t0 = t * 512 + q * 128
                xq = xr[0]
                self.dma(xq[:], x_in[t0:t0 + 128, :])
                for half in range(2):
                    hs = slice(half * 512, (half + 1) * 512)
                    pd = self.next_ps()
                    for j in range(NFC):
                        self.mm(pd[:], hT[:, j, q * 128:(q + 1) * 128], wdn[:, j, hs], start=(j == 0), stop=(j == NFC - 1))
                    self.act(xq[:, hs], xq[:, hs], AF.Copy, scale=ALPHA)
                    self.stt(rq[:, hs], pd[:], 0.5, xq[:, hs], ALU.mult, ALU.add)
                self.finish_tile(rq, g_bc, b_bc, x_out, xT_out, t0, scr)

    def phase_transpose_in(self, x_in, xT_out):
        S = self.S
        self.phase_begin()
        xr = [self.sb("xr%d" % i, [128, D], F32) for i in range(2)]
        scr = dict(xb=self.sb("xb", [128, D], BF16), xTs=self.sb("xTs", [128, D], BF16))
        for i in range(S // 128):
            xq = xr[i % 2]
            self.dma(xq[:], x_in[i * 128:(i + 1) * 128, :])
            self.store_xT(xq, xT_out, i * 128, scr)

    def phase_inproj(self, xT_in, w2, ROPE, FMS, TMB, TMF):
        S = self.S
        self.phase_begin()
        w = self.load_w("win", lambda k: V(w2.h[k * 128:(k + 1) * 128, :], w2.name), NKC, NCOL2)
        xt = self.sb("xT", [128, NKC, 512], BF16)
        tab = self.sb("tab", [128, 4, 2, 512], F32)
        t1 = self.rot("t1", 2, [128, 512], F32)
        t2 = self.rot("t2", 2, [128, 512], F32)
        ob = self.rot("ob", 3, [128, 512], BF16)
        tmb = self.rot("tmb", 2, [128, 960], BF16)
        tmf = self.rot("tmf", 2, [128, 24], F32)
        for t in range(S // 512):
            ss = slice(t * 512, (t + 1) * 512)
            self.dma(xt[:], V(xT_in.h.rearrange("(k p) s -> p k s", p=128)[:, :, ss], xT_in.name))
            self.dma(tab[:], V(ROPE.h[:, :, :, ss].rearrange("t w p s -> p t w s"), ROPE.name))
            for ci, tb in enumerate(ROPED_TABLES):
                pA = self.next_ps()
                pB = self.next_ps()
                for k in range(NKC):
                    self.mm(pA[:], w[:, k, (2 * ci) * 128:(2 * ci + 1) * 128], xt[:, k, :], start=(k == 0), stop=(k == NKC - 1))
                for k in range(NKC):
                    self.mm(pB[:], w[:, k, (2 * ci + 1) * 128:(2 * ci + 2) * 128], xt[:, k, :], start=(k == 0), stop=(k == NKC - 1))
                a1 = self.nx(t1)
                a2 = self.nx(t2)
                o = self.nx(ob)
                self.tt(a1[:], pA[:], tab[:, tb, 0, :], ALU.mult)
                self.tt(a2[:], pB[:], tab[:, tb, 1, :], ALU.mult)
                self.tt(o[:], a1[:], a2[:], ALU.add, eng="pool")
                self.dma(V(FMS.h[ci * 128:(ci + 1) * 128, ss], FMS.name), o[:])
            nr = len(ROPED_TABLES)
            for j in range(7):
                wc = 2 * nr + j
                pA = self.next_ps()
                for k in range(NKC):
                    self.mm(pA[:], w[:, k, wc * 128:(wc + 1) * 128], xt[:, k, :], start=(k == 0), stop=(k == NKC - 1))
                o = self.nx(ob)
                self.copy(o[:], pA[:], eng="act")
                self.dma(V(FMS.h[(nr + j) * 128:(nr + j + 1) * 128, ss], FMS.name), o[:])
            for q in range(4):
                t0 = t * 512 + q * 128
                pA = self.next_ps()
                pB = self.next_ps()
                for k in range(NKC):
                    self.mm(pA[:], xt[:, k, q * 128:(q + 1) * 128], w[:, k, TM0:TM0 + 512], start=(k == 0), stop=(k == NKC - 1))
                for k in range(NKC):
                    self.mm(pB[:, 0:472], xt[:, k, q * 128:(q + 1) * 128], w[:, k, TM0 + 512:TM0 + 984], start=(k == 0), stop=(k == NKC - 1))
                b = self.nx(tmb)
                f = self.nx(tmf)
                self.copy(b[:, 0:512], pA[:], eng="act")
                self.copy(b[:, 512:960], pB[:, 0:448])
                self.copy(f[:], pB[:, 448:472])
                self.dma(TMB[t0:t0 + 128, :], b[:])
                self.dma(TMF[t0:t0 + 128, :], f[:])

    def store_yT(self, y, YT, br, n, yTs_key):
        pb = self.next_psb()
        self.tr(pb[:, 0:128], y[:, 0:128], self.ident[:])
        self.tr(pb[:, 128:256], y[:, 128:256], self.ident[:])
        yTs = self.nx(yTs_key)
        self.copy(yTs[:], pb[:, 0:256])
        self.dma(V(YT.h[br * 256:(br + 1) * 256, n * 128:(n + 1) * 128].rearrange("(c p) t -> p c t", p=128), YT.name),
                 V(yTs.h[:, :].rearrange("p (c t) -> p c t", c=2), yTs.name))

    def phase_ret(self, FMS, TMB, YT):
        S = self.S
        NT = S // 128
        self.phase_begin()
        rq = self.sb("rq", [128, S], BF16)
        rk = self.sb("rk", [128, S], BF16)
        self.dma(rq[:], V(FMS.h[0:128, :], FMS.name))
        self.dma(rk[:], V(FMS.h[128:256, :], FMS.name))
        Sbd = self.sb("Sbd", [128, 256], F32)
        Sbd_bf = self.sb("Sbd_bf", [128, 256], BF16)
        self.memset(Sbd[:], 0.0)
        self.memset(Sbd_bf[:], 0.0)
        vt_k = self.rot("vt", 2, [128, 512], BF16)
        qxi_k = self.rot("qxi", 2, [128, 128], BF16)
        qm_k = self.rot("qm", 2, [128, 4, 128], BF16)
        kz_k = self.rot("kz", 2, [128, 128], BF16)
        PT_k = self.rot("PT", 2, [128, 4, 128], BF16)
        cross_k = self.rot("cross", 2, [128, 256], F32)
        o_k = self.rot("o", 2, [128, 256], F32)
        tmp_k = self.rot("tmp", 2, [128, 256], F32)
        osq_k = self.rot("osq", 2, [128, 256], F32)
        sg_k = self.rot("sg", 2, [128, 256], F32)
        st_k = self.rot("st", 2, [128, 16], F32)
        y_k = self.rot("y", 2, [128, 256], BF16)
        yTs_k = self.rot("yTs", 2, [128, 256], BF16)
        hm = V(self.meta.h[:, 13:17].unsqueeze(2).to_broadcast([128, 4, 128]), self.meta.name)
        for n in range(NT):
            sl = slice(n * 128, (n + 1) * 128)
            vt = self.nx(vt_k)
            self.dma(vt[:], TMB[n * 128:(n + 1) * 128, 0:512])
            qxi = self.nx(qxi_k)
            self.tt(qxi[:], rq[:, sl], self.xi[:], ALU.mult)
            qm = self.nx(qm_k)
            self.tt(qm[:], V(rq.h[:, sl].unsqueeze(1).to_broadcast([128, 4, 128]), rq.name), hm, ALU.mult, eng="pool")
            pb = self.next_psb()
            self.tr(pb[:, 0:128], rk[:, sl], self.ident[:])
            kz = self.nx(kz_k)
            self.tt(kz[:], pb[:, 0:128], self.zeta[:], ALU.mult)
            ps1 = self.next_ps()
            self.mm(ps1[:], rk[:, sl], V(qm.h[:, :, :].rearrange("p h i -> p (h i)"), qm.name))
            PT = self.nx(PT_k)
            self.tt(PT[:], V(ps1.h[:, :].rearrange("p (h i) -> p h i", h=4), ps1.name), self.decayT4[:], ALU.mult)
            ps2 = self.next_ps()
            self.mm(ps2[:, 0:256], qxi[:], Sbd_bf[:])
            cross = self.nx(cross_k)
            self.copy(cross[:], ps2[:, 0:256], eng="act")
            ps3 = self.next_ps()
            for h in range(4):
                self.mm(ps3[:, 64 * h:64 * h + 64], PT[:, h, :], vt[:, 64 * h:64 * h + 64])
            o = self.nx(o_k)
            self.tt(o[:], ps3[:, 0:256], cross[:], ALU.add)
            ps4 = self.next_ps()
            self.mm(ps4[:, 0:256], kz[:], vt[:, 0:256])
            tmp = self.nx(tmp_k)
            self.tt(tmp[:], ps4[:, 0:256], self.bdm[:], ALU.mult)
            self.stt(Sbd[:], Sbd[:], self.cdecay[:, 0:1], tmp[:], ALU.mult, ALU.add)
            self.copy(Sbd_bf[:], Sbd[:], eng="act")
            st = self.nx(st_k)
            o3 = V(o.h[:, :].rearrange("p (h e) -> p h e", h=4), o.name)
            self.red(st[:, 0:4], o3, ALU.add)
            osq = self.nx(osq_k)
            self.tt(osq[:], o[:], o[:], ALU.mult, eng="pool")
            self.red(st[:, 4:8], V(osq.h[:, :].rearrange("p (h e) -> p h e", h=4), osq.name), ALU.add)
            self.ts(st[:, 8:12], st[:, 0:4], 1.0 / 64, None, ALU.mult)
            self.tt(st[:, 12:16], st[:, 8:12], st[:, 8:12], ALU.mult)
            self.stt(st[:, 4:8], st[:, 4:8], 1.0 / 64, st[:, 12:16], ALU.mult, ALU.subtract)
            self.ts(st[:, 4:8], st[:, 4:8], 0.0, LN_EPS, ALU.max, ALU.add)
            self.act(st[:, 4:8], st[:, 4:8], AF.Sqrt)
            self.recip(st[:, 4:8], st[:, 4:8])
            self.tt(o3, o3, V(st.h[:, 8:12].unsqueeze(2).to_broadcast([128, 4, 64]), st.name), ALU.subtract)
            self.tt(o3, o3, V(st.h[:, 4:8].unsqueeze(2).to_broadcast([128, 4, 64]), st.name), ALU.mult)
            sg = self.nx(sg_k)
            self.act(sg[:], vt[:, 256:512], AF.Silu)
            y = self.nx(y_k)
            self.tt(y[:], o[:], sg[:], ALU.mult)
            self.store_yT(y, YT, 0, n, yTs_k)

    def phase_ssd(self, FMS, TMB, TMF, conv_w, conv_b, dt_bias, a_log, d_skip, norm_g, YT):
        S = self.S
        NT = S // 128
        self.phase_begin()
        cw = self.sb("cw", [128, 6, 4], F32)
        for k_ in range(4):
            self.dma_s(cw[:, :, k_], V(conv_w.h[k_].rearrange("(c p) -> p c", p=128), conv_w.name))
        cb = self.sb("cb", [128, 6], F32)
        self.dma_s(cb[:], V(conv_b.h.rearrange("(c p) -> p c", p=128), conv_b.name))
        dtb = self.sb("dtb", [128, 4], F32)
        self.dma(dtb[:], V(dt_bias.h.partition_broadcast(128), dt_bias.name))
        a_bc = self.sb("a_bc", [128, 4], F32)
        self.dma(a_bc[:], V(a_log.h.partition_broadcast(128), a_log.name))
        self.act(a_bc[:], a_bc[:], AF.Exp)
        self.ts(a_bc[:], a_bc[:], -1.0, None, ALU.mult)
        Dbc = self.sb("Dbc", [128, 4], F32)
        self.dma(Dbc[:], V(d_skip.h.partition_broadcast(128), d_skip.name))
        ng_bc = self.sb("ng_bc", [128, 256], F32)
        self.dma(ng_bc[:], V(norm_g.h.partition_broadcast(128), norm_g.name))
        xbcs = self.sb("xbcs", [128, 6, S], BF16)
        raw_k = self.rot("raw", 2, [128, 6, 515], BF16)
        acc_k = self.rot("acc", 2, [128, 512], F32)
        for t in range(S // 512):
            raw = self.nx(raw_k)
            if t == 0:
                self.memset(raw[:, :, 0:3], 0.0)
                self.dma(raw[:, :, 3:515], self.fm_rows(FMS, 14, 6, 0, 512))
            else:
                self.dma(raw[:, :, 0:515], self.fm_rows(FMS, 14, 6, t * 512 - 3, (t + 1) * 512))
            for c in range(6):
                acc = self.nx(acc_k)
                self.ts(acc[:], raw[:, c, 3:515], cw[:, c, 3:4], None, ALU.mult)
                for k in (2, 1, 0):
                    self.stt(acc[:], raw[:, c, k:k + 512], cw[:, c, k:k + 1], acc[:], ALU.mult, ALU.add)
                self.act(xbcs[:, c, t * 512:(t + 1) * 512], acc[:], AF.Silu, bias=cb[:, c:c + 1])
        prev = self.sb("prev", [128, 256], F32)
        prev_bf = self.sb("prev_bf", [128, 256], BF16)
        self.memset(prev[:], 0.0)
        self.memset(prev_bf[:], 0.0)
        xsB_k = self.rot("xsB", 2, [128, 512], BF16)
        tmf_k = self.rot("tmf", 2, [128, 24], F32)
        zt_k = self.rot("zt", 2, [128, 256], BF16)
        st_k = self.rot("st", 2, [128, 32], F32)
        adtb_k = self.rot("adtb", 2, [128, 4, 128], F32)
        seg_k = self.rot("seg", 2, [128, 4, 128], F32)
        MT_k = self.rot("MT", 2, [128, 4, 128], BF16)
        X_k = self.rot("X", 2, [128, 256], BF16)
        Xd_k = self.rot("Xd", 2, [128, 256], BF16)
        yd_k = self.rot("yd", 2, [128, 256], F32)
        y_k = self.rot("y", 2, [128, 256], F32)
        t2_k = self.rot("t2", 2, [128, 256], F32)
        sz_k = self.rot("sz", 2, [128, 256], F32)
        yb_k = self.rot("yb", 2, [128, 256], BF16)
        yTs_k = self.rot("yTs", 2, [128, 256], BF16)
        Ubc = V(self.U.h[:, :].unsqueeze(1).to_broadcast([128, 4, 128]), self.U.name)

        def h4(t_):
            return V(t_.h[:, 0:256].rearrange("p (h e) -> p h e", h=4), t_.name)

        def bc4(v_):
            return V(v_.ap.unsqueeze(2).to_broadcast([128, 4, 64]), v_.b)

        for n in range(NT):
            sl = slice(n * 128, (n + 1) * 128)
            pb = self.next_psb()
            for c in range(4):
                self.tr(pb[:, c * 128:(c + 1) * 128], xbcs[:, c, sl], self.ident[:])
            xsB = self.nx(xsB_k)
            self.copy(xsB[:], pb[:, 0:512])
            tmf = self.nx(tmf_k)
            self.dma(tmf[:], TMF[n * 128:(n + 1) * 128, :])
            zt = self.nx(zt_k)
            self.dma(zt[:], TMB[n * 128:(n + 1) * 128, 704:960])
            st = self.nx(st_k)
            self.tt(st[:, 0:4], tmf[:, 20:24], dtb[:], ALU.add)
            self.act(st[:, 0:4], st[:, 0:4], AF.Exp)
            self.act(st[:, 0:4], st[:, 0:4], AF.Ln, bias=1.0)
            self.tt(st[:, 4:8], st[:, 0:4], a_bc[:], ALU.mult)
            adtb = self.nx(adtb_k)
            self.copy(adtb[:], V(st.h[:, 4:8].unsqueeze(2).to_broadcast([128, 4, 128]), st.name))
            psA = self.next_ps()
            self.mm(psA[:, 0:4], self.U[:], st[:, 4:8])
            self.copy(st[:, 8:12], psA[:, 0:4], eng="act")
            psB = self.next_ps()
            for h in range(4):
                self.mm(psB[:, h * 128:(h + 1) * 128], adtb[:, h, :], self.U[:])
            seg = self.nx(seg_k)
            for h in range(4):
                self.ts(seg[:, h, :], psB[:, h * 128:(h + 1) * 128], st[:, 8 + h:9 + h], 0.0, ALU.subtract, ALU.min)
            self.act(seg[:], seg[:], AF.Exp)
            self.tt(seg[:], seg[:], Ubc, ALU.mult, eng="pool")
            alast = V(psB.h[:, 127:512:128], psB.name)
            self.tt(st[:, 12:16], alast, st[:, 8:12], ALU.subtract)
            self.act(st[:, 12:16], st[:, 12:16], AF.Exp)
            self.act(st[:, 16:20], alast, AF.Exp)
            self.act(st[:, 20:24], st[:, 8:12], AF.Exp)
            psG = self.next_ps()
            for g in range(2):
                self.mm(psG[:, g * 128:(g + 1) * 128], xbcs[:, 2 + g, sl], xbcs[:, 4 + g, sl])
            MT = self.nx(MT_k)
            for g in range(2):
                self.tt(MT[:, 2 * g:2 * g + 2, :], seg[:, 2 * g:2 * g + 2, :],
                        V(psG.h[:, g * 128:(g + 1) * 128].unsqueeze(1).to_broadcast([128, 2, 128]), psG.name), ALU.mult)
            X = self.nx(X_k)
            self.tt(h4(X), h4(xsB), bc4(st[:, 0:4]), ALU.mult)
            psY = self.next_ps()
            for h in range(4):
                self.mm(psY[:, 64 * h:64 * h + 64], MT[:, h, :], X[:, 64 * h:64 * h + 64])
            psO = self.next_ps()
            for g in range(2):
                self.mm(psO[:, 128 * g:128 * g + 128], xbcs[:, 4 + g, sl], prev_bf[:, 128 * g:128 * g + 128])
            yd = self.nx(yd_k)
            self.copy(yd[:], psY[:, 0:256], eng="act")
            y = self.nx(y_k)
            self.tt(h4(y), h4(psO), bc4(st[:, 20:24]), ALU.mult)
            self.tt(y[:], y[:], yd[:], ALU.add)
            t2 = self.nx(t2_k)
            self.tt(h4(t2), h4(xsB), bc4(Dbc[:, 0:4]), ALU.mult, eng="pool")
            self.tt(y[:], y[:], t2[:], ALU.add)
            Xd = self.nx(Xd_k)
            self.tt(h4(Xd), h4(X), bc4(st[:, 12:16]), ALU.mult, eng="pool")
            psS = self.next_ps()
            for g in range(2):
                self.mm(psS[:, 128 * g:128 * g + 128], xsB[:, 256 + 128 * g:256 + 128 * g + 128], Xd[:, 128 * g:128 * g + 128])
            self.tt(h4(prev), h4(prev), bc4(st[:, 16:20]), ALU.mult)
            self.tt(prev[:], prev[:], psS[:, 0:256], ALU.add)
            self.copy(prev_bf[:], prev[:], eng="act")
            sz = self.nx(sz_k)
            self.act(sz[:], zt[:], AF.Silu)
            self.tt(y[:], y[:], sz[:], ALU.mult)
            self.tt(t2[:], y[:], y[:], ALU.mult, eng="pool")
            self.red(st[:, 24:26], V(t2.h[:, :].rearrange("p (g e) -> p g e", g=2), t2.name), ALU.add)
            self.ts(st[:, 24:26], st[:, 24:26], 1.0 / 128, LN_EPS, ALU.mult, ALU.add)
            self.act(st[:, 24:26], st[:, 24:26], AF.Sqrt)
            self.recip(st[:, 24:26], st[:, 24:26])
            y2 = V(y.h[:, :].rearrange("p (g e) -> p g e", g=2), y.name)
            self.tt(y2, y2, V(st.h[:, 24:26].unsqueeze(2).to_broadcast([128, 2, 128]), st.name), ALU.mult)
            yb = self.nx(yb_k)
            self.tt(yb[:], y[:], ng_bc[:], ALU.mult)
            self.store_yT(yb, YT, 3, n, yTs_k)

    def softmax_pv(self, Ssb, nk, Vt, kt0, out, kk, clamp=None):
        st = self.nx(kk["st"])
        self.red(st[:, 0:1], Ssb, ALU.max)
        if clamp is not None:
            self.ts(st[:, 0:1], st[:, 0:1], clamp, None, ALU.max)
        self.ts(st[:, 1:2], st[:, 0:1], -1.0, None, ALU.mult)
        P = self.nx(kk["P"])
        self.act(P[:, 0:nk], Ssb, AF.Exp, bias=st[:, 1:2], accum=st[:, 2:3])
        self.ts(st[:, 3:4], st[:, 2:3], 1e-30, None, ALU.max)
        self.recip(st[:, 4:5], st[:, 3:4])
        po = self.next_ps()
        nkt = nk // 128
        for g0 in range(0, nkt, 8):
            gn = min(8, nkt - g0)
            pb = self.next_psb()
            for j in range(gn):
                self.tr(pb[:, j * 128:(j + 1) * 128], P[:, (g0 + j) * 128:(g0 + j + 1) * 128], self.ident[:])
            PT = self.nx(kk["PT"])
            self.copy(PT[:, 0:gn * 128], pb[:, 0:gn * 128], eng="act")
            for j in range(gn):
                self.mm(po[:, 0:64], PT[:, j * 128:(j + 1) * 128], Vt[:, kt0 + g0 + j, :],
                        start=(g0 + j == 0), stop=(g0 + j == nkt - 1))
        self.ts(out, po[:, 0:64], st[:, 4:5], None, ALU.mult)

    def attn_keys(self, pfx):
        S = self.S
        return dict(st=self.rot(pfx + "sst", 2, [128, 8], F32), P=self.rot(pfx + "P", 1, [128, S], BF16),
                    PT=self.rot(pfx + "PTa", 2, [128, 1024], BF16))

    def dsa_setup(self, FMS, TMB, TMF, YT):
        S = self.S
        NT = S // 128
        c = dict(FMS=FMS, YT=YT)
        c["dk"] = self.sb("dk", [128, S], BF16)
        self.dma(c["dk"][:], V(FMS.h[4 * 128:5 * 128, :], FMS.name))
        ikr = self.sb("ikr", [128, S], BF16)
        self.dma(ikr[:], V(FMS.h[7 * 128:8 * 128, :], FMS.name))
        c["ikm"] = self.sb("ikm", [128, 4, S], BF16)
        for g in range(4):
            self.ts(c["ikm"][:, g, :], ikr[:], self.meta[:, 13 + g:14 + g], None, ALU.mult, eng=("pool" if g % 2 else "dve"))
        c["Vt"] = self.sb("Vt", [128, NT, 64], BF16)
        self.dma(c["Vt"][:], V(TMB.h[:, 512:576].rearrange("(n p) c -> p n c", p=128), TMB.name))
        iw = self.sb("iw", [128, NT, 8], F32)
        self.dma(iw[:], V(TMF.h[:, 0:8].rearrange("(n p) c -> p n c", p=128), TMF.name))
        c["absw"] = self.sb("absw", [128, NT, 8], F32)
        self.act(c["absw"][:], iw[:], AF.Abs, scale=1.0 / 16)
        c["sgn"] = self.sb("sgn", [128, NT, 8], F32)
        self.ts(c["sgn"][:], iw[:], 0.0, 2.0, ALU.is_ge, ALU.mult)
        self.ts(c["sgn"][:], c["sgn"][:], -1.0, None, ALU.add)
        c["I"] = self.sb("I", [128, S], F32)
        c["Ssb"] = self.sb("dSsb", [128, S], F32)
        c["kk"] = self.attn_keys("d")
        c["q"] = self.rot("dqi", 2, [128, 4, 128], BF16)
        c["tmp"] = self.rot("tmpr", 2, [128, 512], F32)
        c["st"] = self.rot("dst", 2, [128, 16], F32)
        c["Rk"] = self.rot("Rk", 2, [128, 20], F32)
        c["nm"] = self.rot("dnm", 2, [128, 2], F32)
        c["c2"] = self.rot("dc2", 2, [128, 2], F32)
        c["o"] = self.rot("do", 2, [128, 256], F32)
        c["y"] = self.rot("dy", 2, [128, 256], BF16)
        c["yTs"] = self.rot("dyTs", 2, [128, 256], BF16)
        return c

    def dsa_tile(self, c, i):
        FMS = c["FMS"]
        I = c["I"]
        Ssb = c["Ssb"]
        nk = 128 * (i + 1)
        nkc = (nk + 511) // 512
        q = self.nx(c["q"])
        self.dma(q[:, 0:2, :], self.fm_rows(FMS, 2, 2, i * 128, (i + 1) * 128))
        self.dma(q[:, 2:4, :], self.fm_rows(FMS, 5, 2, i * 128, (i + 1) * 128))
        for kc in range(nkc):
            c0 = kc * 512
            cols = min(512, nk - c0)
            for h in range(8):
                ps = self.next_ps()
                self.mm(ps[:, 0:cols], q[:, 2 + h // 4, :], c["ikm"][:, h % 4, c0:c0 + cols])
                tmp = self.nx(c["tmp"])
                self.act(tmp[:, 0:cols], ps[:, 0:cols], AF.Relu, scale=c["absw"][:, i, h:h + 1])
                if h == 0:
                    self.ts(I[:, c0:c0 + cols], tmp[:, 0:cols], c["sgn"][:, i, 0:1], None, ALU.mult)
                else:
                    self.stt(I[:, c0:c0 + cols], tmp[:, 0:cols], c["sgn"][:, i, h:h + 1], I[:, c0:c0 + cols], ALU.mult, ALU.add)
        if nk > self.n_keep:
            st = self.nx(c["st"])
            junk = self.nx(c["kk"]["P"])
            self.redabs(st[:, 0:1], I[:, 0:nk])
            self.ts(st[:, 0:1], st[:, 0:1], 1e-20, None, ALU.max)
            self.tt(I[:, nk - 128:nk], I[:, nk - 128:nk], self.cneg30[:], ALU.add)
            Rk = self.nx(c["Rk"])
            self.ts(Rk[:], self.rkc[:], st[:, 0:1], None, ALU.mult)
            self.ts(st[:, 1:2], st[:, 0:1], -1.0, None, ALU.mult)
            n1 = (nk // 2 + 127) // 128 * 128
            n2 = nk - n1
            thr_c = self.n_keep - 0.5 - n2 / 2.0
            for k in range(NBIS):
                self.tt(st[:, 2:3], st[:, 1:2], Rk[:, k:k + 1], ALU.add)
                nm = self.nx(c["nm"])
                c2 = self.nx(c["c2"])
                self.ts(nm[:, 0:1], st[:, 2:3], -1.0, None, ALU.mult)
                self.act(Ssb[:, n1:nk], I[:, n1:nk], AF.Sign, bias=nm[:, 0:1], accum=c2[:, 0:1])
                self.ts(junk[:, 0:n1], I[:, 0:n1], st[:, 2:3], None, ALU.is_ge, ALU.add, accum=st[:, 3:4])
                self.stt(st[:, 4:5], c2[:, 0:1], 0.5, st[:, 3:4], ALU.mult, ALU.add)
                self.ts(st[:, 4:5], st[:, 4:5], thr_c, None, ALU.is_ge)
                self.stt(st[:, 1:2], st[:, 4:5], Rk[:, k:k + 1], st[:, 1:2], ALU.mult, ALU.add)
            self.ts(I[:, 0:nk], I[:, 0:nk], st[:, 1:2], 1000.0, ALU.is_ge, ALU.mult)
        else:
            self.ts(I[:, 0:nk], I[:, 0:nk], 0.0, 1000.0, ALU.mult, ALU.add)
            self.tt(I[:, nk - 128:nk], I[:, nk - 128:nk], self.cneg2k[:], ALU.add)
        o = self.nx(c["o"])
        for h in range(4):
            base = 64 * (h % 2)
            cq = h // 2
            for kc in range(nkc):
                c0 = kc * 512
                cols = min(512, nk - c0)
                ps = self.next_ps()
                self.mm(ps[:, 0:cols], q[base:base + 64, cq, :], c["dk"][base:base + 64, c0:c0 + cols])
                self.stt(Ssb[:, c0:c0 + cols], ps[:, 0:cols], 0.125, I[:, c0:c0 + cols], ALU.mult, ALU.add)
            self.softmax_pv(Ssb[:, 0:nk], nk, c["Vt"], 0, o[:, 64 * h:64 * h + 64], c["kk"])
        y = self.nx(c["y"])
        self.copy(y[:], o[:], eng="act")
        self.store_yT(y, c["YT"], 1, i, c["yTs"])

    def nsa_setup(self, FMS, TMB, TMF, cmp_w1, cmp_w2, cmp_pos, YT):
        S = self.S
        NT = S // 128
        NB = self.NB
        NCP = self.NCP
        NC = (S - 32) // 16 + 1
        NCT = NCP // 128
        c = dict(FMS=FMS, YT=YT)
        c["ksT"] = self.sb("ksT", [128, S], BF16)
        self.dma(c["ksT"][:], V(FMS.h[11 * 128:12 * 128, :], FMS.name))
        c["kwT"] = self.sb("kwT", [128, S], BF16)
        self.dma(c["kwT"][:], V(FMS.h[12 * 128:13 * 128, :], FMS.name))
        c["Vs"] = self.sb("Vs", [128, NT, 64], BF16)
        self.dma(c["Vs"][:], V(TMB.h[:, 576:640].rearrange("(n p) c -> p n c", p=128), TMB.name))
        c["Vw"] = self.sb("Vw", [128, NT, 64], BF16)
        self.dma(c["Vw"][:], V(TMB.h[:, 640:704].rearrange("(n p) c -> p n c", p=128), TMB.name))
        c["ngt"] = self.sb("ngt", [128, NT, 12], F32)
        self.dma(c["ngt"][:], V(TMF.h[:, 8:20].rearrange("(n p) c -> p n c", p=128), TMF.name))
        kcmp = self.sb("kcmp", [128, NCP], BF16)
        vcmp = self.sb("vcmp", [128, NCT, 64], BF16)
        c["kcmp"] = kcmp
        c["vcmp"] = vcmp
        c["Ssb"] = self.sb("nSsb", [128, S], F32)
        c["Sw"] = self.sb("Sw", [128, 640], F32)
        c["kk"] = self.attn_keys("n")
        save = self.sb_cur
        srcT = self.sb("srcT", [128, S], BF16)
        w1 = self.sb("w1", [64, 32, 64], BF16)
        w2 = self.sb("w2", [64, 128], BF16)
        posT = self.sb("posT", [64, 32], F32)
        posb = self.sb("posb", [64, 32], BF16)
        cst = self.sb("cst", [64, 1], F32)
        u = self.sb("u", [64, NCP], F32)
        u2 = self.sb("u2", [64, NCP], F32)
        gl = self.sb("gl", [64, NCP], BF16)
        for i in range(2):
            self.dma(srcT[:], V(FMS.h[(10 + 3 * i) * 128:(11 + 3 * i) * 128, :], FMS.name))
            self.dma(w1[:], V(cmp_w1.h[i].rearrange("(l d) f -> d l f", d=64), cmp_w1.name), eng="pool")
            self.dma(w2[:, 0:64], V(cmp_w2.h[i], cmp_w2.name), eng="pool")
            self.dma(w2[:, 64:128], V(cmp_w2.h[i], cmp_w2.name), eng="pool")
            self.dma_s(posT[:], V(cmp_pos.h[i].rearrange("l d -> d l"), cmp_pos.name))
            self.copy(posb[:], posT[:])
            psc = self.next_ps()
            for l in range(32):
                self.mm(psc[0:64, 0:1], w1[:, l, :], posb[:, l:l + 1], start=(l == 0), stop=(l == 31))
            self.copy(cst[:], psc[0:64, 0:1])
            psh = self.next_ps()
            for l in range(32):
                self.mm(psh[0:64, 0:NC], w1[:, l, :], srcT[0:64, l:l + 16 * (NC - 1) + 1:16], start=(l == 0), stop=(l == 31))
            self.memset(u[:], 0.0)
            self.act(u[:, 0:NC], psh[0:64, 0:NC], AF.Identity, bias=cst[:, 0:1])
            self.tt(u2[:], u[:], u[:], ALU.mult)
            self.tt(u2[:], u2[:], u[:], ALU.mult)
            self.stt(u2[:], u2[:], 0.044715, u[:], ALU.mult, ALU.add)
            self.act(u2[:], u2[:], AF.Tanh, scale=0.7978845608028654)
            self.ts(u2[:], u2[:], 1.0, 0.5, ALU.add, ALU.mult)
            self.tt(gl[:], u2[:], u[:], ALU.mult)
            if i == 0:
                pso = self.next_ps()
                self.mm(pso[:, 0:NCP], w2[:, :], gl[:, :])
                self.copy(kcmp[:], pso[:, 0:NCP])
            else:
                for ct in range(NCT):
                    pso = self.next_ps()
                    self.mm(pso[:, 0:64], gl[:, ct * 128:(ct + 1) * 128], w2[:, 0:64])
                    self.copy(vcmp[:, ct, :], pso[:, 0:64])
        self.sb_cur = save
        self.P.barrier()
        c["q"] = self.rot("nqi", 2, [128, 2, 128], BF16)
        for nm, shp, dt_ in (("vis", [128, NCP], F32), ("pns", [128, NCP], F32), ("pn", [128, NCP], F32), ("Sc", [128, NCP], F32),
                             ("Pc", [128, NCP], F32), ("pnb", [128, NCP], BF16), ("PTc", [128, NCP], BF16), ("pnT", [128, NCP], F32),
                             ("cst2", [128, 8], F32), ("am", [128, NB], F32), ("imp", [128, NB], F32), ("imp2", [128, NB], F32), ("m8", [128, 16], F32),
                             ("selm", [128, NB], F32), ("oc", [128, 256], F32), ("os", [128, 256], F32), ("ow", [128, 256], F32),
                             ("gs", [128, 12], F32), ("o", [128, 256], F32), ("y", [128, 256], BF16), ("yTs", [128, 256], BF16)):
            c[nm] = self.rot("n" + nm, 2, shp, dt_)
        return c

    def nsa_tile(self, c, i):
        NB = self.NB
        NCP = self.NCP
        NCT = NCP // 128
        FMS = c["FMS"]
        Ssb = c["Ssb"]
        Sw = c["Sw"]
        kcmp = c["kcmp"]
        vcmp = c["vcmp"]

        def h4(t_):
            return V(t_.h[:, 0:256].rearrange("p (h e) -> p h e", h=4), t_.name)

        nk = 128 * (i + 1)
        nkc = (nk + 511) // 512
        nq = self.nx(c["q"])
        self.dma(nq[:], self.fm_rows(FMS, 8, 2, i * 128, (i + 1) * 128))
        vis = self.nx(c["vis"])
        self.memset(vis[:], 0.0)
        self.aselect(vis[:], vis[:], [[-16, NCP]], ALU.is_ge, -1000.0, 128 * i - 31, 1)
        pns = self.nx(c["pns"])
        oc = self.nx(c["oc"])
        osl = self.nx(c["os"])
        ow = self.nx(c["ow"])
        for h in range(4):
            base = 64 * (h % 2)
            cq = h // 2
            ps = self.next_ps()
            self.mm(ps[:, 0:NCP], nq[base:base + 64, cq, :], kcmp[base:base + 64, :])
            Sc = self.nx(c["Sc"])
            self.stt(Sc[:], ps[:, 0:NCP], 0.125, vis[:], ALU.mult, ALU.add)
            st = self.nx(c["cst2"])
            self.red(st[:, 0:1], Sc[:], ALU.max)
            self.ts(st[:, 0:1], st[:, 0:1], -500.0, -1.0, ALU.max, ALU.mult)
            Pc = self.nx(c["Pc"])
            self.act(Pc[:], Sc[:], AF.Exp, bias=st[:, 0:1], accum=st[:, 1:2])
            self.ts(st[:, 2:3], st[:, 1:2], 1e-30, None, ALU.max)
            self.recip(st[:, 3:4], st[:, 2:3])
            pn = pns if h == 0 else self.nx(c["pn"])
            self.ts(pn[:], Pc[:], st[:, 3:4], None, ALU.mult)
            pnb = self.nx(c["pnb"])
            self.copy(pnb[:], pn[:], eng="act")
            if h > 0:
                self.tt(pns[:], pns[:], pn[:], ALU.add, eng="pool")
            pb = self.next_psb()
            for ct in range(NCT):
                self.tr(pb[:, ct * 128:(ct + 1) * 128], pnb[:, ct * 128:(ct + 1) * 128], self.ident[:])
            PTc = self.nx(c["PTc"])
            self.copy(PTc[:], pb[:, 0:NCP], eng="act")
            po = self.next_ps()
            for ct in range(NCT):
                self.mm(po[:, 0:64], PTc[:, ct * 128:(ct + 1) * 128], vcmp[:, ct, :], start=(ct == 0), stop=(ct == NCT - 1))
            self.copy(oc[:, 64 * h:64 * h + 64], po[:, 0:64], eng="act")
        selm = self.nx(c["selm"])
        if NB > 16:
            pf = self.next_ps()
            for ct in range(NCT):
                self.tr(pf[:, ct * 128:(ct + 1) * 128], pns[:, ct * 128:(ct + 1) * 128], self.ident_f[:])
            pnT = self.nx(c["pnT"])
            self.copy(pnT[:], pf[:, 0:NCP], eng="act")
            pi = self.next_ps()
            for ct in range(NCT):
                self.mm(pi[:, 0:NB], pnT[:, ct * 128:(ct + 1) * 128], self.ovl[:, ct, :], start=(ct == 0), stop=(ct == NCT - 1))
            am = self.nx(c["am"])
            self.memset(am[:], 0.0)
            for half in range(2):
                cur = 2 * i + half
                r0 = 64 * half
                v_ = am[r0:r0 + 64, :]
                self.aselect(v_, v_, [[-1, NB]], ALU.is_ge, -1e30, cur, 0)
                self.memset(am[r0:r0 + 64, 0:1], 1e30)
                self.memset(am[r0:r0 + 64, cur:cur + 1], 1e30)
                if cur >= 1:
                    self.memset(am[r0:r0 + 64, cur - 1:cur], 1e30)
            imp = self.nx(c["imp"])
            self.tt(imp[:], pi[:, 0:NB], am[:], ALU.add)
            m8 = self.nx(c["m8"])
            self.vmax(m8[:, 0:8], imp[:])
            imp2 = self.nx(c["imp2"])
            self.match_replace(imp2[:], m8[:, 0:8], imp[:], -3.0e38)
            self.vmax(m8[:, 8:16], imp2[:])
            self.ts(selm[:], imp[:], m8[:, 15:16], 1000.0, ALU.is_ge, ALU.mult)
        else:
            self.memset(selm[:], 1000.0)
        for h in range(4):
            base = 64 * (h % 2)
            cq = h // 2
            for kc in range(nkc):
                c0 = kc * 512
                cols = min(512, nk - c0)
                nb_ = cols // 64
                ps = self.next_ps()
                self.mm(ps[:, 0:cols], nq[base:base + 64, cq, :], c["ksT"][base:base + 64, c0:c0 + cols])
                self.stt(V(Ssb.h[:, c0:c0 + cols].rearrange("p (b e) -> p b e", e=64), Ssb.name),
                         V(ps.h[:, 0:cols].rearrange("p (b e) -> p b e", e=64), ps.name), 0.125,
                         V(selm.h[:, c0 // 64:c0 // 64 + nb_].unsqueeze(2).to_broadcast([128, nb_, 64]), selm.name),
                         ALU.mult, ALU.add)
            self.tt(Ssb[:, nk - 128:nk], Ssb[:, nk - 128:nk], self.cneg2k[:], ALU.add)
            self.softmax_pv(Ssb[:, 0:nk], nk, c["Vs"], 0, osl[:, 64 * h:64 * h + 64], c["kk"])
        k0 = max(0, i * 128 - 512)
        nkw = nk - k0
        boff = 640 - nkw
        for h in range(4):
            base = 64 * (h % 2)
            cq = h // 2
            for c0 in range(0, nkw, 512):
                cols = min(512, nkw - c0)
                ps = self.next_ps()
                self.mm(ps[:, 0:cols], nq[base:base + 64, cq, :], c["kwT"][base:base + 64, k0 + c0:k0 + c0 + cols])
                self.stt(Sw[:, c0:c0 + cols], ps[:, 0:cols], 0.125, self.band[:, boff + c0:boff + c0 + cols], ALU.mult, ALU.add)
            self.softmax_pv(Sw[:, 0:nkw], nkw, c["Vw"], k0 // 128, ow[:, 64 * h:64 * h + 64], c["kk"])
        gs = self.nx(c["gs"])
        self.act(gs[:], c["ngt"][:, i, :], AF.Sigmoid)
        o = self.nx(c["o"])

        def gbc(j):
            return V(gs.h[:, j:12:3].unsqueeze(2).to_broadcast([128, 4, 64]), gs.name)
        self.tt(h4(o), h4(oc), gbc(0), ALU.mult)
        self.tt(h4(osl), h4(osl), gbc(1), ALU.mult)
        self.tt(o[:], o[:], osl[:], ALU.add)
        self.tt(h4(ow), h4(ow), gbc(2), ALU.mult)
        self.tt(o[:], o[:], ow[:], ALU.add)
        y = self.nx(c["y"])
        self.copy(y[:], o[:], eng="act")
        self.store_yT(y, c["YT"], 2, i, c["yTs"])

    def phase_dsa_nsa(self, FMS, TMB, TMF, cmp_w1, cmp_w2, cmp_pos, YT):
        S = self.S
        NT = S // 128
        self.phase_begin()
        cd = self.dsa_setup(FMS, TMB, TMF, YT)
        cn = self.nsa_setup(FMS, TMB, TMF, cmp_w1, cmp_w2, cmp_pos, YT)
        P = self.P
        for i in range(NT):
            self.ps_set = (0, 3)
            self.psb_set = (0, 1)
            P.capture = []
            self.dsa_tile(cd, i)
            A = P.capture
            self.ps_set = (3, 2)
            self.psb_set = (1, 1)
            P.capture = []
            self.nsa_tile(cn, i)
            B = P.capture
            P.capture = None
            self.ps_set = (0, 5)
            self.psb_set = (0, 2)
            ia = ib = 0
            na, nb = len(A), len(B)
            while ia < na or ib < nb:
                if ib >= nb or (ia < na and ia * nb <= ib * na):
                    P.add(*A[ia][0], **A[ia][1])
                    ia += 1
                else:
                    P.add(*B[ib][0], **B[ib][1])
                    ib += 1

    def phase_merge(self, x_in, xT_in, w_in_l, w_branch, w_out, ln_g, ln_b, YT, x_out, xT_out):
        S = self.S
        self.phase_begin()
        wg = self.load_w("wg", lambda k: V(w_in_l.h[k * 128:(k + 1) * 128, 3128:7224], w_in_l.name), NKC, 4096)
        wb = self.sb("wb", [128, 4, 2, 1024], BF16)
        for n in range(4):
            for kk_ in range(2):
                self.dma(wb[:, n, kk_, :], V(w_branch.h[n, kk_ * 128:(kk_ + 1) * 128, :], w_branch.name), eng="pool")
        wo = self.load_w("wo", lambda k: V(w_out.h[k * 128:(k + 1) * 128, :], w_out.name), NKC, D)
        g_bc, b_bc, scr = self.ln_setup(ln_g, ln_b)
        xt = self.sb("xT", [128, NKC, 512], BF16)
        yt = self.sb("yT", [128, 8, 512], BF16)
        mT = self.sb("mT", [128, 8, 512], BF16)
        acc_k = self.rot("acc", 2, [128, 512], F32)
        sg_k = self.rot("sg", 2, [128, 512], F32)
        tmp_k = self.rot("tmp", 2, [128, 512], F32)
        xr = [self.sb("xr%d" % i, [128, D], F32) for i in range(2)]
        rq = self.sb("r", [128, D], F32)
        for t in range(S // 512):
            ss = slice(t * 512, (t + 1) * 512)
            self.dma(xt[:], V(xT_in.h.rearrange("(k p) s -> p k s", p=128)[:, :, ss], xT_in.name))
            self.dma(yt[:], V(YT.h[:, ss].rearrange("(c p) s -> p c s", p=128), YT.name))
            for dc in range(8):
                acc = self.nx(acc_k)
                for n in range(4):
                    pg = self.next_ps()
                    for k in range(NKC):
                        self.mm(pg[:], wg[:, k, n * 1024 + dc * 128:n * 1024 + (dc + 1) * 128], xt[:, k, :],
                                start=(k == 0), stop=(k == NKC - 1))
                    pp = self.next_ps()
                    for k2 in range(2):
                        self.mm(pp[:], wb[:, n, k2, dc * 128:(dc + 1) * 128], yt[:, 2 * n + k2, :], start=(k2 == 0), stop=(k2 == 1))
                    sg = self.nx(sg_k)
                    self.act(sg[:], pg[:], AF.Sigmoid)
                    if n == 0:
                        self.tt(acc[:], sg[:], pp[:], ALU.mult)
                    else:
                        tmp = self.nx(tmp_k)
                        self.tt(tmp[:], sg[:], pp[:], ALU.mult)
                        self.tt(acc[:], acc[:], tmp[:], ALU.add, eng="pool")
                self.copy(mT[:, dc, :], acc[:], eng="act")
            for q in range(4):
                t0 = t * 512 + q * 128
                xq = xr[q % 2]
                self.dma(xq[:], x_in[t0:t0 + 128, :])
                for half in range(2):
                    hs = slice(half * 512, (half + 1) * 512)
                    pd = self.next_ps()
                    for dc in range(8):
                        self.mm(pd[:], mT[:, dc, q * 128:(q + 1) * 128], wo[:, dc, hs], start=(dc == 0), stop=(dc == 7))
                    self.act(xq[:, hs], xq[:, hs], AF.Copy, scale=ALPHA)
                    self.stt(rq[:, hs], pd[:], 1.0, xq[:, hs], ALU.mult, ALU.add)
                self.finish_tile(rq, g_bc, b_bc, x_out, xT_out, t0, scr)

    def phase_xattn(self, x_in, xT_in, mem, wq_d, wkv_d, wo_d, ln_g, ln_b, x_out, xT_out):
        S = self.S
        self.phase_begin()
        wq = self.load_w("wq", lambda k: V(wq_d.h[k * 128:(k + 1) * 128, :], wq_d.name), NKC, D)
        wkv = self.load_w("wkv", lambda k: V(wkv_d.h[k * 128:(k + 1) * 128, :], wkv_d.name), NKC, 2 * D)
        wo = self.load_w("wo", lambda k: V(wo_d.h[k * 128:(k + 1) * 128, :], wo_d.name), NKC, D)
        g_bc, b_bc, scr = self.ln_setup(ln_g, ln_b)
        memT = self.sb("memT", [128, 8, 256], BF16)
        mr = self.sb("mr", [128, D], F32)
        mb = self.sb("mb", [128, D], BF16)
        for mt in range(2):
            self.dma(mr[:], mem[mt * 128:(mt + 1) * 128, :])
            self.copy(mb[:], mr[:], eng="act")
            pb = self.next_psb()
            for k in range(8):
                self.tr(pb[:, k * 128:(k + 1) * 128], mb[:, k * 128:(k + 1) * 128], self.ident[:])
            self.copy(memT[:, :, mt * 128:(mt + 1) * 128], V(pb.h[:, :].rearrange("p (k t) -> p k t", k=8), pb.name))
        KT = self.sb("KT", [128, 8, 256], BF16)
        for c in range(8):
            ps = self.next_ps()
            for k in range(NKC):
                self.mm(ps[:, 0:256], wkv[:, k, c * 128:(c + 1) * 128], memT[:, k, :], start=(k == 0), stop=(k == NKC - 1))
            self.copy(KT[:, c, :], ps[:, 0:256], eng=("act" if c % 2 else "dve"))
        Vm = self.sb("Vm", [128, 2, D], BF16)
        for mt in range(2):
            for half in range(2):
                ps = self.next_ps()
                for k in range(NKC):
                    self.mm(ps[:], memT[:, k, mt * 128:(mt + 1) * 128], wkv[:, k, D + half * 512:D + (half + 1) * 512],
                            start=(k == 0), stop=(k == NKC - 1))
                self.copy(Vm[:, mt, half * 512:(half + 1) * 512], ps[:], eng=("act" if half else "dve"))
        xt = self.sb("xT", [128, NKC, 512], BF16)
        qT = self.sb("qT", [128, 8, 512], BF16)
        Pf_k = self.rot("Pf", 2, [128, 4, 256], F32)
        Pb_k = self.rot("Pb", 2, [128, 4, 256], BF16)
        PT_k = self.rot("PTx", 2, [128, 8, 128], BF16)
        oT_k = self.rot("oT", 2, [128, 8, 128], BF16)
        st_k = self.rot("xst", 2, [128, 16], F32)
        xr = [self.sb("xr%d" % i, [128, D], F32) for i in range(2)]
        rq = self.sb("r", [128, D], F32)
        SC = 1.0 / 16
        for t in range(S // 512):
            ss = slice(t * 512, (t + 1) * 512)
            self.dma(xt[:], V(xT_in.h.rearrange("(k p) s -> p k s", p=128)[:, :, ss], xT_in.name))
            for c in range(8):
                ps = self.next_ps()
                for k in range(NKC):
                    self.mm(ps[:], wq[:, k, c * 128:(c + 1) * 128], xt[:, k, :], start=(k == 0), stop=(k == NKC - 1))
                self.copy(qT[:, c, :], ps[:], eng=("act" if c % 2 else "dve"))
            for q in range(4):
                t0 = t * 512 + q * 128
                tq = slice(q * 128, (q + 1) * 128)
                pss = [self.next_ps(), self.next_ps()]
                st = self.nx(st_k)
                Pf = self.nx(Pf_k)
                for h in range(4):
                    pv = pss[h // 2][:, (h % 2) * 256:(h % 2) * 256 + 256]
                    for cc in range(2):
                        self.mm(pv, qT[:, 2 * h + cc, tq], KT[:, 2 * h + cc, :], start=(cc == 0), stop=(cc == 1))
                    self.red(st[:, h:h + 1], pv, ALU.max)
                    self.ts(st[:, 4 + h:5 + h], st[:, h:h + 1], -SC, None, ALU.mult)
                    self.act(Pf[:, h, :], pv, AF.Exp, bias=st[:, 4 + h:5 + h], scale=SC, accum=st[:, 8 + h:9 + h])
                self.recip(st[:, 12:16], st[:, 8:12])
                Pb = self.nx(Pb_k)
                self.tt(Pb[:], Pf[:], V(st.h[:, 12:16].unsqueeze(2).to_broadcast([128, 4, 256]), st.name), ALU.mult)
                pb = self.next_psb()
                for h in range(4):
                    for mc in range(2):
                        j = 2 * h + mc
                        self.tr(pb[:, j * 128:(j + 1) * 128], Pb[:, h, mc * 128:(mc + 1) * 128], self.ident[:])
                PT = self.nx(PT_k)
                self.copy(PT[:], V(pb.h[:, :].rearrange("p (j t) -> p j t", j=8), pb.name))
                oT = self.nx(oT_k)
                pso = [self.next_ps(), self.next_ps()]
                for h in range(4):
                    for dc in range(2):
                        j = 2 * h + dc
                        pv = pso[j // 4][:, (j % 4) * 128:(j % 4) * 128 + 128]
                        for mc in range(2):
                            self.mm(pv, Vm[:, mc, h * 256 + dc * 128:h * 256 + (dc + 1) * 128], PT[:, 2 * h + mc, :],
                                    start=(mc == 0), stop=(mc == 1))
                for j4 in range(2):
                    self.copy(oT[:, 4 * j4:4 * j4 + 4, :], V(pso[j4].h[:, :].rearrange("p (j t) -> p j t", j=4), pso[j4].name),
                              eng=("act" if j4 else "dve"))
                xq = xr[q % 2]
                self.dma(xq[:], x_in[t0:t0 + 128, :])
                for half in range(2):
                    hs = slice(half * 512, (half + 1) * 512)
                    pd = self.next_ps()
                    for c in range(8):
                        self.mm(pd[:], oT[:, c, :], wo[:, c, hs], start=(c == 0), stop=(c == 7))
                    self.act(xq[:, hs], xq[:, hs], AF.Copy, scale=ALPHA)
                    self.stt(rq[:, hs], pd[:], 1.0, xq[:, hs], ALU.mult, ALU.add)
                self.finish_tile(rq, g_bc, b_bc, x_out, xT_out, t0, scr)


OFF = dict(r_q=0, r_k=128, r_v=256, r_g=512, d_q=768, d_k=1024, d_v=1088, i_q=1152, i_k=1408, i_w=1440,
           n_q=1448, n_kc=1704, n_vc=1768, n_ks=1832, n_vs=1896, n_kw=1960, n_vw=2024, n_g=2088,
           s_z=2100, s_xbc=2356, s_dt=3124, br_g=3128)


def _partner(i, headdim, rot):
    half = rot // 2
    j = i % headdim
    b = i - j
    if j < half:
        return b + j + half
    if j < rot:
        return b + j - half
    return i


def build_colidx():
    cols = []

    def roped(name, width, headdim, rot, lo=0, rep=1):
        loc = []
        for r in range(rep):
            loc += list(range(lo, lo + width))
        assert len(loc) == 128
        a = [OFF[name] + i for i in loc]
        b = [OFF[name] + _partner(i, headdim, rot) for i in loc]
        cols.extend(a)
        cols.extend(b)

    roped("r_q", 128, 32, 32)
    roped("r_k", 128, 32, 32)
    roped("d_q", 128, 64, 16, 0)
    roped("d_q", 128, 64, 16, 128)
    roped("d_k", 64, 64, 16, 0, 2)
    roped("i_q", 128, 32, 8, 0)
    roped("i_q", 128, 32, 8, 128)
    roped("i_k", 32, 32, 8, 0, 4)
    roped("n_q", 128, 64, 16, 0)
    roped("n_q", 128, 64, 16, 128)
    roped("n_kc", 64, 64, 16, 0, 2)
    roped("n_ks", 64, 64, 16, 0, 2)
    roped("n_kw", 64, 64, 16, 0, 2)
    cols.extend([OFF["n_vc"] + i for i in range(64)] * 2)
    cols.extend([OFF["s_xbc"] + i for i in range(768)])
    for name, w in (("r_v", 256), ("r_g", 256), ("d_v", 64), ("n_vs", 64), ("n_vw", 64), ("s_z", 256),
                    ("i_w", 8), ("n_g", 12), ("s_dt", 4)):
        cols.extend([OFF[name] + i for i in range(w)])
    return np.asarray(cols, dtype=np.int64)


ROPED_TABLES = [0, 1, 2, 2, 2, 3, 3, 3, 2, 2, 2, 2, 2]
TM0 = (2 * len(ROPED_TABLES) + 7) * 128
NCOL2 = TM0 + 984
NBIS = 18
RET_LNG = [math.log1p(-2.0 ** (-5 - h)) for h in range(4)]


def host_consts(S):
    meta = np.zeros((128, 32), np.float32)

    def fill(t, headdim, rot, theta, scale):
        half = rot // 2
        inv = np.power(np.float32(theta), (-2.0 * np.arange(half, dtype=np.float32) / np.float32(rot)).astype(np.float32)).astype(np.float32)
        for p in range(128):
            i = p % headdim
            if i < rot:
                meta[p, t] = inv[i % half]
                meta[p, 4 + t] = scale
                meta[p, 8 + t] = -scale if i < half else scale
            else:
                meta[p, t] = 0.0
                meta[p, 4 + t] = 1.0
                meta[p, 8 + t] = 0.0

    fill(0, 32, 32, 10000.0, 1.0)
    fill(1, 32, 32, 10000.0, 32.0 ** -0.5)
    fill(2, 64, 16, 500000.0, 1.0)
    fill(3, 32, 8, 500000.0, 1.0)
    for p in range(128):
        meta[p, 12] = RET_LNG[p // 32]
        meta[p, 13 + p // 32] = 1.0
    bdm = np.zeros((128, 256), np.float32)
    for p in range(128):
        bdm[p, 64 * (p // 32):64 * (p // 32) + 64] = 1.0
    NC = (S - 32) // 16 + 1
    NCP = (NC + 127) // 128 * 128
    NB = S // 64
    ovl = np.zeros((NCP, NB), np.float32)
    for c in range(NC):
        for j in range(NB):
            ovl[c, j] = max(min(16 * c + 32, 64 * j + 64) - max(16 * c, 64 * j), 0) / 32.0
    return meta, bdm, ovl, NB, NCP


STAGES = ["ffn1", "inproj", "ret", "ssd", "dsa", "nsa", "merge", "xattn", "ffn2"]


def build(S, depth=DEPTH, stop_after=None):
    kb = KB(S, depth, stop_after)
    meta_np, bdm_np, ovl_np, NB, NCP = host_consts(S)
    kb.n_keep = min(256, S // 4)
    EI = "ExternalInput"
    x = kb.dram("x", [S, D], F32, kind=EI)
    mem = kb.dram("mem", [N_MEM, D], F32, kind=EI)
    ln_g = kb.dram("ln_g", [DEPTH, 4, D], F32, kind=EI)
    ln_b = kb.dram("ln_b", [DEPTH, 4, D], F32, kind=EI)
    f1gu = kb.dram("ffn1_w_gu", [DEPTH, D, 2 * DFF], F32, kind=EI)
    f1dn = kb.dram("ffn1_w_down", [DEPTH, DFF, D], F32, kind=EI)
    w_in = kb.dram("w_in", [DEPTH, D, 7224], F32, kind=EI)
    w2 = kb.dram("w2", [DEPTH, D, NCOL2], F32, kind=EI)
    cmp_w1 = kb.dram("cmp_w1", [DEPTH, 2, 2048, 64], F32, kind=EI)
    cmp_w2 = kb.dram("cmp_w2", [DEPTH, 2, 64, 64], F32, kind=EI)
    cmp_pos = kb.dram("cmp_pos", [DEPTH, 2, 32, 64], F32, kind=EI)
    conv_w = kb.dram("conv_w", [DEPTH, 4, 768], F32, kind=EI)
    conv_b = kb.dram("conv_b", [DEPTH, 768], F32, kind=EI)
    dt_bias = kb.dram("dt_bias", [DEPTH, 4], F32, kind=EI)
    a_log = kb.dram("a_log", [DEPTH, 4], F32, kind=EI)
    d_skip = kb.dram("d_skip", [DEPTH, 4], F32, kind=EI)
    norm_g = kb.dram("ssm_norm_g", [DEPTH, 256], F32, kind=EI)
    w_branch = kb.dram("w_branch", [DEPTH, 4, 256, D], F32, kind=EI)
    w_out = kb.dram("w_out", [DEPTH, D, D], F32, kind=EI)
    xwq = kb.dram("xattn_wq", [DEPTH, D, D], F32, kind=EI)
    xwkv = kb.dram("xattn_wkv", [DEPTH, D, 2 * D], F32, kind=EI)
    xwo = kb.dram("xattn_wo", [DEPTH, D, D], F32, kind=EI)
    f2gu = kb.dram("ffn2_w_gu", [DEPTH, D, 2 * DFF], F32, kind=EI)
    f2dn = kb.dram("ffn2_w_down", [DEPTH, DFF, D], F32, kind=EI)
    meta = kb.dram("meta", [128, 32], F32, kind=EI)
    bdm = kb.dram("bdm", [128, 256], F32, kind=EI)
    ovl = kb.dram("ovl", [NCP, NB], F32, kind=EI)
    out = kb.dram("out", [S, D], F32)
    xTa = kb.dram("xTa", [D, S], BF16)
    xTb = kb.dram("xTb", [D, S], BF16)
    xa = kb.dram("xa", [S, D], F32)
    xb2 = kb.dram("xb2", [S, D], F32)
    FMS = kb.dram("FMS", [20 * 128, S], BF16)
    TMB = kb.dram("TMB", [S, 960], BF16)
    TMF = kb.dram("TMF", [S, 24], F32)
    YT = kb.dram("YT", [1024, S], BF16)
    ROPE = kb.dram("ROPE", [4, 2, 128, S], F32)
    kb.setup()
    kb.setup_consts(meta, bdm, ovl, NB, NCP)
    kb.phase_rope(ROPE)
    kb.phase_transpose_in(x, xTa)

    def L(t, *idx):
        return T(t.h[idx], t.name, True)

    done = False
    xin = x
    for l in range(depth):
        last = (l == depth - 1)

        def stop(name):
            return stop_after == (l, name)
        kb.phase_ffn(xin, xTa, L(f1gu, l), L(f1dn, l), L(ln_g, l, 0), L(ln_b, l, 0), xa, xTb)
        if stop("ffn1"):
            break
        kb.phase_inproj(xTb, L(w2, l), ROPE, FMS, TMB, TMF)
        if stop("inproj"):
            break
        kb.phase_ret(FMS, TMB, YT)
        if stop("ret"):
            break
        kb.phase_ssd(FMS, TMB, TMF, L(conv_w, l), L(conv_b, l), L(dt_bias, l), L(a_log, l), L(d_skip, l), L(norm_g, l), YT)
        if stop("ssd"):
            break
        kb.phase_dsa_nsa(FMS, TMB, TMF, L(cmp_w1, l), L(cmp_w2, l), L(cmp_pos, l), YT)
        if stop("nsa") or stop("dsa"):
            break
        kb.phase_merge(xa, xTb, L(w_in, l), L(w_branch, l), L(w_out, l), L(ln_g, l, 1), L(ln_b, l, 1), YT, xb2, xTa)
        if stop("merge"):
            break
        kb.phase_xattn(xb2, xTa, mem, L(xwq, l), L(xwkv, l), L(xwo, l), L(ln_g, l, 2), L(ln_b, l, 2), xa, xTb)
        if stop("xattn"):
            break
        kb.phase_ffn(xa, xTb, L(f2gu, l), L(f2dn, l), L(ln_g, l, 3), L(ln_b, l, 3), out if last else xb2, None if last else xTa)
        if stop("ffn2"):
            break
        xin = xb2
    st = kb.P.emit()
    kb.stats = st
    return kb


def make_in_maps(inputs, S, ncores):
    meta_np, bdm_np, ovl_np, NB, NCP = host_consts(S)
    colidx = build_colidx()
    w_in = np.asarray(inputs["w_in"], dtype=np.float32)
    w2 = np.ascontiguousarray(w_in[:, :, colidx])
    shared = {k: np.ascontiguousarray(np.asarray(v, dtype=np.float32)) for k, v in inputs.items() if k not in ("x", "mem")}
    shared["w2"] = w2
    shared["meta"] = meta_np
    shared["bdm"] = bdm_np
    shared["ovl"] = ovl_np
    maps = []
    for b in range(ncores):
        m = dict(shared)
        m["x"] = np.ascontiguousarray(np.asarray(inputs["x"][b, :S], dtype=np.float32))
        m["mem"] = np.ascontiguousarray(np.asarray(inputs["mem"][b], dtype=np.float32))
        maps.append(m)
    return maps


def kernel(**inputs):
    S = inputs["x"].shape[1]
    B = inputs["x"].shape[0]
    kb = build(S)
    maps = make_in_maps(inputs, S, B)
    res = run_bass_kernel_spmd(kb.nc, maps, core_ids=list(range(B)))
    out = np.stack([np.asarray(r["out"], dtype=np.float32) for r in res.results], axis=0)
    return out
```

```python
import math
import sys
import numpy as np
import concourse.bass as bass
import concourse.mybir as mybir
from concourse.bass_utils import run_bass_kernel_spmd

F32 = mybir.dt.float32
BF16 = mybir.dt.bfloat16
I32 = mybir.dt.int32
AF = mybir.ActivationFunctionType
ALU = mybir.AluOpType
AX = mybir.AxisListType

SEM_LIMIT = 30000
N_DMA_SEMS = 24


class Buf:
    __slots__ = ("name", "last_w", "readers")

    def __init__(self, name):
        self.name = name
        self.last_w = None
        self.readers = []


class Op:
    __slots__ = ("eng", "fn", "deps", "need_inc", "sem", "val", "is_dma", "idx", "tag")


class Prog:
    def __init__(self, nc):
        self.nc = nc
        self.engs = {"pe": nc.tensor, "act": nc.scalar, "dve": nc.vector, "pool": nc.gpsimd, "sp": nc.sync}
        self.ops = []
        self.bufs = {}
        self.last_on = {}
        self.dmas_since = []
        self.phase_deps = []
        self.phase_bufs = set()
        self.capture = None

    def buf(self, name):
        b = self.bufs.get(name)
        if b is None:
            b = self.bufs[name] = Buf(name)
        return b

    def add(self, eng, fn, reads=(), writes=(), dma=False, extra_deps=()):
        if self.capture is not None:
            self.capture.append(((eng, fn), dict(reads=list(reads), writes=list(writes), dma=dma)))
            return None
        op = Op()
        op.eng = eng
        op.fn = fn
        op.is_dma = dma
        op.need_inc = False
        op.sem = None
        op.val = 0
        op.idx = len(self.ops)
        try:
            f_ = sys._getframe(2)
            op.tag = (f_.f_lineno, f_.f_back.f_lineno if f_.f_back else 0)
        except Exception:
            op.tag = (0, 0)
        deps = {}
        for b in reads:
            b = self.buf(b)
            w = b.last_w
            if w is not None:
                deps[w.idx] = (w, "raw")
        for b in writes:
            b = self.buf(b)
            w = b.last_w
            if w is not None and w.idx not in deps:
                deps[w.idx] = (w, "waw")
            for r in b.readers:
                if r.idx not in deps:
                    deps[r.idx] = (r, "war")
        real = []
        for d, kind in deps.values():
            if (not d.is_dma) and d.eng == eng and not dma:
                if eng == "pe" or kind != "raw":
                    continue
            real.append(d)
        for d in extra_deps:
            real.append(d)
        if self.phase_deps:
            for b in list(reads) + list(writes):
                if b not in self.phase_bufs:
                    self.phase_bufs.add(b)
                    real.extend(self.phase_deps)
        op.deps = real
        for b in writes:
            b = self.buf(b)
            b.last_w = op
            b.readers = []
        for b in reads:
            self.buf(b).readers.append(op)
        self.ops.append(op)
        self.last_on[eng] = op
        if dma:
            self.dmas_since.append(op)
        return op

    def barrier(self):
        self.phase_deps = list(self.last_on.values()) + list(self.dmas_since)
        self.dmas_since = []
        self.phase_bufs = set()

    def emit(self, final_wait_eng="sp"):
        nc = self.nc
        for op in self.ops:
            for d in op.deps:
                d.need_inc = True
            if op.is_dma:
                op.need_inc = True
        eng_sem = {}
        eng_cnt = {}
        dma_sems = [nc.alloc_semaphore("dq%d" % i) for i in range(N_DMA_SEMS)]
        dma_cnt = [0] * N_DMA_SEMS
        dma_last = [None] * N_DMA_SEMS
        ndma = 0
        for op in self.ops:
            if not op.need_inc:
                continue
            if op.is_dma:
                j = ndma % N_DMA_SEMS
                ndma += 1
                if dma_last[j] is not None:
                    op.deps.append(dma_last[j])
                dma_cnt[j] += 16
                op.sem = dma_sems[j]
                op.val = dma_cnt[j]
                dma_last[j] = op
            else:
                e = op.eng
                if e not in eng_sem or eng_cnt[e] >= SEM_LIMIT:
                    eng_sem[e] = nc.alloc_semaphore("s_%s_%d" % (e, op.idx))
                    eng_cnt[e] = 0
                eng_cnt[e] += 1
                op.sem = eng_sem[e]
                op.val = eng_cnt[e]
        waited = {}
        nwaits = 0
        for op in self.ops:
            E = self.engs[op.eng]
            need = {}
            for d in op.deps:
                k = id(d.sem)
                if k not in need or need[k][1] < d.val:
                    need[k] = (d.sem, d.val)
            for k, (sem, val) in need.items():
                wk = (op.eng, k)
                if waited.get(wk, 0) >= val:
                    continue
                E.wait_ge(sem, val)
                nwaits += 1
                waited[wk] = val
            try:
                inst = op.fn()
            except Exception:
                print('EMIT FAIL at op', op.idx, op.eng, 'lines', op.tag)
                raise
            if op.need_inc:
                inst.then_inc(op.sem, 16 if op.is_dma else 1)
        E = self.engs[final_wait_eng]
        for j in range(N_DMA_SEMS):
            if dma_cnt[j] > 0:
                E.wait_ge(dma_sems[j], dma_cnt[j])
        self.stats = dict(n_ops=len(self.ops), n_waits=nwaits, n_dma=ndma,
                          n_inc=sum(1 for o in self.ops if o.need_inc))
        return self.stats


class V:
    __slots__ = ("ap", "b")

    def __init__(self, ap, b):
        self.ap = ap
        self.b = b


class T:
    def __init__(self, h, name, dram=False):
        self.h = h
        self.name = name
        self.dram = dram

    def __getitem__(self, idx):
        if self.dram:
            return V(self.h[idx], self.name)
        return V(self.h[idx], self.name)

    def v(self, ap):
        return V(ap, self.name)


DT_SIZE = {F32: 4, BF16: 2, I32: 4}

D = 1024
DFF = 2816
NKC = D // 128
NFC = DFF // 128
LN_EPS = 1e-5
DEPTH = 2
ALPHA = (2 * DEPTH) ** 0.25
N_MEM = 256


class KB:
    def __init__(self, S, depth=DEPTH, stop_after=None, debug=()):
        self.S = S
        self.depth = depth
        self.stop_after = stop_after
        self.debug = debug
        self.nc = bass.Bass("TRN2", target_bir_lowering=False)
        self.P = Prog(self.nc)
        self.uid = 0
        self.sb_base = 0
        self.sb_cur = 0
        self.outs = {}
        self.arena = None
        self.rots = {}
        self.ps_set = (0, 5)
        self.psb_set = (0, 2)
        self.fill_regs = {}
        self.n_keep = 256

    def sb(self, name, shape, dtype):
        nbytes = int(np.prod(shape[1:])) * DT_SIZE[dtype]
        nbytes = (nbytes + 63) // 64 * 64
        off = self.sb_cur
        self.sb_cur += nbytes
        assert self.sb_cur <= 207 * 1024, ("SBUF overflow", name, self.sb_cur)
        self.uid += 1
        if self.arena is None:
            self.arena = self.nc.alloc_sbuf_tensor("arena", [128, 207 * 1024], mybir.dt.uint8)
        ap = self.arena[:, off:off + int(np.prod(shape[1:])) * DT_SIZE[dtype]].bitcast(dtype)
        if len(shape) == 3:
            ap = ap.rearrange("p (a b) -> p a b", a=shape[1])
        elif len(shape) == 4:
            ap = ap.rearrange("p (a b c) -> p a b c", a=shape[1], b=shape[2])
        if shape[0] < 128:
            ap = ap[0:shape[0]]
        return T(ap, "%s_%d" % (name, self.uid))

    def phase_begin(self):
        self.P.barrier()
        self.sb_cur = self.sb_base

    def dram(self, name, shape, dtype, kind="ExternalOutput"):
        h = self.nc.dram_tensor(name, list(shape), dtype, kind=kind)
        return T(h.ap(), name, dram=True)

    def _rw(self, reads, writes):
        return [r.b for r in reads if isinstance(r, V)], [w.b for w in writes]

    def dma(self, out, in_, eng="sp"):
        nc = self.nc
        E = self.P.engs[eng]
        return self.P.add(eng, lambda: E.dma_start(out=out.ap, in_=in_.ap), reads=[in_.b], writes=[out.b], dma=True)

    def mm(self, out, lhsT, rhs, start=True, stop=True):
        nc = self.nc
        return self.P.add("pe", lambda: nc.tensor.matmul(out.ap, lhsT.ap, rhs.ap, start=start, stop=stop),
                          reads=[lhsT.b, rhs.b], writes=[out.b])

    def tr(self, out, in_, ident):
        nc = self.nc
        return self.P.add("pe", lambda: nc.tensor.transpose(out.ap, in_.ap, ident.ap),
                          reads=[in_.b, ident.b], writes=[out.b])

    def act(self, out, in_, func, bias=None, scale=None, accum=None, eng="act"):
        nc = self.nc
        kw = {}
        reads = [in_.b]
        writes = [out.b]
        if bias is not None:
            if isinstance(bias, V):
                kw["bias"] = bias.ap
                reads.append(bias.b)
            else:
                kw["bias"] = bias
        if scale is not None:
            if isinstance(scale, V):
                kw["scale"] = scale.ap
                reads.append(scale.b)
            else:
                kw["scale"] = scale
        if accum is not None:
            kw["accum_out"] = accum.ap
            writes.append(accum.b)
        return self.P.add("act", lambda: nc.scalar.activation(out=out.ap, in_=in_.ap, func=func, **kw),
                          reads=reads, writes=writes)

    def ts(self, out, in0, s1, s2, op0, op1=None, accum=None, eng="dve"):
        E = self.P.engs[eng]
        reads = [in0.b]
        writes = [out.b]
        a1 = s1
        a2 = s2
        if isinstance(s1, V):
            a1 = s1.ap
            reads.append(s1.b)
        if isinstance(s2, V):
            a2 = s2.ap
            reads.append(s2.b)
        kw = {}
        if op1 is not None:
            kw["op1"] = op1
        if accum is not None:
            kw["accum_out"] = accum.ap
            writes.append(accum.b)
        return self.P.add(eng, lambda: E.tensor_scalar(out=out.ap, in0=in0.ap, scalar1=a1, scalar2=a2, op0=op0, **kw),
                          reads=reads, writes=writes)

    def tt(self, out, in0, in1, op, eng="dve"):
        E = self.P.engs[eng]
        return self.P.add(eng, lambda: E.tensor_tensor(out=out.ap, in0=in0.ap, in1=in1.ap, op=op),
                          reads=[in0.b, in1.b], writes=[out.b])

    def stt(self, out, in0, scalar, in1, op0, op1, accum=None):
        nc = self.nc
        reads = [in0.b, in1.b]
        writes = [out.b]
        a = scalar
        if isinstance(scalar, V):
            a = scalar.ap
            reads.append(scalar.b)
        kw = {}
        if accum is not None:
            kw["accum_out"] = accum.ap
            writes.append(accum.b)
        return self.P.add("dve", lambda: nc.vector.scalar_tensor_tensor(out=out.ap, in0=in0.ap, scalar=a, in1=in1.ap,
                                                                     op0=op0, op1=op1, **kw),
                          reads=reads, writes=writes)

    def copy(self, out, in_, eng="dve"):
        E = self.P.engs[eng]
        if eng == "act":
            return self.P.add(eng, lambda: E.copy(out=out.ap, in_=in_.ap), reads=[in_.b], writes=[out.b])
        return self.P.add(eng, lambda: E.tensor_copy(out=out.ap, in_=in_.ap), reads=[in_.b], writes=[out.b])

    def memset(self, out, val, eng="pool"):
        E = self.P.engs[eng]
        return self.P.add(eng, lambda: E.memset(out.ap, val), writes=[out.b])

    def red(self, out, in_, op, axis=AX.X, eng="dve"):
        E = self.P.engs[eng]
        return self.P.add(eng, lambda: E.tensor_reduce(out=out.ap, in_=in_.ap, axis=axis, op=op),
                          reads=[in_.b], writes=[out.b])

    def recip(self, out, in_):
        nc = self.nc
        return self.P.add("dve", lambda: nc.vector.reciprocal(out=out.ap, in_=in_.ap), reads=[in_.b], writes=[out.b])

    def aselect(self, out, in_, pattern, cmp, fill, base, cm):
        nc = self.nc
        regs = self.fill_regs

        def fn():
            if fill not in regs:
                regs[fill] = nc.gpsimd.to_reg(float(fill))
            return nc.gpsimd.affine_select(out=out.ap, in_=in_.ap, pattern=pattern, compare_op=cmp,
                                           fill=regs[fill], base=base, channel_multiplier=cm)
        return self.P.add("pool", fn, reads=[in_.b], writes=[out.b])

    def iota(self, out, pattern, base, cm):
        nc = self.nc
        return self.P.add("pool", lambda: nc.gpsimd.iota(out.ap, pattern=pattern, base=base, channel_multiplier=cm,
                                                         allow_small_or_imprecise_dtypes=True), writes=[out.b])

    def setup(self):
        nc = self.nc
        self.ps = []
        for i in range(5):
            h = nc.alloc_psum_tensor("ps%d" % i, [128, 512], F32)
            self.ps.append(T(h, "ps%d" % i))
        self.psb = []
        for i in range(2):
            h = nc.alloc_psum_tensor("psb%d" % i, [128, 1024], BF16)
            self.psb.append(T(h, "psb%d" % i))
        self.ps_rr = 0
        self.psb_rr = 0
        self.ident_f = self.sb("identf", [128, 128], F32)
        self.ident = self.sb("ident", [128, 128], BF16)
        self.memset(self.ident_f[:], 1.0)
        self.aselect(self.ident_f[:], self.ident_f[:], [[-1, 128]], ALU.is_equal, 0.0, 0, 1)
        self.copy(self.ident[:], self.ident_f[:], eng="pool")
        self.sb_base = self.sb_cur

    def next_ps(self):
        b0, n = self.ps_set
        t = self.ps[b0 + self.ps_rr % n]
        self.ps_rr += 1
        return t

    def next_psb(self):
        b0, n = self.psb_set
        t = self.psb[b0 + self.psb_rr % n]
        self.psb_rr += 1
        return t

    def load_w(self, name, dram_ap_fn, kchunks, ncols, eng="pool", split=4):
        w = self.sb(name, [128, kchunks, ncols], BF16)
        for k in range(kchunks):
            self.dma(w[:, k, :], dram_ap_fn(k), eng="pool")
        return w

    def layer_norm_tile(self, r, g_bc, b_bc, out_f32, scr):
        st = scr["st"]
        junk = scr["junk"]
        self.act(junk[:], r[:], AF.Identity, accum=st[:, 0:1])
        self.act(junk[:], r[:], AF.Square, accum=st[:, 1:2])
        self.ts(st[:, 2:3], st[:, 0:1], 1.0 / D, None, ALU.mult)
        self.tt(st[:, 3:4], st[:, 2:3], st[:, 2:3], ALU.mult)
        self.stt(st[:, 4:5], st[:, 1:2], 1.0 / D, st[:, 3:4], ALU.mult, ALU.subtract)
        self.ts(st[:, 4:5], st[:, 4:5], 0.0, LN_EPS, ALU.max, ALU.add)
        self.act(st[:, 5:6], st[:, 4:5], AF.Sqrt)
        self.recip(st[:, 6:7], st[:, 5:6])
        self.ts(out_f32[:], r[:], st[:, 2:3], st[:, 6:7], ALU.subtract, ALU.mult)
        self.tt(out_f32[:], out_f32[:], g_bc[:], ALU.mult)
        self.tt(out_f32[:], out_f32[:], b_bc[:], ALU.add)

    def store_xT(self, x_f32, xT_dram, t0, scr):
        xb = scr["xb"]
        xTs = scr["xTs"]
        self.copy(xb[:], x_f32[:], eng="act")
        pb = self.next_psb()
        for k in range(NKC):
            self.tr(pb[:, k * 128:(k + 1) * 128], xb[:, k * 128:(k + 1) * 128], self.ident[:])
        self.copy(xTs[:], pb[:, :], eng="dve")
        self.dma(V(xT_dram.h.rearrange("(k p) s -> p k s", p=128)[:, :, t0:t0 + 128], xT_dram.name),
                 V(xTs.h[:].rearrange("p (k t) -> p k t", k=NKC), xTs.name))

    def dma_s(self, out, in_, eng="sp"):
        E = self.P.engs[eng]
        return self.P.add(eng, lambda: E.dma_start(out=out.ap, in_=in_.ap, allow_slow_non_contiguous=True),
                          reads=[in_.b], writes=[out.b], dma=True)

    def rot(self, name, n, shape, dtype):
        key = "_rot_" + name
        lst = [self.sb(name + str(i), shape, dtype) for i in range(n)]
        self.rots[key] = [lst, 0]
        return key

    def nx(self, key):
        lst, i = self.rots[key]
        self.rots[key][1] = i + 1
        return lst[i % len(lst)]

    def vmax(self, out, in_):
        nc = self.nc
        return self.P.add("dve", lambda: nc.vector.max(out=out.ap, in_=in_.ap), reads=[in_.b], writes=[out.b])

    def match_replace(self, out, rep, vals, imm):
        nc = self.nc
        return self.P.add("dve", lambda: nc.vector.match_replace(out=out.ap, in_to_replace=rep.ap, in_values=vals.ap, imm_value=imm),
                          reads=[rep.b, vals.b], writes=[out.b])

    def redabs(self, out, in_):
        nc = self.nc
        return self.P.add("dve", lambda: nc.vector.tensor_reduce(out=out.ap, in_=in_.ap, axis=AX.X, op=ALU.max,
                                                                 apply_absolute_value=True),
                          reads=[in_.b], writes=[out.b])

    def fm_rows(self, FMS, c0, nchunk, s0, s1):
        return V(FMS.h[c0 * 128:(c0 + nchunk) * 128, s0:s1].rearrange("(c p) s -> p c s", p=128), FMS.name)

    def setup_consts(self, meta, bdm, ovl, NB, NCP):
        S = self.S
        NT = S // 128
        self.NB = NB
        self.NCP = NCP
        self.meta = self.sb("meta", [128, 32], F32)
        self.dma(self.meta[:], meta[:, :])
        self.bdm = self.sb("bdm", [128, 256], F32)
        self.dma(self.bdm[:], bdm[:, :])
        self.ovl = self.sb("ovl", [128, NCP // 128, NB], F32)
        self.dma(self.ovl[:], V(ovl.h.rearrange("(c p) j -> p c j", p=128), ovl.name))
        self.U = self.sb("U", [128, 128], F32)
        self.memset(self.U[:], 1.0)
        self.aselect(self.U[:], self.U[:], [[1, 128]], ALU.is_ge, 0.0, 0, -1)
        self.cneg30 = self.sb("cneg30", [128, 128], F32)
        self.memset(self.cneg30[:], 0.0)
        self.aselect(self.cneg30[:], self.cneg30[:], [[-1, 128]], ALU.is_ge, -1e30, 0, 1)
        self.cneg2k = self.sb("cneg2k", [128, 128], F32)
        self.memset(self.cneg2k[:], 0.0)
        self.aselect(self.cneg2k[:], self.cneg2k[:], [[-1, 128]], ALU.is_ge, -2000.0, 0, 1)
        self.band = self.sb("band", [128, 640], F32)
        self.memset(self.band[:], 0.0)
        self.aselect(self.band[:], self.band[:], [[1, 640]], ALU.is_ge, -2000.0, -1, -1)
        self.aselect(self.band[:], self.band[:], [[-1, 640]], ALU.is_ge, -2000.0, 512, 1)
        self.decayT4 = self.sb("decayT4", [128, 4, 128], F32)
        self.xi = self.sb("xi", [128, 128], F32)
        self.zeta = self.sb("zeta", [128, 128], F32)
        self.cdecay = self.sb("cdecay", [128, 1], F32)
        self.rkc = self.sb("rkc", [128, 20], F32)
        self.sb_base = self.sb_cur
        dji = self.sb("dji", [128, 128], F32)
        self.iota(dji[:], [[1, 128]], 0, -1)
        for h in range(4):
            self.act(self.decayT4[:, h, :], dji[:], AF.Exp, scale=RET_LNG[h])
        self.tt(self.decayT4[:], self.decayT4[:], V(self.U.h[:, :].unsqueeze(1).to_broadcast([128, 4, 128]), self.U.name), ALU.mult)
        ip1 = self.sb("ip1", [128, 128], F32)
        self.iota(ip1[:], [[1, 128]], 1, 0)
        self.act(self.xi[:], ip1[:], AF.Exp, scale=self.meta[:, 12:13])
        jr = self.sb("jr", [128, 128], F32)
        self.iota(jr[:], [[0, 128]], 127, -1)
        for h in range(4):
            self.act(self.zeta[:, 32 * h:32 * h + 32], jr[:, 32 * h:32 * h + 32], AF.Exp, scale=RET_LNG[h])
        c128 = self.sb("c128", [128, 1], F32)
        self.memset(c128[:], 128.0)
        self.act(self.cdecay[:], c128[:], AF.Exp, scale=self.meta[:, 12:13])
        for k in range(20):
            self.memset(self.rkc[:, k:k + 1], 2.0 ** (-k))

    def build_addmask(self):
        NT = self.S // 128
        NB = self.NB
        self.addmask = self.sb("addmask", [128, NT, NB], F32)
        self.memset(self.addmask[:], 0.0)
        for i in range(NT):
            for half in range(2):
                cur = 2 * i + half
                r0 = 64 * half
                v = self.addmask[r0:r0 + 64, i, :]
                self.aselect(v, v, [[-1, NB]], ALU.is_ge, -1e30, cur, 0)
                self.memset(self.addmask[r0:r0 + 64, i, 0:1], 1e30)
                self.memset(self.addmask[r0:r0 + 64, i, cur:cur + 1], 1e30)
                if cur >= 1:
                    self.memset(self.addmask[r0:r0 + 64, i, cur - 1:cur], 1e30)

    def phase_rope(self, ROPE):
        S = self.S
        self.phase_begin()
        pos = self.sb("pos", [128, S], F32)
        self.iota(pos[:], [[1, S]], 0, 0)
        a = self.sb("a", [128, S], F32)
        ki = self.sb("ki", [128, S], I32)
        kf = self.sb("kf", [128, S], F32)
        m = self.sb("m", [128, S], F32)
        r = self.sb("r", [128, S], F32)
        PI = math.pi
        for t in range(4):
            for which in range(2):
                self.ts(a[:], pos[:], self.meta[:, t:t + 1], (PI / 2 if which == 0 else 0.0), ALU.mult, ALU.add)
                self.ts(kf[:], a[:], 1.0 / (2 * PI), None, ALU.mult)
                self.copy(ki[:], kf[:])
                self.copy(kf[:], ki[:])
                self.stt(r[:], kf[:], -2 * PI, a[:], ALU.mult, ALU.add)
                self.ts(m[:], r[:], PI, -2 * PI, ALU.is_gt, ALU.mult)
                self.tt(r[:], r[:], m[:], ALU.add)
                self.ts(m[:], r[:], -PI, 2 * PI, ALU.is_lt, ALU.mult)
                self.tt(r[:], r[:], m[:], ALU.add)
                self.ts(r[:], r[:], PI, -PI, ALU.min, ALU.max)
                self.act(r[:], r[:], AF.Sin)
                col = 4 + 4 * which + t
                self.ts(r[:], r[:], self.meta[:, col:col + 1], None, ALU.mult)
                self.dma(ROPE[t, which], r[:])

    def finish_tile(self, rq, g_bc, b_bc, x_out, xT_out, t0, scr):
        self.layer_norm_tile(rq, g_bc, b_bc, rq, scr)
        self.dma(x_out[t0:t0 + 128, :], rq[:])
        if xT_out is not None:
            self.store_xT(rq, xT_out, t0, scr)

    def ln_setup(self, ln_g, ln_b):
        g_bc = self.sb("g_bc", [128, D], F32)
        b_bc = self.sb("b_bc", [128, D], F32)
        self.dma(g_bc[:], V(ln_g.h.partition_broadcast(128), ln_g.name))
        self.dma(b_bc[:], V(ln_b.h.partition_broadcast(128), ln_b.name))
        scr = dict(st=self.sb("st", [128, 8], F32), junk=self.sb("junk", [128, D], BF16),
                   xb=self.sb("xb", [128, D], BF16), xTs=self.sb("xTs", [128, D], BF16))
        return g_bc, b_bc, scr

    def phase_ffn(self, x_in, xT_in, w_gu, w_down, ln_g, ln_b, x_out, xT_out):
        S = self.S
        self.phase_begin()
        wgu = self.load_w("wgu", lambda k: V(w_gu.h[k * 128:(k + 1) * 128, :], w_gu.name), NKC, 2 * DFF)
        wdn = self.load_w("wdn", lambda k: V(w_down.h[k * 128:(k + 1) * 128, :], w_down.name), NFC, D)
        g_bc, b_bc, scr = self.ln_setup(ln_g, ln_b)
        xt = self.sb("xT", [128, NKC, 512], BF16)
        hT = self.sb("hT", [128, NFC, 512], BF16)
        sg = [self.sb("sg%d" % i, [128, 512], BF16) for i in range(2)]
        xr = [self.sb("xr%d" % i, [128, D], F32) for i in range(2)]
        rqs = [self.sb("r%d" % i, [128, D], F32) for i in range(2)]
        for t in range(S // 512):
            self.dma(xt[:], V(xT_in.h.rearrange("(k p) s -> p k s", p=128)[:, :, t * 512:(t + 1) * 512], xT_in.name))
            for j in range(NFC):
                pg = self.next_ps()
                pu = self.next_ps()
                for k in range(NKC):
                    self.mm(pg[:], wgu[:, k, j * 128:(j + 1) * 128], xt[:, k, :], start=(k == 0), stop=(k == NKC - 1))
                for k in range(NKC):
                    self.mm(pu[:], wgu[:, k, DFF + j * 128:DFF + (j + 1) * 128], xt[:, k, :], start=(k == 0), stop=(k == NKC - 1))
                s = sg[j % 2]
                self.act(s[:], pg[:], AF.Silu)
                self.tt(hT[:, j, :], s[:], pu[:], ALU.mult)
            for q in range(4):
                t0 = t * 512 + q * 128
                xq = xr[q % 2]
                rq = rqs[q % 2]
                self.dma(xq[:], x_in[t0:t0 + 128, :])
                for half in range(2):
                    hs = slice(half * 512, (half + 1) * 512)
                    pd = self.next_ps()
                    for j in range(NFC):
                        self.mm(pd[:], hT[:, j, q * 128:(q + 1) * 128], wdn[:, j, hs], start=(j == 0), stop=(j == NFC - 1))
                    self.act(xq[:, hs], xq[:, hs], AF.Copy, scale=ALPHA)
                    self.stt(rq[:, hs], pd[:], 0.5, xq[:, hs], ALU.mult, ALU.add)
                self.finish_tile(rq, g_bc, b_bc, x_out, xT_out, t0, scr)

    def phase_transpose_in(self, x_in, xT_out):
        S = self.S
        self.phase_begin()
        xr = [self.sb("xr%d" % i, [128, D], F32) for i in range(2)]
        scr = dict(xb=self.sb("xb", [128, D], BF16), xTs=self.sb("xTs", [128, D], BF16))
        for i in range(S // 128):
            xq = xr[i % 2]
            self.dma(xq[:], x_in[i * 128:(i + 1) * 128, :])
            self.store_xT(xq, xT_out, i * 128, scr)

    def phase_inproj(self, xT_in, w2, ROPE, FMS, TMB, TMF):
        S = self.S
        self.phase_begin()
        w = self.load_w("win", lambda k: V(w2.h[k * 128:(k + 1) * 128, :], w2.name), NKC, NCOL2)
        xt = self.sb("xT", [128, NKC, 512], BF16)
        tab = self.sb("tab", [128, 4, 2, 512], F32)
        t1 = self.rot("t1", 3, [128, 512], F32)
        t2 = self.rot("t2", 3, [128, 512], F32)
        ob = self.rot("ob", 4, [128, 512], BF16)
        tmb = self.rot("tmb", 2, [128, 960], BF16)
        tmf = self.rot("tmf", 2, [128, 24], F32)
        for t in range(S // 512):
            ss = slice(t * 512, (t + 1) * 512)
            self.dma(xt[:], V(xT_in.h.rearrange("(k p) s -> p k s", p=128)[:, :, ss], xT_in.name))
            self.dma(tab[:], V(ROPE.h[:, :, :, ss].rearrange("t w p s -> p t w s"), ROPE.name))
            for ci, tb in enumerate(ROPED_TABLES):
                pA = self.next_ps()
                pB = self.next_ps()
                for k in range(NKC):
                    self.mm(pA[:], w[:, k, (2 * ci) * 128:(2 * ci + 1) * 128], xt[:, k, :], start=(k == 0), stop=(k == NKC - 1))
                for k in range(NKC):
                    self.mm(pB[:], w[:, k, (2 * ci + 1) * 128:(2 * ci + 2) * 128], xt[:, k, :], start=(k == 0), stop=(k == NKC - 1))
                a1 = self.nx(t1)
                a2 = self.nx(t2)
                o = self.nx(ob)
                self.tt(a1[:], pA[:], tab[:, tb, 0, :], ALU.mult)
                self.tt(a2[:], pB[:], tab[:, tb, 1, :], ALU.mult)
                self.tt(o[:], a1[:], a2[:], ALU.add)
                self.dma(V(FMS.h[ci * 128:(ci + 1) * 128, ss], FMS.name), o[:])
            nr = len(ROPED_TABLES)
            for j in range(7):
                wc = 2 * nr + j
                pA = self.next_ps()
                for k in range(NKC):
                    self.mm(pA[:], w[:, k, wc * 128:(wc + 1) * 128], xt[:, k, :], start=(k == 0), stop=(k == NKC - 1))
                o = self.nx(ob)
                self.copy(o[:], pA[:], eng="act")
                self.dma(V(FMS.h[(nr + j) * 128:(nr + j + 1) * 128, ss], FMS.name), o[:])
            for q in range(4):
                t0 = t * 512 + q * 128
                pA = self.next_ps()
                pB = self.next_ps()
                for k in range(NKC):
                    self.mm(pA[:], xt[:, k, q * 128:(q + 1) * 128], w[:, k, TM0:TM0 + 512], start=(k == 0), stop=(k == NKC - 1))
                for k in range(NKC):
                    self.mm(pB[:, 0:472], xt[:, k, q * 128:(q + 1) * 128], w[:, k, TM0 + 512:TM0 + 984], start=(k == 0), stop=(k == NKC - 1))
                b = self.nx(tmb)
                f = self.nx(tmf)
                self.copy(b[:, 0:512], pA[:], eng="act")
                self.copy(b[:, 512:960], pB[:, 0:448])
                self.copy(f[:], pB[:, 448:472])
                self.dma(TMB[t0:t0 + 128, :], b[:])
                self.dma(TMF[t0:t0 + 128, :], f[:])

    def store_yT(self, y, YT, br, n, yTs_key):
        pb = self.next_psb()
        self.tr(pb[:, 0:128], y[:, 0:128], self.ident[:])
        self.tr(pb[:, 128:256], y[:, 128:256], self.ident[:])
        yTs = self.nx(yTs_key)
        self.copy(yTs[:], pb[:, 0:256])
        self.dma(V(YT.h[br * 256:(br + 1) * 256, n * 128:(n + 1) * 128].rearrange("(c p) t -> p c t", p=128), YT.name),
                 V(yTs.h[:, :].rearrange("p (c t) -> p c t", c=2), yTs.name))

    def phase_ret(self, FMS, TMB, YT):
        S = self.S
        NT = S // 128
        self.phase_begin()
        rq = self.sb("rq", [128, S], BF16)
        rk = self.sb("rk", [128, S], BF16)
        self.dma(rq[:], V(FMS.h[0:128, :], FMS.name))
        self.dma(rk[:], V(FMS.h[128:256, :], FMS.name))
        Sbd = self.sb("Sbd", [128, 256], F32)
        Sbd_bf = self.sb("Sbd_bf", [128, 256], BF16)
        self.memset(Sbd[:], 0.0)
        self.memset(Sbd_bf[:], 0.0)
        vt_k = self.rot("vt", 2, [128, 512], BF16)
        qxi_k = self.rot("qxi", 2, [128, 128], BF16)
        qm_k = self.rot("qm", 2, [128, 4, 128], BF16)
        kz_k = self.rot("kz", 2, [128, 128], BF16)
        PT_k = self.rot("PT", 2, [128, 4, 128], BF16)
        cross_k = self.rot("cross", 2, [128, 256], F32)
        o_k = self.rot("o", 2, [128, 256], F32)
        tmp_k = self.rot("tmp", 2, [128, 256], F32)
        osq_k = self.rot("osq", 2, [128, 256], F32)
        sg_k = self.rot("sg", 2, [128, 256], F32)
        st_k = self.rot("st", 2, [128, 16], F32)
        y_k = self.rot("y", 2, [128, 256], BF16)
        yTs_k = self.rot("yTs", 2, [128, 256], BF16)
        hm = V(self.meta.h[:, 13:17].unsqueeze(2).to_broadcast([128, 4, 128]), self.meta.name)
        for n in range(NT):
            sl = slice(n * 128, (n + 1) * 128)
            vt = self.nx(vt_k)
            self.dma(vt[:], TMB[n * 128:(n + 1) * 128, 0:512])
            qxi = self.nx(qxi_k)
            self.tt(qxi[:], rq[:, sl], self.xi[:], ALU.mult)
            qm = self.nx(qm_k)
            self.tt(qm[:], V(rq.h[:, sl].unsqueeze(1).to_broadcast([128, 4, 128]), rq.name), hm, ALU.mult, eng="pool")
            pb = self.next_psb()
            self.tr(pb[:, 0:128], rk[:, sl], self.ident[:])
            kz = self.nx(kz_k)
            self.tt(kz[:], pb[:, 0:128], self.zeta[:], ALU.mult)
            ps1 = self.next_ps()
            self.mm(ps1[:], rk[:, sl], V(qm.h[:, :, :].rearrange("p h i -> p (h i)"), qm.name))
            PT = self.nx(PT_k)
            self.tt(PT[:], V(ps1.h[:, :].rearrange("p (h i) -> p h i", h=4), ps1.name), self.decayT4[:], ALU.mult)
            ps2 = self.next_ps()
            self.mm(ps2[:, 0:256], qxi[:], Sbd_bf[:])
            cross = self.nx(cross_k)
            self.copy(cross[:], ps2[:, 0:256], eng="act")
            ps3 = self.next_ps()
            for h in range(4):
                self.mm(ps3[:, 64 * h:64 * h + 64], PT[:, h, :], vt[:, 64 * h:64 * h + 64])
            o = self.nx(o_k)
            self.tt(o[:], ps3[:, 0:256], cross[:], ALU.add)
            ps4 = self.next_ps()
            self.mm(ps4[:, 0:256], kz[:], vt[:, 0:256])
            tmp = self.nx(tmp_k)
            self.tt(tmp[:], ps4[:, 0:256], self.bdm[:], ALU.mult)
            self.stt(Sbd[:], Sbd[:], self.cdecay[:, 0:1], tmp[:], ALU.mult, ALU.add)
            self.copy(Sbd_bf[:], Sbd[:], eng="act")
            st = self.nx(st_k)
            o3 = V(o.h[:, :].rearrange("p (h e) -> p h e", h=4), o.name)
            self.red(st[:, 0:4], o3, ALU.add)
            osq = self.nx(osq_k)
            self.tt(osq[:], o[:], o[:], ALU.mult, eng="pool")
            self.red(st[:, 4:8], V(osq.h[:, :].rearrange("p (h e) -> p h e", h=4), osq.name), ALU.add)
            self.ts(st[:, 8:12], st[:, 0:4], 1.0 / 64, None, ALU.mult)
            self.tt(st[:, 12:16], st[:, 8:12], st[:, 8:12], ALU.mult)
            self.stt(st[:, 4:8], st[:, 4:8], 1.0 / 64, st[:, 12:16], ALU.mult, ALU.subtract)
            self.ts(st[:, 4:8], st[:, 4:8], 0.0, LN_EPS, ALU.max, ALU.add)
            self.act(st[:, 4:8], st[:, 4:8], AF.Sqrt)
            self.recip(st[:, 4:8], st[:, 4:8])
            self.tt(o3, o3, V(st.h[:, 8:12].unsqueeze(2).to_broadcast([128, 4, 64]), st.name), ALU.subtract)
            self.tt(o3, o3, V(st.h[:, 4:8].unsqueeze(2).to_broadcast([128, 4, 64]), st.name), ALU.mult)
            sg = self.nx(sg_k)
            self.act(sg[:], vt[:, 256:512], AF.Silu)
            y = self.nx(y_k)
            self.tt(y[:], o[:], sg[:], ALU.mult)
            self.store_yT(y, YT, 0, n, yTs_k)

    def phase_ssd(self, FMS, TMB, TMF, conv_w, conv_b, dt_bias, a_log, d_skip, norm_g, YT):
        S = self.S
        NT = S // 128
        self.phase_begin()
        cw = self.sb("cw", [128, 6, 4], F32)
        for k_ in range(4):
            self.dma_s(cw[:, :, k_], V(conv_w.h[k_].rearrange("(c p) -> p c", p=128), conv_w.name))
        cb = self.sb("cb", [128, 6], F32)
        self.dma_s(cb[:], V(conv_b.h.rearrange("(c p) -> p c", p=128), conv_b.name))
        dtb = self.sb("dtb", [128, 4], F32)
        self.dma(dtb[:], V(dt_bias.h.partition_broadcast(128), dt_bias.name))
        a_bc = self.sb("a_bc", [128, 4], F32)
        self.dma(a_bc[:], V(a_log.h.partition_broadcast(128), a_log.name))
        self.act(a_bc[:], a_bc[:], AF.Exp)
        self.ts(a_bc[:], a_bc[:], -1.0, None, ALU.mult)
        Dbc = self.sb("Dbc", [128, 4], F32)
        self.dma(Dbc[:], V(d_skip.h.partition_broadcast(128), d_skip.name))
        ng_bc = self.sb("ng_bc", [128, 256], F32)
        self.dma(ng_bc[:], V(norm_g.h.partition_broadcast(128), norm_g.name))
        xbcs = self.sb("xbcs", [128, 6, S], BF16)
        raw_k = self.rot("raw", 2, [128, 6, 515], BF16)
        acc_k = self.rot("acc", 2, [128, 512], F32)
        for t in range(S // 512):
            raw = self.nx(raw_k)
            if t == 0:
                self.memset(raw[:, :, 0:3], 0.0)
                self.dma(raw[:, :, 3:515], self.fm_rows(FMS, 14, 6, 0, 512))
            else:
                self.dma(raw[:, :, 0:515], self.fm_rows(FMS, 14, 6, t * 512 - 3, (t + 1) * 512))
            for c in range(6):
                acc = self.nx(acc_k)
                self.ts(acc[:], raw[:, c, 3:515], cw[:, c, 3:4], None, ALU.mult)
                for k in (2, 1, 0):
                    self.stt(acc[:], raw[:, c, k:k + 512], cw[:, c, k:k + 1], acc[:], ALU.mult, ALU.add)
                self.act(xbcs[:, c, t * 512:(t + 1) * 512], acc[:], AF.Silu, bias=cb[:, c:c + 1])
        prev = self.sb("prev", [128, 256], F32)
        prev_bf = self.sb("prev_bf", [128, 256], BF16)
        self.memset(prev[:], 0.0)
        self.memset(prev_bf[:], 0.0)
        xsB_k = self.rot("xsB", 2, [128, 512], BF16)
        tmf_k = self.rot("tmf", 2, [128, 24], F32)
        zt_k = self.rot("zt", 2, [128, 256], BF16)
        st_k = self.rot("st", 2, [128, 32], F32)
        adtb_k = self.rot("adtb", 2, [128, 4, 128], F32)
        seg_k = self.rot("seg", 2, [128, 4, 128], F32)
        MT_k = self.rot("MT", 2, [128, 4, 128], BF16)
        X_k = self.rot("X", 2, [128, 256], BF16)
        Xd_k = self.rot("Xd", 2, [128, 256], BF16)
        yd_k = self.rot("yd", 2, [128, 256], F32)
        y_k = self.rot("y", 2, [128, 256], F32)
        t2_k = self.rot("t2", 2, [128, 256], F32)
        sz_k = self.rot("sz", 2, [128, 256], F32)
        yb_k = self.rot("yb", 2, [128, 256], BF16)
        yTs_k = self.rot("yTs", 2, [128, 256], BF16)
        Ubc = V(self.U.h[:, :].unsqueeze(1).to_broadcast([128, 4, 128]), self.U.name)

        def h4(t_):
            return V(t_.h[:, 0:256].rearrange("p (h e) -> p h e", h=4), t_.name)

        def bc4(v_):
            return V(v_.ap.unsqueeze(2).to_broadcast([128, 4, 64]), v_.b)

        for n in range(NT):
            sl = slice(n * 128, (n + 1) * 128)
            pb = self.next_psb()
            for c in range(4):
                self.tr(pb[:, c * 128:(c + 1) * 128], xbcs[:, c, sl], self.ident[:])
            xsB = self.nx(xsB_k)
            self.copy(xsB[:], pb[:, 0:512])
            tmf = self.nx(tmf_k)
            self.dma(tmf[:], TMF[n * 128:(n + 1) * 128, :])
            zt = self.nx(zt_k)
            self.dma(zt[:], TMB[n * 128:(n + 1) * 128, 704:960])
            st = self.nx(st_k)
            self.tt(st[:, 0:4], tmf[:, 20:24], dtb[:], ALU.add)
            self.act(st[:, 0:4], st[:, 0:4], AF.Exp)
            self.act(st[:, 0:4], st[:, 0:4], AF.Ln, bias=1.0)
            self.tt(st[:, 4:8], st[:, 0:4], a_bc[:], ALU.mult)
            adtb = self.nx(adtb_k)
            self.copy(adtb[:], V(st.h[:, 4:8].unsqueeze(2).to_broadcast([128, 4, 128]), st.name))
            psA = self.next_ps()
            self.mm(psA[:, 0:4], self.U[:], st[:, 4:8])
            self.copy(st[:, 8:12], psA[:, 0:4], eng="act")
            psB = self.next_ps()
            for h in range(4):
                self.mm(psB[:, h * 128:(h + 1) * 128], adtb[:, h, :], self.U[:])
            seg = self.nx(seg_k)
            for h in range(4):
                self.ts(seg[:, h, :], psB[:, h * 128:(h + 1) * 128], st[:, 8 + h:9 + h], 0.0, ALU.subtract, ALU.min)
            self.act(seg[:], seg[:], AF.Exp)
            self.tt(seg[:], seg[:], Ubc, ALU.mult, eng="pool")
            alast = V(psB.h[:, 127:512:128], psB.name)
            self.tt(st[:, 12:16], alast, st[:, 8:12], ALU.subtract)
            self.act(st[:, 12:16], st[:, 12:16], AF.Exp)
            self.act(st[:, 16:20], alast, AF.Exp)
            self.act(st[:, 20:24], st[:, 8:12], AF.Exp)
            psG = self.next_ps()
            for g in range(2):
                self.mm(psG[:, g * 128:(g + 1) * 128], xbcs[:, 2 + g, sl], xbcs[:, 4 + g, sl])
            MT = self.nx(MT_k)
            for g in range(2):
                self.tt(MT[:, 2 * g:2 * g + 2, :], seg[:, 2 * g:2 * g + 2, :],
                        V(psG.h[:, g * 128:(g + 1) * 128].unsqueeze(1).to_broadcast([128, 2, 128]), psG.name), ALU.mult)
            X = self.nx(X_k)
            self.tt(h4(X), h4(xsB), bc4(st[:, 0:4]), ALU.mult)
            psY = self.next_ps()
            for h in range(4):
                self.mm(psY[:, 64 * h:64 * h + 64], MT[:, h, :], X[:, 64 * h:64 * h + 64])
            psO = self.next_ps()
            for g in range(2):
                self.mm(psO[:, 128 * g:128 * g + 128], xbcs[:, 4 + g, sl], prev_bf[:, 128 * g:128 * g + 128])
            yd = self.nx(yd_k)
            self.copy(yd[:], psY[:, 0:256], eng="act")
            y = self.nx(y_k)
            self.tt(h4(y), h4(psO), bc4(st[:, 20:24]), ALU.mult)
            self.tt(y[:], y[:], yd[:], ALU.add)
            t2 = self.nx(t2_k)
            self.tt(h4(t2), h4(xsB), bc4(Dbc[:, 0:4]), ALU.mult, eng="pool")
            self.tt(y[:], y[:], t2[:], ALU.add)
            Xd = self.nx(Xd_k)
            self.tt(h4(Xd), h4(X), bc4(st[:, 12:16]), ALU.mult, eng="pool")
            psS = self.next_ps()
            for g in range(2):
                self.mm(psS[:, 128 * g:128 * g + 128], xsB[:, 256 + 128 * g:256 + 128 * g + 128], Xd[:, 128 * g:128 * g + 128])
            self.tt(h4(prev), h4(prev), bc4(st[:, 16:20]), ALU.mult)
            self.tt(prev[:], prev[:], psS[:, 0:256], ALU.add)
            self.copy(prev_bf[:], prev[:], eng="act")
            sz = self.nx(sz_k)
            self.act(sz[:], zt[:], AF.Silu)
            self.tt(y[:], y[:], sz[:], ALU.mult)
            self.tt(t2[:], y[:], y[:], ALU.mult, eng="pool")
            self.red(st[:, 24:26], V(t2.h[:, :].rearrange("p (g e) -> p g e", g=2), t2.name), ALU.add)
            self.ts(st[:, 24:26], st[:, 24:26], 1.0 / 128, LN_EPS, ALU.mult, ALU.add)
            self.act(st[:, 24:26], st[:, 24:26], AF.Sqrt)
            self.recip(st[:, 24:26], st[:, 24:26])
            y2 = V(y.h[:, :].rearrange("p (g e) -> p g e", g=2), y.name)
            self.tt(y2, y2, V(st.h[:, 24:26].unsqueeze(2).to_broadcast([128, 2, 128]), st.name), ALU.mult)
            yb = self.nx(yb_k)
            self.tt(yb[:], y[:], ng_bc[:], ALU.mult)
            self.store_yT(yb, YT, 3, n, yTs_k)

    def softmax_pv(self, Ssb, nk, Vt, kt0, out, kk, clamp=None):
        st = self.nx(kk["st"])
        self.red(st[:, 0:1], Ssb, ALU.max)
        if clamp is not None:
            self.ts(st[:, 0:1], st[:, 0:1], clamp, None, ALU.max)
        self.ts(st[:, 1:2], st[:, 0:1], -1.0, None, ALU.mult)
        P = self.nx(kk["P"])
        self.act(P[:, 0:nk], Ssb, AF.Exp, bias=st[:, 1:2], accum=st[:, 2:3])
        self.ts(st[:, 3:4], st[:, 2:3], 1e-30, None, ALU.max)
        self.recip(st[:, 4:5], st[:, 3:4])
        po = self.next_ps()
        nkt = nk // 128
        for g0 in range(0, nkt, 8):
            gn = min(8, nkt - g0)
            pb = self.next_psb()
            for j in range(gn):
                self.tr(pb[:, j * 128:(j + 1) * 128], P[:, (g0 + j) * 128:(g0 + j + 1) * 128], self.ident[:])
            PT = self.nx(kk["PT"])
            self.copy(PT[:, 0:gn * 128], pb[:, 0:gn * 128], eng="act")
            for j in range(gn):
                self.mm(po[:, 0:64], PT[:, j * 128:(j + 1) * 128], Vt[:, kt0 + g0 + j, :],
                        start=(g0 + j == 0), stop=(g0 + j == nkt - 1))
        self.ts(out, po[:, 0:64], st[:, 4:5], None, ALU.mult)

    def attn_keys(self, pfx):
        S = self.S
        return dict(st=self.rot(pfx + "sst", 2, [128, 8], F32), P=self.rot(pfx + "P", 1, [128, S], BF16),
                    PT=self.rot(pfx + "PTa", 2, [128, 1024], BF16))

    def dsa_setup(self, FMS, TMB, TMF, YT):
        S = self.S
        NT = S // 128
        c = dict(FMS=FMS, YT=YT)
        c["dk"] = self.sb("dk", [128, S], BF16)
        self.dma(c["dk"][:], V(FMS.h[4 * 128:5 * 128, :], FMS.name))
        ikr = self.sb("ikr", [128, S], BF16)
        self.dma(ikr[:], V(FMS.h[7 * 128:8 * 128, :], FMS.name))
        c["ikm"] = self.sb("ikm", [128, 4, S], BF16)
        for g in range(4):
            self.ts(c["ikm"][:, g, :], ikr[:], self.meta[:, 13 + g:14 + g], None, ALU.mult, eng=("pool" if g % 2 else "dve"))
        c["Vt"] = self.sb("Vt", [128, NT, 64], BF16)
        self.dma(c["Vt"][:], V(TMB.h[:, 512:576].rearrange("(n p) c -> p n c", p=128), TMB.name))
        iw = self.sb("iw", [128, NT, 8], F32)
        self.dma(iw[:], V(TMF.h[:, 0:8].rearrange("(n p) c -> p n c", p=128), TMF.name))
        c["absw"] = self.sb("absw", [128, NT, 8], F32)
        self.act(c["absw"][:], iw[:], AF.Abs, scale=1.0 / 16)
        c["sgn"] = self.sb("sgn", [128, NT, 8], F32)
        self.ts(c["sgn"][:], iw[:], 0.0, 2.0, ALU.is_ge, ALU.mult)
        self.ts(c["sgn"][:], c["sgn"][:], -1.0, None, ALU.add)
        c["I"] = self.sb("I", [128, S], F32)
        c["Ssb"] = self.sb("dSsb", [128, S], F32)
        c["kk"] = self.attn_keys("d")
        c["q"] = self.rot("dqi", 2, [128, 4, 128], BF16)
        c["tmp"] = self.rot("tmpr", 2, [128, 512], F32)
        c["st"] = self.rot("dst", 2, [128, 16], F32)
        c["Rk"] = self.rot("Rk", 2, [128, 20], F32)
        c["nm"] = self.rot("dnm", 2, [128, 2], F32)
        c["c2"] = self.rot("dc2", 2, [128, 2], F32)
        c["o"] = self.rot("do", 2, [128, 256], F32)
        c["y"] = self.rot("dy", 2, [128, 256], BF16)
        c["yTs"] = self.rot("dyTs", 2, [128, 256], BF16)
        return c

    def dsa_tile(self, c, i):
        FMS = c["FMS"]
        I = c["I"]
        Ssb = c["Ssb"]
        nk = 128 * (i + 1)
        nkc = (nk + 511) // 512
        q = self.nx(c["q"])
        self.dma(q[:, 0:2, :], self.fm_rows(FMS, 2, 2, i * 128, (i + 1) * 128))
        self.dma(q[:, 2:4, :], self.fm_rows(FMS, 5, 2, i * 128, (i + 1) * 128))
        for kc in range(nkc):
            c0 = kc * 512
            cols = min(512, nk - c0)
            for h in range(8):
                ps = self.next_ps()
                self.mm(ps[:, 0:cols], q[:, 2 + h // 4, :], c["ikm"][:, h % 4, c0:c0 + cols])
                tmp = self.nx(c["tmp"])
                self.act(tmp[:, 0:cols], ps[:, 0:cols], AF.Relu, scale=c["absw"][:, i, h:h + 1])
                if h == 0:
                    self.ts(I[:, c0:c0 + cols], tmp[:, 0:cols], c["sgn"][:, i, 0:1], None, ALU.mult)
                else:
                    self.stt(I[:, c0:c0 + cols], tmp[:, 0:cols], c["sgn"][:, i, h:h + 1], I[:, c0:c0 + cols], ALU.mult, ALU.add)
        if nk > self.n_keep:
            st = self.nx(c["st"])
            junk = self.nx(c["kk"]["P"])
            self.redabs(st[:, 0:1], I[:, 0:nk])
            self.ts(st[:, 0:1], st[:, 0:1], 1e-20, None, ALU.max)
            self.tt(I[:, nk - 128:nk], I[:, nk - 128:nk], self.cneg30[:], ALU.add)
            Rk = self.nx(c["Rk"])
            self.ts(Rk[:], self.rkc[:], st[:, 0:1], None, ALU.mult)
            self.ts(st[:, 1:2], st[:, 0:1], -1.0, None, ALU.mult)
            n1 = (nk // 2 + 127) // 128 * 128
            n2 = nk - n1
            thr_c = self.n_keep - 0.5 - n2 / 2.0
            for k in range(NBIS):
                self.tt(st[:, 2:3], st[:, 1:2], Rk[:, k:k + 1], ALU.add)
                nm = self.nx(c["nm"])
                c2 = self.nx(c["c2"])
                self.ts(nm[:, 0:1], st[:, 2:3], -1.0, None, ALU.mult)
                self.act(Ssb[:, n1:nk], I[:, n1:nk], AF.Sign, bias=nm[:, 0:1], accum=c2[:, 0:1])
                self.ts(junk[:, 0:n1], I[:, 0:n1], st[:, 2:3], None, ALU.is_ge, ALU.add, accum=st[:, 3:4])
                self.stt(st[:, 4:5], c2[:, 0:1], 0.5, st[:, 3:4], ALU.mult, ALU.add)
                self.ts(st[:, 4:5], st[:, 4:5], thr_c, None, ALU.is_ge)
                self.stt(st[:, 1:2], st[:, 4:5], Rk[:, k:k + 1], st[:, 1:2], ALU.mult, ALU.add)
            self.ts(I[:, 0:nk], I[:, 0:nk], st[:, 1:2], 1000.0, ALU.is_ge, ALU.mult)
        else:
            self.ts(I[:, 0:nk], I[:, 0:nk], 0.0, 1000.0, ALU.mult, ALU.add)
            self.tt(I[:, nk - 128:nk], I[:, nk - 128:nk], self.cneg2k[:], ALU.add)
        o = self.nx(c["o"])
        for h in range(4):
            base = 64 * (h % 2)
            cq = h // 2
            for kc in range(nkc):
                c0 = kc * 512
                cols = min(512, nk - c0)
                ps = self.next_ps()
                self.mm(ps[:, 0:cols], q[base:base + 64, cq, :], c["dk"][base:base + 64, c0:c0 + cols])
                self.stt(Ssb[:, c0:c0 + cols], ps[:, 0:cols], 0.125, I[:, c0:c0 + cols], ALU.mult, ALU.add)
            self.softmax_pv(Ssb[:, 0:nk], nk, c["Vt"], 0, o[:, 64 * h:64 * h + 64], c["kk"])
        y = self.nx(c["y"])
        self.copy(y[:], o[:], eng="act")
        self.store_yT(y, c["YT"], 1, i, c["yTs"])

    def nsa_setup(self, FMS, TMB, TMF, cmp_w1, cmp_w2, cmp_pos, YT):
        S = self.S
        NT = S // 128
        NB = self.NB
        NCP = self.NCP
        NC = (S - 32) // 16 + 1
        NCT = NCP // 128
        c = dict(FMS=FMS, YT=YT)
        c["ksT"] = self.sb("ksT", [128, S], BF16)
        self.dma(c["ksT"][:], V(FMS.h[11 * 128:12 * 128, :], FMS.name))
        c["kwT"] = self.sb("kwT", [128, S], BF16)
        self.dma(c["kwT"][:], V(FMS.h[12 * 128:13 * 128, :], FMS.name))
        c["Vs"] = self.sb("Vs", [128, NT, 64], BF16)
        self.dma(c["Vs"][:], V(TMB.h[:, 576:640].rearrange("(n p) c -> p n c", p=128), TMB.name))
        c["Vw"] = self.sb("Vw", [128, NT, 64], BF16)
        self.dma(c["Vw"][:], V(TMB.h[:, 640:704].rearrange("(n p) c -> p n c", p=128), TMB.name))
        c["ngt"] = self.sb("ngt", [128, NT, 12], F32)
        self.dma(c["ngt"][:], V(TMF.h[:, 8:20].rearrange("(n p) c -> p n c", p=128), TMF.name))
        kcmp = self.sb("kcmp", [128, NCP], BF16)
        vcmp = self.sb("vcmp", [128, NCT, 64], BF16)
        c["kcmp"] = kcmp
        c["vcmp"] = vcmp
        c["Ssb"] = self.sb("nSsb", [128, S], F32)
        c["Sw"] = self.sb("Sw", [128, 640], F32)
        c["kk"] = self.attn_keys("n")
        save = self.sb_cur
        srcT = self.sb("srcT", [128, S], BF16)
        w1 = self.sb("w1", [64, 32, 64], BF16)
        w2 = self.sb("w2", [64, 128], BF16)
        posT = self.sb("posT", [64, 32], F32)
        posb = self.sb("posb", [64, 32], BF16)
        cst = self.sb("cst", [64, 1], F32)
        u = self.sb("u", [64, NCP], F32)
        u2 = self.sb("u2", [64, NCP], F32)
        gl = self.sb("gl", [64, NCP], BF16)
        for i in range(2):
            self.dma(srcT[:], V(FMS.h[(10 + 3 * i) * 128:(11 + 3 * i) * 128, :], FMS.name))
            self.dma(w1[:], V(cmp_w1.h[i].rearrange("(l d) f -> d l f", d=64), cmp_w1.name), eng="pool")
            self.dma(w2[:, 0:64], V(cmp_w2.h[i], cmp_w2.name), eng="pool")
            self.dma(w2[:, 64:128], V(cmp_w2.h[i], cmp_w2.name), eng="pool")
            self.dma_s(posT[:], V(cmp_pos.h[i].rearrange("l d -> d l"), cmp_pos.name))
            self.copy(posb[:], posT[:])
            psc = self.next_ps()
            for l in range(32):
                self.mm(psc[0:64, 0:1], w1[:, l, :], posb[:, l:l + 1], start=(l == 0), stop=(l == 31))
            self.copy(cst[:], psc[0:64, 0:1])
            psh = self.next_ps()
            for l in range(32):
                self.mm(psh[0:64, 0:NC], w1[:, l, :], srcT[0:64, l:l + 16 * (NC - 1) + 1:16], start=(l == 0), stop=(l == 31))
            self.memset(u[:], 0.0)
            self.act(u[:, 0:NC], psh[0:64, 0:NC], AF.Identity, bias=cst[:, 0:1])
            self.tt(u2[:], u[:], u[:], ALU.mult)
            self.tt(u2[:], u2[:], u[:], ALU.mult)
            self.stt(u2[:], u2[:], 0.044715, u[:], ALU.mult, ALU.add)
            self.act(u2[:], u2[:], AF.Tanh, scale=0.7978845608028654)
            self.ts(u2[:], u2[:], 1.0, 0.5, ALU.add, ALU.mult)
            self.tt(gl[:], u2[:], u[:], ALU.mult)
            if i == 0:
                pso = self.next_ps()
                self.mm(pso[:, 0:NCP], w2[:, :], gl[:, :])
                self.copy(kcmp[:], pso[:, 0:NCP])
            else:
                for ct in range(NCT):
                    pso = self.next_ps()
                    self.mm(pso[:, 0:64], gl[:, ct * 128:(ct + 1) * 128], w2[:, 0:64])
                    self.copy(vcmp[:, ct, :], pso[:, 0:64])
        self.sb_cur = save
        self.P.barrier()
        c["q"] = self.rot("nqi", 2, [128, 2, 128], BF16)
        for nm, shp, dt_ in (("vis", [128, NCP], F32), ("pns", [128, NCP], F32), ("pn", [128, NCP], F32), ("Sc", [128, NCP], F32),
                             ("Pc", [128, NCP], F32), ("pnb", [128, NCP], BF16), ("PTc", [128, NCP], BF16), ("pnT", [128, NCP], F32),
                             ("cst2", [128, 8], F32), ("am", [128, NB], F32), ("imp", [128, NB], F32), ("imp2", [128, NB], F32), ("m8", [128, 16], F32),
                             ("selm", [128, NB], F32), ("oc", [128, 256], F32), ("os", [128, 256], F32), ("ow", [128, 256], F32),
                             ("gs", [128, 12], F32), ("o", [128, 256], F32), ("y", [128, 256], BF16), ("yTs", [128, 256], BF16)):
            c[nm] = self.rot("n" + nm, 2, shp, dt_)
        return c

    def nsa_tile(self, c, i):
        NB = self.NB
        NCP = self.NCP
        NCT = NCP // 128
        FMS = c["FMS"]
        Ssb = c["Ssb"]
        Sw = c["Sw"]
        kcmp = c["kcmp"]
        vcmp = c["vcmp"]

        def h4(t_):
            return V(t_.h[:, 0:256].rearrange("p (h e) -> p h e", h=4), t_.name)

        nk = 128 * (i + 1)
        nkc = (nk + 511) // 512
        nq = self.nx(c["q"])
        self.dma(nq[:], self.fm_rows(FMS, 8, 2, i * 128, (i + 1) * 128))
        vis = self.nx(c["vis"])
        self.memset(vis[:], 0.0)
        self.aselect(vis[:], vis[:], [[-16, NCP]], ALU.is_ge, -1000.0, 128 * i - 31, 1)
        pns = self.nx(c["pns"])
        oc = self.nx(c["oc"])
        osl = self.nx(c["os"])
        ow = self.nx(c["ow"])
        for h in range(4):
            base = 64 * (h % 2)
            cq = h // 2
            ps = self.next_ps()
            self.mm(ps[:, 0:NCP], nq[base:base + 64, cq, :], kcmp[base:base + 64, :])
            Sc = self.nx(c["Sc"])
            self.stt(Sc[:], ps[:, 0:NCP], 0.125, vis[:], ALU.mult, ALU.add)
            st = self.nx(c["cst2"])
            self.red(st[:, 0:1], Sc[:], ALU.max)
            self.ts(st[:, 0:1], st[:, 0:1], -500.0, -1.0, ALU.max, ALU.mult)
            Pc = self.nx(c["Pc"])
            self.act(Pc[:], Sc[:], AF.Exp, bias=st[:, 0:1], accum=st[:, 1:2])
            self.ts(st[:, 2:3], st[:, 1:2], 1e-30, None, ALU.max)
            self.recip(st[:, 3:4], st[:, 2:3])
            pn = pns if h == 0 else self.nx(c["pn"])
            self.ts(pn[:], Pc[:], st[:, 3:4], None, ALU.mult)
            pnb = self.nx(c["pnb"])
            self.copy(pnb[:], pn[:], eng="act")
            if h > 0:
                self.tt(pns[:], pns[:], pn[:], ALU.add, eng="pool")
            pb = self.next_psb()
            for ct in range(NCT):
                self.tr(pb[:, ct * 128:(ct + 1) * 128], pnb[:, ct * 128:(ct + 1) * 128], self.ident[:])
            PTc = self.nx(c["PTc"])
            self.copy(PTc[:], pb[:, 0:NCP], eng="act")
            po = self.next_ps()
            for ct in range(NCT):
                self.mm(po[:, 0:64], PTc[:, ct * 128:(ct + 1) * 128], vcmp[:, ct, :], start=(ct == 0), stop=(ct == NCT - 1))
            self.copy(oc[:, 64 * h:64 * h + 64], po[:, 0:64], eng="act")
        selm = self.nx(c["selm"])
        if NB > 16:
            pf = self.next_ps()
            for ct in range(NCT):
                self.tr(pf[:, ct * 128:(ct + 1) * 128], pns[:, ct * 128:(ct + 1) * 128], self.ident_f[:])
            pnT = self.nx(c["pnT"])
            self.copy(pnT[:], pf[:, 0:NCP], eng="act")
            pi = self.next_ps()
            for ct in range(NCT):
                self.mm(pi[:, 0:NB], pnT[:, ct * 128:(ct + 1) * 128], self.ovl[:, ct, :], start=(ct == 0), stop=(ct == NCT - 1))
            am = self.nx(c["am"])
            self.memset(am[:], 0.0)
            for half in range(2):
                cur = 2 * i + half
                r0 = 64 * half
                v_ = am[r0:r0 + 64, :]
                self.aselect(v_, v_, [[-1, NB]], ALU.is_ge, -1e30, cur, 0)
                self.memset(am[r0:r0 + 64, 0:1], 1e30)
                self.memset(am[r0:r0 + 64, cur:cur + 1], 1e30)
                if cur >= 1:
                    self.memset(am[r0:r0 + 64, cur - 1:cur], 1e30)
            imp = self.nx(c["imp"])
            self.tt(imp[:], pi[:, 0:NB], am[:], ALU.add)
            m8 = self.nx(c["m8"])
            self.vmax(m8[:, 0:8], imp[:])
            imp2 = self.nx(c["imp2"])
            self.match_replace(imp2[:], m8[:, 0:8], imp[:], -3.0e38)
            self.vmax(m8[:, 8:16], imp2[:])
            self.ts(selm[:], imp[:], m8[:, 15:16], 1000.0, ALU.is_ge, ALU.mult)
        else:
            self.memset(selm[:], 1000.0)
        for h in range(4):
            base = 64 * (h % 2)
            cq = h // 2
            for kc in range(nkc):
                c0 = kc * 512
                cols = min(512, nk - c0)
                nb_ = cols // 64
                ps = self.next_ps()
                self.mm(ps[:, 0:cols], nq[base:base + 64, cq, :], c["ksT"][base:base + 64, c0:c0 + cols])
                self.stt(V(Ssb.h[:, c0:c0 + cols].rearrange("p (b e) -> p b e", e=64), Ssb.name),
                         V(ps.h[:, 0:cols].rearrange("p (b e) -> p b e", e=64), ps.name), 0.125,
                         V(selm.h[:, c0 // 64:c0 // 64 + nb_].unsqueeze(2).to_broadcast([128, nb_, 64]), selm.name),
                         ALU.mult, ALU.add)
            self.tt(Ssb[:, nk - 128:nk], Ssb[:, nk - 128:nk], self.cneg2k[:], ALU.add)
            self.softmax_pv(Ssb[:, 0:nk], nk, c["Vs"], 0, osl[:, 64 * h:64 * h + 64], c["kk"])
        k0 = max(0, i * 128 - 512)
        nkw = nk - k0
        boff = 640 - nkw
        for h in range(4):
            base = 64 * (h % 2)
            cq = h // 2
            for c0 in range(0, nkw, 512):
                cols = min(512, nkw - c0)
                ps = self.next_ps()
                self.mm(ps[:, 0:cols], nq[base:base + 64, cq, :], c["kwT"][base:base + 64, k0 + c0:k0 + c0 + cols])
                self.stt(Sw[:, c0:c0 + cols], ps[:, 0:cols], 0.125, self.band[:, boff + c0:boff + c0 + cols], ALU.mult, ALU.add)
            self.softmax_pv(Sw[:, 0:nkw], nkw, c["Vw"], k0 // 128, ow[:, 64 * h:64 * h + 64], c["kk"])
        gs = self.nx(c["gs"])
        self.act(gs[:], c["ngt"][:, i, :], AF.Sigmoid)
        o = self.nx(c["o"])

        def gbc(j):
            return V(gs.h[:, j:12:3].unsqueeze(2).to_broadcast([128, 4, 64]), gs.name)
        self.tt(h4(o), h4(oc), gbc(0), ALU.mult)
        self.tt(h4(osl), h4(osl), gbc(1), ALU.mult)
        self.tt(o[:], o[:], osl[:], ALU.add)
        self.tt(h4(ow), h4(ow), gbc(2), ALU.mult)
        self.tt(o[:], o[:], ow[:], ALU.add)
        y = self.nx(c["y"])
        self.copy(y[:], o[:], eng="act")
        self.store_yT(y, c["YT"], 2, i, c["yTs"])

    def phase_dsa_nsa(self, FMS, TMB, TMF, cmp_w1, cmp_w2, cmp_pos, YT):
        S = self.S
        NT = S // 128
        self.phase_begin()
        cd = self.dsa_setup(FMS, TMB, TMF, YT)
        cn = self.nsa_setup(FMS, TMB, TMF, cmp_w1, cmp_w2, cmp_pos, YT)
        P = self.P
        for i in range(NT):
            self.ps_set = (0, 3)
            self.psb_set = (0, 1)
            P.capture = []
            self.dsa_tile(cd, i)
            A = P.capture
            self.ps_set = (3, 2)
            self.psb_set = (1, 1)
            P.capture = []
            self.nsa_tile(cn, i)
            B = P.capture
            P.capture = None
            self.ps_set = (0, 5)
            self.psb_set = (0, 2)
            ia = ib = 0
            na, nb = len(A), len(B)
            while ia < na or ib < nb:
                if ib >= nb or (ia < na and ia * nb <= ib * na):
                    P.add(*A[ia][0], **A[ia][1])
                    ia += 1
                else:
                    P.add(*B[ib][0], **B[ib][1])
                    ib += 1

    def phase_merge(self, x_in, xT_in, w_in_l, w_branch, w_out, ln_g, ln_b, YT, x_out, xT_out):
        S = self.S
        self.phase_begin()
        wg = self.load_w("wg", lambda k: V(w_in_l.h[k * 128:(k + 1) * 128, 3128:7224], w_in_l.name), NKC, 4096)
        wb = self.sb("wb", [128, 4, 2, 1024], BF16)
        for n in range(4):
            for kk_ in range(2):
                self.dma(wb[:, n, kk_, :], V(w_branch.h[n, kk_ * 128:(kk_ + 1) * 128, :], w_branch.name), eng="pool")
        wo = self.load_w("wo", lambda k: V(w_out.h[k * 128:(k + 1) * 128, :], w_out.name), NKC, D)
        g_bc, b_bc, scr = self.ln_setup(ln_g, ln_b)
        xt = self.sb("xT", [128, NKC, 512], BF16)
        yt = self.sb("yT", [128, 8, 512], BF16)
        mT = self.sb("mT", [128, 8, 512], BF16)
        acc_k = self.rot("acc", 2, [128, 512], F32)
        sg_k = self.rot("sg", 2, [128, 512], F32)
        tmp_k = self.rot("tmp", 2, [128, 512], F32)
        xr = [self.sb("xr%d" % i, [128, D], F32) for i in range(2)]
        rqs = [self.sb("r%d" % i, [128, D], F32) for i in range(2)]
        for t in range(S // 512):
            ss = slice(t * 512, (t + 1) * 512)
            self.dma(xt[:], V(xT_in.h.rearrange("(k p) s -> p k s", p=128)[:, :, ss], xT_in.name))
            self.dma(yt[:], V(YT.h[:, ss].rearrange("(c p) s -> p c s", p=128), YT.name))
            for dc in range(8):
                acc = self.nx(acc_k)
                for n in range(4):
                    pg = self.next_ps()
                    for k in range(NKC):
                        self.mm(pg[:], wg[:, k, n * 1024 + dc * 128:n * 1024 + (dc + 1) * 128], xt[:, k, :],
                                start=(k == 0), stop=(k == NKC - 1))
                    pp = self.next_ps()
                    for k2 in range(2):
                        self.mm(pp[:], wb[:, n, k2, dc * 128:(dc + 1) * 128], yt[:, 2 * n + k2, :], start=(k2 == 0), stop=(k2 == 1))
                    sg = self.nx(sg_k)
                    self.act(sg[:], pg[:], AF.Sigmoid)
                    if n == 0:
                        self.tt(acc[:], sg[:], pp[:], ALU.mult)
                    else:
                        tmp = self.nx(tmp_k)
                        self.tt(tmp[:], sg[:], pp[:], ALU.mult)
                        self.tt(acc[:], acc[:], tmp[:], ALU.add, eng="pool")
                self.copy(mT[:, dc, :], acc[:], eng="act")
            for q in range(4):
                t0 = t * 512 + q * 128
                xq = xr[q % 2]
                rq = rqs[q % 2]
                self.dma(xq[:], x_in[t0:t0 + 128, :])
                for half in range(2):
                    hs = slice(half * 512, (half + 1) * 512)
                    pd = self.next_ps()
                    for dc in range(8):
                        self.mm(pd[:], mT[:, dc, q * 128:(q + 1) * 128], wo[:, dc, hs], start=(dc == 0), stop=(dc == 7))
                    self.act(xq[:, hs], xq[:, hs], AF.Copy, scale=ALPHA)
                    self.stt(rq[:, hs], pd[:], 1.0, xq[:, hs], ALU.mult, ALU.add)
                self.finish_tile(rq, g_bc, b_bc, x_out, xT_out, t0, scr)

    def phase_xattn(self, x_in, xT_in, mem, wq_d, wkv_d, wo_d, ln_g, ln_b, x_out, xT_out):
        S = self.S
        self.phase_begin()
        wq = self.load_w("wq", lambda k: V(wq_d.h[k * 128:(k + 1) * 128, :], wq_d.name), NKC, D)
        wkv = self.load_w("wkv", lambda k: V(wkv_d.h[k * 128:(k + 1) * 128, :], wkv_d.name), NKC, 2 * D)
        wo = self.load_w("wo", lambda k: V(wo_d.h[k * 128:(k + 1) * 128, :], wo_d.name), NKC, D)
        g_bc, b_bc, scr = self.ln_setup(ln_g, ln_b)
        memT = self.sb("memT", [128, 8, 256], BF16)
        mr = self.sb("mr", [128, D], F32)
        mb = self.sb("mb", [128, D], BF16)
        for mt in range(2):
            self.dma(mr[:], mem[mt * 128:(mt + 1) * 128, :])
            self.copy(mb[:], mr[:], eng="act")
            pb = self.next_psb()
            for k in range(8):
                self.tr(pb[:, k * 128:(k + 1) * 128], mb[:, k * 128:(k + 1) * 128], self.ident[:])
            self.copy(memT[:, :, mt * 128:(mt + 1) * 128], V(pb.h[:, :].rearrange("p (k t) -> p k t", k=8), pb.name))
        KT = self.sb("KT", [128, 8, 256], BF16)
        for c in range(8):
            ps = self.next_ps()
            for k in range(NKC):
                self.mm(ps[:, 0:256], wkv[:, k, c * 128:(c + 1) * 128], memT[:, k, :], start=(k == 0), stop=(k == NKC - 1))
            self.copy(KT[:, c, :], ps[:, 0:256], eng=("act" if c % 2 else "dve"))
        Vm = self.sb("Vm", [128, 2, D], BF16)
        for mt in range(2):
            for half in range(2):
                ps = self.next_ps()
                for k in range(NKC):
                    self.mm(ps[:], memT[:, k, mt * 128:(mt + 1) * 128], wkv[:, k, D + half * 512:D + (half + 1) * 512],
                            start=(k == 0), stop=(k == NKC - 1))
                self.copy(Vm[:, mt, half * 512:(half + 1) * 512], ps[:], eng=("act" if half else "dve"))
        xt = self.sb("xT", [128, NKC, 512], BF16)
        qT = self.sb("qT", [128, 8, 512], BF16)
        Pf_k = self.rot("Pf", 2, [128, 4, 256], F32)
        Pb_k = self.rot("Pb", 2, [128, 4, 256], BF16)
        PT_k = self.rot("PTx", 2, [128, 8, 128], BF16)
        oT_k = self.rot("oT", 2, [128, 8, 128], BF16)
        st_k = self.rot("xst", 2, [128, 16], F32)
        xr = [self.sb("xr%d" % i, [128, D], F32) for i in range(2)]
        rqs = [self.sb("r%d" % i, [128, D], F32) for i in range(2)]
        SC = 1.0 / 16
        for t in range(S // 512):
            ss = slice(t * 512, (t + 1) * 512)
            self.dma(xt[:], V(xT_in.h.rearrange("(k p) s -> p k s", p=128)[:, :, ss], xT_in.name))
            for c in range(8):
                ps = self.next_ps()
                for k in range(NKC):
                    self.mm(ps[:], wq[:, k, c * 128:(c + 1) * 128], xt[:, k, :], start=(k == 0), stop=(k == NKC - 1))
                self.copy(qT[:, c, :], ps[:], eng=("act" if c % 2 else "dve"))
            for q in range(4):
                t0 = t * 512 + q * 128
                tq = slice(q * 128, (q + 1) * 128)
                pss = [self.next_ps(), self.next_ps()]
                st = self.nx(st_k)
                Pf = self.nx(Pf_k)
                for h in range(4):
                    pv = pss[h // 2][:, (h % 2) * 256:(h % 2) * 256 + 256]
                    for cc in range(2):
                        self.mm(pv, qT[:, 2 * h + cc, tq], KT[:, 2 * h + cc, :], start=(cc == 0), stop=(cc == 1))
                    self.red(st[:, h:h + 1], pv, ALU.max)
                    self.ts(st[:, 4 + h:5 + h], st[:, h:h + 1], -SC, None, ALU.mult)
                    self.act(Pf[:, h, :], pv, AF.Exp, bias=st[:, 4 + h:5 + h], scale=SC, accum=st[:, 8 + h:9 + h])
                self.recip(st[:, 12:16], st[:, 8:12])
                Pb = self.nx(Pb_k)
                self.tt(Pb[:], Pf[:], V(st.h[:, 12:16].unsqueeze(2).to_broadcast([128, 4, 256]), st.name), ALU.mult)
                pb = self.next_psb()
                for h in range(4):
                    for mc in range(2):
                        j = 2 * h + mc
                        self.tr(pb[:, j * 128:(j + 1) * 128], Pb[:, h, mc * 128:(mc + 1) * 128], self.ident[:])
                PT = self.nx(PT_k)
                self.copy(PT[:], V(pb.h[:, :].rearrange("p (j t) -> p j t", j=8), pb.name))
                oT = self.nx(oT_k)
                pso = [self.next_ps(), self.next_ps()]
                for h in range(4):
                    for dc in range(2):
                        j = 2 * h + dc
                        pv = pso[j // 4][:, (j % 4) * 128:(j % 4) * 128 + 128]
                        for mc in range(2):
                            self.mm(pv, Vm[:, mc, h * 256 + dc * 128:h * 256 + (dc + 1) * 128], PT[:, 2 * h + mc, :],
                                    start=(mc == 0), stop=(mc == 1))
                for j4 in range(2):
                    self.copy(oT[:, 4 * j4:4 * j4 + 4, :], V(pso[j4].h[:, :].rearrange("p (j t) -> p j t", j=4), pso[j4].name),
                              eng=("act" if j4 else "dve"))
                xq = xr[q % 2]
                rq = rqs[q % 2]
                self.dma(xq[:], x_in[t0:t0 + 128, :])
                for half in range(2):
                    hs = slice(half * 512, (half + 1) * 512)
                    pd = self.next_ps()
                    for c in range(8):
                        self.mm(pd[:], oT[:, c, :], wo[:, c, hs], start=(c == 0), stop=(c == 7))
                    self.act(xq[:, hs], xq[:, hs], AF.Copy, scale=ALPHA)
                    self.stt(rq[:, hs], pd[:], 1.0, xq[:, hs], ALU.mult, ALU.add)
                self.finish_tile(rq, g_bc, b_bc, x_out, xT_out, t0, scr)


OFF = dict(r_q=0, r_k=128, r_v=256, r_g=512, d_q=768, d_k=1024, d_v=1088, i_q=1152, i_k=1408, i_w=1440,
           n_q=1448, n_kc=1704, n_vc=1768, n_ks=1832, n_vs=1896, n_kw=1960, n_vw=2024, n_g=2088,
           s_z=2100, s_xbc=2356, s_dt=3124, br_g=3128)


def _partner(i, headdim, rot):
    half = rot // 2
    j = i % headdim
    b = i - j
    if j < half:
        return b + j + half
    if j < rot:
        return b + j - half
    return i


def build_colidx():
    cols = []

    def roped(name, width, headdim, rot, lo=0, rep=1):
        loc = []
        for r in range(rep):
            loc += list(range(lo, lo + width))
        assert len(loc) == 128
        a = [OFF[name] + i for i in loc]
        b = [OFF[name] + _partner(i, headdim, rot) for i in loc]
        cols.extend(a)
        cols.extend(b)

    roped("r_q", 128, 32, 32)
    roped("r_k", 128, 32, 32)
    roped("d_q", 128, 64, 16, 0)
    roped("d_q", 128, 64, 16, 128)
    roped("d_k", 64, 64, 16, 0, 2)
    roped("i_q", 128, 32, 8, 0)
    roped("i_q", 128, 32, 8, 128)
    roped("i_k", 32, 32, 8, 0, 4)
    roped("n_q", 128, 64, 16, 0)
    roped("n_q", 128, 64, 16, 128)
    roped("n_kc", 64, 64, 16, 0, 2)
    roped("n_ks", 64, 64, 16, 0, 2)
    roped("n_kw", 64, 64, 16, 0, 2)
    cols.extend([OFF["n_vc"] + i for i in range(64)] * 2)
    cols.extend([OFF["s_xbc"] + i for i in range(768)])
    for name, w in (("r_v", 256), ("r_g", 256), ("d_v", 64), ("n_vs", 64), ("n_vw", 64), ("s_z", 256),
                    ("i_w", 8), ("n_g", 12), ("s_dt", 4)):
        cols.extend([OFF[name] + i for i in range(w)])
    return np.asarray(cols, dtype=np.int64)


ROPED_TABLES = [0, 1, 2, 2, 2, 3, 3, 3, 2, 2, 2, 2, 2]
TM0 = (2 * len(ROPED_TABLES) + 7) * 128
NCOL2 = TM0 + 984
NBIS = 18
RET_LNG = [math.log1p(-2.0 ** (-5 - h)) for h in range(4)]


def host_consts(S):
    meta = np.zeros((128, 32), np.float32)

    def fill(t, headdim, rot, theta, scale):
        half = rot // 2
        inv = np.power(np.float32(theta), (-2.0 * np.arange(half, dtype=np.float32) / np.float32(rot)).astype(np.float32)).astype(np.float32)
        for p in range(128):
            i = p % headdim
            if i < rot:
                meta[p, t] = inv[i % half]
                meta[p, 4 + t] = scale
                meta[p, 8 + t] = -scale if i < half else scale
            else:
                meta[p, t] = 0.0
                meta[p, 4 + t] = 1.0
                meta[p, 8 + t] = 0.0

    fill(0, 32, 32, 10000.0, 1.0)
    fill(1, 32, 32, 10000.0, 32.0 ** -0.5)
    fill(2, 64, 16, 500000.0, 1.0)
    fill(3, 32, 8, 500000.0, 1.0)
    for p in range(128):
        meta[p, 12] = RET_LNG[p // 32]
        meta[p, 13 + p // 32] = 1.0
    bdm = np.zeros((128, 256), np.float32)
    for p in range(128):
        bdm[p, 64 * (p // 32):64 * (p // 32) + 64] = 1.0
    NC = (S - 32) // 16 + 1
    NCP = (NC + 127) // 128 * 128
    NB = S // 64
    ovl = np.zeros((NCP, NB), np.float32)
    for c in range(NC):
        for j in range(NB):
            ovl[c, j] = max(min(16 * c + 32, 64 * j + 64) - max(16 * c, 64 * j), 0) / 32.0
    return meta, bdm, ovl, NB, NCP


STAGES = ["ffn1", "inproj", "ret", "ssd", "dsa", "nsa", "merge", "xattn", "ffn2"]


def build(S, depth=DEPTH, stop_after=None):
    kb = KB(S, depth, stop_after)
    meta_np, bdm_np, ovl_np, NB, NCP = host_consts(S)
    kb.n_keep = min(256, S // 4)
    EI = "ExternalInput"
    x = kb.dram("x", [S, D], F32, kind=EI)
    mem = kb.dram("mem", [N_MEM, D], F32, kind=EI)
    ln_g = kb.dram("ln_g", [DEPTH, 4, D], F32, kind=EI)
    ln_b = kb.dram("ln_b", [DEPTH, 4, D], F32, kind=EI)
    f1gu = kb.dram("ffn1_w_gu", [DEPTH, D, 2 * DFF], F32, kind=EI)
    f1dn = kb.dram("ffn1_w_down", [DEPTH, DFF, D], F32, kind=EI)
    w_in = kb.dram("w_in", [DEPTH, D, 7224], F32, kind=EI)
    w2 = kb.dram("w2", [DEPTH, D, NCOL2], F32, kind=EI)
    cmp_w1 = kb.dram("cmp_w1", [DEPTH, 2, 2048, 64], F32, kind=EI)
    cmp_w2 = kb.dram("cmp_w2", [DEPTH, 2, 64, 64], F32, kind=EI)
    cmp_pos = kb.dram("cmp_pos", [DEPTH, 2, 32, 64], F32, kind=EI)
    conv_w = kb.dram("conv_w", [DEPTH, 4, 768], F32, kind=EI)
    conv_b = kb.dram("conv_b", [DEPTH, 768], F32, kind=EI)
    dt_bias = kb.dram("dt_bias", [DEPTH, 4], F32, kind=EI)
    a_log = kb.dram("a_log", [DEPTH, 4], F32, kind=EI)
    d_skip = kb.dram("d_skip", [DEPTH, 4], F32, kind=EI)
    norm_g = kb.dram("ssm_norm_g", [DEPTH, 256], F32, kind=EI)
    w_branch = kb.dram("w_branch", [DEPTH, 4, 256, D], F32, kind=EI)
    w_out = kb.dram("w_out", [DEPTH, D, D], F32, kind=EI)
    xwq = kb.dram("xattn_wq", [DEPTH, D, D], F32, kind=EI)
    xwkv = kb.dram("xattn_wkv", [DEPTH, D, 2 * D], F32, kind=EI)
    xwo = kb.dram("xattn_wo", [DEPTH, D, D], F32, kind=EI)
    f2gu = kb.dram("ffn2_w_gu", [DEPTH, D, 2 * DFF], F32, kind=EI)
    f2dn = kb.dram("ffn2_w_down", [DEPTH, DFF, D], F32, kind=EI)
    meta = kb.dram("meta", [128, 32], F32, kind=EI)
    bdm = kb.dram("bdm", [128, 256], F32, kind=EI)
    ovl = kb.dram("ovl", [NCP, NB], F32, kind=EI)
    out = kb.dram("out", [S, D], F32)
    xTa = kb.dram("xTa", [D, S], BF16)
    xTb = kb.dram("xTb", [D, S], BF16)
    xa = kb.dram("xa", [S, D], F32)
    xb2 = kb.dram("xb2", [S, D], F32)
    FMS = kb.dram("FMS", [20 * 128, S], BF16)
    TMB = kb.dram("TMB", [S, 960], BF16)
    TMF = kb.dram("TMF", [S, 24], F32)
    YT = kb.dram("YT", [1024, S], BF16)
    ROPE = kb.dram("ROPE", [4, 2, 128, S], F32)
    kb.setup()
    kb.setup_consts(meta, bdm, ovl, NB, NCP)
    kb.phase_rope(ROPE)
    kb.phase_transpose_in(x, xTa)

    def L(t, *idx):
        return T(t.h[idx], t.name, True)

    done = False
    xin = x
    for l in range(depth):
        last = (l == depth - 1)

        def stop(name):
            return stop_after == (l, name)
        kb.phase_ffn(xin, xTa, L(f1gu, l), L(f1dn, l), L(ln_g, l, 0), L(ln_b, l, 0), xa, xTb)
        if stop("ffn1"):
            break
        kb.phase_inproj(xTb, L(w2, l), ROPE, FMS, TMB, TMF)
        if stop("inproj"):
            break
        kb.phase_ret(FMS, TMB, YT)
        if stop("ret"):
            break
        kb.phase_ssd(FMS, TMB, TMF, L(conv_w, l), L(conv_b, l), L(dt_bias, l), L(a_log, l), L(d_skip, l), L(norm_g, l), YT)
        if stop("ssd"):
            break
        kb.phase_dsa_nsa(FMS, TMB, TMF, L(cmp_w1, l), L(cmp_w2, l), L(cmp_pos, l), YT)
        if stop("nsa") or stop("dsa"):
            break
        kb.phase_merge(xa, xTb, L(w_in, l), L(w_branch, l), L(w_out, l), L(ln_g, l, 1), L(ln_b, l, 1), YT, xb2, xTa)
        if stop("merge"):
            break
        kb.phase_xattn(xb2, xTa, mem, L(xwq, l), L(xwkv, l), L(xwo, l), L(ln_g, l, 2), L(ln_b, l, 2), xa, xTb)
        if stop("xattn"):
            break
        kb.phase_ffn(xa, xTb, L(f2gu, l), L(f2dn, l), L(ln_g, l, 3), L(ln_b, l, 3), out if last else xb2, None if last else xTa)
        if stop("ffn2"):
            break
        xin = xb2
    st = kb.P.emit()
    kb.stats = st
    return kb


def make_in_maps(inputs, S, ncores):
    meta_np, bdm_np, ovl_np, NB, NCP = host_consts(S)
    colidx = build_colidx()
    w_in = np.asarray(inputs["w_in"], dtype=np.float32)
    w2 = np.ascontiguousarray(w_in[:, :, colidx])
    shared = {k: np.ascontiguousarray(np.asarray(v, dtype=np.float32)) for k, v in inputs.items() if k not in ("x", "mem")}
    shared["w2"] = w2
    shared["meta"] = meta_np
    shared["bdm"] = bdm_np
    shared["ovl"] = ovl_np
    maps = []
    for b in range(ncores):
        m = dict(shared)
        m["x"] = np.ascontiguousarray(np.asarray(inputs["x"][b, :S], dtype=np.float32))
        m["mem"] = np.ascontiguousarray(np.asarray(inputs["mem"][b], dtype=np.float32))
        maps.append(m)
    return maps


def kernel(**inputs):
    S = inputs["x"].shape[1]
    B = inputs["x"].shape[0]
    kb = build(S)
    maps = make_in_maps(inputs, S, B)
    res = run_bass_kernel_spmd(kb.nc, maps, core_ids=list(range(B)))
    out = np.stack([np.asarray(r["out"], dtype=np.float32) for r in res.results], axis=0)
    return out
```

```python
import math
import sys
import numpy as np
import concourse.bass as bass
import concourse.mybir as mybir
from concourse.bass_utils import run_bass_kernel_spmd

F32 = mybir.dt.float32
BF16 = mybir.dt.bfloat16
I32 = mybir.dt.int32
AF = mybir.ActivationFunctionType
ALU = mybir.AluOpType
AX = mybir.AxisListType

SEM_LIMIT = 30000
N_DMA_SEMS = 24


class Buf:
    __slots__ = ("name", "last_w", "readers")

    def __init__(self, name):
        self.name = name
        self.last_w = None
        self.readers = []


class Op:
    __slots__ = ("eng", "fn", "deps", "need_inc", "sem", "val", "is_dma", "idx", "tag")


class Prog:
    def __init__(self, nc):
        self.nc = nc
        self.engs = {"pe": nc.tensor, "act": nc.scalar, "dve": nc.vector, "pool": nc.gpsimd, "sp": nc.sync}
        self.ops = []
        self.bufs = {}
        self.last_on = {}
        self.dmas_since = []
        self.phase_deps = []
        self.phase_bufs = set()
        self.capture = None

    def buf(self, name):
        b = self.bufs.get(name)
        if b is None:
            b = self.bufs[name] = Buf(name)
        return b

    def add(self, eng, fn, reads=(), writes=(), dma=False, extra_deps=()):
        if self.capture is not None:
            self.capture.append(((eng, fn), dict(reads=list(reads), writes=list(writes), dma=dma)))
            return None
        op = Op()
        op.eng = eng
        op.fn = fn
        op.is_dma = dma
        op.need_inc = False
        op.sem = None
        op.val = 0
        op.idx = len(self.ops)
        try:
            f_ = sys._getframe(2)
            op.tag = (f_.f_lineno, f_.f_back.f_lineno if f_.f_back else 0)
        except Exception:
            op.tag = (0, 0)
        deps = {}
        for b in reads:
            b = self.buf(b)
            w = b.last_w
            if w is not None:
                deps[w.idx] = (w, "raw")
        for b in writes:
            b = self.buf(b)
            w = b.last_w
            if w is not None and w.idx not in deps:
                deps[w.idx] = (w, "waw")
            for r in b.readers:
                if r.idx not in deps:
                    deps[r.idx] = (r, "war")
        real = []
        for d, kind in deps.values():
            if (not d.is_dma) and d.eng == eng and not dma:
                if eng == "pe" or kind != "raw":
                    continue
            real.append(d)
        for d in extra_deps:
            real.append(d)
        if self.phase_deps:
            for b in list(reads) + list(writes):
                if b not in self.phase_bufs:
                    self.phase_bufs.add(b)
                    real.extend(self.phase_deps)
        op.deps = real
        for b in writes:
            b = self.buf(b)
            b.last_w = op
            b.readers = []
        for b in reads:
            self.buf(b).readers.append(op)
        self.ops.append(op)
        self.last_on[eng] = op
        if dma:
            self.dmas_since.append(op)
        return op

    def barrier(self):
        self.phase_deps = list(self.last_on.values()) + list(self.dmas_since)
        self.dmas_since = []
        self.phase_bufs = set()

    def emit(self, final_wait_eng="sp"):
        nc = self.nc
        for op in self.ops:
            for d in op.deps:
                d.need_inc = True
            if op.is_dma:
                op.need_inc = True
        eng_sem = {}
        eng_cnt = {}
        dma_sems = [nc.alloc_semaphore("dq%d" % i) for i in range(N_DMA_SEMS)]
        dma_cnt = [0] * N_DMA_SEMS
        dma_last = [None] * N_DMA_SEMS
        ndma = 0
        for op in self.ops:
            if not op.need_inc:
                continue
            if op.is_dma:
                j = ndma % N_DMA_SEMS
                ndma += 1
                if dma_last[j] is not None:
                    op.deps.append(dma_last[j])
                dma_cnt[j] += 16
                op.sem = dma_sems[j]
                op.val = dma_cnt[j]
                dma_last[j] = op
            else:
                e = op.eng
                if e not in eng_sem or eng_cnt[e] >= SEM_LIMIT:
                    eng_sem[e] = nc.alloc_semaphore("s_%s_%d" % (e, op.idx))
                    eng_cnt[e] = 0
                eng_cnt[e] += 1
                op.sem = eng_sem[e]
                op.val = eng_cnt[e]
        waited = {}
        nwaits = 0
        for op in self.ops:
            E = self.engs[op.eng]
            need = {}
            for d in op.deps:
                k = id(d.sem)
                if k not in need or need[k][1] < d.val:
                    need[k] = (d.sem, d.val)
            for k, (sem, val) in need.items():
                wk = (op.eng, k)
                if waited.get(wk, 0) >= val:
                    continue
                E.wait_ge(sem, val)
                nwaits += 1
                waited[wk] = val
            try:
                inst = op.fn()
            except Exception:
                print('EMIT FAIL at op', op.idx, op.eng, 'lines', op.tag)
                raise
            if op.need_inc:
                inst.then_inc(op.sem, 16 if op.is_dma else 1)
        E = self.engs[final_wait_eng]
        for j in range(N_DMA_SEMS):
            if dma_cnt[j] > 0:
                E.wait_ge(dma_sems[j], dma_cnt[j])
        self.stats = dict(n_ops=len(self.ops), n_waits=nwaits, n_dma=ndma,
                          n_inc=sum(1 for o in self.ops if o.need_inc))
        return self.stats


class V:
    __slots__ = ("ap", "b")

    def __init__(self, ap, b):
        self.ap = ap
        self.b = b


class T:
    def __init__(self, h, name, dram=False):
        self.h = h
        self.name = name
        self.dram = dram

    def __getitem__(self, idx):
        if self.dram:
            return V(self.h[idx], self.name)
        return V(self.h[idx], self.name)

    def v(self, ap):
        return V(ap, self.name)


DT_SIZE = {F32: 4, BF16: 2, I32: 4}

D = 1024
DFF = 2816
NKC = D // 128
NFC = DFF // 128
LN_EPS = 1e-5
DEPTH = 2
ALPHA = (2 * DEPTH) ** 0.25
N_MEM = 256


class KB:
    def __init__(self, S, depth=DEPTH, stop_after=None, debug=()):
        self.S = S
        self.depth = depth
        self.stop_after = stop_after
        self.debug = debug
        self.nc = bass.Bass("TRN2", target_bir_lowering=False)
        self.P = Prog(self.nc)
        self.uid = 0
        self.sb_base = 0
        self.sb_cur = 0
        self.outs = {}
        self.arena = None
        self.rots = {}
        self.ps_set = (0, 5)
        self.psb_set = (0, 2)
        self.fill_regs = {}
        self.n_keep = 256

    def sb(self, name, shape, dtype):
        nbytes = int(np.prod(shape[1:])) * DT_SIZE[dtype]
        nbytes = (nbytes + 63) // 64 * 64
        off = self.sb_cur
        self.sb_cur += nbytes
        assert self.sb_cur <= 207 * 1024, ("SBUF overflow", name, self.sb_cur)
        self.uid += 1
        if self.arena is None:
            self.arena = self.nc.alloc_sbuf_tensor("arena", [128, 207 * 1024], mybir.dt.uint8)
        ap = self.arena[:, off:off + int(np.prod(shape[1:])) * DT_SIZE[dtype]].bitcast(dtype)
        if len(shape) == 3:
            ap = ap.rearrange("p (a b) -> p a b", a=shape[1])
        elif len(shape) == 4:
            ap = ap.rearrange("p (a b c) -> p a b c", a=shape[1], b=shape[2])
        if shape[0] < 128:
            ap = ap[0:shape[0]]
        return T(ap, "%s_%d" % (name, self.uid))

    def phase_begin(self):
        self.P.barrier()
        self.sb_cur = self.sb_base

    def dram(self, name, shape, dtype, kind="ExternalOutput"):
        h = self.nc.dram_tensor(name, list(shape), dtype, kind=kind)
        return T(h.ap(), name, dram=True)

    def _rw(self, reads, writes):
        return [r.b for r in reads if isinstance(r, V)], [w.b for w in writes]

    def dma(self, out, in_, eng="sp"):
        nc = self.nc
        E = self.P.engs[eng]
        return self.P.add(eng, lambda: E.dma_start(out=out.ap, in_=in_.ap), reads=[in_.b], writes=[out.b], dma=True)

    def mm(self, out, lhsT, rhs, start=True, stop=True):
        nc = self.nc
        return self.P.add("pe", lambda: nc.tensor.matmul(out.ap, lhsT.ap, rhs.ap, start=start, stop=stop),
                          reads=[lhsT.b, rhs.b], writes=[out.b])

    def tr(self, out, in_, ident):
        nc = self.nc
        return self.P.add("pe", lambda: nc.tensor.transpose(out.ap, in_.ap, ident.ap),
                          reads=[in_.b, ident.b], writes=[out.b])

    def act(self, out, in_, func, bias=None, scale=None, accum=None, eng="act"):
        nc = self.nc
        kw = {}
        reads = [in_.b]
        writes = [out.b]
        if bias is not None:
            if isinstance(bias, V):
                kw["bias"] = bias.ap
                reads.append(bias.b)
            else:
                kw["bias"] = bias
        if scale is not None:
            if isinstance(scale, V):
                kw["scale"] = scale.ap
                reads.append(scale.b)
            else:
                kw["scale"] = scale
        if accum is not None:
            kw["accum_out"] = accum.ap
            writes.append(accum.b)
        return self.P.add("act", lambda: nc.scalar.activation(out=out.ap, in_=in_.ap, func=func, **kw),
                          reads=reads, writes=writes)

    def ts(self, out, in0, s1, s2, op0, op1=None, accum=None, eng="dve"):
        E = self.P.engs[eng]
        reads = [in0.b]
        writes = [out.b]
        a1 = s1
        a2 = s2
        if isinstance(s1, V):
            a1 = s1.ap
            reads.append(s1.b)
        if isinstance(s2, V):
            a2 = s2.ap
            reads.append(s2.b)
        kw = {}
        if op1 is not None:
            kw["op1"] = op1
        if accum is not None:
            kw["accum_out"] = accum.ap
            writes.append(accum.b)
        return self.P.add(eng, lambda: E.tensor_scalar(out=out.ap, in0=in0.ap, scalar1=a1, scalar2=a2, op0=op0, **kw),
                          reads=reads, writes=writes)

    def tt(self, out, in0, in1, op, eng="dve"):
        E = self.P.engs[eng]
        return self.P.add(eng, lambda: E.tensor_tensor(out=out.ap, in0=in0.ap, in1=in1.ap, op=op),
                          reads=[in0.b, in1.b], writes=[out.b])

    def stt(self, out, in0, scalar, in1, op0, op1, accum=None):
        nc = self.nc
        reads = [in0.b, in1.b]
        writes = [out.b]
        a = scalar
        if isinstance(scalar, V):
            a = scalar.ap
            reads.append(scalar.b)
        kw = {}
        if accum is not None:
            kw["accum_out"] = accum.ap
            writes.append(accum.b)
        return self.P.add("dve", lambda: nc.vector.scalar_tensor_tensor(out=out.ap, in0=in0.ap, scalar=a, in1=in1.ap,
                                                                     op0=op0, op1=op1, **kw),
                          reads=reads, writes=writes)

    def copy(self, out, in_, eng="dve"):
        E = self.P.engs[eng]
        if eng == "act":
            return self.P.add(eng, lambda: E.copy(out=out.ap, in_=in_.ap), reads=[in_.b], writes=[out.b])
        return self.P.add(eng, lambda: E.tensor_copy(out=out.ap, in_=in_.ap), reads=[in_.b], writes=[out.b])

    def memset(self, out, val, eng="pool"):
        E = self.P.engs[eng]
        return self.P.add(eng, lambda: E.memset(out.ap, val), writes=[out.b])

    def red(self, out, in_, op, axis=AX.X, eng="dve"):
        E = self.P.engs[eng]
        return self.P.add(eng, lambda: E.tensor_reduce(out=out.ap, in_=in_.ap, axis=axis, op=op),
                          reads=[in_.b], writes=[out.b])

    def recip(self, out, in_):
        nc = self.nc
        return self.P.add("dve", lambda: nc.vector.reciprocal(out=out.ap, in_=in_.ap), reads=[in_.b], writes=[out.b])

    def aselect(self, out, in_, pattern, cmp, fill, base, cm):
        nc = self.nc
        regs = self.fill_regs

        def fn():
            if fill not in regs:
                regs[fill] = nc.gpsimd.to_reg(float(fill))
            return nc.gpsimd.affine_select(out=out.ap, in_=in_.ap, pattern=pattern, compare_op=cmp,
                                           fill=regs[fill], base=base, channel_multiplier=cm)
        return self.P.add("pool", fn, reads=[in_.b], writes=[out.b])

    def iota(self, out, pattern, base, cm):
        nc = self.nc
        return self.P.add("pool", lambda: nc.gpsimd.iota(out.ap, pattern=pattern, base=base, channel_multiplier=cm,
                                                         allow_small_or_imprecise_dtypes=True), writes=[out.b])

    def setup(self):
        nc = self.nc
        self.ps = []
        for i in range(5):
            h = nc.alloc_psum_tensor("ps%d" % i, [128, 512], F32)
            self.ps.append(T(h, "ps%d" % i))
        self.psb = []
        for i in range(2):
            h = nc.alloc_psum_tensor("psb%d" % i, [128, 1024], BF16)
            self.psb.append(T(h, "psb%d" % i))
        self.ps_rr = 0
        self.psb_rr = 0
        self.ident_f = self.sb("identf", [128, 128], F32)
        self.ident = self.sb("ident", [128, 128], BF16)
        self.memset(self.ident_f[:], 1.0)
        self.aselect(self.ident_f[:], self.ident_f[:], [[-1, 128]], ALU.is_equal, 0.0, 0, 1)
        self.copy(self.ident[:], self.ident_f[:], eng="pool")
        self.sb_base = self.sb_cur

    def next_ps(self):
        b0, n = self.ps_set
        t = self.ps[b0 + self.ps_rr % n]
        self.ps_rr += 1
        return t

    def next_psb(self):
        b0, n = self.psb_set
        t = self.psb[b0 + self.psb_rr % n]
        self.psb_rr += 1
        return t

    def load_w(self, name, dram_ap_fn, kchunks, ncols, eng="pool", split=4):
        w = self.sb(name, [128, kchunks, ncols], BF16)
        for k in range(kchunks):
            self.dma(w[:, k, :], dram_ap_fn(k), eng="pool")
        return w

    def layer_norm_tile(self, r, g_bc, b_bc, out_f32, scr):
        st = scr["st"]
        junk = scr["junk"]
        self.act(junk[:], r[:], AF.Identity, accum=st[:, 0:1])
        self.act(junk[:], r[:], AF.Square, accum=st[:, 1:2])
        self.ts(st[:, 2:3], st[:, 0:1], 1.0 / D, None, ALU.mult)
        self.tt(st[:, 3:4], st[:, 2:3], st[:, 2:3], ALU.mult)
        self.stt(st[:, 4:5], st[:, 1:2], 1.0 / D, st[:, 3:4], ALU.mult, ALU.subtract)
        self.ts(st[:, 4:5], st[:, 4:5], 0.0, LN_EPS, ALU.max, ALU.add)
        self.act(st[:, 5:6], st[:, 4:5], AF.Sqrt)
        self.recip(st[:, 6:7], st[:, 5:6])
        self.ts(out_f32[:], r[:], st[:, 2:3], st[:, 6:7], ALU.subtract, ALU.mult)
        self.tt(out_f32[:], out_f32[:], g_bc[:], ALU.mult)
        self.tt(out_f32[:], out_f32[:], b_bc[:], ALU.add)

    def store_xT(self, x_f32, xT_dram, t0, scr):
        xb = scr["xb"]
        xTs = scr["xTs"]
        self.copy(xb[:], x_f32[:], eng="act")
        pb = self.next_psb()
        for k in range(NKC):
            self.tr(pb[:, k * 128:(k + 1) * 128], xb[:, k * 128:(k + 1) * 128], self.ident[:])
        self.copy(xTs[:], pb[:, :], eng="dve")
        self.dma(V(xT_dram.h.rearrange("(k p) s -> p k s", p=128)[:, :, t0:t0 + 128], xT_dram.name),
                 V(xTs.h[:].rearrange("p (k t) -> p k t", k=NKC), xTs.name))

    def dma_s(self, out, in_, eng="sp"):
        E = self.P.engs[eng]
        return self.P.add(eng, lambda: E.dma_start(out=out.ap, in_=in_.ap, allow_slow_non_contiguous=True),
                          reads=[in_.b], writes=[out.b], dma=True)

    def rot(self, name, n, shape, dtype):
        key = "_rot_" + name
        lst = [self.sb(name + str(i), shape, dtype) for i in range(n)]
        self.rots[key] = [lst, 0]
        return key

    def nx(self, key):
        lst, i = self.rots[key]
        self.rots[key][1] = i + 1
        return lst[i % len(lst)]

    def vmax(self, out, in_):
        nc = self.nc
        return self.P.add("dve", lambda: nc.vector.max(out=out.ap, in_=in_.ap), reads=[in_.b], writes=[out.b])

    def match_replace(self, out, rep, vals, imm):
        nc = self.nc
        return self.P.add("dve", lambda: nc.vector.match_replace(out=out.ap, in_to_replace=rep.ap, in_values=vals.ap, imm_value=imm),
                          reads=[rep.b, vals.b], writes=[out.b])

    def redabs(self, out, in_):
        nc = self.nc
        return self.P.add("dve", lambda: nc.vector.tensor_reduce(out=out.ap, in_=in_.ap, axis=AX.X, op=ALU.max,
                                                                 apply_absolute_value=True),
                          reads=[in_.b], writes=[out.b])

    def fm_rows(self, FMS, c0, nchunk, s0, s1):
        return V(FMS.h[c0 * 128:(c0 + nchunk) * 128, s0:s1].rearrange("(c p) s -> p c s", p=128), FMS.name)

    def setup_consts(self, meta, bdm, ovl, NB, NCP):
        S = self.S
        NT = S // 128
        self.NB = NB
        self.NCP = NCP
        self.meta = self.sb("meta", [128, 32], F32)
        self.dma(self.meta[:], meta[:, :])
        self.bdm = self.sb("bdm", [128, 256], F32)
        self.dma(self.bdm[:], bdm[:, :])
        self.ovl = self.sb("ovl", [128, NCP // 128, NB], F32)
        self.dma(self.ovl[:], V(ovl.h.rearrange("(c p) j -> p c j", p=128), ovl.name))
        self.U = self.sb("U", [128, 128], F32)
        self.memset(self.U[:], 1.0)
        self.aselect(self.U[:], self.U[:], [[1, 128]], ALU.is_ge, 0.0, 0, -1)
        self.cneg30 = self.sb("cneg30", [128, 128], F32)
        self.memset(self.cneg30[:], 0.0)
        self.aselect(self.cneg30[:], self.cneg30[:], [[-1, 128]], ALU.is_ge, -1e30, 0, 1)
        self.cneg2k = self.sb("cneg2k", [128, 128], F32)
        self.memset(self.cneg2k[:], 0.0)
        self.aselect(self.cneg2k[:], self.cneg2k[:], [[-1, 128]], ALU.is_ge, -2000.0, 0, 1)
        self.band = self.sb("band", [128, 640], F32)
        self.memset(self.band[:], 0.0)
        self.aselect(self.band[:], self.band[:], [[1, 640]], ALU.is_ge, -2000.0, -1, -1)
        self.aselect(self.band[:], self.band[:], [[-1, 640]], ALU.is_ge, -2000.0, 512, 1)
        self.decayT4 = self.sb("decayT4", [128, 4, 128], F32)
        self.xi = self.sb("xi", [128, 128], F32)
        self.zeta = self.sb("zeta", [128, 128], F32)
        self.cdecay = self.sb("cdecay", [128, 1], F32)
        self.rkc = self.sb("rkc", [128, 20], F32)
        self.sb_base = self.sb_cur
        dji = self.sb("dji", [128, 128], F32)
        self.iota(dji[:], [[1, 128]], 0, -1)
        for h in range(4):
            self.act(self.decayT4[:, h, :], dji[:], AF.Exp, scale=RET_LNG[h])
        self.tt(self.decayT4[:], self.decayT4[:], V(self.U.h[:, :].unsqueeze(1).to_broadcast([128, 4, 128]), self.U.name), ALU.mult)
        ip1 = self.sb("ip1", [128, 128], F32)
        self.iota(ip1[:], [[1, 128]], 1, 0)
        self.act(self.xi[:], ip1[:], AF.Exp, scale=self.meta[:, 12:13])
        jr = self.sb("jr", [128, 128], F32)
        self.iota(jr[:], [[0, 128]], 127, -1)
        for h in range(4):
            self.act(self.zeta[:, 32 * h:32 * h + 32], jr[:, 32 * h:32 * h + 32], AF.Exp, scale=RET_LNG[h])
        c128 = self.sb("c128", [128, 1], F32)
        self.memset(c128[:], 128.0)
        self.act(self.cdecay[:], c128[:], AF.Exp, scale=self.meta[:, 12:13])
        for k in range(20):
            self.memset(self.rkc[:, k:k + 1], 2.0 ** (-k))

    def build_addmask(self):
        NT = self.S // 128
        NB = self.NB
        self.addmask = self.sb("addmask", [128, NT, NB], F32)
        self.memset(self.addmask[:], 0.0)
        for i in range(NT):
            for half in range(2):
                cur = 2 * i + half
                r0 = 64 * half
                v = self.addmask[r0:r0 + 64, i, :]
                self.aselect(v, v, [[-1, NB]], ALU.is_ge, -1e30, cur, 0)
                self.memset(self.addmask[r0:r0 + 64, i, 0:1], 1e30)
                self.memset(self.addmask[r0:r0 + 64, i, cur:cur + 1], 1e30)
                if cur >= 1:
                    self.memset(self.addmask[r0:r0 + 64, i, cur - 1:cur], 1e30)

    def phase_rope(self, ROPE):
        S = self.S
        self.phase_begin()
        pos = self.sb("pos", [128, S], F32)
        self.iota(pos[:], [[1, S]], 0, 0)
        a = self.sb("a", [128, S], F32)
        ki = self.sb("ki", [128, S], I32)
        kf = self.sb("kf", [128, S], F32)
        m = self.sb("m", [128, S], F32)
        r = self.sb("r", [128, S], F32)
        PI = math.pi
        for t in range(4):
            for which in range(2):
                self.ts(a[:], pos[:], self.meta[:, t:t + 1], (PI / 2 if which == 0 else 0.0), ALU.mult, ALU.add)
                self.ts(kf[:], a[:], 1.0 / (2 * PI), None, ALU.mult)
                self.copy(ki[:], kf[:])
                self.copy(kf[:], ki[:])
                self.stt(r[:], kf[:], -2 * PI, a[:], ALU.mult, ALU.add)
                self.ts(m[:], r[:], PI, -2 * PI, ALU.is_gt, ALU.mult)
                self.tt(r[:], r[:], m[:], ALU.add)
                self.ts(m[:], r[:], -PI, 2 * PI, ALU.is_lt, ALU.mult)
                self.tt(r[:], r[:], m[:], ALU.add)
                self.ts(r[:], r[:], PI, -PI, ALU.min, ALU.max)
                self.act(r[:], r[:], AF.Sin)
                col = 4 + 4 * which + t
                self.ts(r[:], r[:], self.meta[:, col:col + 1], None, ALU.mult)
                self.dma(ROPE[t, which], r[:])

    def finish_tile(self, rq, g_bc, b_bc, x_out, xT_out, t0, scr):
        self.layer_norm_tile(rq, g_bc, b_bc, rq, scr)
        self.dma(x_out[t0:t0 + 128, :], rq[:])
        if xT_out is not None:
            self.store_xT(rq, xT_out, t0, scr)

    def ln_setup(self, ln_g, ln_b):
        g_bc = self.sb("g_bc", [128, D], F32)
        b_bc = self.sb("b_bc", [128, D], F32)
        self.dma(g_bc[:], V(ln_g.h.partition_broadcast(128), ln_g.name))
        self.dma(b_bc[:], V(ln_b.h.partition_broadcast(128), ln_b.name))
        scr = dict(st=self.sb("st", [128, 8], F32), junk=self.sb("junk", [128, D], BF16),
                   xb=self.sb("xb", [128, D], BF16), xTs=self.sb("xTs", [128, D], BF16))
        return g_bc, b_bc, scr

    def phase_ffn(self, x_in, xT_in, w_gu, w_down, ln_g, ln_b, x_out, xT_out):
        S = self.S
        self.phase_begin()
        wgu = self.load_w("wgu", lambda k: V(w_gu.h[k * 128:(k + 1) * 128, :], w_gu.name), NKC, 2 * DFF)
        wdn = self.load_w("wdn", lambda k: V(w_down.h[k * 128:(k + 1) * 128, :], w_down.name), NFC, D)
        g_bc, b_bc, scr = self.ln_setup(ln_g, ln_b)
        xt = self.sb("xT", [128, NKC, 512], BF16)
        hT = self.sb("hT", [128, NFC, 512], BF16)
        sg = [self.sb("sg%d" % i, [128, 512], BF16) for i in range(2)]
        xr = [self.sb("xr%d" % i, [128, D], F32) for i in range(2)]
        rqs = [self.sb("r%d" % i, [128, D], F32) for i in range(2)]
        ntile = S // 512

        def load_xt(t_):
            self.dma(xt[:], V(xT_in.h.rearrange("(k p) s -> p k s", p=128)[:, :, t_ * 512:(t_ + 1) * 512], xT_in.name))

        def load_xq(idx):
            self.dma(xr[idx % 2][:], x_in[idx * 128:(idx + 1) * 128, :])
        load_xt(0)
        load_xq(0)
        for t in range(ntile):
            for j in range(NFC):
                pg = self.next_ps()
                pu = self.next_ps()
                for k in range(NKC):
                    self.mm(pg[:], wgu[:, k, j * 128:(j + 1) * 128], xt[:, k, :], start=(k == 0), stop=(k == NKC - 1))
                for k in range(NKC):
                    self.mm(pu[:], wgu[:, k, DFF + j * 128:DFF + (j + 1) * 128], xt[:, k, :], start=(k == 0), stop=(k == NKC - 1))
                s = sg[j % 2]
                self.act(s[:], pg[:], AF.Silu)
                self.tt(hT[:, j, :], s[:], pu[:], ALU.mult)
            if t + 1 < ntile:
                load_xt(t + 1)
            for q in range(4):
                t0 = t * 512 + q * 128
                xq = xr[q % 2]
                rq = rqs[q % 2]
                if t * 4 + q + 1 < ntile * 4:
                    load_xq(t * 4 + q + 1)
                for half in range(2):
                    hs = slice(half * 512, (half + 1) * 512)
                    pd = self.next_ps()
                    for j in range(NFC):
                        self.mm(pd[:], hT[:, j, q * 128:(q + 1) * 128], wdn[:, j, hs], start=(j == 0), stop=(j == NFC - 1))
                    self.act(xq[:, hs], xq[:, hs], AF.Copy, scale=ALPHA)
                    self.stt(rq[:, hs], pd[:], 0.5, xq[:, hs], ALU.mult, ALU.add)
                self.finish_tile(rq, g_bc, b_bc, x_out, xT_out, t0, scr)

    def phase_transpose_in(self, x_in, xT_out):
        S = self.S
        self.phase_begin()
        xr = [self.sb("xr%d" % i, [128, D], F32) for i in range(2)]
        scr = dict(xb=self.sb("xb", [128, D], BF16), xTs=self.sb("xTs", [128, D], BF16))
        for i in range(S // 128):
            xq = xr[i % 2]
            self.dma(xq[:], x_in[i * 128:(i + 1) * 128, :])
            self.store_xT(xq, xT_out, i * 128, scr)

    def phase_inproj(self, xT_in, w2, ROPE, FMS, TMB, TMF):
        S = self.S
        self.phase_begin()
        w = self.load_w("win", lambda k: V(w2.h[k * 128:(k + 1) * 128, :], w2.name), NKC, NCOL2)
        xts = [self.sb("xT%d" % i_, [128, NKC, 512], BF16) for i_ in range(2)]
        tabs = [self.sb("tab%d" % i_, [128, 4, 2, 512], F32) for i_ in range(2)]
        t1 = self.rot("t1", 3, [128, 512], F32)
        t2 = self.rot("t2", 3, [128, 512], F32)
        ob = self.rot("ob", 4, [128, 512], BF16)
        tmb = self.rot("tmb", 2, [128, 960], BF16)
        tmf = self.rot("tmf", 2, [128, 24], F32)
        ntile = S // 512

        def load_t(t_):
            ss_ = slice(t_ * 512, (t_ + 1) * 512)
            self.dma(xts[t_ % 2][:], V(xT_in.h.rearrange("(k p) s -> p k s", p=128)[:, :, ss_], xT_in.name))
            self.dma(tabs[t_ % 2][:], V(ROPE.h[:, :, :, ss_].rearrange("t w p s -> p t w s"), ROPE.name))
        load_t(0)
        for t in range(ntile):
            ss = slice(t * 512, (t + 1) * 512)
            xt = xts[t % 2]
            tab = tabs[t % 2]
            if t + 1 < ntile:
                load_t(t + 1)
            for ci, tb in enumerate(ROPED_TABLES):
                pA = self.next_ps()
                pB = self.next_ps()
                for k in range(NKC):
                    self.mm(pA[:], w[:, k, (2 * ci) * 128:(2 * ci + 1) * 128], xt[:, k, :], start=(k == 0), stop=(k == NKC - 1))
                for k in range(NKC):
                    self.mm(pB[:], w[:, k, (2 * ci + 1) * 128:(2 * ci + 2) * 128], xt[:, k, :], start=(k == 0), stop=(k == NKC - 1))
                a1 = self.nx(t1)
                a2 = self.nx(t2)
                o = self.nx(ob)
                self.tt(a1[:], pA[:], tab[:, tb, 0, :], ALU.mult)
                self.tt(a2[:], pB[:], tab[:, tb, 1, :], ALU.mult)
                self.tt(o[:], a1[:], a2[:], ALU.add)
                self.dma(V(FMS.h[ci * 128:(ci + 1) * 128, ss], FMS.name), o[:])
            nr = len(ROPED_TABLES)
            for j in range(7):
                wc = 2 * nr + j
                pA = self.next_ps()
                for k in range(NKC):
                    self.mm(pA[:], w[:, k, wc * 128:(wc + 1) * 128], xt[:, k, :], start=(k == 0), stop=(k == NKC - 1))
                o = self.nx(ob)
                self.copy(o[:], pA[:], eng="act")
                self.dma(V(FMS.h[(nr + j) * 128:(nr + j + 1) * 128, ss], FMS.name), o[:])
            for q in range(4):
                t0 = t * 512 + q * 128
                pA = self.next_ps()
                pB = self.next_ps()
                for k in range(NKC):
                    self.mm(pA[:], xt[:, k, q * 128:(q + 1) * 128], w[:, k, TM0:TM0 + 512], start=(k == 0), stop=(k == NKC - 1))
                for k in range(NKC):
                    self.mm(pB[:, 0:472], xt[:, k, q * 128:(q + 1) * 128], w[:, k, TM0 + 512:TM0 + 984], start=(k == 0), stop=(k == NKC - 1))
                b = self.nx(tmb)
                f = self.nx(tmf)
                self.copy(b[:, 0:512], pA[:], eng="act")
                self.copy(b[:, 512:960], pB[:, 0:448])
                self.copy(f[:], pB[:, 448:472])
                self.dma(TMB[t0:t0 + 128, :], b[:])
                self.dma(TMF[t0:t0 + 128, :], f[:])

    def store_yT(self, y, YT, br, n, yTs_key):
        pb = self.next_psb()
        self.tr(pb[:, 0:128], y[:, 0:128], self.ident[:])
        self.tr(pb[:, 128:256], y[:, 128:256], self.ident[:])
        yTs = self.nx(yTs_key)
        self.copy(yTs[:], pb[:, 0:256])
        self.dma(V(YT.h[br * 256:(br + 1) * 256, n * 128:(n + 1) * 128].rearrange("(c p) t -> p c t", p=128), YT.name),
                 V(yTs.h[:, :].rearrange("p (c t) -> p c t", c=2), yTs.name))

    def phase_ret(self, FMS, TMB, YT):
        S = self.S
        NT = S // 128
        self.phase_begin()
        rq = self.sb("rq", [128, S], BF16)
        rk = self.sb("rk", [128, S], BF16)
        self.dma(rq[:], V(FMS.h[0:128, :], FMS.name))
        self.dma(rk[:], V(FMS.h[128:256, :], FMS.name))
        Sbd = self.sb("Sbd", [128, 256], F32)
        Sbd_bf = self.sb("Sbd_bf", [128, 256], BF16)
        self.memset(Sbd[:], 0.0)
        self.memset(Sbd_bf[:], 0.0)
        vt_k = self.rot("vt", 2, [128, 512], BF16)
        qxi_k = self.rot("qxi", 2, [128, 128], BF16)
        qm_k = self.rot("qm", 2, [128, 4, 128], BF16)
        kz_k = self.rot("kz", 2, [128, 128], BF16)
        PT_k = self.rot("PT", 2, [128, 4, 128], BF16)
        cross_k = self.rot("cross", 2, [128, 256], F32)
        o_k = self.rot("o", 2, [128, 256], F32)
        tmp_k = self.rot("tmp", 2, [128, 256], F32)
        osq_k = self.rot("osq", 2, [128, 256], F32)
        sg_k = self.rot("sg", 2, [128, 256], F32)
        st_k = self.rot("st", 2, [128, 16], F32)
        y_k = self.rot("y", 2, [128, 256], BF16)
        yTs_k = self.rot("yTs", 2, [128, 256], BF16)
        hm = V(self.meta.h[:, 13:17].unsqueeze(2).to_broadcast([128, 4, 128]), self.meta.name)
        for n in range(NT):
            sl = slice(n * 128, (n + 1) * 128)
            vt = self.nx(vt_k)
            self.dma(vt[:], TMB[n * 128:(n + 1) * 128, 0:512])
            qxi = self.nx(qxi_k)
            self.tt(qxi[:], rq[:, sl], self.xi[:], ALU.mult)
            qm = self.nx(qm_k)
            self.tt(qm[:], V(rq.h[:, sl].unsqueeze(1).to_broadcast([128, 4, 128]), rq.name), hm, ALU.mult, eng="pool")
            pb = self.next_psb()
            self.tr(pb[:, 0:128], rk[:, sl], self.ident[:])
            kz = self.nx(kz_k)
            self.tt(kz[:], pb[:, 0:128], self.zeta[:], ALU.mult)
            ps1 = self.next_ps()
            self.mm(ps1[:], rk[:, sl], V(qm.h[:, :, :].rearrange("p h i -> p (h i)"), qm.name))
            PT = self.nx(PT_k)
            self.tt(PT[:], V(ps1.h[:, :].rearrange("p (h i) -> p h i", h=4), ps1.name), self.decayT4[:], ALU.mult)
            ps2 = self.next_ps()
            self.mm(ps2[:, 0:256], qxi[:], Sbd_bf[:])
            cross = self.nx(cross_k)
            self.copy(cross[:], ps2[:, 0:256], eng="act")
            ps3 = self.next_ps()
            for h in range(4):
                self.mm(ps3[:, 64 * h:64 * h + 64], PT[:, h, :], vt[:, 64 * h:64 * h + 64])
            o = self.nx(o_k)
            self.tt(o[:], ps3[:, 0:256], cross[:], ALU.add)
            ps4 = self.next_ps()
            self.mm(ps4[:, 0:256], kz[:], vt[:, 0:256])
            tmp = self.nx(tmp_k)
            self.tt(tmp[:], ps4[:, 0:256], self.bdm[:], ALU.mult)
            self.stt(Sbd[:], Sbd[:], self.cdecay[:, 0:1], tmp[:], ALU.mult, ALU.add)
            self.copy(Sbd_bf[:], Sbd[:], eng="act")
            st = self.nx(st_k)
            o3 = V(o.h[:, :].rearrange("p (h e) -> p h e", h=4), o.name)
            self.red(st[:, 0:4], o3, ALU.add)
            osq = self.nx(osq_k)
            self.tt(osq[:], o[:], o[:], ALU.mult, eng="pool")
            self.red(st[:, 4:8], V(osq.h[:, :].rearrange("p (h e) -> p h e", h=4), osq.name), ALU.add)
            self.ts(st[:, 8:12], st[:, 0:4], 1.0 / 64, None, ALU.mult)
            self.tt(st[:, 12:16], st[:, 8:12], st[:, 8:12], ALU.mult)
            self.stt(st[:, 4:8], st[:, 4:8], 1.0 / 64, st[:, 12:16], ALU.mult, ALU.subtract)
            self.ts(st[:, 4:8], st[:, 4:8], 0.0, LN_EPS, ALU.max, ALU.add)
            self.act(st[:, 4:8], st[:, 4:8], AF.Sqrt)
            self.recip(st[:, 4:8], st[:, 4:8])
            self.tt(o3, o3, V(st.h[:, 8:12].unsqueeze(2).to_broadcast([128, 4, 64]), st.name), ALU.subtract)
            self.tt(o3, o3, V(st.h[:, 4:8].unsqueeze(2).to_broadcast([128, 4, 64]), st.name), ALU.mult)
            sg = self.nx(sg_k)
            self.act(sg[:], vt[:, 256:512], AF.Silu)
            y = self.nx(y_k)
            self.tt(y[:], o[:], sg[:], ALU.mult)
            self.store_yT(y, YT, 0, n, yTs_k)

    def phase_ssd(self, FMS, TMB, TMF, conv_w, conv_b, dt_bias, a_log, d_skip, norm_g, YT):
        S = self.S
        NT = S // 128
        self.phase_begin()
        cw = self.sb("cw", [128, 6, 4], F32)
        for k_ in range(4):
            self.dma_s(cw[:, :, k_], V(conv_w.h[k_].rearrange("(c p) -> p c", p=128), conv_w.name))
        cb = self.sb("cb", [128, 6], F32)
        self.dma_s(cb[:], V(conv_b.h.rearrange("(c p) -> p c", p=128), conv_b.name))
        dtb = self.sb("dtb", [128, 4], F32)
        self.dma(dtb[:], V(dt_bias.h.partition_broadcast(128), dt_bias.name))
        a_bc = self.sb("a_bc", [128, 4], F32)
        self.dma(a_bc[:], V(a_log.h.partition_broadcast(128), a_log.name))
        self.act(a_bc[:], a_bc[:], AF.Exp)
        self.ts(a_bc[:], a_bc[:], -1.0, None, ALU.mult)
        Dbc = self.sb("Dbc", [128, 4], F32)
        self.dma(Dbc[:], V(d_skip.h.partition_broadcast(128), d_skip.name))
        ng_bc = self.sb("ng_bc", [128, 256], F32)
        self.dma(ng_bc[:], V(norm_g.h.partition_broadcast(128), norm_g.name))
        xbcs = self.sb("xbcs", [128, 6, S], BF16)
        raw_k = self.rot("raw", 2, [128, 6, 515], BF16)
        acc_k = self.rot("acc", 2, [128, 512], F32)
        for t in range(S // 512):
            raw = self.nx(raw_k)
            if t == 0:
                self.memset(raw[:, :, 0:3], 0.0)
                self.dma(raw[:, :, 3:515], self.fm_rows(FMS, 14, 6, 0, 512))
            else:
                self.dma(raw[:, :, 0:515], self.fm_rows(FMS, 14, 6, t * 512 - 3, (t + 1) * 512))
            for c in range(6):
                acc = self.nx(acc_k)
                self.ts(acc[:], raw[:, c, 3:515], cw[:, c, 3:4], None, ALU.mult)
                for k in (2, 1, 0):
                    self.stt(acc[:], raw[:, c, k:k + 512], cw[:, c, k:k + 1], acc[:], ALU.mult, ALU.add)
                self.act(xbcs[:, c, t * 512:(t + 1) * 512], acc[:], AF.Silu, bias=cb[:, c:c + 1])
        prev = self.sb("prev", [128, 256], F32)
        prev_bf = self.sb("prev_bf", [128, 256], BF16)
        self.memset(prev[:], 0.0)
        self.memset(prev_bf[:], 0.0)
        xsB_k = self.rot("xsB", 2, [128, 512], BF16)
        tmf_k = self.rot("tmf", 2, [128, 24], F32)
        zt_k = self.rot("zt", 2, [128, 256], BF16)
        st_k = self.rot("st", 2, [128, 32], F32)
        adtb_k = self.rot("adtb", 2, [128, 4, 128], F32)
        seg_k = self.rot("seg", 2, [128, 4, 128], F32)
        MT_k = self.rot("MT", 2, [128, 4, 128], BF16)
        X_k = self.rot("X", 2, [128, 256], BF16)
        Xd_k = self.rot("Xd", 2, [128, 256], BF16)
        yd_k = self.rot("yd", 2, [128, 256], F32)
        y_k = self.rot("y", 2, [128, 256], F32)
        t2_k = self.rot("t2", 2, [128, 256], F32)
        sz_k = self.rot("sz", 2, [128, 256], F32)
        yb_k = self.rot("yb", 2, [128, 256], BF16)
        yTs_k = self.rot("yTs", 2, [128, 256], BF16)
        Ubc = V(self.U.h[:, :].unsqueeze(1).to_broadcast([128, 4, 128]), self.U.name)

        def h4(t_):
            return V(t_.h[:, 0:256].rearrange("p (h e) -> p h e", h=4), t_.name)

        def bc4(v_):
            return V(v_.ap.unsqueeze(2).to_broadcast([128, 4, 64]), v_.b)

        for n in range(NT):
            sl = slice(n * 128, (n + 1) * 128)
            pb = self.next_psb()
            for c in range(4):
                self.tr(pb[:, c * 128:(c + 1) * 128], xbcs[:, c, sl], self.ident[:])
            xsB = self.nx(xsB_k)
            self.copy(xsB[:], pb[:, 0:512])
            tmf = self.nx(tmf_k)
            self.dma(tmf[:], TMF[n * 128:(n + 1) * 128, :])
            zt = self.nx(zt_k)
            self.dma(zt[:], TMB[n * 128:(n + 1) * 128, 704:960])
            st = self.nx(st_k)
            self.tt(st[:, 0:4], tmf[:, 20:24], dtb[:], ALU.add)
            self.act(st[:, 0:4], st[:, 0:4], AF.Exp)
            self.act(st[:, 0:4], st[:, 0:4], AF.Ln, bias=1.0)
            self.tt(st[:, 4:8], st[:, 0:4], a_bc[:], ALU.mult)
            adtb = self.nx(adtb_k)
            self.copy(adtb[:], V(st.h[:, 4:8].unsqueeze(2).to_broadcast([128, 4, 128]), st.name))
            psA = self.next_ps()
            self.mm(psA[:, 0:4], self.U[:], st[:, 4:8])
            self.copy(st[:, 8:12], psA[:, 0:4], eng="act")
            psB = self.next_ps()
            for h in range(4):
                self.mm(psB[:, h * 128:(h + 1) * 128], adtb[:, h, :], self.U[:])
            seg = self.nx(seg_k)
            for h in range(4):
                self.ts(seg[:, h, :], psB[:, h * 128:(h + 1) * 128], st[:, 8 + h:9 + h], 0.0, ALU.subtract, ALU.min)
            self.act(seg[:], seg[:], AF.Exp)
            self.tt(seg[:], seg[:], Ubc, ALU.mult, eng="pool")
            alast = V(psB.h[:, 127:512:128], psB.name)
            self.tt(st[:, 12:16], alast, st[:, 8:12], ALU.subtract)
            self.act(st[:, 12:16], st[:, 12:16], AF.Exp)
            self.act(st[:, 16:20], alast, AF.Exp)
            self.act(st[:, 20:24], st[:, 8:12], AF.Exp)
            psG = self.next_ps()
            for g in range(2):
                self.mm(psG[:, g * 128:(g + 1) * 128], xbcs[:, 2 + g, sl], xbcs[:, 4 + g, sl])
            MT = self.nx(MT_k)
            for g in range(2):
                self.tt(MT[:, 2 * g:2 * g + 2, :], seg[:, 2 * g:2 * g + 2, :],
                        V(psG.h[:, g * 128:(g + 1) * 128].unsqueeze(1).to_broadcast([128, 2, 128]), psG.name), ALU.mult)
            X = self.nx(X_k)
            self.tt(h4(X), h4(xsB), bc4(st[:, 0:4]), ALU.mult)
            psY = self.next_ps()
            for h in range(4):
                self.mm(psY[:, 64 * h:64 * h + 64], MT[:, h, :], X[:, 64 * h:64 * h + 64])
            psO = self.next_ps()
            for g in range(2):
                self.mm(psO[:, 128 * g:128 * g + 128], xbcs[:, 4 + g, sl], prev_bf[:, 128 * g:128 * g + 128])
            yd = self.nx(yd_k)
            self.copy(yd[:], psY[:, 0:256], eng="act")
            y = self.nx(y_k)
            self.tt(h4(y), h4(psO), bc4(st[:, 20:24]), ALU.mult)
            self.tt(y[:], y[:], yd[:], ALU.add)
            t2 = self.nx(t2_k)
            self.tt(h4(t2), h4(xsB), bc4(Dbc[:, 0:4]), ALU.mult, eng="pool")
            self.tt(y[:], y[:], t2[:], ALU.add)
            Xd = self.nx(Xd_k)
            self.tt(h4(Xd), h4(X), bc4(st[:, 12:16]), ALU.mult, eng="pool")
            psS = self.next_ps()
            for g in range(2):
                self.mm(psS[:, 128 * g:128 * g + 128], xsB[:, 256 + 128 * g:256 + 128 * g + 128], Xd[:, 128 * g:128 * g + 128])
            self.tt(h4(prev), h4(prev), bc4(st[:, 16:20]), ALU.mult)
            self.tt(prev[:], prev[:], psS[:, 0:256], ALU.add)
            self.copy(prev_bf[:], prev[:], eng="act")
            sz = self.nx(sz_k)
            self.act(sz[:], zt[:], AF.Silu)
            self.tt(y[:], y[:], sz[:], ALU.mult)
            self.tt(t2[:], y[:], y[:], ALU.mult, eng="pool")
            self.red(st[:, 24:26], V(t2.h[:, :].rearrange("p (g e) -> p g e", g=2), t2.name), ALU.add)
            self.ts(st[:, 24:26], st[:, 24:26], 1.0 / 128, LN_EPS, ALU.mult, ALU.add)
            self.act(st[:, 24:26], st[:, 24:26], AF.Sqrt)
            self.recip(st[:, 24:26], st[:, 24:26])
            y2 = V(y.h[:, :].rearrange("p (g e) -> p g e", g=2), y.name)
            self.tt(y2, y2, V(st.h[:, 24:26].unsqueeze(2).to_broadcast([128, 2, 128]), st.name), ALU.mult)
            yb = self.nx(yb_k)
            self.tt(yb[:], y[:], ng_bc[:], ALU.mult)
            self.store_yT(yb, YT, 3, n, yTs_k)

    def softmax_pv(self, Ssb, nk, Vt, kt0, out, kk, clamp=None):
        st = self.nx(kk["st"])
        self.red(st[:, 0:1], Ssb, ALU.max)
        if clamp is not None:
            self.ts(st[:, 0:1], st[:, 0:1], clamp, None, ALU.max)
        self.ts(st[:, 1:2], st[:, 0:1], -1.0, None, ALU.mult)
        P = self.nx(kk["P"])
        self.act(P[:, 0:nk], Ssb, AF.Exp, bias=st[:, 1:2], accum=st[:, 2:3])
        self.ts(st[:, 3:4], st[:, 2:3], 1e-30, None, ALU.max)
        self.recip(st[:, 4:5], st[:, 3:4])
        po = self.next_ps()
        nkt = nk // 128
        for g0 in range(0, nkt, 8):
            gn = min(8, nkt - g0)
            pb = self.next_psb()
            for j in range(gn):
                self.tr(pb[:, j * 128:(j + 1) * 128], P[:, (g0 + j) * 128:(g0 + j + 1) * 128], self.ident[:])
            PT = self.nx(kk["PT"])
            self.copy(PT[:, 0:gn * 128], pb[:, 0:gn * 128], eng="act")
            for j in range(gn):
                self.mm(po[:, 0:64], PT[:, j * 128:(j + 1) * 128], Vt[:, kt0 + g0 + j, :],
                        start=(g0 + j == 0), stop=(g0 + j == nkt - 1))
        self.ts(out, po[:, 0:64], st[:, 4:5], None, ALU.mult)

    def attn_keys(self, pfx):
        S = self.S
        return dict(st=self.rot(pfx + "sst", 2, [128, 8], F32), P=self.rot(pfx + "P", 1, [128, S], BF16),
                    PT=self.rot(pfx + "PTa", 2, [128, 1024], BF16))

    def dsa_setup(self, FMS, TMB, TMF, YT):
        S = self.S
        NT = S // 128
        c = dict(FMS=FMS, YT=YT)
        c["dk"] = self.sb("dk", [128, S], BF16)
        self.dma(c["dk"][:], V(FMS.h[4 * 128:5 * 128, :], FMS.name))
        ikr = self.sb("ikr", [128, S], BF16)
        self.dma(ikr[:], V(FMS.h[7 * 128:8 * 128, :], FMS.name))
        c["ikm"] = self.sb("ikm", [128, 4, S], BF16)
        for g in range(4):
            self.ts(c["ikm"][:, g, :], ikr[:], self.meta[:, 13 + g:14 + g], None, ALU.mult, eng=("pool" if g % 2 else "dve"))
        c["Vt"] = self.sb("Vt", [128, NT, 64], BF16)
        self.dma(c["Vt"][:], V(TMB.h[:, 512:576].rearrange("(n p) c -> p n c", p=128), TMB.name))
        iw = self.sb("iw", [128, NT, 8], F32)
        self.dma(iw[:], V(TMF.h[:, 0:8].rearrange("(n p) c -> p n c", p=128), TMF.name))
        c["absw"] = self.sb("absw", [128, NT, 8], F32)
        self.act(c["absw"][:], iw[:], AF.Abs, scale=1.0 / 16)
        c["sgn"] = self.sb("sgn", [128, NT, 8], F32)
        self.ts(c["sgn"][:], iw[:], 0.0, 2.0, ALU.is_ge, ALU.mult)
        self.ts(c["sgn"][:], c["sgn"][:], -1.0, None, ALU.add)
        c["I"] = self.sb("I", [128, S], F32)
        c["Ssb"] = self.sb("dSsb", [128, S], F32)
        c["kk"] = self.attn_keys("d")
        c["q"] = self.rot("dqi", 2, [128, 4, 128], BF16)
        c["tmp"] = self.rot("tmpr", 2, [128, 512], F32)
        c["st"] = self.rot("dst", 2, [128, 16], F32)
        c["Rk"] = self.rot("Rk", 2, [128, 20], F32)
        c["nm"] = self.rot("dnm", 2, [128, 2], F32)
        c["c2"] = self.rot("dc2", 2, [128, 2], F32)
        c["o"] = self.rot("do", 2, [128, 256], F32)
        c["y"] = self.rot("dy", 2, [128, 256], BF16)
        c["yTs"] = self.rot("dyTs", 2, [128, 256], BF16)
        return c

    def dsa_tile(self, c, i):
        FMS = c["FMS"]
        I = c["I"]
        Ssb = c["Ssb"]
        nk = 128 * (i + 1)
        nkc = (nk + 511) // 512
        q = self.nx(c["q"])
        self.dma(q[:, 0:2, :], self.fm_rows(FMS, 2, 2, i * 128, (i + 1) * 128))
        self.dma(q[:, 2:4, :], self.fm_rows(FMS, 5, 2, i * 128, (i + 1) * 128))
        for kc in range(nkc):
            c0 = kc * 512
            cols = min(512, nk - c0)
            for h in range(8):
                ps = self.next_ps()
                self.mm(ps[:, 0:cols], q[:, 2 + h // 4, :], c["ikm"][:, h % 4, c0:c0 + cols])
                tmp = self.nx(c["tmp"])
                self.act(tmp[:, 0:cols], ps[:, 0:cols], AF.Relu, scale=c["absw"][:, i, h:h + 1])
                if h == 0:
                    self.ts(I[:, c0:c0 + cols], tmp[:, 0:cols], c["sgn"][:, i, 0:1], None, ALU.mult)
                else:
                    self.stt(I[:, c0:c0 + cols], tmp[:, 0:cols], c["sgn"][:, i, h:h + 1], I[:, c0:c0 + cols], ALU.mult, ALU.add)
        if nk > self.n_keep:
            st = self.nx(c["st"])
            junk = self.nx(c["kk"]["P"])
            self.redabs(st[:, 0:1], I[:, 0:nk])
            self.ts(st[:, 0:1], st[:, 0:1], 1e-20, None, ALU.max)
            self.tt(I[:, nk - 128:nk], I[:, nk - 128:nk], self.cneg30[:], ALU.add)
            Rk = self.nx(c["Rk"])
            self.ts(Rk[:], self.rkc[:], st[:, 0:1], None, ALU.mult)
            self.ts(st[:, 1:2], st[:, 0:1], -1.0, None, ALU.mult)
            n1 = (nk // 2 + 127) // 128 * 128
            n2 = nk - n1
            thr_c = self.n_keep - 0.5 - n2 / 2.0
            for k in range(NBIS):
                self.tt(st[:, 2:3], st[:, 1:2], Rk[:, k:k + 1], ALU.add)
                nm = self.nx(c["nm"])
                c2 = self.nx(c["c2"])
                self.ts(nm[:, 0:1], st[:, 2:3], -1.0, None, ALU.mult)
                self.act(Ssb[:, n1:nk], I[:, n1:nk], AF.Sign, bias=nm[:, 0:1], accum=c2[:, 0:1])
                self.ts(junk[:, 0:n1], I[:, 0:n1], st[:, 2:3], None, ALU.is_ge, ALU.add, accum=st[:, 3:4])
                self.stt(st[:, 4:5], c2[:, 0:1], 0.5, st[:, 3:4], ALU.mult, ALU.add)
                self.ts(st[:, 4:5], st[:, 4:5], thr_c, None, ALU.is_ge)
                self.stt(st[:, 1:2], st[:, 4:5], Rk[:, k:k + 1], st[:, 1:2], ALU.mult, ALU.add)
            self.ts(I[:, 0:nk], I[:, 0:nk], st[:, 1:2], 1000.0, ALU.is_ge, ALU.mult)
        else:
            self.ts(I[:, 0:nk], I[:, 0:nk], 0.0, 1000.0, ALU.mult, ALU.add)
            self.tt(I[:, nk - 128:nk], I[:, nk - 128:nk], self.cneg2k[:], ALU.add)
        o = self.nx(c["o"])
        for h in range(4):
            base = 64 * (h % 2)
            cq = h // 2
            for kc in range(nkc):
                c0 = kc * 512
                cols = min(512, nk - c0)
                ps = self.next_ps()
                self.mm(ps[:, 0:cols], q[base:base + 64, cq, :], c["dk"][base:base + 64, c0:c0 + cols])
                self.stt(Ssb[:, c0:c0 + cols], ps[:, 0:cols], 0.125, I[:, c0:c0 + cols], ALU.mult, ALU.add)
            self.softmax_pv(Ssb[:, 0:nk], nk, c["Vt"], 0, o[:, 64 * h:64 * h + 64], c["kk"])
        y = self.nx(c["y"])
        self.copy(y[:], o[:], eng="act")
        self.store_yT(y, c["YT"], 1, i, c["yTs"])

    def nsa_setup(self, FMS, TMB, TMF, cmp_w1, cmp_w2, cmp_pos, YT):
        S = self.S
        NT = S // 128
        NB = self.NB
        NCP = self.NCP
        NC = (S - 32) // 16 + 1
        NCT = NCP // 128
        c = dict(FMS=FMS, YT=YT)
        c["ksT"] = self.sb("ksT", [128, S], BF16)
        self.dma(c["ksT"][:], V(FMS.h[11 * 128:12 * 128, :], FMS.name))
        c["kwT"] = self.sb("kwT", [128, S], BF16)
        self.dma(c["kwT"][:], V(FMS.h[12 * 128:13 * 128, :], FMS.name))
        c["Vs"] = self.sb("Vs", [128, NT, 64], BF16)
        self.dma(c["Vs"][:], V(TMB.h[:, 576:640].rearrange("(n p) c -> p n c", p=128), TMB.name))
        c["Vw"] = self.sb("Vw", [128, NT, 64], BF16)
        self.dma(c["Vw"][:], V(TMB.h[:, 640:704].rearrange("(n p) c -> p n c", p=128), TMB.name))
        c["ngt"] = self.sb("ngt", [128, NT, 12], F32)
        self.dma(c["ngt"][:], V(TMF.h[:, 8:20].rearrange("(n p) c -> p n c", p=128), TMF.name))
        kcmp = self.sb("kcmp", [128, NCP], BF16)
        vcmp = self.sb("vcmp", [128, NCT, 64], BF16)
        c["kcmp"] = kcmp
        c["vcmp"] = vcmp
        c["Ssb"] = self.sb("nSsb", [128, S], F32)
        c["Sw"] = self.sb("Sw", [128, 640], F32)
        c["kk"] = self.attn_keys("n")
        save = self.sb_cur
        srcT = self.sb("srcT", [128, S], BF16)
        w1 = self.sb("w1", [64, 32, 64], BF16)
        w2 = self.sb("w2", [64, 128], BF16)
        posT = self.sb("posT", [64, 32], F32)
        posb = self.sb("posb", [64, 32], BF16)
        cst = self.sb("cst", [64, 1], F32)
        u = self.sb("u", [64, NCP], F32)
        u2 = self.sb("u2", [64, NCP], F32)
        gl = self.sb("gl", [64, NCP], BF16)
        for i in range(2):
            self.dma(srcT[:], V(FMS.h[(10 + 3 * i) * 128:(11 + 3 * i) * 128, :], FMS.name))
            self.dma(w1[:], V(cmp_w1.h[i].rearrange("(l d) f -> d l f", d=64), cmp_w1.name), eng="pool")
            self.dma(w2[:, 0:64], V(cmp_w2.h[i], cmp_w2.name), eng="pool")
            self.dma(w2[:, 64:128], V(cmp_w2.h[i], cmp_w2.name), eng="pool")
            self.dma_s(posT[:], V(cmp_pos.h[i].rearrange("l d -> d l"), cmp_pos.name))
            self.copy(posb[:], posT[:])
            psc = self.next_ps()
            for l in range(32):
                self.mm(psc[0:64, 0:1], w1[:, l, :], posb[:, l:l + 1], start=(l == 0), stop=(l == 31))
            self.copy(cst[:], psc[0:64, 0:1])
            psh = self.next_ps()
            for l in range(32):
                self.mm(psh[0:64, 0:NC], w1[:, l, :], srcT[0:64, l:l + 16 * (NC - 1) + 1:16], start=(l == 0), stop=(l == 31))
            self.memset(u[:], 0.0)
            self.act(u[:, 0:NC], psh[0:64, 0:NC], AF.Identity, bias=cst[:, 0:1])
            self.tt(u2[:], u[:], u[:], ALU.mult)
            self.tt(u2[:], u2[:], u[:], ALU.mult)
            self.stt(u2[:], u2[:], 0.044715, u[:], ALU.mult, ALU.add)
            self.act(u2[:], u2[:], AF.Tanh, scale=0.7978845608028654)
            self.ts(u2[:], u2[:], 1.0, 0.5, ALU.add, ALU.mult)
            self.tt(gl[:], u2[:], u[:], ALU.mult)
            if i == 0:
                pso = self.next_ps()
                self.mm(pso[:, 0:NCP], w2[:, :], gl[:, :])
                self.copy(kcmp[:], pso[:, 0:NCP])
            else:
                for ct in range(NCT):
                    pso = self.next_ps()
                    self.mm(pso[:, 0:64], gl[:, ct * 128:(ct + 1) * 128], w2[:, 0:64])
                    self.copy(vcmp[:, ct, :], pso[:, 0:64])
        self.sb_cur = save
        self.P.barrier()
        c["q"] = self.rot("nqi", 2, [128, 2, 128], BF16)
        for nm, shp, dt_ in (("vis", [128, NCP], F32), ("pns", [128, NCP], F32), ("pn", [128, NCP], F32), ("Sc", [128, NCP], F32),
                             ("Pc", [128, NCP], F32), ("pnb", [128, NCP], BF16), ("PTc", [128, NCP], BF16), ("pnT", [128, NCP], F32),
                             ("cst2", [128, 8], F32), ("am", [128, NB], F32), ("imp", [128, NB], F32), ("imp2", [128, NB], F32), ("m8", [128, 16], F32),
                             ("selm", [128, NB], F32), ("oc", [128, 256], F32), ("os", [128, 256], F32), ("ow", [128, 256], F32),
                             ("gs", [128, 12], F32), ("o", [128, 256], F32), ("y", [128, 256], BF16), ("yTs", [128, 256], BF16)):
            c[nm] = self.rot("n" + nm, 2, shp, dt_)
        return c

    def nsa_tile(self, c, i):
        NB = self.NB
        NCP = self.NCP
        NCT = NCP // 128
        FMS = c["FMS"]
        Ssb = c["Ssb"]
        Sw = c["Sw"]
        kcmp = c["kcmp"]
        vcmp = c["vcmp"]

        def h4(t_):
            return V(t_.h[:, 0:256].rearrange("p (h e) -> p h e", h=4), t_.name)

        nk = 128 * (i + 1)
        nkc = (nk + 511) // 512
        nq = self.nx(c["q"])
        self.dma(nq[:], self.fm_rows(FMS, 8, 2, i * 128, (i + 1) * 128))
        vis = self.nx(c["vis"])
        self.memset(vis[:], 0.0)
        self.aselect(vis[:], vis[:], [[-16, NCP]], ALU.is_ge, -1000.0, 128 * i - 31, 1)
        pns = self.nx(c["pns"])
        oc = self.nx(c["oc"])
        osl = self.nx(c["os"])
        ow = self.nx(c["ow"])
        for h in range(4):
            base = 64 * (h % 2)
            cq = h // 2
            ps = self.next_ps()
            self.mm(ps[:, 0:NCP], nq[base:base + 64, cq, :], kcmp[base:base + 64, :])
            Sc = self.nx(c["Sc"])
            self.stt(Sc[:], ps[:, 0:NCP], 0.125, vis[:], ALU.mult, ALU.add)
            st = self.nx(c["cst2"])
            self.red(st[:, 0:1], Sc[:], ALU.max)
            self.ts(st[:, 0:1], st[:, 0:1], -500.0, -1.0, ALU.max, ALU.mult)
            Pc = self.nx(c["Pc"])
            self.act(Pc[:], Sc[:], AF.Exp, bias=st[:, 0:1], accum=st[:, 1:2])
            self.ts(st[:, 2:3], st[:, 1:2], 1e-30, None, ALU.max)
            self.recip(st[:, 3:4], st[:, 2:3])
            pn = pns if h == 0 else self.nx(c["pn"])
            self.ts(pn[:], Pc[:], st[:, 3:4], None, ALU.mult)
            pnb = self.nx(c["pnb"])
            self.copy(pnb[:], pn[:], eng="act")
            if h > 0:
                self.tt(pns[:], pns[:], pn[:], ALU.add, eng="pool")
            pb = self.next_psb()
            for ct in range(NCT):
                self.tr(pb[:, ct * 128:(ct + 1) * 128], pnb[:, ct * 128:(ct + 1) * 128], self.ident[:])
            PTc = self.nx(c["PTc"])
            self.copy(PTc[:], pb[:, 0:NCP], eng="act")
            po = self.next_ps()
            for ct in range(NCT):
                self.mm(po[:, 0:64], PTc[:, ct * 128:(ct + 1) * 128], vcmp[:, ct, :], start=(ct == 0), stop=(ct == NCT - 1))
            self.copy(oc[:, 64 * h:64 * h + 64], po[:, 0:64], eng="act")
        selm = self.nx(c["selm"])
        if NB > 16:
            pf = self.next_ps()
            for ct in range(NCT):
                self.tr(pf[:, ct * 128:(ct + 1) * 128], pns[:, ct * 128:(ct + 1) * 128], self.ident_f[:])
            pnT = self.nx(c["pnT"])
            self.copy(pnT[:], pf[:, 0:NCP], eng="act")
            pi = self.next_ps()
            for ct in range(NCT):
                self.mm(pi[:, 0:NB], pnT[:, ct * 128:(ct + 1) * 128], self.ovl[:, ct, :], start=(ct == 0), stop=(ct == NCT - 1))
            am = self.nx(c["am"])
            self.memset(am[:], 0.0)
            for half in range(2):
                cur = 2 * i + half
                r0 = 64 * half
                v_ = am[r0:r0 + 64, :]
                self.aselect(v_, v_, [[-1, NB]], ALU.is_ge, -1e30, cur, 0)
                self.memset(am[r0:r0 + 64, 0:1], 1e30)
                self.memset(am[r0:r0 + 64, cur:cur + 1], 1e30)
                if cur >= 1:
                    self.memset(am[r0:r0 + 64, cur - 1:cur], 1e30)
            imp = self.nx(c["imp"])
            self.tt(imp[:], pi[:, 0:NB], am[:], ALU.add)
            m8 = self.nx(c["m8"])
            self.vmax(m8[:, 0:8], imp[:])
            imp2 = self.nx(c["imp2"])
            self.match_replace(imp2[:], m8[:, 0:8], imp[:], -3.0e38)
            self.vmax(m8[:, 8:16], imp2[:])
            self.ts(selm[:], imp[:], m8[:, 15:16], 1000.0, ALU.is_ge, ALU.mult)
        else:
            self.memset(selm[:], 1000.0)
        for h in range(4):
            base = 64 * (h % 2)
            cq = h // 2
            for kc in range(nkc):
                c0 = kc * 512
                cols = min(512, nk - c0)
                nb_ = cols // 64
                ps = self.next_ps()
                self.mm(ps[:, 0:cols], nq[base:base + 64, cq, :], c["ksT"][base:base + 64, c0:c0 + cols])
                self.stt(V(Ssb.h[:, c0:c0 + cols].rearrange("p (b e) -> p b e", e=64), Ssb.name),
                         V(ps.h[:, 0:cols].rearrange("p (b e) -> p b e", e=64), ps.name), 0.125,
                         V(selm.h[:, c0 // 64:c0 // 64 + nb_].unsqueeze(2).to_broadcast([128, nb_, 64]), selm.name),
                         ALU.mult, ALU.add)
            self.tt(Ssb[:, nk - 128:nk], Ssb[:, nk - 128:nk], self.cneg2k[:], ALU.add)
            self.softmax_pv(Ssb[:, 0:nk], nk, c["Vs"], 0, osl[:, 64 * h:64 * h + 64], c["kk"])
        k0 = max(0, i * 128 - 512)
        nkw = nk - k0
        boff = 640 - nkw
        for h in range(4):
            base = 64 * (h % 2)
            cq = h // 2
            for c0 in range(0, nkw, 512):
                cols = min(512, nkw - c0)
                ps = self.next_ps()
                self.mm(ps[:, 0:cols], nq[base:base + 64, cq, :], c["kwT"][base:base + 64, k0 + c0:k0 + c0 + cols])
                self.stt(Sw[:, c0:c0 + cols], ps[:, 0:cols], 0.125, self.band[:, boff + c0:boff + c0 + cols], ALU.mult, ALU.add)
            self.softmax_pv(Sw[:, 0:nkw], nkw, c["Vw"], k0 // 128, ow[:, 64 * h:64 * h + 64], c["kk"])
        gs = self.nx(c["gs"])
        self.act(gs[:], c["ngt"][:, i, :], AF.Sigmoid)
        o = self.nx(c["o"])

        def gbc(j):
            return V(gs.h[:, j:12:3].unsqueeze(2).to_broadcast([128, 4, 64]), gs.name)
        self.tt(h4(o), h4(oc), gbc(0), ALU.mult)
        self.tt(h4(osl), h4(osl), gbc(1), ALU.mult)
        self.tt(o[:], o[:], osl[:], ALU.add)
        self.tt(h4(ow), h4(ow), gbc(2), ALU.mult)
        self.tt(o[:], o[:], ow[:], ALU.add)
        y = self.nx(c["y"])
        self.copy(y[:], o[:], eng="act")
        self.store_yT(y, c["YT"], 2, i, c["yTs"])

    def phase_dsa_nsa(self, FMS, TMB, TMF, cmp_w1, cmp_w2, cmp_pos, YT):
        S = self.S
        NT = S // 128
        self.phase_begin()
        cd = self.dsa_setup(FMS, TMB, TMF, YT)
        cn = self.nsa_setup(FMS, TMB, TMF, cmp_w1, cmp_w2, cmp_pos, YT)
        P = self.P
        for i in range(NT):
            self.ps_set = (0, 3)
            self.psb_set = (0, 1)
            P.capture = []
            self.dsa_tile(cd, i)
            A = P.capture
            self.ps_set = (3, 2)
            self.psb_set = (1, 1)
            P.capture = []
            self.nsa_tile(cn, i)
            B = P.capture
            P.capture = None
            self.ps_set = (0, 5)
            self.psb_set = (0, 2)
            ia = ib = 0
            na, nb = len(A), len(B)
            while ia < na or ib < nb:
                if ib >= nb or (ia < na and ia * nb <= ib * na):
                    P.add(*A[ia][0], **A[ia][1])
                    ia += 1
                else:
                    P.add(*B[ib][0], **B[ib][1])
                    ib += 1

    def phase_merge(self, x_in, xT_in, w_in_l, w_branch, w_out, ln_g, ln_b, YT, x_out, xT_out):
        S = self.S
        self.phase_begin()
        wg = self.load_w("wg", lambda k: V(w_in_l.h[k * 128:(k + 1) * 128, 3128:7224], w_in_l.name), NKC, 4096)
        wb = self.sb("wb", [128, 4, 2, 1024], BF16)
        for n in range(4):
            for kk_ in range(2):
                self.dma(wb[:, n, kk_, :], V(w_branch.h[n, kk_ * 128:(kk_ + 1) * 128, :], w_branch.name), eng="pool")
        wo = self.load_w("wo", lambda k: V(w_out.h[k * 128:(k + 1) * 128, :], w_out.name), NKC, D)
        g_bc, b_bc, scr = self.ln_setup(ln_g, ln_b)
        xt = self.sb("xT", [128, NKC, 512], BF16)
        yt = self.sb("yT", [128, 8, 512], BF16)
        mT = self.sb("mT", [128, 8, 512], BF16)
        acc_k = self.rot("acc", 2, [128, 512], F32)
        sg_k = self.rot("sg", 2, [128, 512], F32)
        tmp_k = self.rot("tmp", 2, [128, 512], F32)
        xr = [self.sb("xr%d" % i, [128, D], F32) for i in range(2)]
        rqs = [self.sb("r%d" % i, [128, D], F32) for i in range(2)]
        ntile = S // 512

        def load_xt(t_):
            ss_ = slice(t_ * 512, (t_ + 1) * 512)
            self.dma(xt[:], V(xT_in.h.rearrange("(k p) s -> p k s", p=128)[:, :, ss_], xT_in.name))
            self.dma(yt[:], V(YT.h[:, ss_].rearrange("(c p) s -> p c s", p=128), YT.name))

        def load_xq(idx):
            self.dma(xr[idx % 2][:], x_in[idx * 128:(idx + 1) * 128, :])
        load_xt(0)
        load_xq(0)
        for t in range(ntile):
            ss = slice(t * 512, (t + 1) * 512)
            for dc in range(8):
                acc = self.nx(acc_k)
                for n in range(4):
                    pg = self.next_ps()
                    for k in range(NKC):
                        self.mm(pg[:], wg[:, k, n * 1024 + dc * 128:n * 1024 + (dc + 1) * 128], xt[:, k, :],
                                start=(k == 0), stop=(k == NKC - 1))
                    pp = self.next_ps()
                    for k2 in range(2):
                        self.mm(pp[:], wb[:, n, k2, dc * 128:(dc + 1) * 128], yt[:, 2 * n + k2, :], start=(k2 == 0), stop=(k2 == 1))
                    sg = self.nx(sg_k)
                    self.act(sg[:], pg[:], AF.Sigmoid)
                    if n == 0:
                        self.tt(acc[:], sg[:], pp[:], ALU.mult)
                    else:
                        tmp = self.nx(tmp_k)
                        self.tt(tmp[:], sg[:], pp[:], ALU.mult)
                        self.tt(acc[:], acc[:], tmp[:], ALU.add, eng="pool")
                self.copy(mT[:, dc, :], acc[:], eng="act")
            if t + 1 < ntile:
                load_xt(t + 1)
            for q in range(4):
                t0 = t * 512 + q * 128
                xq = xr[q % 2]
                rq = rqs[q % 2]
                if t * 4 + q + 1 < ntile * 4:
                    load_xq(t * 4 + q + 1)
                for half in range(2):
                    hs = slice(half * 512, (half + 1) * 512)
                    pd = self.next_ps()
                    for dc in range(8):
                        self.mm(pd[:], mT[:, dc, q * 128:(q + 1) * 128], wo[:, dc, hs], start=(dc == 0), stop=(dc == 7))
                    self.act(xq[:, hs], xq[:, hs], AF.Copy, scale=ALPHA)
                    self.stt(rq[:, hs], pd[:], 1.0, xq[:, hs], ALU.mult, ALU.add)
                self.finish_tile(rq, g_bc, b_bc, x_out, xT_out, t0, scr)

    def phase_xattn(self, x_in, xT_in, mem, wq_d, wkv_d, wo_d, ln_g, ln_b, x_out, xT_out):
        S = self.S
        self.phase_begin()
        wq = self.load_w("wq", lambda k: V(wq_d.h[k * 128:(k + 1) * 128, :], wq_d.name), NKC, D)
        wkv = self.load_w("wkv", lambda k: V(wkv_d.h[k * 128:(k + 1) * 128, :], wkv_d.name), NKC, 2 * D)
        wo = self.load_w("wo", lambda k: V(wo_d.h[k * 128:(k + 1) * 128, :], wo_d.name), NKC, D)
        g_bc, b_bc, scr = self.ln_setup(ln_g, ln_b)
        memT = self.sb("memT", [128, 8, 256], BF16)
        mr = self.sb("mr", [128, D], F32)
        mb = self.sb("mb", [128, D], BF16)
        for mt in range(2):
            self.dma(mr[:], mem[mt * 128:(mt + 1) * 128, :])
            self.copy(mb[:], mr[:], eng="act")
            pb = self.next_psb()
            for k in range(8):
                self.tr(pb[:, k * 128:(k + 1) * 128], mb[:, k * 128:(k + 1) * 128], self.ident[:])
            self.copy(memT[:, :, mt * 128:(mt + 1) * 128], V(pb.h[:, :].rearrange("p (k t) -> p k t", k=8), pb.name))
        KT = self.sb("KT", [128, 8, 256], BF16)
        for c in range(8):
            ps = self.next_ps()
            for k in range(NKC):
                self.mm(ps[:, 0:256], wkv[:, k, c * 128:(c + 1) * 128], memT[:, k, :], start=(k == 0), stop=(k == NKC - 1))
            self.copy(KT[:, c, :], ps[:, 0:256], eng=("act" if c % 2 else "dve"))
        Vm = self.sb("Vm", [128, 2, D], BF16)
        for mt in range(2):
            for half in range(2):
                ps = self.next_ps()
                for k in range(NKC):
                    self.mm(ps[:], memT[:, k, mt * 128:(mt + 1) * 128], wkv[:, k, D + half * 512:D + (half + 1) * 512],
                            start=(k == 0), stop=(k == NKC - 1))
                self.copy(Vm[:, mt, half * 512:(half + 1) * 512], ps[:], eng=("act" if half else "dve"))
        xt = self.sb("xT", [128, NKC, 512], BF16)
        qT = self.sb("qT", [128, 8, 512], BF16)
        Pf_k = self.rot("Pf", 2, [128, 4, 256], F32)
        Pb_k = self.rot("Pb", 2, [128, 4, 256], BF16)
        PT_k = self.rot("PTx", 2, [128, 8, 128], BF16)
        oT_k = self.rot("oT", 2, [128, 8, 128], BF16)
        st_k = self.rot("xst", 2, [128, 16], F32)
        xr = [self.sb("xr%d" % i, [128, D], F32) for i in range(2)]
        rqs = [self.sb("r%d" % i, [128, D], F32) for i in range(2)]
        SC = 1.0 / 16
        ntile = S // 512

        def load_xt(t_):
            ss_ = slice(t_ * 512, (t_ + 1) * 512)
            self.dma(xt[:], V(xT_in.h.rearrange("(k p) s -> p k s", p=128)[:, :, ss_], xT_in.name))

        def load_xq(idx):
            self.dma(xr[idx % 2][:], x_in[idx * 128:(idx + 1) * 128, :])
        load_xt(0)
        load_xq(0)
        for t in range(ntile):
            ss = slice(t * 512, (t + 1) * 512)
            for c in range(8):
                ps = self.next_ps()
                for k in range(NKC):
                    self.mm(ps[:], wq[:, k, c * 128:(c + 1) * 128], xt[:, k, :], start=(k == 0), stop=(k == NKC - 1))
                self.copy(qT[:, c, :], ps[:], eng=("act" if c % 2 else "dve"))
            if t + 1 < ntile:
                load_xt(t + 1)
            for q in range(4):
                t0 = t * 512 + q * 128
                tq = slice(q * 128, (q + 1) * 128)
                if t * 4 + q + 1 < ntile * 4:
                    load_xq(t * 4 + q + 1)
                pss = [self.next_ps(), self.next_ps()]
                st = self.nx(st_k)
                Pf = self.nx(Pf_k)
                for h in range(4):
                    pv = pss[h // 2][:, (h % 2) * 256:(h % 2) * 256 + 256]
                    for cc in range(2):
                        self.mm(pv, qT[:, 2 * h + cc, tq], KT[:, 2 * h + cc, :], start=(cc == 0), stop=(cc == 1))
                    self.red(st[:, h:h + 1], pv, ALU.max)
                    self.ts(st[:, 4 + h:5 + h], st[:, h:h + 1], -SC, None, ALU.mult)
                    self.act(Pf[:, h, :], pv, AF.Exp, bias=st[:, 4 + h:5 + h], scale=SC, accum=st[:, 8 + h:9 + h])
                self.recip(st[:, 12:16], st[:, 8:12])
                Pb = self.nx(Pb_k)
                self.tt(Pb[:], Pf[:], V(st.h[:, 12:16].unsqueeze(2).to_broadcast([128, 4, 256]), st.name), ALU.mult)
                pb = self.next_psb()
                for h in range(4):
                    for mc in range(2):
                        j = 2 * h + mc
                        self.tr(pb[:, j * 128:(j + 1) * 128], Pb[:, h, mc * 128:(mc + 1) * 128], self.ident[:])
                PT = self.nx(PT_k)
                self.copy(PT[:], V(pb.h[:, :].rearrange("p (j t) -> p j t", j=8), pb.name))
                oT = self.nx(oT_k)
                pso = [self.next_ps(), self.next_ps()]
                for h in range(4):
                    for dc in range(2):
                        j = 2 * h + dc
                        pv = pso[j // 4][:, (j % 4) * 128:(j % 4) * 128 + 128]
                        for mc in range(2):
                            self.mm(pv, Vm[:, mc, h * 256 + dc * 128:h * 256 + (dc + 1) * 128], PT[:, 2 * h + mc, :],
                                    start=(mc == 0), stop=(mc == 1))
                for j4 in range(2):
                    self.copy(oT[:, 4 * j4:4 * j4 + 4, :], V(pso[j4].h[:, :].rearrange("p (j t) -> p j t", j=4), pso[j4].name),
                              eng=("act" if j4 else "dve"))
                xq = xr[q % 2]
                rq = rqs[q % 2]
                for half in range(2):
                    hs = slice(half * 512, (half + 1) * 512)
                    pd = self.next_ps()
                    for c in range(8):
                        self.mm(pd[:], oT[:, c, :], wo[:, c, hs], start=(c == 0), stop=(c == 7))
                    self.act(xq[:, hs], xq[:, hs], AF.Copy, scale=ALPHA)
                    self.stt(rq[:, hs], pd[:], 1.0, xq[:, hs], ALU.mult, ALU.add)
                self.finish_tile(rq, g_bc, b_bc, x_out, xT_out, t0, scr)


OFF = dict(r_q=0, r_k=128, r_v=256, r_g=512, d_q=768, d_k=1024, d_v=1088, i_q=1152, i_k=1408, i_w=1440,
           n_q=1448, n_kc=1704, n_vc=1768, n_ks=1832, n_vs=1896, n_kw=1960, n_vw=2024, n_g=2088,
           s_z=2100, s_xbc=2356, s_dt=3124, br_g=3128)


def _partner(i, headdim, rot):
    half = rot // 2
    j = i % headdim
    b = i - j
    if j < half:
        return b + j + half
    if j < rot:
        return b + j - half
    return i


def build_colidx():
    cols = []

    def roped(name, width, headdim, rot, lo=0, rep=1):
        loc = []
        for r in range(rep):
            loc += list(range(lo, lo + width))
        assert len(loc) == 128
        a = [OFF[name] + i for i in loc]
        b = [OFF[name] + _partner(i, headdim, rot) for i in loc]
        cols.extend(a)
        cols.extend(b)

    roped("r_q", 128, 32, 32)
    roped("r_k", 128, 32, 32)
    roped("d_q", 128, 64, 16, 0)
    roped("d_q", 128, 64, 16, 128)
    roped("d_k", 64, 64, 16, 0, 2)
    roped("i_q", 128, 32, 8, 0)
    roped("i_q", 128, 32, 8, 128)
    roped("i_k", 32, 32, 8, 0, 4)
    roped("n_q", 128, 64, 16, 0)
    roped("n_q", 128, 64, 16, 128)
    roped("n_kc", 64, 64, 16, 0, 2)
    roped("n_ks", 64, 64, 16, 0, 2)
    roped("n_kw", 64, 64, 16, 0, 2)
    cols.extend([OFF["n_vc"] + i for i in range(64)] * 2)
    cols.extend([OFF["s_xbc"] + i for i in range(768)])
    for name, w in (("r_v", 256), ("r_g", 256), ("d_v", 64), ("n_vs", 64), ("n_vw", 64), ("s_z", 256),
                    ("i_w", 8), ("n_g", 12), ("s_dt", 4)):
        cols.extend([OFF[name] + i for i in range(w)])
    return np.asarray(cols, dtype=np.int64)


ROPED_TABLES = [0, 1, 2, 2, 2, 3, 3, 3, 2, 2, 2, 2, 2]
TM0 = (2 * len(ROPED_TABLES) + 7) * 128
NCOL2 = TM0 + 984
NBIS = 18
RET_LNG = [math.log1p(-2.0 ** (-5 - h)) for h in range(4)]


def host_consts(S):
    meta = np.zeros((128, 32), np.float32)

    def fill(t, headdim, rot, theta, scale):
        half = rot // 2
        inv = np.power(np.float32(theta), (-2.0 * np.arange(half, dtype=np.float32) / np.float32(rot)).astype(np.float32)).astype(np.float32)
        for p in range(128):
            i = p % headdim
            if i < rot:
                meta[p, t] = inv[i % half]
                meta[p, 4 + t] = scale
                meta[p, 8 + t] = -scale if i < half else scale
            else:
                meta[p, t] = 0.0
                meta[p, 4 + t] = 1.0
                meta[p, 8 + t] = 0.0

    fill(0, 32, 32, 10000.0, 1.0)
    fill(1, 32, 32, 10000.0, 32.0 ** -0.5)
    fill(2, 64, 16, 500000.0, 1.0)
    fill(3, 32, 8, 500000.0, 1.0)
    for p in range(128):
        meta[p, 12] = RET_LNG[p // 32]
        meta[p, 13 + p // 32] = 1.0
    bdm = np.zeros((128, 256), np.float32)
    for p in range(128):
        bdm[p, 64 * (p // 32):64 * (p // 32) + 64] = 1.0
    NC = (S - 32) // 16 + 1
    NCP = (NC + 127) // 128 * 128
    NB = S // 64
    ovl = np.zeros((NCP, NB), np.float32)
    for c in range(NC):
        for j in range(NB):
            ovl[c, j] = max(min(16 * c + 32, 64 * j + 64) - max(16 * c, 64 * j), 0) / 32.0
    return meta, bdm, ovl, NB, NCP


STAGES = ["ffn1", "inproj", "ret", "ssd", "dsa", "nsa", "merge", "xattn", "ffn2"]


def build(S, depth=DEPTH, stop_after=None):
    kb = KB(S, depth, stop_after)
    meta_np, bdm_np, ovl_np, NB, NCP = host_consts(S)
    kb.n_keep = min(256, S // 4)
    EI = "ExternalInput"
    x = kb.dram("x", [S, D], F32, kind=EI)
    mem = kb.dram("mem", [N_MEM, D], F32, kind=EI)
    ln_g = kb.dram("ln_g", [DEPTH, 4, D], F32, kind=EI)
    ln_b = kb.dram("ln_b", [DEPTH, 4, D], F32, kind=EI)
    f1gu = kb.dram("ffn1_w_gu", [DEPTH, D, 2 * DFF], F32, kind=EI)
    f1dn = kb.dram("ffn1_w_down", [DEPTH, DFF, D], F32, kind=EI)
    w_in = kb.dram("w_in", [DEPTH, D, 7224], F32, kind=EI)
    w2 = kb.dram("w2", [DEPTH, D, NCOL2], F32, kind=EI)
    cmp_w1 = kb.dram("cmp_w1", [DEPTH, 2, 2048, 64], F32, kind=EI)
    cmp_w2 = kb.dram("cmp_w2", [DEPTH, 2, 64, 64], F32, kind=EI)
    cmp_pos = kb.dram("cmp_pos", [DEPTH, 2, 32, 64], F32, kind=EI)
    conv_w = kb.dram("conv_w", [DEPTH, 4, 768], F32, kind=EI)
    conv_b = kb.dram("conv_b", [DEPTH, 768], F32, kind=EI)
    dt_bias = kb.dram("dt_bias", [DEPTH, 4], F32, kind=EI)
    a_log = kb.dram("a_log", [DEPTH, 4], F32, kind=EI)
    d_skip = kb.dram("d_skip", [DEPTH, 4], F32, kind=EI)
    norm_g = kb.dram("ssm_norm_g", [DEPTH, 256], F32, kind=EI)
    w_branch = kb.dram("w_branch", [DEPTH, 4, 256, D], F32, kind=EI)
    w_out = kb.dram("w_out", [DEPTH, D, D], F32, kind=EI)
    xwq = kb.dram("xattn_wq", [DEPTH, D, D], F32, kind=EI)
    xwkv = kb.dram("xattn_wkv", [DEPTH, D, 2 * D], F32, kind=EI)
    xwo = kb.dram("xattn_wo", [DEPTH, D, D], F32, kind=EI)
    f2gu = kb.dram("ffn2_w_gu", [DEPTH, D, 2 * DFF], F32, kind=EI)
    f2dn = kb.dram("ffn2_w_down", [DEPTH, DFF, D], F32, kind=EI)
    meta = kb.dram("meta", [128, 32], F32, kind=EI)
    bdm = kb.dram("bdm", [128, 256], F32, kind=EI)
    ovl = kb.dram("ovl", [NCP, NB], F32, kind=EI)
    out = kb.dram("out", [S, D], F32)
    xTa = kb.dram("xTa", [D, S], BF16)
    xTb = kb.dram("xTb", [D, S], BF16)
    xa = kb.dram("xa", [S, D], F32)
    xb2 = kb.dram("xb2", [S, D], F32)
    FMS = kb.dram("FMS", [20 * 128, S], BF16)
    TMB = kb.dram("TMB", [S, 960], BF16)
    TMF = kb.dram("TMF", [S, 24], F32)
    YT = kb.dram("YT", [1024, S], BF16)
    ROPE = kb.dram("ROPE", [4, 2, 128, S], F32)
    kb.setup()
    kb.setup_consts(meta, bdm, ovl, NB, NCP)
    kb.phase_rope(ROPE)
    kb.phase_transpose_in(x, xTa)

    def L(t, *idx):
        return T(t.h[idx], t.name, True)

    done = False
    xin = x
    for l in range(depth):
        last = (l == depth - 1)

        def stop(name):
            return stop_after == (l, name)
        kb.phase_ffn(xin, xTa, L(f1gu, l), L(f1dn, l), L(ln_g, l, 0), L(ln_b, l, 0), xa, xTb)
        if stop("ffn1"):
            break
        kb.phase_inproj(xTb, L(w2, l), ROPE, FMS, TMB, TMF)
        if stop("inproj"):
            break
        kb.phase_ret(FMS, TMB, YT)
        if stop("ret"):
            break
        kb.phase_ssd(FMS, TMB, TMF, L(conv_w, l), L(conv_b, l), L(dt_bias, l), L(a_log, l), L(d_skip, l), L(norm_g, l), YT)
        if stop("ssd"):
            break
        kb.phase_dsa_nsa(FMS, TMB, TMF, L(cmp_w1, l), L(cmp_w2, l), L(cmp_pos, l), YT)
        if stop("nsa") or stop("dsa"):
            break
        kb.phase_merge(xa, xTb, L(w_in, l), L(w_branch, l), L(w_out, l), L(ln_g, l, 1), L(ln_b, l, 1), YT, xb2, xTa)
        if stop("merge"):
            break
        kb.phase_xattn(xb2, xTa, mem, L(xwq, l), L(xwkv, l), L(xwo, l), L(ln_g, l, 2), L(ln_b, l, 2), xa, xTb)
        if stop("xattn"):
            break
        kb.phase_ffn(xa, xTb, L(f2gu, l), L(f2dn, l), L(ln_g, l, 3), L(ln_b, l, 3), out if last else xb2, None if last else xTa)
        if stop("ffn2"):
            break
        xin = xb2
    st = kb.P.emit()
    kb.stats = st
    return kb


def make_in_maps(inputs, S, ncores):
    meta_np, bdm_np, ovl_np, NB, NCP = host_consts(S)
    colidx = build_colidx()
    w_in = np.asarray(inputs["w_in"], dtype=np.float32)
    w2 = np.ascontiguousarray(w_in[:, :, colidx])
    shared = {k: np.ascontiguousarray(np.asarray(v, dtype=np.float32)) for k, v in inputs.items() if k not in ("x", "mem")}
    shared["w2"] = w2
    shared["meta"] = meta_np
    shared["bdm"] = bdm_np
    shared["ovl"] = ovl_np
    maps = []
    for b in range(ncores):
        m = dict(shared)
        m["x"] = np.ascontiguousarray(np.asarray(inputs["x"][b, :S], dtype=np.float32))
        m["mem"] = np.ascontiguousarray(np.asarray(inputs["mem"][b], dtype=np.float32))
        maps.append(m)
    return maps


def kernel(**inputs):
    S = inputs["x"].shape[1]
    B = inputs["x"].shape[0]
    kb = build(S)
    maps = make_in_maps(inputs, S, B)
    res = run_bass_kernel_spmd(kb.nc, maps, core_ids=list(range(B)))
    out = np.stack([np.asarray(r["out"], dtype=np.float32) for r in res.results], axis=0)
    return out
```

```python
import math
import sys
import numpy as np
import concourse.bass as bass
import concourse.mybir as mybir
from concourse.bass_utils import run_bass_kernel_spmd

F32 = mybir.dt.float32
BF16 = mybir.dt.bfloat16
I32 = mybir.dt.int32
AF = mybir.ActivationFunctionType
ALU = mybir.AluOpType
AX = mybir.AxisListType

SEM_LIMIT = 30000
N_DMA_SEMS = 24


class Buf:
    __slots__ = ("name", "last_w", "readers")

    def __init__(self, name):
        self.name = name
        self.last_w = None
        self.readers = []


class Op:
    __slots__ = ("eng", "fn", "deps", "need_inc", "sem", "val", "is_dma", "idx", "tag")


class Prog:
    def __init__(self, nc):
        self.nc = nc
        self.engs = {"pe": nc.tensor, "act": nc.scalar, "dve": nc.vector, "pool": nc.gpsimd, "sp": nc.sync}
        self.ops = []
        self.bufs = {}
        self.last_on = {}
        self.dmas_since = []
        self.phase_deps = []
        self.phase_bufs = set()
        self.capture = None

    def buf(self, name):
        b = self.bufs.get(name)
        if b is None:
            b = self.bufs[name] = Buf(name)
        return b

    def add(self, eng, fn, reads=(), writes=(), dma=False, extra_deps=()):
        if self.capture is not None:
            self.capture.append(((eng, fn), dict(reads=list(reads), writes=list(writes), dma=dma)))
            return None
        op = Op()
        op.eng = eng
        op.fn = fn
        op.is_dma = dma
        op.need_inc = False
        op.sem = None
        op.val = 0
        op.idx = len(self.ops)
        try:
            f_ = sys._getframe(2)
            op.tag = (f_.f_lineno, f_.f_back.f_lineno if f_.f_back else 0)
        except Exception:
            op.tag = (0, 0)
        deps = {}
        for b in reads:
            b = self.buf(b)
            w = b.last_w
            if w is not None:
                deps[w.idx] = (w, "raw")
        for b in writes:
            b = self.buf(b)
            w = b.last_w
            if w is not None and w.idx not in deps:
                deps[w.idx] = (w, "waw")
            for r in b.readers:
                if r.idx not in deps:
                    deps[r.idx] = (r, "war")
        real = []
        for d, kind in deps.values():
            if (not d.is_dma) and d.eng == eng and not dma:
                if eng == "pe" or kind != "raw":
                    continue
            real.append(d)
        for d in extra_deps:
            real.append(d)
        if self.phase_deps:
            for b in list(reads) + list(writes):
                if b not in self.phase_bufs:
                    self.phase_bufs.add(b)
                    real.extend(self.phase_deps)
        op.deps = real
        for b in writes:
            b = self.buf(b)
            b.last_w = op
            b.readers = []
        for b in reads:
            self.buf(b).readers.append(op)
        self.ops.append(op)
        self.last_on[eng] = op
        if dma:
            self.dmas_since.append(op)
        return op

    def barrier(self):
        self.phase_deps = list(self.last_on.values()) + list(self.dmas_since)
        self.dmas_since = []
        self.phase_bufs = set()

    def emit(self, final_wait_eng="sp"):
        nc = self.nc
        for op in self.ops:
            for d in op.deps:
                d.need_inc = True
            if op.is_dma:
                op.need_inc = True
        eng_sem = {}
        eng_cnt = {}
        dma_sems = [nc.alloc_semaphore("dq%d" % i) for i in range(N_DMA_SEMS)]
        dma_cnt = [0] * N_DMA_SEMS
        dma_last = [None] * N_DMA_SEMS
        ndma = 0
        for op in self.ops:
            if not op.need_inc:
                continue
            if op.is_dma:
                j = ndma % N_DMA_SEMS
                ndma += 1
                if dma_last[j] is not None:
                    op.deps.append(dma_last[j])
                dma_cnt[j] += 16
                op.sem = dma_sems[j]
                op.val = dma_cnt[j]
                dma_last[j] = op
            else:
                e = op.eng
                if e not in eng_sem or eng_cnt[e] >= SEM_LIMIT:
                    eng_sem[e] = nc.alloc_semaphore("s_%s_%d" % (e, op.idx))
                    eng_cnt[e] = 0
                eng_cnt[e] += 1
                op.sem = eng_sem[e]
                op.val = eng_cnt[e]
        waited = {}
        nwaits = 0
        for op in self.ops:
            E = self.engs[op.eng]
            need = {}
            for d in op.deps:
                k = id(d.sem)
                if k not in need or need[k][1] < d.val:
                    need[k] = (d.sem, d.val)
            for k, (sem, val) in need.items():
                wk = (op.eng, k)
                if waited.get(wk, 0) >= val:
                    continue
                E.wait_ge(sem, val)
                nwaits += 1
                waited[wk] = val
            try:
                inst = op.fn()
            except Exception:
                print('EMIT FAIL at op', op.idx, op.eng, 'lines', op.tag)
                raise
            if op.need_inc:
                inst.then_inc(op.sem, 16 if op.is_dma else 1)
        E = self.engs[final_wait_eng]
        for j in range(N_DMA_SEMS):
            if dma_cnt[j] > 0:
                E.wait_ge(dma_sems[j], dma_cnt[j])
        self.stats = dict(n_ops=len(self.ops), n_waits=nwaits, n_dma=ndma,
                          n_inc=sum(1 for o in self.ops if o.need_inc))
        return self.stats


class V:
    __slots__ = ("ap", "b")

    def __init__(self, ap, b):
        self.ap = ap
        self.b = b


class T:
    def __init__(self, h, name, dram=False):
        self.h = h
        self.name = name
        self.dram = dram

    def __getitem__(self, idx):
        if self.dram:
            return V(self.h[idx], self.name)
        return V(self.h[idx], self.name)

    def v(self, ap):
        return V(ap, self.name)


DT_SIZE = {F32: 4, BF16: 2, I32: 4}

D = 1024
DFF = 2816
NKC = D // 128
NFC = DFF // 128
LN_EPS = 1e-5
DEPTH = 2
ALPHA = (2 * DEPTH) ** 0.25
N_MEM = 256


class KB:
    def __init__(self, S, depth=DEPTH, stop_after=None, debug=()):
        self.S = S
        self.depth = depth
        self.stop_after = stop_after
        self.debug = debug
        self.nc = bass.Bass("TRN2", target_bir_lowering=False)
        self.P = Prog(self.nc)
        self.uid = 0
        self.sb_base = 0
        self.sb_cur = 0
        self.outs = {}
        self.arena = None
        self.rots = {}
        self.pending_T = None
        self.xb_cnt = 0
        self.ps_set = (0, 5)
        self.psb_set = (0, 2)
        self.fill_regs = {}
        self.n_keep = 256

    def sb(self, name, shape, dtype):
        nbytes = int(np.prod(shape[1:])) * DT_SIZE[dtype]
        nbytes = (nbytes + 63) // 64 * 64
        off = self.sb_cur
        self.sb_cur += nbytes
        assert self.sb_cur <= 207 * 1024, ("SBUF overflow", name, self.sb_cur)
        self.uid += 1
        if self.arena is None:
            self.arena = self.nc.alloc_sbuf_tensor("arena", [128, 207 * 1024], mybir.dt.uint8)
        ap = self.arena[:, off:off + int(np.prod(shape[1:])) * DT_SIZE[dtype]].bitcast(dtype)
        if len(shape) == 3:
            ap = ap.rearrange("p (a b) -> p a b", a=shape[1])
        elif len(shape) == 4:
            ap = ap.rearrange("p (a b c) -> p a b c", a=shape[1], b=shape[2])
        if shape[0] < 128:
            ap = ap[0:shape[0]]
        return T(ap, "%s_%d" % (name, self.uid))

    def phase_begin(self):
        self.flush_pending()
        self.P.barrier()
        self.sb_cur = self.sb_base

    def dram(self, name, shape, dtype, kind="ExternalOutput"):
        h = self.nc.dram_tensor(name, list(shape), dtype, kind=kind)
        return T(h.ap(), name, dram=True)

    def _rw(self, reads, writes):
        return [r.b for r in reads if isinstance(r, V)], [w.b for w in writes]

    def dma(self, out, in_, eng="sp"):
        nc = self.nc
        E = self.P.engs[eng]
        return self.P.add(eng, lambda: E.dma_start(out=out.ap, in_=in_.ap), reads=[in_.b], writes=[out.b], dma=True)

    def mm(self, out, lhsT, rhs, start=True, stop=True):
        nc = self.nc
        return self.P.add("pe", lambda: nc.tensor.matmul(out.ap, lhsT.ap, rhs.ap, start=start, stop=stop),
                          reads=[lhsT.b, rhs.b], writes=[out.b])

    def tr(self, out, in_, ident):
        nc = self.nc
        return self.P.add("pe", lambda: nc.tensor.transpose(out.ap, in_.ap, ident.ap),
                          reads=[in_.b, ident.b], writes=[out.b])

    def act(self, out, in_, func, bias=None, scale=None, accum=None, eng="act"):
        nc = self.nc
        kw = {}
        reads = [in_.b]
        writes = [out.b]
        if bias is not None:
            if isinstance(bias, V):
                kw["bias"] = bias.ap
                reads.append(bias.b)
            else:
                kw["bias"] = bias
        if scale is not None:
            if isinstance(scale, V):
                kw["scale"] = scale.ap
                reads.append(scale.b)
            else:
                kw["scale"] = scale
        if accum is not None:
            kw["accum_out"] = accum.ap
            writes.append(accum.b)
        return self.P.add("act", lambda: nc.scalar.activation(out=out.ap, in_=in_.ap, func=func, **kw),
                          reads=reads, writes=writes)

    def ts(self, out, in0, s1, s2, op0, op1=None, accum=None, eng="dve"):
        E = self.P.engs[eng]
        reads = [in0.b]
        writes = [out.b]
        a1 = s1
        a2 = s2
        if isinstance(s1, V):
            a1 = s1.ap
            reads.append(s1.b)
        if isinstance(s2, V):
            a2 = s2.ap
            reads.append(s2.b)
        kw = {}
        if op1 is not None:
            kw["op1"] = op1
        if accum is not None:
            kw["accum_out"] = accum.ap
            writes.append(accum.b)
        return self.P.add(eng, lambda: E.tensor_scalar(out=out.ap, in0=in0.ap, scalar1=a1, scalar2=a2, op0=op0, **kw),
                          reads=reads, writes=writes)

    def tt(self, out, in0, in1, op, eng="dve"):
        E = self.P.engs[eng]
        return self.P.add(eng, lambda: E.tensor_tensor(out=out.ap, in0=in0.ap, in1=in1.ap, op=op),
                          reads=[in0.b, in1.b], writes=[out.b])

    def stt(self, out, in0, scalar, in1, op0, op1, accum=None):
        nc = self.nc
        reads = [in0.b, in1.b]
        writes = [out.b]
        a = scalar
        if isinstance(scalar, V):
            a = scalar.ap
            reads.append(scalar.b)
        kw = {}
        if accum is not None:
            kw["accum_out"] = accum.ap
            writes.append(accum.b)
        return self.P.add("dve", lambda: nc.vector.scalar_tensor_tensor(out=out.ap, in0=in0.ap, scalar=a, in1=in1.ap,
                                                                     op0=op0, op1=op1, **kw),
                          reads=reads, writes=writes)

    def copy(self, out, in_, eng="dve"):
        E = self.P.engs[eng]
        if eng == "act":
            return self.P.add(eng, lambda: E.copy(out=out.ap, in_=in_.ap), reads=[in_.b], writes=[out.b])
        return self.P.add(eng, lambda: E.tensor_copy(out=out.ap, in_=in_.ap), reads=[in_.b], writes=[out.b])

    def memset(self, out, val, eng="pool"):
        E = self.P.engs[eng]
        return self.P.add(eng, lambda: E.memset(out.ap, val), writes=[out.b])

    def red(self, out, in_, op, axis=AX.X, eng="dve"):
        E = self.P.engs[eng]
        return self.P.add(eng, lambda: E.tensor_reduce(out=out.ap, in_=in_.ap, axis=axis, op=op),
                          reads=[in_.b], writes=[out.b])

    def recip(self, out, in_):
        nc = self.nc
        return self.P.add("dve", lambda: nc.vector.reciprocal(out=out.ap, in_=in_.ap), reads=[in_.b], writes=[out.b])

    def aselect(self, out, in_, pattern, cmp, fill, base, cm):
        nc = self.nc
        regs = self.fill_regs

        def fn():
            if fill not in regs:
                regs[fill] = nc.gpsimd.to_reg(float(fill))
            return nc.gpsimd.affine_select(out=out.ap, in_=in_.ap, pattern=pattern, compare_op=cmp,
                                           fill=regs[fill], base=base, channel_multiplier=cm)
        return self.P.add("pool", fn, reads=[in_.b], writes=[out.b])

    def iota(self, out, pattern, base, cm):
        nc = self.nc
        return self.P.add("pool", lambda: nc.gpsimd.iota(out.ap, pattern=pattern, base=base, channel_multiplier=cm,
                                                         allow_small_or_imprecise_dtypes=True), writes=[out.b])

    def setup(self):
        nc = self.nc
        self.ps = []
        for i in range(5):
            h = nc.alloc_psum_tensor("ps%d" % i, [128, 512], F32)
            self.ps.append(T(h, "ps%d" % i))
        self.psb = []
        for i in range(2):
            h = nc.alloc_psum_tensor("psb%d" % i, [128, 1024], BF16)
            self.psb.append(T(h, "psb%d" % i))
        self.ps_rr = 0
        self.psb_rr = 0
        self.ident_f = self.sb("identf", [128, 128], F32)
        self.ident = self.sb("ident", [128, 128], BF16)
        self.memset(self.ident_f[:], 1.0)
        self.aselect(self.ident_f[:], self.ident_f[:], [[-1, 128]], ALU.is_equal, 0.0, 0, 1)
        self.copy(self.ident[:], self.ident_f[:], eng="pool")
        self.sb_base = self.sb_cur

    def next_ps(self):
        b0, n = self.ps_set
        t = self.ps[b0 + self.ps_rr % n]
        self.ps_rr += 1
        return t

    def next_psb(self):
        b0, n = self.psb_set
        t = self.psb[b0 + self.psb_rr % n]
        self.psb_rr += 1
        return t

    def load_w(self, name, dram_ap_fn, kchunks, ncols, eng="pool", split=4):
        w = self.sb(name, [128, kchunks, ncols], BF16)
        for k in range(kchunks):
            self.dma(w[:, k, :], dram_ap_fn(k), eng="pool")
        return w

    def layer_norm_tile(self, r, g_bc, b_bc, out_f32, scr):
        st = scr["st"]
        junk = scr["junk"]
        self.act(junk[:], r[:], AF.Identity, accum=st[:, 0:1])
        self.act(junk[:], r[:], AF.Square, accum=st[:, 1:2])
        self.ts(st[:, 2:3], st[:, 0:1], 1.0 / D, None, ALU.mult)
        self.tt(st[:, 3:4], st[:, 2:3], st[:, 2:3], ALU.mult)
        self.stt(st[:, 4:5], st[:, 1:2], 1.0 / D, st[:, 3:4], ALU.mult, ALU.subtract)
        self.ts(st[:, 4:5], st[:, 4:5], 0.0, LN_EPS, ALU.max, ALU.add)
        self.act(st[:, 5:6], st[:, 4:5], AF.Sqrt)
        self.recip(st[:, 6:7], st[:, 5:6])
        self.ts(out_f32[:], r[:], st[:, 2:3], st[:, 6:7], ALU.subtract, ALU.mult)
        self.tt(out_f32[:], out_f32[:], g_bc[:], ALU.mult)
        self.tt(out_f32[:], out_f32[:], b_bc[:], ALU.add)

    def store_xT(self, x_f32, xT_dram, t0, scr, defer=False):
        xbl = scr["xb"]
        if isinstance(xbl, list):
            xb = xbl[self.xb_cnt % len(xbl)]
            self.xb_cnt += 1
        else:
            xb = xbl
        xTs = scr["xTs"]
        self.copy(xb[:], x_f32[:], eng="act")

        def part_b():
            pb = self.next_psb()
            for k in range(NKC):
                self.tr(pb[:, k * 128:(k + 1) * 128], xb[:, k * 128:(k + 1) * 128], self.ident[:])
            self.copy(xTs[:], pb[:, :], eng="dve")
            self.dma(V(xT_dram.h.rearrange("(k p) s -> p k s", p=128)[:, :, t0:t0 + 128], xT_dram.name),
                     V(xTs.h[:].rearrange("p (k t) -> p k t", k=NKC), xTs.name))
        if defer:
            self.flush_pending()
            self.pending_T = part_b
        else:
            part_b()

    def flush_pending(self):
        if self.pending_T is not None:
            f = self.pending_T
            self.pending_T = None
            f()

    def dma_s(self, out, in_, eng="sp"):
        E = self.P.engs[eng]
        return self.P.add(eng, lambda: E.dma_start(out=out.ap, in_=in_.ap, allow_slow_non_contiguous=True),
                          reads=[in_.b], writes=[out.b], dma=True)

    def rot(self, name, n, shape, dtype):
        key = "_rot_" + name
        lst = [self.sb(name + str(i), shape, dtype) for i in range(n)]
        self.rots[key] = [lst, 0]
        return key

    def nx(self, key):
        lst, i = self.rots[key]
        self.rots[key][1] = i + 1
        return lst[i % len(lst)]

    def vmax(self, out, in_):
        nc = self.nc
        return self.P.add("dve", lambda: nc.vector.max(out=out.ap, in_=in_.ap), reads=[in_.b], writes=[out.b])

    def match_replace(self, out, rep, vals, imm):
        nc = self.nc
        return self.P.add("dve", lambda: nc.vector.match_replace(out=out.ap, in_to_replace=rep.ap, in_values=vals.ap, imm_value=imm),
                          reads=[rep.b, vals.b], writes=[out.b])

    def redabs(self, out, in_):
        nc = self.nc
        return self.P.add("dve", lambda: nc.vector.tensor_reduce(out=out.ap, in_=in_.ap, axis=AX.X, op=ALU.max,
                                                                 apply_absolute_value=True),
                          reads=[in_.b], writes=[out.b])

    def fm_rows(self, FMS, c0, nchunk, s0, s1):
        return V(FMS.h[c0 * 128:(c0 + nchunk) * 128, s0:s1].rearrange("(c p) s -> p c s", p=128), FMS.name)

    def setup_consts(self, meta, bdm, ovl, NB, NCP):
        S = self.S
        NT = S // 128
        self.NB = NB
        self.NCP = NCP
        self.meta = self.sb("meta", [128, 32], F32)
        self.dma(self.meta[:], meta[:, :])
        self.bdm = self.sb("bdm", [128, 256], F32)
        self.dma(self.bdm[:], bdm[:, :])
        self.ovl = self.sb("ovl", [128, NCP // 128, NB], F32)
        self.dma(self.ovl[:], V(ovl.h.rearrange("(c p) j -> p c j", p=128), ovl.name))
        self.U = self.sb("U", [128, 128], F32)
        self.memset(self.U[:], 1.0)
        self.aselect(self.U[:], self.U[:], [[1, 128]], ALU.is_ge, 0.0, 0, -1)
        self.cneg30 = self.sb("cneg30", [128, 128], F32)
        self.memset(self.cneg30[:], 0.0)
        self.aselect(self.cneg30[:], self.cneg30[:], [[-1, 128]], ALU.is_ge, -1e30, 0, 1)
        self.cneg2k = self.sb("cneg2k", [128, 128], F32)
        self.memset(self.cneg2k[:], 0.0)
        self.aselect(self.cneg2k[:], self.cneg2k[:], [[-1, 128]], ALU.is_ge, -2000.0, 0, 1)
        self.band = self.sb("band", [128, 640], F32)
        self.memset(self.band[:], 0.0)
        self.aselect(self.band[:], self.band[:], [[1, 640]], ALU.is_ge, -2000.0, -1, -1)
        self.aselect(self.band[:], self.band[:], [[-1, 640]], ALU.is_ge, -2000.0, 512, 1)
        self.decayT4 = self.sb("decayT4", [128, 4, 128], F32)
        self.xi = self.sb("xi", [128, 128], F32)
        self.zeta = self.sb("zeta", [128, 128], F32)
        self.cdecay = self.sb("cdecay", [128, 1], F32)
        self.rkc = self.sb("rkc", [128, 20], F32)
        self.sb_base = self.sb_cur
        dji = self.sb("dji", [128, 128], F32)
        self.iota(dji[:], [[1, 128]], 0, -1)
        for h in range(4):
            self.act(self.decayT4[:, h, :], dji[:], AF.Exp, scale=RET_LNG[h])
        self.tt(self.decayT4[:], self.decayT4[:], V(self.U.h[:, :].unsqueeze(1).to_broadcast([128, 4, 128]), self.U.name), ALU.mult)
        ip1 = self.sb("ip1", [128, 128], F32)
        self.iota(ip1[:], [[1, 128]], 1, 0)
        self.act(self.xi[:], ip1[:], AF.Exp, scale=self.meta[:, 12:13])
        jr = self.sb("jr", [128, 128], F32)
        self.iota(jr[:], [[0, 128]], 127, -1)
        for h in range(4):
            self.act(self.zeta[:, 32 * h:32 * h + 32], jr[:, 32 * h:32 * h + 32], AF.Exp, scale=RET_LNG[h])
        c128 = self.sb("c128", [128, 1], F32)
        self.memset(c128[:], 128.0)
        self.act(self.cdecay[:], c128[:], AF.Exp, scale=self.meta[:, 12:13])
        for k in range(20):
            self.memset(self.rkc[:, k:k + 1], 2.0 ** (-k))

    def build_addmask(self):
        NT = self.S // 128
        NB = self.NB
        self.addmask = self.sb("addmask", [128, NT, NB], F32)
        self.memset(self.addmask[:], 0.0)
        for i in range(NT):
            for half in range(2):
                cur = 2 * i + half
                r0 = 64 * half
                v = self.addmask[r0:r0 + 64, i, :]
                self.aselect(v, v, [[-1, NB]], ALU.is_ge, -1e30, cur, 0)
                self.memset(self.addmask[r0:r0 + 64, i, 0:1], 1e30)
                self.memset(self.addmask[r0:r0 + 64, i, cur:cur + 1], 1e30)
                if cur >= 1:
                    self.memset(self.addmask[r0:r0 + 64, i, cur - 1:cur], 1e30)

    def phase_rope(self, ROPE):
        S = self.S
        self.phase_begin()
        pos = self.sb("pos", [128, S], F32)
        self.iota(pos[:], [[1, S]], 0, 0)
        a = self.sb("a", [128, S], F32)
        ki = self.sb("ki", [128, S], I32)
        kf = self.sb("kf", [128, S], F32)
        m = self.sb("m", [128, S], F32)
        r = self.sb("r", [128, S], F32)
        PI = math.pi
        for t in range(4):
            for which in range(2):
                self.ts(a[:], pos[:], self.meta[:, t:t + 1], (PI / 2 if which == 0 else 0.0), ALU.mult, ALU.add)
                self.ts(kf[:], a[:], 1.0 / (2 * PI), None, ALU.mult)
                self.copy(ki[:], kf[:])
                self.copy(kf[:], ki[:])
                self.stt(r[:], kf[:], -2 * PI, a[:], ALU.mult, ALU.add)
                self.ts(m[:], r[:], PI, -2 * PI, ALU.is_gt, ALU.mult)
                self.tt(r[:], r[:], m[:], ALU.add)
                self.ts(m[:], r[:], -PI, 2 * PI, ALU.is_lt, ALU.mult)
                self.tt(r[:], r[:], m[:], ALU.add)
                self.ts(r[:], r[:], PI, -PI, ALU.min, ALU.max)
                self.act(r[:], r[:], AF.Sin)
                col = 4 + 4 * which + t
                self.ts(r[:], r[:], self.meta[:, col:col + 1], None, ALU.mult)
                self.dma(ROPE[t, which], r[:])

    def finish_tile(self, rq, g_bc, b_bc, x_out, xT_out, t0, scr):
        self.layer_norm_tile(rq, g_bc, b_bc, rq, scr)
        self.dma(x_out[t0:t0 + 128, :], rq[:])
        if xT_out is not None:
            self.store_xT(rq, xT_out, t0, scr, defer=True)

    def ln_setup(self, ln_g, ln_b):
        g_bc = self.sb("g_bc", [128, D], F32)
        b_bc = self.sb("b_bc", [128, D], F32)
        self.dma(g_bc[:], V(ln_g.h.partition_broadcast(128), ln_g.name))
        self.dma(b_bc[:], V(ln_b.h.partition_broadcast(128), ln_b.name))
        scr = dict(st=self.sb("st", [128, 8], F32), junk=self.sb("junk", [128, D], BF16),
                   xb=[self.sb("xb0", [128, D], BF16), self.sb("xb1", [128, D], BF16)], xTs=self.sb("xTs", [128, D], BF16))
        return g_bc, b_bc, scr

    def phase_ffn(self, x_in, xT_in, w_gu, w_down, ln_g, ln_b, x_out, xT_out):
        S = self.S
        self.phase_begin()
        wgu = self.load_w("wgu", lambda k: V(w_gu.h[k * 128:(k + 1) * 128, :], w_gu.name), NKC, 2 * DFF)
        wdn = self.load_w("wdn", lambda k: V(w_down.h[k * 128:(k + 1) * 128, :], w_down.name), NFC, D)
        g_bc, b_bc, scr = self.ln_setup(ln_g, ln_b)
        xt = self.sb("xT", [128, NKC, 512], BF16)
        hT = self.sb("hT", [128, NFC, 512], BF16)
        sg = [self.sb("sg%d" % i, [128, 512], BF16) for i in range(2)]
        xr = [self.sb("xr%d" % i, [128, D], F32) for i in range(2)]
        rqs = [self.sb("r%d" % i, [128, D], F32) for i in range(2)]
        ntile = S // 512

        def load_xt(t_):
            self.dma(xt[:], V(xT_in.h.rearrange("(k p) s -> p k s", p=128)[:, :, t_ * 512:(t_ + 1) * 512], xT_in.name))

        def load_xq(idx):
            self.dma(xr[idx % 2][:], x_in[idx * 128:(idx + 1) * 128, :])
        load_xt(0)
        load_xq(0)
        for t in range(ntile):
            for j in range(NFC):
                pg = self.next_ps()
                pu = self.next_ps()
                for k in range(NKC):
                    self.mm(pg[:], wgu[:, k, j * 128:(j + 1) * 128], xt[:, k, :], start=(k == 0), stop=(k == NKC - 1))
                for k in range(NKC):
                    self.mm(pu[:], wgu[:, k, DFF + j * 128:DFF + (j + 1) * 128], xt[:, k, :], start=(k == 0), stop=(k == NKC - 1))
                s = sg[j % 2]
                self.act(s[:], pg[:], AF.Silu)
                self.tt(hT[:, j, :], s[:], pu[:], ALU.mult)
            if t + 1 < ntile:
                load_xt(t + 1)
            for q in range(4):
                t0 = t * 512 + q * 128
                xq = xr[q % 2]
                rq = rqs[q % 2]
                if t * 4 + q + 1 < ntile * 4:
                    load_xq(t * 4 + q + 1)
                for half in range(2):
                    hs = slice(half * 512, (half + 1) * 512)
                    pd = self.next_ps()
                    for j in range(NFC):
                        self.mm(pd[:], hT[:, j, q * 128:(q + 1) * 128], wdn[:, j, hs], start=(j == 0), stop=(j == NFC - 1))
                    self.act(xq[:, hs], xq[:, hs], AF.Copy, scale=ALPHA)
                    self.stt(rq[:, hs], pd[:], 0.5, xq[:, hs], ALU.mult, ALU.add)
                self.finish_tile(rq, g_bc, b_bc, x_out, xT_out, t0, scr)

    def phase_transpose_in(self, x_in, xT_out):
        S = self.S
        self.phase_begin()
        xr = [self.sb("xr%d" % i, [128, D], F32) for i in range(2)]
        scr = dict(xb=self.sb("xb", [128, D], BF16), xTs=self.sb("xTs", [128, D], BF16))
        for i in range(S // 128):
            xq = xr[i % 2]
            self.dma(xq[:], x_in[i * 128:(i + 1) * 128, :])
            self.store_xT(xq, xT_out, i * 128, scr)

    def phase_inproj(self, xT_in, w2, ROPE, FMS, TMB, TMF):
        S = self.S
        self.phase_begin()
        w = self.load_w("win", lambda k: V(w2.h[k * 128:(k + 1) * 128, :], w2.name), NKC, NCOL2)
        xts = [self.sb("xT%d" % i_, [128, NKC, 512], BF16) for i_ in range(2)]
        tabs = [self.sb("tab%d" % i_, [128, 4, 2, 512], F32) for i_ in range(2)]
        t1 = self.rot("t1", 3, [128, 512], F32)
        t2 = self.rot("t2", 3, [128, 512], F32)
        ob = self.rot("ob", 4, [128, 512], BF16)
        tmb = self.rot("tmb", 2, [128, 960], BF16)
        tmf = self.rot("tmf", 2, [128, 24], F32)
        ntile = S // 512

        def load_t(t_):
            ss_ = slice(t_ * 512, (t_ + 1) * 512)
            self.dma(xts[t_ % 2][:], V(xT_in.h.rearrange("(k p) s -> p k s", p=128)[:, :, ss_], xT_in.name))
            self.dma(tabs[t_ % 2][:], V(ROPE.h[:, :, :, ss_].rearrange("t w p s -> p t w s"), ROPE.name))
        load_t(0)
        for t in range(ntile):
            ss = slice(t * 512, (t + 1) * 512)
            xt = xts[t % 2]
            tab = tabs[t % 2]
            if t + 1 < ntile:
                load_t(t + 1)
            for ci, tb in enumerate(ROPED_TABLES):
                pA = self.next_ps()
                pB = self.next_ps()
                for k in range(NKC):
                    self.mm(pA[:], w[:, k, (2 * ci) * 128:(2 * ci + 1) * 128], xt[:, k, :], start=(k == 0), stop=(k == NKC - 1))
                for k in range(NKC):
                    self.mm(pB[:], w[:, k, (2 * ci + 1) * 128:(2 * ci + 2) * 128], xt[:, k, :], start=(k == 0), stop=(k == NKC - 1))
                a1 = self.nx(t1)
                a2 = self.nx(t2)
                o = self.nx(ob)
                self.tt(a1[:], pA[:], tab[:, tb, 0, :], ALU.mult)
                self.tt(a2[:], pB[:], tab[:, tb, 1, :], ALU.mult)
                self.tt(o[:], a1[:], a2[:], ALU.add)
                self.dma(V(FMS.h[ci * 128:(ci + 1) * 128, ss], FMS.name), o[:])
            nr = len(ROPED_TABLES)
            for j in range(7):
                wc = 2 * nr + j
                pA = self.next_ps()
                for k in range(NKC):
                    self.mm(pA[:], w[:, k, wc * 128:(wc + 1) * 128], xt[:, k, :], start=(k == 0), stop=(k == NKC - 1))
                o = self.nx(ob)
                self.copy(o[:], pA[:], eng="act")
                self.dma(V(FMS.h[(nr + j) * 128:(nr + j + 1) * 128, ss], FMS.name), o[:])
            for q in range(4):
                t0 = t * 512 + q * 128
                pA = self.next_ps()
                pB = self.next_ps()
                for k in range(NKC):
                    self.mm(pA[:], xt[:, k, q * 128:(q + 1) * 128], w[:, k, TM0:TM0 + 512], start=(k == 0), stop=(k == NKC - 1))
                for k in range(NKC):
                    self.mm(pB[:, 0:472], xt[:, k, q * 128:(q + 1) * 128], w[:, k, TM0 + 512:TM0 + 984], start=(k == 0), stop=(k == NKC - 1))
                b = self.nx(tmb)
                f = self.nx(tmf)
                self.copy(b[:, 0:512], pA[:], eng="act")
                self.copy(b[:, 512:960], pB[:, 0:448])
                self.copy(f[:], pB[:, 448:472])
                self.dma(TMB[t0:t0 + 128, :], b[:])
                self.dma(TMF[t0:t0 + 128, :], f[:])

    def store_yT(self, y, YT, br, n, yTs_key):
        pb = self.next_psb()
        self.tr(pb[:, 0:128], y[:, 0:128], self.ident[:])
        self.tr(pb[:, 128:256], y[:, 128:256], self.ident[:])
        yTs = self.nx(yTs_key)
        self.copy(yTs[:], pb[:, 0:256])
        self.dma(V(YT.h[br * 256:(br + 1) * 256, n * 128:(n + 1) * 128].rearrange("(c p) t -> p c t", p=128), YT.name),
                 V(yTs.h[:, :].rearrange("p (c t) -> p c t", c=2), yTs.name))

    def phase_ret(self, FMS, TMB, YT):
        S = self.S
        NT = S // 128
        self.phase_begin()
        rq = self.sb("rq", [128, S], BF16)
        rk = self.sb("rk", [128, S], BF16)
        self.dma(rq[:], V(FMS.h[0:128, :], FMS.name))
        self.dma(rk[:], V(FMS.h[128:256, :], FMS.name))
        Sbd = self.sb("Sbd", [128, 256], F32)
        Sbd_bf = self.sb("Sbd_bf", [128, 256], BF16)
        self.memset(Sbd[:], 0.0)
        self.memset(Sbd_bf[:], 0.0)
        vt_k = self.rot("vt", 2, [128, 512], BF16)
        qxi_k = self.rot("qxi", 2, [128, 128], BF16)
        qm_k = self.rot("qm", 2, [128, 4, 128], BF16)
        kz_k = self.rot("kz", 2, [128, 128], BF16)
        PT_k = self.rot("PT", 2, [128, 4, 128], BF16)
        cross_k = self.rot("cross", 2, [128, 256], F32)
        o_k = self.rot("o", 2, [128, 256], F32)
        tmp_k = self.rot("tmp", 2, [128, 256], F32)
        osq_k = self.rot("osq", 2, [128, 256], F32)
        sg_k = self.rot("sg", 2, [128, 256], F32)
        st_k = self.rot("st", 2, [128, 16], F32)
        y_k = self.rot("y", 2, [128, 256], BF16)
        yTs_k = self.rot("yTs", 2, [128, 256], BF16)
        hm = V(self.meta.h[:, 13:17].unsqueeze(2).to_broadcast([128, 4, 128]), self.meta.name)
        for n in range(NT):
            sl = slice(n * 128, (n + 1) * 128)
            vt = self.nx(vt_k)
            self.dma(vt[:], TMB[n * 128:(n + 1) * 128, 0:512])
            qxi = self.nx(qxi_k)
            self.tt(qxi[:], rq[:, sl], self.xi[:], ALU.mult)
            qm = self.nx(qm_k)
            self.tt(qm[:], V(rq.h[:, sl].unsqueeze(1).to_broadcast([128, 4, 128]), rq.name), hm, ALU.mult, eng="pool")
            pb = self.next_psb()
            self.tr(pb[:, 0:128], rk[:, sl], self.ident[:])
            kz = self.nx(kz_k)
            self.tt(kz[:], pb[:, 0:128], self.zeta[:], ALU.mult)
            ps1 = self.next_ps()
            self.mm(ps1[:], rk[:, sl], V(qm.h[:, :, :].rearrange("p h i -> p (h i)"), qm.name))
            PT = self.nx(PT_k)
            self.tt(PT[:], V(ps1.h[:, :].rearrange("p (h i) -> p h i", h=4), ps1.name), self.decayT4[:], ALU.mult)
            ps2 = self.next_ps()
            self.mm(ps2[:, 0:256], qxi[:], Sbd_bf[:])
            cross = self.nx(cross_k)
            self.copy(cross[:], ps2[:, 0:256], eng="act")
            ps3 = self.next_ps()
            for h in range(4):
                self.mm(ps3[:, 64 * h:64 * h + 64], PT[:, h, :], vt[:, 64 * h:64 * h + 64])
            o = self.nx(o_k)
            self.tt(o[:], ps3[:, 0:256], cross[:], ALU.add)
            ps4 = self.next_ps()
            self.mm(ps4[:, 0:256], kz[:], vt[:, 0:256])
            tmp = self.nx(tmp_k)
            self.tt(tmp[:], ps4[:, 0:256], self.bdm[:], ALU.mult)
            self.stt(Sbd[:], Sbd[:], self.cdecay[:, 0:1], tmp[:], ALU.mult, ALU.add)
            self.copy(Sbd_bf[:], Sbd[:], eng="act")
            st = self.nx(st_k)
            o3 = V(o.h[:, :].rearrange("p (h e) -> p h e", h=4), o.name)
            self.red(st[:, 0:4], o3, ALU.add)
            osq = self.nx(osq_k)
            self.tt(osq[:], o[:], o[:], ALU.mult, eng="pool")
            self.red(st[:, 4:8], V(osq.h[:, :].rearrange("p (h e) -> p h e", h=4), osq.name), ALU.add)
            self.ts(st[:, 8:12], st[:, 0:4], 1.0 / 64, None, ALU.mult)
            self.tt(st[:, 12:16], st[:, 8:12], st[:, 8:12], ALU.mult)
            self.stt(st[:, 4:8], st[:, 4:8], 1.0 / 64, st[:, 12:16], ALU.mult, ALU.subtract)
            self.ts(st[:, 4:8], st[:, 4:8], 0.0, LN_EPS, ALU.max, ALU.add)
            self.act(st[:, 4:8], st[:, 4:8], AF.Sqrt)
            self.recip(st[:, 4:8], st[:, 4:8])
            self.tt(o3, o3, V(st.h[:, 8:12].unsqueeze(2).to_broadcast([128, 4, 64]), st.name), ALU.subtract)
            self.tt(o3, o3, V(st.h[:, 4:8].unsqueeze(2).to_broadcast([128, 4, 64]), st.name), ALU.mult)
            sg = self.nx(sg_k)
            self.act(sg[:], vt[:, 256:512], AF.Silu)
            y = self.nx(y_k)
            self.tt(y[:], o[:], sg[:], ALU.mult)
            self.store_yT(y, YT, 0, n, yTs_k)

    def phase_ssd(self, FMS, TMB, TMF, conv_w, conv_b, dt_bias, a_log, d_skip, norm_g, YT):
        S = self.S
        NT = S // 128
        self.phase_begin()
        cw = self.sb("cw", [128, 6, 4], F32)
        for k_ in range(4):
            self.dma_s(cw[:, :, k_], V(conv_w.h[k_].rearrange("(c p) -> p c", p=128), conv_w.name))
        cb = self.sb("cb", [128, 6], F32)
        self.dma_s(cb[:], V(conv_b.h.rearrange("(c p) -> p c", p=128), conv_b.name))
        dtb = self.sb("dtb", [128, 4], F32)
        self.dma(dtb[:], V(dt_bias.h.partition_broadcast(128), dt_bias.name))
        a_bc = self.sb("a_bc", [128, 4], F32)
        self.dma(a_bc[:], V(a_log.h.partition_broadcast(128), a_log.name))
        self.act(a_bc[:], a_bc[:], AF.Exp)
        self.ts(a_bc[:], a_bc[:], -1.0, None, ALU.mult)
        Dbc = self.sb("Dbc", [128, 4], F32)
        self.dma(Dbc[:], V(d_skip.h.partition_broadcast(128), d_skip.name))
        ng_bc = self.sb("ng_bc", [128, 256], F32)
        self.dma(ng_bc[:], V(norm_g.h.partition_broadcast(128), norm_g.name))
        xbcs = self.sb("xbcs", [128, 6, S], BF16)
        raw_k = self.rot("raw", 2, [128, 6, 515], BF16)
        acc_k = self.rot("acc", 2, [128, 512], F32)
        for t in range(S // 512):
            raw = self.nx(raw_k)
            if t == 0:
                self.memset(raw[:, :, 0:3], 0.0)
                self.dma(raw[:, :, 3:515], self.fm_rows(FMS, 14, 6, 0, 512))
            else:
                self.dma(raw[:, :, 0:515], self.fm_rows(FMS, 14, 6, t * 512 - 3, (t + 1) * 512))
            for c in range(6):
                acc = self.nx(acc_k)
                self.ts(acc[:], raw[:, c, 3:515], cw[:, c, 3:4], None, ALU.mult)
                for k in (2, 1, 0):
                    self.stt(acc[:], raw[:, c, k:k + 512], cw[:, c, k:k + 1], acc[:], ALU.mult, ALU.add)
                self.act(xbcs[:, c, t * 512:(t + 1) * 512], acc[:], AF.Silu, bias=cb[:, c:c + 1])
        prev = self.sb("prev", [128, 256], F32)
        prev_bf = self.sb("prev_bf", [128, 256], BF16)
        self.memset(prev[:], 0.0)
        self.memset(prev_bf[:], 0.0)
        xsB_k = self.rot("xsB", 2, [128, 512], BF16)
        tmf_k = self.rot("tmf", 2, [128, 24], F32)
        zt_k = self.rot("zt", 2, [128, 256], BF16)
        st_k = self.rot("st", 2, [128, 32], F32)
        adtb_k = self.rot("adtb", 2, [128, 4, 128], F32)
        seg_k = self.rot("seg", 2, [128, 4, 128], F32)
        MT_k = self.rot("MT", 2, [128, 4, 128], BF16)
        X_k = self.rot("X", 2, [128, 256], BF16)
        Xd_k = self.rot("Xd", 2, [128, 256], BF16)
        yd_k = self.rot("yd", 2, [128, 256], F32)
        y_k = self.rot("y", 2, [128, 256], F32)
        t2_k = self.rot("t2", 2, [128, 256], F32)
        sz_k = self.rot("sz", 2, [128, 256], F32)
        yb_k = self.rot("yb", 2, [128, 256], BF16)
        yTs_k = self.rot("yTs", 2, [128, 256], BF16)
        Ubc = V(self.U.h[:, :].unsqueeze(1).to_broadcast([128, 4, 128]), self.U.name)

        def h4(t_):
            return V(t_.h[:, 0:256].rearrange("p (h e) -> p h e", h=4), t_.name)

        def bc4(v_):
            return V(v_.ap.unsqueeze(2).to_broadcast([128, 4, 64]), v_.b)

        for n in range(NT):
            sl = slice(n * 128, (n + 1) * 128)
            pb = self.next_psb()
            for c in range(4):
                self.tr(pb[:, c * 128:(c + 1) * 128], xbcs[:, c, sl], self.ident[:])
            xsB = self.nx(xsB_k)
            self.copy(xsB[:], pb[:, 0:512])
            tmf = self.nx(tmf_k)
            self.dma(tmf[:], TMF[n * 128:(n + 1) * 128, :])
            zt = self.nx(zt_k)
            self.dma(zt[:], TMB[n * 128:(n + 1) * 128, 704:960])
            st = self.nx(st_k)
            self.tt(st[:, 0:4], tmf[:, 20:24], dtb[:], ALU.add)
            self.act(st[:, 0:4], st[:, 0:4], AF.Exp)
            self.act(st[:, 0:4], st[:, 0:4], AF.Ln, bias=1.0)
            self.tt(st[:, 4:8], st[:, 0:4], a_bc[:], ALU.mult)
            adtb = self.nx(adtb_k)
            self.copy(adtb[:], V(st.h[:, 4:8].unsqueeze(2).to_broadcast([128, 4, 128]), st.name))
            psA = self.next_ps()
            self.mm(psA[:, 0:4], self.U[:], st[:, 4:8])
            self.copy(st[:, 8:12], psA[:, 0:4], eng="act")
            psB = self.next_ps()
            for h in range(4):
                self.mm(psB[:, h * 128:(h + 1) * 128], adtb[:, h, :], self.U[:])
            seg = self.nx(seg_k)
            for h in range(4):
                self.ts(seg[:, h, :], psB[:, h * 128:(h + 1) * 128], st[:, 8 + h:9 + h], 0.0, ALU.subtract, ALU.min)
            self.act(seg[:], seg[:], AF.Exp)
            self.tt(seg[:], seg[:], Ubc, ALU.mult, eng="pool")
            alast = V(psB.h[:, 127:512:128], psB.name)
            self.tt(st[:, 12:16], alast, st[:, 8:12], ALU.subtract)
            self.act(st[:, 12:16], st[:, 12:16], AF.Exp)
            self.act(st[:, 16:20], alast, AF.Exp)
            self.act(st[:, 20:24], st[:, 8:12], AF.Exp)
            psG = self.next_ps()
            for g in range(2):
                self.mm(psG[:, g * 128:(g + 1) * 128], xbcs[:, 2 + g, sl], xbcs[:, 4 + g, sl])
            MT = self.nx(MT_k)
            for g in range(2):
                self.tt(MT[:, 2 * g:2 * g + 2, :], seg[:, 2 * g:2 * g + 2, :],
                        V(psG.h[:, g * 128:(g + 1) * 128].unsqueeze(1).to_broadcast([128, 2, 128]), psG.name), ALU.mult)
            X = self.nx(X_k)
            self.tt(h4(X), h4(xsB), bc4(st[:, 0:4]), ALU.mult)
            psY = self.next_ps()
            for h in range(4):
                self.mm(psY[:, 64 * h:64 * h + 64], MT[:, h, :], X[:, 64 * h:64 * h + 64])
            psO = self.next_ps()
            for g in range(2):
                self.mm(psO[:, 128 * g:128 * g + 128], xbcs[:, 4 + g, sl], prev_bf[:, 128 * g:128 * g + 128])
            yd = self.nx(yd_k)
            self.copy(yd[:], psY[:, 0:256], eng="act")
            y = self.nx(y_k)
            self.tt(h4(y), h4(psO), bc4(st[:, 20:24]), ALU.mult)
            self.tt(y[:], y[:], yd[:], ALU.add)
            t2 = self.nx(t2_k)
            self.tt(h4(t2), h4(xsB), bc4(Dbc[:, 0:4]), ALU.mult, eng="pool")
            self.tt(y[:], y[:], t2[:], ALU.add)
            Xd = self.nx(Xd_k)
            self.tt(h4(Xd), h4(X), bc4(st[:, 12:16]), ALU.mult, eng="pool")
            psS = self.next_ps()
            for g in range(2):
                self.mm(psS[:, 128 * g:128 * g + 128], xsB[:, 256 + 128 * g:256 + 128 * g + 128], Xd[:, 128 * g:128 * g + 128])
            self.tt(h4(prev), h4(prev), bc4(st[:, 16:20]), ALU.mult)
            self.tt(prev[:], prev[:], psS[:, 0:256], ALU.add)
            self.copy(prev_bf[:], prev[:], eng="act")
            sz = self.nx(sz_k)
            self.act(sz[:], zt[:], AF.Silu)
            self.tt(y[:], y[:], sz[:], ALU.mult)
            self.tt(t2[:], y[:], y[:], ALU.mult, eng="pool")
            self.red(st[:, 24:26], V(t2.h[:, :].rearrange("p (g e) -> p g e", g=2), t2.name), ALU.add)
            self.ts(st[:, 24:26], st[:, 24:26], 1.0 / 128, LN_EPS, ALU.mult, ALU.add)
            self.act(st[:, 24:26], st[:, 24:26], AF.Sqrt)
            self.recip(st[:, 24:26], st[:, 24:26])
            y2 = V(y.h[:, :].rearrange("p (g e) -> p g e", g=2), y.name)
            self.tt(y2, y2, V(st.h[:, 24:26].unsqueeze(2).to_broadcast([128, 2, 128]), st.name), ALU.mult)
            yb = self.nx(yb_k)
            self.tt(yb[:], y[:], ng_bc[:], ALU.mult)
            self.store_yT(yb, YT, 3, n, yTs_k)

    def softmax_pv(self, Ssb, nk, Vt, kt0, out, kk, clamp=None):
        st = self.nx(kk["st"])
        self.red(st[:, 0:1], Ssb, ALU.max)
        if clamp is not None:
            self.ts(st[:, 0:1], st[:, 0:1], clamp, None, ALU.max)
        self.ts(st[:, 1:2], st[:, 0:1], -1.0, None, ALU.mult)
        P = self.nx(kk["P"])
        self.act(P[:, 0:nk], Ssb, AF.Exp, bias=st[:, 1:2], accum=st[:, 2:3])
        self.ts(st[:, 3:4], st[:, 2:3], 1e-30, None, ALU.max)
        self.recip(st[:, 4:5], st[:, 3:4])
        po = self.next_ps()
        nkt = nk // 128
        for g0 in range(0, nkt, 8):
            gn = min(8, nkt - g0)
            pb = self.next_psb()
            for j in range(gn):
                self.tr(pb[:, j * 128:(j + 1) * 128], P[:, (g0 + j) * 128:(g0 + j + 1) * 128], self.ident[:])
            PT = self.nx(kk["PT"])
            self.copy(PT[:, 0:gn * 128], pb[:, 0:gn * 128], eng="act")
            for j in range(gn):
                self.mm(po[:, 0:64], PT[:, j * 128:(j + 1) * 128], Vt[:, kt0 + g0 + j, :],
                        start=(g0 + j == 0), stop=(g0 + j == nkt - 1))
        self.ts(out, po[:, 0:64], st[:, 4:5], None, ALU.mult)

    def attn_keys(self, pfx):
        S = self.S
        return dict(st=self.rot(pfx + "sst", 2, [128, 8], F32), P=self.rot(pfx + "P", 1, [128, S], BF16),
                    PT=self.rot(pfx + "PTa", 2, [128, 1024], BF16))

    def dsa_setup(self, FMS, TMB, TMF, YT):
        S = self.S
        NT = S // 128
        c = dict(FMS=FMS, YT=YT)
        c["dk"] = self.sb("dk", [128, S], BF16)
        self.dma(c["dk"][:], V(FMS.h[4 * 128:5 * 128, :], FMS.name))
        ikr = self.sb("ikr", [128, S], BF16)
        self.dma(ikr[:], V(FMS.h[7 * 128:8 * 128, :], FMS.name))
        c["ikm"] = self.sb("ikm", [128, 4, S], BF16)
        for g in range(4):
            self.ts(c["ikm"][:, g, :], ikr[:], self.meta[:, 13 + g:14 + g], None, ALU.mult, eng=("pool" if g % 2 else "dve"))
        c["Vt"] = self.sb("Vt", [128, NT, 64], BF16)
        self.dma(c["Vt"][:], V(TMB.h[:, 512:576].rearrange("(n p) c -> p n c", p=128), TMB.name))
        iw = self.sb("iw", [128, NT, 8], F32)
        self.dma(iw[:], V(TMF.h[:, 0:8].rearrange("(n p) c -> p n c", p=128), TMF.name))
        c["absw"] = self.sb("absw", [128, NT, 8], F32)
        self.act(c["absw"][:], iw[:], AF.Abs, scale=1.0 / 16)
        c["sgn"] = self.sb("sgn", [128, NT, 8], F32)
        self.ts(c["sgn"][:], iw[:], 0.0, 2.0, ALU.is_ge, ALU.mult)
        self.ts(c["sgn"][:], c["sgn"][:], -1.0, None, ALU.add)
        c["I"] = self.sb("I", [128, S], F32)
        c["Ssb"] = self.sb("dSsb", [128, S], F32)
        c["kk"] = self.attn_keys("d")
        c["q"] = self.rot("dqi", 2, [128, 4, 128], BF16)
        c["tmp"] = self.rot("tmpr", 2, [128, 512], F32)
        c["st"] = self.rot("dst", 2, [128, 16], F32)
        c["Rk"] = self.rot("Rk", 2, [128, 20], F32)
        c["nm"] = self.rot("dnm", 2, [128, 2], F32)
        c["c2"] = self.rot("dc2", 2, [128, 2], F32)
        c["o"] = self.rot("do", 2, [128, 256], F32)
        c["y"] = self.rot("dy", 2, [128, 256], BF16)
        c["yTs"] = self.rot("dyTs", 2, [128, 256], BF16)
        return c

    def dsa_tile(self, c, i):
        FMS = c["FMS"]
        I = c["I"]
        Ssb = c["Ssb"]
        nk = 128 * (i + 1)
        nkc = (nk + 511) // 512
        q = self.nx(c["q"])
        self.dma(q[:, 0:2, :], self.fm_rows(FMS, 2, 2, i * 128, (i + 1) * 128))
        self.dma(q[:, 2:4, :], self.fm_rows(FMS, 5, 2, i * 128, (i + 1) * 128))
        for kc in range(nkc):
            c0 = kc * 512
            cols = min(512, nk - c0)
            for h in range(8):
                ps = self.next_ps()
                self.mm(ps[:, 0:cols], q[:, 2 + h // 4, :], c["ikm"][:, h % 4, c0:c0 + cols])
                tmp = self.nx(c["tmp"])
                self.act(tmp[:, 0:cols], ps[:, 0:cols], AF.Relu, scale=c["absw"][:, i, h:h + 1])
                if h == 0:
                    self.ts(I[:, c0:c0 + cols], tmp[:, 0:cols], c["sgn"][:, i, 0:1], None, ALU.mult)
                else:
                    self.stt(I[:, c0:c0 + cols], tmp[:, 0:cols], c["sgn"][:, i, h:h + 1], I[:, c0:c0 + cols], ALU.mult, ALU.add)
        if nk > self.n_keep:
            st = self.nx(c["st"])
            junk = self.nx(c["kk"]["P"])
            self.redabs(st[:, 0:1], I[:, 0:nk])
            self.ts(st[:, 0:1], st[:, 0:1], 1e-20, None, ALU.max)
            self.tt(I[:, nk - 128:nk], I[:, nk - 128:nk], self.cneg30[:], ALU.add)
            Rk = self.nx(c["Rk"])
            self.ts(Rk[:], self.rkc[:], st[:, 0:1], None, ALU.mult)
            self.ts(st[:, 1:2], st[:, 0:1], -1.0, None, ALU.mult)
            n1 = (nk // 2 + 127) // 128 * 128
            n2 = nk - n1
            thr_c = self.n_keep - 0.5 - n2 / 2.0
            for k in range(NBIS):
                self.tt(st[:, 2:3], st[:, 1:2], Rk[:, k:k + 1], ALU.add)
                nm = self.nx(c["nm"])
                c2 = self.nx(c["c2"])
                self.ts(nm[:, 0:1], st[:, 2:3], -1.0, None, ALU.mult)
                self.act(Ssb[:, n1:nk], I[:, n1:nk], AF.Sign, bias=nm[:, 0:1], accum=c2[:, 0:1])
                self.ts(junk[:, 0:n1], I[:, 0:n1], st[:, 2:3], None, ALU.is_ge, ALU.add, accum=st[:, 3:4])
                self.stt(st[:, 4:5], c2[:, 0:1], 0.5, st[:, 3:4], ALU.mult, ALU.add)
                self.ts(st[:, 4:5], st[:, 4:5], thr_c, None, ALU.is_ge)
                self.stt(st[:, 1:2], st[:, 4:5], Rk[:, k:k + 1], st[:, 1:2], ALU.mult, ALU.add)
            self.ts(I[:, 0:nk], I[:, 0:nk], st[:, 1:2], 1000.0, ALU.is_ge, ALU.mult)
        else:
            self.ts(I[:, 0:nk], I[:, 0:nk], 0.0, 1000.0, ALU.mult, ALU.add)
            self.tt(I[:, nk - 128:nk], I[:, nk - 128:nk], self.cneg2k[:], ALU.add)
        o = self.nx(c["o"])
        for h in range(4):
            base = 64 * (h % 2)
            cq = h // 2
            for kc in range(nkc):
                c0 = kc * 512
                cols = min(512, nk - c0)
                ps = self.next_ps()
                self.mm(ps[:, 0:cols], q[base:base + 64, cq, :], c["dk"][base:base + 64, c0:c0 + cols])
                self.stt(Ssb[:, c0:c0 + cols], ps[:, 0:cols], 0.125, I[:, c0:c0 + cols], ALU.mult, ALU.add)
            self.softmax_pv(Ssb[:, 0:nk], nk, c["Vt"], 0, o[:, 64 * h:64 * h + 64], c["kk"])
        y = self.nx(c["y"])
        self.copy(y[:], o[:], eng="act")
        self.store_yT(y, c["YT"], 1, i, c["yTs"])

    def nsa_setup(self, FMS, TMB, TMF, cmp_w1, cmp_w2, cmp_pos, YT):
        S = self.S
        NT = S // 128
        NB = self.NB
        NCP = self.NCP
        NC = (S - 32) // 16 + 1
        NCT = NCP // 128
        c = dict(FMS=FMS, YT=YT)
        c["ksT"] = self.sb("ksT", [128, S], BF16)
        self.dma(c["ksT"][:], V(FMS.h[11 * 128:12 * 128, :], FMS.name))
        c["kwT"] = self.sb("kwT", [128, S], BF16)
        self.dma(c["kwT"][:], V(FMS.h[12 * 128:13 * 128, :], FMS.name))
        c["Vs"] = self.sb("Vs", [128, NT, 64], BF16)
        self.dma(c["Vs"][:], V(TMB.h[:, 576:640].rearrange("(n p) c -> p n c", p=128), TMB.name))
        c["Vw"] = self.sb("Vw", [128, NT, 64], BF16)
        self.dma(c["Vw"][:], V(TMB.h[:, 640:704].rearrange("(n p) c -> p n c", p=128), TMB.name))
        c["ngt"] = self.sb("ngt", [128, NT, 12], F32)
        self.dma(c["ngt"][:], V(TMF.h[:, 8:20].rearrange("(n p) c -> p n c", p=128), TMF.name))
        kcmp = self.sb("kcmp", [128, NCP], BF16)
        vcmp = self.sb("vcmp", [128, NCT, 64], BF16)
        c["kcmp"] = kcmp
        c["vcmp"] = vcmp
        c["Ssb"] = self.sb("nSsb", [128, S], F32)
        c["Sw"] = self.sb("Sw", [128, 640], F32)
        c["kk"] = self.attn_keys("n")
        save = self.sb_cur
        srcT = self.sb("srcT", [128, S], BF16)
        w1 = self.sb("w1", [64, 32, 64], BF16)
        w2 = self.sb("w2", [64, 128], BF16)
        posT = self.sb("posT", [64, 32], F32)
        posb = self.sb("posb", [64, 32], BF16)
        cst = self.sb("cst", [64, 1], F32)
        u = self.sb("u", [64, NCP], F32)
        u2 = self.sb("u2", [64, NCP], F32)
        gl = self.sb("gl", [64, NCP], BF16)
        for i in range(2):
            self.dma(srcT[:], V(FMS.h[(10 + 3 * i) * 128:(11 + 3 * i) * 128, :], FMS.name))
            self.dma(w1[:], V(cmp_w1.h[i].rearrange("(l d) f -> d l f", d=64), cmp_w1.name), eng="pool")
            self.dma(w2[:, 0:64], V(cmp_w2.h[i], cmp_w2.name), eng="pool")
            self.dma(w2[:, 64:128], V(cmp_w2.h[i], cmp_w2.name), eng="pool")
            self.dma_s(posT[:], V(cmp_pos.h[i].rearrange("l d -> d l"), cmp_pos.name))
            self.copy(posb[:], posT[:])
            psc = self.next_ps()
            for l in range(32):
                self.mm(psc[0:64, 0:1], w1[:, l, :], posb[:, l:l + 1], start=(l == 0), stop=(l == 31))
            self.copy(cst[:], psc[0:64, 0:1])
            psh = self.next_ps()
            for l in range(32):
                self.mm(psh[0:64, 0:NC], w1[:, l, :], srcT[0:64, l:l + 16 * (NC - 1) + 1:16], start=(l == 0), stop=(l == 31))
            self.memset(u[:], 0.0)
            self.act(u[:, 0:NC], psh[0:64, 0:NC], AF.Identity, bias=cst[:, 0:1])
            self.tt(u2[:], u[:], u[:], ALU.mult)
            self.tt(u2[:], u2[:], u[:], ALU.mult)
            self.stt(u2[:], u2[:], 0.044715, u[:], ALU.mult, ALU.add)
            self.act(u2[:], u2[:], AF.Tanh, scale=0.7978845608028654)
            self.ts(u2[:], u2[:], 1.0, 0.5, ALU.add, ALU.mult)
            self.tt(gl[:], u2[:], u[:], ALU.mult)
            if i == 0:
                pso = self.next_ps()
                self.mm(pso[:, 0:NCP], w2[:, :], gl[:, :])
                self.copy(kcmp[:], pso[:, 0:NCP])
            else:
                for ct in range(NCT):
                    pso = self.next_ps()
                    self.mm(pso[:, 0:64], gl[:, ct * 128:(ct + 1) * 128], w2[:, 0:64])
                    self.copy(vcmp[:, ct, :], pso[:, 0:64])
        self.sb_cur = save
        self.P.barrier()
        c["q"] = self.rot("nqi", 2, [128, 2, 128], BF16)
        for nm, shp, dt_ in (("vis", [128, NCP], F32), ("pns", [128, NCP], F32), ("pn", [128, NCP], F32), ("Sc", [128, NCP], F32),
                             ("Pc", [128, NCP], F32), ("pnb", [128, NCP], BF16), ("PTc", [128, NCP], BF16), ("pnT", [128, NCP], F32),
                             ("cst2", [128, 8], F32), ("am", [128, NB], F32), ("imp", [128, NB], F32), ("imp2", [128, NB], F32), ("m8", [128, 16], F32),
                             ("selm", [128, NB], F32), ("oc", [128, 256], F32), ("os", [128, 256], F32), ("ow", [128, 256], F32),
                             ("gs", [128, 12], F32), ("o", [128, 256], F32), ("y", [128, 256], BF16), ("yTs", [128, 256], BF16)):
            c[nm] = self.rot("n" + nm, 2, shp, dt_)
        return c

    def nsa_tile(self, c, i):
        NB = self.NB
        NCP = self.NCP
        NCT = NCP // 128
        FMS = c["FMS"]
        Ssb = c["Ssb"]
        Sw = c["Sw"]
        kcmp = c["kcmp"]
        vcmp = c["vcmp"]

        def h4(t_):
            return V(t_.h[:, 0:256].rearrange("p (h e) -> p h e", h=4), t_.name)

        nk = 128 * (i + 1)
        nkc = (nk + 511) // 512
        nq = self.nx(c["q"])
        self.dma(nq[:], self.fm_rows(FMS, 8, 2, i * 128, (i + 1) * 128))
        vis = self.nx(c["vis"])
        self.memset(vis[:], 0.0)
        self.aselect(vis[:], vis[:], [[-16, NCP]], ALU.is_ge, -1000.0, 128 * i - 31, 1)
        pns = self.nx(c["pns"])
        oc = self.nx(c["oc"])
        osl = self.nx(c["os"])
        ow = self.nx(c["ow"])
        for h in range(4):
            base = 64 * (h % 2)
            cq = h // 2
            ps = self.next_ps()
            self.mm(ps[:, 0:NCP], nq[base:base + 64, cq, :], kcmp[base:base + 64, :])
            Sc = self.nx(c["Sc"])
            self.stt(Sc[:], ps[:, 0:NCP], 0.125, vis[:], ALU.mult, ALU.add)
            st = self.nx(c["cst2"])
            self.red(st[:, 0:1], Sc[:], ALU.max)
            self.ts(st[:, 0:1], st[:, 0:1], -500.0, -1.0, ALU.max, ALU.mult)
            Pc = self.nx(c["Pc"])
            self.act(Pc[:], Sc[:], AF.Exp, bias=st[:, 0:1], accum=st[:, 1:2])
            self.ts(st[:, 2:3], st[:, 1:2], 1e-30, None, ALU.max)
            self.recip(st[:, 3:4], st[:, 2:3])
            pn = pns if h == 0 else self.nx(c["pn"])
            self.ts(pn[:], Pc[:], st[:, 3:4], None, ALU.mult)
            pnb = self.nx(c["pnb"])
            self.copy(pnb[:], pn[:], eng="act")
            if h > 0:
                self.tt(pns[:], pns[:], pn[:], ALU.add, eng="pool")
            pb = self.next_psb()
            for ct in range(NCT):
                self.tr(pb[:, ct * 128:(ct + 1) * 128], pnb[:, ct * 128:(ct + 1) * 128], self.ident[:])
            PTc = self.nx(c["PTc"])
            self.copy(PTc[:], pb[:, 0:NCP], eng="act")
            po = self.next_ps()
            for ct in range(NCT):
                self.mm(po[:, 0:64], PTc[:, ct * 128:(ct + 1) * 128], vcmp[:, ct, :], start=(ct == 0), stop=(ct == NCT - 1))
            self.copy(oc[:, 64 * h:64 * h + 64], po[:, 0:64], eng="act")
        selm = self.nx(c["selm"])
        if NB > 16:
            pf = self.next_ps()
            for ct in range(NCT):
                self.tr(pf[:, ct * 128:(ct + 1) * 128], pns[:, ct * 128:(ct + 1) * 128], self.ident_f[:])
            pnT = self.nx(c["pnT"])
            self.copy(pnT[:], pf[:, 0:NCP], eng="act")
            pi = self.next_ps()
            for ct in range(NCT):
                self.mm(pi[:, 0:NB], pnT[:, ct * 128:(ct + 1) * 128], self.ovl[:, ct, :], start=(ct == 0), stop=(ct == NCT - 1))
            am = self.nx(c["am"])
            self.memset(am[:], 0.0)
            for half in range(2):
                cur = 2 * i + half
                r0 = 64 * half
                v_ = am[r0:r0 + 64, :]
                self.aselect(v_, v_, [[-1, NB]], ALU.is_ge, -1e30, cur, 0)
                self.memset(am[r0:r0 + 64, 0:1], 1e30)
                self.memset(am[r0:r0 + 64, cur:cur + 1], 1e30)
                if cur >= 1:
                    self.memset(am[r0:r0 + 64, cur - 1:cur], 1e30)
            imp = self.nx(c["imp"])
            self.tt(imp[:], pi[:, 0:NB], am[:], ALU.add)
            m8 = self.nx(c["m8"])
            self.vmax(m8[:, 0:8], imp[:])
            imp2 = self.nx(c["imp2"])
            self.match_replace(imp2[:], m8[:, 0:8], imp[:], -3.0e38)
            self.vmax(m8[:, 8:16], imp2[:])
            self.ts(selm[:], imp[:], m8[:, 15:16], 1000.0, ALU.is_ge, ALU.mult)
        else:
            self.memset(selm[:], 1000.0)
        for h in range(4):
            base = 64 * (h % 2)
            cq = h // 2
            for kc in range(nkc):
                c0 = kc * 512
                cols = min(512, nk - c0)
                nb_ = cols // 64
                ps = self.next_ps()
                self.mm(ps[:, 0:cols], nq[base:base + 64, cq, :], c["ksT"][base:base + 64, c0:c0 + cols])
                self.stt(V(Ssb.h[:, c0:c0 + cols].rearrange("p (b e) -> p b e", e=64), Ssb.name),
                         V(ps.h[:, 0:cols].rearrange("p (b e) -> p b e", e=64), ps.name), 0.125,
                         V(selm.h[:, c0 // 64:c0 // 64 + nb_].unsqueeze(2).to_broadcast([128, nb_, 64]), selm.name),
                         ALU.mult, ALU.add)
            self.tt(Ssb[:, nk - 128:nk], Ssb[:, nk - 128:nk], self.cneg2k[:], ALU.add)
            self.softmax_pv(Ssb[:, 0:nk], nk, c["Vs"], 0, osl[:, 64 * h:64 * h + 64], c["kk"])
        k0 = max(0, i * 128 - 512)
        nkw = nk - k0
        boff = 640 - nkw
        for h in range(4):
            base = 64 * (h % 2)
            cq = h // 2
            for c0 in range(0, nkw, 512):
                cols = min(512, nkw - c0)
                ps = self.next_ps()
                self.mm(ps[:, 0:cols], nq[base:base + 64, cq, :], c["kwT"][base:base + 64, k0 + c0:k0 + c0 + cols])
                self.stt(Sw[:, c0:c0 + cols], ps[:, 0:cols], 0.125, self.band[:, boff + c0:boff + c0 + cols], ALU.mult, ALU.add)
            self.softmax_pv(Sw[:, 0:nkw], nkw, c["Vw"], k0 // 128, ow[:, 64 * h:64 * h + 64], c["kk"])
        gs = self.nx(c["gs"])
        self.act(gs[:], c["ngt"][:, i, :], AF.Sigmoid)
        o = self.nx(c["o"])

        def gbc(j):
            return V(gs.h[:, j:12:3].unsqueeze(2).to_broadcast([128, 4, 64]), gs.name)
        self.tt(h4(o), h4(oc), gbc(0), ALU.mult)
        self.tt(h4(osl), h4(osl), gbc(1), ALU.mult)
        self.tt(o[:], o[:], osl[:], ALU.add)
        self.tt(h4(ow), h4(ow), gbc(2), ALU.mult)
        self.tt(o[:], o[:], ow[:], ALU.add)
        y = self.nx(c["y"])
        self.copy(y[:], o[:], eng="act")
        self.store_yT(y, c["YT"], 2, i, c["yTs"])

    def phase_dsa_nsa(self, FMS, TMB, TMF, cmp_w1, cmp_w2, cmp_pos, YT):
        S = self.S
        NT = S // 128
        self.phase_begin()
        cd = self.dsa_setup(FMS, TMB, TMF, YT)
        cn = self.nsa_setup(FMS, TMB, TMF, cmp_w1, cmp_w2, cmp_pos, YT)
        P = self.P
        for i in range(NT):
            self.ps_set = (0, 3)
            self.psb_set = (0, 1)
            P.capture = []
            self.dsa_tile(cd, i)
            A = P.capture
            self.ps_set = (3, 2)
            self.psb_set = (1, 1)
            P.capture = []
            self.nsa_tile(cn, i)
            B = P.capture
            P.capture = None
            self.ps_set = (0, 5)
            self.psb_set = (0, 2)
            ia = ib = 0
            na, nb = len(A), len(B)
            while ia < na or ib < nb:
                if ib >= nb or (ia < na and ia * nb <= ib * na):
                    P.add(*A[ia][0], **A[ia][1])
                    ia += 1
                else:
                    P.add(*B[ib][0], **B[ib][1])
                    ib += 1

    def phase_merge(self, x_in, xT_in, w_in_l, w_branch, w_out, ln_g, ln_b, YT, x_out, xT_out):
        S = self.S
        self.phase_begin()
        wg = self.load_w("wg", lambda k: V(w_in_l.h[k * 128:(k + 1) * 128, 3128:7224], w_in_l.name), NKC, 4096)
        wb = self.sb("wb", [128, 4, 2, 1024], BF16)
        for n in range(4):
            for kk_ in range(2):
                self.dma(wb[:, n, kk_, :], V(w_branch.h[n, kk_ * 128:(kk_ + 1) * 128, :], w_branch.name), eng="pool")
        wo = self.load_w("wo", lambda k: V(w_out.h[k * 128:(k + 1) * 128, :], w_out.name), NKC, D)
        g_bc, b_bc, scr = self.ln_setup(ln_g, ln_b)
        xt = self.sb("xT", [128, NKC, 512], BF16)
        yt = self.sb("yT", [128, 8, 512], BF16)
        mT = self.sb("mT", [128, 8, 512], BF16)
        acc_k = self.rot("acc", 2, [128, 512], F32)
        sg_k = self.rot("sg", 2, [128, 512], F32)
        tmp_k = self.rot("tmp", 2, [128, 512], F32)
        xr = [self.sb("xr%d" % i, [128, D], F32) for i in range(2)]
        rqs = [self.sb("r%d" % i, [128, D], F32) for i in range(2)]
        ntile = S // 512

        def load_xt(t_):
            ss_ = slice(t_ * 512, (t_ + 1) * 512)
            self.dma(xt[:], V(xT_in.h.rearrange("(k p) s -> p k s", p=128)[:, :, ss_], xT_in.name))
            self.dma(yt[:], V(YT.h[:, ss_].rearrange("(c p) s -> p c s", p=128), YT.name))

        def load_xq(idx):
            self.dma(xr[idx % 2][:], x_in[idx * 128:(idx + 1) * 128, :])
        load_xt(0)
        load_xq(0)
        for t in range(ntile):
            ss = slice(t * 512, (t + 1) * 512)
            for dc in range(8):
                acc = self.nx(acc_k)
                for n in range(4):
                    pg = self.next_ps()
                    for k in range(NKC):
                        self.mm(pg[:], wg[:, k, n * 1024 + dc * 128:n * 1024 + (dc + 1) * 128], xt[:, k, :],
                                start=(k == 0), stop=(k == NKC - 1))
                    pp = self.next_ps()
                    for k2 in range(2):
                        self.mm(pp[:], wb[:, n, k2, dc * 128:(dc + 1) * 128], yt[:, 2 * n + k2, :], start=(k2 == 0), stop=(k2 == 1))
                    sg = self.nx(sg_k)
                    self.act(sg[:], pg[:], AF.Sigmoid)
                    if n == 0:
                        self.tt(acc[:], sg[:], pp[:], ALU.mult)
                    else:
                        tmp = self.nx(tmp_k)
                        self.tt(tmp[:], sg[:], pp[:], ALU.mult)
                        self.tt(acc[:], acc[:], tmp[:], ALU.add, eng="pool")
                self.copy(mT[:, dc, :], acc[:], eng="act")
            if t + 1 < ntile:
                load_xt(t + 1)
            for q in range(4):
                t0 = t * 512 + q * 128
                xq = xr[q % 2]
                rq = rqs[q % 2]
                if t * 4 + q + 1 < ntile * 4:
                    load_xq(t * 4 + q + 1)
                for half in range(2):
                    hs = slice(half * 512, (half + 1) * 512)
                    pd = self.next_ps()
                    for dc in range(8):
                        self.mm(pd[:], mT[:, dc, q * 128:(q + 1) * 128], wo[:, dc, hs], start=(dc == 0), stop=(dc == 7))
                    self.act(xq[:, hs], xq[:, hs], AF.Copy, scale=ALPHA)
                    self.stt(rq[:, hs], pd[:], 1.0, xq[:, hs], ALU.mult, ALU.add)
                self.finish_tile(rq, g_bc, b_bc, x_out, xT_out, t0, scr)

    def phase_xattn(self, x_in, xT_in, mem, wq_d, wkv_d, wo_d, ln_g, ln_b, x_out, xT_out):
        S = self.S
        self.phase_begin()
        wq = self.load_w("wq", lambda k: V(wq_d.h[k * 128:(k + 1) * 128, :], wq_d.name), NKC, D)
        wkv = self.load_w("wkv", lambda k: V(wkv_d.h[k * 128:(k + 1) * 128, :], wkv_d.name), NKC, 2 * D)
        wo = self.load_w("wo", lambda k: V(wo_d.h[k * 128:(k + 1) * 128, :], wo_d.name), NKC, D)
        g_bc, b_bc, scr = self.ln_setup(ln_g, ln_b)
        memT = self.sb("memT", [128, 8, 256], BF16)
        mr = self.sb("mr", [128, D], F32)
        mb = self.sb("mb", [128, D], BF16)
        for mt in range(2):
            self.dma(mr[:], mem[mt * 128:(mt + 1) * 128, :])
            self.copy(mb[:], mr[:], eng="act")
            pb = self.next_psb()
            for k in range(8):
                self.tr(pb[:, k * 128:(k + 1) * 128], mb[:, k * 128:(k + 1) * 128], self.ident[:])
            self.copy(memT[:, :, mt * 128:(mt + 1) * 128], V(pb.h[:, :].rearrange("p (k t) -> p k t", k=8), pb.name))
        KT = self.sb("KT", [128, 8, 256], BF16)
        for c in range(8):
            ps = self.next_ps()
            for k in range(NKC):
                self.mm(ps[:, 0:256], wkv[:, k, c * 128:(c + 1) * 128], memT[:, k, :], start=(k == 0), stop=(k == NKC - 1))
            self.copy(KT[:, c, :], ps[:, 0:256], eng=("act" if c % 2 else "dve"))
        Vm = self.sb("Vm", [128, 2, D], BF16)
        for mt in range(2):
            for half in range(2):
                ps = self.next_ps()
                for k in range(NKC):
                    self.mm(ps[:], memT[:, k, mt * 128:(mt + 1) * 128], wkv[:, k, D + half * 512:D + (half + 1) * 512],
                            start=(k == 0), stop=(k == NKC - 1))
                self.copy(Vm[:, mt, half * 512:(half + 1) * 512], ps[:], eng=("act" if half else "dve"))
        xt = self.sb("xT", [128, NKC, 512], BF16)
        qT = self.sb("qT", [128, 8, 512], BF16)
        Pf_k = self.rot("Pf", 2, [128, 4, 256], F32)
        Pb_k = self.rot("Pb", 2, [128, 4, 256], BF16)
        PT_k = self.rot("PTx", 2, [128, 8, 128], BF16)
        oT_k = self.rot("oT", 2, [128, 8, 128], BF16)
        st_k = self.rot("xst", 2, [128, 16], F32)
        xr = [self.sb("xr%d" % i, [128, D], F32) for i in range(2)]
        rqs = [self.sb("r%d" % i, [128, D], F32) for i in range(2)]
        SC = 1.0 / 16
        ntile = S // 512

        def load_xt(t_):
            ss_ = slice(t_ * 512, (t_ + 1) * 512)
            self.dma(xt[:], V(xT_in.h.rearrange("(k p) s -> p k s", p=128)[:, :, ss_], xT_in.name))

        def load_xq(idx):
            self.dma(xr[idx % 2][:], x_in[idx * 128:(idx + 1) * 128, :])
        load_xt(0)
        load_xq(0)
        for t in range(ntile):
            ss = slice(t * 512, (t + 1) * 512)
            for c in range(8):
                ps = self.next_ps()
                for k in range(NKC):
                    self.mm(ps[:], wq[:, k, c * 128:(c + 1) * 128], xt[:, k, :], start=(k == 0), stop=(k == NKC - 1))
                self.copy(qT[:, c, :], ps[:], eng=("act" if c % 2 else "dve"))
            if t + 1 < ntile:
                load_xt(t + 1)
            for q in range(4):
                t0 = t * 512 + q * 128
                tq = slice(q * 128, (q + 1) * 128)
                if t * 4 + q + 1 < ntile * 4:
                    load_xq(t * 4 + q + 1)
                pss = [self.next_ps(), self.next_ps()]
                st = self.nx(st_k)
                Pf = self.nx(Pf_k)
                for h in range(4):
                    pv = pss[h // 2][:, (h % 2) * 256:(h % 2) * 256 + 256]
                    for cc in range(2):
                        self.mm(pv, qT[:, 2 * h + cc, tq], KT[:, 2 * h + cc, :], start=(cc == 0), stop=(cc == 1))
                    self.red(st[:, h:h + 1], pv, ALU.max)
                    self.ts(st[:, 4 + h:5 + h], st[:, h:h + 1], -SC, None, ALU.mult)
                    self.act(Pf[:, h, :], pv, AF.Exp, bias=st[:, 4 + h:5 + h], scale=SC, accum=st[:, 8 + h:9 + h])
                self.recip(st[:, 12:16], st[:, 8:12])
                Pb = self.nx(Pb_k)
                self.tt(Pb[:], Pf[:], V(st.h[:, 12:16].unsqueeze(2).to_broadcast([128, 4, 256]), st.name), ALU.mult)
                pb = self.next_psb()
                for h in range(4):
                    for mc in range(2):
                        j = 2 * h + mc
                        self.tr(pb[:, j * 128:(j + 1) * 128], Pb[:, h, mc * 128:(mc + 1) * 128], self.ident[:])
                PT = self.nx(PT_k)
                self.copy(PT[:], V(pb.h[:, :].rearrange("p (j t) -> p j t", j=8), pb.name))
                oT = self.nx(oT_k)
                pso = [self.next_ps(), self.next_ps()]
                for h in range(4):
                    for dc in range(2):
                        j = 2 * h + dc
                        pv = pso[j // 4][:, (j % 4) * 128:(j % 4) * 128 + 128]
                        for mc in range(2):
                            self.mm(pv, Vm[:, mc, h * 256 + dc * 128:h * 256 + (dc + 1) * 128], PT[:, 2 * h + mc, :],
                                    start=(mc == 0), stop=(mc == 1))
                for j4 in range(2):
                    self.copy(oT[:, 4 * j4:4 * j4 + 4, :], V(pso[j4].h[:, :].rearrange("p (j t) -> p j t", j=4), pso[j4].name),
                              eng=("act" if j4 else "dve"))
                xq = xr[q % 2]
                rq = rqs[q % 2]
                for half in range(2):
                    hs = slice(half * 512, (half + 1) * 512)
                    pd = self.next_ps()
                    for c in range(8):
                        self.mm(pd[:], oT[:, c, :], wo[:, c, hs], start=(c == 0), stop=(c == 7))
                    self.act(xq[:, hs], xq[:, hs], AF.Copy, scale=ALPHA)
                    self.stt(rq[:, hs], pd[:], 1.0, xq[:, hs], ALU.mult, ALU.add)
                self.finish_tile(rq, g_bc, b_bc, x_out, xT_out, t0, scr)


OFF = dict(r_q=0, r_k=128, r_v=256, r_g=512, d_q=768, d_k=1024, d_v=1088, i_q=1152, i_k=1408, i_w=1440,
           n_q=1448, n_kc=1704, n_vc=1768, n_ks=1832, n_vs=1896, n_kw=1960, n_vw=2024, n_g=2088,
           s_z=2100, s_xbc=2356, s_dt=3124, br_g=3128)


def _partner(i, headdim, rot):
    half = rot // 2
    j = i % headdim
    b = i - j
    if j < half:
        return b + j + half
    if j < rot:
        return b + j - half
    return i


def build_colidx():
    cols = []

    def roped(name, width, headdim, rot, lo=0, rep=1):
        loc = []
        for r in range(rep):
            loc += list(range(lo, lo + width))
        assert len(loc) == 128
        a = [OFF[name] + i for i in loc]
        b = [OFF[name] + _partner(i, headdim, rot) for i in loc]
        cols.extend(a)
        cols.extend(b)

    roped("r_q", 128, 32, 32)
    roped("r_k", 128, 32, 32)
    roped("d_q", 128, 64, 16, 0)
    roped("d_q", 128, 64, 16, 128)
    roped("d_k", 64, 64, 16, 0, 2)
    roped("i_q", 128, 32, 8, 0)
    roped("i_q", 128, 32, 8, 128)
    roped("i_k", 32, 32, 8, 0, 4)
    roped("n_q", 128, 64, 16, 0)
    roped("n_q", 128, 64, 16, 128)
    roped("n_kc", 64, 64, 16, 0, 2)
    roped("n_ks", 64, 64, 16, 0, 2)
    roped("n_kw", 64, 64, 16, 0, 2)
    cols.extend([OFF["n_vc"] + i for i in range(64)] * 2)
    cols.extend([OFF["s_xbc"] + i for i in range(768)])
    for name, w in (("r_v", 256), ("r_g", 256), ("d_v", 64), ("n_vs", 64), ("n_vw", 64), ("s_z", 256),
                    ("i_w", 8), ("n_g", 12), ("s_dt", 4)):
        cols.extend([OFF[name] + i for i in range(w)])
    return np.asarray(cols, dtype=np.int64)


ROPED_TABLES = [0, 1, 2, 2, 2, 3, 3, 3, 2, 2, 2, 2, 2]
TM0 = (2 * len(ROPED_TABLES) + 7) * 128
NCOL2 = TM0 + 984
NBIS = 18
RET_LNG = [math.log1p(-2.0 ** (-5 - h)) for h in range(4)]


def host_consts(S):
    meta = np.zeros((128, 32), np.float32)

    def fill(t, headdim, rot, theta, scale):
        half = rot // 2
        inv = np.power(np.float32(theta), (-2.0 * np.arange(half, dtype=np.float32) / np.float32(rot)).astype(np.float32)).astype(np.float32)
        for p in range(128):
            i = p % headdim
            if i < rot:
                meta[p, t] = inv[i % half]
                meta[p, 4 + t] = scale
                meta[p, 8 + t] = -scale if i < half else scale
            else:
                meta[p, t] = 0.0
                meta[p, 4 + t] = 1.0
                meta[p, 8 + t] = 0.0

    fill(0, 32, 32, 10000.0, 1.0)
    fill(1, 32, 32, 10000.0, 32.0 ** -0.5)
    fill(2, 64, 16, 500000.0, 1.0)
    fill(3, 32, 8, 500000.0, 1.0)
    for p in range(128):
        meta[p, 12] = RET_LNG[p // 32]
        meta[p, 13 + p // 32] = 1.0
    bdm = np.zeros((128, 256), np.float32)
    for p in range(128):
        bdm[p, 64 * (p // 32):64 * (p // 32) + 64] = 1.0
    NC = (S - 32) // 16 + 1
    NCP = (NC + 127) // 128 * 128
    NB = S // 64
    ovl = np.zeros((NCP, NB), np.float32)
    for c in range(NC):
        for j in range(NB):
            ovl[c, j] = max(min(16 * c + 32, 64 * j + 64) - max(16 * c, 64 * j), 0) / 32.0
    return meta, bdm, ovl, NB, NCP


STAGES = ["ffn1", "inproj", "ret", "ssd", "dsa", "nsa", "merge", "xattn", "ffn2"]


def build(S, depth=DEPTH, stop_after=None):
    kb = KB(S, depth, stop_after)
    meta_np, bdm_np, ovl_np, NB, NCP = host_consts(S)
    kb.n_keep = min(256, S // 4)
    EI = "ExternalInput"
    x = kb.dram("x", [S, D], F32, kind=EI)
    mem = kb.dram("mem", [N_MEM, D], F32, kind=EI)
    ln_g = kb.dram("ln_g", [DEPTH, 4, D], F32, kind=EI)
    ln_b = kb.dram("ln_b", [DEPTH, 4, D], F32, kind=EI)
    f1gu = kb.dram("ffn1_w_gu", [DEPTH, D, 2 * DFF], F32, kind=EI)
    f1dn = kb.dram("ffn1_w_down", [DEPTH, DFF, D], F32, kind=EI)
    w_in = kb.dram("w_in", [DEPTH, D, 7224], F32, kind=EI)
    w2 = kb.dram("w2", [DEPTH, D, NCOL2], F32, kind=EI)
    cmp_w1 = kb.dram("cmp_w1", [DEPTH, 2, 2048, 64], F32, kind=EI)
    cmp_w2 = kb.dram("cmp_w2", [DEPTH, 2, 64, 64], F32, kind=EI)
    cmp_pos = kb.dram("cmp_pos", [DEPTH, 2, 32, 64], F32, kind=EI)
    conv_w = kb.dram("conv_w", [DEPTH, 4, 768], F32, kind=EI)
    conv_b = kb.dram("conv_b", [DEPTH, 768], F32, kind=EI)
    dt_bias = kb.dram("dt_bias", [DEPTH, 4], F32, kind=EI)
    a_log = kb.dram("a_log", [DEPTH, 4], F32, kind=EI)
    d_skip = kb.dram("d_skip", [DEPTH, 4], F32, kind=EI)
    norm_g = kb.dram("ssm_norm_g", [DEPTH, 256], F32, kind=EI)
    w_branch = kb.dram("w_branch", [DEPTH, 4, 256, D], F32, kind=EI)
    w_out = kb.dram("w_out", [DEPTH, D, D], F32, kind=EI)
    xwq = kb.dram("xattn_wq", [DEPTH, D, D], F32, kind=EI)
    xwkv = kb.dram("xattn_wkv", [DEPTH, D, 2 * D], F32, kind=EI)
    xwo = kb.dram("xattn_wo", [DEPTH, D, D], F32, kind=EI)
    f2gu = kb.dram("ffn2_w_gu", [DEPTH, D, 2 * DFF], F32, kind=EI)
    f2dn = kb.dram("ffn2_w_down", [DEPTH, DFF, D], F32, kind=EI)
    meta = kb.dram("meta", [128, 32], F32, kind=EI)
    bdm = kb.dram("bdm", [128, 256], F32, kind=EI)
    ovl = kb.dram("ovl", [NCP, NB], F32, kind=EI)
    out = kb.dram("out", [S, D], F32)
    xTa = kb.dram("xTa", [D, S], BF16)
    xTb = kb.dram("xTb", [D, S], BF16)
    xa = kb.dram("xa", [S, D], F32)
    xb2 = kb.dram("xb2", [S, D], F32)
    FMS = kb.dram("FMS", [20 * 128, S], BF16)
    TMB = kb.dram("TMB", [S, 960], BF16)
    TMF = kb.dram("TMF", [S, 24], F32)
    YT = kb.dram("YT", [1024, S], BF16)
    ROPE = kb.dram("ROPE", [4, 2, 128, S], F32)
    kb.setup()
    kb.setup_consts(meta, bdm, ovl, NB, NCP)
    kb.phase_rope(ROPE)
    kb.phase_transpose_in(x, xTa)

    def L(t, *idx):
        return T(t.h[idx], t.name, True)

    done = False
    xin = x
    for l in range(depth):
        last = (l == depth - 1)

        def stop(name):
            return stop_after == (l, name)
        kb.phase_ffn(xin, xTa, L(f1gu, l), L(f1dn, l), L(ln_g, l, 0), L(ln_b, l, 0), xa, xTb)
        if stop("ffn1"):
            break
        kb.phase_inproj(xTb, L(w2, l), ROPE, FMS, TMB, TMF)
        if stop("inproj"):
            break
        kb.phase_ret(FMS, TMB, YT)
        if stop("ret"):
            break
        kb.phase_ssd(FMS, TMB, TMF, L(conv_w, l), L(conv_b, l), L(dt_bias, l), L(a_log, l), L(d_skip, l), L(norm_g, l), YT)
        if stop("ssd"):
            break
        kb.phase_dsa_nsa(FMS, TMB, TMF, L(cmp_w1, l), L(cmp_w2, l), L(cmp_pos, l), YT)
        if stop("nsa") or stop("dsa"):
            break
        kb.phase_merge(xa, xTb, L(w_in, l), L(w_branch, l), L(w_out, l), L(ln_g, l, 1), L(ln_b, l, 1), YT, xb2, xTa)
        if stop("merge"):
            break
        kb.phase_xattn(xb2, xTa, mem, L(xwq, l), L(xwkv, l), L(xwo, l), L(ln_g, l, 2), L(ln_b, l, 2), xa, xTb)
        if stop("xattn"):
            break
        kb.phase_ffn(xa, xTb, L(f2gu, l), L(f2dn, l), L(ln_g, l, 3), L(ln_b, l, 3), out if last else xb2, None if last else xTa)
        if stop("ffn2"):
            break
        xin = xb2
    kb.flush_pending()
    st = kb.P.emit()
    kb.stats = st
    return kb


def make_in_maps(inputs, S, ncores):
    meta_np, bdm_np, ovl_np, NB, NCP = host_consts(S)
    colidx = build_colidx()
    w_in = np.asarray(inputs["w_in"], dtype=np.float32)
    w2 = np.ascontiguousarray(w_in[:, :, colidx])
    shared = {k: np.ascontiguousarray(np.asarray(v, dtype=np.float32)) for k, v in inputs.items() if k not in ("x", "mem")}
    shared["w2"] = w2
    shared["meta"] = meta_np
    shared["bdm"] = bdm_np
    shared["ovl"] = ovl_np
    maps = []
    for b in range(ncores):
        m = dict(shared)
        m["x"] = np.ascontiguousarray(np.asarray(inputs["x"][b, :S], dtype=np.float32))
        m["mem"] = np.ascontiguousarray(np.asarray(inputs["mem"][b], dtype=np.float32))
        maps.append(m)
    return maps


def kernel(**inputs):
    S = inputs["x"].shape[1]
    B = inputs["x"].shape[0]
    kb = build(S)
    maps = make_in_maps(inputs, S, B)
    res = run_bass_kernel_spmd(kb.nc, maps, core_ids=list(range(B)))
    out = np.stack([np.asarray(r["out"], dtype=np.float32) for r in res.results], axis=0)
    return out
```

```python
import math
import sys
import numpy as np
import concourse.bass as bass
import concourse.mybir as mybir
from concourse.bass_utils import run_bass_kernel_spmd

F32 = mybir.dt.float32
BF16 = mybir.dt.bfloat16
I32 = mybir.dt.int32
AF = mybir.ActivationFunctionType
ALU = mybir.AluOpType
AX = mybir.AxisListType

SEM_LIMIT = 30000
N_DMA_SEMS = 24


class Buf:
    __slots__ = ("name", "last_w", "readers")

    def __init__(self, name):
        self.name = name
        self.last_w = None
        self.readers = []


class Op:
    __slots__ = ("eng", "fn", "deps", "need_inc", "sem", "val", "is_dma", "idx", "tag", "odeps", "n", "seg", "pfirst", "fin")


class Prog:
    def __init__(self, nc):
        self.nc = nc
        self.engs = {"pe": nc.tensor, "act": nc.scalar, "dve": nc.vector, "pool": nc.gpsimd, "sp": nc.sync}
        self.ops = []
        self.bufs = {}
        self.last_on = {}
        self.dmas_since = []
        self.phase_deps = []
        self.phase_bufs = set()
        self.capture = None
        self.seg = 0
        self.do_sched = True
        self.est_time = 0.0

    def buf(self, name):
        b = self.bufs.get(name)
        if b is None:
            b = self.bufs[name] = Buf(name)
        return b

    def add(self, eng, fn, reads=(), writes=(), dma=False, extra_deps=(), n=64):
        if self.capture is not None:
            self.capture.append(((eng, fn), dict(reads=list(reads), writes=list(writes), dma=dma, n=n)))
            return None
        op = Op()
        op.eng = eng
        op.fn = fn
        op.is_dma = dma
        op.need_inc = False
        op.sem = None
        op.val = 0
        op.n = n
        op.seg = self.seg
        op.pfirst = False
        op.fin = 0.0
        op.idx = len(self.ops)
        op.tag = (0, 0)
        deps = {}
        for b in reads:
            b = self.buf(b)
            w = b.last_w
            if w is not None:
                deps[w.idx] = (w, "raw")
        for b in writes:
            b = self.buf(b)
            w = b.last_w
            if w is not None and w.idx not in deps:
                deps[w.idx] = (w, "waw")
            for r in b.readers:
                if r.idx not in deps:
                    deps[r.idx] = (r, "war")
        real = []
        order = []
        for d, kind in deps.values():
            if (not d.is_dma) and d.eng == eng and not dma:
                if eng == "pe" or kind != "raw":
                    order.append(d)
                    continue
            real.append(d)
        for d in extra_deps:
            real.append(d)
        for b in list(reads) + list(writes):
            if b not in self.phase_bufs:
                self.phase_bufs.add(b)
                op.pfirst = True
        op.deps = real
        op.odeps = order
        for b in writes:
            b = self.buf(b)
            b.last_w = op
            b.readers = []
        for b in reads:
            self.buf(b).readers.append(op)
        self.ops.append(op)
        return op

    def barrier(self):
        self.seg += 1
        self.phase_bufs = set()

    def _cost(self, op):
        n = op.n
        e = op.eng
        if op.is_dma:
            return 0.08, 2.0 + n / 100e3
        if e == "pe":
            c = 0.035 + n / 2400.0
        elif e == "act":
            c = 0.22 + n / 1200.0
        elif e == "dve":
            c = 0.08 + n / 960.0
        elif e == "pool":
            c = 0.15 + n / 500.0
        else:
            c = 0.05
        return c, c

    def schedule(self):
        import heapq
        SCHED = self.do_sched
        order = []
        ops = self.ops
        nseg = self.seg + 1
        segs = [[] for _ in range(nseg)]
        for op in ops:
            segs[op.seg].append(op)
        t_base = 0.0
        engs = list(self.engs.keys())
        for sg in segs:
            if not sg:
                continue
            if not SCHED:
                order.extend(sg)
                continue
            inseg = set(id(o) for o in sg)
            indeg = {}
            succ = {}
            dr = {}
            for op in sg:
                cnt = 0
                for d in op.deps + op.odeps:
                    if id(d) in inseg:
                        cnt += 1
                        succ.setdefault(id(d), []).append(op)
                indeg[id(op)] = cnt
                dr[id(op)] = t_base
            wait_h = {e: [] for e in engs}
            rdy_h = {e: [] for e in engs}
            free = {e: t_base for e in engs}
            for op in sg:
                if indeg[id(op)] == 0:
                    heapq.heappush(wait_h[op.eng], (dr[id(op)], op.idx, op))
            left = len(sg)
            tmax = t_base
            while left:
                best = None
                for e in engs:
                    wh = wait_h[e]
                    rh = rdy_h[e]
                    fe = free[e]
                    while wh and wh[0][0] <= fe:
                        _, ix, o = heapq.heappop(wh)
                        heapq.heappush(rh, (ix, o))
                    if rh:
                        cand = (fe, rh[0][0], e, 0)
                    elif wh:
                        cand = (wh[0][0], wh[0][1], e, 1)
                    else:
                        continue
                    if best is None or cand[:2] < best[:2]:
                        best = cand
                start, _, e, which = best
                if which == 0:
                    _, op = heapq.heappop(rdy_h[e])
                else:
                    _, _, op = heapq.heappop(wait_h[e])
                busy, lat = self._cost(op)
                free[e] = start + busy
                op.fin = start + lat
                if op.fin > tmax:
                    tmax = op.fin
                order.append(op)
                left -= 1
                for sc in succ.get(id(op), ()):
                    k = id(sc)
                    extra = 0.05 if (sc.eng == op.eng and not op.is_dma) else 0.35
                    t = op.fin + extra
                    if t > dr[k]:
                        dr[k] = t
                    indeg[k] -= 1
                    if indeg[k] == 0:
                        heapq.heappush(wait_h[sc.eng], (dr[k], sc.idx, sc))
            t_base = tmax
        self.est_time = t_base
        return order

    def emit(self, final_wait_eng="sp"):
        nc = self.nc
        order = self.schedule()
        last_eng = {}
        prev_last = {}
        prev_dmas = []
        older_dmas = []
        cur_dmas = []
        cur_seg = -1
        for op in order:
            if op.seg != cur_seg:
                cur_seg = op.seg
                prev_last = dict(last_eng)
                prev_dmas = older_dmas + cur_dmas
                older_dmas = cur_dmas
                cur_dmas = []
            if op.pfirst:
                op.deps = op.deps + list(prev_last.values()) + prev_dmas
            if op.is_dma:
                cur_dmas.append(op)
            else:
                last_eng[op.eng] = op
        for op in order:
            for d in op.deps:
                d.need_inc = True
            if op.is_dma:
                op.need_inc = True
        eng_sem = {}
        eng_cnt = {}
        dma_sems = [nc.alloc_semaphore("dq%d" % i) for i in range(N_DMA_SEMS)]
        dma_cnt = [0] * N_DMA_SEMS
        dma_last = [None] * N_DMA_SEMS
        ndma = 0
        for op in order:
            if not op.need_inc:
                continue
            if op.is_dma:
                j = ndma % N_DMA_SEMS
                ndma += 1
                if dma_last[j] is not None:
                    op.deps.append(dma_last[j])
                dma_cnt[j] += 16
                op.sem = dma_sems[j]
                op.val = dma_cnt[j]
                dma_last[j] = op
            else:
                e = op.eng
                if e not in eng_sem or eng_cnt[e] >= SEM_LIMIT:
                    eng_sem[e] = nc.alloc_semaphore("s_%s_%d" % (e, op.idx))
                    eng_cnt[e] = 0
                eng_cnt[e] += 1
                op.sem = eng_sem[e]
                op.val = eng_cnt[e]
        waited = {}
        nwaits = 0
        for op in order:
            E = self.engs[op.eng]
            need = {}
            for d in op.deps:
                k = id(d.sem)
                if k not in need or need[k][1] < d.val:
                    need[k] = (d.sem, d.val)
            for k, (sem, val) in need.items():
                wk = (op.eng, k)
                if waited.get(wk, 0) >= val:
                    continue
                E.wait_ge(sem, val)
                nwaits += 1
                waited[wk] = val
            try:
                inst = op.fn()
            except Exception:
                print('EMIT FAIL at op', op.idx, op.eng)
                raise
            if op.need_inc:
                inst.then_inc(op.sem, 16 if op.is_dma else 1)
        E = self.engs[final_wait_eng]
        for j in range(N_DMA_SEMS):
            if dma_cnt[j] > 0:
                E.wait_ge(dma_sems[j], dma_cnt[j])
        self.stats = dict(n_ops=len(self.ops), n_waits=nwaits, n_dma=ndma,
                          n_inc=sum(1 for o in self.ops if o.need_inc), est_ms=self.est_time / 1e3)
        return self.stats


class V:
    __slots__ = ("ap", "b")

    def __init__(self, ap, b):
        self.ap = ap
        self.b = b


class T:
    def __init__(self, h, name, dram=False):
        self.h = h
        self.name = name
        self.dram = dram

    def __getitem__(self, idx):
        if self.dram:
            return V(self.h[idx], self.name)
        return V(self.h[idx], self.name)

    def v(self, ap):
        return V(ap, self.name)


DT_SIZE = {F32: 4, BF16: 2, I32: 4}

D = 1024
DFF = 2816
NKC = D // 128
NFC = DFF // 128
LN_EPS = 1e-5
DEPTH = 2
ALPHA = (2 * DEPTH) ** 0.25
N_MEM = 256


class KB:
    def __init__(self, S, depth=DEPTH, stop_after=None, debug=()):
        self.S = S
        self.depth = depth
        self.stop_after = stop_after
        self.debug = debug
        self.nc = bass.Bass("TRN2", target_bir_lowering=False)
        self.P = Prog(self.nc)
        self.uid = 0
        self.sb_base = 0
        self.sb_cur = 0
        self.outs = {}
        self.arena = None
        self.rots = {}
        self.pending_T = None
        self.xb_cnt = 0
        self.ps_set = (0, 5)
        self.psb_set = (0, 2)
        self.fill_regs = {}
        self.n_keep = 256

    def sb(self, name, shape, dtype):
        nbytes = int(np.prod(shape[1:])) * DT_SIZE[dtype]
        nbytes = (nbytes + 63) // 64 * 64
        off = self.sb_cur
        self.sb_cur += nbytes
        assert self.sb_cur <= 207 * 1024, ("SBUF overflow", name, self.sb_cur)
        self.uid += 1
        if self.arena is None:
            self.arena = self.nc.alloc_sbuf_tensor("arena", [128, 207 * 1024], mybir.dt.uint8)
        ap = self.arena[:, off:off + int(np.prod(shape[1:])) * DT_SIZE[dtype]].bitcast(dtype)
        if len(shape) == 3:
            ap = ap.rearrange("p (a b) -> p a b", a=shape[1])
        elif len(shape) == 4:
            ap = ap.rearrange("p (a b c) -> p a b c", a=shape[1], b=shape[2])
        if shape[0] < 128:
            ap = ap[0:shape[0]]
        return T(ap, "%s_%d" % (name, self.uid))

    def phase_begin(self):
        self.flush_pending()
        self.P.barrier()
        self.sb_cur = self.sb_base

    def dram(self, name, shape, dtype, kind="ExternalOutput"):
        h = self.nc.dram_tensor(name, list(shape), dtype, kind=kind)
        return T(h.ap(), name, dram=True)

    def _rw(self, reads, writes):
        return [r.b for r in reads if isinstance(r, V)], [w.b for w in writes]

    def dma(self, out, in_, eng="sp"):
        nc = self.nc
        E = self.P.engs[eng]
        return self.P.add(eng, lambda: E.dma_start(out=out.ap, in_=in_.ap), reads=[in_.b], writes=[out.b], dma=True,
                          n=int(np.prod(out.ap.shape)) * 2)

    def mm(self, out, lhsT, rhs, start=True, stop=True):
        nc = self.nc
        return self.P.add("pe", lambda: nc.tensor.matmul(out.ap, lhsT.ap, rhs.ap, start=start, stop=stop),
                          reads=[lhsT.b, rhs.b], writes=[out.b], n=int(np.prod(out.ap.shape[1:])) * (4 if lhsT.ap.dtype == F32 else 1))

    def tr(self, out, in_, ident):
        nc = self.nc
        return self.P.add("pe", lambda: nc.tensor.transpose(out.ap, in_.ap, ident.ap),
                          reads=[in_.b, ident.b], writes=[out.b], n=200)

    def act(self, out, in_, func, bias=None, scale=None, accum=None, eng="act"):
        nc = self.nc
        kw = {}
        reads = [in_.b]
        writes = [out.b]
        if bias is not None:
            if isinstance(bias, V):
                kw["bias"] = bias.ap
                reads.append(bias.b)
            else:
                kw["bias"] = bias
        if scale is not None:
            if isinstance(scale, V):
                kw["scale"] = scale.ap
                reads.append(scale.b)
            else:
                kw["scale"] = scale
        if accum is not None:
            kw["accum_out"] = accum.ap
            writes.append(accum.b)
        return self.P.add("act", lambda: nc.scalar.activation(out=out.ap, in_=in_.ap, func=func, **kw),
                          reads=reads, writes=writes, n=int(np.prod(out.ap.shape[1:])))

    def ts(self, out, in0, s1, s2, op0, op1=None, accum=None, eng="dve"):
        E = self.P.engs[eng]
        reads = [in0.b]
        writes = [out.b]
        a1 = s1
        a2 = s2
        if isinstance(s1, V):
            a1 = s1.ap
            reads.append(s1.b)
        if isinstance(s2, V):
            a2 = s2.ap
            reads.append(s2.b)
        kw = {}
        if op1 is not None:
            kw["op1"] = op1
        if accum is not None:
            kw["accum_out"] = accum.ap
            writes.append(accum.b)
        return self.P.add(eng, lambda: E.tensor_scalar(out=out.ap, in0=in0.ap, scalar1=a1, scalar2=a2, op0=op0, **kw),
                          reads=reads, writes=writes, n=int(np.prod(out.ap.shape[1:])))

    def tt(self, out, in0, in1, op, eng="dve"):
        E = self.P.engs[eng]
        return self.P.add(eng, lambda: E.tensor_tensor(out=out.ap, in0=in0.ap, in1=in1.ap, op=op),
                          reads=[in0.b, in1.b], writes=[out.b], n=int(np.prod(out.ap.shape[1:])))

    def stt(self, out, in0, scalar, in1, op0, op1, accum=None):
        nc = self.nc
        reads = [in0.b, in1.b]
        writes = [out.b]
        a = scalar
        if isinstance(scalar, V):
            a = scalar.ap
            reads.append(scalar.b)
        kw = {}
        if accum is not None:
            kw["accum_out"] = accum.ap
            writes.append(accum.b)
        return self.P.add("dve", lambda: nc.vector.scalar_tensor_tensor(out=out.ap, in0=in0.ap, scalar=a, in1=in1.ap,
                                                                     op0=op0, op1=op1, **kw),
                          reads=reads, writes=writes, n=int(np.prod(out.ap.shape[1:])))

    def copy(self, out, in_, eng="dve"):
        E = self.P.engs[eng]
        if eng == "act":
            return self.P.add(eng, lambda: E.copy(out=out.ap, in_=in_.ap), reads=[in_.b], writes=[out.b], n=int(np.prod(out.ap.shape[1:])))
        return self.P.add(eng, lambda: E.tensor_copy(out=out.ap, in_=in_.ap), reads=[in_.b], writes=[out.b], n=int(np.prod(out.ap.shape[1:])))

    def memset(self, out, val, eng="pool"):
        E = self.P.engs[eng]
        return self.P.add(eng, lambda: E.memset(out.ap, val), writes=[out.b], n=int(np.prod(out.ap.shape[1:])))

    def red(self, out, in_, op, axis=AX.X, eng="dve"):
        E = self.P.engs[eng]
        return self.P.add(eng, lambda: E.tensor_reduce(out=out.ap, in_=in_.ap, axis=axis, op=op),
                          reads=[in_.b], writes=[out.b], n=int(np.prod(in_.ap.shape[1:])))

    def recip(self, out, in_):
        nc = self.nc
        return self.P.add("dve", lambda: nc.vector.reciprocal(out=out.ap, in_=in_.ap), reads=[in_.b], writes=[out.b])

    def aselect(self, out, in_, pattern, cmp, fill, base, cm):
        nc = self.nc
        regs = self.fill_regs

        def fn():
            if fill not in regs:
                regs[fill] = nc.gpsimd.to_reg(float(fill))
            return nc.gpsimd.affine_select(out=out.ap, in_=in_.ap, pattern=pattern, compare_op=cmp,
                                           fill=regs[fill], base=base, channel_multiplier=cm)
        return self.P.add("pool", fn, reads=[in_.b], writes=[out.b], n=int(np.prod(out.ap.shape[1:])))

    def iota(self, out, pattern, base, cm):
        nc = self.nc
        return self.P.add("pool", lambda: nc.gpsimd.iota(out.ap, pattern=pattern, base=base, channel_multiplier=cm,
                                                         allow_small_or_imprecise_dtypes=True), writes=[out.b], n=int(np.prod(out.ap.shape[1:])))

    def setup(self):
        nc = self.nc
        self.ps = []
        for i in range(5):
            h = nc.alloc_psum_tensor("ps%d" % i, [128, 512], F32)
            self.ps.append(T(h, "ps%d" % i))
        self.psb = []
        for i in range(2):
            h = nc.alloc_psum_tensor("psb%d" % i, [128, 1024], BF16)
            self.psb.append(T(h, "psb%d" % i))
        self.ps_rr = 0
        self.psb_rr = 0
        self.ident_f = self.sb("identf", [128, 128], F32)
        self.ident = self.sb("ident", [128, 128], BF16)
        self.memset(self.ident_f[:], 1.0)
        self.aselect(self.ident_f[:], self.ident_f[:], [[-1, 128]], ALU.is_equal, 0.0, 0, 1)
        self.copy(self.ident[:], self.ident_f[:], eng="pool")
        self.sb_base = self.sb_cur

    def next_ps(self):
        b0, n = self.ps_set
        t = self.ps[b0 + self.ps_rr % n]
        self.ps_rr += 1
        return t

    def next_psb(self):
        b0, n = self.psb_set
        t = self.psb[b0 + self.psb_rr % n]
        self.psb_rr += 1
        return t

    def load_w(self, name, dram_ap_fn, kchunks, ncols, eng="pool", split=4):
        w = self.sb(name, [128, kchunks, ncols], BF16)
        for k in range(kchunks):
            self.dma(w[:, k, :], dram_ap_fn(k), eng="pool")
        return w

    def layer_norm_tile(self, r, g_bc, b_bc, out_f32, scr):
        st = scr["st"]
        junk = scr["junk"]
        self.act(junk[:], r[:], AF.Identity, accum=st[:, 0:1])
        self.act(junk[:], r[:], AF.Square, accum=st[:, 1:2])
        self.ts(st[:, 2:3], st[:, 0:1], 1.0 / D, None, ALU.mult)
        self.tt(st[:, 3:4], st[:, 2:3], st[:, 2:3], ALU.mult)
        self.stt(st[:, 4:5], st[:, 1:2], 1.0 / D, st[:, 3:4], ALU.mult, ALU.subtract)
        self.ts(st[:, 4:5], st[:, 4:5], 0.0, LN_EPS, ALU.max, ALU.add)
        self.act(st[:, 5:6], st[:, 4:5], AF.Sqrt)
        self.recip(st[:, 6:7], st[:, 5:6])
        self.ts(out_f32[:], r[:], st[:, 2:3], st[:, 6:7], ALU.subtract, ALU.mult)
        self.tt(out_f32[:], out_f32[:], g_bc[:], ALU.mult)
        self.tt(out_f32[:], out_f32[:], b_bc[:], ALU.add)

    def store_xT(self, x_f32, xT_dram, t0, scr, defer=False):
        xbl = scr["xb"]
        if isinstance(xbl, list):
            xb = xbl[self.xb_cnt % len(xbl)]
            self.xb_cnt += 1
        else:
            xb = xbl
        xTs = scr["xTs"]
        self.copy(xb[:], x_f32[:], eng="act")

        def part_b():
            pb = self.next_psb()
            for k in range(NKC):
                self.tr(pb[:, k * 128:(k + 1) * 128], xb[:, k * 128:(k + 1) * 128], self.ident[:])
            self.copy(xTs[:], pb[:, :], eng="dve")
            self.dma(V(xT_dram.h.rearrange("(k p) s -> p k s", p=128)[:, :, t0:t0 + 128], xT_dram.name),
                     V(xTs.h[:].rearrange("p (k t) -> p k t", k=NKC), xTs.name))
        if defer:
            self.flush_pending()
            self.pending_T = part_b
        else:
            part_b()

    def flush_pending(self):
        if self.pending_T is not None:
            f = self.pending_T
            self.pending_T = None
            f()

    def dma_s(self, out, in_, eng="sp"):
        E = self.P.engs[eng]
        return self.P.add(eng, lambda: E.dma_start(out=out.ap, in_=in_.ap, allow_slow_non_contiguous=True),
                          reads=[in_.b], writes=[out.b], dma=True, n=int(np.prod(out.ap.shape)) * 8)

    def rot(self, name, n, shape, dtype):
        key = "_rot_" + name
        lst = [self.sb(name + str(i), shape, dtype) for i in range(n)]
        self.rots[key] = [lst, 0]
        return key

    def nx(self, key):
        lst, i = self.rots[key]
        self.rots[key][1] = i + 1
        return lst[i % len(lst)]

    def vmax(self, out, in_):
        nc = self.nc
        return self.P.add("dve", lambda: nc.vector.max(out=out.ap, in_=in_.ap), reads=[in_.b], writes=[out.b], n=int(np.prod(in_.ap.shape[1:])))

    def match_replace(self, out, rep, vals, imm):
        nc = self.nc
        return self.P.add("dve", lambda: nc.vector.match_replace(out=out.ap, in_to_replace=rep.ap, in_values=vals.ap, imm_value=imm),
                          reads=[rep.b, vals.b], writes=[out.b], n=int(np.prod(vals.ap.shape[1:])))

    def redabs(self, out, in_):
        nc = self.nc
        return self.P.add("dve", lambda: nc.vector.tensor_reduce(out=out.ap, in_=in_.ap, axis=AX.X, op=ALU.max,
                                                                 apply_absolute_value=True),
                          reads=[in_.b], writes=[out.b], n=int(np.prod(in_.ap.shape[1:])))

    def fm_rows(self, FMS, c0, nchunk, s0, s1):
        return V(FMS.h[c0 * 128:(c0 + nchunk) * 128, s0:s1].rearrange("(c p) s -> p c s", p=128), FMS.name)

    def setup_consts(self, meta, bdm, ovl, NB, NCP):
        S = self.S
        NT = S // 128
        self.NB = NB
        self.NCP = NCP
        self.meta = self.sb("meta", [128, 32], F32)
        self.dma(self.meta[:], meta[:, :])
        self.bdm = self.sb("bdm", [128, 256], F32)
        self.dma(self.bdm[:], bdm[:, :])
        self.ovl = self.sb("ovl", [128, NCP // 128, NB], F32)
        self.dma(self.ovl[:], V(ovl.h.rearrange("(c p) j -> p c j", p=128), ovl.name))
        self.U = self.sb("U", [128, 128], F32)
        self.memset(self.U[:], 1.0)
        self.aselect(self.U[:], self.U[:], [[1, 128]], ALU.is_ge, 0.0, 0, -1)
        self.cneg30 = self.sb("cneg30", [128, 128], F32)
        self.memset(self.cneg30[:], 0.0)
        self.aselect(self.cneg30[:], self.cneg30[:], [[-1, 128]], ALU.is_ge, -1e30, 0, 1)
        self.cneg2k = self.sb("cneg2k", [128, 128], F32)
        self.memset(self.cneg2k[:], 0.0)
        self.aselect(self.cneg2k[:], self.cneg2k[:], [[-1, 128]], ALU.is_ge, -2000.0, 0, 1)
        self.band = self.sb("band", [128, 640], F32)
        self.memset(self.band[:], 0.0)
        self.aselect(self.band[:], self.band[:], [[1, 640]], ALU.is_ge, -2000.0, -1, -1)
        self.aselect(self.band[:], self.band[:], [[-1, 640]], ALU.is_ge, -2000.0, 512, 1)
        self.decayT4 = self.sb("decayT4", [128, 4, 128], F32)
        self.xi = self.sb("xi", [128, 128], F32)
        self.zeta = self.sb("zeta", [128, 128], F32)
        self.cdecay = self.sb("cdecay", [128, 1], F32)
        self.rkc = self.sb("rkc", [128, 20], F32)
        self.sb_base = self.sb_cur
        dji = self.sb("dji", [128, 128], F32)
        self.iota(dji[:], [[1, 128]], 0, -1)
        for h in range(4):
            self.act(self.decayT4[:, h, :], dji[:], AF.Exp, scale=RET_LNG[h])
        self.tt(self.decayT4[:], self.decayT4[:], V(self.U.h[:, :].unsqueeze(1).to_broadcast([128, 4, 128]), self.U.name), ALU.mult)
        ip1 = self.sb("ip1", [128, 128], F32)
        self.iota(ip1[:], [[1, 128]], 1, 0)
        self.act(self.xi[:], ip1[:], AF.Exp, scale=self.meta[:, 12:13])
        jr = self.sb("jr", [128, 128], F32)
        self.iota(jr[:], [[0, 128]], 127, -1)
        for h in range(4):
            self.act(self.zeta[:, 32 * h:32 * h + 32], jr[:, 32 * h:32 * h + 32], AF.Exp, scale=RET_LNG[h])
        c128 = self.sb("c128", [128, 1], F32)
        self.memset(c128[:], 128.0)
        self.act(self.cdecay[:], c128[:], AF.Exp, scale=self.meta[:, 12:13])
        for k in range(20):
            self.memset(self.rkc[:, k:k + 1], 2.0 ** (-k))

    def build_addmask(self):
        NT = self.S // 128
        NB = self.NB
        self.addmask = self.sb("addmask", [128, NT, NB], F32)
        self.memset(self.addmask[:], 0.0)
        for i in range(NT):
            for half in range(2):
                cur = 2 * i + half
                r0 = 64 * half
                v = self.addmask[r0:r0 + 64, i, :]
                self.aselect(v, v, [[-1, NB]], ALU.is_ge, -1e30, cur, 0)
                self.memset(self.addmask[r0:r0 + 64, i, 0:1], 1e30)
                self.memset(self.addmask[r0:r0 + 64, i, cur:cur + 1], 1e30)
                if cur >= 1:
                    self.memset(self.addmask[r0:r0 + 64, i, cur - 1:cur], 1e30)

    def phase_rope(self, ROPE):
        S = self.S
        self.phase_begin()
        pos = self.sb("pos", [128, S], F32)
        self.iota(pos[:], [[1, S]], 0, 0)
        a = self.sb("a", [128, S], F32)
        ki = self.sb("ki", [128, S], I32)
        kf = self.sb("kf", [128, S], F32)
        m = self.sb("m", [128, S], F32)
        r = self.sb("r", [128, S], F32)
        PI = math.pi
        for t in range(4):
            for which in range(2):
                self.ts(a[:], pos[:], self.meta[:, t:t + 1], (PI / 2 if which == 0 else 0.0), ALU.mult, ALU.add)
                self.ts(kf[:], a[:], 1.0 / (2 * PI), None, ALU.mult)
                self.copy(ki[:], kf[:])
                self.copy(kf[:], ki[:])
                self.stt(r[:], kf[:], -2 * PI, a[:], ALU.mult, ALU.add)
                self.ts(m[:], r[:], PI, -2 * PI, ALU.is_gt, ALU.mult)
                self.tt(r[:], r[:], m[:], ALU.add)
                self.ts(m[:], r[:], -PI, 2 * PI, ALU.is_lt, ALU.mult)
                self.tt(r[:], r[:], m[:], ALU.add)
                self.ts(r[:], r[:], PI, -PI, ALU.min, ALU.max)
                self.act(r[:], r[:], AF.Sin)
                col = 4 + 4 * which + t
                self.ts(r[:], r[:], self.meta[:, col:col + 1], None, ALU.mult)
                self.dma(ROPE[t, which], r[:])

    def finish_tile(self, rq, g_bc, b_bc, x_out, xT_out, t0, scr):
        self.layer_norm_tile(rq, g_bc, b_bc, rq, scr)
        self.dma(x_out[t0:t0 + 128, :], rq[:])
        if xT_out is not None:
            self.store_xT(rq, xT_out, t0, scr, defer=True)

    def ln_setup(self, ln_g, ln_b):
        g_bc = self.sb("g_bc", [128, D], F32)
        b_bc = self.sb("b_bc", [128, D], F32)
        self.dma(g_bc[:], V(ln_g.h.partition_broadcast(128), ln_g.name))
        self.dma(b_bc[:], V(ln_b.h.partition_broadcast(128), ln_b.name))
        scr = dict(st=self.sb("st", [128, 8], F32), junk=self.sb("junk", [128, D], BF16),
                   xb=[self.sb("xb0", [128, D], BF16), self.sb("xb1", [128, D], BF16)], xTs=self.sb("xTs", [128, D], BF16))
        return g_bc, b_bc, scr

    def phase_ffn(self, x_in, xT_in, w_gu, w_down, ln_g, ln_b, x_out, xT_out):
        S = self.S
        self.phase_begin()
        wgu = self.load_w("wgu", lambda k: V(w_gu.h[k * 128:(k + 1) * 128, :], w_gu.name), NKC, 2 * DFF)
        wdn = self.load_w("wdn", lambda k: V(w_down.h[k * 128:(k + 1) * 128, :], w_down.name), NFC, D)
        g_bc, b_bc, scr = self.ln_setup(ln_g, ln_b)
        xt = self.sb("xT", [128, NKC, 512], BF16)
        hT = self.sb("hT", [128, NFC, 512], BF16)
        sg = [self.sb("sg%d" % i, [128, 512], BF16) for i in range(2)]
        xr = [self.sb("xr%d" % i, [128, D], F32) for i in range(2)]
        rqs = [self.sb("r%d" % i, [128, D], F32) for i in range(2)]
        ntile = S // 512

        def load_xt(t_):
            self.dma(xt[:], V(xT_in.h.rearrange("(k p) s -> p k s", p=128)[:, :, t_ * 512:(t_ + 1) * 512], xT_in.name))

        def load_xq(idx):
            self.dma(xr[idx % 2][:], x_in[idx * 128:(idx + 1) * 128, :])
        load_xt(0)
        load_xq(0)
        for t in range(ntile):
            for j in range(NFC):
                pg = self.next_ps()
                pu = self.next_ps()
                for k in range(NKC):
                    self.mm(pg[:], wgu[:, k, j * 128:(j + 1) * 128], xt[:, k, :], start=(k == 0), stop=(k == NKC - 1))
                for k in range(NKC):
                    self.mm(pu[:], wgu[:, k, DFF + j * 128:DFF + (j + 1) * 128], xt[:, k, :], start=(k == 0), stop=(k == NKC - 1))
                s = sg[j % 2]
                self.act(s[:], pg[:], AF.Silu)
                self.tt(hT[:, j, :], s[:], pu[:], ALU.mult)
            if t + 1 < ntile:
                load_xt(t + 1)
            for q in range(4):
                t0 = t * 512 + q * 128
                xq = xr[q % 2]
                rq = rqs[q % 2]
                if t * 4 + q + 1 < ntile * 4:
                    load_xq(t * 4 + q + 1)
                for half in range(2):
                    hs = slice(half * 512, (half + 1) * 512)
                    pd = self.next_ps()
                    for j in range(NFC):
                        self.mm(pd[:], hT[:, j, q * 128:(q + 1) * 128], wdn[:, j, hs], start=(j == 0), stop=(j == NFC - 1))
                    self.act(xq[:, hs], xq[:, hs], AF.Copy, scale=ALPHA)
                    self.stt(rq[:, hs], pd[:], 0.5, xq[:, hs], ALU.mult, ALU.add)
                self.finish_tile(rq, g_bc, b_bc, x_out, xT_out, t0, scr)

    def phase_transpose_in(self, x_in, xT_out):
        S = self.S
        self.phase_begin()
        xr = [self.sb("xr%d" % i, [128, D], F32) for i in range(2)]
        scr = dict(xb=self.sb("xb", [128, D], BF16), xTs=self.sb("xTs", [128, D], BF16))
        for i in range(S // 128):
            xq = xr[i % 2]
            self.dma(xq[:], x_in[i * 128:(i + 1) * 128, :])
            self.store_xT(xq, xT_out, i * 128, scr)

    def phase_inproj(self, xT_in, w2, ROPE, FMS, TMB, TMF):
        S = self.S
        self.phase_begin()
        w = self.load_w("win", lambda k: V(w2.h[k * 128:(k + 1) * 128, :], w2.name), NKC, NCOL2)
        xts = [self.sb("xT%d" % i_, [128, NKC, 512], BF16) for i_ in range(2)]
        tabs = [self.sb("tab%d" % i_, [128, 4, 2, 512], F32) for i_ in range(2)]
        t1 = self.rot("t1", 3, [128, 512], F32)
        t2 = self.rot("t2", 3, [128, 512], F32)
        ob = self.rot("ob", 4, [128, 512], BF16)
        tmb = self.rot("tmb", 2, [128, 960], BF16)
        tmf = self.rot("tmf", 2, [128, 24], F32)
        ntile = S // 512

        def load_t(t_):
            ss_ = slice(t_ * 512, (t_ + 1) * 512)
            self.dma(xts[t_ % 2][:], V(xT_in.h.rearrange("(k p) s -> p k s", p=128)[:, :, ss_], xT_in.name))
            self.dma(tabs[t_ % 2][:], V(ROPE.h[:, :, :, ss_].rearrange("t w p s -> p t w s"), ROPE.name))
        load_t(0)
        for t in range(ntile):
            ss = slice(t * 512, (t + 1) * 512)
            xt = xts[t % 2]
            tab = tabs[t % 2]
            if t + 1 < ntile:
                load_t(t + 1)
            for ci, tb in enumerate(ROPED_TABLES):
                pA = self.next_ps()
                pB = self.next_ps()
                for k in range(NKC):
                    self.mm(pA[:], w[:, k, (2 * ci) * 128:(2 * ci + 1) * 128], xt[:, k, :], start=(k == 0), stop=(k == NKC - 1))
                for k in range(NKC):
                    self.mm(pB[:], w[:, k, (2 * ci + 1) * 128:(2 * ci + 2) * 128], xt[:, k, :], start=(k == 0), stop=(k == NKC - 1))
                a1 = self.nx(t1)
                a2 = self.nx(t2)
                o = self.nx(ob)
                self.tt(a1[:], pA[:], tab[:, tb, 0, :], ALU.mult)
                self.tt(a2[:], pB[:], tab[:, tb, 1, :], ALU.mult)
                self.tt(o[:], a1[:], a2[:], ALU.add)
                self.dma(V(FMS.h[ci * 128:(ci + 1) * 128, ss], FMS.name), o[:])
            nr = len(ROPED_TABLES)
            for j in range(7):
                wc = 2 * nr + j
                pA = self.next_ps()
                for k in range(NKC):
                    self.mm(pA[:], w[:, k, wc * 128:(wc + 1) * 128], xt[:, k, :], start=(k == 0), stop=(k == NKC - 1))
                o = self.nx(ob)
                self.copy(o[:], pA[:], eng="act")
                self.dma(V(FMS.h[(nr + j) * 128:(nr + j + 1) * 128, ss], FMS.name), o[:])
            for q in range(4):
                t0 = t * 512 + q * 128
                pA = self.next_ps()
                pB = self.next_ps()
                for k in range(NKC):
                    self.mm(pA[:], xt[:, k, q * 128:(q + 1) * 128], w[:, k, TM0:TM0 + 512], start=(k == 0), stop=(k == NKC - 1))
                for k in range(NKC):
                    self.mm(pB[:, 0:472], xt[:, k, q * 128:(q + 1) * 128], w[:, k, TM0 + 512:TM0 + 984], start=(k == 0), stop=(k == NKC - 1))
                b = self.nx(tmb)
                f = self.nx(tmf)
                self.copy(b[:, 0:512], pA[:], eng="act")
                self.copy(b[:, 512:960], pB[:, 0:448])
                self.copy(f[:], pB[:, 448:472])
                self.dma(TMB[t0:t0 + 128, :], b[:])
                self.dma(TMF[t0:t0 + 128, :], f[:])

    def store_yT(self, y, YT, br, n, yTs_key):
        pb = self.next_psb()
        self.tr(pb[:, 0:128], y[:, 0:128], self.ident[:])
        self.tr(pb[:, 128:256], y[:, 128:256], self.ident[:])
        yTs = self.nx(yTs_key)
        self.copy(yTs[:], pb[:, 0:256])
        self.dma(V(YT.h[br * 256:(br + 1) * 256, n * 128:(n + 1) * 128].rearrange("(c p) t -> p c t", p=128), YT.name),
                 V(yTs.h[:, :].rearrange("p (c t) -> p c t", c=2), yTs.name))

    def phase_ret(self, FMS, TMB, YT):
        S = self.S
        NT = S // 128
        self.phase_begin()
        rq = self.sb("rq", [128, S], BF16)
        rk = self.sb("rk", [128, S], BF16)
        self.dma(rq[:], V(FMS.h[0:128, :], FMS.name))
        self.dma(rk[:], V(FMS.h[128:256, :], FMS.name))
        Sbd = self.sb("Sbd", [128, 256], F32)
        Sbd_bf = self.sb("Sbd_bf", [128, 256], BF16)
        self.memset(Sbd[:], 0.0)
        self.memset(Sbd_bf[:], 0.0)
        vt_k = self.rot("vt", 2, [128, 512], BF16)
        qxi_k = self.rot("qxi", 2, [128, 128], BF16)
        qm_k = self.rot("qm", 2, [128, 4, 128], BF16)
        kz_k = self.rot("kz", 2, [128, 128], BF16)
        PT_k = self.rot("PT", 2, [128, 4, 128], BF16)
        cross_k = self.rot("cross", 2, [128, 256], F32)
        o_k = self.rot("o", 2, [128, 256], F32)
        tmp_k = self.rot("tmp", 2, [128, 256], F32)
        osq_k = self.rot("osq", 2, [128, 256], F32)
        sg_k = self.rot("sg", 2, [128, 256], F32)
        st_k = self.rot("st", 2, [128, 16], F32)
        y_k = self.rot("y", 2, [128, 256], BF16)
        yTs_k = self.rot("yTs", 2, [128, 256], BF16)
        hm = V(self.meta.h[:, 13:17].unsqueeze(2).to_broadcast([128, 4, 128]), self.meta.name)
        for n in range(NT):
            sl = slice(n * 128, (n + 1) * 128)
            vt = self.nx(vt_k)
            self.dma(vt[:], TMB[n * 128:(n + 1) * 128, 0:512])
            qxi = self.nx(qxi_k)
            self.tt(qxi[:], rq[:, sl], self.xi[:], ALU.mult)
            qm = self.nx(qm_k)
            self.tt(qm[:], V(rq.h[:, sl].unsqueeze(1).to_broadcast([128, 4, 128]), rq.name), hm, ALU.mult, eng="pool")
            pb = self.next_psb()
            self.tr(pb[:, 0:128], rk[:, sl], self.ident[:])
            kz = self.nx(kz_k)
            self.tt(kz[:], pb[:, 0:128], self.zeta[:], ALU.mult)
            ps1 = self.next_ps()
            self.mm(ps1[:], rk[:, sl], V(qm.h[:, :, :].rearrange("p h i -> p (h i)"), qm.name))
            PT = self.nx(PT_k)
            self.tt(PT[:], V(ps1.h[:, :].rearrange("p (h i) -> p h i", h=4), ps1.name), self.decayT4[:], ALU.mult)
            ps2 = self.next_ps()
            self.mm(ps2[:, 0:256], qxi[:], Sbd_bf[:])
            cross = self.nx(cross_k)
            self.copy(cross[:], ps2[:, 0:256], eng="act")
            ps3 = self.next_ps()
            for h in range(4):
                self.mm(ps3[:, 64 * h:64 * h + 64], PT[:, h, :], vt[:, 64 * h:64 * h + 64])
            o = self.nx(o_k)
            self.tt(o[:], ps3[:, 0:256], cross[:], ALU.add)
            ps4 = self.next_ps()
            self.mm(ps4[:, 0:256], kz[:], vt[:, 0:256])
            tmp = self.nx(tmp_k)
            self.tt(tmp[:], ps4[:, 0:256], self.bdm[:], ALU.mult)
            self.stt(Sbd[:], Sbd[:], self.cdecay[:, 0:1], tmp[:], ALU.mult, ALU.add)
            self.copy(Sbd_bf[:], Sbd[:], eng="act")
            st = self.nx(st_k)
            o3 = V(o.h[:, :].rearrange("p (h e) -> p h e", h=4), o.name)
            self.red(st[:, 0:4], o3, ALU.add)
            osq = self.nx(osq_k)
            self.tt(osq[:], o[:], o[:], ALU.mult, eng="pool")
            self.red(st[:, 4:8], V(osq.h[:, :].rearrange("p (h e) -> p h e", h=4), osq.name), ALU.add)
            self.ts(st[:, 8:12], st[:, 0:4], 1.0 / 64, None, ALU.mult)
            self.tt(st[:, 12:16], st[:, 8:12], st[:, 8:12], ALU.mult)
            self.stt(st[:, 4:8], st[:, 4:8], 1.0 / 64, st[:, 12:16], ALU.mult, ALU.subtract)
            self.ts(st[:, 4:8], st[:, 4:8], 0.0, LN_EPS, ALU.max, ALU.add)
            self.act(st[:, 4:8], st[:, 4:8], AF.Sqrt)
            self.recip(st[:, 4:8], st[:, 4:8])
            self.tt(o3, o3, V(st.h[:, 8:12].unsqueeze(2).to_broadcast([128, 4, 64]), st.name), ALU.subtract)
            self.tt(o3, o3, V(st.h[:, 4:8].unsqueeze(2).to_broadcast([128, 4, 64]), st.name), ALU.mult)
            sg = self.nx(sg_k)
            self.act(sg[:], vt[:, 256:512], AF.Silu)
            y = self.nx(y_k)
            self.tt(y[:], o[:], sg[:], ALU.mult)
            self.store_yT(y, YT, 0, n, yTs_k)

    def phase_ssd(self, FMS, TMB, TMF, conv_w, conv_b, dt_bias, a_log, d_skip, norm_g, YT):
        S = self.S
        NT = S // 128
        self.phase_begin()
        cw = self.sb("cw", [128, 6, 4], F32)
        for k_ in range(4):
            self.dma_s(cw[:, :, k_], V(conv_w.h[k_].rearrange("(c p) -> p c", p=128), conv_w.name))
        cb = self.sb("cb", [128, 6], F32)
        self.dma_s(cb[:], V(conv_b.h.rearrange("(c p) -> p c", p=128), conv_b.name))
        dtb = self.sb("dtb", [128, 4], F32)
        self.dma(dtb[:], V(dt_bias.h.partition_broadcast(128), dt_bias.name))
        a_bc = self.sb("a_bc", [128, 4], F32)
        self.dma(a_bc[:], V(a_log.h.partition_broadcast(128), a_log.name))
        self.act(a_bc[:], a_bc[:], AF.Exp)
        self.ts(a_bc[:], a_bc[:], -1.0, None, ALU.mult)
        Dbc = self.sb("Dbc", [128, 4], F32)
        self.dma(Dbc[:], V(d_skip.h.partition_broadcast(128), d_skip.name))
        ng_bc = self.sb("ng_bc", [128, 256], F32)
        self.dma(ng_bc[:], V(norm_g.h.partition_broadcast(128), norm_g.name))
        xbcs = self.sb("xbcs", [128, 6, S], BF16)
        raw_k = self.rot("raw", 2, [128, 6, 515], BF16)
        acc_k = self.rot("acc", 2, [128, 512], F32)
        for t in range(S // 512):
            raw = self.nx(raw_k)
            if t == 0:
                self.memset(raw[:, :, 0:3], 0.0)
                self.dma(raw[:, :, 3:515], self.fm_rows(FMS, 14, 6, 0, 512))
            else:
                self.dma(raw[:, :, 0:515], self.fm_rows(FMS, 14, 6, t * 512 - 3, (t + 1) * 512))
            for c in range(6):
                acc = self.nx(acc_k)
                self.ts(acc[:], raw[:, c, 3:515], cw[:, c, 3:4], None, ALU.mult)
                for k in (2, 1, 0):
                    self.stt(acc[:], raw[:, c, k:k + 512], cw[:, c, k:k + 1], acc[:], ALU.mult, ALU.add)
                self.act(xbcs[:, c, t * 512:(t + 1) * 512], acc[:], AF.Silu, bias=cb[:, c:c + 1])
        prev = self.sb("prev", [128, 256], F32)
        prev_bf = self.sb("prev_bf", [128, 256], BF16)
        self.memset(prev[:], 0.0)
        self.memset(prev_bf[:], 0.0)
        xsB_k = self.rot("xsB", 2, [128, 512], BF16)
        tmf_k = self.rot("tmf", 2, [128, 24], F32)
        zt_k = self.rot("zt", 2, [128, 256], BF16)
        st_k = self.rot("st", 2, [128, 32], F32)
        adtb_k = self.rot("adtb", 2, [128, 4, 128], F32)
        seg_k = self.rot("seg", 2, [128, 4, 128], F32)
        MT_k = self.rot("MT", 2, [128, 4, 128], BF16)
        X_k = self.rot("X", 2, [128, 256], BF16)
        Xd_k = self.rot("Xd", 2, [128, 256], BF16)
        yd_k = self.rot("yd", 2, [128, 256], F32)
        y_k = self.rot("y", 2, [128, 256], F32)
        t2_k = self.rot("t2", 2, [128, 256], F32)
        sz_k = self.rot("sz", 2, [128, 256], F32)
        yb_k = self.rot("yb", 2, [128, 256], BF16)
        yTs_k = self.rot("yTs", 2, [128, 256], BF16)
        Ubc = V(self.U.h[:, :].unsqueeze(1).to_broadcast([128, 4, 128]), self.U.name)

        def h4(t_):
            return V(t_.h[:, 0:256].rearrange("p (h e) -> p h e", h=4), t_.name)

        def bc4(v_):
            return V(v_.ap.unsqueeze(2).to_broadcast([128, 4, 64]), v_.b)

        for n in range(NT):
            sl = slice(n * 128, (n + 1) * 128)
            pb = self.next_psb()
            for c in range(4):
                self.tr(pb[:, c * 128:(c + 1) * 128], xbcs[:, c, sl], self.ident[:])
            xsB = self.nx(xsB_k)
            self.copy(xsB[:], pb[:, 0:512])
            tmf = self.nx(tmf_k)
            self.dma(tmf[:], TMF[n * 128:(n + 1) * 128, :])
            zt = self.nx(zt_k)
            self.dma(zt[:], TMB[n * 128:(n + 1) * 128, 704:960])
            st = self.nx(st_k)
            self.tt(st[:, 0:4], tmf[:, 20:24], dtb[:], ALU.add)
            self.act(st[:, 0:4], st[:, 0:4], AF.Exp)
            self.act(st[:, 0:4], st[:, 0:4], AF.Ln, bias=1.0)
            self.tt(st[:, 4:8], st[:, 0:4], a_bc[:], ALU.mult)
            adtb = self.nx(adtb_k)
            self.copy(adtb[:], V(st.h[:, 4:8].unsqueeze(2).to_broadcast([128, 4, 128]), st.name))
            psA = self.next_ps()
            self.mm(psA[:, 0:4], self.U[:], st[:, 4:8])
            self.copy(st[:, 8:12], psA[:, 0:4], eng="act")
            psB = self.next_ps()
            for h in range(4):
                self.mm(psB[:, h * 128:(h + 1) * 128], adtb[:, h, :], self.U[:])
            seg = self.nx(seg_k)
            for h in range(4):
                self.ts(seg[:, h, :], psB[:, h * 128:(h + 1) * 128], st[:, 8 + h:9 + h], 0.0, ALU.subtract, ALU.min)
            self.act(seg[:], seg[:], AF.Exp)
            self.tt(seg[:], seg[:], Ubc, ALU.mult, eng="pool")
            alast = V(psB.h[:, 127:512:128], psB.name)
            self.tt(st[:, 12:16], alast, st[:, 8:12], ALU.subtract)
            self.act(st[:, 12:16], st[:, 12:16], AF.Exp)
            self.act(st[:, 16:20], alast, AF.Exp)
            self.act(st[:, 20:24], st[:, 8:12], AF.Exp)
            psG = self.next_ps()
            for g in range(2):
                self.mm(psG[:, g * 128:(g + 1) * 128], xbcs[:, 2 + g, sl], xbcs[:, 4 + g, sl])
            MT = self.nx(MT_k)
            for g in range(2):
                self.tt(MT[:, 2 * g:2 * g + 2, :], seg[:, 2 * g:2 * g + 2, :],
                        V(psG.h[:, g * 128:(g + 1) * 128].unsqueeze(1).to_broadcast([128, 2, 128]), psG.name), ALU.mult)
            X = self.nx(X_k)
            self.tt(h4(X), h4(xsB), bc4(st[:, 0:4]), ALU.mult)
            psY = self.next_ps()
            for h in range(4):
                self.mm(psY[:, 64 * h:64 * h + 64], MT[:, h, :], X[:, 64 * h:64 * h + 64])
            psO = self.next_ps()
            for g in range(2):
                self.mm(psO[:, 128 * g:128 * g + 128], xbcs[:, 4 + g, sl], prev_bf[:, 128 * g:128 * g + 128])
            yd = self.nx(yd_k)
            self.copy(yd[:], psY[:, 0:256], eng="act")
            y = self.nx(y_k)
            self.tt(h4(y), h4(psO), bc4(st[:, 20:24]), ALU.mult)
            self.tt(y[:], y[:], yd[:], ALU.add)
            t2 = self.nx(t2_k)
            self.tt(h4(t2), h4(xsB), bc4(Dbc[:, 0:4]), ALU.mult, eng="pool")
            self.tt(y[:], y[:], t2[:], ALU.add)
            Xd = self.nx(Xd_k)
            self.tt(h4(Xd), h4(X), bc4(st[:, 12:16]), ALU.mult, eng="pool")
            psS = self.next_ps()
            for g in range(2):
                self.mm(psS[:, 128 * g:128 * g + 128], xsB[:, 256 + 128 * g:256 + 128 * g + 128], Xd[:, 128 * g:128 * g + 128])
            self.tt(h4(prev), h4(prev), bc4(st[:, 16:20]), ALU.mult)
            self.tt(prev[:], prev[:], psS[:, 0:256], ALU.add)
            self.copy(prev_bf[:], prev[:], eng="act")
            sz = self.nx(sz_k)
            self.act(sz[:], zt[:], AF.Silu)
            self.tt(y[:], y[:], sz[:], ALU.mult)
            self.tt(t2[:], y[:], y[:], ALU.mult, eng="pool")
            self.red(st[:, 24:26], V(t2.h[:, :].rearrange("p (g e) -> p g e", g=2), t2.name), ALU.add)
            self.ts(st[:, 24:26], st[:, 24:26], 1.0 / 128, LN_EPS, ALU.mult, ALU.add)
            self.act(st[:, 24:26], st[:, 24:26], AF.Sqrt)
            self.recip(st[:, 24:26], st[:, 24:26])
            y2 = V(y.h[:, :].rearrange("p (g e) -> p g e", g=2), y.name)
            self.tt(y2, y2, V(st.h[:, 24:26].unsqueeze(2).to_broadcast([128, 2, 128]), st.name), ALU.mult)
            yb = self.nx(yb_k)
            self.tt(yb[:], y[:], ng_bc[:], ALU.mult)
            self.store_yT(yb, YT, 3, n, yTs_k)

    def softmax_pv(self, Ssb, nk, Vt, kt0, out, kk, clamp=None):
        st = self.nx(kk["st"])
        self.red(st[:, 0:1], Ssb, ALU.max)
        if clamp is not None:
            self.ts(st[:, 0:1], st[:, 0:1], clamp, None, ALU.max)
        self.ts(st[:, 1:2], st[:, 0:1], -1.0, None, ALU.mult)
        P = self.nx(kk["P"])
        self.act(P[:, 0:nk], Ssb, AF.Exp, bias=st[:, 1:2], accum=st[:, 2:3])
        self.ts(st[:, 3:4], st[:, 2:3], 1e-30, None, ALU.max)
        self.recip(st[:, 4:5], st[:, 3:4])
        po = self.next_ps()
        nkt = nk // 128
        for g0 in range(0, nkt, 8):
            gn = min(8, nkt - g0)
            pb = self.next_psb()
            for j in range(gn):
                self.tr(pb[:, j * 128:(j + 1) * 128], P[:, (g0 + j) * 128:(g0 + j + 1) * 128], self.ident[:])
            PT = self.nx(kk["PT"])
            self.copy(PT[:, 0:gn * 128], pb[:, 0:gn * 128], eng="act")
            for j in range(gn):
                self.mm(po[:, 0:64], PT[:, j * 128:(j + 1) * 128], Vt[:, kt0 + g0 + j, :],
                        start=(g0 + j == 0), stop=(g0 + j == nkt - 1))
        self.ts(out, po[:, 0:64], st[:, 4:5], None, ALU.mult)

    def attn_keys(self, pfx):
        S = self.S
        return dict(st=self.rot(pfx + "sst", 2, [128, 8], F32), P=self.rot(pfx + "P", 1, [128, S], BF16),
                    PT=self.rot(pfx + "PTa", 2, [128, 1024], BF16))

    def dsa_setup(self, FMS, TMB, TMF, YT):
        S = self.S
        NT = S // 128
        c = dict(FMS=FMS, YT=YT)
        c["dk"] = self.sb("dk", [128, S], BF16)
        self.dma(c["dk"][:], V(FMS.h[4 * 128:5 * 128, :], FMS.name))
        ikr = self.sb("ikr", [128, S], BF16)
        self.dma(ikr[:], V(FMS.h[7 * 128:8 * 128, :], FMS.name))
        c["ikm"] = self.sb("ikm", [128, 4, S], BF16)
        for g in range(4):
            self.ts(c["ikm"][:, g, :], ikr[:], self.meta[:, 13 + g:14 + g], None, ALU.mult, eng=("pool" if g % 2 else "dve"))
        c["Vt"] = self.sb("Vt", [128, NT, 64], BF16)
        self.dma(c["Vt"][:], V(TMB.h[:, 512:576].rearrange("(n p) c -> p n c", p=128), TMB.name))
        iw = self.sb("iw", [128, NT, 8], F32)
        self.dma(iw[:], V(TMF.h[:, 0:8].rearrange("(n p) c -> p n c", p=128), TMF.name))
        c["absw"] = self.sb("absw", [128, NT, 8], F32)
        self.act(c["absw"][:], iw[:], AF.Abs, scale=1.0 / 16)
        c["sgn"] = self.sb("sgn", [128, NT, 8], F32)
        self.ts(c["sgn"][:], iw[:], 0.0, 2.0, ALU.is_ge, ALU.mult)
        self.ts(c["sgn"][:], c["sgn"][:], -1.0, None, ALU.add)
        c["I"] = self.sb("I", [128, S], F32)
        c["Ssb"] = self.sb("dSsb", [128, S], F32)
        c["kk"] = self.attn_keys("d")
        c["q"] = self.rot("dqi", 2, [128, 4, 128], BF16)
        c["tmp"] = self.rot("tmpr", 2, [128, 512], F32)
        c["st"] = self.rot("dst", 2, [128, 16], F32)
        c["Rk"] = self.rot("Rk", 2, [128, 20], F32)
        c["nm"] = self.rot("dnm", 2, [128, 2], F32)
        c["c2"] = self.rot("dc2", 2, [128, 2], F32)
        c["o"] = self.rot("do", 2, [128, 256], F32)
        c["y"] = self.rot("dy", 2, [128, 256], BF16)
        c["yTs"] = self.rot("dyTs", 2, [128, 256], BF16)
        return c

    def dsa_tile(self, c, i):
        FMS = c["FMS"]
        I = c["I"]
        Ssb = c["Ssb"]
        nk = 128 * (i + 1)
        nkc = (nk + 511) // 512
        q = self.nx(c["q"])
        self.dma(q[:, 0:2, :], self.fm_rows(FMS, 2, 2, i * 128, (i + 1) * 128))
        self.dma(q[:, 2:4, :], self.fm_rows(FMS, 5, 2, i * 128, (i + 1) * 128))
        for kc in range(nkc):
            c0 = kc * 512
            cols = min(512, nk - c0)
            for h in range(8):
                ps = self.next_ps()
                self.mm(ps[:, 0:cols], q[:, 2 + h // 4, :], c["ikm"][:, h % 4, c0:c0 + cols])
                tmp = self.nx(c["tmp"])
                self.act(tmp[:, 0:cols], ps[:, 0:cols], AF.Relu, scale=c["absw"][:, i, h:h + 1])
                if h == 0:
                    self.ts(I[:, c0:c0 + cols], tmp[:, 0:cols], c["sgn"][:, i, 0:1], None, ALU.mult)
                else:
                    self.stt(I[:, c0:c0 + cols], tmp[:, 0:cols], c["sgn"][:, i, h:h + 1], I[:, c0:c0 + cols], ALU.mult, ALU.add)
        if nk > self.n_keep:
            st = self.nx(c["st"])
            junk = self.nx(c["kk"]["P"])
            self.redabs(st[:, 0:1], I[:, 0:nk])
            self.ts(st[:, 0:1], st[:, 0:1], 1e-20, None, ALU.max)
            self.tt(I[:, nk - 128:nk], I[:, nk - 128:nk], self.cneg30[:], ALU.add)
            Rk = self.nx(c["Rk"])
            self.ts(Rk[:], self.rkc[:], st[:, 0:1], None, ALU.mult)
            self.ts(st[:, 1:2], st[:, 0:1], -1.0, None, ALU.mult)
            n1 = (nk // 2 + 127) // 128 * 128
            n2 = nk - n1
            thr_c = self.n_keep - 0.5 - n2 / 2.0
            for k in range(NBIS):
                self.tt(st[:, 2:3], st[:, 1:2], Rk[:, k:k + 1], ALU.add)
                nm = self.nx(c["nm"])
                c2 = self.nx(c["c2"])
                self.ts(nm[:, 0:1], st[:, 2:3], -1.0, None, ALU.mult)
                self.act(Ssb[:, n1:nk], I[:, n1:nk], AF.Sign, bias=nm[:, 0:1], accum=c2[:, 0:1])
                self.ts(junk[:, 0:n1], I[:, 0:n1], st[:, 2:3], None, ALU.is_ge, ALU.add, accum=st[:, 3:4])
                self.stt(st[:, 4:5], c2[:, 0:1], 0.5, st[:, 3:4], ALU.mult, ALU.add)
                self.ts(st[:, 4:5], st[:, 4:5], thr_c, None, ALU.is_ge)
                self.stt(st[:, 1:2], st[:, 4:5], Rk[:, k:k + 1], st[:, 1:2], ALU.mult, ALU.add)
            self.ts(I[:, 0:nk], I[:, 0:nk], st[:, 1:2], 1000.0, ALU.is_ge, ALU.mult)
        else:
            self.ts(I[:, 0:nk], I[:, 0:nk], 0.0, 1000.0, ALU.mult, ALU.add)
            self.tt(I[:, nk - 128:nk], I[:, nk - 128:nk], self.cneg2k[:], ALU.add)
        o = self.nx(c["o"])
        for h in range(4):
            base = 64 * (h % 2)
            cq = h // 2
            for kc in range(nkc):
                c0 = kc * 512
                cols = min(512, nk - c0)
                ps = self.next_ps()
                self.mm(ps[:, 0:cols], q[base:base + 64, cq, :], c["dk"][base:base + 64, c0:c0 + cols])
                self.stt(Ssb[:, c0:c0 + cols], ps[:, 0:cols], 0.125, I[:, c0:c0 + cols], ALU.mult, ALU.add)
            self.softmax_pv(Ssb[:, 0:nk], nk, c["Vt"], 0, o[:, 64 * h:64 * h + 64], c["kk"])
        y = self.nx(c["y"])
        self.copy(y[:], o[:], eng="act")
        self.store_yT(y, c["YT"], 1, i, c["yTs"])

    def nsa_setup(self, FMS, TMB, TMF, cmp_w1, cmp_w2, cmp_pos, YT):
        S = self.S
        NT = S // 128
        NB = self.NB
        NCP = self.NCP
        NC = (S - 32) // 16 + 1
        NCT = NCP // 128
        c = dict(FMS=FMS, YT=YT)
        c["ksT"] = self.sb("ksT", [128, S], BF16)
        self.dma(c["ksT"][:], V(FMS.h[11 * 128:12 * 128, :], FMS.name))
        c["kwT"] = self.sb("kwT", [128, S], BF16)
        self.dma(c["kwT"][:], V(FMS.h[12 * 128:13 * 128, :], FMS.name))
        c["Vs"] = self.sb("Vs", [128, NT, 64], BF16)
        self.dma(c["Vs"][:], V(TMB.h[:, 576:640].rearrange("(n p) c -> p n c", p=128), TMB.name))
        c["Vw"] = self.sb("Vw", [128, NT, 64], BF16)
        self.dma(c["Vw"][:], V(TMB.h[:, 640:704].rearrange("(n p) c -> p n c", p=128), TMB.name))
        c["ngt"] = self.sb("ngt", [128, NT, 12], F32)
        self.dma(c["ngt"][:], V(TMF.h[:, 8:20].rearrange("(n p) c -> p n c", p=128), TMF.name))
        kcmp = self.sb("kcmp", [128, NCP], BF16)
        vcmp = self.sb("vcmp", [128, NCT, 64], BF16)
        c["kcmp"] = kcmp
        c["vcmp"] = vcmp
        c["Ssb"] = self.sb("nSsb", [128, S], F32)
        c["Sw"] = self.sb("Sw", [128, 640], F32)
        c["kk"] = self.attn_keys("n")
        save = self.sb_cur
        srcT = self.sb("srcT", [128, S], BF16)
        w1 = self.sb("w1", [64, 32, 64], BF16)
        w2 = self.sb("w2", [64, 128], BF16)
        posT = self.sb("posT", [64, 32], F32)
        posb = self.sb("posb", [64, 32], BF16)
        cst = self.sb("cst", [64, 1], F32)
        u = self.sb("u", [64, NCP], F32)
        u2 = self.sb("u2", [64, NCP], F32)
        gl = self.sb("gl", [64, NCP], BF16)
        for i in range(2):
            self.dma(srcT[:], V(FMS.h[(10 + 3 * i) * 128:(11 + 3 * i) * 128, :], FMS.name))
            self.dma(w1[:], V(cmp_w1.h[i].rearrange("(l d) f -> d l f", d=64), cmp_w1.name), eng="pool")
            self.dma(w2[:, 0:64], V(cmp_w2.h[i], cmp_w2.name), eng="pool")
            self.dma(w2[:, 64:128], V(cmp_w2.h[i], cmp_w2.name), eng="pool")
            self.dma_s(posT[:], V(cmp_pos.h[i].rearrange("l d -> d l"), cmp_pos.name))
            self.copy(posb[:], posT[:])
            psc = self.next_ps()
            for l in range(32):
                self.mm(psc[0:64, 0:1], w1[:, l, :], posb[:, l:l + 1], start=(l == 0), stop=(l == 31))
            self.copy(cst[:], psc[0:64, 0:1])
            psh = self.next_ps()
            for l in range(32):
                self.mm(psh[0:64, 0:NC], w1[:, l, :], srcT[0:64, l:l + 16 * (NC - 1) + 1:16], start=(l == 0), stop=(l == 31))
            self.memset(u[:], 0.0)
            self.act(u[:, 0:NC], psh[0:64, 0:NC], AF.Identity, bias=cst[:, 0:1])
            self.tt(u2[:], u[:], u[:], ALU.mult)
            self.tt(u2[:], u2[:], u[:], ALU.mult)
            self.stt(u2[:], u2[:], 0.044715, u[:], ALU.mult, ALU.add)
            self.act(u2[:], u2[:], AF.Tanh, scale=0.7978845608028654)
            self.ts(u2[:], u2[:], 1.0, 0.5, ALU.add, ALU.mult)
            self.tt(gl[:], u2[:], u[:], ALU.mult)
            if i == 0:
                pso = self.next_ps()
                self.mm(pso[:, 0:NCP], w2[:, :], gl[:, :])
                self.copy(kcmp[:], pso[:, 0:NCP])
            else:
                for ct in range(NCT):
                    pso = self.next_ps()
                    self.mm(pso[:, 0:64], gl[:, ct * 128:(ct + 1) * 128], w2[:, 0:64])
                    self.copy(vcmp[:, ct, :], pso[:, 0:64])
        self.sb_cur = save
        self.P.barrier()
        c["q"] = self.rot("nqi", 2, [128, 2, 128], BF16)
        for nm, shp, dt_ in (("vis", [128, NCP], F32), ("pns", [128, NCP], F32), ("pn", [128, NCP], F32), ("Sc", [128, NCP], F32),
                             ("Pc", [128, NCP], F32), ("pnb", [128, NCP], BF16), ("PTc", [128, NCP], BF16), ("pnT", [128, NCP], F32),
                             ("cst2", [128, 8], F32), ("am", [128, NB], F32), ("imp", [128, NB], F32), ("imp2", [128, NB], F32), ("m8", [128, 16], F32),
                             ("selm", [128, NB], F32), ("oc", [128, 256], F32), ("os", [128, 256], F32), ("ow", [128, 256], F32),
                             ("gs", [128, 12], F32), ("o", [128, 256], F32), ("y", [128, 256], BF16), ("yTs", [128, 256], BF16)):
            c[nm] = self.rot("n" + nm, 2, shp, dt_)
        return c

    def nsa_tile(self, c, i):
        NB = self.NB
        NCP = self.NCP
        NCT = NCP // 128
        FMS = c["FMS"]
        Ssb = c["Ssb"]
        Sw = c["Sw"]
        kcmp = c["kcmp"]
        vcmp = c["vcmp"]

        def h4(t_):
            return V(t_.h[:, 0:256].rearrange("p (h e) -> p h e", h=4), t_.name)

        nk = 128 * (i + 1)
        nkc = (nk + 511) // 512
        nq = self.nx(c["q"])
        self.dma(nq[:], self.fm_rows(FMS, 8, 2, i * 128, (i + 1) * 128))
        vis = self.nx(c["vis"])
        self.memset(vis[:], 0.0)
        self.aselect(vis[:], vis[:], [[-16, NCP]], ALU.is_ge, -1000.0, 128 * i - 31, 1)
        pns = self.nx(c["pns"])
        oc = self.nx(c["oc"])
        osl = self.nx(c["os"])
        ow = self.nx(c["ow"])
        for h in range(4):
            base = 64 * (h % 2)
            cq = h // 2
            ps = self.next_ps()
            self.mm(ps[:, 0:NCP], nq[base:base + 64, cq, :], kcmp[base:base + 64, :])
            Sc = self.nx(c["Sc"])
            self.stt(Sc[:], ps[:, 0:NCP], 0.125, vis[:], ALU.mult, ALU.add)
            st = self.nx(c["cst2"])
            self.red(st[:, 0:1], Sc[:], ALU.max)
            self.ts(st[:, 0:1], st[:, 0:1], -500.0, -1.0, ALU.max, ALU.mult)
            Pc = self.nx(c["Pc"])
            self.act(Pc[:], Sc[:], AF.Exp, bias=st[:, 0:1], accum=st[:, 1:2])
            self.ts(st[:, 2:3], st[:, 1:2], 1e-30, None, ALU.max)
            self.recip(st[:, 3:4], st[:, 2:3])
            pn = pns if h == 0 else self.nx(c["pn"])
            self.ts(pn[:], Pc[:], st[:, 3:4], None, ALU.mult)
            pnb = self.nx(c["pnb"])
            self.copy(pnb[:], pn[:], eng="act")
            if h > 0:
                self.tt(pns[:], pns[:], pn[:], ALU.add, eng="pool")
            pb = self.next_psb()
            for ct in range(NCT):
                self.tr(pb[:, ct * 128:(ct + 1) * 128], pnb[:, ct * 128:(ct + 1) * 128], self.ident[:])
            PTc = self.nx(c["PTc"])
            self.copy(PTc[:], pb[:, 0:NCP], eng="act")
            po = self.next_ps()
            for ct in range(NCT):
                self.mm(po[:, 0:64], PTc[:, ct * 128:(ct + 1) * 128], vcmp[:, ct, :], start=(ct == 0), stop=(ct == NCT - 1))
            self.copy(oc[:, 64 * h:64 * h + 64], po[:, 0:64], eng="act")
        selm = self.nx(c["selm"])
        if NB > 16:
            pf = self.next_ps()
            for ct in range(NCT):
                self.tr(pf[:, ct * 128:(ct + 1) * 128], pns[:, ct * 128:(ct + 1) * 128], self.ident_f[:])
            pnT = self.nx(c["pnT"])
            self.copy(pnT[:], pf[:, 0:NCP], eng="act")
            pi = self.next_ps()
            for ct in range(NCT):
                self.mm(pi[:, 0:NB], pnT[:, ct * 128:(ct + 1) * 128], self.ovl[:, ct, :], start=(ct == 0), stop=(ct == NCT - 1))
            am = self.nx(c["am"])
            self.memset(am[:], 0.0)
            for half in range(2):
                cur = 2 * i + half
                r0 = 64 * half
                v_ = am[r0:r0 + 64, :]
                self.aselect(v_, v_, [[-1, NB]], ALU.is_ge, -1e30, cur, 0)
                self.memset(am[r0:r0 + 64, 0:1], 1e30)
                self.memset(am[r0:r0 + 64, cur:cur + 1], 1e30)
                if cur >= 1:
                    self.memset(am[r0:r0 + 64, cur - 1:cur], 1e30)
            imp = self.nx(c["imp"])
            self.tt(imp[:], pi[:, 0:NB], am[:], ALU.add)
            m8 = self.nx(c["m8"])
            self.vmax(m8[:, 0:8], imp[:])
            imp2 = self.nx(c["imp2"])
            self.match_replace(imp2[:], m8[:, 0:8], imp[:], -3.0e38)
            self.vmax(m8[:, 8:16], imp2[:])
            self.ts(selm[:], imp[:], m8[:, 15:16], 1000.0, ALU.is_ge, ALU.mult)
        else:
            self.memset(selm[:], 1000.0)
        for h in range(4):
            base = 64 * (h % 2)
            cq = h // 2
            for kc in range(nkc):
                c0 = kc * 512
                cols = min(512, nk - c0)
                nb_ = cols // 64
                ps = self.next_ps()
                self.mm(ps[:, 0:cols], nq[base:base + 64, cq, :], c["ksT"][base:base + 64, c0:c0 + cols])
                self.stt(V(Ssb.h[:, c0:c0 + cols].rearrange("p (b e) -> p b e", e=64), Ssb.name),
                         V(ps.h[:, 0:cols].rearrange("p (b e) -> p b e", e=64), ps.name), 0.125,
                         V(selm.h[:, c0 // 64:c0 // 64 + nb_].unsqueeze(2).to_broadcast([128, nb_, 64]), selm.name),
                         ALU.mult, ALU.add)
            self.tt(Ssb[:, nk - 128:nk], Ssb[:, nk - 128:nk], self.cneg2k[:], ALU.add)
            self.softmax_pv(Ssb[:, 0:nk], nk, c["Vs"], 0, osl[:, 64 * h:64 * h + 64], c["kk"])
        k0 = max(0, i * 128 - 512)
        nkw = nk - k0
        boff = 640 - nkw
        for h in range(4):
            base = 64 * (h % 2)
            cq = h // 2
            for c0 in range(0, nkw, 512):
                cols = min(512, nkw - c0)
                ps = self.next_ps()
                self.mm(ps[:, 0:cols], nq[base:base + 64, cq, :], c["kwT"][base:base + 64, k0 + c0:k0 + c0 + cols])
                self.stt(Sw[:, c0:c0 + cols], ps[:, 0:cols], 0.125, self.band[:, boff + c0:boff + c0 + cols], ALU.mult, ALU.add)
            self.softmax_pv(Sw[:, 0:nkw], nkw, c["Vw"], k0 // 128, ow[:, 64 * h:64 * h + 64], c["kk"])
        gs = self.nx(c["gs"])
        self.act(gs[:], c["ngt"][:, i, :], AF.Sigmoid)
        o = self.nx(c["o"])

        def gbc(j):
            return V(gs.h[:, j:12:3].unsqueeze(2).to_broadcast([128, 4, 64]), gs.name)
        self.tt(h4(o), h4(oc), gbc(0), ALU.mult)
        self.tt(h4(osl), h4(osl), gbc(1), ALU.mult)
        self.tt(o[:], o[:], osl[:], ALU.add)
        self.tt(h4(ow), h4(ow), gbc(2), ALU.mult)
        self.tt(o[:], o[:], ow[:], ALU.add)
        y = self.nx(c["y"])
        self.copy(y[:], o[:], eng="act")
        self.store_yT(y, c["YT"], 2, i, c["yTs"])

    def phase_dsa_nsa(self, FMS, TMB, TMF, cmp_w1, cmp_w2, cmp_pos, YT):
        S = self.S
        NT = S // 128
        self.phase_begin()
        cd = self.dsa_setup(FMS, TMB, TMF, YT)
        cn = self.nsa_setup(FMS, TMB, TMF, cmp_w1, cmp_w2, cmp_pos, YT)
        P = self.P
        for i in range(NT):
            self.ps_set = (0, 3)
            self.psb_set = (0, 1)
            P.capture = []
            self.dsa_tile(cd, i)
            A = P.capture
            self.ps_set = (3, 2)
            self.psb_set = (1, 1)
            P.capture = []
            self.nsa_tile(cn, i)
            B = P.capture
            P.capture = None
            self.ps_set = (0, 5)
            self.psb_set = (0, 2)
            ia = ib = 0
            na, nb = len(A), len(B)
            while ia < na or ib < nb:
                if ib >= nb or (ia < na and ia * nb <= ib * na):
                    P.add(*A[ia][0], **A[ia][1])
                    ia += 1
                else:
                    P.add(*B[ib][0], **B[ib][1])
                    ib += 1

    def phase_merge(self, x_in, xT_in, w_in_l, w_branch, w_out, ln_g, ln_b, YT, x_out, xT_out):
        S = self.S
        self.phase_begin()
        wg = self.load_w("wg", lambda k: V(w_in_l.h[k * 128:(k + 1) * 128, 3128:7224], w_in_l.name), NKC, 4096)
        wb = self.sb("wb", [128, 4, 2, 1024], BF16)
        for n in range(4):
            for kk_ in range(2):
                self.dma(wb[:, n, kk_, :], V(w_branch.h[n, kk_ * 128:(kk_ + 1) * 128, :], w_branch.name), eng="pool")
        wo = self.load_w("wo", lambda k: V(w_out.h[k * 128:(k + 1) * 128, :], w_out.name), NKC, D)
        g_bc, b_bc, scr = self.ln_setup(ln_g, ln_b)
        xt = self.sb("xT", [128, NKC, 512], BF16)
        yt = self.sb("yT", [128, 8, 512], BF16)
        mT = self.sb("mT", [128, 8, 512], BF16)
        acc_k = self.rot("acc", 2, [128, 512], F32)
        sg_k = self.rot("sg", 2, [128, 512], F32)
        tmp_k = self.rot("tmp", 2, [128, 512], F32)
        xr = [self.sb("xr%d" % i, [128, D], F32) for i in range(2)]
        rqs = [self.sb("r%d" % i, [128, D], F32) for i in range(2)]
        ntile = S // 512

        def load_xt(t_):
            ss_ = slice(t_ * 512, (t_ + 1) * 512)
            self.dma(xt[:], V(xT_in.h.rearrange("(k p) s -> p k s", p=128)[:, :, ss_], xT_in.name))
            self.dma(yt[:], V(YT.h[:, ss_].rearrange("(c p) s -> p c s", p=128), YT.name))

        def load_xq(idx):
            self.dma(xr[idx % 2][:], x_in[idx * 128:(idx + 1) * 128, :])
        load_xt(0)
        load_xq(0)
        for t in range(ntile):
            ss = slice(t * 512, (t + 1) * 512)
            for dc in range(8):
                acc = self.nx(acc_k)
                for n in range(4):
                    pg = self.next_ps()
                    for k in range(NKC):
                        self.mm(pg[:], wg[:, k, n * 1024 + dc * 128:n * 1024 + (dc + 1) * 128], xt[:, k, :],
                                start=(k == 0), stop=(k == NKC - 1))
                    pp = self.next_ps()
                    for k2 in range(2):
                        self.mm(pp[:], wb[:, n, k2, dc * 128:(dc + 1) * 128], yt[:, 2 * n + k2, :], start=(k2 == 0), stop=(k2 == 1))
                    sg = self.nx(sg_k)
                    self.act(sg[:], pg[:], AF.Sigmoid)
                    if n == 0:
                        self.tt(acc[:], sg[:], pp[:], ALU.mult)
                    else:
                        tmp = self.nx(tmp_k)
                        self.tt(tmp[:], sg[:], pp[:], ALU.mult)
                        self.tt(acc[:], acc[:], tmp[:], ALU.add, eng="pool")
                self.copy(mT[:, dc, :], acc[:], eng="act")
            if t + 1 < ntile:
                load_xt(t + 1)
            for q in range(4):
                t0 = t * 512 + q * 128
                xq = xr[q % 2]
                rq = rqs[q % 2]
                if t * 4 + q + 1 < ntile * 4:
                    load_xq(t * 4 + q + 1)
                for half in range(2):
                    hs = slice(half * 512, (half + 1) * 512)
                    pd = self.next_ps()
                    for dc in range(8):
                        self.mm(pd[:], mT[:, dc, q * 128:(q + 1) * 128], wo[:, dc, hs], start=(dc == 0), stop=(dc == 7))
                    self.act(xq[:, hs], xq[:, hs], AF.Copy, scale=ALPHA)
                    self.stt(rq[:, hs], pd[:], 1.0, xq[:, hs], ALU.mult, ALU.add)
                self.finish_tile(rq, g_bc, b_bc, x_out, xT_out, t0, scr)

    def phase_xattn(self, x_in, xT_in, mem, wq_d, wkv_d, wo_d, ln_g, ln_b, x_out, xT_out):
        S = self.S
        self.phase_begin()
        wq = self.load_w("wq", lambda k: V(wq_d.h[k * 128:(k + 1) * 128, :], wq_d.name), NKC, D)
        wkv = self.load_w("wkv", lambda k: V(wkv_d.h[k * 128:(k + 1) * 128, :], wkv_d.name), NKC, 2 * D)
        wo = self.load_w("wo", lambda k: V(wo_d.h[k * 128:(k + 1) * 128, :], wo_d.name), NKC, D)
        g_bc, b_bc, scr = self.ln_setup(ln_g, ln_b)
        memT = self.sb("memT", [128, 8, 256], BF16)
        mr = self.sb("mr", [128, D], F32)
        mb = self.sb("mb", [128, D], BF16)
        for mt in range(2):
            self.dma(mr[:], mem[mt * 128:(mt + 1) * 128, :])
            self.copy(mb[:], mr[:], eng="act")
            pb = self.next_psb()
            for k in range(8):
                self.tr(pb[:, k * 128:(k + 1) * 128], mb[:, k * 128:(k + 1) * 128], self.ident[:])
            self.copy(memT[:, :, mt * 128:(mt + 1) * 128], V(pb.h[:, :].rearrange("p (k t) -> p k t", k=8), pb.name))
        KT = self.sb("KT", [128, 8, 256], BF16)
        for c in range(8):
            ps = self.next_ps()
            for k in range(NKC):
                self.mm(ps[:, 0:256], wkv[:, k, c * 128:(c + 1) * 128], memT[:, k, :], start=(k == 0), stop=(k == NKC - 1))
            self.copy(KT[:, c, :], ps[:, 0:256], eng=("act" if c % 2 else "dve"))
        Vm = self.sb("Vm", [128, 2, D], BF16)
        for mt in range(2):
            for half in range(2):
                ps = self.next_ps()
                for k in range(NKC):
                    self.mm(ps[:], memT[:, k, mt * 128:(mt + 1) * 128], wkv[:, k, D + half * 512:D + (half + 1) * 512],
                            start=(k == 0), stop=(k == NKC - 1))
                self.copy(Vm[:, mt, half * 512:(half + 1) * 512], ps[:], eng=("act" if half else "dve"))
        xt = self.sb("xT", [128, NKC, 512], BF16)
        qT = self.sb("qT", [128, 8, 512], BF16)
        Pf_k = self.rot("Pf", 2, [128, 4, 256], F32)
        Pb_k = self.rot("Pb", 2, [128, 4, 256], BF16)
        PT_k = self.rot("PTx", 2, [128, 8, 128], BF16)
        oT_k = self.rot("oT", 2, [128, 8, 128], BF16)
        st_k = self.rot("xst", 2, [128, 16], F32)
        xr = [self.sb("xr%d" % i, [128, D], F32) for i in range(2)]
        rqs = [self.sb("r%d" % i, [128, D], F32) for i in range(2)]
        SC = 1.0 / 16
        ntile = S // 512

        def load_xt(t_):
            ss_ = slice(t_ * 512, (t_ + 1) * 512)
            self.dma(xt[:], V(xT_in.h.rearrange("(k p) s -> p k s", p=128)[:, :, ss_], xT_in.name))

        def load_xq(idx):
            self.dma(xr[idx % 2][:], x_in[idx * 128:(idx + 1) * 128, :])
        load_xt(0)
        load_xq(0)
        for t in range(ntile):
            ss = slice(t * 512, (t + 1) * 512)
            for c in range(8):
                ps = self.next_ps()
                for k in range(NKC):
                    self.mm(ps[:], wq[:, k, c * 128:(c + 1) * 128], xt[:, k, :], start=(k == 0), stop=(k == NKC - 1))
                self.copy(qT[:, c, :], ps[:], eng=("act" if c % 2 else "dve"))
            if t + 1 < ntile:
                load_xt(t + 1)
            for q in range(4):
                t0 = t * 512 + q * 128
                tq = slice(q * 128, (q + 1) * 128)
                if t * 4 + q + 1 < ntile * 4:
                    load_xq(t * 4 + q + 1)
                pss = [self.next_ps(), self.next_ps()]
                st = self.nx(st_k)
                Pf = self.nx(Pf_k)
                for h in range(4):
                    pv = pss[h // 2][:, (h % 2) * 256:(h % 2) * 256 + 256]
                    for cc in range(2):
                        self.mm(pv, qT[:, 2 * h + cc, tq], KT[:, 2 * h + cc, :], start=(cc == 0), stop=(cc == 1))
                    self.red(st[:, h:h + 1], pv, ALU.max)
                    self.ts(st[:, 4 + h:5 + h], st[:, h:h + 1], -SC, None, ALU.mult)
                    self.act(Pf[:, h, :], pv, AF.Exp, bias=st[:, 4 + h:5 + h], scale=SC, accum=st[:, 8 + h:9 + h])
                self.recip(st[:, 12:16], st[:, 8:12])
                Pb = self.nx(Pb_k)
                self.tt(Pb[:], Pf[:], V(st.h[:, 12:16].unsqueeze(2).to_broadcast([128, 4, 256]), st.name), ALU.mult)
                pb = self.next_psb()
                for h in range(4):
                    for mc in range(2):
                        j = 2 * h + mc
                        self.tr(pb[:, j * 128:(j + 1) * 128], Pb[:, h, mc * 128:(mc + 1) * 128], self.ident[:])
                PT = self.nx(PT_k)
                self.copy(PT[:], V(pb.h[:, :].rearrange("p (j t) -> p j t", j=8), pb.name))
                oT = self.nx(oT_k)
                pso = [self.next_ps(), self.next_ps()]
                for h in range(4):
                    for dc in range(2):
                        j = 2 * h + dc
                        pv = pso[j // 4][:, (j % 4) * 128:(j % 4) * 128 + 128]
                        for mc in range(2):
                            self.mm(pv, Vm[:, mc, h * 256 + dc * 128:h * 256 + (dc + 1) * 128], PT[:, 2 * h + mc, :],
                                    start=(mc == 0), stop=(mc == 1))
                for j4 in range(2):
                    self.copy(oT[:, 4 * j4:4 * j4 + 4, :], V(pso[j4].h[:, :].rearrange("p (j t) -> p j t", j=4), pso[j4].name),
                              eng=("act" if j4 else "dve"))
                xq = xr[q % 2]
                rq = rqs[q % 2]
                for half in range(2):
                    hs = slice(half * 512, (half + 1) * 512)
                    pd = self.next_ps()
                    for c in range(8):
                        self.mm(pd[:], oT[:, c, :], wo[:, c, hs], start=(c == 0), stop=(c == 7))
                    self.act(xq[:, hs], xq[:, hs], AF.Copy, scale=ALPHA)
                    self.stt(rq[:, hs], pd[:], 1.0, xq[:, hs], ALU.mult, ALU.add)
                self.finish_tile(rq, g_bc, b_bc, x_out, xT_out, t0, scr)


OFF = dict(r_q=0, r_k=128, r_v=256, r_g=512, d_q=768, d_k=1024, d_v=1088, i_q=1152, i_k=1408, i_w=1440,
           n_q=1448, n_kc=1704, n_vc=1768, n_ks=1832, n_vs=1896, n_kw=1960, n_vw=2024, n_g=2088,
           s_z=2100, s_xbc=2356, s_dt=3124, br_g=3128)


def _partner(i, headdim, rot):
    half = rot // 2
    j = i % headdim
    b = i - j
    if j < half:
        return b + j + half
    if j < rot:
        return b + j - half
    return i


def build_colidx():
    cols = []

    def roped(name, width, headdim, rot, lo=0, rep=1):
        loc = []
        for r in range(rep):
            loc += list(range(lo, lo + width))
        assert len(loc) == 128
        a = [OFF[name] + i for i in loc]
        b = [OFF[name] + _partner(i, headdim, rot) for i in loc]
        cols.extend(a)
        cols.extend(b)

    roped("r_q", 128, 32, 32)
    roped("r_k", 128, 32, 32)
    roped("d_q", 128, 64, 16, 0)
    roped("d_q", 128, 64, 16, 128)
    roped("d_k", 64, 64, 16, 0, 2)
    roped("i_q", 128, 32, 8, 0)
    roped("i_q", 128, 32, 8, 128)
    roped("i_k", 32, 32, 8, 0, 4)
    roped("n_q", 128, 64, 16, 0)
    roped("n_q", 128, 64, 16, 128)
    roped("n_kc", 64, 64, 16, 0, 2)
    roped("n_ks", 64, 64, 16, 0, 2)
    roped("n_kw", 64, 64, 16, 0, 2)
    cols.extend([OFF["n_vc"] + i for i in range(64)] * 2)
    cols.extend([OFF["s_xbc"] + i for i in range(768)])
    for name, w in (("r_v", 256), ("r_g", 256), ("d_v", 64), ("n_vs", 64), ("n_vw", 64), ("s_z", 256),
                    ("i_w", 8), ("n_g", 12), ("s_dt", 4)):
        cols.extend([OFF[name] + i for i in range(w)])
    return np.asarray(cols, dtype=np.int64)


ROPED_TABLES = [0, 1, 2, 2, 2, 3, 3, 3, 2, 2, 2, 2, 2]
TM0 = (2 * len(ROPED_TABLES) + 7) * 128
NCOL2 = TM0 + 984
NBIS = 18
RET_LNG = [math.log1p(-2.0 ** (-5 - h)) for h in range(4)]


def host_consts(S):
    meta = np.zeros((128, 32), np.float32)

    def fill(t, headdim, rot, theta, scale):
        half = rot // 2
        inv = np.power(np.float32(theta), (-2.0 * np.arange(half, dtype=np.float32) / np.float32(rot)).astype(np.float32)).astype(np.float32)
        for p in range(128):
            i = p % headdim
            if i < rot:
                meta[p, t] = inv[i % half]
                meta[p, 4 + t] = scale
                meta[p, 8 + t] = -scale if i < half else scale
            else:
                meta[p, t] = 0.0
                meta[p, 4 + t] = 1.0
                meta[p, 8 + t] = 0.0

    fill(0, 32, 32, 10000.0, 1.0)
    fill(1, 32, 32, 10000.0, 32.0 ** -0.5)
    fill(2, 64, 16, 500000.0, 1.0)
    fill(3, 32, 8, 500000.0, 1.0)
    for p in range(128):
        meta[p, 12] = RET_LNG[p // 32]
        meta[p, 13 + p // 32] = 1.0
    bdm = np.zeros((128, 256), np.float32)
    for p in range(128):
        bdm[p, 64 * (p // 32):64 * (p // 32) + 64] = 1.0
    NC = (S - 32) // 16 + 1
    NCP = (NC + 127) // 128 * 128
    NB = S // 64
    ovl = np.zeros((NCP, NB), np.float32)
    for c in range(NC):
        for j in range(NB):
            ovl[c, j] = max(min(16 * c + 32, 64 * j + 64) - max(16 * c, 64 * j), 0) / 32.0
    return meta, bdm, ovl, NB, NCP


STAGES = ["ffn1", "inproj", "ret", "ssd", "dsa", "nsa", "merge", "xattn", "ffn2"]


def build(S, depth=DEPTH, stop_after=None):
    kb = KB(S, depth, stop_after)
    meta_np, bdm_np, ovl_np, NB, NCP = host_consts(S)
    kb.n_keep = min(256, S // 4)
    EI = "ExternalInput"
    x = kb.dram("x", [S, D], F32, kind=EI)
    mem = kb.dram("mem", [N_MEM, D], F32, kind=EI)
    ln_g = kb.dram("ln_g", [DEPTH, 4, D], F32, kind=EI)
    ln_b = kb.dram("ln_b", [DEPTH, 4, D], F32, kind=EI)
    f1gu = kb.dram("ffn1_w_gu", [DEPTH, D, 2 * DFF], F32, kind=EI)
    f1dn = kb.dram("ffn1_w_down", [DEPTH, DFF, D], F32, kind=EI)
    w_in = kb.dram("w_in", [DEPTH, D, 7224], F32, kind=EI)
    w2 = kb.dram("w2", [DEPTH, D, NCOL2], F32, kind=EI)
    cmp_w1 = kb.dram("cmp_w1", [DEPTH, 2, 2048, 64], F32, kind=EI)
    cmp_w2 = kb.dram("cmp_w2", [DEPTH, 2, 64, 64], F32, kind=EI)
    cmp_pos = kb.dram("cmp_pos", [DEPTH, 2, 32, 64], F32, kind=EI)
    conv_w = kb.dram("conv_w", [DEPTH, 4, 768], F32, kind=EI)
    conv_b = kb.dram("conv_b", [DEPTH, 768], F32, kind=EI)
    dt_bias = kb.dram("dt_bias", [DEPTH, 4], F32, kind=EI)
    a_log = kb.dram("a_log", [DEPTH, 4], F32, kind=EI)
    d_skip = kb.dram("d_skip", [DEPTH, 4], F32, kind=EI)
    norm_g = kb.dram("ssm_norm_g", [DEPTH, 256], F32, kind=EI)
    w_branch = kb.dram("w_branch", [DEPTH, 4, 256, D], F32, kind=EI)
    w_out = kb.dram("w_out", [DEPTH, D, D], F32, kind=EI)
    xwq = kb.dram("xattn_wq", [DEPTH, D, D], F32, kind=EI)
    xwkv = kb.dram("xattn_wkv", [DEPTH, D, 2 * D], F32, kind=EI)
    xwo = kb.dram("xattn_wo", [DEPTH, D, D], F32, kind=EI)
    f2gu = kb.dram("ffn2_w_gu", [DEPTH, D, 2 * DFF], F32, kind=EI)
    f2dn = kb.dram("ffn2_w_down", [DEPTH, DFF, D], F32, kind=EI)
    meta = kb.dram("meta", [128, 32], F32, kind=EI)
    bdm = kb.dram("bdm", [128, 256], F32, kind=EI)
    ovl = kb.dram("ovl", [NCP, NB], F32, kind=EI)
    out = kb.dram("out", [S, D], F32)
    xTa = kb.dram("xTa", [D, S], BF16)
    xTb = kb.dram("xTb", [D, S], BF16)
    xa = kb.dram("xa", [S, D], F32)
    xb2 = kb.dram("xb2", [S, D], F32)
    FMS = kb.dram("FMS", [20 * 128, S], BF16)
    TMB = kb.dram("TMB", [S, 960], BF16)
    TMF = kb.dram("TMF", [S, 24], F32)
    YT = kb.dram("YT", [1024, S], BF16)
    ROPE = kb.dram("ROPE", [4, 2, 128, S], F32)
    kb.setup()
    kb.setup_consts(meta, bdm, ovl, NB, NCP)
    kb.phase_rope(ROPE)
    kb.phase_transpose_in(x, xTa)

    def L(t, *idx):
        return T(t.h[idx], t.name, True)

    done = False
    xin = x
    for l in range(depth):
        last = (l == depth - 1)

        def stop(name):
            return stop_after == (l, name)
        kb.phase_ffn(xin, xTa, L(f1gu, l), L(f1dn, l), L(ln_g, l, 0), L(ln_b, l, 0), xa, xTb)
        if stop("ffn1"):
            break
        kb.phase_inproj(xTb, L(w2, l), ROPE, FMS, TMB, TMF)
        if stop("inproj"):
            break
        kb.phase_ret(FMS, TMB, YT)
        if stop("ret"):
            break
        kb.phase_ssd(FMS, TMB, TMF, L(conv_w, l), L(conv_b, l), L(dt_bias, l), L(a_log, l), L(d_skip, l), L(norm_g, l), YT)
        if stop("ssd"):
            break
        kb.phase_dsa_nsa(FMS, TMB, TMF, L(cmp_w1, l), L(cmp_w2, l), L(cmp_pos, l), YT)
        if stop("nsa") or stop("dsa"):
            break
        kb.phase_merge(xa, xTb, L(w_in, l), L(w_branch, l), L(w_out, l), L(ln_g, l, 1), L(ln_b, l, 1), YT, xb2, xTa)
        if stop("merge"):
            break
        kb.phase_xattn(xb2, xTa, mem, L(xwq, l), L(xwkv, l), L(xwo, l), L(ln_g, l, 2), L(ln_b, l, 2), xa, xTb)
        if stop("xattn"):
            break
        kb.phase_ffn(xa, xTb, L(f2gu, l), L(f2dn, l), L(ln_g, l, 3), L(ln_b, l, 3), out if last else xb2, None if last else xTa)
        if stop("ffn2"):
            break
        xin = xb2
    kb.flush_pending()
    st = kb.P.emit()
    kb.stats = st
    return kb


def make_in_maps(inputs, S, ncores):
    meta_np, bdm_np, ovl_np, NB, NCP = host_consts(S)
    colidx = build_colidx()
    w_in = np.asarray(inputs["w_in"], dtype=np.float32)
    w2 = np.ascontiguousarray(w_in[:, :, colidx])
    shared = {k: np.ascontiguousarray(np.asarray(v, dtype=np.float32)) for k, v in inputs.items() if k not in ("x", "mem")}
    shared["w2"] = w2
    shared["meta"] = meta_np
    shared["bdm"] = bdm_np
    shared["ovl"] = ovl_np
    maps = []
    for b in range(ncores):
        m = dict(shared)
        m["x"] = np.ascontiguousarray(np.asarray(inputs["x"][b, :S], dtype=np.float32))
        m["mem"] = np.ascontiguousarray(np.asarray(inputs["mem"][b], dtype=np.float32))
        maps.append(m)
    return maps


def kernel(**inputs):
    S = inputs["x"].shape[1]
    B = inputs["x"].shape[0]
    kb = build(S)
    maps = make_in_maps(inputs, S, B)
    res = run_bass_kernel_spmd(kb.nc, maps, core_ids=list(range(B)))
    out = np.stack([np.asarray(r["out"], dtype=np.float32) for r in res.results], axis=0)
    return out
```

```python
import math
import sys
import numpy as np
import concourse.bass as bass
import concourse.mybir as mybir
from concourse.bass_utils import run_bass_kernel_spmd

F32 = mybir.dt.float32
BF16 = mybir.dt.bfloat16
I32 = mybir.dt.int32
AF = mybir.ActivationFunctionType
ALU = mybir.AluOpType
AX = mybir.AxisListType

SEM_LIMIT = 30000
N_DMA_SEMS = 24


class Buf:
    __slots__ = ("name", "last_w", "readers")

    def __init__(self, name):
        self.name = name
        self.last_w = None
        self.readers = []


class Op:
    __slots__ = ("eng", "fn", "deps", "need_inc", "sem", "val", "is_dma", "idx", "tag", "odeps", "n", "seg", "pfirst", "fin", "st0", "crit")


class Prog:
    def __init__(self, nc):
        self.nc = nc
        self.engs = {"pe": nc.tensor, "act": nc.scalar, "dve": nc.vector, "pool": nc.gpsimd, "sp": nc.sync}
        self.ops = []
        self.bufs = {}
        self.last_on = {}
        self.dmas_since = []
        self.phase_deps = []
        self.phase_bufs = set()
        self.capture = None
        self.seg = 0
        self.do_sched = True
        self.est_time = 0.0

    def buf(self, name):
        b = self.bufs.get(name)
        if b is None:
            b = self.bufs[name] = Buf(name)
        return b

    def add(self, eng, fn, reads=(), writes=(), dma=False, extra_deps=(), n=64):
        if self.capture is not None:
            self.capture.append(((eng, fn), dict(reads=list(reads), writes=list(writes), dma=dma, n=n)))
            return None
        op = Op()
        op.eng = eng
        op.fn = fn
        op.is_dma = dma
        op.need_inc = False
        op.sem = None
        op.val = 0
        op.n = n
        op.seg = self.seg
        op.pfirst = False
        op.fin = 0.0
        op.idx = len(self.ops)
        try:
            op.tag = (sys._getframe(2).f_lineno, 0)
        except Exception:
            op.tag = (0, 0)
        deps = {}
        for b in reads:
            b = self.buf(b)
            w = b.last_w
            if w is not None:
                deps[w.idx] = (w, "raw")
        for b in writes:
            b = self.buf(b)
            w = b.last_w
            if w is not None and w.idx not in deps:
                deps[w.idx] = (w, "waw")
            for r in b.readers:
                if r.idx not in deps:
                    deps[r.idx] = (r, "war")
        real = []
        order = []
        for d, kind in deps.values():
            if (not d.is_dma) and d.eng == eng and not dma:
                if eng == "pe" or kind != "raw":
                    order.append(d)
                    continue
            real.append(d)
        for d in extra_deps:
            real.append(d)
        for b in list(reads) + list(writes):
            if b not in self.phase_bufs:
                self.phase_bufs.add(b)
                op.pfirst = True
        op.deps = real
        op.odeps = order
        for b in writes:
            b = self.buf(b)
            b.last_w = op
            b.readers = []
        for b in reads:
            self.buf(b).readers.append(op)
        self.ops.append(op)
        return op

    def barrier(self):
        self.seg += 1
        self.phase_bufs = set()

    def _cost(self, op):
        n = op.n
        e = op.eng
        if op.is_dma:
            return 0.08, 2.0 + n / 100e3
        if e == "pe":
            c = 0.035 + n / 2400.0
        elif e == "act":
            c = 0.22 + n / 1200.0
        elif e == "dve":
            c = 0.08 + n / 960.0
        elif e == "pool":
            c = 0.15 + n / 500.0
        else:
            c = 0.05
        return c, c

    def schedule(self):
        import heapq
        SCHED = self.do_sched
        self.seg_stats = []
        order = []
        ops = self.ops
        nseg = self.seg + 1
        segs = [[] for _ in range(nseg)]
        for op in ops:
            segs[op.seg].append(op)
        t_base = 0.0
        engs = list(self.engs.keys())
        for sg in segs:
            if not sg:
                continue
            if not SCHED:
                order.extend(sg)
                continue
            inseg = set(id(o) for o in sg)
            indeg = {}
            succ = {}
            dr = {}
            for op in sg:
                cnt = 0
                for d in op.deps + op.odeps:
                    if id(d) in inseg:
                        cnt += 1
                        succ.setdefault(id(d), []).append(op)
                indeg[id(op)] = cnt
                dr[id(op)] = t_base
            wait_h = {e: [] for e in engs}
            rdy_h = {e: [] for e in engs}
            free = {e: t_base for e in engs}
            for op in sg:
                if indeg[id(op)] == 0:
                    heapq.heappush(wait_h[op.eng], (dr[id(op)], op.idx, op))
            left = len(sg)
            tmax = t_base
            while left:
                best = None
                for e in engs:
                    wh = wait_h[e]
                    rh = rdy_h[e]
                    fe = free[e]
                    while wh and wh[0][0] <= fe:
                        _, ix, o = heapq.heappop(wh)
                        heapq.heappush(rh, (ix, o))
                    if rh:
                        cand = (fe, rh[0][0], e, 0)
                    elif wh:
                        cand = (wh[0][0], wh[0][1], e, 1)
                    else:
                        continue
                    if best is None or cand[:2] < best[:2]:
                        best = cand
                start, _, e, which = best
                if which == 0:
                    _, op = heapq.heappop(rdy_h[e])
                else:
                    _, _, op = heapq.heappop(wait_h[e])
                busy, lat = self._cost(op)
                free[e] = start + busy
                op.fin = start + lat
                op.st0 = start
                if op.fin > tmax:
                    tmax = op.fin
                order.append(op)
                left -= 1
                for sc in succ.get(id(op), ()):
                    k = id(sc)
                    extra = 0.05 if (sc.eng == op.eng and not op.is_dma) else 0.35
                    t = op.fin + extra
                    if t > dr[k]:
                        dr[k] = t
                    indeg[k] -= 1
                    if indeg[k] == 0:
                        heapq.heappush(wait_h[sc.eng], (dr[k], sc.idx, sc))
            busy_e = {e: 0.0 for e in engs}
            for o in sg:
                busy_e[o.eng] += self._cost(o)[0]
            self.seg_stats.append((sg[0].seg, len(sg), tmax - t_base, busy_e))
            t_base = tmax
        self.est_time = t_base
        return order

    def emit(self, final_wait_eng="sp"):
        nc = self.nc
        order = self.schedule()
        last_eng = {}
        prev_last = {}
        prev_dmas = []
        older_dmas = []
        cur_dmas = []
        cur_seg = -1
        for op in order:
            if op.seg != cur_seg:
                cur_seg = op.seg
                prev_last = dict(last_eng)
                prev_dmas = older_dmas + cur_dmas
                older_dmas = cur_dmas
                cur_dmas = []
            if op.pfirst:
                op.deps = op.deps + list(prev_last.values()) + prev_dmas
            if op.is_dma:
                cur_dmas.append(op)
            else:
                last_eng[op.eng] = op
        for op in order:
            for d in op.deps:
                d.need_inc = True
            if op.is_dma:
                op.need_inc = True
        eng_sem = {}
        eng_cnt = {}
        dma_sems = [nc.alloc_semaphore("dq%d" % i) for i in range(N_DMA_SEMS)]
        dma_cnt = [0] * N_DMA_SEMS
        dma_last = [None] * N_DMA_SEMS
        ndma = 0
        for op in order:
            if not op.need_inc:
                continue
            if op.is_dma:
                j = ndma % N_DMA_SEMS
                ndma += 1
                if dma_last[j] is not None:
                    op.deps.append(dma_last[j])
                dma_cnt[j] += 16
                op.sem = dma_sems[j]
                op.val = dma_cnt[j]
                dma_last[j] = op
            else:
                e = op.eng
                if e not in eng_sem or eng_cnt[e] >= SEM_LIMIT:
                    eng_sem[e] = nc.alloc_semaphore("s_%s_%d" % (e, op.idx))
                    eng_cnt[e] = 0
                eng_cnt[e] += 1
                op.sem = eng_sem[e]
                op.val = eng_cnt[e]
        waited = {}
        nwaits = 0
        for op in order:
            E = self.engs[op.eng]
            need = {}
            for d in op.deps:
                k = id(d.sem)
                if k not in need or need[k][1] < d.val:
                    need[k] = (d.sem, d.val)
            for k, (sem, val) in need.items():
                wk = (op.eng, k)
                if waited.get(wk, 0) >= val:
                    continue
                E.wait_ge(sem, val)
                nwaits += 1
                waited[wk] = val
            try:
                inst = op.fn()
            except Exception:
                print('EMIT FAIL at op', op.idx, op.eng)
                raise
            if op.need_inc:
                inst.then_inc(op.sem, 16 if op.is_dma else 1)
        E = self.engs[final_wait_eng]
        for j in range(N_DMA_SEMS):
            if dma_cnt[j] > 0:
                E.wait_ge(dma_sems[j], dma_cnt[j])
        self.stats = dict(n_ops=len(self.ops), n_waits=nwaits, n_dma=ndma,
                          n_inc=sum(1 for o in self.ops if o.need_inc), est_ms=self.est_time / 1e3)
        return self.stats


class V:
    __slots__ = ("ap", "b")

    def __init__(self, ap, b):
        self.ap = ap
        self.b = b


class T:
    def __init__(self, h, name, dram=False):
        self.h = h
        self.name = name
        self.dram = dram

    def __getitem__(self, idx):
        if self.dram:
            return V(self.h[idx], self.name)
        return V(self.h[idx], self.name)

    def v(self, ap):
        return V(ap, self.name)


DT_SIZE = {F32: 4, BF16: 2, I32: 4}

D = 1024
DFF = 2816
NKC = D // 128
NFC = DFF // 128
LN_EPS = 1e-5
DEPTH = 2
ALPHA = (2 * DEPTH) ** 0.25
N_MEM = 256


class KB:
    def __init__(self, S, depth=DEPTH, stop_after=None, debug=()):
        self.S = S
        self.depth = depth
        self.stop_after = stop_after
        self.debug = debug
        self.nc = bass.Bass("TRN2", target_bir_lowering=False)
        self.P = Prog(self.nc)
        self.uid = 0
        self.sb_base = 0
        self.sb_cur = 0
        self.outs = {}
        self.arena = None
        self.rots = {}
        self.pending_T = None
        self.xb_cnt = 0
        self.ps_set = (0, 5)
        self.psb_set = (0, 2)
        self.fill_regs = {}
        self.n_keep = 256

    def sb(self, name, shape, dtype):
        nbytes = int(np.prod(shape[1:])) * DT_SIZE[dtype]
        nbytes = (nbytes + 63) // 64 * 64
        off = self.sb_cur
        self.sb_cur += nbytes
        assert self.sb_cur <= 207 * 1024, ("SBUF overflow", name, self.sb_cur)
        self.uid += 1
        if self.arena is None:
            self.arena = self.nc.alloc_sbuf_tensor("arena", [128, 207 * 1024], mybir.dt.uint8)
        ap = self.arena[:, off:off + int(np.prod(shape[1:])) * DT_SIZE[dtype]].bitcast(dtype)
        if len(shape) == 3:
            ap = ap.rearrange("p (a b) -> p a b", a=shape[1])
        elif len(shape) == 4:
            ap = ap.rearrange("p (a b c) -> p a b c", a=shape[1], b=shape[2])
        if shape[0] < 128:
            ap = ap[0:shape[0]]
        return T(ap, "%s_%d" % (name, self.uid))

    def phase_begin(self):
        self.flush_pending()
        self.P.barrier()
        self.sb_cur = self.sb_base

    def dram(self, name, shape, dtype, kind="ExternalOutput"):
        h = self.nc.dram_tensor(name, list(shape), dtype, kind=kind)
        return T(h.ap(), name, dram=True)

    def _rw(self, reads, writes):
        return [r.b for r in reads if isinstance(r, V)], [w.b for w in writes]

    def dma(self, out, in_, eng="sp"):
        nc = self.nc
        E = self.P.engs[eng]
        return self.P.add(eng, lambda: E.dma_start(out=out.ap, in_=in_.ap), reads=[in_.b], writes=[out.b], dma=True,
                          n=int(np.prod(out.ap.shape)) * 2)

    def mm(self, out, lhsT, rhs, start=True, stop=True):
        nc = self.nc
        return self.P.add("pe", lambda: nc.tensor.matmul(out.ap, lhsT.ap, rhs.ap, start=start, stop=stop),
                          reads=[lhsT.b, rhs.b], writes=[out.b], n=int(np.prod(out.ap.shape[1:])) * (4 if lhsT.ap.dtype == F32 else 1))

    def tr(self, out, in_, ident):
        nc = self.nc
        return self.P.add("pe", lambda: nc.tensor.transpose(out.ap, in_.ap, ident.ap),
                          reads=[in_.b, ident.b], writes=[out.b], n=200)

    def act(self, out, in_, func, bias=None, scale=None, accum=None, eng="act"):
        nc = self.nc
        kw = {}
        reads = [in_.b]
        writes = [out.b]
        if bias is not None:
            if isinstance(bias, V):
                kw["bias"] = bias.ap
                reads.append(bias.b)
            else:
                kw["bias"] = bias
        if scale is not None:
            if isinstance(scale, V):
                kw["scale"] = scale.ap
                reads.append(scale.b)
            else:
                kw["scale"] = scale
        if accum is not None:
            kw["accum_out"] = accum.ap
            writes.append(accum.b)
        return self.P.add("act", lambda: nc.scalar.activation(out=out.ap, in_=in_.ap, func=func, **kw),
                          reads=reads, writes=writes, n=int(np.prod(out.ap.shape[1:])))

    def ts(self, out, in0, s1, s2, op0, op1=None, accum=None, eng="dve"):
        E = self.P.engs[eng]
        reads = [in0.b]
        writes = [out.b]
        a1 = s1
        a2 = s2
        if isinstance(s1, V):
            a1 = s1.ap
            reads.append(s1.b)
        if isinstance(s2, V):
            a2 = s2.ap
            reads.append(s2.b)
        kw = {}
        if op1 is not None:
            kw["op1"] = op1
        if accum is not None:
            kw["accum_out"] = accum.ap
            writes.append(accum.b)
        return self.P.add(eng, lambda: E.tensor_scalar(out=out.ap, in0=in0.ap, scalar1=a1, scalar2=a2, op0=op0, **kw),
                          reads=reads, writes=writes, n=int(np.prod(out.ap.shape[1:])))

    def tt(self, out, in0, in1, op, eng="dve"):
        E = self.P.engs[eng]
        return self.P.add(eng, lambda: E.tensor_tensor(out=out.ap, in0=in0.ap, in1=in1.ap, op=op),
                          reads=[in0.b, in1.b], writes=[out.b], n=int(np.prod(out.ap.shape[1:])))

    def stt(self, out, in0, scalar, in1, op0, op1, accum=None):
        nc = self.nc
        reads = [in0.b, in1.b]
        writes = [out.b]
        a = scalar
        if isinstance(scalar, V):
            a = scalar.ap
            reads.append(scalar.b)
        kw = {}
        if accum is not None:
            kw["accum_out"] = accum.ap
            writes.append(accum.b)
        return self.P.add("dve", lambda: nc.vector.scalar_tensor_tensor(out=out.ap, in0=in0.ap, scalar=a, in1=in1.ap,
                                                                     op0=op0, op1=op1, **kw),
                          reads=reads, writes=writes, n=int(np.prod(out.ap.shape[1:])))

    def copy(self, out, in_, eng="dve"):
        E = self.P.engs[eng]
        if eng == "act":
            return self.P.add(eng, lambda: E.copy(out=out.ap, in_=in_.ap), reads=[in_.b], writes=[out.b], n=int(np.prod(out.ap.shape[1:])))
        return self.P.add(eng, lambda: E.tensor_copy(out=out.ap, in_=in_.ap), reads=[in_.b], writes=[out.b], n=int(np.prod(out.ap.shape[1:])))

    def memset(self, out, val, eng="pool"):
        E = self.P.engs[eng]
        return self.P.add(eng, lambda: E.memset(out.ap, val), writes=[out.b], n=int(np.prod(out.ap.shape[1:])))

    def red(self, out, in_, op, axis=AX.X, eng="dve"):
        E = self.P.engs[eng]
        return self.P.add(eng, lambda: E.tensor_reduce(out=out.ap, in_=in_.ap, axis=axis, op=op),
                          reads=[in_.b], writes=[out.b], n=int(np.prod(in_.ap.shape[1:])))

    def recip(self, out, in_):
        nc = self.nc
        return self.P.add("dve", lambda: nc.vector.reciprocal(out=out.ap, in_=in_.ap), reads=[in_.b], writes=[out.b])

    def aselect(self, out, in_, pattern, cmp, fill, base, cm):
        nc = self.nc
        regs = self.fill_regs

        def fn():
            if fill not in regs:
                regs[fill] = nc.gpsimd.to_reg(float(fill))
            return nc.gpsimd.affine_select(out=out.ap, in_=in_.ap, pattern=pattern, compare_op=cmp,
                                           fill=regs[fill], base=base, channel_multiplier=cm)
        return self.P.add("pool", fn, reads=[in_.b], writes=[out.b], n=int(np.prod(out.ap.shape[1:])))

    def iota(self, out, pattern, base, cm):
        nc = self.nc
        return self.P.add("pool", lambda: nc.gpsimd.iota(out.ap, pattern=pattern, base=base, channel_multiplier=cm,
                                                         allow_small_or_imprecise_dtypes=True), writes=[out.b], n=int(np.prod(out.ap.shape[1:])))

    def setup(self):
        nc = self.nc
        self.ps = []
        for i in range(5):
            h = nc.alloc_psum_tensor("ps%d" % i, [128, 512], F32)
            self.ps.append(T(h, "ps%d" % i))
        self.psb = []
        for i in range(2):
            h = nc.alloc_psum_tensor("psb%d" % i, [128, 1024], BF16)
            self.psb.append(T(h, "psb%d" % i))
        self.ps_rr = 0
        self.psb_rr = 0
        self.ident_f = self.sb("identf", [128, 128], F32)
        self.ident = self.sb("ident", [128, 128], BF16)
        self.memset(self.ident_f[:], 1.0)
        self.aselect(self.ident_f[:], self.ident_f[:], [[-1, 128]], ALU.is_equal, 0.0, 0, 1)
        self.copy(self.ident[:], self.ident_f[:], eng="pool")
        self.sb_base = self.sb_cur

    def next_ps(self):
        b0, n = self.ps_set
        t = self.ps[b0 + self.ps_rr % n]
        self.ps_rr += 1
        return t

    def next_psb(self):
        b0, n = self.psb_set
        t = self.psb[b0 + self.psb_rr % n]
        self.psb_rr += 1
        return t

    def load_w(self, name, dram_ap_fn, kchunks, ncols, eng="pool", split=4):
        w = self.sb(name, [128, kchunks, ncols], BF16)
        for k in range(kchunks):
            self.dma(w[:, k, :], dram_ap_fn(k), eng="pool")
        return w

    def layer_norm_tile(self, r, g_bc, b_bc, out_f32, scr):
        st = scr["st"]
        junk = scr["junk"]
        self.act(junk[:], r[:], AF.Identity, accum=st[:, 0:1])
        self.act(junk[:], r[:], AF.Square, accum=st[:, 1:2])
        self.ts(st[:, 2:3], st[:, 0:1], 1.0 / D, None, ALU.mult)
        self.tt(st[:, 3:4], st[:, 2:3], st[:, 2:3], ALU.mult)
        self.stt(st[:, 4:5], st[:, 1:2], 1.0 / D, st[:, 3:4], ALU.mult, ALU.subtract)
        self.ts(st[:, 4:5], st[:, 4:5], 0.0, LN_EPS, ALU.max, ALU.add)
        self.act(st[:, 5:6], st[:, 4:5], AF.Sqrt)
        self.recip(st[:, 6:7], st[:, 5:6])
        self.ts(out_f32[:], r[:], st[:, 2:3], st[:, 6:7], ALU.subtract, ALU.mult)
        self.tt(out_f32[:], out_f32[:], g_bc[:], ALU.mult)
        self.tt(out_f32[:], out_f32[:], b_bc[:], ALU.add)

    def store_xT(self, x_f32, xT_dram, t0, scr, defer=False):
        xbl = scr["xb"]
        if isinstance(xbl, list):
            xb = xbl[self.xb_cnt % len(xbl)]
            self.xb_cnt += 1
        else:
            xb = xbl
        xTs = scr["xTs"]
        self.copy(xb[:], x_f32[:], eng="act")

        def part_b():
            pb = self.next_psb()
            for k in range(NKC):
                self.tr(pb[:, k * 128:(k + 1) * 128], xb[:, k * 128:(k + 1) * 128], self.ident[:])
            self.copy(xTs[:], pb[:, :], eng="dve")
            self.dma(V(xT_dram.h.rearrange("(k p) s -> p k s", p=128)[:, :, t0:t0 + 128], xT_dram.name),
                     V(xTs.h[:].rearrange("p (k t) -> p k t", k=NKC), xTs.name))
        if defer:
            self.flush_pending()
            self.pending_T = part_b
        else:
            part_b()

    def flush_pending(self):
        if self.pending_T is not None:
            f = self.pending_T
            self.pending_T = None
            f()

    def dma_s(self, out, in_, eng="sp"):
        E = self.P.engs[eng]
        return self.P.add(eng, lambda: E.dma_start(out=out.ap, in_=in_.ap, allow_slow_non_contiguous=True),
                          reads=[in_.b], writes=[out.b], dma=True, n=int(np.prod(out.ap.shape)) * 8)

    def rot(self, name, n, shape, dtype):
        key = "_rot_" + name
        lst = [self.sb(name + str(i), shape, dtype) for i in range(n)]
        self.rots[key] = [lst, 0]
        return key

    def nx(self, key):
        lst, i = self.rots[key]
        self.rots[key][1] = i + 1
        return lst[i % len(lst)]

    def vmax(self, out, in_):
        nc = self.nc
        return self.P.add("dve", lambda: nc.vector.max(out=out.ap, in_=in_.ap), reads=[in_.b], writes=[out.b], n=int(np.prod(in_.ap.shape[1:])))

    def match_replace(self, out, rep, vals, imm):
        nc = self.nc
        return self.P.add("dve", lambda: nc.vector.match_replace(out=out.ap, in_to_replace=rep.ap, in_values=vals.ap, imm_value=imm),
                          reads=[rep.b, vals.b], writes=[out.b], n=int(np.prod(vals.ap.shape[1:])))

    def redabs(self, out, in_):
        nc = self.nc
        return self.P.add("dve", lambda: nc.vector.tensor_reduce(out=out.ap, in_=in_.ap, axis=AX.X, op=ALU.max,
                                                                 apply_absolute_value=True),
                          reads=[in_.b], writes=[out.b], n=int(np.prod(in_.ap.shape[1:])))

    def fm_rows(self, FMS, c0, nchunk, s0, s1):
        return V(FMS.h[c0 * 128:(c0 + nchunk) * 128, s0:s1].rearrange("(c p) s -> p c s", p=128), FMS.name)

    def setup_consts(self, meta, bdm, ovl, NB, NCP):
        S = self.S
        NT = S // 128
        self.NB = NB
        self.NCP = NCP
        self.meta = self.sb("meta", [128, 32], F32)
        self.dma(self.meta[:], meta[:, :])
        self.bdm = self.sb("bdm", [128, 256], F32)
        self.dma(self.bdm[:], bdm[:, :])
        self.ovl = self.sb("ovl", [128, NCP // 128, NB], F32)
        self.dma(self.ovl[:], V(ovl.h.rearrange("(c p) j -> p c j", p=128), ovl.name))
        self.U = self.sb("U", [128, 128], F32)
        self.memset(self.U[:], 1.0)
        self.aselect(self.U[:], self.U[:], [[1, 128]], ALU.is_ge, 0.0, 0, -1)
        self.cneg30 = self.sb("cneg30", [128, 128], F32)
        self.memset(self.cneg30[:], 0.0)
        self.aselect(self.cneg30[:], self.cneg30[:], [[-1, 128]], ALU.is_ge, -1e30, 0, 1)
        self.cneg2k = self.sb("cneg2k", [128, 128], F32)
        self.memset(self.cneg2k[:], 0.0)
        self.aselect(self.cneg2k[:], self.cneg2k[:], [[-1, 128]], ALU.is_ge, -2000.0, 0, 1)
        self.band = self.sb("band", [128, 640], F32)
        self.memset(self.band[:], 0.0)
        self.aselect(self.band[:], self.band[:], [[1, 640]], ALU.is_ge, -2000.0, -1, -1)
        self.aselect(self.band[:], self.band[:], [[-1, 640]], ALU.is_ge, -2000.0, 512, 1)
        self.decayT4 = self.sb("decayT4", [128, 4, 128], F32)
        self.xi = self.sb("xi", [128, 128], F32)
        self.zeta = self.sb("zeta", [128, 128], F32)
        self.cdecay = self.sb("cdecay", [128, 1], F32)
        self.rkc = self.sb("rkc", [128, 20], F32)
        self.sb_base = self.sb_cur
        dji = self.sb("dji", [128, 128], F32)
        self.iota(dji[:], [[1, 128]], 0, -1)
        for h in range(4):
            self.act(self.decayT4[:, h, :], dji[:], AF.Exp, scale=RET_LNG[h])
        self.tt(self.decayT4[:], self.decayT4[:], V(self.U.h[:, :].unsqueeze(1).to_broadcast([128, 4, 128]), self.U.name), ALU.mult)
        ip1 = self.sb("ip1", [128, 128], F32)
        self.iota(ip1[:], [[1, 128]], 1, 0)
        self.act(self.xi[:], ip1[:], AF.Exp, scale=self.meta[:, 12:13])
        jr = self.sb("jr", [128, 128], F32)
        self.iota(jr[:], [[0, 128]], 127, -1)
        for h in range(4):
            self.act(self.zeta[:, 32 * h:32 * h + 32], jr[:, 32 * h:32 * h + 32], AF.Exp, scale=RET_LNG[h])
        c128 = self.sb("c128", [128, 1], F32)
        self.memset(c128[:], 128.0)
        self.act(self.cdecay[:], c128[:], AF.Exp, scale=self.meta[:, 12:13])
        for k in range(20):
            self.memset(self.rkc[:, k:k + 1], 2.0 ** (-k))

    def build_addmask(self):
        NT = self.S // 128
        NB = self.NB
        self.addmask = self.sb("addmask", [128, NT, NB], F32)
        self.memset(self.addmask[:], 0.0)
        for i in range(NT):
            for half in range(2):
                cur = 2 * i + half
                r0 = 64 * half
                v = self.addmask[r0:r0 + 64, i, :]
                self.aselect(v, v, [[-1, NB]], ALU.is_ge, -1e30, cur, 0)
                self.memset(self.addmask[r0:r0 + 64, i, 0:1], 1e30)
                self.memset(self.addmask[r0:r0 + 64, i, cur:cur + 1], 1e30)
                if cur >= 1:
                    self.memset(self.addmask[r0:r0 + 64, i, cur - 1:cur], 1e30)

    def phase_rope(self, ROPE):
        S = self.S
        self.phase_begin()
        pos = self.sb("pos", [128, S], F32)
        self.iota(pos[:], [[1, S]], 0, 0)
        a = self.sb("a", [128, S], F32)
        ki = self.sb("ki", [128, S], I32)
        kf = self.sb("kf", [128, S], F32)
        m = self.sb("m", [128, S], F32)
        r = self.sb("r", [128, S], F32)
        PI = math.pi
        for t in range(4):
            for which in range(2):
                self.ts(a[:], pos[:], self.meta[:, t:t + 1], (PI / 2 if which == 0 else 0.0), ALU.mult, ALU.add)
                self.ts(kf[:], a[:], 1.0 / (2 * PI), None, ALU.mult)
                self.copy(ki[:], kf[:])
                self.copy(kf[:], ki[:])
                self.stt(r[:], kf[:], -2 * PI, a[:], ALU.mult, ALU.add)
                self.ts(m[:], r[:], PI, -2 * PI, ALU.is_gt, ALU.mult)
                self.tt(r[:], r[:], m[:], ALU.add)
                self.ts(m[:], r[:], -PI, 2 * PI, ALU.is_lt, ALU.mult)
                self.tt(r[:], r[:], m[:], ALU.add)
                self.ts(r[:], r[:], PI, -PI, ALU.min, ALU.max)
                self.act(r[:], r[:], AF.Sin)
                col = 4 + 4 * which + t
                self.ts(r[:], r[:], self.meta[:, col:col + 1], None, ALU.mult)
                self.dma(ROPE[t, which], r[:])

    def finish_tile(self, rq, g_bc, b_bc, x_out, xT_out, t0, scr):
        self.layer_norm_tile(rq, g_bc, b_bc, rq, scr)
        self.dma(x_out[t0:t0 + 128, :], rq[:])
        if xT_out is not None:
            self.store_xT(rq, xT_out, t0, scr, defer=True)

    def ln_setup(self, ln_g, ln_b):
        g_bc = self.sb("g_bc", [128, D], F32)
        b_bc = self.sb("b_bc", [128, D], F32)
        self.dma(g_bc[:], V(ln_g.h.partition_broadcast(128), ln_g.name))
        self.dma(b_bc[:], V(ln_b.h.partition_broadcast(128), ln_b.name))
        scr = dict(st=self.sb("st", [128, 8], F32), junk=self.sb("junk", [128, D], BF16),
                   xb=[self.sb("xb0", [128, D], BF16), self.sb("xb1", [128, D], BF16)], xTs=self.sb("xTs", [128, D], BF16))
        return g_bc, b_bc, scr

    def phase_ffn(self, x_in, xT_in, w_gu, w_down, ln_g, ln_b, x_out, xT_out):
        S = self.S
        self.phase_begin()
        wgu = self.load_w("wgu", lambda k: V(w_gu.h[k * 128:(k + 1) * 128, :], w_gu.name), NKC, 2 * DFF)
        wdn = self.load_w("wdn", lambda k: V(w_down.h[k * 128:(k + 1) * 128, :], w_down.name), NFC, D)
        g_bc, b_bc, scr = self.ln_setup(ln_g, ln_b)
        xt = self.sb("xT", [128, NKC, 512], BF16)
        hT = self.sb("hT", [128, NFC, 512], BF16)
        sg = [self.sb("sg%d" % i, [128, 512], BF16) for i in range(2)]
        xr = [self.sb("xr%d" % i, [128, D], F32) for i in range(2)]
        rqs = [self.sb("r%d" % i, [128, D], F32) for i in range(2)]
        ntile = S // 512

        def load_xt(t_):
            self.dma(xt[:], V(xT_in.h.rearrange("(k p) s -> p k s", p=128)[:, :, t_ * 512:(t_ + 1) * 512], xT_in.name))

        def load_xq(idx):
            self.dma(xr[idx % 2][:], x_in[idx * 128:(idx + 1) * 128, :])
        load_xt(0)
        load_xq(0)
        for t in range(ntile):
            for j in range(NFC):
                pg = self.next_ps()
                pu = self.next_ps()
                for k in range(NKC):
                    self.mm(pg[:], wgu[:, k, j * 128:(j + 1) * 128], xt[:, k, :], start=(k == 0), stop=(k == NKC - 1))
                for k in range(NKC):
                    self.mm(pu[:], wgu[:, k, DFF + j * 128:DFF + (j + 1) * 128], xt[:, k, :], start=(k == 0), stop=(k == NKC - 1))
                s = sg[j % 2]
                self.act(s[:], pg[:], AF.Silu)
                self.tt(hT[:, j, :], s[:], pu[:], ALU.mult)
            if t + 1 < ntile:
                load_xt(t + 1)
            for q in range(4):
                t0 = t * 512 + q * 128
                xq = xr[q % 2]
                rq = rqs[q % 2]
                if t * 4 + q + 1 < ntile * 4:
                    load_xq(t * 4 + q + 1)
                for half in range(2):
                    hs = slice(half * 512, (half + 1) * 512)
                    pd = self.next_ps()
                    for j in range(NFC):
                        self.mm(pd[:], hT[:, j, q * 128:(q + 1) * 128], wdn[:, j, hs], start=(j == 0), stop=(j == NFC - 1))
                    self.act(xq[:, hs], xq[:, hs], AF.Copy, scale=ALPHA)
                    self.stt(rq[:, hs], pd[:], 0.5, xq[:, hs], ALU.mult, ALU.add)
                self.finish_tile(rq, g_bc, b_bc, x_out, xT_out, t0, scr)

    def phase_transpose_in(self, x_in, xT_out):
        S = self.S
        self.phase_begin()
        xr = [self.sb("xr%d" % i, [128, D], F32) for i in range(2)]
        scr = dict(xb=self.sb("xb", [128, D], BF16), xTs=self.sb("xTs", [128, D], BF16))
        for i in range(S // 128):
            xq = xr[i % 2]
            self.dma(xq[:], x_in[i * 128:(i + 1) * 128, :])
            self.store_xT(xq, xT_out, i * 128, scr)

    def phase_inproj(self, xT_in, w2, ROPE, FMS, TMB, TMF):
        S = self.S
        self.phase_begin()
        w = self.load_w("win", lambda k: V(w2.h[k * 128:(k + 1) * 128, :], w2.name), NKC, NCOL2)
        xts = [self.sb("xT%d" % i_, [128, NKC, 512], BF16) for i_ in range(2)]
        tabs = [self.sb("tab%d" % i_, [128, 4, 2, 512], F32) for i_ in range(2)]
        t1 = self.rot("t1", 3, [128, 512], F32)
        t2 = self.rot("t2", 3, [128, 512], F32)
        ob = self.rot("ob", 4, [128, 512], BF16)
        tmb = self.rot("tmb", 2, [128, 960], BF16)
        tmf = self.rot("tmf", 2, [128, 24], F32)
        ntile = S // 512

        def load_t(t_):
            ss_ = slice(t_ * 512, (t_ + 1) * 512)
            self.dma(xts[t_ % 2][:], V(xT_in.h.rearrange("(k p) s -> p k s", p=128)[:, :, ss_], xT_in.name))
            self.dma(tabs[t_ % 2][:], V(ROPE.h[:, :, :, ss_].rearrange("t w p s -> p t w s"), ROPE.name))
        load_t(0)
        for t in range(ntile):
            ss = slice(t * 512, (t + 1) * 512)
            xt = xts[t % 2]
            tab = tabs[t % 2]
            if t + 1 < ntile:
                load_t(t + 1)
            for ci, tb in enumerate(ROPED_TABLES):
                pA = self.next_ps()
                pB = self.next_ps()
                for k in range(NKC):
                    self.mm(pA[:], w[:, k, (2 * ci) * 128:(2 * ci + 1) * 128], xt[:, k, :], start=(k == 0), stop=(k == NKC - 1))
                for k in range(NKC):
                    self.mm(pB[:], w[:, k, (2 * ci + 1) * 128:(2 * ci + 2) * 128], xt[:, k, :], start=(k == 0), stop=(k == NKC - 1))
                a1 = self.nx(t1)
                a2 = self.nx(t2)
                o = self.nx(ob)
                self.tt(a1[:], pA[:], tab[:, tb, 0, :], ALU.mult)
                self.tt(a2[:], pB[:], tab[:, tb, 1, :], ALU.mult)
                self.tt(o[:], a1[:], a2[:], ALU.add)
                self.dma(V(FMS.h[ci * 128:(ci + 1) * 128, ss], FMS.name), o[:])
            nr = len(ROPED_TABLES)
            for j in range(7):
                wc = 2 * nr + j
                pA = self.next_ps()
                for k in range(NKC):
                    self.mm(pA[:], w[:, k, wc * 128:(wc + 1) * 128], xt[:, k, :], start=(k == 0), stop=(k == NKC - 1))
                o = self.nx(ob)
                self.copy(o[:], pA[:], eng="act")
                self.dma(V(FMS.h[(nr + j) * 128:(nr + j + 1) * 128, ss], FMS.name), o[:])
            for q in range(4):
                t0 = t * 512 + q * 128
                pA = self.next_ps()
                pB = self.next_ps()
                for k in range(NKC):
                    self.mm(pA[:], xt[:, k, q * 128:(q + 1) * 128], w[:, k, TM0:TM0 + 512], start=(k == 0), stop=(k == NKC - 1))
                for k in range(NKC):
                    self.mm(pB[:, 0:472], xt[:, k, q * 128:(q + 1) * 128], w[:, k, TM0 + 512:TM0 + 984], start=(k == 0), stop=(k == NKC - 1))
                b = self.nx(tmb)
                f = self.nx(tmf)
                self.copy(b[:, 0:512], pA[:], eng="act")
                self.copy(b[:, 512:960], pB[:, 0:448])
                self.copy(f[:], pB[:, 448:472])
                self.dma(TMB[t0:t0 + 128, :], b[:])
                self.dma(TMF[t0:t0 + 128, :], f[:])

    def store_yT(self, y, YT, br, n, yTs_key):
        pb = self.next_psb()
        self.tr(pb[:, 0:128], y[:, 0:128], self.ident[:])
        self.tr(pb[:, 128:256], y[:, 128:256], self.ident[:])
        yTs = self.nx(yTs_key)
        self.copy(yTs[:], pb[:, 0:256])
        self.dma(V(YT.h[br * 256:(br + 1) * 256, n * 128:(n + 1) * 128].rearrange("(c p) t -> p c t", p=128), YT.name),
                 V(yTs.h[:, :].rearrange("p (c t) -> p c t", c=2), yTs.name))

    def phase_ret(self, FMS, TMB, YT):
        S = self.S
        NT = S // 128
        self.phase_begin()
        rq = self.sb("rq", [128, S], BF16)
        rk = self.sb("rk", [128, S], BF16)
        self.dma(rq[:], V(FMS.h[0:128, :], FMS.name))
        self.dma(rk[:], V(FMS.h[128:256, :], FMS.name))
        Sbd = self.sb("Sbd", [128, 256], F32)
        Sbd_bf = self.sb("Sbd_bf", [128, 256], BF16)
        self.memset(Sbd[:], 0.0)
        self.memset(Sbd_bf[:], 0.0)
        vt_k = self.rot("vt", 2, [128, 512], BF16)
        qxi_k = self.rot("qxi", 2, [128, 128], BF16)
        qm_k = self.rot("qm", 2, [128, 4, 128], BF16)
        kz_k = self.rot("kz", 2, [128, 128], BF16)
        PT_k = self.rot("PT", 2, [128, 4, 128], BF16)
        cross_k = self.rot("cross", 2, [128, 256], F32)
        o_k = self.rot("o", 2, [128, 256], F32)
        tmp_k = self.rot("tmp", 2, [128, 256], F32)
        osq_k = self.rot("osq", 2, [128, 256], F32)
        sg_k = self.rot("sg", 2, [128, 256], F32)
        st_k = self.rot("st", 2, [128, 16], F32)
        y_k = self.rot("y", 2, [128, 256], BF16)
        yTs_k = self.rot("yTs", 2, [128, 256], BF16)
        hm = V(self.meta.h[:, 13:17].unsqueeze(2).to_broadcast([128, 4, 128]), self.meta.name)
        for n in range(NT):
            sl = slice(n * 128, (n + 1) * 128)
            vt = self.nx(vt_k)
            self.dma(vt[:], TMB[n * 128:(n + 1) * 128, 0:512])
            qxi = self.nx(qxi_k)
            self.tt(qxi[:], rq[:, sl], self.xi[:], ALU.mult)
            qm = self.nx(qm_k)
            self.tt(qm[:], V(rq.h[:, sl].unsqueeze(1).to_broadcast([128, 4, 128]), rq.name), hm, ALU.mult, eng="pool")
            pb = self.next_psb()
            self.tr(pb[:, 0:128], rk[:, sl], self.ident[:])
            kz = self.nx(kz_k)
            self.tt(kz[:], pb[:, 0:128], self.zeta[:], ALU.mult)
            ps1 = self.next_ps()
            self.mm(ps1[:], rk[:, sl], V(qm.h[:, :, :].rearrange("p h i -> p (h i)"), qm.name))
            PT = self.nx(PT_k)
            self.tt(PT[:], V(ps1.h[:, :].rearrange("p (h i) -> p h i", h=4), ps1.name), self.decayT4[:], ALU.mult)
            ps2 = self.next_ps()
            self.mm(ps2[:, 0:256], qxi[:], Sbd_bf[:])
            cross = self.nx(cross_k)
            self.copy(cross[:], ps2[:, 0:256], eng="act")
            ps3 = self.next_ps()
            for h in range(4):
                self.mm(ps3[:, 64 * h:64 * h + 64], PT[:, h, :], vt[:, 64 * h:64 * h + 64])
            o = self.nx(o_k)
            self.tt(o[:], ps3[:, 0:256], cross[:], ALU.add)
            ps4 = self.next_ps()
            self.mm(ps4[:, 0:256], kz[:], vt[:, 0:256])
            tmp = self.nx(tmp_k)
            self.tt(tmp[:], ps4[:, 0:256], self.bdm[:], ALU.mult)
            self.stt(Sbd[:], Sbd[:], self.cdecay[:, 0:1], tmp[:], ALU.mult, ALU.add)
            self.copy(Sbd_bf[:], Sbd[:], eng="act")
            st = self.nx(st_k)
            o3 = V(o.h[:, :].rearrange("p (h e) -> p h e", h=4), o.name)
            self.red(st[:, 0:4], o3, ALU.add)
            osq = self.nx(osq_k)
            self.tt(osq[:], o[:], o[:], ALU.mult, eng="pool")
            self.red(st[:, 4:8], V(osq.h[:, :].rearrange("p (h e) -> p h e", h=4), osq.name), ALU.add)
            self.ts(st[:, 8:12], st[:, 0:4], 1.0 / 64, None, ALU.mult)
            self.tt(st[:, 12:16], st[:, 8:12], st[:, 8:12], ALU.mult)
            self.stt(st[:, 4:8], st[:, 4:8], 1.0 / 64, st[:, 12:16], ALU.mult, ALU.subtract)
            self.ts(st[:, 4:8], st[:, 4:8], 0.0, LN_EPS, ALU.max, ALU.add)
            self.act(st[:, 4:8], st[:, 4:8], AF.Sqrt)
            self.recip(st[:, 4:8], st[:, 4:8])
            self.tt(o3, o3, V(st.h[:, 8:12].unsqueeze(2).to_broadcast([128, 4, 64]), st.name), ALU.subtract)
            self.tt(o3, o3, V(st.h[:, 4:8].unsqueeze(2).to_broadcast([128, 4, 64]), st.name), ALU.mult)
            sg = self.nx(sg_k)
            self.act(sg[:], vt[:, 256:512], AF.Silu)
            y = self.nx(y_k)
            self.tt(y[:], o[:], sg[:], ALU.mult)
            self.store_yT(y, YT, 0, n, yTs_k)

    def phase_ssd(self, FMS, TMB, TMF, conv_w, conv_b, dt_bias, a_log, d_skip, norm_g, YT):
        S = self.S
        NT = S // 128
        self.phase_begin()
        cw = self.sb("cw", [128, 6, 4], F32)
        for k_ in range(4):
            self.dma_s(cw[:, :, k_], V(conv_w.h[k_].rearrange("(c p) -> p c", p=128), conv_w.name))
        cb = self.sb("cb", [128, 6], F32)
        self.dma_s(cb[:], V(conv_b.h.rearrange("(c p) -> p c", p=128), conv_b.name))
        dtb = self.sb("dtb", [128, 4], F32)
        self.dma(dtb[:], V(dt_bias.h.partition_broadcast(128), dt_bias.name))
        a_bc = self.sb("a_bc", [128, 4], F32)
        self.dma(a_bc[:], V(a_log.h.partition_broadcast(128), a_log.name))
        self.act(a_bc[:], a_bc[:], AF.Exp)
        self.ts(a_bc[:], a_bc[:], -1.0, None, ALU.mult)
        Dbc = self.sb("Dbc", [128, 4], F32)
        self.dma(Dbc[:], V(d_skip.h.partition_broadcast(128), d_skip.name))
        ng_bc = self.sb("ng_bc", [128, 256], F32)
        self.dma(ng_bc[:], V(norm_g.h.partition_broadcast(128), norm_g.name))
        xbcs = self.sb("xbcs", [128, 6, S], BF16)
        raw_k = self.rot("raw", 2, [128, 6, 515], BF16)
        acc_k = self.rot("acc", 2, [128, 512], F32)
        for t in range(S // 512):
            raw = self.nx(raw_k)
            if t == 0:
                self.memset(raw[:, :, 0:3], 0.0)
                self.dma(raw[:, :, 3:515], self.fm_rows(FMS, 14, 6, 0, 512))
            else:
                self.dma(raw[:, :, 0:515], self.fm_rows(FMS, 14, 6, t * 512 - 3, (t + 1) * 512))
            for c in range(6):
                acc = self.nx(acc_k)
                self.ts(acc[:], raw[:, c, 3:515], cw[:, c, 3:4], None, ALU.mult)
                for k in (2, 1, 0):
                    self.stt(acc[:], raw[:, c, k:k + 512], cw[:, c, k:k + 1], acc[:], ALU.mult, ALU.add)
                self.act(xbcs[:, c, t * 512:(t + 1) * 512], acc[:], AF.Silu, bias=cb[:, c:c + 1])
        prev = self.sb("prev", [128, 256], F32)
        prev_bf = self.sb("prev_bf", [128, 256], BF16)
        self.memset(prev[:], 0.0)
        self.memset(prev_bf[:], 0.0)
        xsB_k = self.rot("xsB", 2, [128, 512], BF16)
        tmf_k = self.rot("tmf", 2, [128, 24], F32)
        zt_k = self.rot("zt", 2, [128, 256], BF16)
        st_k = self.rot("st", 2, [128, 32], F32)
        adtb_k = self.rot("adtb", 2, [128, 4, 128], F32)
        seg_k = self.rot("seg", 2, [128, 4, 128], F32)
        MT_k = self.rot("MT", 2, [128, 4, 128], BF16)
        X_k = self.rot("X", 2, [128, 256], BF16)
        Xd_k = self.rot("Xd", 2, [128, 256], BF16)
        yd_k = self.rot("yd", 2, [128, 256], F32)
        y_k = self.rot("y", 2, [128, 256], F32)
        t2_k = self.rot("t2", 2, [128, 256], F32)
        sz_k = self.rot("sz", 2, [128, 256], F32)
        yb_k = self.rot("yb", 2, [128, 256], BF16)
        yTs_k = self.rot("yTs", 2, [128, 256], BF16)
        Ubc = V(self.U.h[:, :].unsqueeze(1).to_broadcast([128, 4, 128]), self.U.name)

        def h4(t_):
            return V(t_.h[:, 0:256].rearrange("p (h e) -> p h e", h=4), t_.name)

        def bc4(v_):
            return V(v_.ap.unsqueeze(2).to_broadcast([128, 4, 64]), v_.b)

        for n in range(NT):
            sl = slice(n * 128, (n + 1) * 128)
            pb = self.next_psb()
            for c in range(4):
                self.tr(pb[:, c * 128:(c + 1) * 128], xbcs[:, c, sl], self.ident[:])
            xsB = self.nx(xsB_k)
            self.copy(xsB[:], pb[:, 0:512])
            tmf = self.nx(tmf_k)
            self.dma(tmf[:], TMF[n * 128:(n + 1) * 128, :])
            zt = self.nx(zt_k)
            self.dma(zt[:], TMB[n * 128:(n + 1) * 128, 704:960])
            st = self.nx(st_k)
            self.tt(st[:, 0:4], tmf[:, 20:24], dtb[:], ALU.add)
            self.act(st[:, 0:4], st[:, 0:4], AF.Exp)
            self.act(st[:, 0:4], st[:, 0:4], AF.Ln, bias=1.0)
            self.tt(st[:, 4:8], st[:, 0:4], a_bc[:], ALU.mult)
            adtb = self.nx(adtb_k)
            self.copy(adtb[:], V(st.h[:, 4:8].unsqueeze(2).to_broadcast([128, 4, 128]), st.name))
            psA = self.next_ps()
            self.mm(psA[:, 0:4], self.U[:], st[:, 4:8])
            self.copy(st[:, 8:12], psA[:, 0:4], eng="act")
            psB = self.next_ps()
            for h in range(4):
                self.mm(psB[:, h * 128:(h + 1) * 128], adtb[:, h, :], self.U[:])
            seg = self.nx(seg_k)
            for h in range(4):
                self.ts(seg[:, h, :], psB[:, h * 128:(h + 1) * 128], st[:, 8 + h:9 + h], 0.0, ALU.subtract, ALU.min)
            self.act(seg[:], seg[:], AF.Exp)
            self.tt(seg[:], seg[:], Ubc, ALU.mult, eng="pool")
            alast = V(psB.h[:, 127:512:128], psB.name)
            self.tt(st[:, 12:16], alast, st[:, 8:12], ALU.subtract)
            self.act(st[:, 12:16], st[:, 12:16], AF.Exp)
            self.act(st[:, 16:20], alast, AF.Exp)
            self.act(st[:, 20:24], st[:, 8:12], AF.Exp)
            psG = self.next_ps()
            for g in range(2):
                self.mm(psG[:, g * 128:(g + 1) * 128], xbcs[:, 2 + g, sl], xbcs[:, 4 + g, sl])
            MT = self.nx(MT_k)
            for g in range(2):
                self.tt(MT[:, 2 * g:2 * g + 2, :], seg[:, 2 * g:2 * g + 2, :],
                        V(psG.h[:, g * 128:(g + 1) * 128].unsqueeze(1).to_broadcast([128, 2, 128]), psG.name), ALU.mult)
            X = self.nx(X_k)
            self.tt(h4(X), h4(xsB), bc4(st[:, 0:4]), ALU.mult)
            psY = self.next_ps()
            for h in range(4):
                self.mm(psY[:, 64 * h:64 * h + 64], MT[:, h, :], X[:, 64 * h:64 * h + 64])
            psO = self.next_ps()
            for g in range(2):
                self.mm(psO[:, 128 * g:128 * g + 128], xbcs[:, 4 + g, sl], prev_bf[:, 128 * g:128 * g + 128])
            yd = self.nx(yd_k)
            self.copy(yd[:], psY[:, 0:256], eng="act")
            y = self.nx(y_k)
            self.tt(h4(y), h4(psO), bc4(st[:, 20:24]), ALU.mult)
            self.tt(y[:], y[:], yd[:], ALU.add)
            t2 = self.nx(t2_k)
            self.tt(h4(t2), h4(xsB), bc4(Dbc[:, 0:4]), ALU.mult, eng="pool")
            self.tt(y[:], y[:], t2[:], ALU.add)
            Xd = self.nx(Xd_k)
            self.tt(h4(Xd), h4(X), bc4(st[:, 12:16]), ALU.mult, eng="pool")
            psS = self.next_ps()
            for g in range(2):
                self.mm(psS[:, 128 * g:128 * g + 128], xsB[:, 256 + 128 * g:256 + 128 * g + 128], Xd[:, 128 * g:128 * g + 128])
            self.tt(h4(prev), h4(prev), bc4(st[:, 16:20]), ALU.mult)
            self.tt(prev[:], prev[:], psS[:, 0:256], ALU.add)
            self.copy(prev_bf[:], prev[:], eng="act")
            sz = self.nx(sz_k)
            self.act(sz[:], zt[:], AF.Silu)
            self.tt(y[:], y[:], sz[:], ALU.mult)
            self.tt(t2[:], y[:], y[:], ALU.mult, eng="pool")
            self.red(st[:, 24:26], V(t2.h[:, :].rearrange("p (g e) -> p g e", g=2), t2.name), ALU.add)
            self.ts(st[:, 24:26], st[:, 24:26], 1.0 / 128, LN_EPS, ALU.mult, ALU.add)
            self.act(st[:, 24:26], st[:, 24:26], AF.Sqrt)
            self.recip(st[:, 24:26], st[:, 24:26])
            y2 = V(y.h[:, :].rearrange("p (g e) -> p g e", g=2), y.name)
            self.tt(y2, y2, V(st.h[:, 24:26].unsqueeze(2).to_broadcast([128, 2, 128]), st.name), ALU.mult)
            yb = self.nx(yb_k)
            self.tt(yb[:], y[:], ng_bc[:], ALU.mult)
            self.store_yT(yb, YT, 3, n, yTs_k)

    def softmax_pv(self, Ssb, nk, Vt, kt0, out, kk, clamp=None):
        st = self.nx(kk["st"])
        self.red(st[:, 0:1], Ssb, ALU.max)
        if clamp is not None:
            self.ts(st[:, 0:1], st[:, 0:1], clamp, None, ALU.max)
        self.ts(st[:, 1:2], st[:, 0:1], -1.0, None, ALU.mult)
        P = self.nx(kk["P"])
        self.act(P[:, 0:nk], Ssb, AF.Exp, bias=st[:, 1:2], accum=st[:, 2:3])
        self.ts(st[:, 3:4], st[:, 2:3], 1e-30, None, ALU.max)
        self.recip(st[:, 4:5], st[:, 3:4])
        po = self.next_ps()
        nkt = nk // 128
        for g0 in range(0, nkt, 8):
            gn = min(8, nkt - g0)
            pb = self.next_psb()
            for j in range(gn):
                self.tr(pb[:, j * 128:(j + 1) * 128], P[:, (g0 + j) * 128:(g0 + j + 1) * 128], self.ident[:])
            PT = self.nx(kk["PT"])
            self.copy(PT[:, 0:gn * 128], pb[:, 0:gn * 128], eng="act")
            for j in range(gn):
                self.mm(po[:, 0:64], PT[:, j * 128:(j + 1) * 128], Vt[:, kt0 + g0 + j, :],
                        start=(g0 + j == 0), stop=(g0 + j == nkt - 1))
        self.ts(out, po[:, 0:64], st[:, 4:5], None, ALU.mult)

    def attn_keys(self, pfx):
        S = self.S
        return dict(st=self.rot(pfx + "sst", 2, [128, 8], F32), P=self.rot(pfx + "P", 1, [128, S], BF16),
                    PT=self.rot(pfx + "PTa", 2, [128, 1024], BF16))

    def dsa_setup(self, FMS, TMB, TMF, YT):
        S = self.S
        NT = S // 128
        c = dict(FMS=FMS, YT=YT)
        c["dk"] = self.sb("dk", [128, S], BF16)
        self.dma(c["dk"][:], V(FMS.h[4 * 128:5 * 128, :], FMS.name))
        c["ikr"] = self.sb("ikr", [128, S], BF16)
        self.dma(c["ikr"][:], V(FMS.h[7 * 128:8 * 128, :], FMS.name))
        c["qm"] = self.rot("dqm", 2, [128, 8, 128], BF16)
        c["Vt"] = self.sb("Vt", [128, NT, 64], BF16)
        self.dma(c["Vt"][:], V(TMB.h[:, 512:576].rearrange("(n p) c -> p n c", p=128), TMB.name))
        iw = self.sb("iw", [128, NT, 8], F32)
        self.dma(iw[:], V(TMF.h[:, 0:8].rearrange("(n p) c -> p n c", p=128), TMF.name))
        c["absw"] = self.sb("absw", [128, NT, 8], F32)
        self.act(c["absw"][:], iw[:], AF.Abs, scale=1.0 / 16)
        c["sgn"] = self.sb("sgn", [128, NT, 8], F32)
        self.ts(c["sgn"][:], iw[:], 0.0, 2.0, ALU.is_ge, ALU.mult)
        self.ts(c["sgn"][:], c["sgn"][:], -1.0, None, ALU.add)
        c["I"] = self.rot("I", 2, [128, S], F32)
        c["Ssb"] = self.rot("dSsb", 2, [128, S], F32)
        c["kk"] = self.attn_keys("d")
        c["q"] = self.rot("dqi", 2, [128, 4, 128], BF16)
        c["tmp"] = self.rot("tmpr", 2, [128, 512], F32)
        c["st"] = self.rot("dst", 2, [128, 16], F32)
        c["Rk"] = self.rot("Rk", 2, [128, 20], F32)
        c["nm"] = self.rot("dnm", 2, [128, 2], F32)
        c["c2"] = self.rot("dc2", 2, [128, 2], F32)
        c["o"] = self.rot("do", 2, [128, 256], F32)
        c["y"] = self.rot("dy", 2, [128, 256], BF16)
        c["yTs"] = self.rot("dyTs", 2, [128, 256], BF16)
        return c

    def dsa_tile(self, c, i):
        FMS = c["FMS"]
        I = self.nx(c["I"])
        Ssb = self.nx(c["Ssb"])
        nk = 128 * (i + 1)
        nkc = (nk + 511) // 512
        q = self.nx(c["q"])
        self.dma(q[:, 0:2, :], self.fm_rows(FMS, 2, 2, i * 128, (i + 1) * 128))
        self.dma(q[:, 2:4, :], self.fm_rows(FMS, 5, 2, i * 128, (i + 1) * 128))
        qm = self.nx(c["qm"])
        hm = V(self.meta.h[:, 13:17].unsqueeze(2).to_broadcast([128, 4, 128]), self.meta.name)
        for cc in range(2):
            self.tt(qm[:, 4 * cc:4 * cc + 4, :], V(q.h[:, 2 + cc, :].unsqueeze(1).to_broadcast([128, 4, 128]), q.name), hm,
                    ALU.mult, eng="pool")
        for kc in range(nkc):
            c0 = kc * 512
            cols = min(512, nk - c0)
            for h in range(8):
                ps = self.next_ps()
                self.mm(ps[:, 0:cols], qm[:, h, :], c["ikr"][:, c0:c0 + cols])
                tmp = self.nx(c["tmp"])
                self.act(tmp[:, 0:cols], ps[:, 0:cols], AF.Relu, scale=c["absw"][:, i, h:h + 1])
                if h == 0:
                    self.ts(I[:, c0:c0 + cols], tmp[:, 0:cols], c["sgn"][:, i, 0:1], None, ALU.mult)
                else:
                    self.stt(I[:, c0:c0 + cols], tmp[:, 0:cols], c["sgn"][:, i, h:h + 1], I[:, c0:c0 + cols], ALU.mult, ALU.add)
        if nk > self.n_keep:
            st = self.nx(c["st"])
            junk = self.nx(c["kk"]["P"])
            self.redabs(st[:, 0:1], I[:, 0:nk])
            self.ts(st[:, 0:1], st[:, 0:1], 1e-20, None, ALU.max)
            self.tt(I[:, nk - 128:nk], I[:, nk - 128:nk], self.cneg30[:], ALU.add)
            Rk = self.nx(c["Rk"])
            self.ts(Rk[:], self.rkc[:], st[:, 0:1], None, ALU.mult)
            self.ts(st[:, 1:2], st[:, 0:1], -1.0, None, ALU.mult)
            n1 = (nk // 2 + 127) // 128 * 128
            n2 = nk - n1
            thr_c = self.n_keep - 0.5 - n2 / 2.0
            for k in range(NBIS):
                nm = self.nx(c["nm"])
                c2 = self.nx(c["c2"])
                self.tt(nm[:, 0:1], st[:, 1:2], Rk[:, k:k + 1], ALU.add)
                self.act(Ssb[:, n1:nk], I[:, n1:nk], AF.Sign, bias=nm[:, 0:1], scale=-1.0, accum=c2[:, 0:1])
                self.ts(junk[:, 0:n1], I[:, 0:n1], nm[:, 0:1], None, ALU.is_ge, ALU.add, accum=st[:, 3:4])
                self.stt(st[:, 4:5], c2[:, 0:1], -0.5, st[:, 3:4], ALU.mult, ALU.add)
                self.ts(st[:, 4:5], st[:, 4:5], thr_c, None, ALU.is_ge)
                self.stt(st[:, 1:2], st[:, 4:5], Rk[:, k:k + 1], st[:, 1:2], ALU.mult, ALU.add)
            self.ts(I[:, 0:nk], I[:, 0:nk], st[:, 1:2], 1000.0, ALU.is_ge, ALU.mult)
        else:
            self.ts(I[:, 0:nk], I[:, 0:nk], 0.0, 1000.0, ALU.mult, ALU.add)
            self.tt(I[:, nk - 128:nk], I[:, nk - 128:nk], self.cneg2k[:], ALU.add)
        o = self.nx(c["o"])
        for h in range(4):
            base = 64 * (h % 2)
            cq = h // 2
            for kc in range(nkc):
                c0 = kc * 512
                cols = min(512, nk - c0)
                ps = self.next_ps()
                self.mm(ps[:, 0:cols], q[base:base + 64, cq, :], c["dk"][base:base + 64, c0:c0 + cols])
                self.stt(Ssb[:, c0:c0 + cols], ps[:, 0:cols], 0.125, I[:, c0:c0 + cols], ALU.mult, ALU.add)
            self.softmax_pv(Ssb[:, 0:nk], nk, c["Vt"], 0, o[:, 64 * h:64 * h + 64], c["kk"])
        y = self.nx(c["y"])
        self.copy(y[:], o[:], eng="act")
        self.store_yT(y, c["YT"], 1, i, c["yTs"])

    def nsa_setup(self, FMS, TMB, TMF, cmp_w1, cmp_w2, cmp_pos, YT):
        S = self.S
        NT = S // 128
        NB = self.NB
        NCP = self.NCP
        NC = (S - 32) // 16 + 1
        NCT = NCP // 128
        c = dict(FMS=FMS, YT=YT)
        c["ksT"] = self.sb("ksT", [128, S], BF16)
        self.dma(c["ksT"][:], V(FMS.h[11 * 128:12 * 128, :], FMS.name))
        c["kwT"] = self.sb("kwT", [128, S], BF16)
        self.dma(c["kwT"][:], V(FMS.h[12 * 128:13 * 128, :], FMS.name))
        c["Vs"] = self.sb("Vs", [128, NT, 64], BF16)
        self.dma(c["Vs"][:], V(TMB.h[:, 576:640].rearrange("(n p) c -> p n c", p=128), TMB.name))
        c["Vw"] = self.sb("Vw", [128, NT, 64], BF16)
        self.dma(c["Vw"][:], V(TMB.h[:, 640:704].rearrange("(n p) c -> p n c", p=128), TMB.name))
        c["ngt"] = self.sb("ngt", [128, NT, 12], F32)
        self.dma(c["ngt"][:], V(TMF.h[:, 8:20].rearrange("(n p) c -> p n c", p=128), TMF.name))
        kcmp = self.sb("kcmp", [128, NCP], BF16)
        vcmp = self.sb("vcmp", [128, NCT, 64], BF16)
        c["kcmp"] = kcmp
        c["vcmp"] = vcmp
        c["Ssb"] = self.sb("nSsb", [128, S], F32)
        c["Sw"] = self.sb("Sw", [128, 640], F32)
        c["kk"] = self.attn_keys("n")
        save = self.sb_cur
        srcT = self.sb("srcT", [128, S], BF16)
        w1 = self.sb("w1", [64, 32, 64], BF16)
        w2 = self.sb("w2", [64, 128], BF16)
        posT = self.sb("posT", [64, 32], F32)
        posb = self.sb("posb", [64, 32], BF16)
        cst = self.sb("cst", [64, 1], F32)
        u = self.sb("u", [64, NCP], F32)
        u2 = self.sb("u2", [64, NCP], F32)
        gl = self.sb("gl", [64, NCP], BF16)
        for i in range(2):
            self.dma(srcT[:], V(FMS.h[(10 + 3 * i) * 128:(11 + 3 * i) * 128, :], FMS.name))
            self.dma(w1[:], V(cmp_w1.h[i].rearrange("(l d) f -> d l f", d=64), cmp_w1.name), eng="pool")
            self.dma(w2[:, 0:64], V(cmp_w2.h[i], cmp_w2.name), eng="pool")
            self.dma(w2[:, 64:128], V(cmp_w2.h[i], cmp_w2.name), eng="pool")
            self.dma_s(posT[:], V(cmp_pos.h[i].rearrange("l d -> d l"), cmp_pos.name))
            self.copy(posb[:], posT[:])
            psc = self.next_ps()
            for l in range(32):
                self.mm(psc[0:64, 0:1], w1[:, l, :], posb[:, l:l + 1], start=(l == 0), stop=(l == 31))
            self.copy(cst[:], psc[0:64, 0:1])
            psh = self.next_ps()
            for l in range(32):
                self.mm(psh[0:64, 0:NC], w1[:, l, :], srcT[0:64, l:l + 16 * (NC - 1) + 1:16], start=(l == 0), stop=(l == 31))
            self.memset(u[:], 0.0)
            self.act(u[:, 0:NC], psh[0:64, 0:NC], AF.Identity, bias=cst[:, 0:1])
            self.tt(u2[:], u[:], u[:], ALU.mult)
            self.tt(u2[:], u2[:], u[:], ALU.mult)
            self.stt(u2[:], u2[:], 0.044715, u[:], ALU.mult, ALU.add)
            self.act(u2[:], u2[:], AF.Tanh, scale=0.7978845608028654)
            self.ts(u2[:], u2[:], 1.0, 0.5, ALU.add, ALU.mult)
            self.tt(gl[:], u2[:], u[:], ALU.mult)
            if i == 0:
                pso = self.next_ps()
                self.mm(pso[:, 0:NCP], w2[:, :], gl[:, :])
                self.copy(kcmp[:], pso[:, 0:NCP])
            else:
                for ct in range(NCT):
                    pso = self.next_ps()
                    self.mm(pso[:, 0:64], gl[:, ct * 128:(ct + 1) * 128], w2[:, 0:64])
                    self.copy(vcmp[:, ct, :], pso[:, 0:64])
        self.sb_cur = save
        self.P.barrier()
        c["q"] = self.rot("nqi", 2, [128, 2, 128], BF16)
        for nm, shp, dt_ in (("vis", [128, NCP], F32), ("pns", [128, NCP], F32), ("pn", [128, NCP], F32), ("Sc", [128, NCP], F32),
                             ("Pc", [128, NCP], F32), ("pnb", [128, NCP], BF16), ("PTc", [128, NCP], BF16), ("pnT", [128, NCP], F32),
                             ("cst2", [128, 8], F32), ("am", [128, NB], F32), ("imp", [128, NB], F32), ("imp2", [128, NB], F32), ("m8", [128, 16], F32),
                             ("selm", [128, NB], F32), ("oc", [128, 256], F32), ("os", [128, 256], F32), ("ow", [128, 256], F32),
                             ("gs", [128, 12], F32), ("o", [128, 256], F32), ("y", [128, 256], BF16), ("yTs", [128, 256], BF16)):
            c[nm] = self.rot("n" + nm, (1 if nm in ("pnT", "Pc", "vis", "imp2") else 2), shp, dt_)
        return c

    def nsa_tile(self, c, i):
        NB = self.NB
        NCP = self.NCP
        NCT = NCP // 128
        FMS = c["FMS"]
        Ssb = c["Ssb"]
        Sw = c["Sw"]
        kcmp = c["kcmp"]
        vcmp = c["vcmp"]

        def h4(t_):
            return V(t_.h[:, 0:256].rearrange("p (h e) -> p h e", h=4), t_.name)

        nk = 128 * (i + 1)
        nkc = (nk + 511) // 512
        nq = self.nx(c["q"])
        self.dma(nq[:], self.fm_rows(FMS, 8, 2, i * 128, (i + 1) * 128))
        vis = self.nx(c["vis"])
        self.memset(vis[:], 0.0)
        self.aselect(vis[:], vis[:], [[-16, NCP]], ALU.is_ge, -1000.0, 128 * i - 31, 1)
        pns = self.nx(c["pns"])
        oc = self.nx(c["oc"])
        osl = self.nx(c["os"])
        ow = self.nx(c["ow"])
        for h in range(4):
            base = 64 * (h % 2)
            cq = h // 2
            ps = self.next_ps()
            self.mm(ps[:, 0:NCP], nq[base:base + 64, cq, :], kcmp[base:base + 64, :])
            Sc = self.nx(c["Sc"])
            self.stt(Sc[:], ps[:, 0:NCP], 0.125, vis[:], ALU.mult, ALU.add)
            st = self.nx(c["cst2"])
            self.red(st[:, 0:1], Sc[:], ALU.max)
            self.ts(st[:, 0:1], st[:, 0:1], -500.0, -1.0, ALU.max, ALU.mult)
            Pc = self.nx(c["Pc"])
            self.act(Pc[:], Sc[:], AF.Exp, bias=st[:, 0:1], accum=st[:, 1:2])
            self.ts(st[:, 2:3], st[:, 1:2], 1e-30, None, ALU.max)
            self.recip(st[:, 3:4], st[:, 2:3])
            pn = pns if h == 0 else self.nx(c["pn"])
            self.ts(pn[:], Pc[:], st[:, 3:4], None, ALU.mult)
            pnb = self.nx(c["pnb"])
            self.copy(pnb[:], pn[:], eng="act")
            if h > 0:
                self.tt(pns[:], pns[:], pn[:], ALU.add, eng="pool")
            pb = self.next_psb()
            for ct in range(NCT):
                self.tr(pb[:, ct * 128:(ct + 1) * 128], pnb[:, ct * 128:(ct + 1) * 128], self.ident[:])
            PTc = self.nx(c["PTc"])
            self.copy(PTc[:], pb[:, 0:NCP], eng="act")
            po = self.next_ps()
            for ct in range(NCT):
                self.mm(po[:, 0:64], PTc[:, ct * 128:(ct + 1) * 128], vcmp[:, ct, :], start=(ct == 0), stop=(ct == NCT - 1))
            self.copy(oc[:, 64 * h:64 * h + 64], po[:, 0:64], eng="act")
        selm = self.nx(c["selm"])
        if NB > 16:
            pf = self.next_ps()
            for ct in range(NCT):
                self.tr(pf[:, ct * 128:(ct + 1) * 128], pns[:, ct * 128:(ct + 1) * 128], self.ident_f[:])
            pnT = self.nx(c["pnT"])
            self.copy(pnT[:], pf[:, 0:NCP], eng="act")
            pi = self.next_ps()
            for ct in range(NCT):
                self.mm(pi[:, 0:NB], pnT[:, ct * 128:(ct + 1) * 128], self.ovl[:, ct, :], start=(ct == 0), stop=(ct == NCT - 1))
            am = self.nx(c["am"])
            self.memset(am[:], 0.0)
            for half in range(2):
                cur = 2 * i + half
                r0 = 64 * half
                v_ = am[r0:r0 + 64, :]
                self.aselect(v_, v_, [[-1, NB]], ALU.is_ge, -1e30, cur, 0)
                self.memset(am[r0:r0 + 64, 0:1], 1e30)
                self.memset(am[r0:r0 + 64, cur:cur + 1], 1e30)
                if cur >= 1:
                    self.memset(am[r0:r0 + 64, cur - 1:cur], 1e30)
            imp = self.nx(c["imp"])
            self.tt(imp[:], pi[:, 0:NB], am[:], ALU.add)
            m8 = self.nx(c["m8"])
            self.vmax(m8[:, 0:8], imp[:])
            imp2 = self.nx(c["imp2"])
            self.match_replace(imp2[:], m8[:, 0:8], imp[:], -3.0e38)
            self.vmax(m8[:, 8:16], imp2[:])
            self.ts(selm[:], imp[:], m8[:, 15:16], 1000.0, ALU.is_ge, ALU.mult)
        else:
            self.memset(selm[:], 1000.0)
        for h in range(4):
            base = 64 * (h % 2)
            cq = h // 2
            for kc in range(nkc):
                c0 = kc * 512
                cols = min(512, nk - c0)
                nb_ = cols // 64
                ps = self.next_ps()
                self.mm(ps[:, 0:cols], nq[base:base + 64, cq, :], c["ksT"][base:base + 64, c0:c0 + cols])
                self.stt(V(Ssb.h[:, c0:c0 + cols].rearrange("p (b e) -> p b e", e=64), Ssb.name),
                         V(ps.h[:, 0:cols].rearrange("p (b e) -> p b e", e=64), ps.name), 0.125,
                         V(selm.h[:, c0 // 64:c0 // 64 + nb_].unsqueeze(2).to_broadcast([128, nb_, 64]), selm.name),
                         ALU.mult, ALU.add)
            self.tt(Ssb[:, nk - 128:nk], Ssb[:, nk - 128:nk], self.cneg2k[:], ALU.add)
            self.softmax_pv(Ssb[:, 0:nk], nk, c["Vs"], 0, osl[:, 64 * h:64 * h + 64], c["kk"])
        k0 = max(0, i * 128 - 512)
        nkw = nk - k0
        boff = 640 - nkw
        for h in range(4):
            base = 64 * (h % 2)
            cq = h // 2
            for c0 in range(0, nkw, 512):
                cols = min(512, nkw - c0)
                ps = self.next_ps()
                self.mm(ps[:, 0:cols], nq[base:base + 64, cq, :], c["kwT"][base:base + 64, k0 + c0:k0 + c0 + cols])
                self.stt(Sw[:, c0:c0 + cols], ps[:, 0:cols], 0.125, self.band[:, boff + c0:boff + c0 + cols], ALU.mult, ALU.add)
            self.softmax_pv(Sw[:, 0:nkw], nkw, c["Vw"], k0 // 128, ow[:, 64 * h:64 * h + 64], c["kk"])
        gs = self.nx(c["gs"])
        self.act(gs[:], c["ngt"][:, i, :], AF.Sigmoid)
        o = self.nx(c["o"])

        def gbc(j):
            return V(gs.h[:, j:12:3].unsqueeze(2).to_broadcast([128, 4, 64]), gs.name)
        self.tt(h4(o), h4(oc), gbc(0), ALU.mult)
        self.tt(h4(osl), h4(osl), gbc(1), ALU.mult)
        self.tt(o[:], o[:], osl[:], ALU.add)
        self.tt(h4(ow), h4(ow), gbc(2), ALU.mult)
        self.tt(o[:], o[:], ow[:], ALU.add)
        y = self.nx(c["y"])
        self.copy(y[:], o[:], eng="act")
        self.store_yT(y, c["YT"], 2, i, c["yTs"])

    def phase_dsa_nsa(self, FMS, TMB, TMF, cmp_w1, cmp_w2, cmp_pos, YT):
        S = self.S
        NT = S // 128
        self.phase_begin()
        cd = self.dsa_setup(FMS, TMB, TMF, YT)
        cn = self.nsa_setup(FMS, TMB, TMF, cmp_w1, cmp_w2, cmp_pos, YT)
        P = self.P
        for i in range(NT):
            self.ps_set = (0, 3)
            self.psb_set = (0, 1)
            P.capture = []
            self.dsa_tile(cd, i)
            A = P.capture
            self.ps_set = (3, 2)
            self.psb_set = (1, 1)
            P.capture = []
            self.nsa_tile(cn, i)
            B = P.capture
            P.capture = None
            self.ps_set = (0, 5)
            self.psb_set = (0, 2)
            ia = ib = 0
            na, nb = len(A), len(B)
            while ia < na or ib < nb:
                if ib >= nb or (ia < na and ia * nb <= ib * na):
                    P.add(*A[ia][0], **A[ia][1])
                    ia += 1
                else:
                    P.add(*B[ib][0], **B[ib][1])
                    ib += 1

    def phase_merge(self, x_in, xT_in, w_in_l, w_branch, w_out, ln_g, ln_b, YT, x_out, xT_out):
        S = self.S
        self.phase_begin()
        wg = self.load_w("wg", lambda k: V(w_in_l.h[k * 128:(k + 1) * 128, 3128:7224], w_in_l.name), NKC, 4096)
        wb = self.sb("wb", [128, 4, 2, 1024], BF16)
        for n in range(4):
            for kk_ in range(2):
                self.dma(wb[:, n, kk_, :], V(w_branch.h[n, kk_ * 128:(kk_ + 1) * 128, :], w_branch.name), eng="pool")
        wo = self.load_w("wo", lambda k: V(w_out.h[k * 128:(k + 1) * 128, :], w_out.name), NKC, D)
        g_bc, b_bc, scr = self.ln_setup(ln_g, ln_b)
        xt = self.sb("xT", [128, NKC, 512], BF16)
        yt = self.sb("yT", [128, 8, 512], BF16)
        mT = self.sb("mT", [128, 8, 512], BF16)
        acc_k = self.rot("acc", 2, [128, 512], F32)
        sg_k = self.rot("sg", 2, [128, 512], F32)
        tmp_k = self.rot("tmp", 2, [128, 512], F32)
        xr = [self.sb("xr%d" % i, [128, D], F32) for i in range(2)]
        rqs = [self.sb("r%d" % i, [128, D], F32) for i in range(2)]
        ntile = S // 512

        def load_xt(t_):
            ss_ = slice(t_ * 512, (t_ + 1) * 512)
            self.dma(xt[:], V(xT_in.h.rearrange("(k p) s -> p k s", p=128)[:, :, ss_], xT_in.name))
            self.dma(yt[:], V(YT.h[:, ss_].rearrange("(c p) s -> p c s", p=128), YT.name))

        def load_xq(idx):
            self.dma(xr[idx % 2][:], x_in[idx * 128:(idx + 1) * 128, :])
        load_xt(0)
        load_xq(0)
        for t in range(ntile):
            ss = slice(t * 512, (t + 1) * 512)
            for dc in range(8):
                acc = self.nx(acc_k)
                for n in range(4):
                    pg = self.next_ps()
                    for k in range(NKC):
                        self.mm(pg[:], wg[:, k, n * 1024 + dc * 128:n * 1024 + (dc + 1) * 128], xt[:, k, :],
                                start=(k == 0), stop=(k == NKC - 1))
                    pp = self.next_ps()
                    for k2 in range(2):
                        self.mm(pp[:], wb[:, n, k2, dc * 128:(dc + 1) * 128], yt[:, 2 * n + k2, :], start=(k2 == 0), stop=(k2 == 1))
                    sg = self.nx(sg_k)
                    self.act(sg[:], pg[:], AF.Sigmoid)
                    if n == 0:
                        self.tt(acc[:], sg[:], pp[:], ALU.mult)
                    else:
                        tmp = self.nx(tmp_k)
                        self.tt(tmp[:], sg[:], pp[:], ALU.mult)
                        self.tt(acc[:], acc[:], tmp[:], ALU.add, eng="pool")
                self.copy(mT[:, dc, :], acc[:], eng="act")
            if t + 1 < ntile:
                load_xt(t + 1)
            for q in range(4):
                t0 = t * 512 + q * 128
                xq = xr[q % 2]
                rq = rqs[q % 2]
                if t * 4 + q + 1 < ntile * 4:
                    load_xq(t * 4 + q + 1)
                for half in range(2):
                    hs = slice(half * 512, (half + 1) * 512)
                    pd = self.next_ps()
                    for dc in range(8):
                        self.mm(pd[:], mT[:, dc, q * 128:(q + 1) * 128], wo[:, dc, hs], start=(dc == 0), stop=(dc == 7))
                    self.act(xq[:, hs], xq[:, hs], AF.Copy, scale=ALPHA)
                    self.stt(rq[:, hs], pd[:], 1.0, xq[:, hs], ALU.mult, ALU.add)
                self.finish_tile(rq, g_bc, b_bc, x_out, xT_out, t0, scr)

    def phase_xattn(self, x_in, xT_in, mem, wq_d, wkv_d, wo_d, ln_g, ln_b, x_out, xT_out):
        S = self.S
        self.phase_begin()
        wq = self.load_w("wq", lambda k: V(wq_d.h[k * 128:(k + 1) * 128, :], wq_d.name), NKC, D)
        wkv = self.load_w("wkv", lambda k: V(wkv_d.h[k * 128:(k + 1) * 128, :], wkv_d.name), NKC, 2 * D)
        wo = self.load_w("wo", lambda k: V(wo_d.h[k * 128:(k + 1) * 128, :], wo_d.name), NKC, D)
        g_bc, b_bc, scr = self.ln_setup(ln_g, ln_b)
        memT = self.sb("memT", [128, 8, 256], BF16)
        mr = self.sb("mr", [128, D], F32)
        mb = self.sb("mb", [128, D], BF16)
        for mt in range(2):
            self.dma(mr[:], mem[mt * 128:(mt + 1) * 128, :])
            self.copy(mb[:], mr[:], eng="act")
            pb = self.next_psb()
            for k in range(8):
                self.tr(pb[:, k * 128:(k + 1) * 128], mb[:, k * 128:(k + 1) * 128], self.ident[:])
            self.copy(memT[:, :, mt * 128:(mt + 1) * 128], V(pb.h[:, :].rearrange("p (k t) -> p k t", k=8), pb.name))
        KT = self.sb("KT", [128, 8, 256], BF16)
        for c in range(8):
            ps = self.next_ps()
            for k in range(NKC):
                self.mm(ps[:, 0:256], wkv[:, k, c * 128:(c + 1) * 128], memT[:, k, :], start=(k == 0), stop=(k == NKC - 1))
            self.copy(KT[:, c, :], ps[:, 0:256], eng=("act" if c % 2 else "dve"))
        Vm = self.sb("Vm", [128, 2, D], BF16)
        for mt in range(2):
            for half in range(2):
                ps = self.next_ps()
                for k in range(NKC):
                    self.mm(ps[:], memT[:, k, mt * 128:(mt + 1) * 128], wkv[:, k, D + half * 512:D + (half + 1) * 512],
                            start=(k == 0), stop=(k == NKC - 1))
                self.copy(Vm[:, mt, half * 512:(half + 1) * 512], ps[:], eng=("act" if half else "dve"))
        xt = self.sb("xT", [128, NKC, 512], BF16)
        qT = self.sb("qT", [128, 8, 512], BF16)
        Pf_k = self.rot("Pf", 2, [128, 4, 256], F32)
        Pb_k = self.rot("Pb", 2, [128, 4, 256], BF16)
        PT_k = self.rot("PTx", 2, [128, 8, 128], BF16)
        oT_k = self.rot("oT", 2, [128, 8, 128], BF16)
        st_k = self.rot("xst", 2, [128, 16], F32)
        xr = [self.sb("xr%d" % i, [128, D], F32) for i in range(2)]
        rqs = [self.sb("r%d" % i, [128, D], F32) for i in range(2)]
        SC = 1.0 / 16
        ntile = S // 512

        def load_xt(t_):
            ss_ = slice(t_ * 512, (t_ + 1) * 512)
            self.dma(xt[:], V(xT_in.h.rearrange("(k p) s -> p k s", p=128)[:, :, ss_], xT_in.name))

        def load_xq(idx):
            self.dma(xr[idx % 2][:], x_in[idx * 128:(idx + 1) * 128, :])
        load_xt(0)
        load_xq(0)
        for t in range(ntile):
            ss = slice(t * 512, (t + 1) * 512)
            for c in range(8):
                ps = self.next_ps()
                for k in range(NKC):
                    self.mm(ps[:], wq[:, k, c * 128:(c + 1) * 128], xt[:, k, :], start=(k == 0), stop=(k == NKC - 1))
                self.copy(qT[:, c, :], ps[:], eng=("act" if c % 2 else "dve"))
            if t + 1 < ntile:
                load_xt(t + 1)
            for q in range(4):
                t0 = t * 512 + q * 128
                tq = slice(q * 128, (q + 1) * 128)
                if t * 4 + q + 1 < ntile * 4:
                    load_xq(t * 4 + q + 1)
                pss = [self.next_ps(), self.next_ps()]
                st = self.nx(st_k)
                Pf = self.nx(Pf_k)
                for h in range(4):
                    pv = pss[h // 2][:, (h % 2) * 256:(h % 2) * 256 + 256]
                    for cc in range(2):
                        self.mm(pv, qT[:, 2 * h + cc, tq], KT[:, 2 * h + cc, :], start=(cc == 0), stop=(cc == 1))
                    self.red(st[:, h:h + 1], pv, ALU.max)
                    self.ts(st[:, 4 + h:5 + h], st[:, h:h + 1], -SC, None, ALU.mult)
                    self.act(Pf[:, h, :], pv, AF.Exp, bias=st[:, 4 + h:5 + h], scale=SC, accum=st[:, 8 + h:9 + h])
                self.recip(st[:, 12:16], st[:, 8:12])
                Pb = self.nx(Pb_k)
                self.tt(Pb[:], Pf[:], V(st.h[:, 12:16].unsqueeze(2).to_broadcast([128, 4, 256]), st.name), ALU.mult)
                pb = self.next_psb()
                for h in range(4):
                    for mc in range(2):
                        j = 2 * h + mc
                        self.tr(pb[:, j * 128:(j + 1) * 128], Pb[:, h, mc * 128:(mc + 1) * 128], self.ident[:])
                PT = self.nx(PT_k)
                self.copy(PT[:], V(pb.h[:, :].rearrange("p (j t) -> p j t", j=8), pb.name))
                oT = self.nx(oT_k)
                pso = [self.next_ps(), self.next_ps()]
                for h in range(4):
                    for dc in range(2):
                        j = 2 * h + dc
                        pv = pso[j // 4][:, (j % 4) * 128:(j % 4) * 128 + 128]
                        for mc in range(2):
                            self.mm(pv, Vm[:, mc, h * 256 + dc * 128:h * 256 + (dc + 1) * 128], PT[:, 2 * h + mc, :],
                                    start=(mc == 0), stop=(mc == 1))
                for j4 in range(2):
                    self.copy(oT[:, 4 * j4:4 * j4 + 4, :], V(pso[j4].h[:, :].rearrange("p (j t) -> p j t", j=4), pso[j4].name),
                              eng=("act" if j4 else "dve"))
                xq = xr[q % 2]
                rq = rqs[q % 2]
                for half in range(2):
                    hs = slice(half * 512, (half + 1) * 512)
                    pd = self.next_ps()
                    for c in range(8):
                        self.mm(pd[:], oT[:, c, :], wo[:, c, hs], start=(c == 0), stop=(c == 7))
                    self.act(xq[:, hs], xq[:, hs], AF.Copy, scale=ALPHA)
                    self.stt(rq[:, hs], pd[:], 1.0, xq[:, hs], ALU.mult, ALU.add)
                self.finish_tile(rq, g_bc, b_bc, x_out, xT_out, t0, scr)


OFF = dict(r_q=0, r_k=128, r_v=256, r_g=512, d_q=768, d_k=1024, d_v=1088, i_q=1152, i_k=1408, i_w=1440,
           n_q=1448, n_kc=1704, n_vc=1768, n_ks=1832, n_vs=1896, n_kw=1960, n_vw=2024, n_g=2088,
           s_z=2100, s_xbc=2356, s_dt=3124, br_g=3128)


def _partner(i, headdim, rot):
    half = rot // 2
    j = i % headdim
    b = i - j
    if j < half:
        return b + j + half
    if j < rot:
        return b + j - half
    return i


def build_colidx():
    cols = []

    def roped(name, width, headdim, rot, lo=0, rep=1):
        loc = []
        for r in range(rep):
            loc += list(range(lo, lo + width))
        assert len(loc) == 128
        a = [OFF[name] + i for i in loc]
        b = [OFF[name] + _partner(i, headdim, rot) for i in loc]
        cols.extend(a)
        cols.extend(b)

    roped("r_q", 128, 32, 32)
    roped("r_k", 128, 32, 32)
    roped("d_q", 128, 64, 16, 0)
    roped("d_q", 128, 64, 16, 128)
    roped("d_k", 64, 64, 16, 0, 2)
    roped("i_q", 128, 32, 8, 0)
    roped("i_q", 128, 32, 8, 128)
    roped("i_k", 32, 32, 8, 0, 4)
    roped("n_q", 128, 64, 16, 0)
    roped("n_q", 128, 64, 16, 128)
    roped("n_kc", 64, 64, 16, 0, 2)
    roped("n_ks", 64, 64, 16, 0, 2)
    roped("n_kw", 64, 64, 16, 0, 2)
    cols.extend([OFF["n_vc"] + i for i in range(64)] * 2)
    cols.extend([OFF["s_xbc"] + i for i in range(768)])
    for name, w in (("r_v", 256), ("r_g", 256), ("d_v", 64), ("n_vs", 64), ("n_vw", 64), ("s_z", 256),
                    ("i_w", 8), ("n_g", 12), ("s_dt", 4)):
        cols.extend([OFF[name] + i for i in range(w)])
    return np.asarray(cols, dtype=np.int64)


ROPED_TABLES = [0, 1, 2, 2, 2, 3, 3, 3, 2, 2, 2, 2, 2]
TM0 = (2 * len(ROPED_TABLES) + 7) * 128
NCOL2 = TM0 + 984
NBIS = 17
RET_LNG = [math.log1p(-2.0 ** (-5 - h)) for h in range(4)]


def host_consts(S):
    meta = np.zeros((128, 32), np.float32)

    def fill(t, headdim, rot, theta, scale):
        half = rot // 2
        inv = np.power(np.float32(theta), (-2.0 * np.arange(half, dtype=np.float32) / np.float32(rot)).astype(np.float32)).astype(np.float32)
        for p in range(128):
            i = p % headdim
            if i < rot:
                meta[p, t] = inv[i % half]
                meta[p, 4 + t] = scale
                meta[p, 8 + t] = -scale if i < half else scale
            else:
                meta[p, t] = 0.0
                meta[p, 4 + t] = 1.0
                meta[p, 8 + t] = 0.0

    fill(0, 32, 32, 10000.0, 1.0)
    fill(1, 32, 32, 10000.0, 32.0 ** -0.5)
    fill(2, 64, 16, 500000.0, 1.0)
    fill(3, 32, 8, 500000.0, 1.0)
    for p in range(128):
        meta[p, 12] = RET_LNG[p // 32]
        meta[p, 13 + p // 32] = 1.0
    bdm = np.zeros((128, 256), np.float32)
    for p in range(128):
        bdm[p, 64 * (p // 32):64 * (p // 32) + 64] = 1.0
    NC = (S - 32) // 16 + 1
    NCP = (NC + 127) // 128 * 128
    NB = S // 64
    ovl = np.zeros((NCP, NB), np.float32)
    for c in range(NC):
        for j in range(NB):
            ovl[c, j] = max(min(16 * c + 32, 64 * j + 64) - max(16 * c, 64 * j), 0) / 32.0
    return meta, bdm, ovl, NB, NCP


STAGES = ["ffn1", "inproj", "ret", "ssd", "dsa", "nsa", "merge", "xattn", "ffn2"]


def build(S, depth=DEPTH, stop_after=None):
    kb = KB(S, depth, stop_after)
    meta_np, bdm_np, ovl_np, NB, NCP = host_consts(S)
    kb.n_keep = min(256, S // 4)
    EI = "ExternalInput"
    x = kb.dram("x", [S, D], F32, kind=EI)
    mem = kb.dram("mem", [N_MEM, D], F32, kind=EI)
    ln_g = kb.dram("ln_g", [DEPTH, 4, D], F32, kind=EI)
    ln_b = kb.dram("ln_b", [DEPTH, 4, D], F32, kind=EI)
    f1gu = kb.dram("ffn1_w_gu", [DEPTH, D, 2 * DFF], F32, kind=EI)
    f1dn = kb.dram("ffn1_w_down", [DEPTH, DFF, D], F32, kind=EI)
    w_in = kb.dram("w_in", [DEPTH, D, 7224], F32, kind=EI)
    w2 = kb.dram("w2", [DEPTH, D, NCOL2], F32, kind=EI)
    cmp_w1 = kb.dram("cmp_w1", [DEPTH, 2, 2048, 64], F32, kind=EI)
    cmp_w2 = kb.dram("cmp_w2", [DEPTH, 2, 64, 64], F32, kind=EI)
    cmp_pos = kb.dram("cmp_pos", [DEPTH, 2, 32, 64], F32, kind=EI)
    conv_w = kb.dram("conv_w", [DEPTH, 4, 768], F32, kind=EI)
    conv_b = kb.dram("conv_b", [DEPTH, 768], F32, kind=EI)
    dt_bias = kb.dram("dt_bias", [DEPTH, 4], F32, kind=EI)
    a_log = kb.dram("a_log", [DEPTH, 4], F32, kind=EI)
    d_skip = kb.dram("d_skip", [DEPTH, 4], F32, kind=EI)
    norm_g = kb.dram("ssm_norm_g", [DEPTH, 256], F32, kind=EI)
    w_branch = kb.dram("w_branch", [DEPTH, 4, 256, D], F32, kind=EI)
    w_out = kb.dram("w_out", [DEPTH, D, D], F32, kind=EI)
    xwq = kb.dram("xattn_wq", [DEPTH, D, D], F32, kind=EI)
    xwkv = kb.dram("xattn_wkv", [DEPTH, D, 2 * D], F32, kind=EI)
    xwo = kb.dram("xattn_wo", [DEPTH, D, D], F32, kind=EI)
    f2gu = kb.dram("ffn2_w_gu", [DEPTH, D, 2 * DFF], F32, kind=EI)
    f2dn = kb.dram("ffn2_w_down", [DEPTH, DFF, D], F32, kind=EI)
    meta = kb.dram("meta", [128, 32], F32, kind=EI)
    bdm = kb.dram("bdm", [128, 256], F32, kind=EI)
    ovl = kb.dram("ovl", [NCP, NB], F32, kind=EI)
    out = kb.dram("out", [S, D], F32)
    xTa = kb.dram("xTa", [D, S], BF16)
    xTb = kb.dram("xTb", [D, S], BF16)
    xa = kb.dram("xa", [S, D], F32)
    xb2 = kb.dram("xb2", [S, D], F32)
    FMS = kb.dram("FMS", [20 * 128, S], BF16)
    TMB = kb.dram("TMB", [S, 960], BF16)
    TMF = kb.dram("TMF", [S, 24], F32)
    YT = kb.dram("YT", [1024, S], BF16)
    ROPE = kb.dram("ROPE", [4, 2, 128, S], F32)
    kb.setup()
    kb.setup_consts(meta, bdm, ovl, NB, NCP)
    kb.phase_rope(ROPE)
    kb.phase_transpose_in(x, xTa)

    def L(t, *idx):
        return T(t.h[idx], t.name, True)

    done = False
    xin = x
    for l in range(depth):
        last = (l == depth - 1)

        def stop(name):
            return stop_after == (l, name)
        kb.phase_ffn(xin, xTa, L(f1gu, l), L(f1dn, l), L(ln_g, l, 0), L(ln_b, l, 0), xa, xTb)
        if stop("ffn1"):
            break
        kb.phase_inproj(xTb, L(w2, l), ROPE, FMS, TMB, TMF)
        if stop("inproj"):
            break
        kb.phase_ret(FMS, TMB, YT)
        if stop("ret"):
            break
        kb.phase_ssd(FMS, TMB, TMF, L(conv_w, l), L(conv_b, l), L(dt_bias, l), L(a_log, l), L(d_skip, l), L(norm_g, l), YT)
        if stop("ssd"):
            break
        kb.phase_dsa_nsa(FMS, TMB, TMF, L(cmp_w1, l), L(cmp_w2, l), L(cmp_pos, l), YT)
        if stop("nsa") or stop("dsa"):
            break
        kb.phase_merge(xa, xTb, L(w_in, l), L(w_branch, l), L(w_out, l), L(ln_g, l, 1), L(ln_b, l, 1), YT, xb2, xTa)
        if stop("merge"):
            break
        kb.phase_xattn(xb2, xTa, mem, L(xwq, l), L(xwkv, l), L(xwo, l), L(ln_g, l, 2), L(ln_b, l, 2), xa, xTb)
        if stop("xattn"):
            break
        kb.phase_ffn(xa, xTb, L(f2gu, l), L(f2dn, l), L(ln_g, l, 3), L(ln_b, l, 3), out if last else xb2, None if last else xTa)
        if stop("ffn2"):
            break
        xin = xb2
    kb.flush_pending()
    st = kb.P.emit()
    kb.stats = st
    return kb


def make_in_maps(inputs, S, ncores):
    meta_np, bdm_np, ovl_np, NB, NCP = host_consts(S)
    colidx = build_colidx()
    w_in = np.asarray(inputs["w_in"], dtype=np.float32)
    w2 = np.ascontiguousarray(w_in[:, :, colidx])
    shared = {k: np.ascontiguousarray(np.asarray(v, dtype=np.float32)) for k, v in inputs.items() if k not in ("x", "mem")}
    shared["w2"] = w2
    shared["meta"] = meta_np
    shared["bdm"] = bdm_np
    shared["ovl"] = ovl_np
    maps = []
    for b in range(ncores):
        m = dict(shared)
        m["x"] = np.ascontiguousarray(np.asarray(inputs["x"][b, :S], dtype=np.float32))
        m["mem"] = np.ascontiguousarray(np.asarray(inputs["mem"][b], dtype=np.float32))
        maps.append(m)
    return maps


def kernel(**inputs):
    S = inputs["x"].shape[1]
    B = inputs["x"].shape[0]
    kb = build(S)
    maps = make_in_maps(inputs, S, B)
    res = run_bass_kernel_spmd(kb.nc, maps, core_ids=list(range(B)))
    out = np.stack([np.asarray(r["out"], dtype=np.float32) for r in res.results], axis=0)
    return out
```

```python
import math
import sys
import numpy as np
import concourse.bass as bass
import concourse.mybir as mybir
from concourse.bass_utils import run_bass_kernel_spmd

F32 = mybir.dt.float32
BF16 = mybir.dt.bfloat16
I32 = mybir.dt.int32
AF = mybir.ActivationFunctionType
ALU = mybir.AluOpType
AX = mybir.AxisListType

SEM_LIMIT = 30000
N_DMA_SEMS = 24


class Buf:
    __slots__ = ("name", "last_w", "readers")

    def __init__(self, name):
        self.name = name
        self.last_w = None
        self.readers = []


class Op:
    __slots__ = ("eng", "fn", "deps", "need_inc", "sem", "val", "is_dma", "idx", "tag", "odeps", "n", "seg", "pfirst", "fin", "st0", "crit")


class Prog:
    def __init__(self, nc):
        self.nc = nc
        self.engs = {"pe": nc.tensor, "act": nc.scalar, "dve": nc.vector, "pool": nc.gpsimd, "sp": nc.sync}
        self.ops = []
        self.bufs = {}
        self.last_on = {}
        self.dmas_since = []
        self.phase_deps = []
        self.phase_bufs = set()
        self.capture = None
        self.seg = 0
        self.do_sched = True
        self.est_time = 0.0

    def buf(self, name):
        b = self.bufs.get(name)
        if b is None:
            b = self.bufs[name] = Buf(name)
        return b

    def add(self, eng, fn, reads=(), writes=(), dma=False, extra_deps=(), n=64):
        if self.capture is not None:
            self.capture.append(((eng, fn), dict(reads=list(reads), writes=list(writes), dma=dma, n=n)))
            return None
        op = Op()
        op.eng = eng
        op.fn = fn
        op.is_dma = dma
        op.need_inc = False
        op.sem = None
        op.val = 0
        op.n = n
        op.seg = self.seg
        op.pfirst = False
        op.fin = 0.0
        op.idx = len(self.ops)
        try:
            op.tag = (sys._getframe(2).f_lineno, 0)
        except Exception:
            op.tag = (0, 0)
        deps = {}
        for b in reads:
            b = self.buf(b)
            w = b.last_w
            if w is not None:
                deps[w.idx] = (w, "raw")
        for b in writes:
            b = self.buf(b)
            w = b.last_w
            if w is not None and w.idx not in deps:
                deps[w.idx] = (w, "waw")
            for r in b.readers:
                if r.idx not in deps:
                    deps[r.idx] = (r, "war")
        real = []
        order = []
        for d, kind in deps.values():
            if (not d.is_dma) and d.eng == eng and not dma:
                if eng == "pe" or kind != "raw":
                    order.append(d)
                    continue
            real.append(d)
        for d in extra_deps:
            real.append(d)
        for b in list(reads) + list(writes):
            if b not in self.phase_bufs:
                self.phase_bufs.add(b)
                op.pfirst = True
        op.deps = real
        op.odeps = order
        for b in writes:
            b = self.buf(b)
            b.last_w = op
            b.readers = []
        for b in reads:
            self.buf(b).readers.append(op)
        self.ops.append(op)
        return op

    def barrier(self):
        self.seg += 1
        self.phase_bufs = set()

    def _cost(self, op):
        n = op.n
        e = op.eng
        if op.is_dma:
            return 0.08, 2.0 + n / 100e3
        if e == "pe":
            c = 0.035 + n / 2400.0
        elif e == "act":
            c = 0.22 + n / 1200.0
        elif e == "dve":
            c = 0.08 + n / 960.0
        elif e == "pool":
            c = 0.15 + n / 500.0
        else:
            c = 0.05
        return c, c

    def schedule(self):
        import heapq
        SCHED = self.do_sched
        self.seg_stats = []
        order = []
        ops = self.ops
        nseg = self.seg + 1
        segs = [[] for _ in range(nseg)]
        for op in ops:
            segs[op.seg].append(op)
        t_base = 0.0
        engs = list(self.engs.keys())
        for sg in segs:
            if not sg:
                continue
            if not SCHED:
                order.extend(sg)
                continue
            inseg = set(id(o) for o in sg)
            indeg = {}
            succ = {}
            dr = {}
            for op in sg:
                cnt = 0
                for d in op.deps + op.odeps:
                    if id(d) in inseg:
                        cnt += 1
                        succ.setdefault(id(d), []).append(op)
                indeg[id(op)] = cnt
                dr[id(op)] = t_base
            wait_h = {e: [] for e in engs}
            rdy_h = {e: [] for e in engs}
            free = {e: t_base for e in engs}
            for op in sg:
                if indeg[id(op)] == 0:
                    heapq.heappush(wait_h[op.eng], (dr[id(op)], op.idx, op))
            left = len(sg)
            tmax = t_base
            while left:
                best = None
                for e in engs:
                    wh = wait_h[e]
                    rh = rdy_h[e]
                    fe = free[e]
                    while wh and wh[0][0] <= fe:
                        _, ix, o = heapq.heappop(wh)
                        heapq.heappush(rh, (ix, o))
                    if rh:
                        cand = (fe, rh[0][0], e, 0)
                    elif wh:
                        cand = (wh[0][0], wh[0][1], e, 1)
                    else:
                        continue
                    if best is None or cand[:2] < best[:2]:
                        best = cand
                start, _, e, which = best
                if which == 0:
                    _, op = heapq.heappop(rdy_h[e])
                else:
                    _, _, op = heapq.heappop(wait_h[e])
                busy, lat = self._cost(op)
                free[e] = start + busy
                op.fin = start + lat
                op.st0 = start
                if op.fin > tmax:
                    tmax = op.fin
                order.append(op)
                left -= 1
                for sc in succ.get(id(op), ()):
                    k = id(sc)
                    extra = 0.05 if (sc.eng == op.eng and not op.is_dma) else 0.35
                    t = op.fin + extra
                    if t > dr[k]:
                        dr[k] = t
                    indeg[k] -= 1
                    if indeg[k] == 0:
                        heapq.heappush(wait_h[sc.eng], (dr[k], sc.idx, sc))
            busy_e = {e: 0.0 for e in engs}
            for o in sg:
                busy_e[o.eng] += self._cost(o)[0]
            self.seg_stats.append((sg[0].seg, len(sg), tmax - t_base, busy_e))
            t_base = tmax
        self.est_time = t_base
        return order

    def emit(self, final_wait_eng="sp"):
        nc = self.nc
        order = self.schedule()
        last_eng = {}
        prev_last = {}
        prev_dmas = []
        older_dmas = []
        cur_dmas = []
        cur_seg = -1
        for op in order:
            if op.seg != cur_seg:
                cur_seg = op.seg
                prev_last = dict(last_eng)
                prev_dmas = older_dmas + cur_dmas
                older_dmas = cur_dmas
                cur_dmas = []
            if op.pfirst:
                op.deps = op.deps + list(prev_last.values()) + prev_dmas
            if op.is_dma:
                cur_dmas.append(op)
            else:
                last_eng[op.eng] = op
        for op in order:
            for d in op.deps:
                d.need_inc = True
            if op.is_dma:
                op.need_inc = True
        eng_sem = {}
        eng_cnt = {}
        dma_sems = [nc.alloc_semaphore("dq%d" % i) for i in range(N_DMA_SEMS)]
        dma_cnt = [0] * N_DMA_SEMS
        dma_last = [None] * N_DMA_SEMS
        ndma = 0
        for op in order:
            if not op.need_inc:
                continue
            if op.is_dma:
                j = ndma % N_DMA_SEMS
                ndma += 1
                if dma_last[j] is not None:
                    op.deps.append(dma_last[j])
                dma_cnt[j] += 16
                op.sem = dma_sems[j]
                op.val = dma_cnt[j]
                dma_last[j] = op
            else:
                e = op.eng
                if e not in eng_sem or eng_cnt[e] >= SEM_LIMIT:
                    eng_sem[e] = nc.alloc_semaphore("s_%s_%d" % (e, op.idx))
                    eng_cnt[e] = 0
                eng_cnt[e] += 1
                op.sem = eng_sem[e]
                op.val = eng_cnt[e]
        waited = {}
        nwaits = 0
        for op in order:
            E = self.engs[op.eng]
            need = {}
            for d in op.deps:
                k = id(d.sem)
                if k not in need or need[k][1] < d.val:
                    need[k] = (d.sem, d.val)
            for k, (sem, val) in need.items():
                wk = (op.eng, k)
                if waited.get(wk, 0) >= val:
                    continue
                E.wait_ge(sem, val)
                nwaits += 1
                waited[wk] = val
            try:
                inst = op.fn()
            except Exception:
                print('EMIT FAIL at op', op.idx, op.eng)
                raise
            if op.need_inc:
                inst.then_inc(op.sem, 16 if op.is_dma else 1)
        E = self.engs[final_wait_eng]
        for j in range(N_DMA_SEMS):
            if dma_cnt[j] > 0:
                E.wait_ge(dma_sems[j], dma_cnt[j])
        self.stats = dict(n_ops=len(self.ops), n_waits=nwaits, n_dma=ndma,
                          n_inc=sum(1 for o in self.ops if o.need_inc), est_ms=self.est_time / 1e3)
        return self.stats


class V:
    __slots__ = ("ap", "b")

    def __init__(self, ap, b):
        self.ap = ap
        self.b = b


class T:
    def __init__(self, h, name, dram=False):
        self.h = h
        self.name = name
        self.dram = dram

    def __getitem__(self, idx):
        if self.dram:
            return V(self.h[idx], self.name)
        return V(self.h[idx], self.name)

    def v(self, ap):
        return V(ap, self.name)


DT_SIZE = {F32: 4, BF16: 2, I32: 4}

D = 1024
DFF = 2816
NKC = D // 128
NFC = DFF // 128
LN_EPS = 1e-5
DEPTH = 2
ALPHA = (2 * DEPTH) ** 0.25
N_MEM = 256


class KB:
    def __init__(self, S, depth=DEPTH, stop_after=None, debug=()):
        self.S = S
        self.depth = depth
        self.stop_after = stop_after
        self.debug = debug
        self.nc = bass.Bass("TRN2", target_bir_lowering=False)
        self.P = Prog(self.nc)
        self.uid = 0
        self.sb_base = 0
        self.sb_cur = 0
        self.outs = {}
        self.arena = None
        self.rots = {}
        self.pending_T = None
        self.xb_cnt = 0
        self.ps_set = (0, 5)
        self.psb_set = (0, 2)
        self.fill_regs = {}
        self.n_keep = 256

    def sb(self, name, shape, dtype):
        nbytes = int(np.prod(shape[1:])) * DT_SIZE[dtype]
        nbytes = (nbytes + 63) // 64 * 64
        off = self.sb_cur
        self.sb_cur += nbytes
        assert self.sb_cur <= 207 * 1024, ("SBUF overflow", name, self.sb_cur)
        self.uid += 1
        if self.arena is None:
            self.arena = self.nc.alloc_sbuf_tensor("arena", [128, 207 * 1024], mybir.dt.uint8)
        ap = self.arena[:, off:off + int(np.prod(shape[1:])) * DT_SIZE[dtype]].bitcast(dtype)
        if len(shape) == 3:
            ap = ap.rearrange("p (a b) -> p a b", a=shape[1])
        elif len(shape) == 4:
            ap = ap.rearrange("p (a b c) -> p a b c", a=shape[1], b=shape[2])
        if shape[0] < 128:
            ap = ap[0:shape[0]]
        return T(ap, "%s_%d" % (name, self.uid))

    def phase_begin(self):
        self.flush_pending()
        self.P.barrier()
        self.sb_cur = self.sb_base

    def dram(self, name, shape, dtype, kind="ExternalOutput"):
        h = self.nc.dram_tensor(name, list(shape), dtype, kind=kind)
        return T(h.ap(), name, dram=True)

    def _rw(self, reads, writes):
        return [r.b for r in reads if isinstance(r, V)], [w.b for w in writes]

    def dma(self, out, in_, eng="sp"):
        nc = self.nc
        E = self.P.engs[eng]
        return self.P.add(eng, lambda: E.dma_start(out=out.ap, in_=in_.ap), reads=[in_.b], writes=[out.b], dma=True,
                          n=int(np.prod(out.ap.shape)) * 2)

    def mm(self, out, lhsT, rhs, start=True, stop=True):
        nc = self.nc
        return self.P.add("pe", lambda: nc.tensor.matmul(out.ap, lhsT.ap, rhs.ap, start=start, stop=stop),
                          reads=[lhsT.b, rhs.b], writes=[out.b], n=int(np.prod(out.ap.shape[1:])) * (4 if lhsT.ap.dtype == F32 else 1))

    def tr(self, out, in_, ident):
        nc = self.nc
        return self.P.add("pe", lambda: nc.tensor.transpose(out.ap, in_.ap, ident.ap),
                          reads=[in_.b, ident.b], writes=[out.b], n=200)

    def act(self, out, in_, func, bias=None, scale=None, accum=None, eng="act"):
        nc = self.nc
        kw = {}
        reads = [in_.b]
        writes = [out.b]
        if bias is not None:
            if isinstance(bias, V):
                kw["bias"] = bias.ap
                reads.append(bias.b)
            else:
                kw["bias"] = bias
        if scale is not None:
            if isinstance(scale, V):
                kw["scale"] = scale.ap
                reads.append(scale.b)
            else:
                kw["scale"] = scale
        if accum is not None:
            kw["accum_out"] = accum.ap
            writes.append(accum.b)
        return self.P.add("act", lambda: nc.scalar.activation(out=out.ap, in_=in_.ap, func=func, **kw),
                          reads=reads, writes=writes, n=int(np.prod(out.ap.shape[1:])))

    def ts(self, out, in0, s1, s2, op0, op1=None, accum=None, eng="dve"):
        E = self.P.engs[eng]
        reads = [in0.b]
        writes = [out.b]
        a1 = s1
        a2 = s2
        if isinstance(s1, V):
            a1 = s1.ap
            reads.append(s1.b)
        if isinstance(s2, V):
            a2 = s2.ap
            reads.append(s2.b)
        kw = {}
        if op1 is not None:
            kw["op1"] = op1
        if accum is not None:
            kw["accum_out"] = accum.ap
            writes.append(accum.b)
        return self.P.add(eng, lambda: E.tensor_scalar(out=out.ap, in0=in0.ap, scalar1=a1, scalar2=a2, op0=op0, **kw),
                          reads=reads, writes=writes, n=int(np.prod(out.ap.shape[1:])))

    def tt(self, out, in0, in1, op, eng="dve"):
        E = self.P.engs[eng]
        return self.P.add(eng, lambda: E.tensor_tensor(out=out.ap, in0=in0.ap, in1=in1.ap, op=op),
                          reads=[in0.b, in1.b], writes=[out.b], n=int(np.prod(out.ap.shape[1:])))

    def stt(self, out, in0, scalar, in1, op0, op1, accum=None):
        nc = self.nc
        reads = [in0.b, in1.b]
        writes = [out.b]
        a = scalar
        if isinstance(scalar, V):
            a = scalar.ap
            reads.append(scalar.b)
        kw = {}
        if accum is not None:
            kw["accum_out"] = accum.ap
            writes.append(accum.b)
        return self.P.add("dve", lambda: nc.vector.scalar_tensor_tensor(out=out.ap, in0=in0.ap, scalar=a, in1=in1.ap,
                                                                     op0=op0, op1=op1, **kw),
                          reads=reads, writes=writes, n=int(np.prod(out.ap.shape[1:])))

    def copy(self, out, in_, eng="dve"):
        E = self.P.engs[eng]
        if eng == "act":
            return self.P.add(eng, lambda: E.copy(out=out.ap, in_=in_.ap), reads=[in_.b], writes=[out.b], n=int(np.prod(out.ap.shape[1:])))
        return self.P.add(eng, lambda: E.tensor_copy(out=out.ap, in_=in_.ap), reads=[in_.b], writes=[out.b], n=int(np.prod(out.ap.shape[1:])))

    def memset(self, out, val, eng="pool"):
        E = self.P.engs[eng]
        return self.P.add(eng, lambda: E.memset(out.ap, val), writes=[out.b], n=int(np.prod(out.ap.shape[1:])))

    def red(self, out, in_, op, axis=AX.X, eng="dve"):
        E = self.P.engs[eng]
        return self.P.add(eng, lambda: E.tensor_reduce(out=out.ap, in_=in_.ap, axis=axis, op=op),
                          reads=[in_.b], writes=[out.b], n=int(np.prod(in_.ap.shape[1:])))

    def recip(self, out, in_):
        nc = self.nc
        return self.P.add("dve", lambda: nc.vector.reciprocal(out=out.ap, in_=in_.ap), reads=[in_.b], writes=[out.b])

    def aselect(self, out, in_, pattern, cmp, fill, base, cm):
        nc = self.nc
        regs = self.fill_regs

        def fn():
            if fill not in regs:
                regs[fill] = nc.gpsimd.to_reg(float(fill))
            return nc.gpsimd.affine_select(out=out.ap, in_=in_.ap, pattern=pattern, compare_op=cmp,
                                           fill=regs[fill], base=base, channel_multiplier=cm)
        return self.P.add("pool", fn, reads=[in_.b], writes=[out.b], n=int(np.prod(out.ap.shape[1:])))

    def iota(self, out, pattern, base, cm):
        nc = self.nc
        return self.P.add("pool", lambda: nc.gpsimd.iota(out.ap, pattern=pattern, base=base, channel_multiplier=cm,
                                                         allow_small_or_imprecise_dtypes=True), writes=[out.b], n=int(np.prod(out.ap.shape[1:])))

    def setup(self):
        nc = self.nc
        self.ps = []
        for i in range(5):
            h = nc.alloc_psum_tensor("ps%d" % i, [128, 512], F32)
            self.ps.append(T(h, "ps%d" % i))
        self.psb = []
        for i in range(2):
            h = nc.alloc_psum_tensor("psb%d" % i, [128, 1024], BF16)
            self.psb.append(T(h, "psb%d" % i))
        self.ps_rr = 0
        self.psb_rr = 0
        self.ident_f = self.sb("identf", [128, 128], F32)
        self.ident = self.sb("ident", [128, 128], BF16)
        self.memset(self.ident_f[:], 1.0)
        self.aselect(self.ident_f[:], self.ident_f[:], [[-1, 128]], ALU.is_equal, 0.0, 0, 1)
        self.copy(self.ident[:], self.ident_f[:], eng="pool")
        self.sb_base = self.sb_cur

    def next_ps(self):
        b0, n = self.ps_set
        t = self.ps[b0 + self.ps_rr % n]
        self.ps_rr += 1
        return t

    def next_psb(self):
        b0, n = self.psb_set
        t = self.psb[b0 + self.psb_rr % n]
        self.psb_rr += 1
        return t

    def load_w(self, name, dram_ap_fn, kchunks, ncols, eng="pool", split=4):
        w = self.sb(name, [128, kchunks, ncols], BF16)
        for k in range(kchunks):
            self.dma(w[:, k, :], dram_ap_fn(k), eng="pool")
        return w

    def layer_norm_tile(self, r, g_bc, b_bc, out_f32, scr):
        st = scr["st"]
        junk = scr["junk"]
        self.act(junk[:], r[:], AF.Identity, accum=st[:, 0:1])
        self.act(junk[:], r[:], AF.Square, accum=st[:, 1:2])
        self.ts(st[:, 2:3], st[:, 0:1], 1.0 / D, None, ALU.mult)
        self.tt(st[:, 3:4], st[:, 2:3], st[:, 2:3], ALU.mult)
        self.stt(st[:, 4:5], st[:, 1:2], 1.0 / D, st[:, 3:4], ALU.mult, ALU.subtract)
        self.ts(st[:, 4:5], st[:, 4:5], 0.0, LN_EPS, ALU.max, ALU.add)
        self.act(st[:, 5:6], st[:, 4:5], AF.Sqrt)
        self.recip(st[:, 6:7], st[:, 5:6])
        self.ts(out_f32[:], r[:], st[:, 2:3], st[:, 6:7], ALU.subtract, ALU.mult)
        self.tt(out_f32[:], out_f32[:], g_bc[:], ALU.mult)
        self.tt(out_f32[:], out_f32[:], b_bc[:], ALU.add)

    def store_xT(self, x_f32, xT_dram, t0, scr, defer=False):
        xbl = scr["xb"]
        if isinstance(xbl, list):
            xb = xbl[self.xb_cnt % len(xbl)]
            self.xb_cnt += 1
        else:
            xb = xbl
        xTs = scr["xTs"]
        self.copy(xb[:], x_f32[:], eng="act")

        def part_b():
            pb = self.next_psb()
            for k in range(NKC):
                self.tr(pb[:, k * 128:(k + 1) * 128], xb[:, k * 128:(k + 1) * 128], self.ident[:])
            self.copy(xTs[:], pb[:, :], eng="dve")
            self.dma(V(xT_dram.h.rearrange("(k p) s -> p k s", p=128)[:, :, t0:t0 + 128], xT_dram.name),
                     V(xTs.h[:].rearrange("p (k t) -> p k t", k=NKC), xTs.name))
        if defer:
            self.flush_pending()
            self.pending_T = part_b
        else:
            part_b()

    def flush_pending(self):
        if self.pending_T is not None:
            f = self.pending_T
            self.pending_T = None
            f()

    def dma_s(self, out, in_, eng="sp"):
        E = self.P.engs[eng]
        return self.P.add(eng, lambda: E.dma_start(out=out.ap, in_=in_.ap, allow_slow_non_contiguous=True),
                          reads=[in_.b], writes=[out.b], dma=True, n=int(np.prod(out.ap.shape)) * 8)

    def rot(self, name, n, shape, dtype):
        key = "_rot_" + name
        lst = [self.sb(name + str(i), shape, dtype) for i in range(n)]
        self.rots[key] = [lst, 0]
        return key

    def nx(self, key):
        lst, i = self.rots[key]
        self.rots[key][1] = i + 1
        return lst[i % len(lst)]

    def vmax(self, out, in_):
        nc = self.nc
        return self.P.add("dve", lambda: nc.vector.max(out=out.ap, in_=in_.ap), reads=[in_.b], writes=[out.b], n=int(np.prod(in_.ap.shape[1:])))

    def match_replace(self, out, rep, vals, imm):
        nc = self.nc
        return self.P.add("dve", lambda: nc.vector.match_replace(out=out.ap, in_to_replace=rep.ap, in_values=vals.ap, imm_value=imm),
                          reads=[rep.b, vals.b], writes=[out.b], n=int(np.prod(vals.ap.shape[1:])))

    def redabs(self, out, in_):
        nc = self.nc
        return self.P.add("dve", lambda: nc.vector.tensor_reduce(out=out.ap, in_=in_.ap, axis=AX.X, op=ALU.max,
                                                                 apply_absolute_value=True),
                          reads=[in_.b], writes=[out.b], n=int(np.prod(in_.ap.shape[1:])))

    def fm_rows(self, FMS, c0, nchunk, s0, s1):
        return V(FMS.h[c0 * 128:(c0 + nchunk) * 128, s0:s1].rearrange("(c p) s -> p c s", p=128), FMS.name)

    def setup_consts(self, meta, bdm, ovl, NB, NCP):
        S = self.S
        NT = S // 128
        self.NB = NB
        self.NCP = NCP
        self.meta = self.sb("meta", [128, 32], F32)
        self.dma(self.meta[:], meta[:, :])
        self.bdm = self.sb("bdm", [128, 256], F32)
        self.dma(self.bdm[:], bdm[:, :])
        self.ovl = self.sb("ovl", [128, NCP // 128, NB], F32)
        self.dma(self.ovl[:], V(ovl.h.rearrange("(c p) j -> p c j", p=128), ovl.name))
        self.U = self.sb("U", [128, 128], F32)
        self.memset(self.U[:], 1.0)
        self.aselect(self.U[:], self.U[:], [[1, 128]], ALU.is_ge, 0.0, 0, -1)
        self.cneg30 = self.sb("cneg30", [128, 128], F32)
        self.memset(self.cneg30[:], 0.0)
        self.aselect(self.cneg30[:], self.cneg30[:], [[-1, 128]], ALU.is_ge, -1e30, 0, 1)
        self.cneg2k = self.sb("cneg2k", [128, 128], F32)
        self.memset(self.cneg2k[:], 0.0)
        self.aselect(self.cneg2k[:], self.cneg2k[:], [[-1, 128]], ALU.is_ge, -2000.0, 0, 1)
        self.band = self.sb("band", [128, 640], F32)
        self.memset(self.band[:], 0.0)
        self.aselect(self.band[:], self.band[:], [[1, 640]], ALU.is_ge, -2000.0, -1, -1)
        self.aselect(self.band[:], self.band[:], [[-1, 640]], ALU.is_ge, -2000.0, 512, 1)
        self.decayT4 = self.sb("decayT4", [128, 4, 128], F32)
        self.xi = self.sb("xi", [128, 128], F32)
        self.zeta = self.sb("zeta", [128, 128], F32)
        self.cdecay = self.sb("cdecay", [128, 1], F32)
        self.rkc = self.sb("rkc", [128, 20], F32)
        self.sb_base = self.sb_cur
        dji = self.sb("dji", [128, 128], F32)
        self.iota(dji[:], [[1, 128]], 0, -1)
        for h in range(4):
            self.act(self.decayT4[:, h, :], dji[:], AF.Exp, scale=RET_LNG[h])
        self.tt(self.decayT4[:], self.decayT4[:], V(self.U.h[:, :].unsqueeze(1).to_broadcast([128, 4, 128]), self.U.name), ALU.mult)
        ip1 = self.sb("ip1", [128, 128], F32)
        self.iota(ip1[:], [[1, 128]], 1, 0)
        self.act(self.xi[:], ip1[:], AF.Exp, scale=self.meta[:, 12:13])
        jr = self.sb("jr", [128, 128], F32)
        self.iota(jr[:], [[0, 128]], 127, -1)
        for h in range(4):
            self.act(self.zeta[:, 32 * h:32 * h + 32], jr[:, 32 * h:32 * h + 32], AF.Exp, scale=RET_LNG[h])
        c128 = self.sb("c128", [128, 1], F32)
        self.memset(c128[:], 128.0)
        self.act(self.cdecay[:], c128[:], AF.Exp, scale=self.meta[:, 12:13])
        for k in range(20):
            self.memset(self.rkc[:, k:k + 1], 2.0 ** (-k))

    def build_addmask(self):
        NT = self.S // 128
        NB = self.NB
        self.addmask = self.sb("addmask", [128, NT, NB], F32)
        self.memset(self.addmask[:], 0.0)
        for i in range(NT):
            for half in range(2):
                cur = 2 * i + half
                r0 = 64 * half
                v = self.addmask[r0:r0 + 64, i, :]
                self.aselect(v, v, [[-1, NB]], ALU.is_ge, -1e30, cur, 0)
                self.memset(self.addmask[r0:r0 + 64, i, 0:1], 1e30)
                self.memset(self.addmask[r0:r0 + 64, i, cur:cur + 1], 1e30)
                if cur >= 1:
                    self.memset(self.addmask[r0:r0 + 64, i, cur - 1:cur], 1e30)

    def phase_rope(self, ROPE):
        S = self.S
        self.phase_begin()
        pos = self.sb("pos", [128, S], F32)
        self.iota(pos[:], [[1, S]], 0, 0)
        a = self.sb("a", [128, S], F32)
        ki = self.sb("ki", [128, S], I32)
        kf = self.sb("kf", [128, S], F32)
        m = self.sb("m", [128, S], F32)
        r = self.sb("r", [128, S], F32)
        PI = math.pi
        for t in range(4):
            for which in range(2):
                self.ts(a[:], pos[:], self.meta[:, t:t + 1], (PI / 2 if which == 0 else 0.0), ALU.mult, ALU.add)
                self.ts(kf[:], a[:], 1.0 / (2 * PI), None, ALU.mult)
                self.copy(ki[:], kf[:])
                self.copy(kf[:], ki[:])
                self.stt(r[:], kf[:], -2 * PI, a[:], ALU.mult, ALU.add)
                self.ts(m[:], r[:], PI, -2 * PI, ALU.is_gt, ALU.mult)
                self.tt(r[:], r[:], m[:], ALU.add)
                self.ts(m[:], r[:], -PI, 2 * PI, ALU.is_lt, ALU.mult)
                self.tt(r[:], r[:], m[:], ALU.add)
                self.ts(r[:], r[:], PI, -PI, ALU.min, ALU.max)
                self.act(r[:], r[:], AF.Sin)
                col = 4 + 4 * which + t
                self.ts(r[:], r[:], self.meta[:, col:col + 1], None, ALU.mult)
                self.dma(ROPE[t, which], r[:])

    def finish_tile(self, rq, g_bc, b_bc, x_out, xT_out, t0, scr):
        self.layer_norm_tile(rq, g_bc, b_bc, rq, scr)
        self.dma(x_out[t0:t0 + 128, :], rq[:])
        if xT_out is not None:
            self.store_xT(rq, xT_out, t0, scr, defer=True)

    def ln_setup(self, ln_g, ln_b):
        g_bc = self.sb("g_bc", [128, D], F32)
        b_bc = self.sb("b_bc", [128, D], F32)
        self.dma(g_bc[:], V(ln_g.h.partition_broadcast(128), ln_g.name))
        self.dma(b_bc[:], V(ln_b.h.partition_broadcast(128), ln_b.name))
        scr = dict(st=self.sb("st", [128, 8], F32), junk=self.sb("junk", [128, D], BF16),
                   xb=[self.sb("xb0", [128, D], BF16), self.sb("xb1", [128, D], BF16)], xTs=self.sb("xTs", [128, D], BF16))
        return g_bc, b_bc, scr

    def phase_ffn(self, x_in, xT_in, w_gu, w_down, ln_g, ln_b, x_out, xT_out):
        S = self.S
        self.phase_begin()
        wgu = self.load_w("wgu", lambda k: V(w_gu.h[k * 128:(k + 1) * 128, :], w_gu.name), NKC, 2 * DFF)
        wdn = self.load_w("wdn", lambda k: V(w_down.h[k * 128:(k + 1) * 128, :], w_down.name), NFC, D)
        g_bc, b_bc, scr = self.ln_setup(ln_g, ln_b)
        xt = self.sb("xT", [128, NKC, 512], BF16)
        hT = self.sb("hT", [128, NFC, 512], BF16)
        sg = [self.sb("sg%d" % i, [128, 512], BF16) for i in range(2)]
        xr = [self.sb("xr%d" % i, [128, D], F32) for i in range(2)]
        rqs = [self.sb("r%d" % i, [128, D], F32) for i in range(2)]
        ntile = S // 512

        def load_xt(t_):
            self.dma(xt[:], V(xT_in.h.rearrange("(k p) s -> p k s", p=128)[:, :, t_ * 512:(t_ + 1) * 512], xT_in.name))

        def load_xq(idx):
            self.dma(xr[idx % 2][:], x_in[idx * 128:(idx + 1) * 128, :])
        load_xt(0)
        load_xq(0)
        for t in range(ntile):
            for j in range(NFC):
                pg = self.next_ps()
                pu = self.next_ps()
                for k in range(NKC):
                    self.mm(pg[:], wgu[:, k, j * 128:(j + 1) * 128], xt[:, k, :], start=(k == 0), stop=(k == NKC - 1))
                for k in range(NKC):
                    self.mm(pu[:], wgu[:, k, DFF + j * 128:DFF + (j + 1) * 128], xt[:, k, :], start=(k == 0), stop=(k == NKC - 1))
                s = sg[j % 2]
                self.act(s[:], pg[:], AF.Silu)
                self.tt(hT[:, j, :], s[:], pu[:], ALU.mult)
            if t + 1 < ntile:
                load_xt(t + 1)
            for q in range(4):
                t0 = t * 512 + q * 128
                xq = xr[q % 2]
                rq = rqs[q % 2]
                if t * 4 + q + 1 < ntile * 4:
                    load_xq(t * 4 + q + 1)
                for half in range(2):
                    hs = slice(half * 512, (half + 1) * 512)
                    pd = self.next_ps()
                    for j in range(NFC):
                        self.mm(pd[:], hT[:, j, q * 128:(q + 1) * 128], wdn[:, j, hs], start=(j == 0), stop=(j == NFC - 1))
                    self.act(xq[:, hs], xq[:, hs], AF.Copy, scale=ALPHA)
                    self.stt(rq[:, hs], pd[:], 0.5, xq[:, hs], ALU.mult, ALU.add)
                self.finish_tile(rq, g_bc, b_bc, x_out, xT_out, t0, scr)

    def phase_transpose_in(self, x_in, xT_out):
        S = self.S
        self.phase_begin()
        xr = [self.sb("xr%d" % i, [128, D], F32) for i in range(2)]
        scr = dict(xb=self.sb("xb", [128, D], BF16), xTs=self.sb("xTs", [128, D], BF16))
        for i in range(S // 128):
            xq = xr[i % 2]
            self.dma(xq[:], x_in[i * 128:(i + 1) * 128, :])
            self.store_xT(xq, xT_out, i * 128, scr)

    def phase_inproj(self, xT_in, w2, ROPE, FMS, TMB, TMF):
        S = self.S
        self.phase_begin()
        w = self.load_w("win", lambda k: V(w2.h[k * 128:(k + 1) * 128, :], w2.name), NKC, NCOL2)
        xts = [self.sb("xT%d" % i_, [128, NKC, 512], BF16) for i_ in range(2)]
        tabs = [self.sb("tab%d" % i_, [128, 4, 2, 512], F32) for i_ in range(2)]
        t1 = self.rot("t1", 3, [128, 512], F32)
        t2 = self.rot("t2", 3, [128, 512], F32)
        ob = self.rot("ob", 4, [128, 512], BF16)
        tmb = self.rot("tmb", 2, [128, 960], BF16)
        tmf = self.rot("tmf", 2, [128, 24], F32)
        ntile = S // 512

        def load_t(t_):
            ss_ = slice(t_ * 512, (t_ + 1) * 512)
            self.dma(xts[t_ % 2][:], V(xT_in.h.rearrange("(k p) s -> p k s", p=128)[:, :, ss_], xT_in.name))
            self.dma(tabs[t_ % 2][:], V(ROPE.h[:, :, :, ss_].rearrange("t w p s -> p t w s"), ROPE.name))
        load_t(0)
        for t in range(ntile):
            ss = slice(t * 512, (t + 1) * 512)
            xt = xts[t % 2]
            tab = tabs[t % 2]
            if t + 1 < ntile:
                load_t(t + 1)
            for ci, tb in enumerate(ROPED_TABLES):
                pA = self.next_ps()
                pB = self.next_ps()
                for k in range(NKC):
                    self.mm(pA[:], w[:, k, (2 * ci) * 128:(2 * ci + 1) * 128], xt[:, k, :], start=(k == 0), stop=(k == NKC - 1))
                for k in range(NKC):
                    self.mm(pB[:], w[:, k, (2 * ci + 1) * 128:(2 * ci + 2) * 128], xt[:, k, :], start=(k == 0), stop=(k == NKC - 1))
                a1 = self.nx(t1)
                a2 = self.nx(t2)
                o = self.nx(ob)
                self.tt(a1[:], pA[:], tab[:, tb, 0, :], ALU.mult)
                self.tt(a2[:], pB[:], tab[:, tb, 1, :], ALU.mult)
                self.tt(o[:], a1[:], a2[:], ALU.add)
                self.dma(V(FMS.h[ci * 128:(ci + 1) * 128, ss], FMS.name), o[:])
            nr = len(ROPED_TABLES)
            for j in range(7):
                wc = 2 * nr + j
                pA = self.next_ps()
                for k in range(NKC):
                    self.mm(pA[:], w[:, k, wc * 128:(wc + 1) * 128], xt[:, k, :], start=(k == 0), stop=(k == NKC - 1))
                o = self.nx(ob)
                self.copy(o[:], pA[:], eng="act")
                self.dma(V(FMS.h[(nr + j) * 128:(nr + j + 1) * 128, ss], FMS.name), o[:])
            for q in range(4):
                t0 = t * 512 + q * 128
                pA = self.next_ps()
                pB = self.next_ps()
                for k in range(NKC):
                    self.mm(pA[:], xt[:, k, q * 128:(q + 1) * 128], w[:, k, TM0:TM0 + 512], start=(k == 0), stop=(k == NKC - 1))
                for k in range(NKC):
                    self.mm(pB[:, 0:472], xt[:, k, q * 128:(q + 1) * 128], w[:, k, TM0 + 512:TM0 + 984], start=(k == 0), stop=(k == NKC - 1))
                b = self.nx(tmb)
                f = self.nx(tmf)
                self.copy(b[:, 0:512], pA[:], eng="act")
                self.copy(b[:, 512:960], pB[:, 0:448])
                self.copy(f[:], pB[:, 448:472])
                self.dma(TMB[t0:t0 + 128, :], b[:])
                self.dma(TMF[t0:t0 + 128, :], f[:])

    def store_yT(self, y, YT, br, n, yTs_key):
        pb = self.next_psb()
        self.tr(pb[:, 0:128], y[:, 0:128], self.ident[:])
        self.tr(pb[:, 128:256], y[:, 128:256], self.ident[:])
        yTs = self.nx(yTs_key)
        self.copy(yTs[:], pb[:, 0:256])
        self.dma(V(YT.h[br * 256:(br + 1) * 256, n * 128:(n + 1) * 128].rearrange("(c p) t -> p c t", p=128), YT.name),
                 V(yTs.h[:, :].rearrange("p (c t) -> p c t", c=2), yTs.name))

    def phase_ret(self, FMS, TMB, YT):
        S = self.S
        NT = S // 128
        self.phase_begin()
        rq = self.sb("rq", [128, S], BF16)
        rk = self.sb("rk", [128, S], BF16)
        self.dma(rq[:], V(FMS.h[0:128, :], FMS.name))
        self.dma(rk[:], V(FMS.h[128:256, :], FMS.name))
        Sbd = self.sb("Sbd", [128, 256], F32)
        Sbd_bf = self.sb("Sbd_bf", [128, 256], BF16)
        self.memset(Sbd[:], 0.0)
        self.memset(Sbd_bf[:], 0.0)
        vt_k = self.rot("vt", 2, [128, 512], BF16)
        qxi_k = self.rot("qxi", 2, [128, 128], BF16)
        qm_k = self.rot("qm", 2, [128, 4, 128], BF16)
        kz_k = self.rot("kz", 2, [128, 128], BF16)
        PT_k = self.rot("PT", 2, [128, 4, 128], BF16)
        cross_k = self.rot("cross", 2, [128, 256], F32)
        o_k = self.rot("o", 2, [128, 256], F32)
        tmp_k = self.rot("tmp", 2, [128, 256], F32)
        osq_k = self.rot("osq", 2, [128, 256], F32)
        sg_k = self.rot("sg", 2, [128, 256], F32)
        st_k = self.rot("st", 2, [128, 16], F32)
        y_k = self.rot("y", 2, [128, 256], BF16)
        yTs_k = self.rot("yTs", 2, [128, 256], BF16)
        hm = V(self.meta.h[:, 13:17].unsqueeze(2).to_broadcast([128, 4, 128]), self.meta.name)
        for n in range(NT):
            sl = slice(n * 128, (n + 1) * 128)
            vt = self.nx(vt_k)
            self.dma(vt[:], TMB[n * 128:(n + 1) * 128, 0:512])
            qxi = self.nx(qxi_k)
            self.tt(qxi[:], rq[:, sl], self.xi[:], ALU.mult)
            qm = self.nx(qm_k)
            self.tt(qm[:], V(rq.h[:, sl].unsqueeze(1).to_broadcast([128, 4, 128]), rq.name), hm, ALU.mult, eng="pool")
            pb = self.next_psb()
            self.tr(pb[:, 0:128], rk[:, sl], self.ident[:])
            kz = self.nx(kz_k)
            self.tt(kz[:], pb[:, 0:128], self.zeta[:], ALU.mult)
            ps1 = self.next_ps()
            self.mm(ps1[:], rk[:, sl], V(qm.h[:, :, :].rearrange("p h i -> p (h i)"), qm.name))
            PT = self.nx(PT_k)
            self.tt(PT[:], V(ps1.h[:, :].rearrange("p (h i) -> p h i", h=4), ps1.name), self.decayT4[:], ALU.mult)
            ps2 = self.next_ps()
            self.mm(ps2[:, 0:256], qxi[:], Sbd_bf[:])
            cross = self.nx(cross_k)
            self.copy(cross[:], ps2[:, 0:256], eng="act")
            ps3 = self.next_ps()
            for h in range(4):
                self.mm(ps3[:, 64 * h:64 * h + 64], PT[:, h, :], vt[:, 64 * h:64 * h + 64])
            o = self.nx(o_k)
            self.tt(o[:], ps3[:, 0:256], cross[:], ALU.add)
            ps4 = self.next_ps()
            self.mm(ps4[:, 0:256], kz[:], vt[:, 0:256])
            tmp = self.nx(tmp_k)
            self.tt(tmp[:], ps4[:, 0:256], self.bdm[:], ALU.mult)
            self.stt(Sbd[:], Sbd[:], self.cdecay[:, 0:1], tmp[:], ALU.mult, ALU.add)
            self.copy(Sbd_bf[:], Sbd[:], eng="act")
            st = self.nx(st_k)
            o3 = V(o.h[:, :].rearrange("p (h e) -> p h e", h=4), o.name)
            self.red(st[:, 0:4], o3, ALU.add)
            osq = self.nx(osq_k)
            self.tt(osq[:], o[:], o[:], ALU.mult, eng="pool")
            self.red(st[:, 4:8], V(osq.h[:, :].rearrange("p (h e) -> p h e", h=4), osq.name), ALU.add)
            self.ts(st[:, 8:12], st[:, 0:4], 1.0 / 64, None, ALU.mult)
            self.tt(st[:, 12:16], st[:, 8:12], st[:, 8:12], ALU.mult)
            self.stt(st[:, 4:8], st[:, 4:8], 1.0 / 64, st[:, 12:16], ALU.mult, ALU.subtract)
            self.ts(st[:, 4:8], st[:, 4:8], 0.0, LN_EPS, ALU.max, ALU.add)
            self.act(st[:, 4:8], st[:, 4:8], AF.Sqrt)
            self.recip(st[:, 4:8], st[:, 4:8])
            self.tt(o3, o3, V(st.h[:, 8:12].unsqueeze(2).to_broadcast([128, 4, 64]), st.name), ALU.subtract)
            self.tt(o3, o3, V(st.h[:, 4:8].unsqueeze(2).to_broadcast([128, 4, 64]), st.name), ALU.mult)
            sg = self.nx(sg_k)
            self.act(sg[:], vt[:, 256:512], AF.Silu)
            y = self.nx(y_k)
            self.tt(y[:], o[:], sg[:], ALU.mult)
            self.store_yT(y, YT, 0, n, yTs_k)

    def phase_ssd(self, FMS, TMB, TMF, conv_w, conv_b, dt_bias, a_log, d_skip, norm_g, YT):
        S = self.S
        NT = S // 128
        self.phase_begin()
        cw = self.sb("cw", [128, 6, 4], F32)
        for k_ in range(4):
            self.dma_s(cw[:, :, k_], V(conv_w.h[k_].rearrange("(c p) -> p c", p=128), conv_w.name))
        cb = self.sb("cb", [128, 6], F32)
        self.dma_s(cb[:], V(conv_b.h.rearrange("(c p) -> p c", p=128), conv_b.name))
        dtb = self.sb("dtb", [128, 4], F32)
        self.dma(dtb[:], V(dt_bias.h.partition_broadcast(128), dt_bias.name))
        a_bc = self.sb("a_bc", [128, 4], F32)
        self.dma(a_bc[:], V(a_log.h.partition_broadcast(128), a_log.name))
        self.act(a_bc[:], a_bc[:], AF.Exp)
        self.ts(a_bc[:], a_bc[:], -1.0, None, ALU.mult)
        Dbc = self.sb("Dbc", [128, 4], F32)
        self.dma(Dbc[:], V(d_skip.h.partition_broadcast(128), d_skip.name))
        ng_bc = self.sb("ng_bc", [128, 256], F32)
        self.dma(ng_bc[:], V(norm_g.h.partition_broadcast(128), norm_g.name))
        xbcs = self.sb("xbcs", [128, 6, S], BF16)
        raw_k = self.rot("raw", 2, [128, 6, 515], BF16)
        acc_k = self.rot("acc", 2, [128, 512], F32)
        for t in range(S // 512):
            raw = self.nx(raw_k)
            if t == 0:
                self.memset(raw[:, :, 0:3], 0.0)
                self.dma(raw[:, :, 3:515], self.fm_rows(FMS, 14, 6, 0, 512))
            else:
                self.dma(raw[:, :, 0:515], self.fm_rows(FMS, 14, 6, t * 512 - 3, (t + 1) * 512))
            for c in range(6):
                acc = self.nx(acc_k)
                self.ts(acc[:], raw[:, c, 3:515], cw[:, c, 3:4], None, ALU.mult)
                for k in (2, 1, 0):
                    self.stt(acc[:], raw[:, c, k:k + 512], cw[:, c, k:k + 1], acc[:], ALU.mult, ALU.add)
                self.act(xbcs[:, c, t * 512:(t + 1) * 512], acc[:], AF.Silu, bias=cb[:, c:c + 1])
        prev = self.sb("prev", [128, 256], F32)
        prev_bf = self.sb("prev_bf", [128, 256], BF16)
        self.memset(prev[:], 0.0)
        self.memset(prev_bf[:], 0.0)
        xsB_k = self.rot("xsB", 2, [128, 512], BF16)
        tmf_k = self.rot("tmf", 2, [128, 24], F32)
        zt_k = self.rot("zt", 2, [128, 256], BF16)
        st_k = self.rot("st", 2, [128, 32], F32)
        adtb_k = self.rot("adtb", 2, [128, 4, 128], F32)
        seg_k = self.rot("seg", 2, [128, 4, 128], F32)
        MT_k = self.rot("MT", 2, [128, 4, 128], BF16)
        X_k = self.rot("X", 2, [128, 256], BF16)
        Xd_k = self.rot("Xd", 2, [128, 256], BF16)
        yd_k = self.rot("yd", 2, [128, 256], F32)
        y_k = self.rot("y", 2, [128, 256], F32)
        t2_k = self.rot("t2", 2, [128, 256], F32)
        sz_k = self.rot("sz", 2, [128, 256], F32)
        yb_k = self.rot("yb", 2, [128, 256], BF16)
        yTs_k = self.rot("yTs", 2, [128, 256], BF16)
        Ubc = V(self.U.h[:, :].unsqueeze(1).to_broadcast([128, 4, 128]), self.U.name)

        def h4(t_):
            return V(t_.h[:, 0:256].rearrange("p (h e) -> p h e", h=4), t_.name)

        def bc4(v_):
            return V(v_.ap.unsqueeze(2).to_broadcast([128, 4, 64]), v_.b)

        for n in range(NT):
            sl = slice(n * 128, (n + 1) * 128)
            pb = self.next_psb()
            for c in range(4):
                self.tr(pb[:, c * 128:(c + 1) * 128], xbcs[:, c, sl], self.ident[:])
            xsB = self.nx(xsB_k)
            self.copy(xsB[:], pb[:, 0:512])
            tmf = self.nx(tmf_k)
            self.dma(tmf[:], TMF[n * 128:(n + 1) * 128, :])
            zt = self.nx(zt_k)
            self.dma(zt[:], TMB[n * 128:(n + 1) * 128, 704:960])
            st = self.nx(st_k)
            self.tt(st[:, 0:4], tmf[:, 20:24], dtb[:], ALU.add)
            self.act(st[:, 0:4], st[:, 0:4], AF.Exp)
            self.act(st[:, 0:4], st[:, 0:4], AF.Ln, bias=1.0)
            self.tt(st[:, 4:8], st[:, 0:4], a_bc[:], ALU.mult)
            adtb = self.nx(adtb_k)
            self.copy(adtb[:], V(st.h[:, 4:8].unsqueeze(2).to_broadcast([128, 4, 128]), st.name))
            psA = self.next_ps()
            self.mm(psA[:, 0:4], self.U[:], st[:, 4:8])
            self.copy(st[:, 8:12], psA[:, 0:4], eng="act")
            psB = self.next_ps()
            for h in range(4):
                self.mm(psB[:, h * 128:(h + 1) * 128], adtb[:, h, :], self.U[:])
            seg = self.nx(seg_k)
            for h in range(4):
                self.ts(seg[:, h, :], psB[:, h * 128:(h + 1) * 128], st[:, 8 + h:9 + h], 0.0, ALU.subtract, ALU.min)
            self.act(seg[:], seg[:], AF.Exp)
            self.tt(seg[:], seg[:], Ubc, ALU.mult, eng="pool")
            alast = V(psB.h[:, 127:512:128], psB.name)
            self.tt(st[:, 12:16], alast, st[:, 8:12], ALU.subtract)
            self.act(st[:, 12:16], st[:, 12:16], AF.Exp)
            self.act(st[:, 16:20], alast, AF.Exp)
            self.act(st[:, 20:24], st[:, 8:12], AF.Exp)
            psG = self.next_ps()
            for g in range(2):
                self.mm(psG[:, g * 128:(g + 1) * 128], xbcs[:, 2 + g, sl], xbcs[:, 4 + g, sl])
            MT = self.nx(MT_k)
            for g in range(2):
                self.tt(MT[:, 2 * g:2 * g + 2, :], seg[:, 2 * g:2 * g + 2, :],
                        V(psG.h[:, g * 128:(g + 1) * 128].unsqueeze(1).to_broadcast([128, 2, 128]), psG.name), ALU.mult)
            X = self.nx(X_k)
            self.tt(h4(X), h4(xsB), bc4(st[:, 0:4]), ALU.mult)
            psY = self.next_ps()
            for h in range(4):
                self.mm(psY[:, 64 * h:64 * h + 64], MT[:, h, :], X[:, 64 * h:64 * h + 64])
            psO = self.next_ps()
            for g in range(2):
                self.mm(psO[:, 128 * g:128 * g + 128], xbcs[:, 4 + g, sl], prev_bf[:, 128 * g:128 * g + 128])
            yd = self.nx(yd_k)
            self.copy(yd[:], psY[:, 0:256], eng="act")
            y = self.nx(y_k)
            self.tt(h4(y), h4(psO), bc4(st[:, 20:24]), ALU.mult)
            self.tt(y[:], y[:], yd[:], ALU.add)
            t2 = self.nx(t2_k)
            self.tt(h4(t2), h4(xsB), bc4(Dbc[:, 0:4]), ALU.mult, eng="pool")
            self.tt(y[:], y[:], t2[:], ALU.add)
            Xd = self.nx(Xd_k)
            self.tt(h4(Xd), h4(X), bc4(st[:, 12:16]), ALU.mult, eng="pool")
            psS = self.next_ps()
            for g in range(2):
                self.mm(psS[:, 128 * g:128 * g + 128], xsB[:, 256 + 128 * g:256 + 128 * g + 128], Xd[:, 128 * g:128 * g + 128])
            self.tt(h4(prev), h4(prev), bc4(st[:, 16:20]), ALU.mult)
            self.tt(prev[:], prev[:], psS[:, 0:256], ALU.add)
            self.copy(prev_bf[:], prev[:], eng="act")
            sz = self.nx(sz_k)
            self.act(sz[:], zt[:], AF.Silu)
            self.tt(y[:], y[:], sz[:], ALU.mult)
            self.tt(t2[:], y[:], y[:], ALU.mult, eng="pool")
            self.red(st[:, 24:26], V(t2.h[:, :].rearrange("p (g e) -> p g e", g=2), t2.name), ALU.add)
            self.ts(st[:, 24:26], st[:, 24:26], 1.0 / 128, LN_EPS, ALU.mult, ALU.add)
            self.act(st[:, 24:26], st[:, 24:26], AF.Sqrt)
            self.recip(st[:, 24:26], st[:, 24:26])
            y2 = V(y.h[:, :].rearrange("p (g e) -> p g e", g=2), y.name)
            self.tt(y2, y2, V(st.h[:, 24:26].unsqueeze(2).to_broadcast([128, 2, 128]), st.name), ALU.mult)
            yb = self.nx(yb_k)
            self.tt(yb[:], y[:], ng_bc[:], ALU.mult)
            self.store_yT(yb, YT, 3, n, yTs_k)

    def softmax_pv(self, Ssb, nk, Vt, kt0, out, kk, clamp=None, premax=None):
        st = self.nx(kk["st"])
        self.red(st[:, 0:1], (Ssb if premax is None else premax), ALU.max)
        if clamp is not None:
            self.ts(st[:, 0:1], st[:, 0:1], clamp, None, ALU.max)
        self.ts(st[:, 1:2], st[:, 0:1], -1.0, None, ALU.mult)
        P = self.nx(kk["P"])
        self.act(P[:, 0:nk], Ssb, AF.Exp, bias=st[:, 1:2], accum=st[:, 2:3])
        self.ts(st[:, 3:4], st[:, 2:3], 1e-30, None, ALU.max)
        self.recip(st[:, 4:5], st[:, 3:4])
        po = self.next_ps()
        nkt = nk // 128
        for g0 in range(0, nkt, 8):
            gn = min(8, nkt - g0)
            pb = self.next_psb()
            for j in range(gn):
                self.tr(pb[:, j * 128:(j + 1) * 128], P[:, (g0 + j) * 128:(g0 + j + 1) * 128], self.ident[:])
            PT = self.nx(kk["PT"])
            self.copy(PT[:, 0:gn * 128], pb[:, 0:gn * 128], eng="act")
            for j in range(gn):
                self.mm(po[:, 0:64], PT[:, j * 128:(j + 1) * 128], Vt[:, kt0 + g0 + j, :],
                        start=(g0 + j == 0), stop=(g0 + j == nkt - 1))
        self.ts(out, po[:, 0:64], st[:, 4:5], None, ALU.mult)

    def attn_keys(self, pfx):
        S = self.S
        return dict(st=self.rot(pfx + "sst", 2, [128, 8], F32), P=self.rot(pfx + "P", 1, [128, S], BF16),
                    PT=self.rot(pfx + "PTa", 2, [128, 1024], BF16))

    def dsa_setup(self, FMS, TMB, TMF, YT):
        S = self.S
        NT = S // 128
        c = dict(FMS=FMS, YT=YT)
        c["dk"] = self.sb("dk", [128, S], BF16)
        self.dma(c["dk"][:], V(FMS.h[4 * 128:5 * 128, :], FMS.name))
        c["ikr"] = self.sb("ikr", [128, S], BF16)
        self.dma(c["ikr"][:], V(FMS.h[7 * 128:8 * 128, :], FMS.name))
        c["qm"] = self.rot("dqm", 2, [128, 8, 128], BF16)
        c["Vt"] = self.sb("Vt", [128, NT, 64], BF16)
        self.dma(c["Vt"][:], V(TMB.h[:, 512:576].rearrange("(n p) c -> p n c", p=128), TMB.name))
        iw = self.sb("iw", [128, NT, 8], F32)
        self.dma(iw[:], V(TMF.h[:, 0:8].rearrange("(n p) c -> p n c", p=128), TMF.name))
        c["absw"] = self.sb("absw", [128, NT, 8], F32)
        self.act(c["absw"][:], iw[:], AF.Abs, scale=1.0 / 16)
        c["sgn"] = self.sb("sgn", [128, NT, 8], F32)
        self.ts(c["sgn"][:], iw[:], 0.0, 2.0, ALU.is_ge, ALU.mult)
        self.ts(c["sgn"][:], c["sgn"][:], -1.0, None, ALU.add)
        c["I"] = self.rot("I", 1, [128, S], F32)
        c["Ssb"] = self.rot("dSsb", 1, [128, S], F32)
        c["Mb"] = self.rot("dMb", 2, [128, S], BF16)
        c["cm"] = self.rot("dcm", 2, [128, 8], F32)
        c["kk"] = self.attn_keys("d")
        c["q"] = self.rot("dqi", 2, [128, 4, 128], BF16)
        c["tmp"] = self.rot("tmpr", 2, [128, 512], F32)
        c["st"] = self.rot("dst", 2, [128, 16], F32)
        c["Rk"] = self.rot("Rk", 2, [128, 20], F32)
        c["nm"] = self.rot("dnm", 2, [128, 2], F32)
        c["c2"] = self.rot("dc2", 2, [128, 2], F32)
        c["o"] = self.rot("do", 2, [128, 256], F32)
        c["y"] = self.rot("dy", 2, [128, 256], BF16)
        c["yTs"] = self.rot("dyTs", 2, [128, 256], BF16)
        return c

    def dsa_tile(self, c, i):
        FMS = c["FMS"]
        I = self.nx(c["I"])
        Ssb = self.nx(c["Ssb"])
        nk = 128 * (i + 1)
        nkc = (nk + 511) // 512
        q = self.nx(c["q"])
        self.dma(q[:, 0:2, :], self.fm_rows(FMS, 2, 2, i * 128, (i + 1) * 128))
        self.dma(q[:, 2:4, :], self.fm_rows(FMS, 5, 2, i * 128, (i + 1) * 128))
        qm = self.nx(c["qm"])
        hm = V(self.meta.h[:, 13:17].unsqueeze(2).to_broadcast([128, 4, 128]), self.meta.name)
        for cc in range(2):
            self.tt(qm[:, 4 * cc:4 * cc + 4, :], V(q.h[:, 2 + cc, :].unsqueeze(1).to_broadcast([128, 4, 128]), q.name), hm,
                    ALU.mult, eng="pool")
        for kc in range(nkc):
            c0 = kc * 512
            cols = min(512, nk - c0)
            for h in range(8):
                ps = self.next_ps()
                self.mm(ps[:, 0:cols], qm[:, h, :], c["ikr"][:, c0:c0 + cols])
                tmp = self.nx(c["tmp"])
                self.act(tmp[:, 0:cols], ps[:, 0:cols], AF.Relu, scale=c["absw"][:, i, h:h + 1])
                if h == 0:
                    self.ts(I[:, c0:c0 + cols], tmp[:, 0:cols], c["sgn"][:, i, 0:1], None, ALU.mult)
                else:
                    self.stt(I[:, c0:c0 + cols], tmp[:, 0:cols], c["sgn"][:, i, h:h + 1], I[:, c0:c0 + cols], ALU.mult, ALU.add)
        if nk > self.n_keep:
            st = self.nx(c["st"])
            junk = self.nx(c["kk"]["P"])
            self.redabs(st[:, 0:1], I[:, 0:nk])
            self.ts(st[:, 0:1], st[:, 0:1], 1e-20, None, ALU.max)
            self.tt(I[:, nk - 128:nk], I[:, nk - 128:nk], self.cneg30[:], ALU.add)
            Rk = self.nx(c["Rk"])
            self.ts(Rk[:], self.rkc[:], st[:, 0:1], None, ALU.mult)
            self.ts(st[:, 1:2], st[:, 0:1], -1.0, None, ALU.mult)
            n1 = (nk // 2 + 127) // 128 * 128
            n2 = nk - n1
            thr_c = self.n_keep - 0.5 - n2 / 2.0
            for k in range(NBIS):
                nm = self.nx(c["nm"])
                c2 = self.nx(c["c2"])
                self.tt(nm[:, 0:1], st[:, 1:2], Rk[:, k:k + 1], ALU.add)
                self.act(Ssb[:, n1:nk], I[:, n1:nk], AF.Sign, bias=nm[:, 0:1], scale=-1.0, accum=c2[:, 0:1])
                self.ts(junk[:, 0:n1], I[:, 0:n1], nm[:, 0:1], None, ALU.is_ge, ALU.add, accum=st[:, 3:4])
                self.stt(st[:, 4:5], c2[:, 0:1], -0.5, st[:, 3:4], ALU.mult, ALU.add)
                self.ts(st[:, 4:5], st[:, 4:5], thr_c, None, ALU.is_ge)
                self.stt(st[:, 1:2], st[:, 4:5], Rk[:, k:k + 1], st[:, 1:2], ALU.mult, ALU.add)
            Mb = self.nx(c["Mb"])
            self.ts(Mb[:, 0:nk], I[:, 0:nk], st[:, 1:2], 8000.0, ALU.is_ge, ALU.mult)
        else:
            Mb = self.nx(c["Mb"])
            self.memset(Mb[:, 0:nk], 8000.0)
            self.tt(Mb[:, nk - 128:nk], Mb[:, nk - 128:nk], self.cnegb[:], ALU.add)
        o = self.nx(c["o"])
        for h in range(4):
            base = 64 * (h % 2)
            cq = h // 2
            cm = self.nx(c["cm"])
            for kc in range(nkc):
                c0 = kc * 512
                cols = min(512, nk - c0)
                ps = self.next_ps()
                self.mm(ps[:, 0:cols], q[base:base + 64, cq, :], c["dk"][base:base + 64, c0:c0 + cols], start=True, stop=False)
                self.mm(ps[:, 0:cols], self.ident[:], Mb[:, c0:c0 + cols], start=False, stop=True)
                self.ts(Ssb[:, c0:c0 + cols], ps[:, 0:cols], 0.125, None, ALU.mult, ALU.max, accum=cm[:, kc:kc + 1])
            self.softmax_pv(Ssb[:, 0:nk], nk, c["Vt"], 0, o[:, 64 * h:64 * h + 64], c["kk"], premax=cm[:, 0:nkc])
        y = self.nx(c["y"])
        self.copy(y[:], o[:], eng="act")
        self.store_yT(y, c["YT"], 1, i, c["yTs"])

    def nsa_setup(self, FMS, TMB, TMF, cmp_w1, cmp_w2, cmp_pos, YT):
        S = self.S
        NT = S // 128
        NB = self.NB
        NCP = self.NCP
        NC = (S - 32) // 16 + 1
        NCT = NCP // 128
        c = dict(FMS=FMS, YT=YT)
        c["ksT"] = self.sb("ksT", [128, S], BF16)
        self.dma(c["ksT"][:], V(FMS.h[11 * 128:12 * 128, :], FMS.name))
        c["kwT"] = self.sb("kwT", [128, S], BF16)
        self.dma(c["kwT"][:], V(FMS.h[12 * 128:13 * 128, :], FMS.name))
        c["Vs"] = self.sb("Vs", [128, NT, 64], BF16)
        self.dma(c["Vs"][:], V(TMB.h[:, 576:640].rearrange("(n p) c -> p n c", p=128), TMB.name))
        c["Vw"] = self.sb("Vw", [128, NT, 64], BF16)
        self.dma(c["Vw"][:], V(TMB.h[:, 640:704].rearrange("(n p) c -> p n c", p=128), TMB.name))
        c["ngt"] = self.sb("ngt", [128, NT, 12], F32)
        self.dma(c["ngt"][:], V(TMF.h[:, 8:20].rearrange("(n p) c -> p n c", p=128), TMF.name))
        kcmp = self.sb("kcmp", [128, NCP], BF16)
        vcmp = self.sb("vcmp", [128, NCT, 64], BF16)
        c["kcmp"] = kcmp
        c["vcmp"] = vcmp
        c["Ssb"] = self.sb("nSsb", [128, S], F32)
        c["Sw"] = self.sb("Sw", [128, 640], F32)
        c["kk"] = self.attn_keys("n")
        save = self.sb_cur
        srcT = self.sb("srcT", [128, S], BF16)
        w1 = self.sb("w1", [64, 32, 64], BF16)
        w2 = self.sb("w2", [64, 128], BF16)
        posT = self.sb("posT", [64, 32], F32)
        posb = self.sb("posb", [64, 32], BF16)
        cst = self.sb("cst", [64, 1], F32)
        u = self.sb("u", [64, NCP], F32)
        u2 = self.sb("u2", [64, NCP], F32)
        gl = self.sb("gl", [64, NCP], BF16)
        for i in range(2):
            self.dma(srcT[:], V(FMS.h[(10 + 3 * i) * 128:(11 + 3 * i) * 128, :], FMS.name))
            self.dma(w1[:], V(cmp_w1.h[i].rearrange("(l d) f -> d l f", d=64), cmp_w1.name), eng="pool")
            self.dma(w2[:, 0:64], V(cmp_w2.h[i], cmp_w2.name), eng="pool")
            self.dma(w2[:, 64:128], V(cmp_w2.h[i], cmp_w2.name), eng="pool")
            self.dma_s(posT[:], V(cmp_pos.h[i].rearrange("l d -> d l"), cmp_pos.name))
            self.copy(posb[:], posT[:])
            psc = self.next_ps()
            for l in range(32):
                self.mm(psc[0:64, 0:1], w1[:, l, :], posb[:, l:l + 1], start=(l == 0), stop=(l == 31))
            self.copy(cst[:], psc[0:64, 0:1])
            psh = self.next_ps()
            for l in range(32):
                self.mm(psh[0:64, 0:NC], w1[:, l, :], srcT[0:64, l:l + 16 * (NC - 1) + 1:16], start=(l == 0), stop=(l == 31))
            self.memset(u[:], 0.0)
            self.act(u[:, 0:NC], psh[0:64, 0:NC], AF.Identity, bias=cst[:, 0:1])
            self.tt(u2[:], u[:], u[:], ALU.mult)
            self.tt(u2[:], u2[:], u[:], ALU.mult)
            self.stt(u2[:], u2[:], 0.044715, u[:], ALU.mult, ALU.add)
            self.act(u2[:], u2[:], AF.Tanh, scale=0.7978845608028654)
            self.ts(u2[:], u2[:], 1.0, 0.5, ALU.add, ALU.mult)
            self.tt(gl[:], u2[:], u[:], ALU.mult)
            if i == 0:
                pso = self.next_ps()
                self.mm(pso[:, 0:NCP], w2[:, :], gl[:, :])
                self.copy(kcmp[:], pso[:, 0:NCP])
            else:
                for ct in range(NCT):
                    pso = self.next_ps()
                    self.mm(pso[:, 0:64], gl[:, ct * 128:(ct + 1) * 128], w2[:, 0:64])
                    self.copy(vcmp[:, ct, :], pso[:, 0:64])
        self.sb_cur = save
        self.P.barrier()
        c["q"] = self.rot("nqi", 2, [128, 2, 128], BF16)
        c["Mn"] = self.rot("nMn", 1, [128, S], BF16)
        for nm, shp, dt_ in (("vis", [128, NCP], F32), ("pns", [128, NCP], F32), ("pn", [128, NCP], F32), ("Sc", [128, NCP], F32),
                             ("Pc", [128, NCP], F32), ("pnb", [128, NCP], BF16), ("PTc", [128, NCP], BF16), ("pnT", [128, NCP], F32),
                             ("cst2", [128, 8], F32), ("am", [128, NB], F32), ("imp", [128, NB], F32), ("imp2", [128, NB], F32), ("m8", [128, 16], F32),
                             ("selm", [128, NB], F32), ("cm", [128, 8], F32), ("oc", [128, 256], F32), ("os", [128, 256], F32), ("ow", [128, 256], F32),
                             ("gs", [128, 12], F32), ("o", [128, 256], F32), ("y", [128, 256], BF16), ("yTs", [128, 256], BF16)):
            c[nm] = self.rot("n" + nm, (1 if nm in ("pnT", "Pc", "vis", "imp2") else 2), shp, dt_)
        return c

    def nsa_tile(self, c, i):
        NB = self.NB
        NCP = self.NCP
        NCT = NCP // 128
        FMS = c["FMS"]
        Ssb = c["Ssb"]
        Sw = c["Sw"]
        kcmp = c["kcmp"]
        vcmp = c["vcmp"]

        def h4(t_):
            return V(t_.h[:, 0:256].rearrange("p (h e) -> p h e", h=4), t_.name)

        nk = 128 * (i + 1)
        nkc = (nk + 511) // 512
        nq = self.nx(c["q"])
        self.dma(nq[:], self.fm_rows(FMS, 8, 2, i * 128, (i + 1) * 128))
        vis = self.nx(c["vis"])
        self.memset(vis[:], 0.0)
        self.aselect(vis[:], vis[:], [[-16, NCP]], ALU.is_ge, -1000.0, 128 * i - 31, 1)
        pns = self.nx(c["pns"])
        oc = self.nx(c["oc"])
        osl = self.nx(c["os"])
        ow = self.nx(c["ow"])
        for h in range(4):
            base = 64 * (h % 2)
            cq = h // 2
            ps = self.next_ps()
            self.mm(ps[:, 0:NCP], nq[base:base + 64, cq, :], kcmp[base:base + 64, :])
            Sc = self.nx(c["Sc"])
            self.stt(Sc[:], ps[:, 0:NCP], 0.125, vis[:], ALU.mult, ALU.add)
            st = self.nx(c["cst2"])
            self.red(st[:, 0:1], Sc[:], ALU.max)
            self.ts(st[:, 0:1], st[:, 0:1], -500.0, -1.0, ALU.max, ALU.mult)
            Pc = self.nx(c["Pc"])
            self.act(Pc[:], Sc[:], AF.Exp, bias=st[:, 0:1], accum=st[:, 1:2])
            self.ts(st[:, 2:3], st[:, 1:2], 1e-30, None, ALU.max)
            self.recip(st[:, 3:4], st[:, 2:3])
            pn = pns if h == 0 else self.nx(c["pn"])
            self.ts(pn[:], Pc[:], st[:, 3:4], None, ALU.mult)
            pnb = self.nx(c["pnb"])
            self.copy(pnb[:], pn[:], eng="act")
            if h > 0:
                self.tt(pns[:], pns[:], pn[:], ALU.add, eng="pool")
            pb = self.next_psb()
            for ct in range(NCT):
                self.tr(pb[:, ct * 128:(ct + 1) * 128], pnb[:, ct * 128:(ct + 1) * 128], self.ident[:])
            PTc = self.nx(c["PTc"])
            self.copy(PTc[:], pb[:, 0:NCP], eng="act")
            po = self.next_ps()
            for ct in range(NCT):
                self.mm(po[:, 0:64], PTc[:, ct * 128:(ct + 1) * 128], vcmp[:, ct, :], start=(ct == 0), stop=(ct == NCT - 1))
            self.copy(oc[:, 64 * h:64 * h + 64], po[:, 0:64], eng="act")
        selm = self.nx(c["selm"])
        if NB > 16:
            pf = self.next_ps()
            for ct in range(NCT):
                self.tr(pf[:, ct * 128:(ct + 1) * 128], pns[:, ct * 128:(ct + 1) * 128], self.ident_f[:])
            pnT = self.nx(c["pnT"])
            self.copy(pnT[:], pf[:, 0:NCP], eng="act")
            pi = self.next_ps()
            for ct in range(NCT):
                self.mm(pi[:, 0:NB], pnT[:, ct * 128:(ct + 1) * 128], self.ovl[:, ct, :], start=(ct == 0), stop=(ct == NCT - 1))
            am = self.nx(c["am"])
            self.memset(am[:], 0.0)
            for half in range(2):
                cur = 2 * i + half
                r0 = 64 * half
                v_ = am[r0:r0 + 64, :]
                self.aselect(v_, v_, [[-1, NB]], ALU.is_ge, -1e30, cur, 0)
                self.memset(am[r0:r0 + 64, 0:1], 1e30)
                self.memset(am[r0:r0 + 64, cur:cur + 1], 1e30)
                if cur >= 1:
                    self.memset(am[r0:r0 + 64, cur - 1:cur], 1e30)
            imp = self.nx(c["imp"])
            self.tt(imp[:], pi[:, 0:NB], am[:], ALU.add)
            m8 = self.nx(c["m8"])
            self.vmax(m8[:, 0:8], imp[:])
            imp2 = self.nx(c["imp2"])
            self.match_replace(imp2[:], m8[:, 0:8], imp[:], -3.0e38)
            self.vmax(m8[:, 8:16], imp2[:])
            self.ts(selm[:], imp[:], m8[:, 15:16], 8000.0, ALU.is_ge, ALU.mult)
        else:
            self.memset(selm[:], 8000.0)
        Mn = self.nx(c["Mn"])
        nbk = nk // 64
        self.act(V(Mn.h[:, 0:nk].rearrange("p (b e) -> p b e", e=64), Mn.name),
                 V(selm.h[:, 0:nbk].unsqueeze(2).to_broadcast([128, nbk, 64]), selm.name), AF.Copy)
        self.tt(Mn[:, nk - 128:nk], Mn[:, nk - 128:nk], self.cnegb[:], ALU.add)
        for h in range(4):
            base = 64 * (h % 2)
            cq = h // 2
            cm = self.nx(c["cm"])
            for kc in range(nkc):
                c0 = kc * 512
                cols = min(512, nk - c0)
                ps = self.next_ps()
                self.mm(ps[:, 0:cols], nq[base:base + 64, cq, :], c["ksT"][base:base + 64, c0:c0 + cols], start=True, stop=False)
                self.mm(ps[:, 0:cols], self.ident[:], Mn[:, c0:c0 + cols], start=False, stop=True)
                self.ts(Ssb[:, c0:c0 + cols], ps[:, 0:cols], 0.125, None, ALU.mult, ALU.max, accum=cm[:, kc:kc + 1])
            self.softmax_pv(Ssb[:, 0:nk], nk, c["Vs"], 0, osl[:, 64 * h:64 * h + 64], c["kk"], premax=cm[:, 0:nkc])
        k0 = max(0, i * 128 - 512)
        nkw = nk - k0
        boff = 640 - nkw
        for h in range(4):
            base = 64 * (h % 2)
            cq = h // 2
            cm = self.nx(c["cm"])
            nwc = 0
            for c0 in range(0, nkw, 512):
                cols = min(512, nkw - c0)
                ps = self.next_ps()
                self.mm(ps[:, 0:cols], nq[base:base + 64, cq, :], c["kwT"][base:base + 64, k0 + c0:k0 + c0 + cols], start=True, stop=False)
                self.mm(ps[:, 0:cols], self.ident[:], self.bandb[:, boff + c0:boff + c0 + cols], start=False, stop=True)
                self.ts(Sw[:, c0:c0 + cols], ps[:, 0:cols], 0.125, None, ALU.mult, ALU.max, accum=cm[:, nwc:nwc + 1])
                nwc += 1
            self.softmax_pv(Sw[:, 0:nkw], nkw, c["Vw"], k0 // 128, ow[:, 64 * h:64 * h + 64], c["kk"], premax=cm[:, 0:nwc])
        gs = self.nx(c["gs"])
        self.act(gs[:], c["ngt"][:, i, :], AF.Sigmoid)
        o = self.nx(c["o"])

        def gbc(j):
            return V(gs.h[:, j:12:3].unsqueeze(2).to_broadcast([128, 4, 64]), gs.name)
        self.tt(h4(o), h4(oc), gbc(0), ALU.mult)
        self.tt(h4(osl), h4(osl), gbc(1), ALU.mult)
        self.tt(o[:], o[:], osl[:], ALU.add)
        self.tt(h4(ow), h4(ow), gbc(2), ALU.mult)
        self.tt(o[:], o[:], ow[:], ALU.add)
        y = self.nx(c["y"])
        self.copy(y[:], o[:], eng="act")
        self.store_yT(y, c["YT"], 2, i, c["yTs"])

    def phase_dsa_nsa(self, FMS, TMB, TMF, cmp_w1, cmp_w2, cmp_pos, YT):
        S = self.S
        NT = S // 128
        self.phase_begin()
        self.cnegb = self.sb("cnegb", [128, 128], BF16)
        self.ts(self.cnegb[:], self.cneg2k[:], 8.0, None, ALU.mult)
        self.bandb = self.sb("bandb", [128, 640], BF16)
        self.ts(self.bandb[:], self.band[:], 8.0, None, ALU.mult)
        cd = self.dsa_setup(FMS, TMB, TMF, YT)
        cn = self.nsa_setup(FMS, TMB, TMF, cmp_w1, cmp_w2, cmp_pos, YT)
        P = self.P
        for i in range(NT):
            self.ps_set = (0, 3)
            self.psb_set = (0, 1)
            P.capture = []
            self.dsa_tile(cd, i)
            A = P.capture
            self.ps_set = (3, 2)
            self.psb_set = (1, 1)
            P.capture = []
            self.nsa_tile(cn, i)
            B = P.capture
            P.capture = None
            self.ps_set = (0, 5)
            self.psb_set = (0, 2)
            ia = ib = 0
            na, nb = len(A), len(B)
            while ia < na or ib < nb:
                if ib >= nb or (ia < na and ia * nb <= ib * na):
                    P.add(*A[ia][0], **A[ia][1])
                    ia += 1
                else:
                    P.add(*B[ib][0], **B[ib][1])
                    ib += 1

    def phase_merge(self, x_in, xT_in, w_in_l, w_branch, w_out, ln_g, ln_b, YT, x_out, xT_out):
        S = self.S
        self.phase_begin()
        wg = self.load_w("wg", lambda k: V(w_in_l.h[k * 128:(k + 1) * 128, 3128:7224], w_in_l.name), NKC, 4096)
        wb = self.sb("wb", [128, 4, 2, 1024], BF16)
        for n in range(4):
            for kk_ in range(2):
                self.dma(wb[:, n, kk_, :], V(w_branch.h[n, kk_ * 128:(kk_ + 1) * 128, :], w_branch.name), eng="pool")
        wo = self.load_w("wo", lambda k: V(w_out.h[k * 128:(k + 1) * 128, :], w_out.name), NKC, D)
        g_bc, b_bc, scr = self.ln_setup(ln_g, ln_b)
        xt = self.sb("xT", [128, NKC, 512], BF16)
        yt = self.sb("yT", [128, 8, 512], BF16)
        mT = self.sb("mT", [128, 8, 512], BF16)
        acc_k = self.rot("acc", 2, [128, 512], F32)
        sg_k = self.rot("sg", 2, [128, 512], F32)
        tmp_k = self.rot("tmp", 2, [128, 512], F32)
        xr = [self.sb("xr%d" % i, [128, D], F32) for i in range(2)]
        rqs = [self.sb("r%d" % i, [128, D], F32) for i in range(2)]
        ntile = S // 512

        def load_xt(t_):
            ss_ = slice(t_ * 512, (t_ + 1) * 512)
            self.dma(xt[:], V(xT_in.h.rearrange("(k p) s -> p k s", p=128)[:, :, ss_], xT_in.name))
            self.dma(yt[:], V(YT.h[:, ss_].rearrange("(c p) s -> p c s", p=128), YT.name))

        def load_xq(idx):
            self.dma(xr[idx % 2][:], x_in[idx * 128:(idx + 1) * 128, :])
        load_xt(0)
        load_xq(0)
        for t in range(ntile):
            ss = slice(t * 512, (t + 1) * 512)
            for dc in range(8):
                acc = self.nx(acc_k)
                for n in range(4):
                    pg = self.next_ps()
                    for k in range(NKC):
                        self.mm(pg[:], wg[:, k, n * 1024 + dc * 128:n * 1024 + (dc + 1) * 128], xt[:, k, :],
                                start=(k == 0), stop=(k == NKC - 1))
                    pp = self.next_ps()
                    for k2 in range(2):
                        self.mm(pp[:], wb[:, n, k2, dc * 128:(dc + 1) * 128], yt[:, 2 * n + k2, :], start=(k2 == 0), stop=(k2 == 1))
                    sg = self.nx(sg_k)
                    self.act(sg[:], pg[:], AF.Sigmoid)
                    if n == 0:
                        self.tt(acc[:], sg[:], pp[:], ALU.mult)
                    else:
                        tmp = self.nx(tmp_k)
                        self.tt(tmp[:], sg[:], pp[:], ALU.mult)
                        self.tt(acc[:], acc[:], tmp[:], ALU.add, eng="pool")
                self.copy(mT[:, dc, :], acc[:], eng="act")
            if t + 1 < ntile:
                load_xt(t + 1)
            for q in range(4):
                t0 = t * 512 + q * 128
                xq = xr[q % 2]
                rq = rqs[q % 2]
                if t * 4 + q + 1 < ntile * 4:
                    load_xq(t * 4 + q + 1)
                for half in range(2):
                    hs = slice(half * 512, (half + 1) * 512)
                    pd = self.next_ps()
                    for dc in range(8):
                        self.mm(pd[:], mT[:, dc, q * 128:(q + 1) * 128], wo[:, dc, hs], start=(dc == 0), stop=(dc == 7))
                    self.act(xq[:, hs], xq[:, hs], AF.Copy, scale=ALPHA)
                    self.stt(rq[:, hs], pd[:], 1.0, xq[:, hs], ALU.mult, ALU.add)
                self.finish_tile(rq, g_bc, b_bc, x_out, xT_out, t0, scr)

    def phase_xattn(self, x_in, xT_in, mem, wq_d, wkv_d, wo_d, ln_g, ln_b, x_out, xT_out):
        S = self.S
        self.phase_begin()
        wq = self.load_w("wq", lambda k: V(wq_d.h[k * 128:(k + 1) * 128, :], wq_d.name), NKC, D)
        wkv = self.load_w("wkv", lambda k: V(wkv_d.h[k * 128:(k + 1) * 128, :], wkv_d.name), NKC, 2 * D)
        wo = self.load_w("wo", lambda k: V(wo_d.h[k * 128:(k + 1) * 128, :], wo_d.name), NKC, D)
        g_bc, b_bc, scr = self.ln_setup(ln_g, ln_b)
        memT = self.sb("memT", [128, 8, 256], BF16)
        mr = self.sb("mr", [128, D], F32)
        mb = self.sb("mb", [128, D], BF16)
        for mt in range(2):
            self.dma(mr[:], mem[mt * 128:(mt + 1) * 128, :])
            self.copy(mb[:], mr[:], eng="act")
            pb = self.next_psb()
            for k in range(8):
                self.tr(pb[:, k * 128:(k + 1) * 128], mb[:, k * 128:(k + 1) * 128], self.ident[:])
            self.copy(memT[:, :, mt * 128:(mt + 1) * 128], V(pb.h[:, :].rearrange("p (k t) -> p k t", k=8), pb.name))
        KT = self.sb("KT", [128, 8, 256], BF16)
        for c in range(8):
            ps = self.next_ps()
            for k in range(NKC):
                self.mm(ps[:, 0:256], wkv[:, k, c * 128:(c + 1) * 128], memT[:, k, :], start=(k == 0), stop=(k == NKC - 1))
            self.copy(KT[:, c, :], ps[:, 0:256], eng=("act" if c % 2 else "dve"))
        Vm = self.sb("Vm", [128, 2, D], BF16)
        for mt in range(2):
            for half in range(2):
                ps = self.next_ps()
                for k in range(NKC):
                    self.mm(ps[:], memT[:, k, mt * 128:(mt + 1) * 128], wkv[:, k, D + half * 512:D + (half + 1) * 512],
                            start=(k == 0), stop=(k == NKC - 1))
                self.copy(Vm[:, mt, half * 512:(half + 1) * 512], ps[:], eng=("act" if half else "dve"))
        xt = self.sb("xT", [128, NKC, 512], BF16)
        qT = self.sb("qT", [128, 8, 512], BF16)
        Pf_k = self.rot("Pf", 2, [128, 4, 256], F32)
        Pb_k = self.rot("Pb", 2, [128, 4, 256], BF16)
        PT_k = self.rot("PTx", 2, [128, 8, 128], BF16)
        oT_k = self.rot("oT", 2, [128, 8, 128], BF16)
        st_k = self.rot("xst", 2, [128, 16], F32)
        xr = [self.sb("xr%d" % i, [128, D], F32) for i in range(2)]
        rqs = [self.sb("r%d" % i, [128, D], F32) for i in range(2)]
        SC = 1.0 / 16
        ntile = S // 512

        def load_xt(t_):
            ss_ = slice(t_ * 512, (t_ + 1) * 512)
            self.dma(xt[:], V(xT_in.h.rearrange("(k p) s -> p k s", p=128)[:, :, ss_], xT_in.name))

        def load_xq(idx):
            self.dma(xr[idx % 2][:], x_in[idx * 128:(idx + 1) * 128, :])
        load_xt(0)
        load_xq(0)
        for t in range(ntile):
            ss = slice(t * 512, (t + 1) * 512)
            for c in range(8):
                ps = self.next_ps()
                for k in range(NKC):
                    self.mm(ps[:], wq[:, k, c * 128:(c + 1) * 128], xt[:, k, :], start=(k == 0), stop=(k == NKC - 1))
                self.copy(qT[:, c, :], ps[:], eng=("act" if c % 2 else "dve"))
            if t + 1 < ntile:
                load_xt(t + 1)
            for q in range(4):
                t0 = t * 512 + q * 128
                tq = slice(q * 128, (q + 1) * 128)
                if t * 4 + q + 1 < ntile * 4:
                    load_xq(t * 4 + q + 1)
                pss = [self.next_ps(), self.next_ps()]
                st = self.nx(st_k)
                Pf = self.nx(Pf_k)
                for h in range(4):
                    pv = pss[h // 2][:, (h % 2) * 256:(h % 2) * 256 + 256]
                    for cc in range(2):
                        self.mm(pv, qT[:, 2 * h + cc, tq], KT[:, 2 * h + cc, :], start=(cc == 0), stop=(cc == 1))
                    self.red(st[:, h:h + 1], pv, ALU.max)
                    self.ts(st[:, 4 + h:5 + h], st[:, h:h + 1], -SC, None, ALU.mult)
                    self.act(Pf[:, h, :], pv, AF.Exp, bias=st[:, 4 + h:5 + h], scale=SC, accum=st[:, 8 + h:9 + h])
                self.recip(st[:, 12:16], st[:, 8:12])
                Pb = self.nx(Pb_k)
                self.tt(Pb[:], Pf[:], V(st.h[:, 12:16].unsqueeze(2).to_broadcast([128, 4, 256]), st.name), ALU.mult)
                pb = self.next_psb()
                for h in range(4):
                    for mc in range(2):
                        j = 2 * h + mc
                        self.tr(pb[:, j * 128:(j + 1) * 128], Pb[:, h, mc * 128:(mc + 1) * 128], self.ident[:])
                PT = self.nx(PT_k)
                self.copy(PT[:], V(pb.h[:, :].rearrange("p (j t) -> p j t", j=8), pb.name))
                oT = self.nx(oT_k)
                pso = [self.next_ps(), self.next_ps()]
                for h in range(4):
                    for dc in range(2):
                        j = 2 * h + dc
                        pv = pso[j // 4][:, (j % 4) * 128:(j % 4) * 128 + 128]
                        for mc in range(2):
                            self.mm(pv, Vm[:, mc, h * 256 + dc * 128:h * 256 + (dc + 1) * 128], PT[:, 2 * h + mc, :],
                                    start=(mc == 0), stop=(mc == 1))
                for j4 in range(2):
                    self.copy(oT[:, 4 * j4:4 * j4 + 4, :], V(pso[j4].h[:, :].rearrange("p (j t) -> p j t", j=4), pso[j4].name),
                              eng=("act" if j4 else "dve"))
                xq = xr[q % 2]
                rq = rqs[q % 2]
                for half in range(2):
                    hs = slice(half * 512, (half + 1) * 512)
                    pd = self.next_ps()
                    for c in range(8):
                        self.mm(pd[:], oT[:, c, :], wo[:, c, hs], start=(c == 0), stop=(c == 7))
                    self.act(xq[:, hs], xq[:, hs], AF.Copy, scale=ALPHA)
                    self.stt(rq[:, hs], pd[:], 1.0, xq[:, hs], ALU.mult, ALU.add)
                self.finish_tile(rq, g_bc, b_bc, x_out, xT_out, t0, scr)


OFF = dict(r_q=0, r_k=128, r_v=256, r_g=512, d_q=768, d_k=1024, d_v=1088, i_q=1152, i_k=1408, i_w=1440,
           n_q=1448, n_kc=1704, n_vc=1768, n_ks=1832, n_vs=1896, n_kw=1960, n_vw=2024, n_g=2088,
           s_z=2100, s_xbc=2356, s_dt=3124, br_g=3128)


def _partner(i, headdim, rot):
    half = rot // 2
    j = i % headdim
    b = i - j
    if j < half:
        return b + j + half
    if j < rot:
        return b + j - half
    return i


def build_colidx():
    cols = []

    def roped(name, width, headdim, rot, lo=0, rep=1):
        loc = []
        for r in range(rep):
            loc += list(range(lo, lo + width))
        assert len(loc) == 128
        a = [OFF[name] + i for i in loc]
        b = [OFF[name] + _partner(i, headdim, rot) for i in loc]
        cols.extend(a)
        cols.extend(b)

    roped("r_q", 128, 32, 32)
    roped("r_k", 128, 32, 32)
    roped("d_q", 128, 64, 16, 0)
    roped("d_q", 128, 64, 16, 128)
    roped("d_k", 64, 64, 16, 0, 2)
    roped("i_q", 128, 32, 8, 0)
    roped("i_q", 128, 32, 8, 128)
    roped("i_k", 32, 32, 8, 0, 4)
    roped("n_q", 128, 64, 16, 0)
    roped("n_q", 128, 64, 16, 128)
    roped("n_kc", 64, 64, 16, 0, 2)
    roped("n_ks", 64, 64, 16, 0, 2)
    roped("n_kw", 64, 64, 16, 0, 2)
    cols.extend([OFF["n_vc"] + i for i in range(64)] * 2)
    cols.extend([OFF["s_xbc"] + i for i in range(768)])
    for name, w in (("r_v", 256), ("r_g", 256), ("d_v", 64), ("n_vs", 64), ("n_vw", 64), ("s_z", 256),
                    ("i_w", 8), ("n_g", 12), ("s_dt", 4)):
        cols.extend([OFF[name] + i for i in range(w)])
    return np.asarray(cols, dtype=np.int64)


ROPED_TABLES = [0, 1, 2, 2, 2, 3, 3, 3, 2, 2, 2, 2, 2]
TM0 = (2 * len(ROPED_TABLES) + 7) * 128
NCOL2 = TM0 + 984
NBIS = 17
RET_LNG = [math.log1p(-2.0 ** (-5 - h)) for h in range(4)]


def host_consts(S):
    meta = np.zeros((128, 32), np.float32)

    def fill(t, headdim, rot, theta, scale):
        half = rot // 2
        inv = np.power(np.float32(theta), (-2.0 * np.arange(half, dtype=np.float32) / np.float32(rot)).astype(np.float32)).astype(np.float32)
        for p in range(128):
            i = p % headdim
            if i < rot:
                meta[p, t] = inv[i % half]
                meta[p, 4 + t] = scale
                meta[p, 8 + t] = -scale if i < half else scale
            else:
                meta[p, t] = 0.0
                meta[p, 4 + t] = 1.0
                meta[p, 8 + t] = 0.0

    fill(0, 32, 32, 10000.0, 1.0)
    fill(1, 32, 32, 10000.0, 32.0 ** -0.5)
    fill(2, 64, 16, 500000.0, 1.0)
    fill(3, 32, 8, 500000.0, 1.0)
    for p in range(128):
        meta[p, 12] = RET_LNG[p // 32]
        meta[p, 13 + p // 32] = 1.0
    bdm = np.zeros((128, 256), np.float32)
    for p in range(128):
        bdm[p, 64 * (p // 32):64 * (p // 32) + 64] = 1.0
    NC = (S - 32) // 16 + 1
    NCP = (NC + 127) // 128 * 128
    NB = S // 64
    ovl = np.zeros((NCP, NB), np.float32)
    for c in range(NC):
        for j in range(NB):
            ovl[c, j] = max(min(16 * c + 32, 64 * j + 64) - max(16 * c, 64 * j), 0) / 32.0
    return meta, bdm, ovl, NB, NCP


STAGES = ["ffn1", "inproj", "ret", "ssd", "dsa", "nsa", "merge", "xattn", "ffn2"]


def build(S, depth=DEPTH, stop_after=None):
    kb = KB(S, depth, stop_after)
    meta_np, bdm_np, ovl_np, NB, NCP = host_consts(S)
    kb.n_keep = min(256, S // 4)
    EI = "ExternalInput"
    x = kb.dram("x", [S, D], F32, kind=EI)
    mem = kb.dram("mem", [N_MEM, D], F32, kind=EI)
    ln_g = kb.dram("ln_g", [DEPTH, 4, D], F32, kind=EI)
    ln_b = kb.dram("ln_b", [DEPTH, 4, D], F32, kind=EI)
    f1gu = kb.dram("ffn1_w_gu", [DEPTH, D, 2 * DFF], F32, kind=EI)
    f1dn = kb.dram("ffn1_w_down", [DEPTH, DFF, D], F32, kind=EI)
    w_in = kb.dram("w_in", [DEPTH, D, 7224], F32, kind=EI)
    w2 = kb.dram("w2", [DEPTH, D, NCOL2], F32, kind=EI)
    cmp_w1 = kb.dram("cmp_w1", [DEPTH, 2, 2048, 64], F32, kind=EI)
    cmp_w2 = kb.dram("cmp_w2", [DEPTH, 2, 64, 64], F32, kind=EI)
    cmp_pos = kb.dram("cmp_pos", [DEPTH, 2, 32, 64], F32, kind=EI)
    conv_w = kb.dram("conv_w", [DEPTH, 4, 768], F32, kind=EI)
    conv_b = kb.dram("conv_b", [DEPTH, 768], F32, kind=EI)
    dt_bias = kb.dram("dt_bias", [DEPTH, 4], F32, kind=EI)
    a_log = kb.dram("a_log", [DEPTH, 4], F32, kind=EI)
    d_skip = kb.dram("d_skip", [DEPTH, 4], F32, kind=EI)
    norm_g = kb.dram("ssm_norm_g", [DEPTH, 256], F32, kind=EI)
    w_branch = kb.dram("w_branch", [DEPTH, 4, 256, D], F32, kind=EI)
    w_out = kb.dram("w_out", [DEPTH, D, D], F32, kind=EI)
    xwq = kb.dram("xattn_wq", [DEPTH, D, D], F32, kind=EI)
    xwkv = kb.dram("xattn_wkv", [DEPTH, D, 2 * D], F32, kind=EI)
    xwo = kb.dram("xattn_wo", [DEPTH, D, D], F32, kind=EI)
    f2gu = kb.dram("ffn2_w_gu", [DEPTH, D, 2 * DFF], F32, kind=EI)
    f2dn = kb.dram("ffn2_w_down", [DEPTH, DFF, D], F32, kind=EI)
    meta = kb.dram("meta", [128, 32], F32, kind=EI)
    bdm = kb.dram("bdm", [128, 256], F32, kind=EI)
    ovl = kb.dram("ovl", [NCP, NB], F32, kind=EI)
    out = kb.dram("out", [S, D], F32)
    xTa = kb.dram("xTa", [D, S], BF16)
    xTb = kb.dram("xTb", [D, S], BF16)
    xa = kb.dram("xa", [S, D], F32)
    xb2 = kb.dram("xb2", [S, D], F32)
    FMS = kb.dram("FMS", [20 * 128, S], BF16)
    TMB = kb.dram("TMB", [S, 960], BF16)
    TMF = kb.dram("TMF", [S, 24], F32)
    YT = kb.dram("YT", [1024, S], BF16)
    ROPE = kb.dram("ROPE", [4, 2, 128, S], F32)
    kb.setup()
    kb.setup_consts(meta, bdm, ovl, NB, NCP)
    kb.phase_rope(ROPE)
    kb.phase_transpose_in(x, xTa)

    def L(t, *idx):
        return T(t.h[idx], t.name, True)

    done = False
    xin = x
    for l in range(depth):
        last = (l == depth - 1)

        def stop(name):
            return stop_after == (l, name)
        kb.phase_ffn(xin, xTa, L(f1gu, l), L(f1dn, l), L(ln_g, l, 0), L(ln_b, l, 0), xa, xTb)
        if stop("ffn1"):
            break
        kb.phase_inproj(xTb, L(w2, l), ROPE, FMS, TMB, TMF)
        if stop("inproj"):
            break
        kb.phase_ret(FMS, TMB, YT)
        if stop("ret"):
            break
        kb.phase_ssd(FMS, TMB, TMF, L(conv_w, l), L(conv_b, l), L(dt_bias, l), L(a_log, l), L(d_skip, l), L(norm_g, l), YT)
        if stop("ssd"):
            break
        kb.phase_dsa_nsa(FMS, TMB, TMF, L(cmp_w1, l), L(cmp_w2, l), L(cmp_pos, l), YT)
        if stop("nsa") or stop("dsa"):
            break
        kb.phase_merge(xa, xTb, L(w_in, l), L(w_branch, l), L(w_out, l), L(ln_g, l, 1), L(ln_b, l, 1), YT, xb2, xTa)
        if stop("merge"):
            break
        kb.phase_xattn(xb2, xTa, mem, L(xwq, l), L(xwkv, l), L(xwo, l), L(ln_g, l, 2), L(ln_b, l, 2), xa, xTb)
        if stop("xattn"):
            break
        kb.phase_ffn(xa, xTb, L(f2gu, l), L(f2dn, l), L(ln_g, l, 3), L(ln_b, l, 3), out if last else xb2, None if last else xTa)
        if stop("ffn2"):
            break
        xin = xb2
    kb.flush_pending()
    st = kb.P.emit()
    kb.stats = st
    return kb


def make_in_maps(inputs, S, ncores):
    meta_np, bdm_np, ovl_np, NB, NCP = host_consts(S)
    colidx = build_colidx()
    w_in = np.asarray(inputs["w_in"], dtype=np.float32)
    w2 = np.ascontiguousarray(w_in[:, :, colidx])
    shared = {k: np.ascontiguousarray(np.asarray(v, dtype=np.float32)) for k, v in inputs.items() if k not in ("x", "mem")}
    shared["w2"] = w2
    shared["meta"] = meta_np
    shared["bdm"] = bdm_np
    shared["ovl"] = ovl_np
    maps = []
    for b in range(ncores):
        m = dict(shared)
        m["x"] = np.ascontiguousarray(np.asarray(inputs["x"][b, :S], dtype=np.float32))
        m["mem"] = np.ascontiguousarray(np.asarray(inputs["mem"][b], dtype=np.float32))
        maps.append(m)
    return maps


def kernel(**inputs):
    S = inputs["x"].shape[1]
    B = inputs["x"].shape[0]
    kb = build(S)
    maps = make_in_maps(inputs, S, B)
    res = run_bass_kernel_spmd(kb.nc, maps, core_ids=list(range(B)))
    out = np.stack([np.asarray(r["out"], dtype=np.float32) for r in res.results], axis=0)
    return out
```

```python
import math
import sys
import numpy as np
import concourse.bass as bass
import concourse.mybir as mybir
from concourse.bass_utils import run_bass_kernel_spmd

F32 = mybir.dt.float32
BF16 = mybir.dt.bfloat16
I32 = mybir.dt.int32
AF = mybir.ActivationFunctionType
ALU = mybir.AluOpType
AX = mybir.AxisListType

SEM_LIMIT = 30000
N_DMA_SEMS = 24


class Buf:
    __slots__ = ("name", "last_w", "readers")

    def __init__(self, name):
        self.name = name
        self.last_w = None
        self.readers = []


class Op:
    __slots__ = ("eng", "fn", "deps", "need_inc", "sem", "val", "is_dma", "idx", "tag", "odeps", "n", "seg", "pfirst", "fin", "st0", "crit")


class Prog:
    def __init__(self, nc):
        self.nc = nc
        self.engs = {"pe": nc.tensor, "act": nc.scalar, "dve": nc.vector, "pool": nc.gpsimd, "sp": nc.sync}
        self.ops = []
        self.bufs = {}
        self.last_on = {}
        self.dmas_since = []
        self.phase_deps = []
        self.phase_bufs = set()
        self.capture = None
        self.seg = 0
        self.do_sched = True
        self.est_time = 0.0

    def buf(self, name):
        b = self.bufs.get(name)
        if b is None:
            b = self.bufs[name] = Buf(name)
        return b

    def add(self, eng, fn, reads=(), writes=(), dma=False, extra_deps=(), n=64):
        if self.capture is not None:
            self.capture.append(((eng, fn), dict(reads=list(reads), writes=list(writes), dma=dma, n=n)))
            return None
        op = Op()
        op.eng = eng
        op.fn = fn
        op.is_dma = dma
        op.need_inc = False
        op.sem = None
        op.val = 0
        op.n = n
        op.seg = self.seg
        op.pfirst = False
        op.fin = 0.0
        op.idx = len(self.ops)
        try:
            op.tag = (sys._getframe(2).f_lineno, 0)
        except Exception:
            op.tag = (0, 0)
        deps = {}
        for b in reads:
            b = self.buf(b)
            w = b.last_w
            if w is not None:
                deps[w.idx] = (w, "raw")
        for b in writes:
            b = self.buf(b)
            w = b.last_w
            if w is not None and w.idx not in deps:
                deps[w.idx] = (w, "waw")
            for r in b.readers:
                if r.idx not in deps:
                    deps[r.idx] = (r, "war")
        real = []
        order = []
        for d, kind in deps.values():
            if (not d.is_dma) and d.eng == eng and not dma:
                if eng == "pe" or kind != "raw":
                    order.append(d)
                    continue
            real.append(d)
        for d in extra_deps:
            real.append(d)
        for b in list(reads) + list(writes):
            if b not in self.phase_bufs:
                self.phase_bufs.add(b)
                op.pfirst = True
        op.deps = real
        op.odeps = order
        for b in writes:
            b = self.buf(b)
            b.last_w = op
            b.readers = []
        for b in reads:
            self.buf(b).readers.append(op)
        self.ops.append(op)
        return op

    def barrier(self):
        self.seg += 1
        self.phase_bufs = set()

    def _cost(self, op):
        n = op.n
        e = op.eng
        if op.is_dma:
            return 0.08, 2.0 + n / 100e3
        if e == "pe":
            c = 0.035 + n / 2400.0
        elif e == "act":
            c = 0.22 + n / 1200.0
        elif e == "dve":
            c = 0.08 + n / 960.0
        elif e == "pool":
            c = 0.15 + n / 500.0
        else:
            c = 0.05
        return c, c

    def schedule(self):
        import heapq
        SCHED = self.do_sched
        self.seg_stats = []
        order = []
        ops = self.ops
        nseg = self.seg + 1
        segs = [[] for _ in range(nseg)]
        for op in ops:
            segs[op.seg].append(op)
        t_base = 0.0
        engs = list(self.engs.keys())
        for sg in segs:
            if not sg:
                continue
            if not SCHED:
                order.extend(sg)
                continue
            inseg = set(id(o) for o in sg)
            indeg = {}
            succ = {}
            dr = {}
            for op in sg:
                cnt = 0
                for d in op.deps + op.odeps:
                    if id(d) in inseg:
                        cnt += 1
                        succ.setdefault(id(d), []).append(op)
                indeg[id(op)] = cnt
                dr[id(op)] = t_base
            wait_h = {e: [] for e in engs}
            rdy_h = {e: [] for e in engs}
            free = {e: t_base for e in engs}
            for op in sg:
                if indeg[id(op)] == 0:
                    heapq.heappush(wait_h[op.eng], (dr[id(op)], op.idx, op))
            left = len(sg)
            tmax = t_base
            while left:
                best = None
                for e in engs:
                    wh = wait_h[e]
                    rh = rdy_h[e]
                    fe = free[e]
                    while wh and wh[0][0] <= fe:
                        _, ix, o = heapq.heappop(wh)
                        heapq.heappush(rh, (ix, o))
                    if rh:
                        cand = (fe, rh[0][0], e, 0)
                    elif wh:
                        cand = (wh[0][0], wh[0][1], e, 1)
                    else:
                        continue
                    if best is None or cand[:2] < best[:2]:
                        best = cand
                start, _, e, which = best
                if which == 0:
                    _, op = heapq.heappop(rdy_h[e])
                else:
                    _, _, op = heapq.heappop(wait_h[e])
                busy, lat = self._cost(op)
                free[e] = start + busy
                op.fin = start + lat
                op.st0 = start
                if op.fin > tmax:
                    tmax = op.fin
                order.append(op)
                left -= 1
                for sc in succ.get(id(op), ()):
                    k = id(sc)
                    extra = 0.05 if (sc.eng == op.eng and not op.is_dma) else 0.35
                    t = op.fin + extra
                    if t > dr[k]:
                        dr[k] = t
                    indeg[k] -= 1
                    if indeg[k] == 0:
                        heapq.heappush(wait_h[sc.eng], (dr[k], sc.idx, sc))
            busy_e = {e: 0.0 for e in engs}
            for o in sg:
                busy_e[o.eng] += self._cost(o)[0]
            self.seg_stats.append((sg[0].seg, len(sg), tmax - t_base, busy_e))
            t_base = tmax
        self.est_time = t_base
        return order

    def emit(self, final_wait_eng="sp"):
        nc = self.nc
        order = self.schedule()
        last_eng = {}
        prev_last = {}
        prev_dmas = []
        older_dmas = []
        cur_dmas = []
        cur_seg = -1
        for op in order:
            if op.seg != cur_seg:
                cur_seg = op.seg
                prev_last = dict(last_eng)
                prev_dmas = older_dmas + cur_dmas
                older_dmas = cur_dmas
                cur_dmas = []
            if op.pfirst:
                op.deps = op.deps + list(prev_last.values()) + prev_dmas
            if op.is_dma:
                cur_dmas.append(op)
            else:
                last_eng[op.eng] = op
        for op in order:
            for d in op.deps:
                d.need_inc = True
            if op.is_dma:
                op.need_inc = True
        eng_sem = {}
        eng_cnt = {}
        dma_sems = [nc.alloc_semaphore("dq%d" % i) for i in range(N_DMA_SEMS)]
        dma_cnt = [0] * N_DMA_SEMS
        dma_last = [None] * N_DMA_SEMS
        ndma = 0
        for op in order:
            if not op.need_inc:
                continue
            if op.is_dma:
                j = ndma % N_DMA_SEMS
                ndma += 1
                if dma_last[j] is not None:
                    op.deps.append(dma_last[j])
                dma_cnt[j] += 16
                op.sem = dma_sems[j]
                op.val = dma_cnt[j]
                dma_last[j] = op
            else:
                e = op.eng
                if e not in eng_sem or eng_cnt[e] >= SEM_LIMIT:
                    eng_sem[e] = nc.alloc_semaphore("s_%s_%d" % (e, op.idx))
                    eng_cnt[e] = 0
                eng_cnt[e] += 1
                op.sem = eng_sem[e]
                op.val = eng_cnt[e]
        waited = {}
        nwaits = 0
        for op in order:
            E = self.engs[op.eng]
            need = {}
            for d in op.deps:
                k = id(d.sem)
                if k not in need or need[k][1] < d.val:
                    need[k] = (d.sem, d.val)
            for k, (sem, val) in need.items():
                wk = (op.eng, k)
                if waited.get(wk, 0) >= val:
                    continue
                E.wait_ge(sem, val)
                nwaits += 1
                waited[wk] = val
            try:
                inst = op.fn()
            except Exception:
                print('EMIT FAIL at op', op.idx, op.eng)
                raise
            if op.need_inc:
                inst.then_inc(op.sem, 16 if op.is_dma else 1)
        E = self.engs[final_wait_eng]
        for j in range(N_DMA_SEMS):
            if dma_cnt[j] > 0:
                E.wait_ge(dma_sems[j], dma_cnt[j])
        self.stats = dict(n_ops=len(self.ops), n_waits=nwaits, n_dma=ndma,
                          n_inc=sum(1 for o in self.ops if o.need_inc), est_ms=self.est_time / 1e3)
        return self.stats


class V:
    __slots__ = ("ap", "b")

    def __init__(self, ap, b):
        self.ap = ap
        self.b = b


class T:
    def __init__(self, h, name, dram=False):
        self.h = h
        self.name = name
        self.dram = dram

    def __getitem__(self, idx):
        if self.dram:
            return V(self.h[idx], self.name)
        return V(self.h[idx], self.name)

    def v(self, ap):
        return V(ap, self.name)


DT_SIZE = {F32: 4, BF16: 2, I32: 4}

D = 1024
DFF = 2816
NKC = D // 128
NFC = DFF // 128
LN_EPS = 1e-5
DEPTH = 2
ALPHA = (2 * DEPTH) ** 0.25
N_MEM = 256


class KB:
    def __init__(self, S, depth=DEPTH, stop_after=None, debug=()):
        self.S = S
        self.depth = depth
        self.stop_after = stop_after
        self.debug = debug
        self.nc = bass.Bass("TRN2", target_bir_lowering=False)
        self.P = Prog(self.nc)
        self.uid = 0
        self.sb_base = 0
        self.sb_cur = 0
        self.outs = {}
        self.arena = None
        self.rots = {}
        self.pt_cnt = 0
        self.pending_T = None
        self.xb_cnt = 0
        self.ps_set = (0, 5)
        self.psb_set = (0, 2)
        self.fill_regs = {}
        self.n_keep = 256

    def sb(self, name, shape, dtype):
        nbytes = int(np.prod(shape[1:])) * DT_SIZE[dtype]
        nbytes = (nbytes + 63) // 64 * 64
        off = self.sb_cur
        self.sb_cur += nbytes
        assert self.sb_cur <= 207 * 1024, ("SBUF overflow", name, self.sb_cur)
        self.uid += 1
        if self.arena is None:
            self.arena = self.nc.alloc_sbuf_tensor("arena", [128, 207 * 1024], mybir.dt.uint8)
        ap = self.arena[:, off:off + int(np.prod(shape[1:])) * DT_SIZE[dtype]].bitcast(dtype)
        if len(shape) == 3:
            ap = ap.rearrange("p (a b) -> p a b", a=shape[1])
        elif len(shape) == 4:
            ap = ap.rearrange("p (a b c) -> p a b c", a=shape[1], b=shape[2])
        if shape[0] < 128:
            ap = ap[0:shape[0]]
        return T(ap, "%s_%d" % (name, self.uid))

    def phase_begin(self):
        self.flush_pending()
        self.P.barrier()
        self.sb_cur = self.sb_base

    def dram(self, name, shape, dtype, kind="ExternalOutput"):
        h = self.nc.dram_tensor(name, list(shape), dtype, kind=kind)
        return T(h.ap(), name, dram=True)

    def _rw(self, reads, writes):
        return [r.b for r in reads if isinstance(r, V)], [w.b for w in writes]

    def dma(self, out, in_, eng="sp"):
        nc = self.nc
        E = self.P.engs[eng]
        return self.P.add(eng, lambda: E.dma_start(out=out.ap, in_=in_.ap), reads=[in_.b], writes=[out.b], dma=True,
                          n=int(np.prod(out.ap.shape)) * 2)

    def mm(self, out, lhsT, rhs, start=True, stop=True):
        nc = self.nc
        return self.P.add("pe", lambda: nc.tensor.matmul(out.ap, lhsT.ap, rhs.ap, start=start, stop=stop),
                          reads=[lhsT.b, rhs.b], writes=[out.b], n=int(np.prod(out.ap.shape[1:])) * (4 if lhsT.ap.dtype == F32 else 1))

    def tr(self, out, in_, ident):
        nc = self.nc
        return self.P.add("pe", lambda: nc.tensor.transpose(out.ap, in_.ap, ident.ap),
                          reads=[in_.b, ident.b], writes=[out.b], n=200)

    def act(self, out, in_, func, bias=None, scale=None, accum=None, eng="act"):
        nc = self.nc
        kw = {}
        reads = [in_.b]
        writes = [out.b]
        if bias is not None:
            if isinstance(bias, V):
                kw["bias"] = bias.ap
                reads.append(bias.b)
            else:
                kw["bias"] = bias
        if scale is not None:
            if isinstance(scale, V):
                kw["scale"] = scale.ap
                reads.append(scale.b)
            else:
                kw["scale"] = scale
        if accum is not None:
            kw["accum_out"] = accum.ap
            writes.append(accum.b)
        return self.P.add("act", lambda: nc.scalar.activation(out=out.ap, in_=in_.ap, func=func, **kw),
                          reads=reads, writes=writes, n=int(np.prod(out.ap.shape[1:])))

    def ts(self, out, in0, s1, s2, op0, op1=None, accum=None, eng="dve"):
        E = self.P.engs[eng]
        reads = [in0.b]
        writes = [out.b]
        a1 = s1
        a2 = s2
        if isinstance(s1, V):
            a1 = s1.ap
            reads.append(s1.b)
        if isinstance(s2, V):
            a2 = s2.ap
            reads.append(s2.b)
        kw = {}
        if op1 is not None:
            kw["op1"] = op1
        if accum is not None:
            kw["accum_out"] = accum.ap
            writes.append(accum.b)
        return self.P.add(eng, lambda: E.tensor_scalar(out=out.ap, in0=in0.ap, scalar1=a1, scalar2=a2, op0=op0, **kw),
                          reads=reads, writes=writes, n=int(np.prod(out.ap.shape[1:])))

    def tt(self, out, in0, in1, op, eng="dve"):
        E = self.P.engs[eng]
        return self.P.add(eng, lambda: E.tensor_tensor(out=out.ap, in0=in0.ap, in1=in1.ap, op=op),
                          reads=[in0.b, in1.b], writes=[out.b], n=int(np.prod(out.ap.shape[1:])))

    def stt(self, out, in0, scalar, in1, op0, op1, accum=None):
        nc = self.nc
        reads = [in0.b, in1.b]
        writes = [out.b]
        a = scalar
        if isinstance(scalar, V):
            a = scalar.ap
            reads.append(scalar.b)
        kw = {}
        if accum is not None:
            kw["accum_out"] = accum.ap
            writes.append(accum.b)
        return self.P.add("dve", lambda: nc.vector.scalar_tensor_tensor(out=out.ap, in0=in0.ap, scalar=a, in1=in1.ap,
                                                                     op0=op0, op1=op1, **kw),
                          reads=reads, writes=writes, n=int(np.prod(out.ap.shape[1:])))

    def copy(self, out, in_, eng="dve"):
        E = self.P.engs[eng]
        if eng == "act":
            return self.P.add(eng, lambda: E.copy(out=out.ap, in_=in_.ap), reads=[in_.b], writes=[out.b], n=int(np.prod(out.ap.shape[1:])))
        return self.P.add(eng, lambda: E.tensor_copy(out=out.ap, in_=in_.ap), reads=[in_.b], writes=[out.b], n=int(np.prod(out.ap.shape[1:])))

    def memset(self, out, val, eng="pool"):
        E = self.P.engs[eng]
        return self.P.add(eng, lambda: E.memset(out.ap, val), writes=[out.b], n=int(np.prod(out.ap.shape[1:])))

    def red(self, out, in_, op, axis=AX.X, eng="dve"):
        E = self.P.engs[eng]
        return self.P.add(eng, lambda: E.tensor_reduce(out=out.ap, in_=in_.ap, axis=axis, op=op),
                          reads=[in_.b], writes=[out.b], n=int(np.prod(in_.ap.shape[1:])))

    def recip(self, out, in_):
        nc = self.nc
        return self.P.add("dve", lambda: nc.vector.reciprocal(out=out.ap, in_=in_.ap), reads=[in_.b], writes=[out.b])

    def aselect(self, out, in_, pattern, cmp, fill, base, cm):
        nc = self.nc
        regs = self.fill_regs

        def fn():
            if fill not in regs:
                regs[fill] = nc.gpsimd.to_reg(float(fill))
            return nc.gpsimd.affine_select(out=out.ap, in_=in_.ap, pattern=pattern, compare_op=cmp,
                                           fill=regs[fill], base=base, channel_multiplier=cm)
        return self.P.add("pool", fn, reads=[in_.b], writes=[out.b], n=int(np.prod(out.ap.shape[1:])))

    def iota(self, out, pattern, base, cm):
        nc = self.nc
        return self.P.add("pool", lambda: nc.gpsimd.iota(out.ap, pattern=pattern, base=base, channel_multiplier=cm,
                                                         allow_small_or_imprecise_dtypes=True), writes=[out.b], n=int(np.prod(out.ap.shape[1:])))

    def setup(self):
        nc = self.nc
        self.ps = []
        for i in range(5):
            h = nc.alloc_psum_tensor("ps%d" % i, [128, 512], F32)
            self.ps.append(T(h, "ps%d" % i))
        self.psb = []
        for i in range(2):
            h = nc.alloc_psum_tensor("psb%d" % i, [128, 1024], BF16)
            self.psb.append(T(h, "psb%d" % i))
        self.ps_rr = 0
        self.psb_rr = 0
        self.ident_f = self.sb("identf", [128, 128], F32)
        self.ident = self.sb("ident", [128, 128], BF16)
        self.memset(self.ident_f[:], 1.0)
        self.aselect(self.ident_f[:], self.ident_f[:], [[-1, 128]], ALU.is_equal, 0.0, 0, 1)
        self.copy(self.ident[:], self.ident_f[:], eng="pool")
        self.sb_base = self.sb_cur

    def next_ps(self):
        b0, n = self.ps_set
        t = self.ps[b0 + self.ps_rr % n]
        self.ps_rr += 1
        return t

    def next_psb(self):
        b0, n = self.psb_set
        t = self.psb[b0 + self.psb_rr % n]
        self.psb_rr += 1
        return t

    def load_w(self, name, dram_ap_fn, kchunks, ncols, eng="pool", split=4):
        w = self.sb(name, [128, kchunks, ncols], BF16)
        for k in range(kchunks):
            self.dma(w[:, k, :], dram_ap_fn(k), eng="pool")
        return w

    def layer_norm_tile(self, r, g_bc, b_bc, out_f32, scr):
        st = scr["st"]
        junk = scr["junk"]
        self.act(junk[:], r[:], AF.Identity, accum=st[:, 0:1])
        self.act(junk[:], r[:], AF.Square, accum=st[:, 1:2])
        self.ts(st[:, 2:3], st[:, 0:1], 1.0 / D, None, ALU.mult)
        self.tt(st[:, 3:4], st[:, 2:3], st[:, 2:3], ALU.mult)
        self.stt(st[:, 4:5], st[:, 1:2], 1.0 / D, st[:, 3:4], ALU.mult, ALU.subtract)
        self.ts(st[:, 4:5], st[:, 4:5], 0.0, LN_EPS, ALU.max, ALU.add)
        self.act(st[:, 5:6], st[:, 4:5], AF.Sqrt)
        self.recip(st[:, 6:7], st[:, 5:6])
        self.ts(out_f32[:], r[:], st[:, 2:3], st[:, 6:7], ALU.subtract, ALU.mult)
        self.tt(out_f32[:], out_f32[:], g_bc[:], ALU.mult)
        self.tt(out_f32[:], out_f32[:], b_bc[:], ALU.add)

    def store_xT(self, x_f32, xT_dram, t0, scr, defer=False):
        xbl = scr["xb"]
        if isinstance(xbl, list):
            xb = xbl[self.xb_cnt % len(xbl)]
            self.xb_cnt += 1
        else:
            xb = xbl
        xTs = scr["xTs"]
        self.copy(xb[:], x_f32[:], eng="act")

        def part_b():
            pb = self.next_psb()
            for k in range(NKC):
                self.tr(pb[:, k * 128:(k + 1) * 128], xb[:, k * 128:(k + 1) * 128], self.ident[:])
            self.copy(xTs[:], pb[:, :], eng="dve")
            self.dma(V(xT_dram.h.rearrange("(k p) s -> p k s", p=128)[:, :, t0:t0 + 128], xT_dram.name),
                     V(xTs.h[:].rearrange("p (k t) -> p k t", k=NKC), xTs.name))
        if defer:
            self.flush_pending()
            self.pending_T = part_b
        else:
            part_b()

    def flush_pending(self):
        if self.pending_T is not None:
            f = self.pending_T
            self.pending_T = None
            f()

    def dma_s(self, out, in_, eng="sp"):
        E = self.P.engs[eng]
        return self.P.add(eng, lambda: E.dma_start(out=out.ap, in_=in_.ap, allow_slow_non_contiguous=True),
                          reads=[in_.b], writes=[out.b], dma=True, n=int(np.prod(out.ap.shape)) * 8)

    def rot(self, name, n, shape, dtype):
        key = "_rot_" + name
        lst = [self.sb(name + str(i), shape, dtype) for i in range(n)]
        self.rots[key] = [lst, 0]
        return key

    def nx(self, key):
        lst, i = self.rots[key]
        self.rots[key][1] = i + 1
        return lst[i % len(lst)]

    def vmax(self, out, in_):
        nc = self.nc
        return self.P.add("dve", lambda: nc.vector.max(out=out.ap, in_=in_.ap), reads=[in_.b], writes=[out.b], n=int(np.prod(in_.ap.shape[1:])))

    def match_replace(self, out, rep, vals, imm):
        nc = self.nc
        return self.P.add("dve", lambda: nc.vector.match_replace(out=out.ap, in_to_replace=rep.ap, in_values=vals.ap, imm_value=imm),
                          reads=[rep.b, vals.b], writes=[out.b], n=int(np.prod(vals.ap.shape[1:])))

    def redabs(self, out, in_):
        nc = self.nc
        return self.P.add("dve", lambda: nc.vector.tensor_reduce(out=out.ap, in_=in_.ap, axis=AX.X, op=ALU.max,
                                                                 apply_absolute_value=True),
                          reads=[in_.b], writes=[out.b], n=int(np.prod(in_.ap.shape[1:])))

    def fm_rows(self, FMS, c0, nchunk, s0, s1):
        return V(FMS.h[c0 * 128:(c0 + nchunk) * 128, s0:s1].rearrange("(c p) s -> p c s", p=128), FMS.name)

    def setup_consts(self, meta, bdm, ovl, NB, NCP):
        S = self.S
        NT = S // 128
        self.NB = NB
        self.NCP = NCP
        self.meta = self.sb("meta", [128, 32], F32)
        self.dma(self.meta[:], meta[:, :])
        self.bdm = self.sb("bdm", [128, 256], F32)
        self.dma(self.bdm[:], bdm[:, :])
        self.ovl = self.sb("ovl", [128, NCP // 128, NB], F32)
        self.dma(self.ovl[:], V(ovl.h.rearrange("(c p) j -> p c j", p=128), ovl.name))
        self.U = self.sb("U", [128, 128], F32)
        self.memset(self.U[:], 1.0)
        self.aselect(self.U[:], self.U[:], [[1, 128]], ALU.is_ge, 0.0, 0, -1)
        self.cneg30 = self.sb("cneg30", [128, 128], F32)
        self.memset(self.cneg30[:], 0.0)
        self.aselect(self.cneg30[:], self.cneg30[:], [[-1, 128]], ALU.is_ge, -1e30, 0, 1)
        self.cneg2k = self.sb("cneg2k", [128, 128], F32)
        self.memset(self.cneg2k[:], 0.0)
        self.aselect(self.cneg2k[:], self.cneg2k[:], [[-1, 128]], ALU.is_ge, -2000.0, 0, 1)
        self.band = self.sb("band", [128, 640], F32)
        self.memset(self.band[:], 0.0)
        self.aselect(self.band[:], self.band[:], [[1, 640]], ALU.is_ge, -2000.0, -1, -1)
        self.aselect(self.band[:], self.band[:], [[-1, 640]], ALU.is_ge, -2000.0, 512, 1)
        self.decayT4 = self.sb("decayT4", [128, 4, 128], F32)
        self.xi = self.sb("xi", [128, 128], F32)
        self.zeta = self.sb("zeta", [128, 128], F32)
        self.cdecay = self.sb("cdecay", [128, 1], F32)
        self.rkc = self.sb("rkc", [128, 20], F32)
        self.sb_base = self.sb_cur
        dji = self.sb("dji", [128, 128], F32)
        self.iota(dji[:], [[1, 128]], 0, -1)
        for h in range(4):
            self.act(self.decayT4[:, h, :], dji[:], AF.Exp, scale=RET_LNG[h])
        self.tt(self.decayT4[:], self.decayT4[:], V(self.U.h[:, :].unsqueeze(1).to_broadcast([128, 4, 128]), self.U.name), ALU.mult)
        ip1 = self.sb("ip1", [128, 128], F32)
        self.iota(ip1[:], [[1, 128]], 1, 0)
        self.act(self.xi[:], ip1[:], AF.Exp, scale=self.meta[:, 12:13])
        jr = self.sb("jr", [128, 128], F32)
        self.iota(jr[:], [[0, 128]], 127, -1)
        for h in range(4):
            self.act(self.zeta[:, 32 * h:32 * h + 32], jr[:, 32 * h:32 * h + 32], AF.Exp, scale=RET_LNG[h])
        c128 = self.sb("c128", [128, 1], F32)
        self.memset(c128[:], 128.0)
        self.act(self.cdecay[:], c128[:], AF.Exp, scale=self.meta[:, 12:13])
        for k in range(20):
            self.memset(self.rkc[:, k:k + 1], 2.0 ** (-k))

    def build_addmask(self):
        NT = self.S // 128
        NB = self.NB
        self.addmask = self.sb("addmask", [128, NT, NB], F32)
        self.memset(self.addmask[:], 0.0)
        for i in range(NT):
            for half in range(2):
                cur = 2 * i + half
                r0 = 64 * half
                v = self.addmask[r0:r0 + 64, i, :]
                self.aselect(v, v, [[-1, NB]], ALU.is_ge, -1e30, cur, 0)
                self.memset(self.addmask[r0:r0 + 64, i, 0:1], 1e30)
                self.memset(self.addmask[r0:r0 + 64, i, cur:cur + 1], 1e30)
                if cur >= 1:
                    self.memset(self.addmask[r0:r0 + 64, i, cur - 1:cur], 1e30)

    def phase_rope(self, ROPE):
        S = self.S
        self.phase_begin()
        pos = self.sb("pos", [128, S], F32)
        self.iota(pos[:], [[1, S]], 0, 0)
        a = self.sb("a", [128, S], F32)
        ki = self.sb("ki", [128, S], I32)
        kf = self.sb("kf", [128, S], F32)
        m = self.sb("m", [128, S], F32)
        r = self.sb("r", [128, S], F32)
        PI = math.pi
        for t in range(4):
            for which in range(2):
                self.ts(a[:], pos[:], self.meta[:, t:t + 1], (PI / 2 if which == 0 else 0.0), ALU.mult, ALU.add)
                self.ts(kf[:], a[:], 1.0 / (2 * PI), None, ALU.mult)
                self.copy(ki[:], kf[:])
                self.copy(kf[:], ki[:])
                self.stt(r[:], kf[:], -2 * PI, a[:], ALU.mult, ALU.add)
                self.ts(m[:], r[:], PI, -2 * PI, ALU.is_gt, ALU.mult)
                self.tt(r[:], r[:], m[:], ALU.add)
                self.ts(m[:], r[:], -PI, 2 * PI, ALU.is_lt, ALU.mult)
                self.tt(r[:], r[:], m[:], ALU.add)
                self.ts(r[:], r[:], PI, -PI, ALU.min, ALU.max)
                self.act(r[:], r[:], AF.Sin)
                col = 4 + 4 * which + t
                self.ts(r[:], r[:], self.meta[:, col:col + 1], None, ALU.mult)
                self.dma(ROPE[t, which], r[:])

    def finish_tile(self, rq, g_bc, b_bc, x_out, xT_out, t0, scr):
        self.layer_norm_tile(rq, g_bc, b_bc, rq, scr)
        self.dma(x_out[t0:t0 + 128, :], rq[:])
        if xT_out is not None:
            self.store_xT(rq, xT_out, t0, scr, defer=True)

    def ln_setup(self, ln_g, ln_b):
        g_bc = self.sb("g_bc", [128, D], F32)
        b_bc = self.sb("b_bc", [128, D], F32)
        self.dma(g_bc[:], V(ln_g.h.partition_broadcast(128), ln_g.name))
        self.dma(b_bc[:], V(ln_b.h.partition_broadcast(128), ln_b.name))
        scr = dict(st=self.sb("st", [128, 8], F32), junk=self.sb("junk", [128, D], BF16),
                   xb=[self.sb("xb0", [128, D], BF16), self.sb("xb1", [128, D], BF16)], xTs=self.sb("xTs", [128, D], BF16))
        return g_bc, b_bc, scr

    def phase_ffn(self, x_in, xT_in, w_gu, w_down, ln_g, ln_b, x_out, xT_out):
        S = self.S
        self.phase_begin()
        wgu_v = w_gu.h.rearrange("(k p) c -> p k c", p=128)
        wgb = []
        for jb in range(NFC // 2):
            blk = self.sb("wgu%d" % jb, [128, NKC, 512], BF16)
            self.dma(blk[:, :, 0:256], V(wgu_v[:, :, jb * 256:(jb + 1) * 256], w_gu.name), eng="pool")
            self.dma(blk[:, :, 256:512], V(wgu_v[:, :, DFF + jb * 256:DFF + (jb + 1) * 256], w_gu.name), eng="pool")
            wgb.append(blk)
        wdn = self.load_w("wdn", lambda k: V(w_down.h[k * 128:(k + 1) * 128, :], w_down.name), NFC, D)
        g_bc, b_bc, scr = self.ln_setup(ln_g, ln_b)
        xt = self.sb("xT", [128, NKC, 512], BF16)
        hT = self.sb("hT", [128, NFC, 512], BF16)
        sg = [self.sb("sg%d" % i, [128, 512], BF16) for i in range(2)]
        xr = [self.sb("xr%d" % i, [128, D], F32) for i in range(2)]
        rqs = [self.sb("r%d" % i, [128, D], F32) for i in range(2)]
        ntile = S // 512

        def load_xt(t_):
            self.dma(xt[:], V(xT_in.h.rearrange("(k p) s -> p k s", p=128)[:, :, t_ * 512:(t_ + 1) * 512], xT_in.name))

        def load_xq(idx):
            self.dma(xr[idx % 2][:], x_in[idx * 128:(idx + 1) * 128, :])
        load_xt(0)
        load_xq(0)
        for t in range(ntile):
            for j in range(NFC):
                pg = self.next_ps()
                pu = self.next_ps()
                wb_ = wgb[j // 2]
                o_ = (j % 2) * 128
                for k in range(NKC):
                    self.mm(pg[:], wb_[:, k, o_:o_ + 128], xt[:, k, :], start=(k == 0), stop=(k == NKC - 1))
                for k in range(NKC):
                    self.mm(pu[:], wb_[:, k, 256 + o_:256 + o_ + 128], xt[:, k, :], start=(k == 0), stop=(k == NKC - 1))
                s = sg[j % 2]
                self.act(s[:], pg[:], AF.Silu)
                self.tt(hT[:, j, :], s[:], pu[:], ALU.mult)
            if t + 1 < ntile:
                load_xt(t + 1)
            for q in range(4):
                t0 = t * 512 + q * 128
                xq = xr[q % 2]
                rq = rqs[q % 2]
                if t * 4 + q + 1 < ntile * 4:
                    load_xq(t * 4 + q + 1)
                for half in range(2):
                    hs = slice(half * 512, (half + 1) * 512)
                    pd = self.next_ps()
                    for j in range(NFC):
                        self.mm(pd[:], hT[:, j, q * 128:(q + 1) * 128], wdn[:, j, hs], start=(j == 0), stop=(j == NFC - 1))
                    self.act(xq[:, hs], xq[:, hs], AF.Copy, scale=ALPHA)
                    self.stt(rq[:, hs], pd[:], 0.5, xq[:, hs], ALU.mult, ALU.add)
                self.finish_tile(rq, g_bc, b_bc, x_out, xT_out, t0, scr)

    def phase_transpose_in(self, x_in, xT_out):
        S = self.S
        self.phase_begin()
        xr = [self.sb("xr%d" % i, [128, D], F32) for i in range(2)]
        scr = dict(xb=self.sb("xb", [128, D], BF16), xTs=self.sb("xTs", [128, D], BF16))
        for i in range(S // 128):
            xq = xr[i % 2]
            self.dma(xq[:], x_in[i * 128:(i + 1) * 128, :])
            self.store_xT(xq, xT_out, i * 128, scr)

    def phase_inproj(self, xT_in, w2, ROPE, FMS, TMB, TMF):
        S = self.S
        self.phase_begin()
        w = self.sb("win", [128, NKC, NCOL2], BF16)
        w2_v = w2.h.rearrange("(k p) c -> p k c", p=128)
        bounds = list(range(0, 4096 + 1, 512)) + [TM0, TM0 + 512, NCOL2]
        for bi in range(len(bounds) - 1):
            a_, b_ = bounds[bi], bounds[bi + 1]
            self.P.add("pool", (lambda a=a_, b=b_: self.nc.gpsimd.dma_start(out=w.h[:, :, a:b], in_=w2_v[:, :, a:b])),
                       reads=[w2.name], writes=["win_b%d" % bi], dma=True, n=128 * NKC * (b_ - a_) * 2)

        def wv(k, a, b):
            for bi in range(len(bounds) - 1):
                if bounds[bi] <= a and b <= bounds[bi + 1]:
                    return V(w.h[:, k, a:b], "win_b%d" % bi)
            raise AssertionError((a, b))
        xts = [self.sb("xT%d" % i_, [128, NKC, 512], BF16) for i_ in range(2)]
        tabs = [self.sb("tab%d" % i_, [128, 4, 2, 512], F32) for i_ in range(2)]
        t1 = self.rot("t1", 3, [128, 512], F32)
        t2 = self.rot("t2", 3, [128, 512], F32)
        ob = self.rot("ob", 4, [128, 512], BF16)
        tmb = self.rot("tmb", 2, [128, 960], BF16)
        tmf = self.rot("tmf", 2, [128, 24], F32)
        ntile = S // 512

        def load_t(t_):
            ss_ = slice(t_ * 512, (t_ + 1) * 512)
            self.dma(xts[t_ % 2][:], V(xT_in.h.rearrange("(k p) s -> p k s", p=128)[:, :, ss_], xT_in.name))
            self.dma(tabs[t_ % 2][:], V(ROPE.h[:, :, :, ss_].rearrange("t w p s -> p t w s"), ROPE.name))
        load_t(0)
        for t in range(ntile):
            ss = slice(t * 512, (t + 1) * 512)
            xt = xts[t % 2]
            tab = tabs[t % 2]
            if t + 1 < ntile:
                load_t(t + 1)
            for ci, tb in enumerate(ROPED_TABLES):
                pA = self.next_ps()
                pB = self.next_ps()
                for k in range(NKC):
                    self.mm(pA[:], wv(k, (2 * ci) * 128, (2 * ci + 1) * 128), xt[:, k, :], start=(k == 0), stop=(k == NKC - 1))
                for k in range(NKC):
                    self.mm(pB[:], wv(k, (2 * ci + 1) * 128, (2 * ci + 2) * 128), xt[:, k, :], start=(k == 0), stop=(k == NKC - 1))
                a1 = self.nx(t1)
                a2 = self.nx(t2)
                o = self.nx(ob)
                self.tt(a1[:], pA[:], tab[:, tb, 0, :], ALU.mult)
                self.tt(a2[:], pB[:], tab[:, tb, 1, :], ALU.mult)
                self.tt(o[:], a1[:], a2[:], ALU.add)
                self.dma(V(FMS.h[ci * 128:(ci + 1) * 128, ss], FMS.name), o[:])
            nr = len(ROPED_TABLES)
            for j in range(7):
                wc = 2 * nr + j
                pA = self.next_ps()
                for k in range(NKC):
                    self.mm(pA[:], wv(k, wc * 128, (wc + 1) * 128), xt[:, k, :], start=(k == 0), stop=(k == NKC - 1))
                o = self.nx(ob)
                self.copy(o[:], pA[:], eng="act")
                self.dma(V(FMS.h[(nr + j) * 128:(nr + j + 1) * 128, ss], FMS.name), o[:])
            for q in range(4):
                t0 = t * 512 + q * 128
                pA = self.next_ps()
                pB = self.next_ps()
                for k in range(NKC):
                    self.mm(pA[:], xt[:, k, q * 128:(q + 1) * 128], wv(k, TM0, TM0 + 512), start=(k == 0), stop=(k == NKC - 1))
                for k in range(NKC):
                    self.mm(pB[:, 0:472], xt[:, k, q * 128:(q + 1) * 128], wv(k, TM0 + 512, TM0 + 984), start=(k == 0), stop=(k == NKC - 1))
                b = self.nx(tmb)
                f = self.nx(tmf)
                self.copy(b[:, 0:512], pA[:], eng="act")
                self.copy(b[:, 512:960], pB[:, 0:448])
                self.copy(f[:], pB[:, 448:472])
                self.dma(TMB[t0:t0 + 128, :], b[:])
                self.dma(TMF[t0:t0 + 128, :], f[:])

    def store_yT(self, y, YT, br, n, yTs_key):
        pb = self.next_psb()
        self.tr(pb[:, 0:128], y[:, 0:128], self.ident[:])
        self.tr(pb[:, 128:256], y[:, 128:256], self.ident[:])
        yTs = self.nx(yTs_key)
        self.copy(yTs[:], pb[:, 0:256])
        self.dma(V(YT.h[br * 256:(br + 1) * 256, n * 128:(n + 1) * 128].rearrange("(c p) t -> p c t", p=128), YT.name),
                 V(yTs.h[:, :].rearrange("p (c t) -> p c t", c=2), yTs.name))

    def phase_ret(self, FMS, TMB, YT):
        S = self.S
        NT = S // 128
        self.phase_begin()
        rq = self.sb("rq", [128, S], BF16)
        rk = self.sb("rk", [128, S], BF16)
        self.dma(rq[:], V(FMS.h[0:128, :], FMS.name))
        self.dma(rk[:], V(FMS.h[128:256, :], FMS.name))
        Sbd = self.sb("Sbd", [128, 256], F32)
        Sbd_bf = self.sb("Sbd_bf", [128, 256], BF16)
        self.memset(Sbd[:], 0.0)
        self.memset(Sbd_bf[:], 0.0)
        vt_k = self.rot("vt", 2, [128, 512], BF16)
        qxi_k = self.rot("qxi", 2, [128, 128], BF16)
        qm_k = self.rot("qm", 2, [128, 4, 128], BF16)
        kz_k = self.rot("kz", 2, [128, 128], BF16)
        PT_k = self.rot("PT", 2, [128, 4, 128], BF16)
        cross_k = self.rot("cross", 2, [128, 256], F32)
        o_k = self.rot("o", 2, [128, 256], F32)
        tmp_k = self.rot("tmp", 2, [128, 256], F32)
        osq_k = self.rot("osq", 2, [128, 256], F32)
        sg_k = self.rot("sg", 2, [128, 256], F32)
        st_k = self.rot("st", 2, [128, 16], F32)
        y_k = self.rot("y", 2, [128, 256], BF16)
        yTs_k = self.rot("yTs", 2, [128, 256], BF16)
        hm = V(self.meta.h[:, 13:17].unsqueeze(2).to_broadcast([128, 4, 128]), self.meta.name)
        for n in range(NT):
            sl = slice(n * 128, (n + 1) * 128)
            vt = self.nx(vt_k)
            self.dma(vt[:], TMB[n * 128:(n + 1) * 128, 0:512])
            qxi = self.nx(qxi_k)
            self.tt(qxi[:], rq[:, sl], self.xi[:], ALU.mult)
            qm = self.nx(qm_k)
            self.tt(qm[:], V(rq.h[:, sl].unsqueeze(1).to_broadcast([128, 4, 128]), rq.name), hm, ALU.mult, eng="pool")
            pb = self.next_psb()
            self.tr(pb[:, 0:128], rk[:, sl], self.ident[:])
            kz = self.nx(kz_k)
            self.tt(kz[:], pb[:, 0:128], self.zeta[:], ALU.mult)
            ps1 = self.next_ps()
            self.mm(ps1[:], rk[:, sl], V(qm.h[:, :, :].rearrange("p h i -> p (h i)"), qm.name))
            PT = self.nx(PT_k)
            self.tt(PT[:], V(ps1.h[:, :].rearrange("p (h i) -> p h i", h=4), ps1.name), self.decayT4[:], ALU.mult)
            ps2 = self.next_ps()
            self.mm(ps2[:, 0:256], qxi[:], Sbd_bf[:])
            cross = self.nx(cross_k)
            self.copy(cross[:], ps2[:, 0:256], eng="act")
            ps3 = self.next_ps()
            for h in range(4):
                self.mm(ps3[:, 64 * h:64 * h + 64], PT[:, h, :], vt[:, 64 * h:64 * h + 64])
            o = self.nx(o_k)
            self.tt(o[:], ps3[:, 0:256], cross[:], ALU.add)
            ps4 = self.next_ps()
            self.mm(ps4[:, 0:256], kz[:], vt[:, 0:256])
            tmp = self.nx(tmp_k)
            self.tt(tmp[:], ps4[:, 0:256], self.bdm[:], ALU.mult)
            self.stt(Sbd[:], Sbd[:], self.cdecay[:, 0:1], tmp[:], ALU.mult, ALU.add)
            self.copy(Sbd_bf[:], Sbd[:], eng="act")
            st = self.nx(st_k)
            o3 = V(o.h[:, :].rearrange("p (h e) -> p h e", h=4), o.name)
            self.red(st[:, 0:4], o3, ALU.add)
            osq = self.nx(osq_k)
            self.tt(osq[:], o[:], o[:], ALU.mult, eng="pool")
            self.red(st[:, 4:8], V(osq.h[:, :].rearrange("p (h e) -> p h e", h=4), osq.name), ALU.add)
            self.ts(st[:, 8:12], st[:, 0:4], 1.0 / 64, None, ALU.mult)
            self.tt(st[:, 12:16], st[:, 8:12], st[:, 8:12], ALU.mult)
            self.stt(st[:, 4:8], st[:, 4:8], 1.0 / 64, st[:, 12:16], ALU.mult, ALU.subtract)
            self.ts(st[:, 4:8], st[:, 4:8], 0.0, LN_EPS, ALU.max, ALU.add)
            self.act(st[:, 4:8], st[:, 4:8], AF.Sqrt)
            self.recip(st[:, 4:8], st[:, 4:8])
            self.tt(o3, o3, V(st.h[:, 8:12].unsqueeze(2).to_broadcast([128, 4, 64]), st.name), ALU.subtract)
            self.tt(o3, o3, V(st.h[:, 4:8].unsqueeze(2).to_broadcast([128, 4, 64]), st.name), ALU.mult)
            sg = self.nx(sg_k)
            self.act(sg[:], vt[:, 256:512], AF.Silu)
            y = self.nx(y_k)
            self.tt(y[:], o[:], sg[:], ALU.mult)
            self.store_yT(y, YT, 0, n, yTs_k)

    def phase_ssd(self, FMS, TMB, TMF, conv_w, conv_b, dt_bias, a_log, d_skip, norm_g, YT):
        S = self.S
        NT = S // 128
        self.phase_begin()
        cw = self.sb("cw", [128, 6, 4], F32)
        for k_ in range(4):
            self.dma_s(cw[:, :, k_], V(conv_w.h[k_].rearrange("(c p) -> p c", p=128), conv_w.name))
        cb = self.sb("cb", [128, 6], F32)
        self.dma_s(cb[:], V(conv_b.h.rearrange("(c p) -> p c", p=128), conv_b.name))
        dtb = self.sb("dtb", [128, 4], F32)
        self.dma(dtb[:], V(dt_bias.h.partition_broadcast(128), dt_bias.name))
        a_bc = self.sb("a_bc", [128, 4], F32)
        self.dma(a_bc[:], V(a_log.h.partition_broadcast(128), a_log.name))
        self.act(a_bc[:], a_bc[:], AF.Exp)
        self.ts(a_bc[:], a_bc[:], -1.0, None, ALU.mult)
        Dbc = self.sb("Dbc", [128, 4], F32)
        self.dma(Dbc[:], V(d_skip.h.partition_broadcast(128), d_skip.name))
        ng_bc = self.sb("ng_bc", [128, 256], F32)
        self.dma(ng_bc[:], V(norm_g.h.partition_broadcast(128), norm_g.name))
        xbcs = self.sb("xbcs", [128, 6, S], BF16)
        raw_k = self.rot("raw", 2, [128, 6, 515], BF16)
        acc_k = self.rot("acc", 2, [128, 512], F32)
        for t in range(S // 512):
            raw = self.nx(raw_k)
            if t == 0:
                self.memset(raw[:, :, 0:3], 0.0)
                self.dma(raw[:, :, 3:515], self.fm_rows(FMS, 14, 6, 0, 512))
            else:
                self.dma(raw[:, :, 0:515], self.fm_rows(FMS, 14, 6, t * 512 - 3, (t + 1) * 512))
            for c in range(6):
                acc = self.nx(acc_k)
                self.ts(acc[:], raw[:, c, 3:515], cw[:, c, 3:4], None, ALU.mult)
                for k in (2, 1, 0):
                    self.stt(acc[:], raw[:, c, k:k + 512], cw[:, c, k:k + 1], acc[:], ALU.mult, ALU.add)
                self.act(xbcs[:, c, t * 512:(t + 1) * 512], acc[:], AF.Silu, bias=cb[:, c:c + 1])
        prev = self.sb("prev", [128, 256], F32)
        prev_bf = self.sb("prev_bf", [128, 256], BF16)
        self.memset(prev[:], 0.0)
        self.memset(prev_bf[:], 0.0)
        xsB_k = self.rot("xsB", 2, [128, 512], BF16)
        tmf_k = self.rot("tmf", 2, [128, 24], F32)
        zt_k = self.rot("zt", 2, [128, 256], BF16)
        st_k = self.rot("st", 2, [128, 32], F32)
        adtb_k = self.rot("adtb", 2, [128, 4, 128], F32)
        seg_k = self.rot("seg", 2, [128, 4, 128], F32)
        MT_k = self.rot("MT", 2, [128, 4, 128], BF16)
        X_k = self.rot("X", 2, [128, 256], BF16)
        Xd_k = self.rot("Xd", 2, [128, 256], BF16)
        yd_k = self.rot("yd", 2, [128, 256], F32)
        y_k = self.rot("y", 2, [128, 256], F32)
        t2_k = self.rot("t2", 2, [128, 256], F32)
        sz_k = self.rot("sz", 2, [128, 256], F32)
        yb_k = self.rot("yb", 2, [128, 256], BF16)
        yTs_k = self.rot("yTs", 2, [128, 256], BF16)
        Ubc = V(self.U.h[:, :].unsqueeze(1).to_broadcast([128, 4, 128]), self.U.name)

        def h4(t_):
            return V(t_.h[:, 0:256].rearrange("p (h e) -> p h e", h=4), t_.name)

        def bc4(v_):
            return V(v_.ap.unsqueeze(2).to_broadcast([128, 4, 64]), v_.b)

        for n in range(NT):
            sl = slice(n * 128, (n + 1) * 128)
            pb = self.next_psb()
            for c in range(4):
                self.tr(pb[:, c * 128:(c + 1) * 128], xbcs[:, c, sl], self.ident[:])
            xsB = self.nx(xsB_k)
            self.copy(xsB[:], pb[:, 0:512])
            tmf = self.nx(tmf_k)
            self.dma(tmf[:], TMF[n * 128:(n + 1) * 128, :])
            zt = self.nx(zt_k)
            self.dma(zt[:], TMB[n * 128:(n + 1) * 128, 704:960])
            st = self.nx(st_k)
            self.tt(st[:, 0:4], tmf[:, 20:24], dtb[:], ALU.add)
            self.act(st[:, 0:4], st[:, 0:4], AF.Exp)
            self.act(st[:, 0:4], st[:, 0:4], AF.Ln, bias=1.0)
            self.tt(st[:, 4:8], st[:, 0:4], a_bc[:], ALU.mult)
            adtb = self.nx(adtb_k)
            self.copy(adtb[:], V(st.h[:, 4:8].unsqueeze(2).to_broadcast([128, 4, 128]), st.name))
            psA = self.next_ps()
            self.mm(psA[:, 0:4], self.U[:], st[:, 4:8])
            self.copy(st[:, 8:12], psA[:, 0:4], eng="act")
            psB = self.next_ps()
            for h in range(4):
                self.mm(psB[:, h * 128:(h + 1) * 128], adtb[:, h, :], self.U[:])
            seg = self.nx(seg_k)
            for h in range(4):
                self.ts(seg[:, h, :], psB[:, h * 128:(h + 1) * 128], st[:, 8 + h:9 + h], 0.0, ALU.subtract, ALU.min)
            self.act(seg[:], seg[:], AF.Exp)
            self.tt(seg[:], seg[:], Ubc, ALU.mult, eng="pool")
            alast = V(psB.h[:, 127:512:128], psB.name)
            self.tt(st[:, 12:16], alast, st[:, 8:12], ALU.subtract)
            self.act(st[:, 12:16], st[:, 12:16], AF.Exp)
            self.act(st[:, 16:20], alast, AF.Exp)
            self.act(st[:, 20:24], st[:, 8:12], AF.Exp)
            psG = self.next_ps()
            for g in range(2):
                self.mm(psG[:, g * 128:(g + 1) * 128], xbcs[:, 2 + g, sl], xbcs[:, 4 + g, sl])
            MT = self.nx(MT_k)
            for g in range(2):
                self.tt(MT[:, 2 * g:2 * g + 2, :], seg[:, 2 * g:2 * g + 2, :],
                        V(psG.h[:, g * 128:(g + 1) * 128].unsqueeze(1).to_broadcast([128, 2, 128]), psG.name), ALU.mult)
            X = self.nx(X_k)
            self.tt(h4(X), h4(xsB), bc4(st[:, 0:4]), ALU.mult)
            psY = self.next_ps()
            for h in range(4):
                self.mm(psY[:, 64 * h:64 * h + 64], MT[:, h, :], X[:, 64 * h:64 * h + 64])
            psO = self.next_ps()
            for g in range(2):
                self.mm(psO[:, 128 * g:128 * g + 128], xbcs[:, 4 + g, sl], prev_bf[:, 128 * g:128 * g + 128])
            yd = self.nx(yd_k)
            self.copy(yd[:], psY[:, 0:256], eng="act")
            y = self.nx(y_k)
            self.tt(h4(y), h4(psO), bc4(st[:, 20:24]), ALU.mult)
            self.tt(y[:], y[:], yd[:], ALU.add)
            t2 = self.nx(t2_k)
            self.tt(h4(t2), h4(xsB), bc4(Dbc[:, 0:4]), ALU.mult, eng="pool")
            self.tt(y[:], y[:], t2[:], ALU.add)
            Xd = self.nx(Xd_k)
            self.tt(h4(Xd), h4(X), bc4(st[:, 12:16]), ALU.mult, eng="pool")
            psS = self.next_ps()
            for g in range(2):
                self.mm(psS[:, 128 * g:128 * g + 128], xsB[:, 256 + 128 * g:256 + 128 * g + 128], Xd[:, 128 * g:128 * g + 128])
            self.tt(h4(prev), h4(prev), bc4(st[:, 16:20]), ALU.mult)
            self.tt(prev[:], prev[:], psS[:, 0:256], ALU.add)
            self.copy(prev_bf[:], prev[:], eng="act")
            sz = self.nx(sz_k)
            self.act(sz[:], zt[:], AF.Silu)
            self.tt(y[:], y[:], sz[:], ALU.mult)
            self.tt(t2[:], y[:], y[:], ALU.mult, eng="pool")
            self.red(st[:, 24:26], V(t2.h[:, :].rearrange("p (g e) -> p g e", g=2), t2.name), ALU.add)
            self.ts(st[:, 24:26], st[:, 24:26], 1.0 / 128, LN_EPS, ALU.mult, ALU.add)
            self.act(st[:, 24:26], st[:, 24:26], AF.Sqrt)
            self.recip(st[:, 24:26], st[:, 24:26])
            y2 = V(y.h[:, :].rearrange("p (g e) -> p g e", g=2), y.name)
            self.tt(y2, y2, V(st.h[:, 24:26].unsqueeze(2).to_broadcast([128, 2, 128]), st.name), ALU.mult)
            yb = self.nx(yb_k)
            self.tt(yb[:], y[:], ng_bc[:], ALU.mult)
            self.store_yT(yb, YT, 3, n, yTs_k)

    def softmax_pv(self, Ssb, nk, Vt, kt0, out, kk, clamp=None, premax=None):
        st = self.nx(kk["st"])
        self.red(st[:, 0:1], (Ssb if premax is None else premax), ALU.max)
        if clamp is not None:
            self.ts(st[:, 0:1], st[:, 0:1], clamp, None, ALU.max)
        self.ts(st[:, 1:2], st[:, 0:1], -1.0, None, ALU.mult)
        P = self.nx(kk["P"])
        self.act(P[:, 0:nk], Ssb, AF.Exp, bias=st[:, 1:2], accum=st[:, 2:3])
        self.ts(st[:, 3:4], st[:, 2:3], 1e-30, None, ALU.max)
        self.recip(st[:, 4:5], st[:, 3:4])
        po = self.next_ps()
        nkt = nk // 128
        for g0 in range(0, nkt, 8):
            gn = min(8, nkt - g0)
            pb = self.next_psb()
            for j in range(gn):
                self.tr(pb[:, j * 128:(j + 1) * 128], P[:, (g0 + j) * 128:(g0 + j + 1) * 128], self.ident[:])
            PT = self.nx(kk["PT"])
            self.pt_cnt += 1
            self.copy(PT[:, 0:gn * 128], pb[:, 0:gn * 128], eng=("act" if self.pt_cnt % 3 else "dve"))
            for j in range(gn):
                self.mm(po[:, 0:64], PT[:, j * 128:(j + 1) * 128], Vt[:, kt0 + g0 + j, :],
                        start=(g0 + j == 0), stop=(g0 + j == nkt - 1))
        self.ts(out, po[:, 0:64], st[:, 4:5], None, ALU.mult)

    def attn_keys(self, pfx):
        S = self.S
        return dict(st=self.rot(pfx + "sst", 2, [128, 8], F32), P=self.rot(pfx + "P", 1, [128, S], BF16),
                    PT=self.rot(pfx + "PTa", 2, [128, 1024], BF16))

    def dsa_setup(self, FMS, TMB, TMF, YT):
        S = self.S
        NT = S // 128
        c = dict(FMS=FMS, YT=YT)
        c["dk"] = self.sb("dk", [128, S], BF16)
        self.dma(c["dk"][:], V(FMS.h[4 * 128:5 * 128, :], FMS.name))
        c["ikr"] = self.sb("ikr", [128, S], BF16)
        self.dma(c["ikr"][:], V(FMS.h[7 * 128:8 * 128, :], FMS.name))
        c["qm"] = self.rot("dqm", 2, [128, 8, 128], BF16)
        c["Vt"] = self.sb("Vt", [128, NT, 64], BF16)
        self.dma(c["Vt"][:], V(TMB.h[:, 512:576].rearrange("(n p) c -> p n c", p=128), TMB.name))
        iw = self.sb("iw", [128, NT, 8], F32)
        self.dma(iw[:], V(TMF.h[:, 0:8].rearrange("(n p) c -> p n c", p=128), TMF.name))
        c["absw"] = self.sb("absw", [128, NT, 8], F32)
        self.act(c["absw"][:], iw[:], AF.Abs, scale=1.0 / 16)
        c["sgn"] = self.sb("sgn", [128, NT, 8], F32)
        self.ts(c["sgn"][:], iw[:], 0.0, 2.0, ALU.is_ge, ALU.mult)
        self.ts(c["sgn"][:], c["sgn"][:], -1.0, None, ALU.add)
        c["I"] = self.rot("I", 1, [128, S], F32)
        c["Ssb"] = self.rot("dSsb", 1, [128, S], F32)
        c["Mb"] = self.rot("dMb", 2, [128, S], BF16)
        c["cm"] = self.rot("dcm", 2, [128, 8], F32)
        c["kk"] = self.attn_keys("d")
        c["q"] = self.rot("dqi", 2, [128, 4, 128], BF16)
        c["tmp"] = self.rot("tmpr", 2, [128, 512], F32)
        c["st"] = self.rot("dst", 2, [128, 16], F32)
        c["Rk"] = self.rot("Rk", 2, [128, 20], F32)
        c["nm"] = self.rot("dnm", 2, [128, 2], F32)
        c["c2"] = self.rot("dc2", 2, [128, 2], F32)
        c["o"] = self.rot("do", 2, [128, 256], F32)
        c["y"] = self.rot("dy", 2, [128, 256], BF16)
        c["yTs"] = self.rot("dyTs", 2, [128, 256], BF16)
        return c

    def dsa_tile(self, c, i):
        FMS = c["FMS"]
        I = self.nx(c["I"])
        Ssb = self.nx(c["Ssb"])
        nk = 128 * (i + 1)
        nkc = (nk + 511) // 512
        q = self.nx(c["q"])
        self.dma(q[:, 0:2, :], self.fm_rows(FMS, 2, 2, i * 128, (i + 1) * 128))
        self.dma(q[:, 2:4, :], self.fm_rows(FMS, 5, 2, i * 128, (i + 1) * 128))
        qm = self.nx(c["qm"])
        hm = V(self.meta.h[:, 13:17].unsqueeze(2).to_broadcast([128, 4, 128]), self.meta.name)
        for cc in range(2):
            self.tt(qm[:, 4 * cc:4 * cc + 4, :], V(q.h[:, 2 + cc, :].unsqueeze(1).to_broadcast([128, 4, 128]), q.name), hm,
                    ALU.mult, eng="pool")
        for kc in range(nkc):
            c0 = kc * 512
            cols = min(512, nk - c0)
            for h in range(8):
                ps = self.next_ps()
                self.mm(ps[:, 0:cols], qm[:, h, :], c["ikr"][:, c0:c0 + cols])
                tmp = self.nx(c["tmp"])
                self.act(tmp[:, 0:cols], ps[:, 0:cols], AF.Relu, scale=c["absw"][:, i, h:h + 1])
                if h == 0:
                    self.ts(I[:, c0:c0 + cols], tmp[:, 0:cols], c["sgn"][:, i, 0:1], None, ALU.mult)
                else:
                    self.stt(I[:, c0:c0 + cols], tmp[:, 0:cols], c["sgn"][:, i, h:h + 1], I[:, c0:c0 + cols], ALU.mult, ALU.add)
        if nk > self.n_keep:
            st = self.nx(c["st"])
            junk = self.nx(c["kk"]["P"])
            self.redabs(st[:, 0:1], I[:, 0:nk])
            self.ts(st[:, 0:1], st[:, 0:1], 1e-20, None, ALU.max)
            self.tt(I[:, nk - 128:nk], I[:, nk - 128:nk], self.cneg30[:], ALU.add)
            Rk = self.nx(c["Rk"])
            self.ts(Rk[:], self.rkc[:], st[:, 0:1], None, ALU.mult)
            self.ts(st[:, 1:2], st[:, 0:1], -1.0, None, ALU.mult)
            n1 = (nk // 2 + 127) // 128 * 128
            n2 = nk - n1
            thr_c = self.n_keep - 0.5 - n2 / 2.0
            for k in range(NBIS):
                nm = self.nx(c["nm"])
                c2 = self.nx(c["c2"])
                self.tt(nm[:, 0:1], st[:, 1:2], Rk[:, k:k + 1], ALU.add)
                self.act(Ssb[:, n1:nk], I[:, n1:nk], AF.Sign, bias=nm[:, 0:1], scale=-1.0, accum=c2[:, 0:1])
                self.ts(junk[:, 0:n1], I[:, 0:n1], nm[:, 0:1], None, ALU.is_ge, ALU.add, accum=st[:, 3:4])
                self.stt(st[:, 4:5], c2[:, 0:1], -0.5, st[:, 3:4], ALU.mult, ALU.add)
                self.ts(st[:, 4:5], st[:, 4:5], thr_c, None, ALU.is_ge)
                self.stt(st[:, 1:2], st[:, 4:5], Rk[:, k:k + 1], st[:, 1:2], ALU.mult, ALU.add)
            Mb = self.nx(c["Mb"])
            self.ts(Mb[:, 0:nk], I[:, 0:nk], st[:, 1:2], 8000.0, ALU.is_ge, ALU.mult)
        else:
            Mb = self.nx(c["Mb"])
            self.memset(Mb[:, 0:nk], 8000.0)
            self.tt(Mb[:, nk - 128:nk], Mb[:, nk - 128:nk], self.cnegb[:], ALU.add)
        o = self.nx(c["o"])
        for h in range(4):
            base = 64 * (h % 2)
            cq = h // 2
            cm = self.nx(c["cm"])
            for kc in range(nkc):
                c0 = kc * 512
                cols = min(512, nk - c0)
                ps = self.next_ps()
                self.mm(ps[:, 0:cols], q[base:base + 64, cq, :], c["dk"][base:base + 64, c0:c0 + cols], start=True, stop=False)
                self.mm(ps[:, 0:cols], self.ident[:], Mb[:, c0:c0 + cols], start=False, stop=True)
                self.ts(Ssb[:, c0:c0 + cols], ps[:, 0:cols], 0.125, None, ALU.mult, ALU.max, accum=cm[:, kc:kc + 1])
            self.softmax_pv(Ssb[:, 0:nk], nk, c["Vt"], 0, o[:, 64 * h:64 * h + 64], c["kk"], premax=cm[:, 0:nkc])
        y = self.nx(c["y"])
        self.copy(y[:], o[:], eng="act")
        self.store_yT(y, c["YT"], 1, i, c["yTs"])

    def nsa_setup(self, FMS, TMB, TMF, cmp_w1, cmp_w2, cmp_pos, YT):
        S = self.S
        NT = S // 128
        NB = self.NB
        NCP = self.NCP
        NC = (S - 32) // 16 + 1
        NCT = NCP // 128
        c = dict(FMS=FMS, YT=YT)
        c["ksT"] = self.sb("ksT", [128, S], BF16)
        self.dma(c["ksT"][:], V(FMS.h[11 * 128:12 * 128, :], FMS.name))
        c["kwT"] = self.sb("kwT", [128, S], BF16)
        self.dma(c["kwT"][:], V(FMS.h[12 * 128:13 * 128, :], FMS.name))
        c["Vs"] = self.sb("Vs", [128, NT, 64], BF16)
        self.dma(c["Vs"][:], V(TMB.h[:, 576:640].rearrange("(n p) c -> p n c", p=128), TMB.name))
        c["Vw"] = self.sb("Vw", [128, NT, 64], BF16)
        self.dma(c["Vw"][:], V(TMB.h[:, 640:704].rearrange("(n p) c -> p n c", p=128), TMB.name))
        c["ngt"] = self.sb("ngt", [128, NT, 12], F32)
        self.dma(c["ngt"][:], V(TMF.h[:, 8:20].rearrange("(n p) c -> p n c", p=128), TMF.name))
        kcmp = self.sb("kcmp", [128, NCP], BF16)
        vcmp = self.sb("vcmp", [128, NCT, 64], BF16)
        c["kcmp"] = kcmp
        c["vcmp"] = vcmp
        c["Ssb"] = self.sb("nSsb", [128, S], F32)
        c["Sw"] = self.sb("Sw", [128, 640], F32)
        c["kk"] = self.attn_keys("n")
        save = self.sb_cur
        srcT = self.sb("srcT", [128, S], BF16)
        w1 = self.sb("w1", [64, 32, 64], BF16)
        w2 = self.sb("w2", [64, 128], BF16)
        posT = self.sb("posT", [64, 32], F32)
        posb = self.sb("posb", [64, 32], BF16)
        cst = self.sb("cst", [64, 1], F32)
        u = self.sb("u", [64, NCP], F32)
        u2 = self.sb("u2", [64, NCP], F32)
        gl = self.sb("gl", [64, NCP], BF16)
        for i in range(2):
            self.dma(srcT[:], V(FMS.h[(10 + 3 * i) * 128:(11 + 3 * i) * 128, :], FMS.name))
            self.dma(w1[:], V(cmp_w1.h[i].rearrange("(l d) f -> d l f", d=64), cmp_w1.name), eng="pool")
            self.dma(w2[:, 0:64], V(cmp_w2.h[i], cmp_w2.name), eng="pool")
            self.dma(w2[:, 64:128], V(cmp_w2.h[i], cmp_w2.name), eng="pool")
            self.dma_s(posT[:], V(cmp_pos.h[i].rearrange("l d -> d l"), cmp_pos.name))
            self.copy(posb[:], posT[:])
            psc = self.next_ps()
            for l in range(32):
                self.mm(psc[0:64, 0:1], w1[:, l, :], posb[:, l:l + 1], start=(l == 0), stop=(l == 31))
            self.copy(cst[:], psc[0:64, 0:1])
            psh = self.next_ps()
            for l in range(32):
                self.mm(psh[0:64, 0:NC], w1[:, l, :], srcT[0:64, l:l + 16 * (NC - 1) + 1:16], start=(l == 0), stop=(l == 31))
            self.memset(u[:], 0.0)
            self.act(u[:, 0:NC], psh[0:64, 0:NC], AF.Identity, bias=cst[:, 0:1])
            self.tt(u2[:], u[:], u[:], ALU.mult)
            self.tt(u2[:], u2[:], u[:], ALU.mult)
            self.stt(u2[:], u2[:], 0.044715, u[:], ALU.mult, ALU.add)
            self.act(u2[:], u2[:], AF.Tanh, scale=0.7978845608028654)
            self.ts(u2[:], u2[:], 1.0, 0.5, ALU.add, ALU.mult)
            self.tt(gl[:], u2[:], u[:], ALU.mult)
            if i == 0:
                pso = self.next_ps()
                self.mm(pso[:, 0:NCP], w2[:, :], gl[:, :])
                self.copy(kcmp[:], pso[:, 0:NCP])
            else:
                for ct in range(NCT):
                    pso = self.next_ps()
                    self.mm(pso[:, 0:64], gl[:, ct * 128:(ct + 1) * 128], w2[:, 0:64])
                    self.copy(vcmp[:, ct, :], pso[:, 0:64])
        self.sb_cur = save
        self.P.barrier()
        c["q"] = self.rot("nqi", 2, [128, 2, 128], BF16)
        c["Mn"] = self.rot("nMn", 1, [128, S], BF16)
        for nm, shp, dt_ in (("vis", [128, NCP], F32), ("pns", [128, NCP], F32), ("pn", [128, NCP], F32), ("Sc", [128, NCP], F32),
                             ("Pc", [128, NCP], F32), ("pnb", [128, NCP], BF16), ("PTc", [128, NCP], BF16), ("pnT", [128, NCP], F32),
                             ("cst2", [128, 8], F32), ("am", [128, NB], F32), ("imp", [128, NB], F32), ("imp2", [128, NB], F32), ("m8", [128, 16], F32),
                             ("selm", [128, NB], F32), ("cm", [128, 8], F32), ("oc", [128, 256], F32), ("os", [128, 256], F32), ("ow", [128, 256], F32),
                             ("gs", [128, 12], F32), ("o", [128, 256], F32), ("y", [128, 256], BF16), ("yTs", [128, 256], BF16)):
            c[nm] = self.rot("n" + nm, (1 if nm in ("pnT", "Pc", "vis", "imp2") else 2), shp, dt_)
        return c

    def nsa_tile(self, c, i):
        NB = self.NB
        NCP = self.NCP
        NCT = NCP // 128
        FMS = c["FMS"]
        Ssb = c["Ssb"]
        Sw = c["Sw"]
        kcmp = c["kcmp"]
        vcmp = c["vcmp"]

        def h4(t_):
            return V(t_.h[:, 0:256].rearrange("p (h e) -> p h e", h=4), t_.name)

        nk = 128 * (i + 1)
        nkc = (nk + 511) // 512
        nq = self.nx(c["q"])
        self.dma(nq[:], self.fm_rows(FMS, 8, 2, i * 128, (i + 1) * 128))
        vis = self.nx(c["vis"])
        self.memset(vis[:], 0.0)
        self.aselect(vis[:], vis[:], [[-16, NCP]], ALU.is_ge, -1000.0, 128 * i - 31, 1)
        pns = self.nx(c["pns"])
        oc = self.nx(c["oc"])
        osl = self.nx(c["os"])
        ow = self.nx(c["ow"])
        for h in range(4):
            base = 64 * (h % 2)
            cq = h // 2
            ps = self.next_ps()
            self.mm(ps[:, 0:NCP], nq[base:base + 64, cq, :], kcmp[base:base + 64, :])
            Sc = self.nx(c["Sc"])
            self.stt(Sc[:], ps[:, 0:NCP], 0.125, vis[:], ALU.mult, ALU.add)
            st = self.nx(c["cst2"])
            self.red(st[:, 0:1], Sc[:], ALU.max)
            self.ts(st[:, 0:1], st[:, 0:1], -500.0, -1.0, ALU.max, ALU.mult)
            Pc = self.nx(c["Pc"])
            self.act(Pc[:], Sc[:], AF.Exp, bias=st[:, 0:1], accum=st[:, 1:2])
            self.ts(st[:, 2:3], st[:, 1:2], 1e-30, None, ALU.max)
            self.recip(st[:, 3:4], st[:, 2:3])
            pn = pns if h == 0 else self.nx(c["pn"])
            self.ts(pn[:], Pc[:], st[:, 3:4], None, ALU.mult)
            pnb = self.nx(c["pnb"])
            self.copy(pnb[:], pn[:], eng="act")
            if h > 0:
                self.tt(pns[:], pns[:], pn[:], ALU.add, eng="pool")
            pb = self.next_psb()
            for ct in range(NCT):
                self.tr(pb[:, ct * 128:(ct + 1) * 128], pnb[:, ct * 128:(ct + 1) * 128], self.ident[:])
            PTc = self.nx(c["PTc"])
            self.copy(PTc[:], pb[:, 0:NCP], eng="act")
            po = self.next_ps()
            for ct in range(NCT):
                self.mm(po[:, 0:64], PTc[:, ct * 128:(ct + 1) * 128], vcmp[:, ct, :], start=(ct == 0), stop=(ct == NCT - 1))
            self.copy(oc[:, 64 * h:64 * h + 64], po[:, 0:64], eng="act")
        selm = self.nx(c["selm"])
        if NB > 16:
            pf = self.next_ps()
            for ct in range(NCT):
                self.tr(pf[:, ct * 128:(ct + 1) * 128], pns[:, ct * 128:(ct + 1) * 128], self.ident_f[:])
            pnT = self.nx(c["pnT"])
            self.copy(pnT[:], pf[:, 0:NCP], eng="act")
            pi = self.next_ps()
            for ct in range(NCT):
                self.mm(pi[:, 0:NB], pnT[:, ct * 128:(ct + 1) * 128], self.ovl[:, ct, :], start=(ct == 0), stop=(ct == NCT - 1))
            am = self.nx(c["am"])
            self.memset(am[:], 0.0)
            for half in range(2):
                cur = 2 * i + half
                r0 = 64 * half
                v_ = am[r0:r0 + 64, :]
                self.aselect(v_, v_, [[-1, NB]], ALU.is_ge, -1e30, cur, 0)
                self.memset(am[r0:r0 + 64, 0:1], 1e30)
                self.memset(am[r0:r0 + 64, cur:cur + 1], 1e30)
                if cur >= 1:
                    self.memset(am[r0:r0 + 64, cur - 1:cur], 1e30)
            imp = self.nx(c["imp"])
            self.tt(imp[:], pi[:, 0:NB], am[:], ALU.add)
            m8 = self.nx(c["m8"])
            self.vmax(m8[:, 0:8], imp[:])
            imp2 = self.nx(c["imp2"])
            self.match_replace(imp2[:], m8[:, 0:8], imp[:], -3.0e38)
            self.vmax(m8[:, 8:16], imp2[:])
            self.ts(selm[:], imp[:], m8[:, 15:16], 8000.0, ALU.is_ge, ALU.mult)
        else:
            self.memset(selm[:], 8000.0)
        Mn = self.nx(c["Mn"])
        nbk = nk // 64
        self.act(V(Mn.h[:, 0:nk].rearrange("p (b e) -> p b e", e=64), Mn.name),
                 V(selm.h[:, 0:nbk].unsqueeze(2).to_broadcast([128, nbk, 64]), selm.name), AF.Copy)
        self.tt(Mn[:, nk - 128:nk], Mn[:, nk - 128:nk], self.cnegb[:], ALU.add)
        for h in range(4):
            base = 64 * (h % 2)
            cq = h // 2
            cm = self.nx(c["cm"])
            for kc in range(nkc):
                c0 = kc * 512
                cols = min(512, nk - c0)
                ps = self.next_ps()
                self.mm(ps[:, 0:cols], nq[base:base + 64, cq, :], c["ksT"][base:base + 64, c0:c0 + cols], start=True, stop=False)
                self.mm(ps[:, 0:cols], self.ident[:], Mn[:, c0:c0 + cols], start=False, stop=True)
                self.ts(Ssb[:, c0:c0 + cols], ps[:, 0:cols], 0.125, None, ALU.mult, ALU.max, accum=cm[:, kc:kc + 1])
            self.softmax_pv(Ssb[:, 0:nk], nk, c["Vs"], 0, osl[:, 64 * h:64 * h + 64], c["kk"], premax=cm[:, 0:nkc])
        k0 = max(0, i * 128 - 512)
        nkw = nk - k0
        boff = 640 - nkw
        for h in range(4):
            base = 64 * (h % 2)
            cq = h // 2
            cm = self.nx(c["cm"])
            nwc = 0
            for c0 in range(0, nkw, 512):
                cols = min(512, nkw - c0)
                ps = self.next_ps()
                self.mm(ps[:, 0:cols], nq[base:base + 64, cq, :], c["kwT"][base:base + 64, k0 + c0:k0 + c0 + cols], start=True, stop=False)
                self.mm(ps[:, 0:cols], self.ident[:], self.bandb[:, boff + c0:boff + c0 + cols], start=False, stop=True)
                self.ts(Sw[:, c0:c0 + cols], ps[:, 0:cols], 0.125, None, ALU.mult, ALU.max, accum=cm[:, nwc:nwc + 1])
                nwc += 1
            self.softmax_pv(Sw[:, 0:nkw], nkw, c["Vw"], k0 // 128, ow[:, 64 * h:64 * h + 64], c["kk"], premax=cm[:, 0:nwc])
        gs = self.nx(c["gs"])
        self.act(gs[:], c["ngt"][:, i, :], AF.Sigmoid)
        o = self.nx(c["o"])

        def gbc(j):
            return V(gs.h[:, j:12:3].unsqueeze(2).to_broadcast([128, 4, 64]), gs.name)
        self.tt(h4(o), h4(oc), gbc(0), ALU.mult)
        self.tt(h4(osl), h4(osl), gbc(1), ALU.mult)
        self.tt(o[:], o[:], osl[:], ALU.add)
        self.tt(h4(ow), h4(ow), gbc(2), ALU.mult)
        self.tt(o[:], o[:], ow[:], ALU.add)
        y = self.nx(c["y"])
        self.copy(y[:], o[:], eng="act")
        self.store_yT(y, c["YT"], 2, i, c["yTs"])

    def phase_dsa_nsa(self, FMS, TMB, TMF, cmp_w1, cmp_w2, cmp_pos, YT):
        S = self.S
        NT = S // 128
        self.phase_begin()
        self.cnegb = self.sb("cnegb", [128, 128], BF16)
        self.ts(self.cnegb[:], self.cneg2k[:], 8.0, None, ALU.mult)
        self.bandb = self.sb("bandb", [128, 640], BF16)
        self.ts(self.bandb[:], self.band[:], 8.0, None, ALU.mult)
        cd = self.dsa_setup(FMS, TMB, TMF, YT)
        cn = self.nsa_setup(FMS, TMB, TMF, cmp_w1, cmp_w2, cmp_pos, YT)
        P = self.P
        for i in range(NT):
            self.ps_set = (0, 3)
            self.psb_set = (0, 1)
            P.capture = []
            self.dsa_tile(cd, i)
            A = P.capture
            self.ps_set = (3, 2)
            self.psb_set = (1, 1)
            P.capture = []
            self.nsa_tile(cn, i)
            B = P.capture
            P.capture = None
            self.ps_set = (0, 5)
            self.psb_set = (0, 2)
            ia = ib = 0
            na, nb = len(A), len(B)
            while ia < na or ib < nb:
                if ib >= nb or (ia < na and ia * nb <= ib * na):
                    P.add(*A[ia][0], **A[ia][1])
                    ia += 1
                else:
                    P.add(*B[ib][0], **B[ib][1])
                    ib += 1

    def phase_merge(self, x_in, xT_in, w_in_l, w_branch, w_out, ln_g, ln_b, YT, x_out, xT_out):
        S = self.S
        self.phase_begin()
        wg = self.load_w("wg", lambda k: V(w_in_l.h[k * 128:(k + 1) * 128, 3128:7224], w_in_l.name), NKC, 4096)
        wb = self.sb("wb", [128, 4, 2, 1024], BF16)
        for n in range(4):
            for kk_ in range(2):
                self.dma(wb[:, n, kk_, :], V(w_branch.h[n, kk_ * 128:(kk_ + 1) * 128, :], w_branch.name), eng="pool")
        wo = self.load_w("wo", lambda k: V(w_out.h[k * 128:(k + 1) * 128, :], w_out.name), NKC, D)
        g_bc, b_bc, scr = self.ln_setup(ln_g, ln_b)
        xt = self.sb("xT", [128, NKC, 512], BF16)
        yt = self.sb("yT", [128, 8, 512], BF16)
        mT = self.sb("mT", [128, 8, 512], BF16)
        acc_k = self.rot("acc", 2, [128, 512], F32)
        sg_k = self.rot("sg", 2, [128, 512], F32)
        tmp_k = self.rot("tmp", 2, [128, 512], F32)
        xr = [self.sb("xr%d" % i, [128, D], F32) for i in range(2)]
        rqs = [self.sb("r%d" % i, [128, D], F32) for i in range(2)]
        ntile = S // 512

        def load_xt(t_):
            ss_ = slice(t_ * 512, (t_ + 1) * 512)
            self.dma(xt[:], V(xT_in.h.rearrange("(k p) s -> p k s", p=128)[:, :, ss_], xT_in.name))
            self.dma(yt[:], V(YT.h[:, ss_].rearrange("(c p) s -> p c s", p=128), YT.name))

        def load_xq(idx):
            self.dma(xr[idx % 2][:], x_in[idx * 128:(idx + 1) * 128, :])
        load_xt(0)
        load_xq(0)
        for t in range(ntile):
            ss = slice(t * 512, (t + 1) * 512)
            for dc in range(8):
                acc = self.nx(acc_k)
                for n in range(4):
                    pg = self.next_ps()
                    for k in range(NKC):
                        self.mm(pg[:], wg[:, k, n * 1024 + dc * 128:n * 1024 + (dc + 1) * 128], xt[:, k, :],
                                start=(k == 0), stop=(k == NKC - 1))
                    pp = self.next_ps()
                    for k2 in range(2):
                        self.mm(pp[:], wb[:, n, k2, dc * 128:(dc + 1) * 128], yt[:, 2 * n + k2, :], start=(k2 == 0), stop=(k2 == 1))
                    sg = self.nx(sg_k)
                    self.act(sg[:], pg[:], AF.Sigmoid)
                    if n == 0:
                        self.tt(acc[:], sg[:], pp[:], ALU.mult)
                    else:
                        tmp = self.nx(tmp_k)
                        self.tt(tmp[:], sg[:], pp[:], ALU.mult)
                        self.tt(acc[:], acc[:], tmp[:], ALU.add, eng="pool")
                self.copy(mT[:, dc, :], acc[:], eng="act")
            if t + 1 < ntile:
                load_xt(t + 1)
            for q in range(4):
                t0 = t * 512 + q * 128
                xq = xr[q % 2]
                rq = rqs[q % 2]
                if t * 4 + q + 1 < ntile * 4:
                    load_xq(t * 4 + q + 1)
                for half in range(2):
                    hs = slice(half * 512, (half + 1) * 512)
                    pd = self.next_ps()
                    for dc in range(8):
                        self.mm(pd[:], mT[:, dc, q * 128:(q + 1) * 128], wo[:, dc, hs], start=(dc == 0), stop=(dc == 7))
                    self.act(xq[:, hs], xq[:, hs], AF.Copy, scale=ALPHA)
                    self.stt(rq[:, hs], pd[:], 1.0, xq[:, hs], ALU.mult, ALU.add)
                self.finish_tile(rq, g_bc, b_bc, x_out, xT_out, t0, scr)

    def phase_xattn(self, x_in, xT_in, mem, wq_d, wkv_d, wo_d, ln_g, ln_b, x_out, xT_out):
        S = self.S
        self.phase_begin()
        wq = self.load_w("wq", lambda k: V(wq_d.h[k * 128:(k + 1) * 128, :], wq_d.name), NKC, D)
        wkv = self.load_w("wkv", lambda k: V(wkv_d.h[k * 128:(k + 1) * 128, :], wkv_d.name), NKC, 2 * D)
        wo = self.load_w("wo", lambda k: V(wo_d.h[k * 128:(k + 1) * 128, :], wo_d.name), NKC, D)
        g_bc, b_bc, scr = self.ln_setup(ln_g, ln_b)
        memT = self.sb("memT", [128, 8, 256], BF16)
        mr = self.sb("mr", [128, D], F32)
        mb = self.sb("mb", [128, D], BF16)
        for mt in range(2):
            self.dma(mr[:], mem[mt * 128:(mt + 1) * 128, :])
            self.copy(mb[:], mr[:], eng="act")
            pb = self.next_psb()
            for k in range(8):
                self.tr(pb[:, k * 128:(k + 1) * 128], mb[:, k * 128:(k + 1) * 128], self.ident[:])
            self.copy(memT[:, :, mt * 128:(mt + 1) * 128], V(pb.h[:, :].rearrange("p (k t) -> p k t", k=8), pb.name))
        KT = self.sb("KT", [128, 8, 256], BF16)
        for c in range(8):
            ps = self.next_ps()
            for k in range(NKC):
                self.mm(ps[:, 0:256], wkv[:, k, c * 128:(c + 1) * 128], memT[:, k, :], start=(k == 0), stop=(k == NKC - 1))
            self.copy(KT[:, c, :], ps[:, 0:256], eng=("act" if c % 2 else "dve"))
        Vm = self.sb("Vm", [128, 2, D], BF16)
        for mt in range(2):
            for half in range(2):
                ps = self.next_ps()
                for k in range(NKC):
                    self.mm(ps[:], memT[:, k, mt * 128:(mt + 1) * 128], wkv[:, k, D + half * 512:D + (half + 1) * 512],
                            start=(k == 0), stop=(k == NKC - 1))
                self.copy(Vm[:, mt, half * 512:(half + 1) * 512], ps[:], eng=("act" if half else "dve"))
        xt = self.sb("xT", [128, NKC, 512], BF16)
        qT = self.sb("qT", [128, 8, 512], BF16)
        Pf_k = self.rot("Pf", 2, [128, 4, 256], F32)
        Pb_k = self.rot("Pb", 2, [128, 4, 256], BF16)
        PT_k = self.rot("PTx", 2, [128, 8, 128], BF16)
        oT_k = self.rot("oT", 2, [128, 8, 128], BF16)
        st_k = self.rot("xst", 2, [128, 16], F32)
        xr = [self.sb("xr%d" % i, [128, D], F32) for i in range(2)]
        rqs = [self.sb("r%d" % i, [128, D], F32) for i in range(2)]
        SC = 1.0 / 16
        ntile = S // 512

        def load_xt(t_):
            ss_ = slice(t_ * 512, (t_ + 1) * 512)
            self.dma(xt[:], V(xT_in.h.rearrange("(k p) s -> p k s", p=128)[:, :, ss_], xT_in.name))

        def load_xq(idx):
            self.dma(xr[idx % 2][:], x_in[idx * 128:(idx + 1) * 128, :])
        load_xt(0)
        load_xq(0)
        for t in range(ntile):
            ss = slice(t * 512, (t + 1) * 512)
            for c in range(8):
                ps = self.next_ps()
                for k in range(NKC):
                    self.mm(ps[:], wq[:, k, c * 128:(c + 1) * 128], xt[:, k, :], start=(k == 0), stop=(k == NKC - 1))
                self.copy(qT[:, c, :], ps[:], eng=("act" if c % 2 else "dve"))
            if t + 1 < ntile:
                load_xt(t + 1)
            for q in range(4):
                t0 = t * 512 + q * 128
                tq = slice(q * 128, (q + 1) * 128)
                if t * 4 + q + 1 < ntile * 4:
                    load_xq(t * 4 + q + 1)
                pss = [self.next_ps(), self.next_ps()]
                st = self.nx(st_k)
                Pf = self.nx(Pf_k)
                for h in range(4):
                    pv = pss[h // 2][:, (h % 2) * 256:(h % 2) * 256 + 256]
                    for cc in range(2):
                        self.mm(pv, qT[:, 2 * h + cc, tq], KT[:, 2 * h + cc, :], start=(cc == 0), stop=(cc == 1))
                    self.red(st[:, h:h + 1], pv, ALU.max)
                    self.ts(st[:, 4 + h:5 + h], st[:, h:h + 1], -SC, None, ALU.mult)
                    self.act(Pf[:, h, :], pv, AF.Exp, bias=st[:, 4 + h:5 + h], scale=SC, accum=st[:, 8 + h:9 + h])
                self.recip(st[:, 12:16], st[:, 8:12])
                Pb = self.nx(Pb_k)
                self.tt(Pb[:], Pf[:], V(st.h[:, 12:16].unsqueeze(2).to_broadcast([128, 4, 256]), st.name), ALU.mult)
                pb = self.next_psb()
                for h in range(4):
                    for mc in range(2):
                        j = 2 * h + mc
                        self.tr(pb[:, j * 128:(j + 1) * 128], Pb[:, h, mc * 128:(mc + 1) * 128], self.ident[:])
                PT = self.nx(PT_k)
                self.copy(PT[:], V(pb.h[:, :].rearrange("p (j t) -> p j t", j=8), pb.name))
                oT = self.nx(oT_k)
                pso = [self.next_ps(), self.next_ps()]
                for h in range(4):
                    for dc in range(2):
                        j = 2 * h + dc
                        pv = pso[j // 4][:, (j % 4) * 128:(j % 4) * 128 + 128]
                        for mc in range(2):
                            self.mm(pv, Vm[:, mc, h * 256 + dc * 128:h * 256 + (dc + 1) * 128], PT[:, 2 * h + mc, :],
                                    start=(mc == 0), stop=(mc == 1))
                for j4 in range(2):
                    self.copy(oT[:, 4 * j4:4 * j4 + 4, :], V(pso[j4].h[:, :].rearrange("p (j t) -> p j t", j=4), pso[j4].name),
                              eng=("act" if j4 else "dve"))
                xq = xr[q % 2]
                rq = rqs[q % 2]
                for half in range(2):
                    hs = slice(half * 512, (half + 1) * 512)
                    pd = self.next_ps()
                    for c in range(8):
                        self.mm(pd[:], oT[:, c, :], wo[:, c, hs], start=(c == 0), stop=(c == 7))
                    self.act(xq[:, hs], xq[:, hs], AF.Copy, scale=ALPHA)
                    self.stt(rq[:, hs], pd[:], 1.0, xq[:, hs], ALU.mult, ALU.add)
                self.finish_tile(rq, g_bc, b_bc, x_out, xT_out, t0, scr)


OFF = dict(r_q=0, r_k=128, r_v=256, r_g=512, d_q=768, d_k=1024, d_v=1088, i_q=1152, i_k=1408, i_w=1440,
           n_q=1448, n_kc=1704, n_vc=1768, n_ks=1832, n_vs=1896, n_kw=1960, n_vw=2024, n_g=2088,
           s_z=2100, s_xbc=2356, s_dt=3124, br_g=3128)


def _partner(i, headdim, rot):
    half = rot // 2
    j = i % headdim
    b = i - j
    if j < half:
        return b + j + half
    if j < rot:
        return b + j - half
    return i


def build_colidx():
    cols = []

    def roped(name, width, headdim, rot, lo=0, rep=1):
        loc = []
        for r in range(rep):
            loc += list(range(lo, lo + width))
        assert len(loc) == 128
        a = [OFF[name] + i for i in loc]
        b = [OFF[name] + _partner(i, headdim, rot) for i in loc]
        cols.extend(a)
        cols.extend(b)

    roped("r_q", 128, 32, 32)
    roped("r_k", 128, 32, 32)
    roped("d_q", 128, 64, 16, 0)
    roped("d_q", 128, 64, 16, 128)
    roped("d_k", 64, 64, 16, 0, 2)
    roped("i_q", 128, 32, 8, 0)
    roped("i_q", 128, 32, 8, 128)
    roped("i_k", 32, 32, 8, 0, 4)
    roped("n_q", 128, 64, 16, 0)
    roped("n_q", 128, 64, 16, 128)
    roped("n_kc", 64, 64, 16, 0, 2)
    roped("n_ks", 64, 64, 16, 0, 2)
    roped("n_kw", 64, 64, 16, 0, 2)
    cols.extend([OFF["n_vc"] + i for i in range(64)] * 2)
    cols.extend([OFF["s_xbc"] + i for i in range(768)])
    for name, w in (("r_v", 256), ("r_g", 256), ("d_v", 64), ("n_vs", 64), ("n_vw", 64), ("s_z", 256),
                    ("i_w", 8), ("n_g", 12), ("s_dt", 4)):
        cols.extend([OFF[name] + i for i in range(w)])
    return np.asarray(cols, dtype=np.int64)


ROPED_TABLES = [0, 1, 2, 2, 2, 3, 3, 3, 2, 2, 2, 2, 2]
TM0 = (2 * len(ROPED_TABLES) + 7) * 128
NCOL2 = TM0 + 984
NBIS = 17
RET_LNG = [math.log1p(-2.0 ** (-5 - h)) for h in range(4)]


def host_consts(S):
    meta = np.zeros((128, 32), np.float32)

    def fill(t, headdim, rot, theta, scale):
        half = rot // 2
        inv = np.power(np.float32(theta), (-2.0 * np.arange(half, dtype=np.float32) / np.float32(rot)).astype(np.float32)).astype(np.float32)
        for p in range(128):
            i = p % headdim
            if i < rot:
                meta[p, t] = inv[i % half]
                meta[p, 4 + t] = scale
                meta[p, 8 + t] = -scale if i < half else scale
            else:
                meta[p, t] = 0.0
                meta[p, 4 + t] = 1.0
                meta[p, 8 + t] = 0.0

    fill(0, 32, 32, 10000.0, 1.0)
    fill(1, 32, 32, 10000.0, 32.0 ** -0.5)
    fill(2, 64, 16, 500000.0, 1.0)
    fill(3, 32, 8, 500000.0, 1.0)
    for p in range(128):
        meta[p, 12] = RET_LNG[p // 32]
        meta[p, 13 + p // 32] = 1.0
    bdm = np.zeros((128, 256), np.float32)
    for p in range(128):
        bdm[p, 64 * (p // 32):64 * (p // 32) + 64] = 1.0
    NC = (S - 32) // 16 + 1
    NCP = (NC + 127) // 128 * 128
    NB = S // 64
    ovl = np.zeros((NCP, NB), np.float32)
    for c in range(NC):
        for j in range(NB):
            ovl[c, j] = max(min(16 * c + 32, 64 * j + 64) - max(16 * c, 64 * j), 0) / 32.0
    return meta, bdm, ovl, NB, NCP


STAGES = ["ffn1", "inproj", "ret", "ssd", "dsa", "nsa", "merge", "xattn", "ffn2"]


def build(S, depth=DEPTH, stop_after=None):
    kb = KB(S, depth, stop_after)
    meta_np, bdm_np, ovl_np, NB, NCP = host_consts(S)
    kb.n_keep = min(256, S // 4)
    EI = "ExternalInput"
    x = kb.dram("x", [S, D], F32, kind=EI)
    mem = kb.dram("mem", [N_MEM, D], F32, kind=EI)
    ln_g = kb.dram("ln_g", [DEPTH, 4, D], F32, kind=EI)
    ln_b = kb.dram("ln_b", [DEPTH, 4, D], F32, kind=EI)
    f1gu = kb.dram("ffn1_w_gu", [DEPTH, D, 2 * DFF], F32, kind=EI)
    f1dn = kb.dram("ffn1_w_down", [DEPTH, DFF, D], F32, kind=EI)
    w_in = kb.dram("w_in", [DEPTH, D, 7224], F32, kind=EI)
    w2 = kb.dram("w2", [DEPTH, D, NCOL2], F32, kind=EI)
    cmp_w1 = kb.dram("cmp_w1", [DEPTH, 2, 2048, 64], F32, kind=EI)
    cmp_w2 = kb.dram("cmp_w2", [DEPTH, 2, 64, 64], F32, kind=EI)
    cmp_pos = kb.dram("cmp_pos", [DEPTH, 2, 32, 64], F32, kind=EI)
    conv_w = kb.dram("conv_w", [DEPTH, 4, 768], F32, kind=EI)
    conv_b = kb.dram("conv_b", [DEPTH, 768], F32, kind=EI)
    dt_bias = kb.dram("dt_bias", [DEPTH, 4], F32, kind=EI)
    a_log = kb.dram("a_log", [DEPTH, 4], F32, kind=EI)
    d_skip = kb.dram("d_skip", [DEPTH, 4], F32, kind=EI)
    norm_g = kb.dram("ssm_norm_g", [DEPTH, 256], F32, kind=EI)
    w_branch = kb.dram("w_branch", [DEPTH, 4, 256, D], F32, kind=EI)
    w_out = kb.dram("w_out", [DEPTH, D, D], F32, kind=EI)
    xwq = kb.dram("xattn_wq", [DEPTH, D, D], F32, kind=EI)
    xwkv = kb.dram("xattn_wkv", [DEPTH, D, 2 * D], F32, kind=EI)
    xwo = kb.dram("xattn_wo", [DEPTH, D, D], F32, kind=EI)
    f2gu = kb.dram("ffn2_w_gu", [DEPTH, D, 2 * DFF], F32, kind=EI)
    f2dn = kb.dram("ffn2_w_down", [DEPTH, DFF, D], F32, kind=EI)
    meta = kb.dram("meta", [128, 32], F32, kind=EI)
    bdm = kb.dram("bdm", [128, 256], F32, kind=EI)
    ovl = kb.dram("ovl", [NCP, NB], F32, kind=EI)
    out = kb.dram("out", [S, D], F32)
    xTa = kb.dram("xTa", [D, S], BF16)
    xTb = kb.dram("xTb", [D, S], BF16)
    xa = kb.dram("xa", [S, D], F32)
    xb2 = kb.dram("xb2", [S, D], F32)
    FMS = kb.dram("FMS", [20 * 128, S], BF16)
    TMB = kb.dram("TMB", [S, 960], BF16)
    TMF = kb.dram("TMF", [S, 24], F32)
    YT = kb.dram("YT", [1024, S], BF16)
    ROPE = kb.dram("ROPE", [4, 2, 128, S], F32)
    kb.setup()
    kb.setup_consts(meta, bdm, ovl, NB, NCP)
    kb.phase_rope(ROPE)
    kb.phase_transpose_in(x, xTa)

    def L(t, *idx):
        return T(t.h[idx], t.name, True)

    done = False
    xin = x
    for l in range(depth):
        last = (l == depth - 1)

        def stop(name):
            return stop_after == (l, name)
        kb.phase_ffn(xin, xTa, L(f1gu, l), L(f1dn, l), L(ln_g, l, 0), L(ln_b, l, 0), xa, xTb)
        if stop("ffn1"):
            break
        kb.phase_inproj(xTb, L(w2, l), ROPE, FMS, TMB, TMF)
        if stop("inproj"):
            break
        kb.phase_ret(FMS, TMB, YT)
        if stop("ret"):
            break
        kb.phase_ssd(FMS, TMB, TMF, L(conv_w, l), L(conv_b, l), L(dt_bias, l), L(a_log, l), L(d_skip, l), L(norm_g, l), YT)
        if stop("ssd"):
            break
        kb.phase_dsa_nsa(FMS, TMB, TMF, L(cmp_w1, l), L(cmp_w2, l), L(cmp_pos, l), YT)
        if stop("nsa") or stop("dsa"):
            break
        kb.phase_merge(xa, xTb, L(w_in, l), L(w_branch, l), L(w_out, l), L(ln_g, l, 1), L(ln_b, l, 1), YT, xb2, xTa)
        if stop("merge"):
            break
        kb.phase_xattn(xb2, xTa, mem, L(xwq, l), L(xwkv, l), L(xwo, l), L(ln_g, l, 2), L(ln_b, l, 2), xa, xTb)
        if stop("xattn"):
            break
        kb.phase_ffn(xa, xTb, L(f2gu, l), L(f2dn, l), L(ln_g, l, 3), L(ln_b, l, 3), out if last else xb2, None if last else xTa)
        if stop("ffn2"):
            break
        xin = xb2
    kb.flush_pending()
    st = kb.P.emit()
    kb.stats = st
    return kb


def make_in_maps(inputs, S, ncores):
    meta_np, bdm_np, ovl_np, NB, NCP = host_consts(S)
    colidx = build_colidx()
    w_in = np.asarray(inputs["w_in"], dtype=np.float32)
    w2 = np.ascontiguousarray(w_in[:, :, colidx])
    shared = {k: np.ascontiguousarray(np.asarray(v, dtype=np.float32)) for k, v in inputs.items() if k not in ("x", "mem")}
    shared["w2"] = w2
    shared["meta"] = meta_np
    shared["bdm"] = bdm_np
    shared["ovl"] = ovl_np
    maps = []
    for b in range(ncores):
        m = dict(shared)
        m["x"] = np.ascontiguousarray(np.asarray(inputs["x"][b, :S], dtype=np.float32))
        m["mem"] = np.ascontiguousarray(np.asarray(inputs["mem"][b], dtype=np.float32))
        maps.append(m)
    return maps


def kernel(**inputs):
    S = inputs["x"].shape[1]
    B = inputs["x"].shape[0]
    kb = build(S)
    maps = make_in_maps(inputs, S, B)
    res = run_bass_kernel_spmd(kb.nc, maps, core_ids=list(range(B)))
    out = np.stack([np.asarray(r["out"], dtype=np.float32) for r in res.results], axis=0)
    return out
```

```python
import math
import sys
import numpy as np
import concourse.bass as bass
import concourse.mybir as mybir
from concourse.bass_utils import run_bass_kernel_spmd

F32 = mybir.dt.float32
BF16 = mybir.dt.bfloat16
I32 = mybir.dt.int32
AF = mybir.ActivationFunctionType
ALU = mybir.AluOpType
AX = mybir.AxisListType

SEM_LIMIT = 30000
N_DMA_SEMS = 24


class Buf:
    __slots__ = ("name", "last_w", "readers")

    def __init__(self, name):
        self.name = name
        self.last_w = None
        self.readers = []


class Op:
    __slots__ = ("eng", "fn", "deps", "need_inc", "sem", "val", "is_dma", "idx", "tag", "odeps", "n", "seg", "pfirst", "fin", "st0", "crit")


class Prog:
    def __init__(self, nc):
        self.nc = nc
        self.engs = {"pe": nc.tensor, "act": nc.scalar, "dve": nc.vector, "pool": nc.gpsimd, "sp": nc.sync}
        self.ops = []
        self.bufs = {}
        self.last_on = {}
        self.dmas_since = []
        self.phase_deps = []
        self.phase_bufs = set()
        self.capture = None
        self.seg = 0
        self.do_sched = True
        self.est_time = 0.0

    def buf(self, name):
        b = self.bufs.get(name)
        if b is None:
            b = self.bufs[name] = Buf(name)
        return b

    def add(self, eng, fn, reads=(), writes=(), dma=False, extra_deps=(), n=64):
        if self.capture is not None:
            self.capture.append(((eng, fn), dict(reads=list(reads), writes=list(writes), dma=dma, n=n)))
            return None
        op = Op()
        op.eng = eng
        op.fn = fn
        op.is_dma = dma
        op.need_inc = False
        op.sem = None
        op.val = 0
        op.n = n
        op.seg = self.seg
        op.pfirst = False
        op.fin = 0.0
        op.idx = len(self.ops)
        try:
            op.tag = (sys._getframe(2).f_lineno, 0)
        except Exception:
            op.tag = (0, 0)
        deps = {}
        for b in reads:
            b = self.buf(b)
            w = b.last_w
            if w is not None:
                deps[w.idx] = (w, "raw")
        for b in writes:
            b = self.buf(b)
            w = b.last_w
            if w is not None and w.idx not in deps:
                deps[w.idx] = (w, "waw")
            for r in b.readers:
                if r.idx not in deps:
                    deps[r.idx] = (r, "war")
        real = []
        order = []
        for d, kind in deps.values():
            if (not d.is_dma) and d.eng == eng and not dma:
                if eng == "pe" or kind != "raw":
                    order.append(d)
                    continue
            real.append(d)
        for d in extra_deps:
            real.append(d)
        for b in list(reads) + list(writes):
            if b not in self.phase_bufs:
                self.phase_bufs.add(b)
                op.pfirst = True
        op.deps = real
        op.odeps = order
        for b in writes:
            b = self.buf(b)
            b.last_w = op
            b.readers = []
        for b in reads:
            self.buf(b).readers.append(op)
        self.ops.append(op)
        return op

    def barrier(self):
        self.seg += 1
        self.phase_bufs = set()

    def _cost(self, op):
        n = op.n
        e = op.eng
        if op.is_dma:
            return 0.08, 2.0 + n / 100e3
        if e == "pe":
            c = 0.035 + n / 2400.0
        elif e == "act":
            c = 0.22 + n / 1200.0
        elif e == "dve":
            c = 0.08 + n / 960.0
        elif e == "pool":
            c = 0.15 + n / 500.0
        else:
            c = 0.05
        return c, c

    def schedule(self):
        import heapq
        SCHED = self.do_sched
        self.seg_stats = []
        order = []
        ops = self.ops
        nseg = self.seg + 1
        segs = [[] for _ in range(nseg)]
        for op in ops:
            segs[op.seg].append(op)
        t_base = 0.0
        engs = list(self.engs.keys())
        for sg in segs:
            if not sg:
                continue
            if not SCHED:
                order.extend(sg)
                continue
            inseg = set(id(o) for o in sg)
            indeg = {}
            succ = {}
            dr = {}
            for op in sg:
                cnt = 0
                for d in op.deps + op.odeps:
                    if id(d) in inseg:
                        cnt += 1
                        succ.setdefault(id(d), []).append(op)
                indeg[id(op)] = cnt
                dr[id(op)] = t_base
            wait_h = {e: [] for e in engs}
            rdy_h = {e: [] for e in engs}
            free = {e: t_base for e in engs}
            for op in sg:
                if indeg[id(op)] == 0:
                    heapq.heappush(wait_h[op.eng], (dr[id(op)], op.idx, op))
            left = len(sg)
            tmax = t_base
            while left:
                best = None
                for e in engs:
                    wh = wait_h[e]
                    rh = rdy_h[e]
                    fe = free[e]
                    while wh and wh[0][0] <= fe:
                        _, ix, o = heapq.heappop(wh)
                        heapq.heappush(rh, (ix, o))
                    if rh:
                        cand = (fe, rh[0][0], e, 0)
                    elif wh:
                        cand = (wh[0][0], wh[0][1], e, 1)
                    else:
                        continue
                    if best is None or cand[:2] < best[:2]:
                        best = cand
                start, _, e, which = best
                if which == 0:
                    _, op = heapq.heappop(rdy_h[e])
                else:
                    _, _, op = heapq.heappop(wait_h[e])
                busy, lat = self._cost(op)
                free[e] = start + busy
                op.fin = start + lat
                op.st0 = start
                if op.fin > tmax:
                    tmax = op.fin
                order.append(op)
                left -= 1
                for sc in succ.get(id(op), ()):
                    k = id(sc)
                    extra = 0.05 if (sc.eng == op.eng and not op.is_dma) else 0.35
                    t = op.fin + extra
                    if t > dr[k]:
                        dr[k] = t
                    indeg[k] -= 1
                    if indeg[k] == 0:
                        heapq.heappush(wait_h[sc.eng], (dr[k], sc.idx, sc))
            busy_e = {e: 0.0 for e in engs}
            for o in sg:
                busy_e[o.eng] += self._cost(o)[0]
            self.seg_stats.append((sg[0].seg, len(sg), tmax - t_base, busy_e))
            t_base = tmax
        self.est_time = t_base
        return order

    def emit(self, final_wait_eng="sp"):
        nc = self.nc
        order = self.schedule()
        last_eng = {}
        prev_last = {}
        prev_dmas = []
        older_dmas = []
        cur_dmas = []
        cur_seg = -1
        for op in order:
            if op.seg != cur_seg:
                cur_seg = op.seg
                prev_last = dict(last_eng)
                prev_dmas = older_dmas + cur_dmas
                older_dmas = cur_dmas
                cur_dmas = []
            if op.pfirst:
                op.deps = op.deps + list(prev_last.values()) + prev_dmas
            if op.is_dma:
                cur_dmas.append(op)
            else:
                last_eng[op.eng] = op
        for op in order:
            for d in op.deps:
                d.need_inc = True
            if op.is_dma:
                op.need_inc = True
        eng_sem = {}
        eng_cnt = {}
        dma_sems = [nc.alloc_semaphore("dq%d" % i) for i in range(N_DMA_SEMS)]
        dma_cnt = [0] * N_DMA_SEMS
        dma_last = [None] * N_DMA_SEMS
        ndma = 0
        for op in order:
            if not op.need_inc:
                continue
            if op.is_dma:
                j = ndma % N_DMA_SEMS
                ndma += 1
                if dma_last[j] is not None:
                    op.deps.append(dma_last[j])
                dma_cnt[j] += 16
                op.sem = dma_sems[j]
                op.val = dma_cnt[j]
                dma_last[j] = op
            else:
                e = op.eng
                if e not in eng_sem or eng_cnt[e] >= SEM_LIMIT:
                    eng_sem[e] = nc.alloc_semaphore("s_%s_%d" % (e, op.idx))
                    eng_cnt[e] = 0
                eng_cnt[e] += 1
                op.sem = eng_sem[e]
                op.val = eng_cnt[e]
        waited = {}
        nwaits = 0
        for op in order:
            E = self.engs[op.eng]
            need = {}
            for d in op.deps:
                k = id(d.sem)
                if k not in need or need[k][1] < d.val:
                    need[k] = (d.sem, d.val)
            for k, (sem, val) in need.items():
                wk = (op.eng, k)
                if waited.get(wk, 0) >= val:
                    continue
                E.wait_ge(sem, val)
                nwaits += 1
                waited[wk] = val
            try:
                inst = op.fn()
            except Exception:
                print('EMIT FAIL at op', op.idx, op.eng)
                raise
            if op.need_inc:
                inst.then_inc(op.sem, 16 if op.is_dma else 1)
        E = self.engs[final_wait_eng]
        for j in range(N_DMA_SEMS):
            if dma_cnt[j] > 0:
                E.wait_ge(dma_sems[j], dma_cnt[j])
        self.stats = dict(n_ops=len(self.ops), n_waits=nwaits, n_dma=ndma,
                          n_inc=sum(1 for o in self.ops if o.need_inc), est_ms=self.est_time / 1e3)
        return self.stats


class V:
    __slots__ = ("ap", "b")

    def __init__(self, ap, b):
        self.ap = ap
        self.b = b


class T:
    def __init__(self, h, name, dram=False):
        self.h = h
        self.name = name
        self.dram = dram

    def __getitem__(self, idx):
        if self.dram:
            return V(self.h[idx], self.name)
        return V(self.h[idx], self.name)

    def v(self, ap):
        return V(ap, self.name)


DT_SIZE = {F32: 4, BF16: 2, I32: 4}

D = 1024
DFF = 2816
NKC = D // 128
NFC = DFF // 128
LN_EPS = 1e-5
DEPTH = 2
ALPHA = (2 * DEPTH) ** 0.25
N_MEM = 256


class KB:
    def __init__(self, S, depth=DEPTH, stop_after=None, debug=()):
        self.S = S
        self.depth = depth
        self.stop_after = stop_after
        self.debug = debug
        self.nc = bass.Bass("TRN2", target_bir_lowering=False)
        self.P = Prog(self.nc)
        self.uid = 0
        self.sb_base = 0
        self.sb_cur = 0
        self.outs = {}
        self.arena = None
        self.rots = {}
        self.pt_cnt = 0
        self.pending_T = None
        self.xb_cnt = 0
        self.ps_set = (0, 5)
        self.psb_set = (0, 2)
        self.fill_regs = {}
        self.n_keep = 256

    def sb(self, name, shape, dtype):
        nbytes = int(np.prod(shape[1:])) * DT_SIZE[dtype]
        nbytes = (nbytes + 63) // 64 * 64
        off = self.sb_cur
        self.sb_cur += nbytes
        assert self.sb_cur <= 207 * 1024, ("SBUF overflow", name, self.sb_cur)
        self.uid += 1
        if self.arena is None:
            self.arena = self.nc.alloc_sbuf_tensor("arena", [128, 207 * 1024], mybir.dt.uint8)
        ap = self.arena[:, off:off + int(np.prod(shape[1:])) * DT_SIZE[dtype]].bitcast(dtype)
        if len(shape) == 3:
            ap = ap.rearrange("p (a b) -> p a b", a=shape[1])
        elif len(shape) == 4:
            ap = ap.rearrange("p (a b c) -> p a b c", a=shape[1], b=shape[2])
        if shape[0] < 128:
            ap = ap[0:shape[0]]
        return T(ap, "%s_%d" % (name, self.uid))

    def phase_begin(self):
        self.flush_pending()
        self.P.barrier()
        self.sb_cur = self.sb_base

    def dram(self, name, shape, dtype, kind="ExternalOutput"):
        h = self.nc.dram_tensor(name, list(shape), dtype, kind=kind)
        return T(h.ap(), name, dram=True)

    def _rw(self, reads, writes):
        return [r.b for r in reads if isinstance(r, V)], [w.b for w in writes]

    def dma(self, out, in_, eng="sp"):
        nc = self.nc
        E = self.P.engs[eng]
        return self.P.add(eng, lambda: E.dma_start(out=out.ap, in_=in_.ap), reads=[in_.b], writes=[out.b], dma=True,
                          n=int(np.prod(out.ap.shape)) * 2)

    def mm(self, out, lhsT, rhs, start=True, stop=True):
        nc = self.nc
        return self.P.add("pe", lambda: nc.tensor.matmul(out.ap, lhsT.ap, rhs.ap, start=start, stop=stop),
                          reads=[lhsT.b, rhs.b], writes=[out.b], n=int(np.prod(out.ap.shape[1:])) * (4 if lhsT.ap.dtype == F32 else 1))

    def tr(self, out, in_, ident):
        nc = self.nc
        return self.P.add("pe", lambda: nc.tensor.transpose(out.ap, in_.ap, ident.ap),
                          reads=[in_.b, ident.b], writes=[out.b], n=200)

    def act(self, out, in_, func, bias=None, scale=None, accum=None, eng="act"):
        nc = self.nc
        kw = {}
        reads = [in_.b]
        writes = [out.b]
        if bias is not None:
            if isinstance(bias, V):
                kw["bias"] = bias.ap
                reads.append(bias.b)
            else:
                kw["bias"] = bias
        if scale is not None:
            if isinstance(scale, V):
                kw["scale"] = scale.ap
                reads.append(scale.b)
            else:
                kw["scale"] = scale
        if accum is not None:
            kw["accum_out"] = accum.ap
            writes.append(accum.b)
        return self.P.add("act", lambda: nc.scalar.activation(out=out.ap, in_=in_.ap, func=func, **kw),
                          reads=reads, writes=writes, n=int(np.prod(out.ap.shape[1:])))

    def ts(self, out, in0, s1, s2, op0, op1=None, accum=None, eng="dve"):
        E = self.P.engs[eng]
        reads = [in0.b]
        writes = [out.b]
        a1 = s1
        a2 = s2
        if isinstance(s1, V):
            a1 = s1.ap
            reads.append(s1.b)
        if isinstance(s2, V):
            a2 = s2.ap
            reads.append(s2.b)
        kw = {}
        if op1 is not None:
            kw["op1"] = op1
        if accum is not None:
            kw["accum_out"] = accum.ap
            writes.append(accum.b)
        return self.P.add(eng, lambda: E.tensor_scalar(out=out.ap, in0=in0.ap, scalar1=a1, scalar2=a2, op0=op0, **kw),
                          reads=reads, writes=writes, n=int(np.prod(out.ap.shape[1:])))

    def tt(self, out, in0, in1, op, eng="dve"):
        E = self.P.engs[eng]
        return self.P.add(eng, lambda: E.tensor_tensor(out=out.ap, in0=in0.ap, in1=in1.ap, op=op),
                          reads=[in0.b, in1.b], writes=[out.b], n=int(np.prod(out.ap.shape[1:])))

    def stt(self, out, in0, scalar, in1, op0, op1, accum=None):
        nc = self.nc
        reads = [in0.b, in1.b]
        writes = [out.b]
        a = scalar
        if isinstance(scalar, V):
            a = scalar.ap
            reads.append(scalar.b)
        kw = {}
        if accum is not None:
            kw["accum_out"] = accum.ap
            writes.append(accum.b)
        return self.P.add("dve", lambda: nc.vector.scalar_tensor_tensor(out=out.ap, in0=in0.ap, scalar=a, in1=in1.ap,
                                                                     op0=op0, op1=op1, **kw),
                          reads=reads, writes=writes, n=int(np.prod(out.ap.shape[1:])))

    def copy(self, out, in_, eng="dve"):
        E = self.P.engs[eng]
        if eng == "act":
            return self.P.add(eng, lambda: E.copy(out=out.ap, in_=in_.ap), reads=[in_.b], writes=[out.b], n=int(np.prod(out.ap.shape[1:])))
        return self.P.add(eng, lambda: E.tensor_copy(out=out.ap, in_=in_.ap), reads=[in_.b], writes=[out.b], n=int(np.prod(out.ap.shape[1:])))

    def memset(self, out, val, eng="pool"):
        E = self.P.engs[eng]
        return self.P.add(eng, lambda: E.memset(out.ap, val), writes=[out.b], n=int(np.prod(out.ap.shape[1:])))

    def red(self, out, in_, op, axis=AX.X, eng="dve"):
        E = self.P.engs[eng]
        return self.P.add(eng, lambda: E.tensor_reduce(out=out.ap, in_=in_.ap, axis=axis, op=op),
                          reads=[in_.b], writes=[out.b], n=int(np.prod(in_.ap.shape[1:])))

    def recip(self, out, in_):
        nc = self.nc
        return self.P.add("dve", lambda: nc.vector.reciprocal(out=out.ap, in_=in_.ap), reads=[in_.b], writes=[out.b])

    def aselect(self, out, in_, pattern, cmp, fill, base, cm):
        nc = self.nc
        regs = self.fill_regs

        def fn():
            if fill not in regs:
                regs[fill] = nc.gpsimd.to_reg(float(fill))
            return nc.gpsimd.affine_select(out=out.ap, in_=in_.ap, pattern=pattern, compare_op=cmp,
                                           fill=regs[fill], base=base, channel_multiplier=cm)
        return self.P.add("pool", fn, reads=[in_.b], writes=[out.b], n=int(np.prod(out.ap.shape[1:])))

    def iota(self, out, pattern, base, cm):
        nc = self.nc
        return self.P.add("pool", lambda: nc.gpsimd.iota(out.ap, pattern=pattern, base=base, channel_multiplier=cm,
                                                         allow_small_or_imprecise_dtypes=True), writes=[out.b], n=int(np.prod(out.ap.shape[1:])))

    def setup(self):
        nc = self.nc
        self.ps = []
        for i in range(5):
            h = nc.alloc_psum_tensor("ps%d" % i, [128, 512], F32)
            self.ps.append(T(h, "ps%d" % i))
        self.psb = []
        for i in range(2):
            h = nc.alloc_psum_tensor("psb%d" % i, [128, 1024], BF16)
            self.psb.append(T(h, "psb%d" % i))
        self.ps_rr = 0
        self.psb_rr = 0
        self.ident_f = self.sb("identf", [128, 128], F32)
        self.ident = self.sb("ident", [128, 128], BF16)
        self.memset(self.ident_f[:], 1.0)
        self.aselect(self.ident_f[:], self.ident_f[:], [[-1, 128]], ALU.is_equal, 0.0, 0, 1)
        self.copy(self.ident[:], self.ident_f[:], eng="pool")
        self.sb_base = self.sb_cur

    def next_ps(self):
        b0, n = self.ps_set
        t = self.ps[b0 + self.ps_rr % n]
        self.ps_rr += 1
        return t

    def next_psb(self):
        b0, n = self.psb_set
        t = self.psb[b0 + self.psb_rr % n]
        self.psb_rr += 1
        return t

    def load_w(self, name, dram_ap_fn, kchunks, ncols, eng="pool", split=4):
        w = self.sb(name, [128, kchunks, ncols], BF16)
        for k in range(kchunks):
            self.dma(w[:, k, :], dram_ap_fn(k), eng="pool")
        return w

    def layer_norm_tile(self, r, g_bc, b_bc, out_f32, scr):
        st = scr["st"]
        junk = scr["junk"]
        self.act(junk[:], r[:], AF.Identity, accum=st[:, 0:1])
        self.act(junk[:], r[:], AF.Square, accum=st[:, 1:2])
        self.ts(st[:, 2:3], st[:, 0:1], 1.0 / D, None, ALU.mult)
        self.tt(st[:, 3:4], st[:, 2:3], st[:, 2:3], ALU.mult)
        self.stt(st[:, 4:5], st[:, 1:2], 1.0 / D, st[:, 3:4], ALU.mult, ALU.subtract)
        self.ts(st[:, 4:5], st[:, 4:5], 0.0, LN_EPS, ALU.max, ALU.add)
        self.act(st[:, 5:6], st[:, 4:5], AF.Sqrt)
        self.recip(st[:, 6:7], st[:, 5:6])
        self.ts(out_f32[:], r[:], st[:, 2:3], st[:, 6:7], ALU.subtract, ALU.mult)
        self.tt(out_f32[:], out_f32[:], g_bc[:], ALU.mult)
        self.tt(out_f32[:], out_f32[:], b_bc[:], ALU.add)

    def store_xT(self, x_f32, xT_dram, t0, scr, defer=False):
        xbl = scr["xb"]
        if isinstance(xbl, list):
            xb = xbl[self.xb_cnt % len(xbl)]
            self.xb_cnt += 1
        else:
            xb = xbl
        xTs = scr["xTs"]
        self.copy(xb[:], x_f32[:], eng="act")

        def part_b():
            pb = self.next_psb()
            for k in range(NKC):
                self.tr(pb[:, k * 128:(k + 1) * 128], xb[:, k * 128:(k + 1) * 128], self.ident[:])
            self.copy(xTs[:], pb[:, :], eng="dve")
            self.dma(V(xT_dram.h.rearrange("(k p) s -> p k s", p=128)[:, :, t0:t0 + 128], xT_dram.name),
                     V(xTs.h[:].rearrange("p (k t) -> p k t", k=NKC), xTs.name))
        if defer:
            self.flush_pending()
            self.pending_T = part_b
        else:
            part_b()

    def flush_pending(self):
        if self.pending_T is not None:
            f = self.pending_T
            self.pending_T = None
            f()

    def dma_s(self, out, in_, eng="sp"):
        E = self.P.engs[eng]
        return self.P.add(eng, lambda: E.dma_start(out=out.ap, in_=in_.ap, allow_slow_non_contiguous=True),
                          reads=[in_.b], writes=[out.b], dma=True, n=int(np.prod(out.ap.shape)) * 8)

    def rot(self, name, n, shape, dtype):
        key = "_rot_" + name
        lst = [self.sb(name + str(i), shape, dtype) for i in range(n)]
        self.rots[key] = [lst, 0]
        return key

    def nx(self, key):
        lst, i = self.rots[key]
        self.rots[key][1] = i + 1
        return lst[i % len(lst)]

    def vmax(self, out, in_):
        nc = self.nc
        return self.P.add("dve", lambda: nc.vector.max(out=out.ap, in_=in_.ap), reads=[in_.b], writes=[out.b], n=int(np.prod(in_.ap.shape[1:])))

    def match_replace(self, out, rep, vals, imm):
        nc = self.nc
        return self.P.add("dve", lambda: nc.vector.match_replace(out=out.ap, in_to_replace=rep.ap, in_values=vals.ap, imm_value=imm),
                          reads=[rep.b, vals.b], writes=[out.b], n=int(np.prod(vals.ap.shape[1:])))

    def redabs(self, out, in_):
        nc = self.nc
        return self.P.add("dve", lambda: nc.vector.tensor_reduce(out=out.ap, in_=in_.ap, axis=AX.X, op=ALU.max,
                                                                 apply_absolute_value=True),
                          reads=[in_.b], writes=[out.b], n=int(np.prod(in_.ap.shape[1:])))

    def fm_rows(self, FMS, c0, nchunk, s0, s1):
        return V(FMS.h[c0 * 128:(c0 + nchunk) * 128, s0:s1].rearrange("(c p) s -> p c s", p=128), FMS.name)

    def setup_consts(self, meta, bdm, ovl, NB, NCP):
        S = self.S
        NT = S // 128
        self.NB = NB
        self.NCP = NCP
        self.meta = self.sb("meta", [128, 32], F32)
        self.dma(self.meta[:], meta[:, :])
        self.bdm = self.sb("bdm", [128, 256], F32)
        self.dma(self.bdm[:], bdm[:, :])
        self.ovl = self.sb("ovl", [128, NCP // 128, NB], F32)
        self.dma(self.ovl[:], V(ovl.h.rearrange("(c p) j -> p c j", p=128), ovl.name))
        self.U = self.sb("U", [128, 128], F32)
        self.memset(self.U[:], 1.0)
        self.aselect(self.U[:], self.U[:], [[1, 128]], ALU.is_ge, 0.0, 0, -1)
        self.cneg30 = self.sb("cneg30", [128, 128], F32)
        self.memset(self.cneg30[:], 0.0)
        self.aselect(self.cneg30[:], self.cneg30[:], [[-1, 128]], ALU.is_ge, -1e30, 0, 1)
        self.cneg2k = self.sb("cneg2k", [128, 128], F32)
        self.memset(self.cneg2k[:], 0.0)
        self.aselect(self.cneg2k[:], self.cneg2k[:], [[-1, 128]], ALU.is_ge, -2000.0, 0, 1)
        self.band = self.sb("band", [128, 640], F32)
        self.memset(self.band[:], 0.0)
        self.aselect(self.band[:], self.band[:], [[1, 640]], ALU.is_ge, -2000.0, -1, -1)
        self.aselect(self.band[:], self.band[:], [[-1, 640]], ALU.is_ge, -2000.0, 512, 1)
        self.decayT4 = self.sb("decayT4", [128, 4, 128], F32)
        self.xi = self.sb("xi", [128, 128], F32)
        self.zeta = self.sb("zeta", [128, 128], F32)
        self.cdecay = self.sb("cdecay", [128, 1], F32)
        self.rkc = self.sb("rkc", [128, 20], F32)
        self.sb_base = self.sb_cur
        dji = self.sb("dji", [128, 128], F32)
        self.iota(dji[:], [[1, 128]], 0, -1)
        for h in range(4):
            self.act(self.decayT4[:, h, :], dji[:], AF.Exp, scale=RET_LNG[h])
        self.tt(self.decayT4[:], self.decayT4[:], V(self.U.h[:, :].unsqueeze(1).to_broadcast([128, 4, 128]), self.U.name), ALU.mult)
        ip1 = self.sb("ip1", [128, 128], F32)
        self.iota(ip1[:], [[1, 128]], 1, 0)
        self.act(self.xi[:], ip1[:], AF.Exp, scale=self.meta[:, 12:13])
        jr = self.sb("jr", [128, 128], F32)
        self.iota(jr[:], [[0, 128]], 127, -1)
        for h in range(4):
            self.act(self.zeta[:, 32 * h:32 * h + 32], jr[:, 32 * h:32 * h + 32], AF.Exp, scale=RET_LNG[h])
        c128 = self.sb("c128", [128, 1], F32)
        self.memset(c128[:], 128.0)
        self.act(self.cdecay[:], c128[:], AF.Exp, scale=self.meta[:, 12:13])
        for k in range(20):
            self.memset(self.rkc[:, k:k + 1], 2.0 ** (-k))

    def build_addmask(self):
        NT = self.S // 128
        NB = self.NB
        self.addmask = self.sb("addmask", [128, NT, NB], F32)
        self.memset(self.addmask[:], 0.0)
        for i in range(NT):
            for half in range(2):
                cur = 2 * i + half
                r0 = 64 * half
                v = self.addmask[r0:r0 + 64, i, :]
                self.aselect(v, v, [[-1, NB]], ALU.is_ge, -1e30, cur, 0)
                self.memset(self.addmask[r0:r0 + 64, i, 0:1], 1e30)
                self.memset(self.addmask[r0:r0 + 64, i, cur:cur + 1], 1e30)
                if cur >= 1:
                    self.memset(self.addmask[r0:r0 + 64, i, cur - 1:cur], 1e30)

    def phase_rope(self, ROPE):
        S = self.S
        self.phase_begin()
        pos = self.sb("pos", [128, S], F32)
        self.iota(pos[:], [[1, S]], 0, 0)
        a = self.sb("a", [128, S], F32)
        ki = self.sb("ki", [128, S], I32)
        kf = self.sb("kf", [128, S], F32)
        m = self.sb("m", [128, S], F32)
        r = self.sb("r", [128, S], F32)
        PI = math.pi
        for t in range(4):
            for which in range(2):
                self.ts(a[:], pos[:], self.meta[:, t:t + 1], (PI / 2 if which == 0 else 0.0), ALU.mult, ALU.add)
                self.ts(kf[:], a[:], 1.0 / (2 * PI), None, ALU.mult)
                self.copy(ki[:], kf[:])
                self.copy(kf[:], ki[:])
                self.stt(r[:], kf[:], -2 * PI, a[:], ALU.mult, ALU.add)
                self.ts(m[:], r[:], PI, -2 * PI, ALU.is_gt, ALU.mult)
                self.tt(r[:], r[:], m[:], ALU.add)
                self.ts(m[:], r[:], -PI, 2 * PI, ALU.is_lt, ALU.mult)
                self.tt(r[:], r[:], m[:], ALU.add)
                self.ts(r[:], r[:], PI, -PI, ALU.min, ALU.max)
                self.act(r[:], r[:], AF.Sin)
                col = 4 + 4 * which + t
                self.ts(r[:], r[:], self.meta[:, col:col + 1], None, ALU.mult)
                self.dma(ROPE[t, which], r[:])

    def finish_tile(self, rq, g_bc, b_bc, x_out, xT_out, t0, scr):
        self.layer_norm_tile(rq, g_bc, b_bc, rq, scr)
        self.dma(x_out[t0:t0 + 128, :], rq[:])
        if xT_out is not None:
            self.store_xT(rq, xT_out, t0, scr, defer=True)

    def ln_setup(self, ln_g, ln_b):
        g_bc = self.sb("g_bc", [128, D], F32)
        b_bc = self.sb("b_bc", [128, D], F32)
        self.dma(g_bc[:], V(ln_g.h.partition_broadcast(128), ln_g.name))
        self.dma(b_bc[:], V(ln_b.h.partition_broadcast(128), ln_b.name))
        scr = dict(st=self.sb("st", [128, 8], F32), junk=self.sb("junk", [128, D], BF16),
                   xb=[self.sb("xb0", [128, D], BF16), self.sb("xb1", [128, D], BF16)], xTs=self.sb("xTs", [128, D], BF16))
        return g_bc, b_bc, scr

    def phase_ffn(self, x_in, xT_in, w_gu, w_down, ln_g, ln_b, x_out, xT_out):
        S = self.S
        self.phase_begin()
        wgu_v = w_gu.h.rearrange("(k p) c -> p k c", p=128)
        wgb = []
        for jb in range(NFC // 2):
            blk = self.sb("wgu%d" % jb, [128, NKC, 512], BF16)
            self.dma(blk[:, :, 0:256], V(wgu_v[:, :, jb * 256:(jb + 1) * 256], w_gu.name), eng="pool")
            self.dma(blk[:, :, 256:512], V(wgu_v[:, :, DFF + jb * 256:DFF + (jb + 1) * 256], w_gu.name), eng="pool")
            wgb.append(blk)
        wdn = self.load_w("wdn", lambda k: V(w_down.h[k * 128:(k + 1) * 128, :], w_down.name), NFC, D)
        g_bc, b_bc, scr = self.ln_setup(ln_g, ln_b)
        xt = self.sb("xT", [128, NKC, 512], BF16)
        hT = self.sb("hT", [128, NFC, 512], BF16)
        sg = [self.sb("sg%d" % i, [128, 512], BF16) for i in range(2)]
        xr = [self.sb("xr%d" % i, [128, D], F32) for i in range(2)]
        rqs = [self.sb("r%d" % i, [128, D], F32) for i in range(2)]
        ntile = S // 512

        def load_xt(t_):
            self.dma(xt[:], V(xT_in.h.rearrange("(k p) s -> p k s", p=128)[:, :, t_ * 512:(t_ + 1) * 512], xT_in.name))

        def load_xq(idx):
            self.dma(xr[idx % 2][:], x_in[idx * 128:(idx + 1) * 128, :])
        load_xt(0)
        load_xq(0)
        for t in range(ntile):
            for j in range(NFC):
                pg = self.next_ps()
                pu = self.next_ps()
                wb_ = wgb[j // 2]
                o_ = (j % 2) * 128
                for k in range(NKC):
                    self.mm(pg[:], wb_[:, k, o_:o_ + 128], xt[:, k, :], start=(k == 0), stop=(k == NKC - 1))
                for k in range(NKC):
                    self.mm(pu[:], wb_[:, k, 256 + o_:256 + o_ + 128], xt[:, k, :], start=(k == 0), stop=(k == NKC - 1))
                s = sg[j % 2]
                self.act(s[:], pg[:], AF.Silu)
                self.tt(hT[:, j, :], s[:], pu[:], ALU.mult)
            if t + 1 < ntile:
                load_xt(t + 1)
            for q in range(4):
                t0 = t * 512 + q * 128
                xq = xr[q % 2]
                rq = rqs[q % 2]
                if t * 4 + q + 1 < ntile * 4:
                    load_xq(t * 4 + q + 1)
                for half in range(2):
                    hs = slice(half * 512, (half + 1) * 512)
                    pd = self.next_ps()
                    for j in range(NFC):
                        self.mm(pd[:], hT[:, j, q * 128:(q + 1) * 128], wdn[:, j, hs], start=(j == 0), stop=(j == NFC - 1))
                    self.act(xq[:, hs], xq[:, hs], AF.Copy, scale=ALPHA)
                    self.stt(rq[:, hs], pd[:], 0.5, xq[:, hs], ALU.mult, ALU.add)
                self.finish_tile(rq, g_bc, b_bc, x_out, xT_out, t0, scr)

    def phase_transpose_in(self, x_in, xT_out):
        S = self.S
        self.phase_begin()
        xr = [self.sb("xr%d" % i, [128, D], F32) for i in range(2)]
        scr = dict(xb=self.sb("xb", [128, D], BF16), xTs=self.sb("xTs", [128, D], BF16))
        for i in range(S // 128):
            xq = xr[i % 2]
            self.dma(xq[:], x_in[i * 128:(i + 1) * 128, :])
            self.store_xT(xq, xT_out, i * 128, scr)

    def phase_inproj(self, xT_in, w2, ROPE, FMS, TMB, TMF):
        S = self.S
        self.phase_begin()
        w = self.sb("win", [128, NKC, NCOL2], BF16)
        w2_v = w2.h.rearrange("(k p) c -> p k c", p=128)
        bounds = list(range(0, 4096 + 1, 512)) + [TM0, TM0 + 512, NCOL2]
        for bi in range(len(bounds) - 1):
            a_, b_ = bounds[bi], bounds[bi + 1]
            self.P.add("pool", (lambda a=a_, b=b_: self.nc.gpsimd.dma_start(out=w.h[:, :, a:b], in_=w2_v[:, :, a:b])),
                       reads=[w2.name], writes=["win_b%d" % bi], dma=True, n=128 * NKC * (b_ - a_) * 2)

        def wv(k, a, b):
            for bi in range(len(bounds) - 1):
                if bounds[bi] <= a and b <= bounds[bi + 1]:
                    return V(w.h[:, k, a:b], "win_b%d" % bi)
            raise AssertionError((a, b))
        xts = [self.sb("xT%d" % i_, [128, NKC, 512], BF16) for i_ in range(2)]
        tabs = [self.sb("tab%d" % i_, [128, 4, 2, 512], F32) for i_ in range(2)]
        t1 = self.rot("t1", 3, [128, 512], F32)
        t2 = self.rot("t2", 3, [128, 512], F32)
        ob = self.rot("ob", 4, [128, 512], BF16)
        tmb = self.rot("tmb", 2, [128, 960], BF16)
        tmf = self.rot("tmf", 2, [128, 24], F32)
        ntile = S // 512

        def load_t(t_):
            ss_ = slice(t_ * 512, (t_ + 1) * 512)
            self.dma(xts[t_ % 2][:], V(xT_in.h.rearrange("(k p) s -> p k s", p=128)[:, :, ss_], xT_in.name))
            self.dma(tabs[t_ % 2][:], V(ROPE.h[:, :, :, ss_].rearrange("t w p s -> p t w s"), ROPE.name))
        load_t(0)
        for t in range(ntile):
            ss = slice(t * 512, (t + 1) * 512)
            xt = xts[t % 2]
            tab = tabs[t % 2]
            if t + 1 < ntile:
                load_t(t + 1)
            for ci, tb in enumerate(ROPED_TABLES):
                pA = self.next_ps()
                pB = self.next_ps()
                for k in range(NKC):
                    self.mm(pA[:], wv(k, (2 * ci) * 128, (2 * ci + 1) * 128), xt[:, k, :], start=(k == 0), stop=(k == NKC - 1))
                for k in range(NKC):
                    self.mm(pB[:], wv(k, (2 * ci + 1) * 128, (2 * ci + 2) * 128), xt[:, k, :], start=(k == 0), stop=(k == NKC - 1))
                a1 = self.nx(t1)
                a2 = self.nx(t2)
                o = self.nx(ob)
                self.tt(a1[:], pA[:], tab[:, tb, 0, :], ALU.mult)
                self.tt(a2[:], pB[:], tab[:, tb, 1, :], ALU.mult)
                self.tt(o[:], a1[:], a2[:], ALU.add)
                self.dma(V(FMS.h[ci * 128:(ci + 1) * 128, ss], FMS.name), o[:])
            nr = len(ROPED_TABLES)
            for j in range(7):
                wc = 2 * nr + j
                pA = self.next_ps()
                for k in range(NKC):
                    self.mm(pA[:], wv(k, wc * 128, (wc + 1) * 128), xt[:, k, :], start=(k == 0), stop=(k == NKC - 1))
                o = self.nx(ob)
                self.copy(o[:], pA[:], eng="act")
                self.dma(V(FMS.h[(nr + j) * 128:(nr + j + 1) * 128, ss], FMS.name), o[:])
            for q in range(4):
                t0 = t * 512 + q * 128
                pA = self.next_ps()
                pB = self.next_ps()
                for k in range(NKC):
                    self.mm(pA[:], xt[:, k, q * 128:(q + 1) * 128], wv(k, TM0, TM0 + 512), start=(k == 0), stop=(k == NKC - 1))
                for k in range(NKC):
                    self.mm(pB[:, 0:472], xt[:, k, q * 128:(q + 1) * 128], wv(k, TM0 + 512, TM0 + 984), start=(k == 0), stop=(k == NKC - 1))
                b = self.nx(tmb)
                f = self.nx(tmf)
                self.copy(b[:, 0:512], pA[:], eng="act")
                self.copy(b[:, 512:960], pB[:, 0:448])
                self.copy(f[:], pB[:, 448:472])
                self.dma(TMB[t0:t0 + 128, :], b[:])
                self.dma(TMF[t0:t0 + 128, :], f[:])

    def store_yT(self, y, YT, br, n, yTs_key):
        pb = self.next_psb()
        self.tr(pb[:, 0:128], y[:, 0:128], self.ident[:])
        self.tr(pb[:, 128:256], y[:, 128:256], self.ident[:])
        yTs = self.nx(yTs_key)
        self.copy(yTs[:], pb[:, 0:256])
        self.dma(V(YT.h[br * 256:(br + 1) * 256, n * 128:(n + 1) * 128].rearrange("(c p) t -> p c t", p=128), YT.name),
                 V(yTs.h[:, :].rearrange("p (c t) -> p c t", c=2), yTs.name))

    def phase_ret(self, FMS, TMB, YT):
        S = self.S
        NT = S // 128
        self.phase_begin()
        rq = self.sb("rq", [128, S], BF16)
        rk = self.sb("rk", [128, S], BF16)
        self.dma(rq[:], V(FMS.h[0:128, :], FMS.name))
        self.dma(rk[:], V(FMS.h[128:256, :], FMS.name))
        Sbd = self.sb("Sbd", [128, 256], F32)
        Sbd_bf = self.sb("Sbd_bf", [128, 256], BF16)
        self.memset(Sbd[:], 0.0)
        self.memset(Sbd_bf[:], 0.0)
        vt_k = self.rot("vt", 2, [128, 512], BF16)
        qxi_k = self.rot("qxi", 2, [128, 128], BF16)
        qm_k = self.rot("qm", 2, [128, 4, 128], BF16)
        kz_k = self.rot("kz", 2, [128, 128], BF16)
        PT_k = self.rot("PT", 2, [128, 4, 128], BF16)
        cross_k = self.rot("cross", 2, [128, 256], F32)
        o_k = self.rot("o", 2, [128, 256], F32)
        tmp_k = self.rot("tmp", 2, [128, 256], F32)
        osq_k = self.rot("osq", 2, [128, 256], F32)
        sg_k = self.rot("sg", 2, [128, 256], F32)
        st_k = self.rot("st", 2, [128, 16], F32)
        y_k = self.rot("y", 2, [128, 256], BF16)
        yTs_k = self.rot("yTs", 2, [128, 256], BF16)
        hm = V(self.meta.h[:, 13:17].unsqueeze(2).to_broadcast([128, 4, 128]), self.meta.name)
        for n in range(NT):
            sl = slice(n * 128, (n + 1) * 128)
            vt = self.nx(vt_k)
            self.dma(vt[:], TMB[n * 128:(n + 1) * 128, 0:512])
            qxi = self.nx(qxi_k)
            self.tt(qxi[:], rq[:, sl], self.xi[:], ALU.mult)
            qm = self.nx(qm_k)
            self.tt(qm[:], V(rq.h[:, sl].unsqueeze(1).to_broadcast([128, 4, 128]), rq.name), hm, ALU.mult, eng="pool")
            pb = self.next_psb()
            self.tr(pb[:, 0:128], rk[:, sl], self.ident[:])
            kz = self.nx(kz_k)
            self.tt(kz[:], pb[:, 0:128], self.zeta[:], ALU.mult)
            ps1 = self.next_ps()
            self.mm(ps1[:], rk[:, sl], V(qm.h[:, :, :].rearrange("p h i -> p (h i)"), qm.name))
            PT = self.nx(PT_k)
            self.tt(PT[:], V(ps1.h[:, :].rearrange("p (h i) -> p h i", h=4), ps1.name), self.decayT4[:], ALU.mult)
            ps2 = self.next_ps()
            self.mm(ps2[:, 0:256], qxi[:], Sbd_bf[:])
            cross = self.nx(cross_k)
            self.copy(cross[:], ps2[:, 0:256], eng="act")
            ps3 = self.next_ps()
            for h in range(4):
                self.mm(ps3[:, 64 * h:64 * h + 64], PT[:, h, :], vt[:, 64 * h:64 * h + 64])
            o = self.nx(o_k)
            self.tt(o[:], ps3[:, 0:256], cross[:], ALU.add)
            ps4 = self.next_ps()
            self.mm(ps4[:, 0:256], kz[:], vt[:, 0:256])
            tmp = self.nx(tmp_k)
            self.tt(tmp[:], ps4[:, 0:256], self.bdm[:], ALU.mult)
            self.stt(Sbd[:], Sbd[:], self.cdecay[:, 0:1], tmp[:], ALU.mult, ALU.add)
            self.copy(Sbd_bf[:], Sbd[:], eng="act")
            st = self.nx(st_k)
            o3 = V(o.h[:, :].rearrange("p (h e) -> p h e", h=4), o.name)
            self.red(st[:, 0:4], o3, ALU.add)
            osq = self.nx(osq_k)
            self.tt(osq[:], o[:], o[:], ALU.mult, eng="pool")
            self.red(st[:, 4:8], V(osq.h[:, :].rearrange("p (h e) -> p h e", h=4), osq.name), ALU.add)
            self.ts(st[:, 8:12], st[:, 0:4], 1.0 / 64, None, ALU.mult)
            self.tt(st[:, 12:16], st[:, 8:12], st[:, 8:12], ALU.mult)
            self.stt(st[:, 4:8], st[:, 4:8], 1.0 / 64, st[:, 12:16], ALU.mult, ALU.subtract)
            self.ts(st[:, 4:8], st[:, 4:8], 0.0, LN_EPS, ALU.max, ALU.add)
            self.act(st[:, 4:8], st[:, 4:8], AF.Sqrt)
            self.recip(st[:, 4:8], st[:, 4:8])
            self.tt(o3, o3, V(st.h[:, 8:12].unsqueeze(2).to_broadcast([128, 4, 64]), st.name), ALU.subtract)
            self.tt(o3, o3, V(st.h[:, 4:8].unsqueeze(2).to_broadcast([128, 4, 64]), st.name), ALU.mult)
            sg = self.nx(sg_k)
            self.act(sg[:], vt[:, 256:512], AF.Silu)
            y = self.nx(y_k)
            self.tt(y[:], o[:], sg[:], ALU.mult)
            self.store_yT(y, YT, 0, n, yTs_k)

    def phase_ssd(self, FMS, TMB, TMF, conv_w, conv_b, dt_bias, a_log, d_skip, norm_g, YT):
        S = self.S
        NT = S // 128
        cw = self.sb("cw", [128, 6, 4], F32)
        for k_ in range(4):
            self.dma_s(cw[:, :, k_], V(conv_w.h[k_].rearrange("(c p) -> p c", p=128), conv_w.name))
        cb = self.sb("cb", [128, 6], F32)
        self.dma_s(cb[:], V(conv_b.h.rearrange("(c p) -> p c", p=128), conv_b.name))
        dtb = self.sb("dtb", [128, 4], F32)
        self.dma(dtb[:], V(dt_bias.h.partition_broadcast(128), dt_bias.name))
        a_bc = self.sb("a_bc", [128, 4], F32)
        self.dma(a_bc[:], V(a_log.h.partition_broadcast(128), a_log.name))
        self.act(a_bc[:], a_bc[:], AF.Exp)
        self.ts(a_bc[:], a_bc[:], -1.0, None, ALU.mult)
        Dbc = self.sb("Dbc", [128, 4], F32)
        self.dma(Dbc[:], V(d_skip.h.partition_broadcast(128), d_skip.name))
        ng_bc = self.sb("ng_bc", [128, 256], F32)
        self.dma(ng_bc[:], V(norm_g.h.partition_broadcast(128), norm_g.name))
        xbcs = self.sb("xbcs", [128, 6, S], BF16)
        raw_k = self.rot("raw", 2, [128, 6, 515], BF16)
        acc_k = self.rot("acc", 2, [128, 512], F32)
        for t in range(S // 512):
            raw = self.nx(raw_k)
            if t == 0:
                self.memset(raw[:, :, 0:3], 0.0)
                self.dma(raw[:, :, 3:515], self.fm_rows(FMS, 14, 6, 0, 512))
            else:
                self.dma(raw[:, :, 0:515], self.fm_rows(FMS, 14, 6, t * 512 - 3, (t + 1) * 512))
            for c in range(6):
                acc = self.nx(acc_k)
                self.ts(acc[:], raw[:, c, 3:515], cw[:, c, 3:4], None, ALU.mult)
                for k in (2, 1, 0):
                    self.stt(acc[:], raw[:, c, k:k + 512], cw[:, c, k:k + 1], acc[:], ALU.mult, ALU.add)
                self.act(xbcs[:, c, t * 512:(t + 1) * 512], acc[:], AF.Silu, bias=cb[:, c:c + 1])
        prev = self.sb("prev", [128, 256], F32)
        prev_bf = self.sb("prev_bf", [128, 256], BF16)
        self.memset(prev[:], 0.0)
        self.memset(prev_bf[:], 0.0)
        xsB_k = self.rot("xsB", 2, [128, 512], BF16)
        tmf_k = self.rot("tmf", 2, [128, 24], F32)
        zt_k = self.rot("zt", 2, [128, 256], BF16)
        st_k = self.rot("st", 2, [128, 32], F32)
        adtb_k = self.rot("adtb", 2, [128, 4, 128], F32)
        seg_k = self.rot("seg", 2, [128, 4, 128], F32)
        MT_k = self.rot("MT", 2, [128, 4, 128], BF16)
        X_k = self.rot("X", 2, [128, 256], BF16)
        Xd_k = self.rot("Xd", 2, [128, 256], BF16)
        yd_k = self.rot("yd", 2, [128, 256], F32)
        y_k = self.rot("y", 2, [128, 256], F32)
        t2_k = self.rot("t2", 2, [128, 256], F32)
        sz_k = self.rot("sz", 2, [128, 256], F32)
        yb_k = self.rot("yb", 2, [128, 256], BF16)
        yTs_k = self.rot("yTs", 2, [128, 256], BF16)
        Ubc = V(self.U.h[:, :].unsqueeze(1).to_broadcast([128, 4, 128]), self.U.name)

        def h4(t_):
            return V(t_.h[:, 0:256].rearrange("p (h e) -> p h e", h=4), t_.name)

        def bc4(v_):
            return V(v_.ap.unsqueeze(2).to_broadcast([128, 4, 64]), v_.b)

        for n in range(NT):
            sl = slice(n * 128, (n + 1) * 128)
            pb = self.next_psb()
            for c in range(4):
                self.tr(pb[:, c * 128:(c + 1) * 128], xbcs[:, c, sl], self.ident[:])
            xsB = self.nx(xsB_k)
            self.copy(xsB[:], pb[:, 0:512])
            tmf = self.nx(tmf_k)
            self.dma(tmf[:], TMF[n * 128:(n + 1) * 128, :])
            zt = self.nx(zt_k)
            self.dma(zt[:], TMB[n * 128:(n + 1) * 128, 704:960])
            st = self.nx(st_k)
            self.tt(st[:, 0:4], tmf[:, 20:24], dtb[:], ALU.add)
            self.act(st[:, 0:4], st[:, 0:4], AF.Exp)
            self.act(st[:, 0:4], st[:, 0:4], AF.Ln, bias=1.0)
            self.tt(st[:, 4:8], st[:, 0:4], a_bc[:], ALU.mult)
            adtb = self.nx(adtb_k)
            self.copy(adtb[:], V(st.h[:, 4:8].unsqueeze(2).to_broadcast([128, 4, 128]), st.name))
            psA = self.next_ps()
            self.mm(psA[:, 0:4], self.U[:], st[:, 4:8])
            self.copy(st[:, 8:12], psA[:, 0:4], eng="act")
            psB = self.next_ps()
            for h in range(4):
                self.mm(psB[:, h * 128:(h + 1) * 128], adtb[:, h, :], self.U[:])
            seg = self.nx(seg_k)
            for h in range(4):
                self.ts(seg[:, h, :], psB[:, h * 128:(h + 1) * 128], st[:, 8 + h:9 + h], 0.0, ALU.subtract, ALU.min)
            self.act(seg[:], seg[:], AF.Exp)
            self.tt(seg[:], seg[:], Ubc, ALU.mult, eng="pool")
            alast = V(psB.h[:, 127:512:128], psB.name)
            self.tt(st[:, 12:16], alast, st[:, 8:12], ALU.subtract)
            self.act(st[:, 12:16], st[:, 12:16], AF.Exp)
            self.act(st[:, 16:20], alast, AF.Exp)
            self.act(st[:, 20:24], st[:, 8:12], AF.Exp)
            psG = self.next_ps()
            for g in range(2):
                self.mm(psG[:, g * 128:(g + 1) * 128], xbcs[:, 2 + g, sl], xbcs[:, 4 + g, sl])
            MT = self.nx(MT_k)
            for g in range(2):
                self.tt(MT[:, 2 * g:2 * g + 2, :], seg[:, 2 * g:2 * g + 2, :],
                        V(psG.h[:, g * 128:(g + 1) * 128].unsqueeze(1).to_broadcast([128, 2, 128]), psG.name), ALU.mult)
            X = self.nx(X_k)
            self.tt(h4(X), h4(xsB), bc4(st[:, 0:4]), ALU.mult)
            psY = self.next_ps()
            for h in range(4):
                self.mm(psY[:, 64 * h:64 * h + 64], MT[:, h, :], X[:, 64 * h:64 * h + 64])
            psO = self.next_ps()
            for g in range(2):
                self.mm(psO[:, 128 * g:128 * g + 128], xbcs[:, 4 + g, sl], prev_bf[:, 128 * g:128 * g + 128])
            yd = self.nx(yd_k)
            self.copy(yd[:], psY[:, 0:256], eng="act")
            y = self.nx(y_k)
            self.tt(h4(y), h4(psO), bc4(st[:, 20:24]), ALU.mult)
            self.tt(y[:], y[:], yd[:], ALU.add)
            t2 = self.nx(t2_k)
            self.tt(h4(t2), h4(xsB), bc4(Dbc[:, 0:4]), ALU.mult, eng="pool")
            self.tt(y[:], y[:], t2[:], ALU.add)
            Xd = self.nx(Xd_k)
            self.tt(h4(Xd), h4(X), bc4(st[:, 12:16]), ALU.mult, eng="pool")
            psS = self.next_ps()
            for g in range(2):
                self.mm(psS[:, 128 * g:128 * g + 128], xsB[:, 256 + 128 * g:256 + 128 * g + 128], Xd[:, 128 * g:128 * g + 128])
            self.tt(h4(prev), h4(prev), bc4(st[:, 16:20]), ALU.mult)
            self.tt(prev[:], prev[:], psS[:, 0:256], ALU.add)
            self.copy(prev_bf[:], prev[:], eng="act")
            sz = self.nx(sz_k)
            self.act(sz[:], zt[:], AF.Silu)
            self.tt(y[:], y[:], sz[:], ALU.mult)
            self.tt(t2[:], y[:], y[:], ALU.mult, eng="pool")
            self.red(st[:, 24:26], V(t2.h[:, :].rearrange("p (g e) -> p g e", g=2), t2.name), ALU.add)
            self.ts(st[:, 24:26], st[:, 24:26], 1.0 / 128, LN_EPS, ALU.mult, ALU.add)
            self.act(st[:, 24:26], st[:, 24:26], AF.Sqrt)
            self.recip(st[:, 24:26], st[:, 24:26])
            y2 = V(y.h[:, :].rearrange("p (g e) -> p g e", g=2), y.name)
            self.tt(y2, y2, V(st.h[:, 24:26].unsqueeze(2).to_broadcast([128, 2, 128]), st.name), ALU.mult)
            yb = self.nx(yb_k)
            self.tt(yb[:], y[:], ng_bc[:], ALU.mult)
            self.store_yT(yb, YT, 3, n, yTs_k)

    def softmax_pv(self, Ssb, nk, Vt, kt0, out, kk, clamp=None, premax=None):
        st = self.nx(kk["st"])
        self.red(st[:, 0:1], (Ssb if premax is None else premax), ALU.max)
        if clamp is not None:
            self.ts(st[:, 0:1], st[:, 0:1], clamp, None, ALU.max)
        self.ts(st[:, 1:2], st[:, 0:1], -1.0, None, ALU.mult)
        P = self.nx(kk["P"])
        self.act(P[:, 0:nk], Ssb, AF.Exp, bias=st[:, 1:2], accum=st[:, 2:3])
        self.ts(st[:, 3:4], st[:, 2:3], 1e-30, None, ALU.max)
        self.recip(st[:, 4:5], st[:, 3:4])
        po = self.next_ps()
        nkt = nk // 128
        for g0 in range(0, nkt, 8):
            gn = min(8, nkt - g0)
            pb = self.next_psb()
            for j in range(gn):
                self.tr(pb[:, j * 128:(j + 1) * 128], P[:, (g0 + j) * 128:(g0 + j + 1) * 128], self.ident[:])
            PT = self.nx(kk["PT"])
            self.pt_cnt += 1
            self.copy(PT[:, 0:gn * 128], pb[:, 0:gn * 128], eng=("act" if self.pt_cnt % 3 else "dve"))
            for j in range(gn):
                self.mm(po[:, 0:64], PT[:, j * 128:(j + 1) * 128], Vt[:, kt0 + g0 + j, :],
                        start=(g0 + j == 0), stop=(g0 + j == nkt - 1))
        self.ts(out, po[:, 0:64], st[:, 4:5], None, ALU.mult)

    def attn_keys(self, pfx):
        S = self.S
        return dict(st=self.rot(pfx + "sst", 2, [128, 8], F32), P=self.rot(pfx + "P", 1, [128, S], BF16),
                    PT=self.rot(pfx + "PTa", 2, [128, 1024], BF16))

    def dsa_setup(self, FMS, TMB, TMF, YT):
        S = self.S
        NT = S // 128
        c = dict(FMS=FMS, YT=YT)
        c["dk"] = self.sb("dk", [128, S], BF16)
        self.dma(c["dk"][:], V(FMS.h[4 * 128:5 * 128, :], FMS.name))
        c["ikr"] = self.sb("ikr", [128, S], BF16)
        self.dma(c["ikr"][:], V(FMS.h[7 * 128:8 * 128, :], FMS.name))
        c["qm"] = self.rot("dqm", 2, [128, 8, 128], BF16)
        c["Vt"] = self.sb("Vt", [128, NT, 64], BF16)
        self.dma(c["Vt"][:], V(TMB.h[:, 512:576].rearrange("(n p) c -> p n c", p=128), TMB.name))
        iw = self.sb("iw", [128, NT, 8], F32)
        self.dma(iw[:], V(TMF.h[:, 0:8].rearrange("(n p) c -> p n c", p=128), TMF.name))
        c["absw"] = self.sb("absw", [128, NT, 8], F32)
        self.act(c["absw"][:], iw[:], AF.Abs, scale=1.0 / 16)
        c["sgn"] = self.sb("sgn", [128, NT, 8], F32)
        self.ts(c["sgn"][:], iw[:], 0.0, 2.0, ALU.is_ge, ALU.mult)
        self.ts(c["sgn"][:], c["sgn"][:], -1.0, None, ALU.add)
        c["I"] = self.rot("I", 1, [128, S], F32)
        c["Ssb"] = self.rot("dSsb", 1, [128, S], F32)
        c["Mb"] = self.rot("dMb", 2, [128, S], BF16)
        c["cm"] = self.rot("dcm", 2, [128, 8], F32)
        c["kk"] = self.attn_keys("d")
        c["q"] = self.rot("dqi", 2, [128, 4, 128], BF16)
        c["tmp"] = self.rot("tmpr", 2, [128, 512], F32)
        c["st"] = self.rot("dst", 2, [128, 16], F32)
        c["Rk"] = self.rot("Rk", 2, [128, 20], F32)
        c["nm"] = self.rot("dnm", 2, [128, 2], F32)
        c["c2"] = self.rot("dc2", 2, [128, 2], F32)
        c["o"] = self.rot("do", 2, [128, 256], F32)
        c["y"] = self.rot("dy", 2, [128, 256], BF16)
        c["yTs"] = self.rot("dyTs", 2, [128, 256], BF16)
        return c

    def dsa_tile(self, c, i):
        FMS = c["FMS"]
        I = self.nx(c["I"])
        Ssb = self.nx(c["Ssb"])
        nk = 128 * (i + 1)
        nkc = (nk + 511) // 512
        q = self.nx(c["q"])
        self.dma(q[:, 0:2, :], self.fm_rows(FMS, 2, 2, i * 128, (i + 1) * 128))
        self.dma(q[:, 2:4, :], self.fm_rows(FMS, 5, 2, i * 128, (i + 1) * 128))
        qm = self.nx(c["qm"])
        hm = V(self.meta.h[:, 13:17].unsqueeze(2).to_broadcast([128, 4, 128]), self.meta.name)
        for cc in range(2):
            self.tt(qm[:, 4 * cc:4 * cc + 4, :], V(q.h[:, 2 + cc, :].unsqueeze(1).to_broadcast([128, 4, 128]), q.name), hm,
                    ALU.mult, eng="pool")
        for kc in range(nkc):
            c0 = kc * 512
            cols = min(512, nk - c0)
            for h in range(8):
                ps = self.next_ps()
                self.mm(ps[:, 0:cols], qm[:, h, :], c["ikr"][:, c0:c0 + cols])
                tmp = self.nx(c["tmp"])
                self.act(tmp[:, 0:cols], ps[:, 0:cols], AF.Relu, scale=c["absw"][:, i, h:h + 1])
                if h == 0:
                    self.ts(I[:, c0:c0 + cols], tmp[:, 0:cols], c["sgn"][:, i, 0:1], None, ALU.mult)
                else:
                    self.stt(I[:, c0:c0 + cols], tmp[:, 0:cols], c["sgn"][:, i, h:h + 1], I[:, c0:c0 + cols], ALU.mult, ALU.add)
        if nk > self.n_keep:
            st = self.nx(c["st"])
            junk = self.nx(c["kk"]["P"])
            self.redabs(st[:, 0:1], I[:, 0:nk])
            self.ts(st[:, 0:1], st[:, 0:1], 1e-20, None, ALU.max)
            self.tt(I[:, nk - 128:nk], I[:, nk - 128:nk], self.cneg30[:], ALU.add)
            Rk = self.nx(c["Rk"])
            self.ts(Rk[:], self.rkc[:], st[:, 0:1], None, ALU.mult)
            self.ts(st[:, 1:2], st[:, 0:1], -1.0, None, ALU.mult)
            n1 = (nk // 2 + 127) // 128 * 128
            n2 = nk - n1
            thr_c = self.n_keep - 0.5 - n2 / 2.0
            for k in range(NBIS):
                nm = self.nx(c["nm"])
                c2 = self.nx(c["c2"])
                self.tt(nm[:, 0:1], st[:, 1:2], Rk[:, k:k + 1], ALU.add)
                self.act(Ssb[:, n1:nk], I[:, n1:nk], AF.Sign, bias=nm[:, 0:1], scale=-1.0, accum=c2[:, 0:1])
                self.ts(junk[:, 0:n1], I[:, 0:n1], nm[:, 0:1], None, ALU.is_ge, ALU.add, accum=st[:, 3:4])
                self.stt(st[:, 4:5], c2[:, 0:1], -0.5, st[:, 3:4], ALU.mult, ALU.add)
                self.ts(st[:, 4:5], st[:, 4:5], thr_c, None, ALU.is_ge)
                self.stt(st[:, 1:2], st[:, 4:5], Rk[:, k:k + 1], st[:, 1:2], ALU.mult, ALU.add)
            Mb = self.nx(c["Mb"])
            self.ts(Mb[:, 0:nk], I[:, 0:nk], st[:, 1:2], 8000.0, ALU.is_ge, ALU.mult)
        else:
            Mb = self.nx(c["Mb"])
            self.memset(Mb[:, 0:nk], 8000.0)
            self.tt(Mb[:, nk - 128:nk], Mb[:, nk - 128:nk], self.cnegb[:], ALU.add)
        o = self.nx(c["o"])
        for h in range(4):
            base = 64 * (h % 2)
            cq = h // 2
            cm = self.nx(c["cm"])
            for kc in range(nkc):
                c0 = kc * 512
                cols = min(512, nk - c0)
                ps = self.next_ps()
                self.mm(ps[:, 0:cols], q[base:base + 64, cq, :], c["dk"][base:base + 64, c0:c0 + cols], start=True, stop=False)
                self.mm(ps[:, 0:cols], self.ident[:], Mb[:, c0:c0 + cols], start=False, stop=True)
                self.ts(Ssb[:, c0:c0 + cols], ps[:, 0:cols], 0.125, None, ALU.mult, ALU.max, accum=cm[:, kc:kc + 1])
            self.softmax_pv(Ssb[:, 0:nk], nk, c["Vt"], 0, o[:, 64 * h:64 * h + 64], c["kk"], premax=cm[:, 0:nkc])
        y = self.nx(c["y"])
        self.copy(y[:], o[:], eng="act")
        self.store_yT(y, c["YT"], 1, i, c["yTs"])

    def nsa_setup(self, FMS, TMB, TMF, cmp_w1, cmp_w2, cmp_pos, YT):
        S = self.S
        NT = S // 128
        NB = self.NB
        NCP = self.NCP
        NC = (S - 32) // 16 + 1
        NCT = NCP // 128
        c = dict(FMS=FMS, YT=YT)
        c["ksT"] = self.sb("ksT", [128, S], BF16)
        self.dma(c["ksT"][:], V(FMS.h[11 * 128:12 * 128, :], FMS.name))
        c["kwT"] = self.sb("kwT", [128, S], BF16)
        self.dma(c["kwT"][:], V(FMS.h[12 * 128:13 * 128, :], FMS.name))
        c["Vs"] = self.sb("Vs", [128, NT, 64], BF16)
        self.dma(c["Vs"][:], V(TMB.h[:, 576:640].rearrange("(n p) c -> p n c", p=128), TMB.name))
        c["Vw"] = self.sb("Vw", [128, NT, 64], BF16)
        self.dma(c["Vw"][:], V(TMB.h[:, 640:704].rearrange("(n p) c -> p n c", p=128), TMB.name))
        c["ngt"] = self.sb("ngt", [128, NT, 12], F32)
        self.dma(c["ngt"][:], V(TMF.h[:, 8:20].rearrange("(n p) c -> p n c", p=128), TMF.name))
        kcmp = self.sb("kcmp", [128, NCP], BF16)
        vcmp = self.sb("vcmp", [128, NCT, 64], BF16)
        c["kcmp"] = kcmp
        c["vcmp"] = vcmp
        c["Ssb"] = self.sb("nSsb", [128, S], F32)
        c["Sw"] = self.sb("Sw", [128, 640], F32)
        c["kk"] = self.attn_keys("n")
        save = self.sb_cur
        srcT = self.sb("srcT", [128, S], BF16)
        w1 = self.sb("w1", [64, 32, 64], BF16)
        w2 = self.sb("w2", [64, 128], BF16)
        posT = self.sb("posT", [64, 32], F32)
        posb = self.sb("posb", [64, 32], BF16)
        cst = self.sb("cst", [64, 1], F32)
        u = self.sb("u", [64, NCP], F32)
        u2 = self.sb("u2", [64, NCP], F32)
        gl = self.sb("gl", [64, NCP], BF16)
        for i in range(2):
            self.dma(srcT[:], V(FMS.h[(10 + 3 * i) * 128:(11 + 3 * i) * 128, :], FMS.name))
            self.dma(w1[:], V(cmp_w1.h[i].rearrange("(l d) f -> d l f", d=64), cmp_w1.name), eng="pool")
            self.dma(w2[:, 0:64], V(cmp_w2.h[i], cmp_w2.name), eng="pool")
            self.dma(w2[:, 64:128], V(cmp_w2.h[i], cmp_w2.name), eng="pool")
            self.dma_s(posT[:], V(cmp_pos.h[i].rearrange("l d -> d l"), cmp_pos.name))
            self.copy(posb[:], posT[:])
            psc = self.next_ps()
            for l in range(32):
                self.mm(psc[0:64, 0:1], w1[:, l, :], posb[:, l:l + 1], start=(l == 0), stop=(l == 31))
            self.copy(cst[:], psc[0:64, 0:1])
            psh = self.next_ps()
            for l in range(32):
                self.mm(psh[0:64, 0:NC], w1[:, l, :], srcT[0:64, l:l + 16 * (NC - 1) + 1:16], start=(l == 0), stop=(l == 31))
            self.memset(u[:], 0.0)
            self.act(u[:, 0:NC], psh[0:64, 0:NC], AF.Identity, bias=cst[:, 0:1])
            self.tt(u2[:], u[:], u[:], ALU.mult)
            self.tt(u2[:], u2[:], u[:], ALU.mult)
            self.stt(u2[:], u2[:], 0.044715, u[:], ALU.mult, ALU.add)
            self.act(u2[:], u2[:], AF.Tanh, scale=0.7978845608028654)
            self.ts(u2[:], u2[:], 1.0, 0.5, ALU.add, ALU.mult)
            self.tt(gl[:], u2[:], u[:], ALU.mult)
            if i == 0:
                pso = self.next_ps()
                self.mm(pso[:, 0:NCP], w2[:, :], gl[:, :])
                self.copy(kcmp[:], pso[:, 0:NCP])
            else:
                for ct in range(NCT):
                    pso = self.next_ps()
                    self.mm(pso[:, 0:64], gl[:, ct * 128:(ct + 1) * 128], w2[:, 0:64])
                    self.copy(vcmp[:, ct, :], pso[:, 0:64])
        self.sb_cur = save
        self.P.barrier()
        c["q"] = self.rot("nqi", 2, [128, 2, 128], BF16)
        c["Mn"] = self.rot("nMn", 1, [128, S], BF16)
        for nm, shp, dt_ in (("vis", [128, NCP], F32), ("pns", [128, NCP], F32), ("pn", [128, NCP], F32), ("Sc", [128, NCP], F32),
                             ("Pc", [128, NCP], F32), ("pnb", [128, NCP], BF16), ("PTc", [128, NCP], BF16), ("pnT", [128, NCP], F32),
                             ("cst2", [128, 8], F32), ("am", [128, NB], F32), ("imp", [128, NB], F32), ("imp2", [128, NB], F32), ("m8", [128, 16], F32),
                             ("selm", [128, NB], F32), ("cm", [128, 8], F32), ("oc", [128, 256], F32), ("os", [128, 256], F32), ("ow", [128, 256], F32),
                             ("gs", [128, 12], F32), ("o", [128, 256], F32), ("y", [128, 256], BF16), ("yTs", [128, 256], BF16)):
            c[nm] = self.rot("n" + nm, (1 if nm in ("pnT", "Pc", "vis", "imp2") else 2), shp, dt_)
        return c

    def nsa_tile(self, c, i):
        NB = self.NB
        NCP = self.NCP
        NCT = NCP // 128
        FMS = c["FMS"]
        Ssb = c["Ssb"]
        Sw = c["Sw"]
        kcmp = c["kcmp"]
        vcmp = c["vcmp"]

        def h4(t_):
            return V(t_.h[:, 0:256].rearrange("p (h e) -> p h e", h=4), t_.name)

        nk = 128 * (i + 1)
        nkc = (nk + 511) // 512
        nq = self.nx(c["q"])
        self.dma(nq[:], self.fm_rows(FMS, 8, 2, i * 128, (i + 1) * 128))
        vis = self.nx(c["vis"])
        self.memset(vis[:], 0.0)
        self.aselect(vis[:], vis[:], [[-16, NCP]], ALU.is_ge, -1000.0, 128 * i - 31, 1)
        pns = self.nx(c["pns"])
        oc = self.nx(c["oc"])
        osl = self.nx(c["os"])
        ow = self.nx(c["ow"])
        for h in range(4):
            base = 64 * (h % 2)
            cq = h // 2
            ps = self.next_ps()
            self.mm(ps[:, 0:NCP], nq[base:base + 64, cq, :], kcmp[base:base + 64, :])
            Sc = self.nx(c["Sc"])
            self.stt(Sc[:], ps[:, 0:NCP], 0.125, vis[:], ALU.mult, ALU.add)
            st = self.nx(c["cst2"])
            self.red(st[:, 0:1], Sc[:], ALU.max)
            self.ts(st[:, 0:1], st[:, 0:1], -500.0, -1.0, ALU.max, ALU.mult)
            Pc = self.nx(c["Pc"])
            self.act(Pc[:], Sc[:], AF.Exp, bias=st[:, 0:1], accum=st[:, 1:2])
            self.ts(st[:, 2:3], st[:, 1:2], 1e-30, None, ALU.max)
            self.recip(st[:, 3:4], st[:, 2:3])
            pn = pns if h == 0 else self.nx(c["pn"])
            self.ts(pn[:], Pc[:], st[:, 3:4], None, ALU.mult)
            pnb = self.nx(c["pnb"])
            self.copy(pnb[:], pn[:], eng="act")
            if h > 0:
                self.tt(pns[:], pns[:], pn[:], ALU.add, eng="pool")
            pb = self.next_psb()
            for ct in range(NCT):
                self.tr(pb[:, ct * 128:(ct + 1) * 128], pnb[:, ct * 128:(ct + 1) * 128], self.ident[:])
            PTc = self.nx(c["PTc"])
            self.copy(PTc[:], pb[:, 0:NCP], eng="act")
            po = self.next_ps()
            for ct in range(NCT):
                self.mm(po[:, 0:64], PTc[:, ct * 128:(ct + 1) * 128], vcmp[:, ct, :], start=(ct == 0), stop=(ct == NCT - 1))
            self.copy(oc[:, 64 * h:64 * h + 64], po[:, 0:64], eng="act")
        selm = self.nx(c["selm"])
        if NB > 16:
            pf = self.next_ps()
            for ct in range(NCT):
                self.tr(pf[:, ct * 128:(ct + 1) * 128], pns[:, ct * 128:(ct + 1) * 128], self.ident_f[:])
            pnT = self.nx(c["pnT"])
            self.copy(pnT[:], pf[:, 0:NCP], eng="act")
            pi = self.next_ps()
            for ct in range(NCT):
                self.mm(pi[:, 0:NB], pnT[:, ct * 128:(ct + 1) * 128], self.ovl[:, ct, :], start=(ct == 0), stop=(ct == NCT - 1))
            am = self.nx(c["am"])
            self.memset(am[:], 0.0)
            for half in range(2):
                cur = 2 * i + half
                r0 = 64 * half
                v_ = am[r0:r0 + 64, :]
                self.aselect(v_, v_, [[-1, NB]], ALU.is_ge, -1e30, cur, 0)
                self.memset(am[r0:r0 + 64, 0:1], 1e30)
                self.memset(am[r0:r0 + 64, cur:cur + 1], 1e30)
                if cur >= 1:
                    self.memset(am[r0:r0 + 64, cur - 1:cur], 1e30)
            imp = self.nx(c["imp"])
            self.tt(imp[:], pi[:, 0:NB], am[:], ALU.add)
            m8 = self.nx(c["m8"])
            self.vmax(m8[:, 0:8], imp[:])
            imp2 = self.nx(c["imp2"])
            self.match_replace(imp2[:], m8[:, 0:8], imp[:], -3.0e38)
            self.vmax(m8[:, 8:16], imp2[:])
            self.ts(selm[:], imp[:], m8[:, 15:16], 8000.0, ALU.is_ge, ALU.mult)
        else:
            self.memset(selm[:], 8000.0)
        Mn = self.nx(c["Mn"])
        nbk = nk // 64
        self.act(V(Mn.h[:, 0:nk].rearrange("p (b e) -> p b e", e=64), Mn.name),
                 V(selm.h[:, 0:nbk].unsqueeze(2).to_broadcast([128, nbk, 64]), selm.name), AF.Copy)
        self.tt(Mn[:, nk - 128:nk], Mn[:, nk - 128:nk], self.cnegb[:], ALU.add)
        for h in range(4):
            base = 64 * (h % 2)
            cq = h // 2
            cm = self.nx(c["cm"])
            for kc in range(nkc):
                c0 = kc * 512
                cols = min(512, nk - c0)
                ps = self.next_ps()
                self.mm(ps[:, 0:cols], nq[base:base + 64, cq, :], c["ksT"][base:base + 64, c0:c0 + cols], start=True, stop=False)
                self.mm(ps[:, 0:cols], self.ident[:], Mn[:, c0:c0 + cols], start=False, stop=True)
                self.ts(Ssb[:, c0:c0 + cols], ps[:, 0:cols], 0.125, None, ALU.mult, ALU.max, accum=cm[:, kc:kc + 1])
            self.softmax_pv(Ssb[:, 0:nk], nk, c["Vs"], 0, osl[:, 64 * h:64 * h + 64], c["kk"], premax=cm[:, 0:nkc])
        k0 = max(0, i * 128 - 512)
        nkw = nk - k0
        boff = 640 - nkw
        for h in range(4):
            base = 64 * (h % 2)
            cq = h // 2
            cm = self.nx(c["cm"])
            nwc = 0
            for c0 in range(0, nkw, 512):
                cols = min(512, nkw - c0)
                ps = self.next_ps()
                self.mm(ps[:, 0:cols], nq[base:base + 64, cq, :], c["kwT"][base:base + 64, k0 + c0:k0 + c0 + cols], start=True, stop=False)
                self.mm(ps[:, 0:cols], self.ident[:], self.bandb[:, boff + c0:boff + c0 + cols], start=False, stop=True)
                self.ts(Sw[:, c0:c0 + cols], ps[:, 0:cols], 0.125, None, ALU.mult, ALU.max, accum=cm[:, nwc:nwc + 1])
                nwc += 1
            self.softmax_pv(Sw[:, 0:nkw], nkw, c["Vw"], k0 // 128, ow[:, 64 * h:64 * h + 64], c["kk"], premax=cm[:, 0:nwc])
        gs = self.nx(c["gs"])
        self.act(gs[:], c["ngt"][:, i, :], AF.Sigmoid)
        o = self.nx(c["o"])

        def gbc(j):
            return V(gs.h[:, j:12:3].unsqueeze(2).to_broadcast([128, 4, 64]), gs.name)
        self.tt(h4(o), h4(oc), gbc(0), ALU.mult)
        self.tt(h4(osl), h4(osl), gbc(1), ALU.mult)
        self.tt(o[:], o[:], osl[:], ALU.add)
        self.tt(h4(ow), h4(ow), gbc(2), ALU.mult)
        self.tt(o[:], o[:], ow[:], ALU.add)
        y = self.nx(c["y"])
        self.copy(y[:], o[:], eng="act")
        self.store_yT(y, c["YT"], 2, i, c["yTs"])

    def phase_dsa_nsa(self, FMS, TMB, TMF, cmp_w1, cmp_w2, cmp_pos, YT):
        S = self.S
        NT = S // 128
        self.phase_begin()
        self.cnegb = self.sb("cnegb", [128, 128], BF16)
        self.ts(self.cnegb[:], self.cneg2k[:], 8.0, None, ALU.mult)
        self.bandb = self.sb("bandb", [128, 640], BF16)
        self.ts(self.bandb[:], self.band[:], 8.0, None, ALU.mult)
        cd = self.dsa_setup(FMS, TMB, TMF, YT)
        cn = self.nsa_setup(FMS, TMB, TMF, cmp_w1, cmp_w2, cmp_pos, YT)
        P = self.P
        for i in range(NT):
            self.ps_set = (0, 3)
            self.psb_set = (0, 1)
            P.capture = []
            self.dsa_tile(cd, i)
            A = P.capture
            self.ps_set = (3, 2)
            self.psb_set = (1, 1)
            P.capture = []
            self.nsa_tile(cn, i)
            B = P.capture
            P.capture = None
            self.ps_set = (0, 5)
            self.psb_set = (0, 2)
            ia = ib = 0
            na, nb = len(A), len(B)
            while ia < na or ib < nb:
                if ib >= nb or (ia < na and ia * nb <= ib * na):
                    P.add(*A[ia][0], **A[ia][1])
                    ia += 1
                else:
                    P.add(*B[ib][0], **B[ib][1])
                    ib += 1

    def phase_merge(self, x_in, xT_in, w_in_l, w_branch, w_out, ln_g, ln_b, YT, x_out, xT_out):
        S = self.S
        self.phase_begin()
        wg = self.load_w("wg", lambda k: V(w_in_l.h[k * 128:(k + 1) * 128, 3128:7224], w_in_l.name), NKC, 4096)
        wb = self.sb("wb", [128, 4, 2, 1024], BF16)
        for n in range(4):
            for kk_ in range(2):
                self.dma(wb[:, n, kk_, :], V(w_branch.h[n, kk_ * 128:(kk_ + 1) * 128, :], w_branch.name), eng="pool")
        wo = self.load_w("wo", lambda k: V(w_out.h[k * 128:(k + 1) * 128, :], w_out.name), NKC, D)
        g_bc, b_bc, scr = self.ln_setup(ln_g, ln_b)
        xt = self.sb("xT", [128, NKC, 512], BF16)
        yt = self.sb("yT", [128, 8, 512], BF16)
        mTs = [self.sb("mT%d" % i_, [128, 8, 512], BF16) for i_ in range(2)]
        acc_k = self.rot("acc", 3, [128, 512], F32)
        sg_k = self.rot("sg", 2, [128, 512], F32)
        tmp_k = self.rot("tmp", 2, [128, 512], F32)
        xr = [self.sb("xr%d" % i, [128, D], F32) for i in range(2)]
        rqs = [self.sb("r%d" % i, [128, D], F32) for i in range(2)]
        ntile = S // 512

        def load_xt(t_):
            ss_ = slice(t_ * 512, (t_ + 1) * 512)
            self.dma(xt[:], V(xT_in.h.rearrange("(k p) s -> p k s", p=128)[:, :, ss_], xT_in.name))
            self.dma(yt[:], V(YT.h[:, ss_].rearrange("(c p) s -> p c s", p=128), YT.name))

        def load_xq(idx):
            self.dma(xr[idx % 2][:], x_in[idx * 128:(idx + 1) * 128, :])
        load_xt(0)
        load_xq(0)
        for t in range(ntile):
            ss = slice(t * 512, (t + 1) * 512)
            mT = mTs[t % 2]
            for dc in range(8):
                acc = self.nx(acc_k)
                for n in range(4):
                    pg = self.next_ps()
                    for k in range(NKC):
                        self.mm(pg[:], wg[:, k, n * 1024 + dc * 128:n * 1024 + (dc + 1) * 128], xt[:, k, :],
                                start=(k == 0), stop=(k == NKC - 1))
                    pp = self.next_ps()
                    for k2 in range(2):
                        self.mm(pp[:], wb[:, n, k2, dc * 128:(dc + 1) * 128], yt[:, 2 * n + k2, :], start=(k2 == 0), stop=(k2 == 1))
                    sg = self.nx(sg_k)
                    self.act(sg[:], pg[:], AF.Sigmoid)
                    if n == 0:
                        self.tt(acc[:], sg[:], pp[:], ALU.mult)
                    else:
                        tmp = self.nx(tmp_k)
                        self.tt(tmp[:], sg[:], pp[:], ALU.mult)
                        self.tt(acc[:], acc[:], tmp[:], ALU.add, eng="pool")
                self.copy(mT[:, dc, :], acc[:], eng="act")
            if t + 1 < ntile:
                load_xt(t + 1)
            for q in range(4):
                t0 = t * 512 + q * 128
                xq = xr[q % 2]
                rq = rqs[q % 2]
                if t * 4 + q + 1 < ntile * 4:
                    load_xq(t * 4 + q + 1)
                for half in range(2):
                    hs = slice(half * 512, (half + 1) * 512)
                    pd = self.next_ps()
                    for dc in range(8):
                        self.mm(pd[:], mT[:, dc, q * 128:(q + 1) * 128], wo[:, dc, hs], start=(dc == 0), stop=(dc == 7))
                    self.act(xq[:, hs], xq[:, hs], AF.Copy, scale=ALPHA)
                    self.stt(rq[:, hs], pd[:], 1.0, xq[:, hs], ALU.mult, ALU.add)
                self.finish_tile(rq, g_bc, b_bc, x_out, xT_out, t0, scr)

    def phase_xattn(self, x_in, xT_in, mem, wq_d, wkv_d, wo_d, ln_g, ln_b, x_out, xT_out):
        S = self.S
        self.phase_begin()
        wkv = self.load_w("wkv", lambda k: V(wkv_d.h[k * 128:(k + 1) * 128, :], wkv_d.name), NKC, 2 * D)
        wq = self.load_w("wq", lambda k: V(wq_d.h[k * 128:(k + 1) * 128, :], wq_d.name), NKC, D)
        wo = self.load_w("wo", lambda k: V(wo_d.h[k * 128:(k + 1) * 128, :], wo_d.name), NKC, D)
        g_bc, b_bc, scr = self.ln_setup(ln_g, ln_b)
        memT = self.sb("memT", [128, 8, 256], BF16)
        mr = self.sb("mr", [128, D], F32)
        mb = self.sb("mb", [128, D], BF16)
        for mt in range(2):
            self.dma(mr[:], mem[mt * 128:(mt + 1) * 128, :])
            self.copy(mb[:], mr[:], eng="act")
            pb = self.next_psb()
            for k in range(8):
                self.tr(pb[:, k * 128:(k + 1) * 128], mb[:, k * 128:(k + 1) * 128], self.ident[:])
            self.copy(memT[:, :, mt * 128:(mt + 1) * 128], V(pb.h[:, :].rearrange("p (k t) -> p k t", k=8), pb.name))
        KT = self.sb("KT", [128, 8, 256], BF16)
        for c in range(8):
            ps = self.next_ps()
            for k in range(NKC):
                self.mm(ps[:, 0:256], wkv[:, k, c * 128:(c + 1) * 128], memT[:, k, :], start=(k == 0), stop=(k == NKC - 1))
            self.copy(KT[:, c, :], ps[:, 0:256], eng=("act" if c % 2 else "dve"))
        Vm = self.sb("Vm", [128, 2, D], BF16)
        for mt in range(2):
            for half in range(2):
                ps = self.next_ps()
                for k in range(NKC):
                    self.mm(ps[:], memT[:, k, mt * 128:(mt + 1) * 128], wkv[:, k, D + half * 512:D + (half + 1) * 512],
                            start=(k == 0), stop=(k == NKC - 1))
                self.copy(Vm[:, mt, half * 512:(half + 1) * 512], ps[:], eng=("act" if half else "dve"))
        xt = self.sb("xT", [128, NKC, 512], BF16)
        qT = self.sb("qT", [128, 8, 512], BF16)
        Pf_k = self.rot("Pf", 2, [128, 4, 256], F32)
        Pb_k = self.rot("Pb", 2, [128, 4, 256], BF16)
        PT_k = self.rot("PTx", 2, [128, 8, 128], BF16)
        oT_k = self.rot("oT", 2, [128, 8, 128], BF16)
        st_k = self.rot("xst", 2, [128, 16], F32)
        xr = [self.sb("xr%d" % i, [128, D], F32) for i in range(2)]
        rqs = [self.sb("r%d" % i, [128, D], F32) for i in range(2)]
        SC = 1.0 / 16
        ntile = S // 512

        def load_xt(t_):
            ss_ = slice(t_ * 512, (t_ + 1) * 512)
            self.dma(xt[:], V(xT_in.h.rearrange("(k p) s -> p k s", p=128)[:, :, ss_], xT_in.name))

        def load_xq(idx):
            self.dma(xr[idx % 2][:], x_in[idx * 128:(idx + 1) * 128, :])
        load_xt(0)
        load_xq(0)
        for t in range(ntile):
            ss = slice(t * 512, (t + 1) * 512)
            for c in range(8):
                ps = self.next_ps()
                for k in range(NKC):
                    self.mm(ps[:], wq[:, k, c * 128:(c + 1) * 128], xt[:, k, :], start=(k == 0), stop=(k == NKC - 1))
                self.copy(qT[:, c, :], ps[:], eng=("act" if c % 2 else "dve"))
            if t + 1 < ntile:
                load_xt(t + 1)
            for q in range(4):
                t0 = t * 512 + q * 128
                tq = slice(q * 128, (q + 1) * 128)
                if t * 4 + q + 1 < ntile * 4:
                    load_xq(t * 4 + q + 1)
                pss = [self.next_ps(), self.next_ps()]
                st = self.nx(st_k)
                Pf = self.nx(Pf_k)
                for h in range(4):
                    pv = pss[h // 2][:, (h % 2) * 256:(h % 2) * 256 + 256]
                    for cc in range(2):
                        self.mm(pv, qT[:, 2 * h + cc, tq], KT[:, 2 * h + cc, :], start=(cc == 0), stop=(cc == 1))
                    self.red(st[:, h:h + 1], pv, ALU.max)
                    self.ts(st[:, 4 + h:5 + h], st[:, h:h + 1], -SC, None, ALU.mult)
                    self.act(Pf[:, h, :], pv, AF.Exp, bias=st[:, 4 + h:5 + h], scale=SC, accum=st[:, 8 + h:9 + h])
                self.recip(st[:, 12:16], st[:, 8:12])
                Pb = self.nx(Pb_k)
                self.tt(Pb[:], Pf[:], V(st.h[:, 12:16].unsqueeze(2).to_broadcast([128, 4, 256]), st.name), ALU.mult)
                pb = self.next_psb()
                for h in range(4):
                    for mc in range(2):
                        j = 2 * h + mc
                        self.tr(pb[:, j * 128:(j + 1) * 128], Pb[:, h, mc * 128:(mc + 1) * 128], self.ident[:])
                PT = self.nx(PT_k)
                self.copy(PT[:], V(pb.h[:, :].rearrange("p (j t) -> p j t", j=8), pb.name))
                oT = self.nx(oT_k)
                pso = [self.next_ps(), self.next_ps()]
                for h in range(4):
                    for dc in range(2):
                        j = 2 * h + dc
                        pv = pso[j // 4][:, (j % 4) * 128:(j % 4) * 128 + 128]
                        for mc in range(2):
                            self.mm(pv, Vm[:, mc, h * 256 + dc * 128:h * 256 + (dc + 1) * 128], PT[:, 2 * h + mc, :],
                                    start=(mc == 0), stop=(mc == 1))
                for j4 in range(2):
                    self.copy(oT[:, 4 * j4:4 * j4 + 4, :], V(pso[j4].h[:, :].rearrange("p (j t) -> p j t", j=4), pso[j4].name),
                              eng=("act" if j4 else "dve"))
                xq = xr[q % 2]
                rq = rqs[q % 2]
                for half in range(2):
                    hs = slice(half * 512, (half + 1) * 512)
                    pd = self.next_ps()
                    for c in range(8):
                        self.mm(pd[:], oT[:, c, :], wo[:, c, hs], start=(c == 0), stop=(c == 7))
                    self.act(xq[:, hs], xq[:, hs], AF.Copy, scale=ALPHA)
                    self.stt(rq[:, hs], pd[:], 1.0, xq[:, hs], ALU.mult, ALU.add)
                self.finish_tile(rq, g_bc, b_bc, x_out, xT_out, t0, scr)


OFF = dict(r_q=0, r_k=128, r_v=256, r_g=512, d_q=768, d_k=1024, d_v=1088, i_q=1152, i_k=1408, i_w=1440,
           n_q=1448, n_kc=1704, n_vc=1768, n_ks=1832, n_vs=1896, n_kw=1960, n_vw=2024, n_g=2088,
           s_z=2100, s_xbc=2356, s_dt=3124, br_g=3128)


def _partner(i, headdim, rot):
    half = rot // 2
    j = i % headdim
    b = i - j
    if j < half:
        return b + j + half
    if j < rot:
        return b + j - half
    return i


def build_colidx():
    cols = []

    def roped(name, width, headdim, rot, lo=0, rep=1):
        loc = []
        for r in range(rep):
            loc += list(range(lo, lo + width))
        assert len(loc) == 128
        a = [OFF[name] + i for i in loc]
        b = [OFF[name] + _partner(i, headdim, rot) for i in loc]
        cols.extend(a)
        cols.extend(b)

    roped("r_q", 128, 32, 32)
    roped("r_k", 128, 32, 32)
    roped("d_q", 128, 64, 16, 0)
    roped("d_q", 128, 64, 16, 128)
    roped("d_k", 64, 64, 16, 0, 2)
    roped("i_q", 128, 32, 8, 0)
    roped("i_q", 128, 32, 8, 128)
    roped("i_k", 32, 32, 8, 0, 4)
    roped("n_q", 128, 64, 16, 0)
    roped("n_q", 128, 64, 16, 128)
    roped("n_kc", 64, 64, 16, 0, 2)
    roped("n_ks", 64, 64, 16, 0, 2)
    roped("n_kw", 64, 64, 16, 0, 2)
    cols.extend([OFF["n_vc"] + i for i in range(64)] * 2)
    cols.extend([OFF["s_xbc"] + i for i in range(768)])
    for name, w in (("r_v", 256), ("r_g", 256), ("d_v", 64), ("n_vs", 64), ("n_vw", 64), ("s_z", 256),
                    ("i_w", 8), ("n_g", 12), ("s_dt", 4)):
        cols.extend([OFF[name] + i for i in range(w)])
    return np.asarray(cols, dtype=np.int64)


ROPED_TABLES = [0, 1, 2, 2, 2, 3, 3, 3, 2, 2, 2, 2, 2]
TM0 = (2 * len(ROPED_TABLES) + 7) * 128
NCOL2 = TM0 + 984
NBIS = 17
RET_LNG = [math.log1p(-2.0 ** (-5 - h)) for h in range(4)]


def host_consts(S):
    meta = np.zeros((128, 32), np.float32)

    def fill(t, headdim, rot, theta, scale):
        half = rot // 2
        inv = np.power(np.float32(theta), (-2.0 * np.arange(half, dtype=np.float32) / np.float32(rot)).astype(np.float32)).astype(np.float32)
        for p in range(128):
            i = p % headdim
            if i < rot:
                meta[p, t] = inv[i % half]
                meta[p, 4 + t] = scale
                meta[p, 8 + t] = -scale if i < half else scale
            else:
                meta[p, t] = 0.0
                meta[p, 4 + t] = 1.0
                meta[p, 8 + t] = 0.0

    fill(0, 32, 32, 10000.0, 1.0)
    fill(1, 32, 32, 10000.0, 32.0 ** -0.5)
    fill(2, 64, 16, 500000.0, 1.0)
    fill(3, 32, 8, 500000.0, 1.0)
    for p in range(128):
        meta[p, 12] = RET_LNG[p // 32]
        meta[p, 13 + p // 32] = 1.0
    bdm = np.zeros((128, 256), np.float32)
    for p in range(128):
        bdm[p, 64 * (p // 32):64 * (p // 32) + 64] = 1.0
    NC = (S - 32) // 16 + 1
    NCP = (NC + 127) // 128 * 128
    NB = S // 64
    ovl = np.zeros((NCP, NB), np.float32)
    for c in range(NC):
        for j in range(NB):
            ovl[c, j] = max(min(16 * c + 32, 64 * j + 64) - max(16 * c, 64 * j), 0) / 32.0
    return meta, bdm, ovl, NB, NCP


STAGES = ["ffn1", "inproj", "ret", "ssd", "dsa", "nsa", "merge", "xattn", "ffn2"]


def build(S, depth=DEPTH, stop_after=None):
    kb = KB(S, depth, stop_after)
    meta_np, bdm_np, ovl_np, NB, NCP = host_consts(S)
    kb.n_keep = min(256, S // 4)
    EI = "ExternalInput"
    x = kb.dram("x", [S, D], F32, kind=EI)
    mem = kb.dram("mem", [N_MEM, D], F32, kind=EI)
    ln_g = kb.dram("ln_g", [DEPTH, 4, D], F32, kind=EI)
    ln_b = kb.dram("ln_b", [DEPTH, 4, D], F32, kind=EI)
    f1gu = kb.dram("ffn1_w_gu", [DEPTH, D, 2 * DFF], F32, kind=EI)
    f1dn = kb.dram("ffn1_w_down", [DEPTH, DFF, D], F32, kind=EI)
    w_in = kb.dram("w_in", [DEPTH, D, 7224], F32, kind=EI)
    w2 = kb.dram("w2", [DEPTH, D, NCOL2], F32, kind=EI)
    cmp_w1 = kb.dram("cmp_w1", [DEPTH, 2, 2048, 64], F32, kind=EI)
    cmp_w2 = kb.dram("cmp_w2", [DEPTH, 2, 64, 64], F32, kind=EI)
    cmp_pos = kb.dram("cmp_pos", [DEPTH, 2, 32, 64], F32, kind=EI)
    conv_w = kb.dram("conv_w", [DEPTH, 4, 768], F32, kind=EI)
    conv_b = kb.dram("conv_b", [DEPTH, 768], F32, kind=EI)
    dt_bias = kb.dram("dt_bias", [DEPTH, 4], F32, kind=EI)
    a_log = kb.dram("a_log", [DEPTH, 4], F32, kind=EI)
    d_skip = kb.dram("d_skip", [DEPTH, 4], F32, kind=EI)
    norm_g = kb.dram("ssm_norm_g", [DEPTH, 256], F32, kind=EI)
    w_branch = kb.dram("w_branch", [DEPTH, 4, 256, D], F32, kind=EI)
    w_out = kb.dram("w_out", [DEPTH, D, D], F32, kind=EI)
    xwq = kb.dram("xattn_wq", [DEPTH, D, D], F32, kind=EI)
    xwkv = kb.dram("xattn_wkv", [DEPTH, D, 2 * D], F32, kind=EI)
    xwo = kb.dram("xattn_wo", [DEPTH, D, D], F32, kind=EI)
    f2gu = kb.dram("ffn2_w_gu", [DEPTH, D, 2 * DFF], F32, kind=EI)
    f2dn = kb.dram("ffn2_w_down", [DEPTH, DFF, D], F32, kind=EI)
    meta = kb.dram("meta", [128, 32], F32, kind=EI)
    bdm = kb.dram("bdm", [128, 256], F32, kind=EI)
    ovl = kb.dram("ovl", [NCP, NB], F32, kind=EI)
    out = kb.dram("out", [S, D], F32)
    xTa = kb.dram("xTa", [D, S], BF16)
    xTb = kb.dram("xTb", [D, S], BF16)
    xa = kb.dram("xa", [S, D], F32)
    xb2 = kb.dram("xb2", [S, D], F32)
    FMS = kb.dram("FMS", [20 * 128, S], BF16)
    TMB = kb.dram("TMB", [S, 960], BF16)
    TMF = kb.dram("TMF", [S, 24], F32)
    YT = kb.dram("YT", [1024, S], BF16)
    ROPE = kb.dram("ROPE", [4, 2, 128, S], F32)
    kb.setup()
    kb.setup_consts(meta, bdm, ovl, NB, NCP)
    kb.phase_rope(ROPE)
    kb.phase_transpose_in(x, xTa)

    def L(t, *idx):
        return T(t.h[idx], t.name, True)

    done = False
    xin = x
    for l in range(depth):
        last = (l == depth - 1)

        def stop(name):
            return stop_after == (l, name)
        kb.phase_ffn(xin, xTa, L(f1gu, l), L(f1dn, l), L(ln_g, l, 0), L(ln_b, l, 0), xa, xTb)
        if stop("ffn1"):
            break
        kb.phase_inproj(xTb, L(w2, l), ROPE, FMS, TMB, TMF)
        if stop("inproj"):
            break
        kb.phase_ret(FMS, TMB, YT)
        if stop("ret"):
            break
        kb.phase_ssd(FMS, TMB, TMF, L(conv_w, l), L(conv_b, l), L(dt_bias, l), L(a_log, l), L(d_skip, l), L(norm_g, l), YT)
        if stop("ssd"):
            break
        kb.phase_dsa_nsa(FMS, TMB, TMF, L(cmp_w1, l), L(cmp_w2, l), L(cmp_pos, l), YT)
        if stop("nsa") or stop("dsa"):
            break
        kb.phase_merge(xa, xTb, L(w_in, l), L(w_branch, l), L(w_out, l), L(ln_g, l, 1), L(ln_b, l, 1), YT, xb2, xTa)
        if stop("merge"):
            break
        kb.phase_xattn(xb2, xTa, mem, L(xwq, l), L(xwkv, l), L(xwo, l), L(ln_g, l, 2), L(ln_b, l, 2), xa, xTb)
        if stop("xattn"):
            break
        kb.phase_ffn(xa, xTb, L(f2gu, l), L(f2dn, l), L(ln_g, l, 3), L(ln_b, l, 3), out if last else xb2, None if last else xTa)
        if stop("ffn2"):
            break
        xin = xb2
    kb.flush_pending()
    st = kb.P.emit()
    kb.stats = st
    return kb


def make_in_maps(inputs, S, ncores):
    meta_np, bdm_np, ovl_np, NB, NCP = host_consts(S)
    colidx = build_colidx()
    w_in = np.asarray(inputs["w_in"], dtype=np.float32)
    w2 = np.ascontiguousarray(w_in[:, :, colidx])
    shared = {k: np.ascontiguousarray(np.asarray(v, dtype=np.float32)) for k, v in inputs.items() if k not in ("x", "mem")}
    shared["w2"] = w2
    shared["meta"] = meta_np
    shared["bdm"] = bdm_np
    shared["ovl"] = ovl_np
    maps = []
    for b in range(ncores):
        m = dict(shared)
        m["x"] = np.ascontiguousarray(np.asarray(inputs["x"][b, :S], dtype=np.float32))
        m["mem"] = np.ascontiguousarray(np.asarray(inputs["mem"][b], dtype=np.float32))
        maps.append(m)
    return maps


def kernel(**inputs):
    S = inputs["x"].shape[1]
    B = inputs["x"].shape[0]
    kb = build(S)
    maps = make_in_maps(inputs, S, B)
    res = run_bass_kernel_spmd(kb.nc, maps, core_ids=list(range(B)))
    out = np.stack([np.asarray(r["out"], dtype=np.float32) for r in res.results], axis=0)
    return out
```

```python
import math
import sys
import numpy as np
import concourse.bass as bass
import concourse.mybir as mybir
from concourse.bass_utils import run_bass_kernel_spmd

F32 = mybir.dt.float32
BF16 = mybir.dt.bfloat16
I32 = mybir.dt.int32
AF = mybir.ActivationFunctionType
ALU = mybir.AluOpType
AX = mybir.AxisListType

SEM_LIMIT = 30000
N_DMA_SEMS = 24


class Buf:
    __slots__ = ("name", "last_w", "readers")

    def __init__(self, name):
        self.name = name
        self.last_w = None
        self.readers = []


class Op:
    __slots__ = ("eng", "fn", "deps", "need_inc", "sem", "val", "is_dma", "idx", "tag", "odeps", "n", "seg", "pfirst", "fin", "st0", "crit")


class Prog:
    def __init__(self, nc):
        self.nc = nc
        self.engs = {"pe": nc.tensor, "act": nc.scalar, "dve": nc.vector, "pool": nc.gpsimd, "sp": nc.sync}
        self.ops = []
        self.bufs = {}
        self.last_on = {}
        self.dmas_since = []
        self.phase_deps = []
        self.phase_bufs = set()
        self.capture = None
        self.seg = 0
        self.do_sched = True
        self.est_time = 0.0

    def buf(self, name):
        b = self.bufs.get(name)
        if b is None:
            b = self.bufs[name] = Buf(name)
        return b

    def add(self, eng, fn, reads=(), writes=(), dma=False, extra_deps=(), n=64, tag=None):
        if self.capture is not None:
            try:
                tg = (sys._getframe(2).f_lineno, 0)
            except Exception:
                tg = (0, 0)
            self.capture.append(((eng, fn), dict(reads=list(reads), writes=list(writes), dma=dma, n=n, tag=tg)))
            return None
        op = Op()
        op.eng = eng
        op.fn = fn
        op.is_dma = dma
        op.need_inc = False
        op.sem = None
        op.val = 0
        op.n = n
        op.seg = self.seg
        op.pfirst = False
        op.fin = 0.0
        op.idx = len(self.ops)
        if tag is not None:
            op.tag = tag
        else:
            try:
                op.tag = (sys._getframe(2).f_lineno, 0)
            except Exception:
                op.tag = (0, 0)
        deps = {}
        for b in reads:
            b = self.buf(b)
            w = b.last_w
            if w is not None:
                deps[w.idx] = (w, "raw")
        for b in writes:
            b = self.buf(b)
            w = b.last_w
            if w is not None and w.idx not in deps:
                deps[w.idx] = (w, "waw")
            for r in b.readers:
                if r.idx not in deps:
                    deps[r.idx] = (r, "war")
        real = []
        order = []
        for d, kind in deps.values():
            if (not d.is_dma) and d.eng == eng and not dma:
                if eng == "pe" or kind != "raw":
                    order.append(d)
                    continue
            real.append(d)
        for d in extra_deps:
            real.append(d)
        for b in list(reads) + list(writes):
            if b not in self.phase_bufs:
                self.phase_bufs.add(b)
                op.pfirst = True
        op.deps = real
        op.odeps = order
        for b in writes:
            b = self.buf(b)
            b.last_w = op
            b.readers = []
        for b in reads:
            self.buf(b).readers.append(op)
        self.ops.append(op)
        return op

    def barrier(self):
        self.seg += 1
        self.phase_bufs = set()

    def _cost(self, op):
        n = op.n
        e = op.eng
        if op.is_dma:
            return 0.08, 2.0 + n / 100e3
        if e == "pe":
            c = 0.035 + n / 2400.0
        elif e == "act":
            c = 0.22 + n / 1200.0
        elif e == "dve":
            c = 0.08 + n / 960.0
        elif e == "pool":
            c = 0.15 + n / 500.0
        else:
            c = 0.05
        return c, c

    def schedule(self):
        import heapq
        SCHED = self.do_sched
        self.seg_stats = []
        order = []
        ops = self.ops
        nseg = self.seg + 1
        segs = [[] for _ in range(nseg)]
        for op in ops:
            segs[op.seg].append(op)
        t_base = 0.0
        engs = list(self.engs.keys())
        for sg in segs:
            if not sg:
                continue
            if not SCHED:
                order.extend(sg)
                continue
            inseg = set(id(o) for o in sg)
            indeg = {}
            succ = {}
            dr = {}
            for op in sg:
                cnt = 0
                for d in op.deps + op.odeps:
                    if id(d) in inseg:
                        cnt += 1
                        succ.setdefault(id(d), []).append(op)
                indeg[id(op)] = cnt
                dr[id(op)] = t_base
            wait_h = {e: [] for e in engs}
            rdy_h = {e: [] for e in engs}
            free = {e: t_base for e in engs}
            for op in sg:
                if indeg[id(op)] == 0:
                    heapq.heappush(wait_h[op.eng], (dr[id(op)], op.idx, op))
            left = len(sg)
            tmax = t_base
            while left:
                best = None
                for e in engs:
                    wh = wait_h[e]
                    rh = rdy_h[e]
                    fe = free[e]
                    while wh and wh[0][0] <= fe:
                        _, ix, o = heapq.heappop(wh)
                        heapq.heappush(rh, (ix, o))
                    if rh:
                        cand = (fe, rh[0][0], e, 0)
                    elif wh:
                        cand = (wh[0][0], wh[0][1], e, 1)
                    else:
                        continue
                    if best is None or cand[:2] < best[:2]:
                        best = cand
                start, _, e, which = best
                if which == 0:
                    _, op = heapq.heappop(rdy_h[e])
                else:
                    _, _, op = heapq.heappop(wait_h[e])
                busy, lat = self._cost(op)
                free[e] = start + busy
                op.fin = start + lat
                op.st0 = start
                if op.fin > tmax:
                    tmax = op.fin
                order.append(op)
                left -= 1
                for sc in succ.get(id(op), ()):
                    k = id(sc)
                    extra = 0.05 if (sc.eng == op.eng and not op.is_dma) else 0.35
                    t = op.fin + extra
                    if t > dr[k]:
                        dr[k] = t
                    indeg[k] -= 1
                    if indeg[k] == 0:
                        heapq.heappush(wait_h[sc.eng], (dr[k], sc.idx, sc))
            busy_e = {e: 0.0 for e in engs}
            for o in sg:
                busy_e[o.eng] += self._cost(o)[0]
            self.seg_stats.append((sg[0].seg, len(sg), tmax - t_base, busy_e))
            t_base = tmax
        self.est_time = t_base
        return order

    def emit(self, final_wait_eng="sp"):
        nc = self.nc
        order = self.schedule()
        last_eng = {}
        prev_last = {}
        prev_dmas = []
        older_dmas = []
        cur_dmas = []
        cur_seg = -1
        for op in order:
            if op.seg != cur_seg:
                cur_seg = op.seg
                prev_last = dict(last_eng)
                prev_dmas = older_dmas + cur_dmas
                older_dmas = cur_dmas
                cur_dmas = []
            if op.pfirst:
                op.deps = op.deps + list(prev_last.values()) + prev_dmas
            if op.is_dma:
                cur_dmas.append(op)
            else:
                last_eng[op.eng] = op
        for op in order:
            for d in op.deps:
                d.need_inc = True
            if op.is_dma:
                op.need_inc = True
        eng_sem = {}
        eng_cnt = {}
        dma_sems = [nc.alloc_semaphore("dq%d" % i) for i in range(N_DMA_SEMS)]
        dma_cnt = [0] * N_DMA_SEMS
        dma_last = [None] * N_DMA_SEMS
        ndma = 0
        for op in order:
            if not op.need_inc:
                continue
            if op.is_dma:
                j = ndma % N_DMA_SEMS
                ndma += 1
                if dma_last[j] is not None:
                    op.deps.append(dma_last[j])
                dma_cnt[j] += 16
                op.sem = dma_sems[j]
                op.val = dma_cnt[j]
                dma_last[j] = op
            else:
                e = op.eng
                if e not in eng_sem or eng_cnt[e] >= SEM_LIMIT:
                    eng_sem[e] = nc.alloc_semaphore("s_%s_%d" % (e, op.idx))
                    eng_cnt[e] = 0
                eng_cnt[e] += 1
                op.sem = eng_sem[e]
                op.val = eng_cnt[e]
        waited = {}
        nwaits = 0
        for op in order:
            E = self.engs[op.eng]
            need = {}
            for d in op.deps:
                k = id(d.sem)
                if k not in need or need[k][1] < d.val:
                    need[k] = (d.sem, d.val)
            for k, (sem, val) in need.items():
                wk = (op.eng, k)
                if waited.get(wk, 0) >= val:
                    continue
                E.wait_ge(sem, val)
                nwaits += 1
                waited[wk] = val
            try:
                inst = op.fn()
            except Exception:
                print('EMIT FAIL at op', op.idx, op.eng)
                raise
            if op.need_inc:
                inst.then_inc(op.sem, 16 if op.is_dma else 1)
        E = self.engs[final_wait_eng]
        for j in range(N_DMA_SEMS):
            if dma_cnt[j] > 0:
                E.wait_ge(dma_sems[j], dma_cnt[j])
        self.stats = dict(n_ops=len(self.ops), n_waits=nwaits, n_dma=ndma,
                          n_inc=sum(1 for o in self.ops if o.need_inc), est_ms=self.est_time / 1e3)
        return self.stats


class V:
    __slots__ = ("ap", "b")

    def __init__(self, ap, b):
        self.ap = ap
        self.b = b


class T:
    def __init__(self, h, name, dram=False):
        self.h = h
        self.name = name
        self.dram = dram

    def __getitem__(self, idx):
        if self.dram:
            return V(self.h[idx], self.name)
        return V(self.h[idx], self.name)

    def v(self, ap):
        return V(ap, self.name)


DT_SIZE = {F32: 4, BF16: 2, I32: 4}

D = 1024
DFF = 2816
NKC = D // 128
NFC = DFF // 128
LN_EPS = 1e-5
DEPTH = 2
ALPHA = (2 * DEPTH) ** 0.25
N_MEM = 256


class KB:
    def __init__(self, S, depth=DEPTH, stop_after=None, debug=()):
        self.S = S
        self.depth = depth
        self.stop_after = stop_after
        self.debug = debug
        self.nc = bass.Bass("TRN2", target_bir_lowering=False)
        self.P = Prog(self.nc)
        self.uid = 0
        self.sb_base = 0
        self.sb_cur = 0
        self.outs = {}
        self.arena = None
        self.rots = {}
        self.pt_cnt = 0
        self.pending_T = None
        self.xb_cnt = 0
        self.ps_set = (0, 5)
        self.psb_set = (0, 2)
        self.fill_regs = {}
        self.n_keep = 256

    def sb(self, name, shape, dtype):
        nbytes = int(np.prod(shape[1:])) * DT_SIZE[dtype]
        nbytes = (nbytes + 63) // 64 * 64
        off = self.sb_cur
        self.sb_cur += nbytes
        assert self.sb_cur <= 207 * 1024, ("SBUF overflow", name, self.sb_cur)
        self.uid += 1
        if self.arena is None:
            self.arena = self.nc.alloc_sbuf_tensor("arena", [128, 207 * 1024], mybir.dt.uint8)
        ap = self.arena[:, off:off + int(np.prod(shape[1:])) * DT_SIZE[dtype]].bitcast(dtype)
        if len(shape) == 3:
            ap = ap.rearrange("p (a b) -> p a b", a=shape[1])
        elif len(shape) == 4:
            ap = ap.rearrange("p (a b c) -> p a b c", a=shape[1], b=shape[2])
        if shape[0] < 128:
            ap = ap[0:shape[0]]
        return T(ap, "%s_%d" % (name, self.uid))

    def phase_begin(self):
        self.flush_pending()
        self.P.barrier()
        self.sb_cur = self.sb_base

    def dram(self, name, shape, dtype, kind="ExternalOutput"):
        h = self.nc.dram_tensor(name, list(shape), dtype, kind=kind)
        return T(h.ap(), name, dram=True)

    def _rw(self, reads, writes):
        return [r.b for r in reads if isinstance(r, V)], [w.b for w in writes]

    def dma(self, out, in_, eng="sp"):
        nc = self.nc
        E = self.P.engs[eng]
        return self.P.add(eng, lambda: E.dma_start(out=out.ap, in_=in_.ap), reads=[in_.b], writes=[out.b], dma=True,
                          n=int(np.prod(out.ap.shape)) * 2)

    def mm(self, out, lhsT, rhs, start=True, stop=True):
        nc = self.nc
        return self.P.add("pe", lambda: nc.tensor.matmul(out.ap, lhsT.ap, rhs.ap, start=start, stop=stop),
                          reads=[lhsT.b, rhs.b], writes=[out.b], n=int(np.prod(out.ap.shape[1:])) * (4 if lhsT.ap.dtype == F32 else 1))

    def tr(self, out, in_, ident):
        nc = self.nc
        return self.P.add("pe", lambda: nc.tensor.transpose(out.ap, in_.ap, ident.ap),
                          reads=[in_.b, ident.b], writes=[out.b], n=200)

    def act(self, out, in_, func, bias=None, scale=None, accum=None, eng="act"):
        nc = self.nc
        kw = {}
        reads = [in_.b]
        writes = [out.b]
        if bias is not None:
            if isinstance(bias, V):
                kw["bias"] = bias.ap
                reads.append(bias.b)
            else:
                kw["bias"] = bias
        if scale is not None:
            if isinstance(scale, V):
                kw["scale"] = scale.ap
                reads.append(scale.b)
            else:
                kw["scale"] = scale
        if accum is not None:
            kw["accum_out"] = accum.ap
            writes.append(accum.b)
        return self.P.add("act", lambda: nc.scalar.activation(out=out.ap, in_=in_.ap, func=func, **kw),
                          reads=reads, writes=writes, n=int(np.prod(out.ap.shape[1:])))

    def ts(self, out, in0, s1, s2, op0, op1=None, accum=None, eng="dve"):
        E = self.P.engs[eng]
        reads = [in0.b]
        writes = [out.b]
        a1 = s1
        a2 = s2
        if isinstance(s1, V):
            a1 = s1.ap
            reads.append(s1.b)
        if isinstance(s2, V):
            a2 = s2.ap
            reads.append(s2.b)
        kw = {}
        if op1 is not None:
            kw["op1"] = op1
        if accum is not None:
            kw["accum_out"] = accum.ap
            writes.append(accum.b)
        return self.P.add(eng, lambda: E.tensor_scalar(out=out.ap, in0=in0.ap, scalar1=a1, scalar2=a2, op0=op0, **kw),
                          reads=reads, writes=writes, n=int(np.prod(out.ap.shape[1:])))

    def tt(self, out, in0, in1, op, eng="dve"):
        E = self.P.engs[eng]
        return self.P.add(eng, lambda: E.tensor_tensor(out=out.ap, in0=in0.ap, in1=in1.ap, op=op),
                          reads=[in0.b, in1.b], writes=[out.b], n=int(np.prod(out.ap.shape[1:])))

    def stt(self, out, in0, scalar, in1, op0, op1, accum=None):
        nc = self.nc
        reads = [in0.b, in1.b]
        writes = [out.b]
        a = scalar
        if isinstance(scalar, V):
            a = scalar.ap
            reads.append(scalar.b)
        kw = {}
        if accum is not None:
            kw["accum_out"] = accum.ap
            writes.append(accum.b)
        return self.P.add("dve", lambda: nc.vector.scalar_tensor_tensor(out=out.ap, in0=in0.ap, scalar=a, in1=in1.ap,
                                                                     op0=op0, op1=op1, **kw),
                          reads=reads, writes=writes, n=int(np.prod(out.ap.shape[1:])))

    def copy(self, out, in_, eng="dve"):
        E = self.P.engs[eng]
        if eng == "act":
            return self.P.add(eng, lambda: E.copy(out=out.ap, in_=in_.ap), reads=[in_.b], writes=[out.b], n=int(np.prod(out.ap.shape[1:])))
        return self.P.add(eng, lambda: E.tensor_copy(out=out.ap, in_=in_.ap), reads=[in_.b], writes=[out.b], n=int(np.prod(out.ap.shape[1:])))

    def memset(self, out, val, eng="pool"):
        E = self.P.engs[eng]
        return self.P.add(eng, lambda: E.memset(out.ap, val), writes=[out.b], n=int(np.prod(out.ap.shape[1:])))

    def red(self, out, in_, op, axis=AX.X, eng="dve"):
        E = self.P.engs[eng]
        return self.P.add(eng, lambda: E.tensor_reduce(out=out.ap, in_=in_.ap, axis=axis, op=op),
                          reads=[in_.b], writes=[out.b], n=int(np.prod(in_.ap.shape[1:])))

    def recip(self, out, in_):
        nc = self.nc
        return self.P.add("dve", lambda: nc.vector.reciprocal(out=out.ap, in_=in_.ap), reads=[in_.b], writes=[out.b])

    def aselect(self, out, in_, pattern, cmp, fill, base, cm):
        nc = self.nc
        regs = self.fill_regs

        def fn():
            if fill not in regs:
                regs[fill] = nc.gpsimd.to_reg(float(fill))
            return nc.gpsimd.affine_select(out=out.ap, in_=in_.ap, pattern=pattern, compare_op=cmp,
                                           fill=regs[fill], base=base, channel_multiplier=cm)
        return self.P.add("pool", fn, reads=[in_.b], writes=[out.b], n=int(np.prod(out.ap.shape[1:])))

    def iota(self, out, pattern, base, cm):
        nc = self.nc
        return self.P.add("pool", lambda: nc.gpsimd.iota(out.ap, pattern=pattern, base=base, channel_multiplier=cm,
                                                         allow_small_or_imprecise_dtypes=True), writes=[out.b], n=int(np.prod(out.ap.shape[1:])))

    def setup(self):
        nc = self.nc
        self.ps = []
        for i in range(5):
            h = nc.alloc_psum_tensor("ps%d" % i, [128, 512], F32)
            self.ps.append(T(h, "ps%d" % i))
        self.psb = []
        for i in range(2):
            h = nc.alloc_psum_tensor("psb%d" % i, [128, 1024], BF16)
            self.psb.append(T(h, "psb%d" % i))
        self.ps_rr = 0
        self.psb_rr = 0
        self.ident_f = self.sb("identf", [128, 128], F32)
        self.ident = self.sb("ident", [128, 128], BF16)
        self.memset(self.ident_f[:], 1.0)
        self.aselect(self.ident_f[:], self.ident_f[:], [[-1, 128]], ALU.is_equal, 0.0, 0, 1)
        self.copy(self.ident[:], self.ident_f[:], eng="pool")
        self.sb_base = self.sb_cur

    def next_ps(self):
        b0, n = self.ps_set
        t = self.ps[b0 + self.ps_rr % n]
        self.ps_rr += 1
        return t

    def next_psb(self):
        b0, n = self.psb_set
        t = self.psb[b0 + self.psb_rr % n]
        self.psb_rr += 1
        return t

    def load_w(self, name, dram_ap_fn, kchunks, ncols, eng="pool", split=4):
        w = self.sb(name, [128, kchunks, ncols], BF16)
        for k in range(kchunks):
            self.dma(w[:, k, :], dram_ap_fn(k), eng="pool")
        return w

    def layer_norm_tile(self, r, g_bc, b_bc, out_f32, scr):
        st = scr["st"]
        junk = scr["junk"]
        self.act(junk[:], r[:], AF.Identity, accum=st[:, 0:1])
        self.act(junk[:], r[:], AF.Square, accum=st[:, 1:2])
        self.ts(st[:, 2:3], st[:, 0:1], 1.0 / D, None, ALU.mult)
        self.tt(st[:, 3:4], st[:, 2:3], st[:, 2:3], ALU.mult)
        self.stt(st[:, 4:5], st[:, 1:2], 1.0 / D, st[:, 3:4], ALU.mult, ALU.subtract)
        self.ts(st[:, 4:5], st[:, 4:5], 0.0, LN_EPS, ALU.max, ALU.add)
        self.act(st[:, 5:6], st[:, 4:5], AF.Sqrt)
        self.recip(st[:, 6:7], st[:, 5:6])
        self.ts(out_f32[:], r[:], st[:, 2:3], st[:, 6:7], ALU.subtract, ALU.mult)
        self.tt(out_f32[:], out_f32[:], g_bc[:], ALU.mult)
        self.tt(out_f32[:], out_f32[:], b_bc[:], ALU.add)

    def store_xT(self, x_f32, xT_dram, t0, scr, defer=False):
        xbl = scr["xb"]
        if isinstance(xbl, list):
            xb = xbl[self.xb_cnt % len(xbl)]
            self.xb_cnt += 1
        else:
            xb = xbl
        xTs = scr["xTs"]
        self.copy(xb[:], x_f32[:], eng="act")

        def part_b():
            pb = self.next_psb()
            for k in range(NKC):
                self.tr(pb[:, k * 128:(k + 1) * 128], xb[:, k * 128:(k + 1) * 128], self.ident[:])
            self.copy(xTs[:], pb[:, :], eng="dve")
            self.dma(V(xT_dram.h.rearrange("(k p) s -> p k s", p=128)[:, :, t0:t0 + 128], xT_dram.name),
                     V(xTs.h[:].rearrange("p (k t) -> p k t", k=NKC), xTs.name))
        if defer:
            self.flush_pending()
            self.pending_T = part_b
        else:
            part_b()

    def flush_pending(self):
        if self.pending_T is not None:
            f = self.pending_T
            self.pending_T = None
            f()

    def dma_s(self, out, in_, eng="sp"):
        E = self.P.engs[eng]
        return self.P.add(eng, lambda: E.dma_start(out=out.ap, in_=in_.ap, allow_slow_non_contiguous=True),
                          reads=[in_.b], writes=[out.b], dma=True, n=int(np.prod(out.ap.shape)) * 8)

    def rot(self, name, n, shape, dtype):
        key = "_rot_" + name
        lst = [self.sb(name + str(i), shape, dtype) for i in range(n)]
        self.rots[key] = [lst, 0]
        return key

    def nx(self, key):
        lst, i = self.rots[key]
        self.rots[key][1] = i + 1
        return lst[i % len(lst)]

    def vmax(self, out, in_):
        nc = self.nc
        return self.P.add("dve", lambda: nc.vector.max(out=out.ap, in_=in_.ap), reads=[in_.b], writes=[out.b], n=int(np.prod(in_.ap.shape[1:])))

    def match_replace(self, out, rep, vals, imm):
        nc = self.nc
        return self.P.add("dve", lambda: nc.vector.match_replace(out=out.ap, in_to_replace=rep.ap, in_values=vals.ap, imm_value=imm),
                          reads=[rep.b, vals.b], writes=[out.b], n=int(np.prod(vals.ap.shape[1:])))

    def redabs(self, out, in_):
        nc = self.nc
        return self.P.add("dve", lambda: nc.vector.tensor_reduce(out=out.ap, in_=in_.ap, axis=AX.X, op=ALU.max,
                                                                 apply_absolute_value=True),
                          reads=[in_.b], writes=[out.b], n=int(np.prod(in_.ap.shape[1:])))

    def fm_rows(self, FMS, c0, nchunk, s0, s1):
        return V(FMS.h[c0 * 128:(c0 + nchunk) * 128, s0:s1].rearrange("(c p) s -> p c s", p=128), FMS.name)

    def setup_consts(self, meta, bdm, ovl, NB, NCP):
        S = self.S
        NT = S // 128
        self.NB = NB
        self.NCP = NCP
        self.meta = self.sb("meta", [128, 32], F32)
        self.dma(self.meta[:], meta[:, :])
        self.bdm = self.sb("bdm", [128, 256], F32)
        self.dma(self.bdm[:], bdm[:, :])
        self.ovl = self.sb("ovl", [128, NCP // 128, NB], F32)
        self.dma(self.ovl[:], V(ovl.h.rearrange("(c p) j -> p c j", p=128), ovl.name))
        self.U = self.sb("U", [128, 128], F32)
        self.memset(self.U[:], 1.0)
        self.aselect(self.U[:], self.U[:], [[1, 128]], ALU.is_ge, 0.0, 0, -1)
        self.cneg30 = self.sb("cneg30", [128, 128], F32)
        self.memset(self.cneg30[:], 0.0)
        self.aselect(self.cneg30[:], self.cneg30[:], [[-1, 128]], ALU.is_ge, -1e30, 0, 1)
        self.cneg2k = self.sb("cneg2k", [128, 128], F32)
        self.memset(self.cneg2k[:], 0.0)
        self.aselect(self.cneg2k[:], self.cneg2k[:], [[-1, 128]], ALU.is_ge, -2000.0, 0, 1)
        self.band = self.sb("band", [128, 640], F32)
        self.memset(self.band[:], 0.0)
        self.aselect(self.band[:], self.band[:], [[1, 640]], ALU.is_ge, -2000.0, -1, -1)
        self.aselect(self.band[:], self.band[:], [[-1, 640]], ALU.is_ge, -2000.0, 512, 1)
        self.decayT4 = self.sb("decayT4", [128, 4, 128], F32)
        self.xi = self.sb("xi", [128, 128], F32)
        self.zeta = self.sb("zeta", [128, 128], F32)
        self.cdecay = self.sb("cdecay", [128, 1], F32)
        self.rkc = self.sb("rkc", [128, 20], F32)
        self.sb_base = self.sb_cur
        dji = self.sb("dji", [128, 128], F32)
        self.iota(dji[:], [[1, 128]], 0, -1)
        for h in range(4):
            self.act(self.decayT4[:, h, :], dji[:], AF.Exp, scale=RET_LNG[h])
        self.tt(self.decayT4[:], self.decayT4[:], V(self.U.h[:, :].unsqueeze(1).to_broadcast([128, 4, 128]), self.U.name), ALU.mult)
        ip1 = self.sb("ip1", [128, 128], F32)
        self.iota(ip1[:], [[1, 128]], 1, 0)
        self.act(self.xi[:], ip1[:], AF.Exp, scale=self.meta[:, 12:13])
        jr = self.sb("jr", [128, 128], F32)
        self.iota(jr[:], [[0, 128]], 127, -1)
        for h in range(4):
            self.act(self.zeta[:, 32 * h:32 * h + 32], jr[:, 32 * h:32 * h + 32], AF.Exp, scale=RET_LNG[h])
        c128 = self.sb("c128", [128, 1], F32)
        self.memset(c128[:], 128.0)
        self.act(self.cdecay[:], c128[:], AF.Exp, scale=self.meta[:, 12:13])
        for k in range(20):
            self.memset(self.rkc[:, k:k + 1], 2.0 ** (-k))

    def build_addmask(self):
        NT = self.S // 128
        NB = self.NB
        self.addmask = self.sb("addmask", [128, NT, NB], F32)
        self.memset(self.addmask[:], 0.0)
        for i in range(NT):
            for half in range(2):
                cur = 2 * i + half
                r0 = 64 * half
                v = self.addmask[r0:r0 + 64, i, :]
                self.aselect(v, v, [[-1, NB]], ALU.is_ge, -1e30, cur, 0)
                self.memset(self.addmask[r0:r0 + 64, i, 0:1], 1e30)
                self.memset(self.addmask[r0:r0 + 64, i, cur:cur + 1], 1e30)
                if cur >= 1:
                    self.memset(self.addmask[r0:r0 + 64, i, cur - 1:cur], 1e30)

    def phase_rope(self, ROPE):
        S = self.S
        self.phase_begin()
        pos = self.sb("pos", [128, S], F32)
        self.iota(pos[:], [[1, S]], 0, 0)
        a = self.sb("a", [128, S], F32)
        ki = self.sb("ki", [128, S], I32)
        kf = self.sb("kf", [128, S], F32)
        m = self.sb("m", [128, S], F32)
        r = self.sb("r", [128, S], F32)
        PI = math.pi
        for t in range(4):
            for which in range(2):
                self.ts(a[:], pos[:], self.meta[:, t:t + 1], (PI / 2 if which == 0 else 0.0), ALU.mult, ALU.add)
                self.ts(kf[:], a[:], 1.0 / (2 * PI), None, ALU.mult)
                self.copy(ki[:], kf[:])
                self.copy(kf[:], ki[:])
                self.stt(r[:], kf[:], -2 * PI, a[:], ALU.mult, ALU.add)
                self.ts(m[:], r[:], PI, -2 * PI, ALU.is_gt, ALU.mult)
                self.tt(r[:], r[:], m[:], ALU.add)
                self.ts(m[:], r[:], -PI, 2 * PI, ALU.is_lt, ALU.mult)
                self.tt(r[:], r[:], m[:], ALU.add)
                self.ts(r[:], r[:], PI, -PI, ALU.min, ALU.max)
                self.act(r[:], r[:], AF.Sin)
                col = 4 + 4 * which + t
                self.ts(r[:], r[:], self.meta[:, col:col + 1], None, ALU.mult)
                self.dma(ROPE[t, which], r[:])

    def finish_tile(self, rq, g_bc, b_bc, x_out, xT_out, t0, scr):
        self.layer_norm_tile(rq, g_bc, b_bc, rq, scr)
        self.dma(x_out[t0:t0 + 128, :], rq[:])
        if xT_out is not None:
            self.store_xT(rq, xT_out, t0, scr, defer=True)

    def ln_setup(self, ln_g, ln_b):
        g_bc = self.sb("g_bc", [128, D], F32)
        b_bc = self.sb("b_bc", [128, D], F32)
        self.dma(g_bc[:], V(ln_g.h.partition_broadcast(128), ln_g.name))
        self.dma(b_bc[:], V(ln_b.h.partition_broadcast(128), ln_b.name))
        scr = dict(st=self.sb("st", [128, 8], F32), junk=self.sb("junk", [128, D], BF16),
                   xb=[self.sb("xb0", [128, D], BF16), self.sb("xb1", [128, D], BF16)], xTs=self.sb("xTs", [128, D], BF16))
        return g_bc, b_bc, scr

    def phase_ffn(self, x_in, xT_in, w_gu, w_down, ln_g, ln_b, x_out, xT_out):
        S = self.S
        self.phase_begin()
        wgu_v = w_gu.h.rearrange("(k p) c -> p k c", p=128)
        wgb = []
        for jb in range(NFC // 2):
            blk = self.sb("wgu%d" % jb, [128, NKC, 512], BF16)
            self.dma(blk[:, :, 0:256], V(wgu_v[:, :, jb * 256:(jb + 1) * 256], w_gu.name), eng="pool")
            self.dma(blk[:, :, 256:512], V(wgu_v[:, :, DFF + jb * 256:DFF + (jb + 1) * 256], w_gu.name), eng="pool")
            wgb.append(blk)
        wdn = self.load_w("wdn", lambda k: V(w_down.h[k * 128:(k + 1) * 128, :], w_down.name), NFC, D)
        g_bc, b_bc, scr = self.ln_setup(ln_g, ln_b)
        xt = self.sb("xT", [128, NKC, 512], BF16)
        hT = self.sb("hT", [128, NFC, 512], BF16)
        sg = [self.sb("sg%d" % i, [128, 512], BF16) for i in range(2)]
        xr = [self.sb("xr%d" % i, [128, D], F32) for i in range(2)]
        rqs = [self.sb("r%d" % i, [128, D], F32) for i in range(2)]
        ntile = S // 512

        def load_xt(t_):
            self.dma(xt[:], V(xT_in.h.rearrange("(k p) s -> p k s", p=128)[:, :, t_ * 512:(t_ + 1) * 512], xT_in.name))

        def load_xq(idx):
            self.dma(xr[idx % 2][:], x_in[idx * 128:(idx + 1) * 128, :])
        load_xt(0)
        load_xq(0)
        for t in range(ntile):
            for j in range(NFC):
                pg = self.next_ps()
                pu = self.next_ps()
                wb_ = wgb[j // 2]
                o_ = (j % 2) * 128
                for k in range(NKC):
                    self.mm(pg[:], wb_[:, k, o_:o_ + 128], xt[:, k, :], start=(k == 0), stop=(k == NKC - 1))
                for k in range(NKC):
                    self.mm(pu[:], wb_[:, k, 256 + o_:256 + o_ + 128], xt[:, k, :], start=(k == 0), stop=(k == NKC - 1))
                s = sg[j % 2]
                self.act(s[:], pg[:], AF.Silu)
                self.tt(hT[:, j, :], s[:], pu[:], ALU.mult)
            if t + 1 < ntile:
                load_xt(t + 1)
            for q in range(4):
                t0 = t * 512 + q * 128
                xq = xr[q % 2]
                rq = rqs[q % 2]
                if t * 4 + q + 1 < ntile * 4:
                    load_xq(t * 4 + q + 1)
                for half in range(2):
                    hs = slice(half * 512, (half + 1) * 512)
                    pd = self.next_ps()
                    for j in range(NFC):
                        self.mm(pd[:], hT[:, j, q * 128:(q + 1) * 128], wdn[:, j, hs], start=(j == 0), stop=(j == NFC - 1))
                    self.act(xq[:, hs], xq[:, hs], AF.Copy, scale=ALPHA)
                    self.stt(rq[:, hs], pd[:], 0.5, xq[:, hs], ALU.mult, ALU.add)
                self.finish_tile(rq, g_bc, b_bc, x_out, xT_out, t0, scr)

    def phase_transpose_in(self, x_in, xT_out):
        S = self.S
        self.phase_begin()
        xr = [self.sb("xr%d" % i, [128, D], F32) for i in range(2)]
        scr = dict(xb=self.sb("xb", [128, D], BF16), xTs=self.sb("xTs", [128, D], BF16))
        for i in range(S // 128):
            xq = xr[i % 2]
            self.dma(xq[:], x_in[i * 128:(i + 1) * 128, :])
            self.store_xT(xq, xT_out, i * 128, scr)

    def phase_inproj(self, xT_in, w2, ROPE, FMS, TMB, TMF):
        S = self.S
        self.phase_begin()
        w = self.sb("win", [128, NKC, NCOL2], BF16)
        w2_v = w2.h.rearrange("(k p) c -> p k c", p=128)
        bounds = list(range(0, 4096 + 1, 512)) + [TM0, TM0 + 512, NCOL2]
        for bi in range(len(bounds) - 1):
            a_, b_ = bounds[bi], bounds[bi + 1]
            self.P.add("pool", (lambda a=a_, b=b_: self.nc.gpsimd.dma_start(out=w.h[:, :, a:b], in_=w2_v[:, :, a:b])),
                       reads=[w2.name], writes=["win_b%d" % bi], dma=True, n=128 * NKC * (b_ - a_) * 2)

        def wv(k, a, b):
            for bi in range(len(bounds) - 1):
                if bounds[bi] <= a and b <= bounds[bi + 1]:
                    return V(w.h[:, k, a:b], "win_b%d" % bi)
            raise AssertionError((a, b))
        xts = [self.sb("xT%d" % i_, [128, NKC, 512], BF16) for i_ in range(2)]
        tabs = [self.sb("tab%d" % i_, [128, 4, 2, 512], F32) for i_ in range(2)]
        t1 = self.rot("t1", 3, [128, 512], F32)
        t2 = self.rot("t2", 3, [128, 512], F32)
        ob = self.rot("ob", 4, [128, 512], BF16)
        tmb = self.rot("tmb", 2, [128, 960], BF16)
        tmf = self.rot("tmf", 2, [128, 24], F32)
        ntile = S // 512

        def load_t(t_):
            ss_ = slice(t_ * 512, (t_ + 1) * 512)
            self.dma(xts[t_ % 2][:], V(xT_in.h.rearrange("(k p) s -> p k s", p=128)[:, :, ss_], xT_in.name))
            self.dma(tabs[t_ % 2][:], V(ROPE.h[:, :, :, ss_].rearrange("t w p s -> p t w s"), ROPE.name))
        load_t(0)
        for t in range(ntile):
            ss = slice(t * 512, (t + 1) * 512)
            xt = xts[t % 2]
            tab = tabs[t % 2]
            if t + 1 < ntile:
                load_t(t + 1)
            for ci, tb in enumerate(ROPED_TABLES):
                pA = self.next_ps()
                pB = self.next_ps()
                for k in range(NKC):
                    self.mm(pA[:], wv(k, (2 * ci) * 128, (2 * ci + 1) * 128), xt[:, k, :], start=(k == 0), stop=(k == NKC - 1))
                for k in range(NKC):
                    self.mm(pB[:], wv(k, (2 * ci + 1) * 128, (2 * ci + 2) * 128), xt[:, k, :], start=(k == 0), stop=(k == NKC - 1))
                a1 = self.nx(t1)
                a2 = self.nx(t2)
                o = self.nx(ob)
                self.tt(a1[:], pA[:], tab[:, tb, 0, :], ALU.mult)
                self.tt(a2[:], pB[:], tab[:, tb, 1, :], ALU.mult)
                self.tt(o[:], a1[:], a2[:], ALU.add)
                self.dma(V(FMS.h[ci * 128:(ci + 1) * 128, ss], FMS.name), o[:])
            nr = len(ROPED_TABLES)
            for j in range(7):
                wc = 2 * nr + j
                pA = self.next_ps()
                for k in range(NKC):
                    self.mm(pA[:], wv(k, wc * 128, (wc + 1) * 128), xt[:, k, :], start=(k == 0), stop=(k == NKC - 1))
                o = self.nx(ob)
                self.copy(o[:], pA[:], eng="act")
                self.dma(V(FMS.h[(nr + j) * 128:(nr + j + 1) * 128, ss], FMS.name), o[:])
            for q in range(4):
                t0 = t * 512 + q * 128
                pA = self.next_ps()
                pB = self.next_ps()
                for k in range(NKC):
                    self.mm(pA[:], xt[:, k, q * 128:(q + 1) * 128], wv(k, TM0, TM0 + 512), start=(k == 0), stop=(k == NKC - 1))
                for k in range(NKC):
                    self.mm(pB[:, 0:472], xt[:, k, q * 128:(q + 1) * 128], wv(k, TM0 + 512, TM0 + 984), start=(k == 0), stop=(k == NKC - 1))
                b = self.nx(tmb)
                f = self.nx(tmf)
                self.copy(b[:, 0:512], pA[:], eng="act")
                self.copy(b[:, 512:960], pB[:, 0:448])
                self.copy(f[:], pB[:, 448:472])
                self.dma(TMB[t0:t0 + 128, :], b[:])
                self.dma(TMF[t0:t0 + 128, :], f[:])

    def store_yT(self, y, YT, br, n, yTs_key):
        pb = self.next_psb()
        self.tr(pb[:, 0:128], y[:, 0:128], self.ident[:])
        self.tr(pb[:, 128:256], y[:, 128:256], self.ident[:])
        yTs = self.nx(yTs_key)
        self.copy(yTs[:], pb[:, 0:256])
        self.dma(V(YT.h[br * 256:(br + 1) * 256, n * 128:(n + 1) * 128].rearrange("(c p) t -> p c t", p=128), YT.name),
                 V(yTs.h[:, :].rearrange("p (c t) -> p c t", c=2), yTs.name))

    def phase_ret(self, FMS, TMB, YT):
        S = self.S
        NT = S // 128
        self.phase_begin()
        rq = self.sb("rq", [128, S], BF16)
        rk = self.sb("rk", [128, S], BF16)
        self.dma(rq[:], V(FMS.h[0:128, :], FMS.name))
        self.dma(rk[:], V(FMS.h[128:256, :], FMS.name))
        Sbd = self.sb("Sbd", [128, 256], F32)
        Sbd_bf = self.sb("Sbd_bf", [128, 256], BF16)
        self.memset(Sbd[:], 0.0)
        self.memset(Sbd_bf[:], 0.0)
        vt_k = self.rot("vt", 2, [128, 512], BF16)
        qxi_k = self.rot("qxi", 2, [128, 128], BF16)
        qm_k = self.rot("qm", 2, [128, 4, 128], BF16)
        kz_k = self.rot("kz", 2, [128, 128], BF16)
        PT_k = self.rot("PT", 2, [128, 4, 128], BF16)
        cross_k = self.rot("cross", 2, [128, 256], F32)
        o_k = self.rot("o", 2, [128, 256], F32)
        tmp_k = self.rot("tmp", 2, [128, 256], F32)
        osq_k = self.rot("osq", 2, [128, 256], F32)
        sg_k = self.rot("sg", 2, [128, 256], F32)
        st_k = self.rot("st", 2, [128, 16], F32)
        y_k = self.rot("y", 2, [128, 256], BF16)
        yTs_k = self.rot("yTs", 2, [128, 256], BF16)
        hm = V(self.meta.h[:, 13:17].unsqueeze(2).to_broadcast([128, 4, 128]), self.meta.name)
        for n in range(NT):
            sl = slice(n * 128, (n + 1) * 128)
            vt = self.nx(vt_k)
            self.dma(vt[:], TMB[n * 128:(n + 1) * 128, 0:512])
            qxi = self.nx(qxi_k)
            self.tt(qxi[:], rq[:, sl], self.xi[:], ALU.mult)
            qm = self.nx(qm_k)
            self.tt(qm[:], V(rq.h[:, sl].unsqueeze(1).to_broadcast([128, 4, 128]), rq.name), hm, ALU.mult, eng="pool")
            pb = self.next_psb()
            self.tr(pb[:, 0:128], rk[:, sl], self.ident[:])
            kz = self.nx(kz_k)
            self.tt(kz[:], pb[:, 0:128], self.zeta[:], ALU.mult)
            ps1 = self.next_ps()
            self.mm(ps1[:], rk[:, sl], V(qm.h[:, :, :].rearrange("p h i -> p (h i)"), qm.name))
            PT = self.nx(PT_k)
            self.tt(PT[:], V(ps1.h[:, :].rearrange("p (h i) -> p h i", h=4), ps1.name), self.decayT4[:], ALU.mult)
            ps2 = self.next_ps()
            self.mm(ps2[:, 0:256], qxi[:], Sbd_bf[:])
            cross = self.nx(cross_k)
            self.copy(cross[:], ps2[:, 0:256], eng="act")
            ps3 = self.next_ps()
            for h in range(4):
                self.mm(ps3[:, 64 * h:64 * h + 64], PT[:, h, :], vt[:, 64 * h:64 * h + 64])
            o = self.nx(o_k)
            self.tt(o[:], ps3[:, 0:256], cross[:], ALU.add)
            ps4 = self.next_ps()
            self.mm(ps4[:, 0:256], kz[:], vt[:, 0:256])
            tmp = self.nx(tmp_k)
            self.tt(tmp[:], ps4[:, 0:256], self.bdm[:], ALU.mult)
            self.stt(Sbd[:], Sbd[:], self.cdecay[:, 0:1], tmp[:], ALU.mult, ALU.add)
            self.copy(Sbd_bf[:], Sbd[:], eng="act")
            st = self.nx(st_k)
            o3 = V(o.h[:, :].rearrange("p (h e) -> p h e", h=4), o.name)
            self.red(st[:, 0:4], o3, ALU.add)
            osq = self.nx(osq_k)
            self.tt(osq[:], o[:], o[:], ALU.mult, eng="pool")
            self.red(st[:, 4:8], V(osq.h[:, :].rearrange("p (h e) -> p h e", h=4), osq.name), ALU.add)
            self.ts(st[:, 8:12], st[:, 0:4], 1.0 / 64, None, ALU.mult)
            self.tt(st[:, 12:16], st[:, 8:12], st[:, 8:12], ALU.mult)
            self.stt(st[:, 4:8], st[:, 4:8], 1.0 / 64, st[:, 12:16], ALU.mult, ALU.subtract)
            self.ts(st[:, 4:8], st[:, 4:8], 0.0, LN_EPS, ALU.max, ALU.add)
            self.act(st[:, 4:8], st[:, 4:8], AF.Sqrt)
            self.recip(st[:, 4:8], st[:, 4:8])
            self.tt(o3, o3, V(st.h[:, 8:12].unsqueeze(2).to_broadcast([128, 4, 64]), st.name), ALU.subtract)
            self.tt(o3, o3, V(st.h[:, 4:8].unsqueeze(2).to_broadcast([128, 4, 64]), st.name), ALU.mult)
            sg = self.nx(sg_k)
            self.act(sg[:], vt[:, 256:512], AF.Silu)
            y = self.nx(y_k)
            self.tt(y[:], o[:], sg[:], ALU.mult)
            self.store_yT(y, YT, 0, n, yTs_k)

    def phase_ssd(self, FMS, TMB, TMF, conv_w, conv_b, dt_bias, a_log, d_skip, norm_g, YT):
        S = self.S
        NT = S // 128
        self.phase_begin()
        cw = self.sb("cw", [128, 6, 4], F32)
        for k_ in range(4):
            self.dma_s(cw[:, :, k_], V(conv_w.h[k_].rearrange("(c p) -> p c", p=128), conv_w.name))
        cb = self.sb("cb", [128, 6], F32)
        self.dma_s(cb[:], V(conv_b.h.rearrange("(c p) -> p c", p=128), conv_b.name))
        dtb = self.sb("dtb", [128, 4], F32)
        self.dma(dtb[:], V(dt_bias.h.partition_broadcast(128), dt_bias.name))
        a_bc = self.sb("a_bc", [128, 4], F32)
        self.dma(a_bc[:], V(a_log.h.partition_broadcast(128), a_log.name))
        self.act(a_bc[:], a_bc[:], AF.Exp)
        self.ts(a_bc[:], a_bc[:], -1.0, None, ALU.mult)
        Dbc = self.sb("Dbc", [128, 4], F32)
        self.dma(Dbc[:], V(d_skip.h.partition_broadcast(128), d_skip.name))
        ng_bc = self.sb("ng_bc", [128, 256], F32)
        self.dma(ng_bc[:], V(norm_g.h.partition_broadcast(128), norm_g.name))
        xbcs = self.sb("xbcs", [128, 6, S], BF16)
        raw_k = self.rot("raw", 2, [128, 6, 515], BF16)
        acc_k = self.rot("acc", 2, [128, 512], F32)
        for t in range(S // 512):
            raw = self.nx(raw_k)
            if t == 0:
                self.memset(raw[:, :, 0:3], 0.0)
                self.dma(raw[:, :, 3:515], self.fm_rows(FMS, 14, 6, 0, 512))
            else:
                self.dma(raw[:, :, 0:515], self.fm_rows(FMS, 14, 6, t * 512 - 3, (t + 1) * 512))
            for c in range(6):
                acc = self.nx(acc_k)
                self.ts(acc[:], raw[:, c, 3:515], cw[:, c, 3:4], None, ALU.mult)
                for k in (2, 1, 0):
                    self.stt(acc[:], raw[:, c, k:k + 512], cw[:, c, k:k + 1], acc[:], ALU.mult, ALU.add)
                self.act(xbcs[:, c, t * 512:(t + 1) * 512], acc[:], AF.Silu, bias=cb[:, c:c + 1])
        prev = self.sb("prev", [128, 256], F32)
        prev_bf = self.sb("prev_bf", [128, 256], BF16)
        self.memset(prev[:], 0.0)
        self.memset(prev_bf[:], 0.0)
        xsB_k = self.rot("xsB", 2, [128, 512], BF16)
        tmf_k = self.rot("tmf", 2, [128, 24], F32)
        zt_k = self.rot("zt", 2, [128, 256], BF16)
        st_k = self.rot("st", 2, [128, 32], F32)
        adtb_k = self.rot("adtb", 2, [128, 4, 128], F32)
        seg_k = self.rot("seg", 2, [128, 4, 128], F32)
        MT_k = self.rot("MT", 2, [128, 4, 128], BF16)
        X_k = self.rot("X", 2, [128, 256], BF16)
        Xd_k = self.rot("Xd", 2, [128, 256], BF16)
        yd_k = self.rot("yd", 2, [128, 256], F32)
        y_k = self.rot("y", 2, [128, 256], F32)
        t2_k = self.rot("t2", 2, [128, 256], F32)
        sz_k = self.rot("sz", 2, [128, 256], F32)
        yb_k = self.rot("yb", 2, [128, 256], BF16)
        yTs_k = self.rot("yTs", 2, [128, 256], BF16)
        Ubc = V(self.U.h[:, :].unsqueeze(1).to_broadcast([128, 4, 128]), self.U.name)

        def h4(t_):
            return V(t_.h[:, 0:256].rearrange("p (h e) -> p h e", h=4), t_.name)

        def bc4(v_):
            return V(v_.ap.unsqueeze(2).to_broadcast([128, 4, 64]), v_.b)

        for n in range(NT):
            sl = slice(n * 128, (n + 1) * 128)
            pb = self.next_psb()
            for c in range(4):
                self.tr(pb[:, c * 128:(c + 1) * 128], xbcs[:, c, sl], self.ident[:])
            xsB = self.nx(xsB_k)
            self.copy(xsB[:], pb[:, 0:512])
            tmf = self.nx(tmf_k)
            self.dma(tmf[:], TMF[n * 128:(n + 1) * 128, :])
            zt = self.nx(zt_k)
            self.dma(zt[:], TMB[n * 128:(n + 1) * 128, 704:960])
            st = self.nx(st_k)
            self.tt(st[:, 0:4], tmf[:, 20:24], dtb[:], ALU.add)
            self.act(st[:, 0:4], st[:, 0:4], AF.Exp)
            self.act(st[:, 0:4], st[:, 0:4], AF.Ln, bias=1.0)
            self.tt(st[:, 4:8], st[:, 0:4], a_bc[:], ALU.mult)
            adtb = self.nx(adtb_k)
            self.copy(adtb[:], V(st.h[:, 4:8].unsqueeze(2).to_broadcast([128, 4, 128]), st.name))
            psA = self.next_ps()
            self.mm(psA[:, 0:4], self.U[:], st[:, 4:8])
            self.copy(st[:, 8:12], psA[:, 0:4], eng="act")
            psB = self.next_ps()
            for h in range(4):
                self.mm(psB[:, h * 128:(h + 1) * 128], adtb[:, h, :], self.U[:])
            seg = self.nx(seg_k)
            for h in range(4):
                self.ts(seg[:, h, :], psB[:, h * 128:(h + 1) * 128], st[:, 8 + h:9 + h], 0.0, ALU.subtract, ALU.min)
            self.act(seg[:], seg[:], AF.Exp)
            self.tt(seg[:], seg[:], Ubc, ALU.mult, eng="pool")
            alast = V(psB.h[:, 127:512:128], psB.name)
            self.tt(st[:, 12:16], alast, st[:, 8:12], ALU.subtract)
            self.act(st[:, 12:16], st[:, 12:16], AF.Exp)
            self.act(st[:, 16:20], alast, AF.Exp)
            self.act(st[:, 20:24], st[:, 8:12], AF.Exp)
            psG = self.next_ps()
            for g in range(2):
                self.mm(psG[:, g * 128:(g + 1) * 128], xbcs[:, 2 + g, sl], xbcs[:, 4 + g, sl])
            MT = self.nx(MT_k)
            for g in range(2):
                self.tt(MT[:, 2 * g:2 * g + 2, :], seg[:, 2 * g:2 * g + 2, :],
                        V(psG.h[:, g * 128:(g + 1) * 128].unsqueeze(1).to_broadcast([128, 2, 128]), psG.name), ALU.mult)
            X = self.nx(X_k)
            self.tt(h4(X), h4(xsB), bc4(st[:, 0:4]), ALU.mult)
            psY = self.next_ps()
            for h in range(4):
                self.mm(psY[:, 64 * h:64 * h + 64], MT[:, h, :], X[:, 64 * h:64 * h + 64])
            psO = self.next_ps()
            for g in range(2):
                self.mm(psO[:, 128 * g:128 * g + 128], xbcs[:, 4 + g, sl], prev_bf[:, 128 * g:128 * g + 128])
            yd = self.nx(yd_k)
            self.copy(yd[:], psY[:, 0:256], eng="act")
            y = self.nx(y_k)
            self.tt(h4(y), h4(psO), bc4(st[:, 20:24]), ALU.mult)
            self.tt(y[:], y[:], yd[:], ALU.add)
            t2 = self.nx(t2_k)
            self.tt(h4(t2), h4(xsB), bc4(Dbc[:, 0:4]), ALU.mult, eng="pool")
            self.tt(y[:], y[:], t2[:], ALU.add)
            Xd = self.nx(Xd_k)
            self.tt(h4(Xd), h4(X), bc4(st[:, 12:16]), ALU.mult, eng="pool")
            psS = self.next_ps()
            for g in range(2):
                self.mm(psS[:, 128 * g:128 * g + 128], xsB[:, 256 + 128 * g:256 + 128 * g + 128], Xd[:, 128 * g:128 * g + 128])
            self.tt(h4(prev), h4(prev), bc4(st[:, 16:20]), ALU.mult)
            self.tt(prev[:], prev[:], psS[:, 0:256], ALU.add)
            self.copy(prev_bf[:], prev[:], eng="act")
            sz = self.nx(sz_k)
            self.act(sz[:], zt[:], AF.Silu)
            self.tt(y[:], y[:], sz[:], ALU.mult)
            self.tt(t2[:], y[:], y[:], ALU.mult, eng="pool")
            self.red(st[:, 24:26], V(t2.h[:, :].rearrange("p (g e) -> p g e", g=2), t2.name), ALU.add)
            self.ts(st[:, 24:26], st[:, 24:26], 1.0 / 128, LN_EPS, ALU.mult, ALU.add)
            self.act(st[:, 24:26], st[:, 24:26], AF.Sqrt)
            self.recip(st[:, 24:26], st[:, 24:26])
            y2 = V(y.h[:, :].rearrange("p (g e) -> p g e", g=2), y.name)
            self.tt(y2, y2, V(st.h[:, 24:26].unsqueeze(2).to_broadcast([128, 2, 128]), st.name), ALU.mult)
            yb = self.nx(yb_k)
            self.tt(yb[:], y[:], ng_bc[:], ALU.mult)
            self.store_yT(yb, YT, 3, n, yTs_k)

    def softmax_pv(self, Ssb, nk, Vt, kt0, out, kk, clamp=None, premax=None):
        st = self.nx(kk["st"])
        self.red(st[:, 0:1], (Ssb if premax is None else premax), ALU.max)
        if clamp is not None:
            self.ts(st[:, 0:1], st[:, 0:1], clamp, None, ALU.max)
        self.ts(st[:, 1:2], st[:, 0:1], -1.0, None, ALU.mult)
        Ps = kk["Ps"]
        ng = (nk + 1023) // 1024
        for g in range(ng):
            a = g * 1024
            b = min(nk, a + 1024)
            self.act(Ps[g][:, 0:b - a], V(Ssb.ap[:, a:b], Ssb.b), AF.Exp, bias=st[:, 1:2], accum=st[:, 8 + g:9 + g])
        if ng > 1:
            self.red(st[:, 2:3], st[:, 8:8 + ng], ALU.add)
            ssum = st[:, 2:3]
        else:
            ssum = st[:, 8:9]
        self.ts(st[:, 3:4], ssum, 1e-30, None, ALU.max)
        self.recip(st[:, 4:5], st[:, 3:4])
        po = self.next_ps()
        nkt = nk // 128
        for g0 in range(0, nkt, 8):
            gn = min(8, nkt - g0)
            Pg = Ps[g0 // 8]
            pb = self.next_psb()
            for j in range(gn):
                self.tr(pb[:, j * 128:(j + 1) * 128], Pg[:, j * 128:(j + 1) * 128], self.ident[:])
            PT = self.nx(kk["PT"])
            self.pt_cnt += 1
            self.copy(PT[:, 0:gn * 128], pb[:, 0:gn * 128], eng=("act" if self.pt_cnt % 3 else "dve"))
            for j in range(gn):
                self.mm(po[:, 0:64], PT[:, j * 128:(j + 1) * 128], Vt[:, kt0 + g0 + j, :],
                        start=(g0 + j == 0), stop=(g0 + j == nkt - 1))
        self.ts(out, po[:, 0:64], st[:, 4:5], None, ALU.mult)

    def attn_keys(self, pfx):
        S = self.S
        npc = (S + 1023) // 1024
        return dict(st=self.rot(pfx + "sst", 2, [128, 16], F32),
                    Ps=[self.sb(pfx + "P%d" % g_, [128, 1024], BF16) for g_ in range(npc)],
                    PT=self.rot(pfx + "PTa", 2, [128, 1024], BF16))

    def dsa_setup(self, FMS, TMB, TMF, YT):
        S = self.S
        NT = S // 128
        c = dict(FMS=FMS, YT=YT)
        c["dk"] = self.sb("dk", [128, S], BF16)
        self.dma(c["dk"][:], V(FMS.h[4 * 128:5 * 128, :], FMS.name))
        c["ikr"] = self.sb("ikr", [128, S], BF16)
        self.dma(c["ikr"][:], V(FMS.h[7 * 128:8 * 128, :], FMS.name))
        c["qm"] = self.rot("dqm", 2, [128, 8, 128], BF16)
        c["Vt"] = self.sb("Vt", [128, NT, 64], BF16)
        self.dma(c["Vt"][:], V(TMB.h[:, 512:576].rearrange("(n p) c -> p n c", p=128), TMB.name))
        iw = self.sb("iw", [128, NT, 8], F32)
        self.dma(iw[:], V(TMF.h[:, 0:8].rearrange("(n p) c -> p n c", p=128), TMF.name))
        c["absw"] = self.sb("absw", [128, NT, 8], F32)
        self.act(c["absw"][:], iw[:], AF.Abs, scale=1.0 / 16)
        c["sgn"] = self.sb("sgn", [128, NT, 8], F32)
        self.ts(c["sgn"][:], iw[:], 0.0, 2.0, ALU.is_ge, ALU.mult)
        self.ts(c["sgn"][:], c["sgn"][:], -1.0, None, ALU.add)
        c["I"] = self.rot("I", 1, [128, S], F32)
        c["Ssb"] = self.rot("dSsb", 1, [128, S], F32)
        c["Mb"] = self.rot("dMb", 2, [128, S], BF16)
        c["cm"] = self.rot("dcm", 2, [128, 8], F32)
        c["junk"] = self.sb("djunk", [128, S], BF16)
        c["kk"] = self.attn_keys("d")
        c["q"] = self.rot("dqi", 2, [128, 4, 128], BF16)
        c["tmp"] = self.rot("tmpr", 2, [128, 512], F32)
        c["st"] = self.rot("dst", 2, [128, 16], F32)
        c["Rk"] = self.rot("Rk", 2, [128, 20], F32)
        c["nm"] = self.rot("dnm", 2, [128, 2], F32)
        c["c2"] = self.rot("dc2", 2, [128, 2], F32)
        c["o"] = self.rot("do", 2, [128, 256], F32)
        c["y"] = self.rot("dy", 2, [128, 256], BF16)
        c["yTs"] = self.rot("dyTs", 2, [128, 256], BF16)
        return c

    def dsa_tile(self, c, i):
        FMS = c["FMS"]
        I = self.nx(c["I"])
        Ssb = self.nx(c["Ssb"])
        nk = 128 * (i + 1)
        nkc = (nk + 511) // 512
        q = self.nx(c["q"])
        self.dma(q[:, 0:2, :], self.fm_rows(FMS, 2, 2, i * 128, (i + 1) * 128))
        self.dma(q[:, 2:4, :], self.fm_rows(FMS, 5, 2, i * 128, (i + 1) * 128))
        qm = self.nx(c["qm"])
        hm = V(self.meta.h[:, 13:17].unsqueeze(2).to_broadcast([128, 4, 128]), self.meta.name)
        for cc in range(2):
            self.tt(qm[:, 4 * cc:4 * cc + 4, :], V(q.h[:, 2 + cc, :].unsqueeze(1).to_broadcast([128, 4, 128]), q.name), hm,
                    ALU.mult, eng="pool")
        for kc in range(nkc):
            c0 = kc * 512
            cols = min(512, nk - c0)
            for h in range(8):
                ps = self.next_ps()
                self.mm(ps[:, 0:cols], qm[:, h, :], c["ikr"][:, c0:c0 + cols])
                tmp = self.nx(c["tmp"])
                self.act(tmp[:, 0:cols], ps[:, 0:cols], AF.Relu, scale=c["absw"][:, i, h:h + 1])
                if h == 0:
                    self.ts(I[:, c0:c0 + cols], tmp[:, 0:cols], c["sgn"][:, i, 0:1], None, ALU.mult)
                else:
                    self.stt(I[:, c0:c0 + cols], tmp[:, 0:cols], c["sgn"][:, i, h:h + 1], I[:, c0:c0 + cols], ALU.mult, ALU.add)
        if nk > self.n_keep:
            st = self.nx(c["st"])
            junk = c["junk"]
            self.redabs(st[:, 0:1], I[:, 0:nk])
            self.ts(st[:, 0:1], st[:, 0:1], 1e-20, None, ALU.max)
            self.tt(I[:, nk - 128:nk], I[:, nk - 128:nk], self.cneg30[:], ALU.add)
            Rk = self.nx(c["Rk"])
            self.ts(Rk[:], self.rkc[:], st[:, 0:1], None, ALU.mult)
            self.ts(st[:, 1:2], st[:, 0:1], -1.0, None, ALU.mult)
            n1 = (nk // 2 + 127) // 128 * 128
            n2 = nk - n1
            thr_c = self.n_keep - 0.5 - n2 / 2.0
            for k in range(NBIS):
                nm = self.nx(c["nm"])
                c2 = self.nx(c["c2"])
                self.tt(nm[:, 0:1], st[:, 1:2], Rk[:, k:k + 1], ALU.add)
                self.act(Ssb[:, n1:nk], I[:, n1:nk], AF.Sign, bias=nm[:, 0:1], scale=-1.0, accum=c2[:, 0:1])
                self.ts(junk[:, 0:n1], I[:, 0:n1], nm[:, 0:1], None, ALU.is_ge, ALU.add, accum=st[:, 3:4])
                self.stt(st[:, 4:5], c2[:, 0:1], -0.5, st[:, 3:4], ALU.mult, ALU.add)
                self.ts(st[:, 4:5], st[:, 4:5], thr_c, None, ALU.is_ge)
                self.stt(st[:, 1:2], st[:, 4:5], Rk[:, k:k + 1], st[:, 1:2], ALU.mult, ALU.add)
            Mb = self.nx(c["Mb"])
            self.ts(Mb[:, 0:nk], I[:, 0:nk], st[:, 1:2], 8000.0, ALU.is_ge, ALU.mult)
        else:
            Mb = self.nx(c["Mb"])
            self.memset(Mb[:, 0:nk], 8000.0)
            self.tt(Mb[:, nk - 128:nk], Mb[:, nk - 128:nk], self.cnegb[:], ALU.add)
        o = self.nx(c["o"])
        for h in range(4):
            base = 64 * (h % 2)
            cq = h // 2
            cm = self.nx(c["cm"])
            for kc in range(nkc):
                c0 = kc * 512
                cols = min(512, nk - c0)
                ps = self.next_ps()
                self.mm(ps[:, 0:cols], q[base:base + 64, cq, :], c["dk"][base:base + 64, c0:c0 + cols], start=True, stop=False)
                self.mm(ps[:, 0:cols], self.ident[:], Mb[:, c0:c0 + cols], start=False, stop=True)
                self.ts(Ssb[:, c0:c0 + cols], ps[:, 0:cols], 0.125, None, ALU.mult, ALU.max, accum=cm[:, kc:kc + 1])
            self.softmax_pv(Ssb[:, 0:nk], nk, c["Vt"], 0, o[:, 64 * h:64 * h + 64], c["kk"], premax=cm[:, 0:nkc])
        y = self.nx(c["y"])
        self.copy(y[:], o[:], eng="act")
        self.store_yT(y, c["YT"], 1, i, c["yTs"])

    def nsa_setup(self, FMS, TMB, TMF, cmp_w1, cmp_w2, cmp_pos, YT):
        S = self.S
        NT = S // 128
        NB = self.NB
        NCP = self.NCP
        NC = (S - 32) // 16 + 1
        NCT = NCP // 128
        c = dict(FMS=FMS, YT=YT)
        c["ksT"] = self.sb("ksT", [128, S], BF16)
        self.dma(c["ksT"][:], V(FMS.h[11 * 128:12 * 128, :], FMS.name))
        c["kwT"] = self.sb("kwT", [128, S], BF16)
        self.dma(c["kwT"][:], V(FMS.h[12 * 128:13 * 128, :], FMS.name))
        c["Vs"] = self.sb("Vs", [128, NT, 64], BF16)
        self.dma(c["Vs"][:], V(TMB.h[:, 576:640].rearrange("(n p) c -> p n c", p=128), TMB.name))
        c["Vw"] = self.sb("Vw", [128, NT, 64], BF16)
        self.dma(c["Vw"][:], V(TMB.h[:, 640:704].rearrange("(n p) c -> p n c", p=128), TMB.name))
        c["ngt"] = self.sb("ngt", [128, NT, 12], F32)
        self.dma(c["ngt"][:], V(TMF.h[:, 8:20].rearrange("(n p) c -> p n c", p=128), TMF.name))
        kcmp = self.sb("kcmp", [128, NCP], BF16)
        vcmp = self.sb("vcmp", [128, NCT, 64], BF16)
        c["kcmp"] = kcmp
        c["vcmp"] = vcmp
        c["Ssb"] = self.sb("nSsb", [128, S], F32)
        c["Sw"] = self.sb("Sw", [128, 640], F32)
        c["kk"] = self.attn_keys("n")
        save = self.sb_cur
        srcT = self.sb("srcT", [128, S], BF16)
        w1 = self.sb("w1", [64, 32, 64], BF16)
        w2 = self.sb("w2", [64, 128], BF16)
        posT = self.sb("posT", [64, 32], F32)
        posb = self.sb("posb", [64, 32], BF16)
        cst = self.sb("cst", [64, 1], F32)
        u = self.sb("u", [64, NCP], F32)
        u2 = self.sb("u2", [64, NCP], F32)
        gl = self.sb("gl", [64, NCP], BF16)
        for i in range(2):
            self.dma(srcT[:], V(FMS.h[(10 + 3 * i) * 128:(11 + 3 * i) * 128, :], FMS.name))
            self.dma(w1[:], V(cmp_w1.h[i].rearrange("(l d) f -> d l f", d=64), cmp_w1.name), eng="pool")
            self.dma(w2[:, 0:64], V(cmp_w2.h[i], cmp_w2.name), eng="pool")
            self.dma(w2[:, 64:128], V(cmp_w2.h[i], cmp_w2.name), eng="pool")
            self.dma_s(posT[:], V(cmp_pos.h[i].rearrange("l d -> d l"), cmp_pos.name))
            self.copy(posb[:], posT[:])
            psc = self.next_ps()
            for l in range(32):
                self.mm(psc[0:64, 0:1], w1[:, l, :], posb[:, l:l + 1], start=(l == 0), stop=(l == 31))
            self.copy(cst[:], psc[0:64, 0:1])
            psh = self.next_ps()
            for l in range(32):
                self.mm(psh[0:64, 0:NC], w1[:, l, :], srcT[0:64, l:l + 16 * (NC - 1) + 1:16], start=(l == 0), stop=(l == 31))
            self.memset(u[:], 0.0)
            self.act(u[:, 0:NC], psh[0:64, 0:NC], AF.Identity, bias=cst[:, 0:1])
            self.tt(u2[:], u[:], u[:], ALU.mult)
            self.tt(u2[:], u2[:], u[:], ALU.mult)
            self.stt(u2[:], u2[:], 0.044715, u[:], ALU.mult, ALU.add)
            self.act(u2[:], u2[:], AF.Tanh, scale=0.7978845608028654)
            self.ts(u2[:], u2[:], 1.0, 0.5, ALU.add, ALU.mult)
            self.tt(gl[:], u2[:], u[:], ALU.mult)
            if i == 0:
                pso = self.next_ps()
                self.mm(pso[:, 0:NCP], w2[:, :], gl[:, :])
                self.copy(kcmp[:], pso[:, 0:NCP])
            else:
                for ct in range(NCT):
                    pso = self.next_ps()
                    self.mm(pso[:, 0:64], gl[:, ct * 128:(ct + 1) * 128], w2[:, 0:64])
                    self.copy(vcmp[:, ct, :], pso[:, 0:64])
        self.sb_cur = save
        self.P.barrier()
        c["q"] = self.rot("nqi", 2, [128, 2, 128], BF16)
        c["Mn"] = self.rot("nMn", 1, [128, S], BF16)
        for nm, shp, dt_ in (("vis", [128, NCP], F32), ("pns", [128, NCP], F32), ("pn", [128, NCP], F32), ("Sc", [128, NCP], F32),
                             ("Pc", [128, NCP], F32), ("pnb", [128, NCP], BF16), ("PTc", [128, NCP], BF16), ("pnT", [128, NCP], F32),
                             ("cst2", [128, 8], F32), ("am", [128, NB], F32), ("imp", [128, NB], F32), ("imp2", [128, NB], F32), ("m8", [128, 16], F32),
                             ("selm", [128, NB], F32), ("cm", [128, 8], F32), ("oc", [128, 256], F32), ("os", [128, 256], F32), ("ow", [128, 256], F32),
                             ("gs", [128, 12], F32), ("o", [128, 256], F32), ("y", [128, 256], BF16), ("yTs", [128, 256], BF16)):
            c[nm] = self.rot("n" + nm, (1 if nm in ("pnT", "Pc", "vis", "imp2") else 2), shp, dt_)
        return c

    def nsa_tile(self, c, i):
        NB = self.NB
        NCP = self.NCP
        NCT = NCP // 128
        FMS = c["FMS"]
        Ssb = c["Ssb"]
        Sw = c["Sw"]
        kcmp = c["kcmp"]
        vcmp = c["vcmp"]

        def h4(t_):
            return V(t_.h[:, 0:256].rearrange("p (h e) -> p h e", h=4), t_.name)

        nk = 128 * (i + 1)
        nkc = (nk + 511) // 512
        nq = self.nx(c["q"])
        self.dma(nq[:], self.fm_rows(FMS, 8, 2, i * 128, (i + 1) * 128))
        vis = self.nx(c["vis"])
        self.memset(vis[:], 0.0)
        self.aselect(vis[:], vis[:], [[-16, NCP]], ALU.is_ge, -1000.0, 128 * i - 31, 1)
        pns = self.nx(c["pns"])
        oc = self.nx(c["oc"])
        osl = self.nx(c["os"])
        ow = self.nx(c["ow"])
        for h in range(4):
            base = 64 * (h % 2)
            cq = h // 2
            ps = self.next_ps()
            self.mm(ps[:, 0:NCP], nq[base:base + 64, cq, :], kcmp[base:base + 64, :])
            Sc = self.nx(c["Sc"])
            self.stt(Sc[:], ps[:, 0:NCP], 0.125, vis[:], ALU.mult, ALU.add)
            st = self.nx(c["cst2"])
            self.red(st[:, 0:1], Sc[:], ALU.max)
            self.ts(st[:, 0:1], st[:, 0:1], -500.0, -1.0, ALU.max, ALU.mult)
            Pc = self.nx(c["Pc"])
            self.act(Pc[:], Sc[:], AF.Exp, bias=st[:, 0:1], accum=st[:, 1:2])
            self.ts(st[:, 2:3], st[:, 1:2], 1e-30, None, ALU.max)
            self.recip(st[:, 3:4], st[:, 2:3])
            pn = pns if h == 0 else self.nx(c["pn"])
            self.ts(pn[:], Pc[:], st[:, 3:4], None, ALU.mult)
            pnb = self.nx(c["pnb"])
            self.copy(pnb[:], pn[:], eng="act")
            if h > 0:
                self.tt(pns[:], pns[:], pn[:], ALU.add, eng="pool")
            pb = self.next_psb()
            for ct in range(NCT):
                self.tr(pb[:, ct * 128:(ct + 1) * 128], pnb[:, ct * 128:(ct + 1) * 128], self.ident[:])
            PTc = self.nx(c["PTc"])
            self.copy(PTc[:], pb[:, 0:NCP], eng="act")
            po = self.next_ps()
            for ct in range(NCT):
                self.mm(po[:, 0:64], PTc[:, ct * 128:(ct + 1) * 128], vcmp[:, ct, :], start=(ct == 0), stop=(ct == NCT - 1))
            self.copy(oc[:, 64 * h:64 * h + 64], po[:, 0:64], eng="act")
        selm = self.nx(c["selm"])
        if NB > 16:
            pf = self.next_ps()
            for ct in range(NCT):
                self.tr(pf[:, ct * 128:(ct + 1) * 128], pns[:, ct * 128:(ct + 1) * 128], self.ident_f[:])
            pnT = self.nx(c["pnT"])
            self.copy(pnT[:], pf[:, 0:NCP], eng="act")
            pi = self.next_ps()
            for ct in range(NCT):
                self.mm(pi[:, 0:NB], pnT[:, ct * 128:(ct + 1) * 128], self.ovl[:, ct, :], start=(ct == 0), stop=(ct == NCT - 1))
            am = self.nx(c["am"])
            self.memset(am[:], 0.0)
            for half in range(2):
                cur = 2 * i + half
                r0 = 64 * half
                v_ = am[r0:r0 + 64, :]
                self.aselect(v_, v_, [[-1, NB]], ALU.is_ge, -1e30, cur, 0)
                self.memset(am[r0:r0 + 64, 0:1], 1e30)
                self.memset(am[r0:r0 + 64, cur:cur + 1], 1e30)
                if cur >= 1:
                    self.memset(am[r0:r0 + 64, cur - 1:cur], 1e30)
            imp = self.nx(c["imp"])
            self.tt(imp[:], pi[:, 0:NB], am[:], ALU.add)
            m8 = self.nx(c["m8"])
            self.vmax(m8[:, 0:8], imp[:])
            imp2 = self.nx(c["imp2"])
            self.match_replace(imp2[:], m8[:, 0:8], imp[:], -3.0e38)
            self.vmax(m8[:, 8:16], imp2[:])
            self.ts(selm[:], imp[:], m8[:, 15:16], 8000.0, ALU.is_ge, ALU.mult)
        else:
            self.memset(selm[:], 8000.0)
        Mn = self.nx(c["Mn"])
        nbk = nk // 64
        self.act(V(Mn.h[:, 0:nk].rearrange("p (b e) -> p b e", e=64), Mn.name),
                 V(selm.h[:, 0:nbk].unsqueeze(2).to_broadcast([128, nbk, 64]), selm.name), AF.Copy)
        self.tt(Mn[:, nk - 128:nk], Mn[:, nk - 128:nk], self.cnegb[:], ALU.add)
        for h in range(4):
            base = 64 * (h % 2)
            cq = h // 2
            cm = self.nx(c["cm"])
            for kc in range(nkc):
                c0 = kc * 512
                cols = min(512, nk - c0)
                ps = self.next_ps()
                self.mm(ps[:, 0:cols], nq[base:base + 64, cq, :], c["ksT"][base:base + 64, c0:c0 + cols], start=True, stop=False)
                self.mm(ps[:, 0:cols], self.ident[:], Mn[:, c0:c0 + cols], start=False, stop=True)
                self.ts(Ssb[:, c0:c0 + cols], ps[:, 0:cols], 0.125, None, ALU.mult, ALU.max, accum=cm[:, kc:kc + 1])
            self.softmax_pv(Ssb[:, 0:nk], nk, c["Vs"], 0, osl[:, 64 * h:64 * h + 64], c["kk"], premax=cm[:, 0:nkc])
        k0 = max(0, i * 128 - 512)
        nkw = nk - k0
        boff = 640 - nkw
        for h in range(4):
            base = 64 * (h % 2)
            cq = h // 2
            cm = self.nx(c["cm"])
            nwc = 0
            for c0 in range(0, nkw, 512):
                cols = min(512, nkw - c0)
                ps = self.next_ps()
                self.mm(ps[:, 0:cols], nq[base:base + 64, cq, :], c["kwT"][base:base + 64, k0 + c0:k0 + c0 + cols], start=True, stop=False)
                self.mm(ps[:, 0:cols], self.ident[:], self.bandb[:, boff + c0:boff + c0 + cols], start=False, stop=True)
                self.ts(Sw[:, c0:c0 + cols], ps[:, 0:cols], 0.125, None, ALU.mult, ALU.max, accum=cm[:, nwc:nwc + 1])
                nwc += 1
            self.softmax_pv(Sw[:, 0:nkw], nkw, c["Vw"], k0 // 128, ow[:, 64 * h:64 * h + 64], c["kk"], premax=cm[:, 0:nwc])
        gs = self.nx(c["gs"])
        self.act(gs[:], c["ngt"][:, i, :], AF.Sigmoid)
        o = self.nx(c["o"])

        def gbc(j):
            return V(gs.h[:, j:12:3].unsqueeze(2).to_broadcast([128, 4, 64]), gs.name)
        self.tt(h4(o), h4(oc), gbc(0), ALU.mult)
        self.tt(h4(osl), h4(osl), gbc(1), ALU.mult)
        self.tt(o[:], o[:], osl[:], ALU.add)
        self.tt(h4(ow), h4(ow), gbc(2), ALU.mult)
        self.tt(o[:], o[:], ow[:], ALU.add)
        y = self.nx(c["y"])
        self.copy(y[:], o[:], eng="act")
        self.store_yT(y, c["YT"], 2, i, c["yTs"])

    def phase_dsa_nsa(self, FMS, TMB, TMF, cmp_w1, cmp_w2, cmp_pos, YT):
        S = self.S
        NT = S // 128
        self.phase_begin()
        self.cnegb = self.sb("cnegb", [128, 128], BF16)
        self.ts(self.cnegb[:], self.cneg2k[:], 8.0, None, ALU.mult)
        self.bandb = self.sb("bandb", [128, 640], BF16)
        self.ts(self.bandb[:], self.band[:], 8.0, None, ALU.mult)
        cd = self.dsa_setup(FMS, TMB, TMF, YT)
        cn = self.nsa_setup(FMS, TMB, TMF, cmp_w1, cmp_w2, cmp_pos, YT)
        P = self.P
        for i in range(NT):
            self.ps_set = (0, 3)
            self.psb_set = (0, 1)
            P.capture = []
            self.dsa_tile(cd, i)
            A = P.capture
            self.ps_set = (3, 2)
            self.psb_set = (1, 1)
            P.capture = []
            self.nsa_tile(cn, i)
            B = P.capture
            P.capture = None
            self.ps_set = (0, 5)
            self.psb_set = (0, 2)
            ia = ib = 0
            na, nb = len(A), len(B)
            while ia < na or ib < nb:
                if ib >= nb or (ia < na and ia * nb <= ib * na):
                    P.add(*A[ia][0], **A[ia][1])
                    ia += 1
                else:
                    P.add(*B[ib][0], **B[ib][1])
                    ib += 1

    def phase_merge(self, x_in, xT_in, w_in_l, w_branch, w_out, ln_g, ln_b, YT, x_out, xT_out):
        S = self.S
        self.phase_begin()
        wg = self.load_w("wg", lambda k: V(w_in_l.h[k * 128:(k + 1) * 128, 3128:7224], w_in_l.name), NKC, 4096)
        wb = self.sb("wb", [128, 4, 2, 1024], BF16)
        for n in range(4):
            for kk_ in range(2):
                self.dma(wb[:, n, kk_, :], V(w_branch.h[n, kk_ * 128:(kk_ + 1) * 128, :], w_branch.name), eng="pool")
        wo = self.load_w("wo", lambda k: V(w_out.h[k * 128:(k + 1) * 128, :], w_out.name), NKC, D)
        g_bc, b_bc, scr = self.ln_setup(ln_g, ln_b)
        xt = self.sb("xT", [128, NKC, 512], BF16)
        yt = self.sb("yT", [128, 8, 512], BF16)
        mTs = [self.sb("mT%d" % i_, [128, 8, 512], BF16) for i_ in range(2)]
        acc_k = self.rot("acc", 3, [128, 512], F32)
        sg_k = self.rot("sg", 2, [128, 512], F32)
        tmp_k = self.rot("tmp", 2, [128, 512], F32)
        xr = [self.sb("xr%d" % i, [128, D], F32) for i in range(2)]
        rqs = [self.sb("r%d" % i, [128, D], F32) for i in range(2)]
        ntile = S // 512

        def load_xt(t_):
            ss_ = slice(t_ * 512, (t_ + 1) * 512)
            self.dma(xt[:], V(xT_in.h.rearrange("(k p) s -> p k s", p=128)[:, :, ss_], xT_in.name))
            self.dma(yt[:], V(YT.h[:, ss_].rearrange("(c p) s -> p c s", p=128), YT.name))

        def load_xq(idx):
            self.dma(xr[idx % 2][:], x_in[idx * 128:(idx + 1) * 128, :])
        load_xt(0)
        load_xq(0)
        for t in range(ntile):
            ss = slice(t * 512, (t + 1) * 512)
            mT = mTs[t % 2]
            for dc in range(8):
                acc = self.nx(acc_k)
                for n in range(4):
                    pg = self.next_ps()
                    for k in range(NKC):
                        self.mm(pg[:], wg[:, k, n * 1024 + dc * 128:n * 1024 + (dc + 1) * 128], xt[:, k, :],
                                start=(k == 0), stop=(k == NKC - 1))
                    pp = self.next_ps()
                    for k2 in range(2):
                        self.mm(pp[:], wb[:, n, k2, dc * 128:(dc + 1) * 128], yt[:, 2 * n + k2, :], start=(k2 == 0), stop=(k2 == 1))
                    sg = self.nx(sg_k)
                    self.act(sg[:], pg[:], AF.Sigmoid)
                    if n == 0:
                        self.tt(acc[:], sg[:], pp[:], ALU.mult)
                    else:
                        tmp = self.nx(tmp_k)
                        self.tt(tmp[:], sg[:], pp[:], ALU.mult)
                        self.tt(acc[:], acc[:], tmp[:], ALU.add, eng="pool")
                self.copy(mT[:, dc, :], acc[:], eng="act")
            if t + 1 < ntile:
                load_xt(t + 1)
            for q in range(4):
                t0 = t * 512 + q * 128
                xq = xr[q % 2]
                rq = rqs[q % 2]
                if t * 4 + q + 1 < ntile * 4:
                    load_xq(t * 4 + q + 1)
                for half in range(2):
                    hs = slice(half * 512, (half + 1) * 512)
                    pd = self.next_ps()
                    for dc in range(8):
                        self.mm(pd[:], mT[:, dc, q * 128:(q + 1) * 128], wo[:, dc, hs], start=(dc == 0), stop=(dc == 7))
                    self.act(xq[:, hs], xq[:, hs], AF.Copy, scale=ALPHA)
                    self.stt(rq[:, hs], pd[:], 1.0, xq[:, hs], ALU.mult, ALU.add)
                self.finish_tile(rq, g_bc, b_bc, x_out, xT_out, t0, scr)

    def phase_xattn(self, x_in, xT_in, mem, wq_d, wkv_d, wo_d, ln_g, ln_b, x_out, xT_out):
        S = self.S
        self.phase_begin()
        wkv = self.load_w("wkv", lambda k: V(wkv_d.h[k * 128:(k + 1) * 128, :], wkv_d.name), NKC, 2 * D)
        wq = self.load_w("wq", lambda k: V(wq_d.h[k * 128:(k + 1) * 128, :], wq_d.name), NKC, D)
        wo = self.load_w("wo", lambda k: V(wo_d.h[k * 128:(k + 1) * 128, :], wo_d.name), NKC, D)
        g_bc, b_bc, scr = self.ln_setup(ln_g, ln_b)
        memT = self.sb("memT", [128, 8, 256], BF16)
        mr = self.sb("mr", [128, D], F32)
        mb = self.sb("mb", [128, D], BF16)
        for mt in range(2):
            self.dma(mr[:], mem[mt * 128:(mt + 1) * 128, :])
            self.copy(mb[:], mr[:], eng="act")
            pb = self.next_psb()
            for k in range(8):
                self.tr(pb[:, k * 128:(k + 1) * 128], mb[:, k * 128:(k + 1) * 128], self.ident[:])
            self.copy(memT[:, :, mt * 128:(mt + 1) * 128], V(pb.h[:, :].rearrange("p (k t) -> p k t", k=8), pb.name))
        KT = self.sb("KT", [128, 8, 256], BF16)
        for c in range(8):
            ps = self.next_ps()
            for k in range(NKC):
                self.mm(ps[:, 0:256], wkv[:, k, c * 128:(c + 1) * 128], memT[:, k, :], start=(k == 0), stop=(k == NKC - 1))
            self.copy(KT[:, c, :], ps[:, 0:256], eng=("act" if c % 2 else "dve"))
        Vm = self.sb("Vm", [128, 2, D], BF16)
        for mt in range(2):
            for half in range(2):
                ps = self.next_ps()
                for k in range(NKC):
                    self.mm(ps[:], memT[:, k, mt * 128:(mt + 1) * 128], wkv[:, k, D + half * 512:D + (half + 1) * 512],
                            start=(k == 0), stop=(k == NKC - 1))
                self.copy(Vm[:, mt, half * 512:(half + 1) * 512], ps[:], eng=("act" if half else "dve"))
        xt = self.sb("xT", [128, NKC, 512], BF16)
        qT = self.sb("qT", [128, 8, 512], BF16)
        Pf_k = self.rot("Pf", 2, [128, 4, 256], F32)
        Pb_k = self.rot("Pb", 2, [128, 4, 256], BF16)
        PT_k = self.rot("PTx", 2, [128, 8, 128], BF16)
        oT_k = self.rot("oT", 2, [128, 8, 128], BF16)
        st_k = self.rot("xst", 2, [128, 16], F32)
        xr = [self.sb("xr%d" % i, [128, D], F32) for i in range(2)]
        rqs = [self.sb("r%d" % i, [128, D], F32) for i in range(2)]
        SC = 1.0 / 16
        ntile = S // 512

        def load_xt(t_):
            ss_ = slice(t_ * 512, (t_ + 1) * 512)
            self.dma(xt[:], V(xT_in.h.rearrange("(k p) s -> p k s", p=128)[:, :, ss_], xT_in.name))

        def load_xq(idx):
            self.dma(xr[idx % 2][:], x_in[idx * 128:(idx + 1) * 128, :])
        load_xt(0)
        load_xq(0)
        for t in range(ntile):
            ss = slice(t * 512, (t + 1) * 512)
            for c in range(8):
                ps = self.next_ps()
                for k in range(NKC):
                    self.mm(ps[:], wq[:, k, c * 128:(c + 1) * 128], xt[:, k, :], start=(k == 0), stop=(k == NKC - 1))
                self.copy(qT[:, c, :], ps[:], eng=("act" if c % 2 else "dve"))
            if t + 1 < ntile:
                load_xt(t + 1)
            for q in range(4):
                t0 = t * 512 + q * 128
                tq = slice(q * 128, (q + 1) * 128)
                if t * 4 + q + 1 < ntile * 4:
                    load_xq(t * 4 + q + 1)
                pss = [self.next_ps(), self.next_ps()]
                st = self.nx(st_k)
                Pf = self.nx(Pf_k)
                for h in range(4):
                    pv = pss[h // 2][:, (h % 2) * 256:(h % 2) * 256 + 256]
                    for cc in range(2):
                        self.mm(pv, qT[:, 2 * h + cc, tq], KT[:, 2 * h + cc, :], start=(cc == 0), stop=(cc == 1))
                    self.red(st[:, h:h + 1], pv, ALU.max)
                    self.ts(st[:, 4 + h:5 + h], st[:, h:h + 1], -SC, None, ALU.mult)
                    self.act(Pf[:, h, :], pv, AF.Exp, bias=st[:, 4 + h:5 + h], scale=SC, accum=st[:, 8 + h:9 + h])
                self.recip(st[:, 12:16], st[:, 8:12])
                Pb = self.nx(Pb_k)
                self.tt(Pb[:], Pf[:], V(st.h[:, 12:16].unsqueeze(2).to_broadcast([128, 4, 256]), st.name), ALU.mult)
                pb = self.next_psb()
                for h in range(4):
                    for mc in range(2):
                        j = 2 * h + mc
                        self.tr(pb[:, j * 128:(j + 1) * 128], Pb[:, h, mc * 128:(mc + 1) * 128], self.ident[:])
                PT = self.nx(PT_k)
                self.copy(PT[:], V(pb.h[:, :].rearrange("p (j t) -> p j t", j=8), pb.name))
                oT = self.nx(oT_k)
                pso = [self.next_ps(), self.next_ps()]
                for h in range(4):
                    for dc in range(2):
                        j = 2 * h + dc
                        pv = pso[j // 4][:, (j % 4) * 128:(j % 4) * 128 + 128]
                        for mc in range(2):
                            self.mm(pv, Vm[:, mc, h * 256 + dc * 128:h * 256 + (dc + 1) * 128], PT[:, 2 * h + mc, :],
                                    start=(mc == 0), stop=(mc == 1))
                for j4 in range(2):
                    self.copy(oT[:, 4 * j4:4 * j4 + 4, :], V(pso[j4].h[:, :].rearrange("p (j t) -> p j t", j=4), pso[j4].name),
                              eng=("act" if j4 else "dve"))
                xq = xr[q % 2]
                rq = rqs[q % 2]
                for half in range(2):
                    hs = slice(half * 512, (half + 1) * 512)
                    pd = self.next_ps()
                    for c in range(8):
                        self.mm(pd[:], oT[:, c, :], wo[:, c, hs], start=(c == 0), stop=(c == 7))
                    self.act(xq[:, hs], xq[:, hs], AF.Copy, scale=ALPHA)
                    self.stt(rq[:, hs], pd[:], 1.0, xq[:, hs], ALU.mult, ALU.add)
                self.finish_tile(rq, g_bc, b_bc, x_out, xT_out, t0, scr)


OFF = dict(r_q=0, r_k=128, r_v=256, r_g=512, d_q=768, d_k=1024, d_v=1088, i_q=1152, i_k=1408, i_w=1440,
           n_q=1448, n_kc=1704, n_vc=1768, n_ks=1832, n_vs=1896, n_kw=1960, n_vw=2024, n_g=2088,
           s_z=2100, s_xbc=2356, s_dt=3124, br_g=3128)


def _partner(i, headdim, rot):
    half = rot // 2
    j = i % headdim
    b = i - j
    if j < half:
        return b + j + half
    if j < rot:
        return b + j - half
    return i


def build_colidx():
    cols = []

    def roped(name, width, headdim, rot, lo=0, rep=1):
        loc = []
        for r in range(rep):
            loc += list(range(lo, lo + width))
        assert len(loc) == 128
        a = [OFF[name] + i for i in loc]
        b = [OFF[name] + _partner(i, headdim, rot) for i in loc]
        cols.extend(a)
        cols.extend(b)

    roped("r_q", 128, 32, 32)
    roped("r_k", 128, 32, 32)
    roped("d_q", 128, 64, 16, 0)
    roped("d_q", 128, 64, 16, 128)
    roped("d_k", 64, 64, 16, 0, 2)
    roped("i_q", 128, 32, 8, 0)
    roped("i_q", 128, 32, 8, 128)
    roped("i_k", 32, 32, 8, 0, 4)
    roped("n_q", 128, 64, 16, 0)
    roped("n_q", 128, 64, 16, 128)
    roped("n_kc", 64, 64, 16, 0, 2)
    roped("n_ks", 64, 64, 16, 0, 2)
    roped("n_kw", 64, 64, 16, 0, 2)
    cols.extend([OFF["n_vc"] + i for i in range(64)] * 2)
    cols.extend([OFF["s_xbc"] + i for i in range(768)])
    for name, w in (("r_v", 256), ("r_g", 256), ("d_v", 64), ("n_vs", 64), ("n_vw", 64), ("s_z", 256),
                    ("i_w", 8), ("n_g", 12), ("s_dt", 4)):
        cols.extend([OFF[name] + i for i in range(w)])
    return np.asarray(cols, dtype=np.int64)


ROPED_TABLES = [0, 1, 2, 2, 2, 3, 3, 3, 2, 2, 2, 2, 2]
TM0 = (2 * len(ROPED_TABLES) + 7) * 128
NCOL2 = TM0 + 984
NBIS = 17
RET_LNG = [math.log1p(-2.0 ** (-5 - h)) for h in range(4)]


def host_consts(S):
    meta = np.zeros((128, 32), np.float32)

    def fill(t, headdim, rot, theta, scale):
        half = rot // 2
        inv = np.power(np.float32(theta), (-2.0 * np.arange(half, dtype=np.float32) / np.float32(rot)).astype(np.float32)).astype(np.float32)
        for p in range(128):
            i = p % headdim
            if i < rot:
                meta[p, t] = inv[i % half]
                meta[p, 4 + t] = scale
                meta[p, 8 + t] = -scale if i < half else scale
            else:
                meta[p, t] = 0.0
                meta[p, 4 + t] = 1.0
                meta[p, 8 + t] = 0.0

    fill(0, 32, 32, 10000.0, 1.0)
    fill(1, 32, 32, 10000.0, 32.0 ** -0.5)
    fill(2, 64, 16, 500000.0, 1.0)
    fill(3, 32, 8, 500000.0, 1.0)
    for p in range(128):
        meta[p, 12] = RET_LNG[p // 32]
        meta[p, 13 + p // 32] = 1.0
    bdm = np.zeros((128, 256), np.float32)
    for p in range(128):
        bdm[p, 64 * (p // 32):64 * (p // 32) + 64] = 1.0
    NC = (S - 32) // 16 + 1
    NCP = (NC + 127) // 128 * 128
    NB = S // 64
    ovl = np.zeros((NCP, NB), np.float32)
    for c in range(NC):
        for j in range(NB):
            ovl[c, j] = max(min(16 * c + 32, 64 * j + 64) - max(16 * c, 64 * j), 0) / 32.0
    return meta, bdm, ovl, NB, NCP


STAGES = ["ffn1", "inproj", "ret", "ssd", "dsa", "nsa", "merge", "xattn", "ffn2"]


def build(S, depth=DEPTH, stop_after=None):
    kb = KB(S, depth, stop_after)
    meta_np, bdm_np, ovl_np, NB, NCP = host_consts(S)
    kb.n_keep = min(256, S // 4)
    EI = "ExternalInput"
    x = kb.dram("x", [S, D], F32, kind=EI)
    mem = kb.dram("mem", [N_MEM, D], F32, kind=EI)
    ln_g = kb.dram("ln_g", [DEPTH, 4, D], F32, kind=EI)
    ln_b = kb.dram("ln_b", [DEPTH, 4, D], F32, kind=EI)
    f1gu = kb.dram("ffn1_w_gu", [DEPTH, D, 2 * DFF], F32, kind=EI)
    f1dn = kb.dram("ffn1_w_down", [DEPTH, DFF, D], F32, kind=EI)
    w_in = kb.dram("w_in", [DEPTH, D, 7224], F32, kind=EI)
    w2 = kb.dram("w2", [DEPTH, D, NCOL2], F32, kind=EI)
    cmp_w1 = kb.dram("cmp_w1", [DEPTH, 2, 2048, 64], F32, kind=EI)
    cmp_w2 = kb.dram("cmp_w2", [DEPTH, 2, 64, 64], F32, kind=EI)
    cmp_pos = kb.dram("cmp_pos", [DEPTH, 2, 32, 64], F32, kind=EI)
    conv_w = kb.dram("conv_w", [DEPTH, 4, 768], F32, kind=EI)
    conv_b = kb.dram("conv_b", [DEPTH, 768], F32, kind=EI)
    dt_bias = kb.dram("dt_bias", [DEPTH, 4], F32, kind=EI)
    a_log = kb.dram("a_log", [DEPTH, 4], F32, kind=EI)
    d_skip = kb.dram("d_skip", [DEPTH, 4], F32, kind=EI)
    norm_g = kb.dram("ssm_norm_g", [DEPTH, 256], F32, kind=EI)
    w_branch = kb.dram("w_branch", [DEPTH, 4, 256, D], F32, kind=EI)
    w_out = kb.dram("w_out", [DEPTH, D, D], F32, kind=EI)
    xwq = kb.dram("xattn_wq", [DEPTH, D, D], F32, kind=EI)
    xwkv = kb.dram("xattn_wkv", [DEPTH, D, 2 * D], F32, kind=EI)
    xwo = kb.dram("xattn_wo", [DEPTH, D, D], F32, kind=EI)
    f2gu = kb.dram("ffn2_w_gu", [DEPTH, D, 2 * DFF], F32, kind=EI)
    f2dn = kb.dram("ffn2_w_down", [DEPTH, DFF, D], F32, kind=EI)
    meta = kb.dram("meta", [128, 32], F32, kind=EI)
    bdm = kb.dram("bdm", [128, 256], F32, kind=EI)
    ovl = kb.dram("ovl", [NCP, NB], F32, kind=EI)
    out = kb.dram("out", [S, D], F32)
    xTa = kb.dram("xTa", [D, S], BF16)
    xTb = kb.dram("xTb", [D, S], BF16)
    xa = kb.dram("xa", [S, D], F32)
    xb2 = kb.dram("xb2", [S, D], F32)
    FMS = kb.dram("FMS", [20 * 128, S], BF16)
    TMB = kb.dram("TMB", [S, 960], BF16)
    TMF = kb.dram("TMF", [S, 24], F32)
    YT = kb.dram("YT", [1024, S], BF16)
    ROPE = kb.dram("ROPE", [4, 2, 128, S], F32)
    kb.setup()
    kb.setup_consts(meta, bdm, ovl, NB, NCP)
    kb.phase_rope(ROPE)
    kb.phase_transpose_in(x, xTa)

    def L(t, *idx):
        return T(t.h[idx], t.name, True)

    done = False
    xin = x
    for l in range(depth):
        last = (l == depth - 1)

        def stop(name):
            return stop_after == (l, name)
        kb.phase_ffn(xin, xTa, L(f1gu, l), L(f1dn, l), L(ln_g, l, 0), L(ln_b, l, 0), xa, xTb)
        if stop("ffn1"):
            break
        kb.phase_inproj(xTb, L(w2, l), ROPE, FMS, TMB, TMF)
        if stop("inproj"):
            break
        kb.phase_ret(FMS, TMB, YT)
        if stop("ret"):
            break
        kb.phase_ssd(FMS, TMB, TMF, L(conv_w, l), L(conv_b, l), L(dt_bias, l), L(a_log, l), L(d_skip, l), L(norm_g, l), YT)
        if stop("ssd"):
            break
        kb.phase_dsa_nsa(FMS, TMB, TMF, L(cmp_w1, l), L(cmp_w2, l), L(cmp_pos, l), YT)
        if stop("nsa") or stop("dsa"):
            break
        kb.phase_merge(xa, xTb, L(w_in, l), L(w_branch, l), L(w_out, l), L(ln_g, l, 1), L(ln_b, l, 1), YT, xb2, xTa)
        if stop("merge"):
            break
        kb.phase_xattn(xb2, xTa, mem, L(xwq, l), L(xwkv, l), L(xwo, l), L(ln_g, l, 2), L(ln_b, l, 2), xa, xTb)
        if stop("xattn"):
            break
        kb.phase_ffn(xa, xTb, L(f2gu, l), L(f2dn, l), L(ln_g, l, 3), L(ln_b, l, 3), out if last else xb2, None if last else xTa)
        if stop("ffn2"):
            break
        xin = xb2
    kb.flush_pending()
    st = kb.P.emit()
    kb.stats = st
    return kb


def make_in_maps(inputs, S, ncores):
    meta_np, bdm_np, ovl_np, NB, NCP = host_consts(S)
    colidx = build_colidx()
    w_in = np.asarray(inputs["w_in"], dtype=np.float32)
    w2 = np.ascontiguousarray(w_in[:, :, colidx])
    shared = {k: np.ascontiguousarray(np.asarray(v, dtype=np.float32)) for k, v in inputs.items() if k not in ("x", "mem")}
    shared["w2"] = w2
    shared["meta"] = meta_np
    shared["bdm"] = bdm_np
    shared["ovl"] = ovl_np
    maps = []
    for b in range(ncores):
        m = dict(shared)
        m["x"] = np.ascontiguousarray(np.asarray(inputs["x"][b, :S], dtype=np.float32))
        m["mem"] = np.ascontiguousarray(np.asarray(inputs["mem"][b], dtype=np.float32))
        maps.append(m)
    return maps


def kernel(**inputs):
    S = inputs["x"].shape[1]
    B = inputs["x"].shape[0]
    kb = build(S)
    maps = make_in_maps(inputs, S, B)
    res = run_bass_kernel_spmd(kb.nc, maps, core_ids=list(range(B)))
    out = np.stack([np.asarray(r["out"], dtype=np.float32) for r in res.results], axis=0)
    return out
```

```python
import math
import sys
import numpy as np
import concourse.bass as bass
import concourse.mybir as mybir
from concourse.bass_utils import run_bass_kernel_spmd

F32 = mybir.dt.float32
BF16 = mybir.dt.bfloat16
I32 = mybir.dt.int32
AF = mybir.ActivationFunctionType
ALU = mybir.AluOpType
AX = mybir.AxisListType

SEM_LIMIT = 30000
N_DMA_SEMS = 24


class Buf:
    __slots__ = ("name", "last_w", "readers")

    def __init__(self, name):
        self.name = name
        self.last_w = None
        self.readers = []


class Op:
    __slots__ = ("eng", "fn", "deps", "need_inc", "sem", "val", "is_dma", "idx", "tag", "odeps", "n", "seg", "pfirst", "fin", "st0", "crit")


class Prog:
    def __init__(self, nc):
        self.nc = nc
        self.engs = {"pe": nc.tensor, "act": nc.scalar, "dve": nc.vector, "pool": nc.gpsimd, "sp": nc.sync}
        self.ops = []
        self.bufs = {}
        self.last_on = {}
        self.dmas_since = []
        self.phase_deps = []
        self.phase_bufs = set()
        self.capture = None
        self.seg = 0
        self.do_sched = True
        self.est_time = 0.0

    def buf(self, name):
        b = self.bufs.get(name)
        if b is None:
            b = self.bufs[name] = Buf(name)
        return b

    def add(self, eng, fn, reads=(), writes=(), dma=False, extra_deps=(), n=64, tag=None):
        if self.capture is not None:
            try:
                tg = (sys._getframe(2).f_lineno, 0)
            except Exception:
                tg = (0, 0)
            self.capture.append(((eng, fn), dict(reads=list(reads), writes=list(writes), dma=dma, n=n, tag=tg)))
            return None
        op = Op()
        op.eng = eng
        op.fn = fn
        op.is_dma = dma
        op.need_inc = False
        op.sem = None
        op.val = 0
        op.n = n
        op.seg = self.seg
        op.pfirst = False
        op.fin = 0.0
        op.idx = len(self.ops)
        if tag is not None:
            op.tag = tag
        else:
            try:
                op.tag = (sys._getframe(2).f_lineno, 0)
            except Exception:
                op.tag = (0, 0)
        deps = {}
        for b in reads:
            b = self.buf(b)
            w = b.last_w
            if w is not None:
                deps[w.idx] = (w, "raw")
        for b in writes:
            b = self.buf(b)
            w = b.last_w
            if w is not None and w.idx not in deps:
                deps[w.idx] = (w, "waw")
            for r in b.readers:
                if r.idx not in deps:
                    deps[r.idx] = (r, "war")
        real = []
        order = []
        for d, kind in deps.values():
            if (not d.is_dma) and d.eng == eng and not dma:
                if eng == "pe" or kind != "raw":
                    order.append(d)
                    continue
            real.append(d)
        for d in extra_deps:
            real.append(d)
        for b in list(reads) + list(writes):
            if b not in self.phase_bufs:
                self.phase_bufs.add(b)
                op.pfirst = True
        op.deps = real
        op.odeps = order
        for b in writes:
            b = self.buf(b)
            b.last_w = op
            b.readers = []
        for b in reads:
            self.buf(b).readers.append(op)
        self.ops.append(op)
        return op

    def barrier(self):
        self.seg += 1
        self.phase_bufs = set()

    def _cost(self, op):
        n = op.n
        e = op.eng
        if op.is_dma:
            return 0.08, 2.0 + n / 100e3
        if e == "pe":
            c = 0.035 + n / 2400.0
        elif e == "act":
            c = 0.22 + n / 1200.0
        elif e == "dve":
            c = 0.08 + n / 960.0
        elif e == "pool":
            c = 0.15 + n / 500.0
        else:
            c = 0.05
        return c, c

    def schedule(self):
        import heapq
        SCHED = self.do_sched
        self.seg_stats = []
        order = []
        ops = self.ops
        nseg = self.seg + 1
        segs = [[] for _ in range(nseg)]
        for op in ops:
            segs[op.seg].append(op)
        t_base = 0.0
        engs = list(self.engs.keys())
        for sg in segs:
            if not sg:
                continue
            if not SCHED:
                order.extend(sg)
                continue
            inseg = set(id(o) for o in sg)
            indeg = {}
            succ = {}
            dr = {}
            for op in sg:
                cnt = 0
                for d in op.deps + op.odeps:
                    if id(d) in inseg:
                        cnt += 1
                        succ.setdefault(id(d), []).append(op)
                indeg[id(op)] = cnt
                dr[id(op)] = t_base
            wait_h = {e: [] for e in engs}
            rdy_h = {e: [] for e in engs}
            free = {e: t_base for e in engs}
            for op in sg:
                if indeg[id(op)] == 0:
                    heapq.heappush(wait_h[op.eng], (dr[id(op)], op.idx, op))
            left = len(sg)
            tmax = t_base
            while left:
                best = None
                for e in engs:
                    wh = wait_h[e]
                    rh = rdy_h[e]
                    fe = free[e]
                    while wh and wh[0][0] <= fe:
                        _, ix, o = heapq.heappop(wh)
                        heapq.heappush(rh, (ix, o))
                    if rh:
                        cand = (fe, rh[0][0], e, 0)
                    elif wh:
                        cand = (wh[0][0], wh[0][1], e, 1)
                    else:
                        continue
                    if best is None or cand[:2] < best[:2]:
                        best = cand
                start, _, e, which = best
                if which == 0:
                    _, op = heapq.heappop(rdy_h[e])
                else:
                    _, _, op = heapq.heappop(wait_h[e])
                busy, lat = self._cost(op)
                free[e] = start + busy
                op.fin = start + lat
                op.st0 = start
                if op.fin > tmax:
                    tmax = op.fin
                order.append(op)
                left -= 1
                for sc in succ.get(id(op), ()):
                    k = id(sc)
                    extra = 0.05 if (sc.eng == op.eng and not op.is_dma) else 0.35
                    t = op.fin + extra
                    if t > dr[k]:
                        dr[k] = t
                    indeg[k] -= 1
                    if indeg[k] == 0:
                        heapq.heappush(wait_h[sc.eng], (dr[k], sc.idx, sc))
            busy_e = {e: 0.0 for e in engs}
            for o in sg:
                busy_e[o.eng] += self._cost(o)[0]
            self.seg_stats.append((sg[0].seg, len(sg), tmax - t_base, busy_e))
            t_base = tmax
        self.est_time = t_base
        return order

    def emit(self, final_wait_eng="sp"):
        nc = self.nc
        order = self.schedule()
        last_eng = {}
        prev_last = {}
        prev_dmas = []
        older_dmas = []
        cur_dmas = []
        cur_seg = -1
        for op in order:
            if op.seg != cur_seg:
                cur_seg = op.seg
                prev_last = dict(last_eng)
                prev_dmas = older_dmas + cur_dmas
                older_dmas = cur_dmas
                cur_dmas = []
            if op.pfirst:
                op.deps = op.deps + list(prev_last.values()) + prev_dmas
            if op.is_dma:
                cur_dmas.append(op)
            else:
                last_eng[op.eng] = op
        for op in order:
            for d in op.deps:
                d.need_inc = True
            if op.is_dma:
                op.need_inc = True
        eng_sem = {}
        eng_cnt = {}
        dma_sems = [nc.alloc_semaphore("dq%d" % i) for i in range(N_DMA_SEMS)]
        dma_cnt = [0] * N_DMA_SEMS
        dma_last = [None] * N_DMA_SEMS
        ndma = 0
        for op in order:
            if not op.need_inc:
                continue
            if op.is_dma:
                j = ndma % N_DMA_SEMS
                ndma += 1
                if dma_last[j] is not None:
                    op.deps.append(dma_last[j])
                dma_cnt[j] += 16
                op.sem = dma_sems[j]
                op.val = dma_cnt[j]
                dma_last[j] = op
            else:
                e = op.eng
                if e not in eng_sem or eng_cnt[e] >= SEM_LIMIT:
                    eng_sem[e] = nc.alloc_semaphore("s_%s_%d" % (e, op.idx))
                    eng_cnt[e] = 0
                eng_cnt[e] += 1
                op.sem = eng_sem[e]
                op.val = eng_cnt[e]
        waited = {}
        nwaits = 0
        for op in order:
            E = self.engs[op.eng]
            need = {}
            for d in op.deps:
                k = id(d.sem)
                if k not in need or need[k][1] < d.val:
                    need[k] = (d.sem, d.val)
            for k, (sem, val) in need.items():
                wk = (op.eng, k)
                if waited.get(wk, 0) >= val:
                    continue
                E.wait_ge(sem, val)
                nwaits += 1
                waited[wk] = val
            try:
                inst = op.fn()
            except Exception:
                print('EMIT FAIL at op', op.idx, op.eng)
                raise
            if op.need_inc:
                inst.then_inc(op.sem, 16 if op.is_dma else 1)
        E = self.engs[final_wait_eng]
        for j in range(N_DMA_SEMS):
            if dma_cnt[j] > 0:
                E.wait_ge(dma_sems[j], dma_cnt[j])
        self.stats = dict(n_ops=len(self.ops), n_waits=nwaits, n_dma=ndma,
                          n_inc=sum(1 for o in self.ops if o.need_inc), est_ms=self.est_time / 1e3)
        return self.stats


class V:
    __slots__ = ("ap", "b")

    def __init__(self, ap, b):
        self.ap = ap
        self.b = b


class T:
    def __init__(self, h, name, dram=False):
        self.h = h
        self.name = name
        self.dram = dram

    def __getitem__(self, idx):
        if self.dram:
            return V(self.h[idx], self.name)
        return V(self.h[idx], self.name)

    def v(self, ap):
        return V(ap, self.name)


DT_SIZE = {F32: 4, BF16: 2, I32: 4}

D = 1024
DFF = 2816
NKC = D // 128
NFC = DFF // 128
LN_EPS = 1e-5
DEPTH = 2
ALPHA = (2 * DEPTH) ** 0.25
N_MEM = 256


class KB:
    def __init__(self, S, depth=DEPTH, stop_after=None, debug=()):
        self.S = S
        self.depth = depth
        self.stop_after = stop_after
        self.debug = debug
        self.nc = bass.Bass("TRN2", target_bir_lowering=False)
        self.P = Prog(self.nc)
        self.uid = 0
        self.sb_base = 0
        self.sb_cur = 0
        self.outs = {}
        self.arena = None
        self.rots = {}
        self.pt_cnt = 0
        self.pending_T = None
        self.xb_cnt = 0
        self.ps_set = (0, 5)
        self.psb_set = (0, 2)
        self.fill_regs = {}
        self.n_keep = 256

    def sb(self, name, shape, dtype):
        nbytes = int(np.prod(shape[1:])) * DT_SIZE[dtype]
        nbytes = (nbytes + 63) // 64 * 64
        off = self.sb_cur
        self.sb_cur += nbytes
        assert self.sb_cur <= 207 * 1024, ("SBUF overflow", name, self.sb_cur)
        self.uid += 1
        if self.arena is None:
            self.arena = self.nc.alloc_sbuf_tensor("arena", [128, 207 * 1024], mybir.dt.uint8)
        ap = self.arena[:, off:off + int(np.prod(shape[1:])) * DT_SIZE[dtype]].bitcast(dtype)
        if len(shape) == 3:
            ap = ap.rearrange("p (a b) -> p a b", a=shape[1])
        elif len(shape) == 4:
            ap = ap.rearrange("p (a b c) -> p a b c", a=shape[1], b=shape[2])
        if shape[0] < 128:
            ap = ap[0:shape[0]]
        return T(ap, "%s_%d" % (name, self.uid))

    def phase_begin(self):
        self.flush_pending()
        self.P.barrier()
        self.sb_cur = self.sb_base

    def dram(self, name, shape, dtype, kind="ExternalOutput"):
        h = self.nc.dram_tensor(name, list(shape), dtype, kind=kind)
        return T(h.ap(), name, dram=True)

    def _rw(self, reads, writes):
        return [r.b for r in reads if isinstance(r, V)], [w.b for w in writes]

    def dma(self, out, in_, eng="sp"):
        nc = self.nc
        E = self.P.engs[eng]
        return self.P.add(eng, lambda: E.dma_start(out=out.ap, in_=in_.ap), reads=[in_.b], writes=[out.b], dma=True,
                          n=int(np.prod(out.ap.shape)) * 2)

    def mm(self, out, lhsT, rhs, start=True, stop=True):
        nc = self.nc
        return self.P.add("pe", lambda: nc.tensor.matmul(out.ap, lhsT.ap, rhs.ap, start=start, stop=stop),
                          reads=[lhsT.b, rhs.b], writes=[out.b], n=int(np.prod(out.ap.shape[1:])) * (4 if lhsT.ap.dtype == F32 else 1))

    def tr(self, out, in_, ident):
        nc = self.nc
        return self.P.add("pe", lambda: nc.tensor.transpose(out.ap, in_.ap, ident.ap),
                          reads=[in_.b, ident.b], writes=[out.b], n=200)

    def act(self, out, in_, func, bias=None, scale=None, accum=None, eng="act"):
        nc = self.nc
        kw = {}
        reads = [in_.b]
        writes = [out.b]
        if bias is not None:
            if isinstance(bias, V):
                kw["bias"] = bias.ap
                reads.append(bias.b)
            else:
                kw["bias"] = bias
        if scale is not None:
            if isinstance(scale, V):
                kw["scale"] = scale.ap
                reads.append(scale.b)
            else:
                kw["scale"] = scale
        if accum is not None:
            kw["accum_out"] = accum.ap
            writes.append(accum.b)
        return self.P.add("act", lambda: nc.scalar.activation(out=out.ap, in_=in_.ap, func=func, **kw),
                          reads=reads, writes=writes, n=int(np.prod(out.ap.shape[1:])))

    def ts(self, out, in0, s1, s2, op0, op1=None, accum=None, eng="dve"):
        E = self.P.engs[eng]
        reads = [in0.b]
        writes = [out.b]
        a1 = s1
        a2 = s2
        if isinstance(s1, V):
            a1 = s1.ap
            reads.append(s1.b)
        if isinstance(s2, V):
            a2 = s2.ap
            reads.append(s2.b)
        kw = {}
        if op1 is not None:
            kw["op1"] = op1
        if accum is not None:
            kw["accum_out"] = accum.ap
            writes.append(accum.b)
        return self.P.add(eng, lambda: E.tensor_scalar(out=out.ap, in0=in0.ap, scalar1=a1, scalar2=a2, op0=op0, **kw),
                          reads=reads, writes=writes, n=int(np.prod(out.ap.shape[1:])))

    def tt(self, out, in0, in1, op, eng="dve"):
        E = self.P.engs[eng]
        return self.P.add(eng, lambda: E.tensor_tensor(out=out.ap, in0=in0.ap, in1=in1.ap, op=op),
                          reads=[in0.b, in1.b], writes=[out.b], n=int(np.prod(out.ap.shape[1:])))

    def stt(self, out, in0, scalar, in1, op0, op1, accum=None):
        nc = self.nc
        reads = [in0.b, in1.b]
        writes = [out.b]
        a = scalar
        if isinstance(scalar, V):
            a = scalar.ap
            reads.append(scalar.b)
        kw = {}
        if accum is not None:
            kw["accum_out"] = accum.ap
            writes.append(accum.b)
        return self.P.add("dve", lambda: nc.vector.scalar_tensor_tensor(out=out.ap, in0=in0.ap, scalar=a, in1=in1.ap,
                                                                     op0=op0, op1=op1, **kw),
                          reads=reads, writes=writes, n=int(np.prod(out.ap.shape[1:])))

    def copy(self, out, in_, eng="dve"):
        E = self.P.engs[eng]
        if eng == "act":
            return self.P.add(eng, lambda: E.copy(out=out.ap, in_=in_.ap), reads=[in_.b], writes=[out.b], n=int(np.prod(out.ap.shape[1:])))
        return self.P.add(eng, lambda: E.tensor_copy(out=out.ap, in_=in_.ap), reads=[in_.b], writes=[out.b], n=int(np.prod(out.ap.shape[1:])))

    def memset(self, out, val, eng="pool"):
        E = self.P.engs[eng]
        return self.P.add(eng, lambda: E.memset(out.ap, val), writes=[out.b], n=int(np.prod(out.ap.shape[1:])))

    def red(self, out, in_, op, axis=AX.X, eng="dve"):
        E = self.P.engs[eng]
        return self.P.add(eng, lambda: E.tensor_reduce(out=out.ap, in_=in_.ap, axis=axis, op=op),
                          reads=[in_.b], writes=[out.b], n=int(np.prod(in_.ap.shape[1:])))

    def recip(self, out, in_):
        nc = self.nc
        return self.P.add("dve", lambda: nc.vector.reciprocal(out=out.ap, in_=in_.ap), reads=[in_.b], writes=[out.b])

    def aselect(self, out, in_, pattern, cmp, fill, base, cm):
        nc = self.nc
        regs = self.fill_regs

        def fn():
            if fill not in regs:
                regs[fill] = nc.gpsimd.to_reg(float(fill))
            return nc.gpsimd.affine_select(out=out.ap, in_=in_.ap, pattern=pattern, compare_op=cmp,
                                           fill=regs[fill], base=base, channel_multiplier=cm)
        return self.P.add("pool", fn, reads=[in_.b], writes=[out.b], n=int(np.prod(out.ap.shape[1:])))

    def iota(self, out, pattern, base, cm):
        nc = self.nc
        return self.P.add("pool", lambda: nc.gpsimd.iota(out.ap, pattern=pattern, base=base, channel_multiplier=cm,
                                                         allow_small_or_imprecise_dtypes=True), writes=[out.b], n=int(np.prod(out.ap.shape[1:])))

    def setup(self):
        nc = self.nc
        self.ps = []
        for i in range(5):
            h = nc.alloc_psum_tensor("ps%d" % i, [128, 512], F32)
            self.ps.append(T(h, "ps%d" % i))
        self.psb = []
        for i in range(2):
            h = nc.alloc_psum_tensor("psb%d" % i, [128, 1024], BF16)
            self.psb.append(T(h, "psb%d" % i))
        self.ps_rr = 0
        self.psb_rr = 0
        self.ident_f = self.sb("identf", [128, 128], F32)
        self.ident = self.sb("ident", [128, 128], BF16)
        self.memset(self.ident_f[:], 1.0)
        self.aselect(self.ident_f[:], self.ident_f[:], [[-1, 128]], ALU.is_equal, 0.0, 0, 1)
        self.copy(self.ident[:], self.ident_f[:], eng="pool")
        self.sb_base = self.sb_cur

    def next_ps(self):
        b0, n = self.ps_set
        t = self.ps[b0 + self.ps_rr % n]
        self.ps_rr += 1
        return t

    def next_psb(self):
        b0, n = self.psb_set
        t = self.psb[b0 + self.psb_rr % n]
        self.psb_rr += 1
        return t

    def load_w(self, name, dram_ap_fn, kchunks, ncols, eng="pool", split=4):
        w = self.sb(name, [128, kchunks, ncols], BF16)
        for k in range(kchunks):
            self.dma(w[:, k, :], dram_ap_fn(k), eng="pool")
        return w

    def layer_norm_tile(self, r, g_bc, b_bc, out_f32, scr):
        st = scr["st"]
        junk = scr["junk"]
        self.act(junk[:], r[:], AF.Identity, accum=st[:, 0:1])
        self.act(junk[:], r[:], AF.Square, accum=st[:, 1:2])
        self.ts(st[:, 2:3], st[:, 0:1], 1.0 / D, None, ALU.mult)
        self.tt(st[:, 3:4], st[:, 2:3], st[:, 2:3], ALU.mult)
        self.stt(st[:, 4:5], st[:, 1:2], 1.0 / D, st[:, 3:4], ALU.mult, ALU.subtract)
        self.ts(st[:, 4:5], st[:, 4:5], 0.0, LN_EPS, ALU.max, ALU.add)
        self.act(st[:, 5:6], st[:, 4:5], AF.Sqrt)
        self.recip(st[:, 6:7], st[:, 5:6])
        self.ts(out_f32[:], r[:], st[:, 2:3], st[:, 6:7], ALU.subtract, ALU.mult)
        self.tt(out_f32[:], out_f32[:], g_bc[:], ALU.mult)
        self.tt(out_f32[:], out_f32[:], b_bc[:], ALU.add)

    def store_xT(self, x_f32, xT_dram, t0, scr, defer=False):
        xbl = scr["xb"]
        if isinstance(xbl, list):
            xb = xbl[self.xb_cnt % len(xbl)]
            self.xb_cnt += 1
        else:
            xb = xbl
        xTs = scr["xTs"]
        self.copy(xb[:], x_f32[:], eng="act")

        def part_b():
            pb = self.next_psb()
            for k in range(NKC):
                self.tr(pb[:, k * 128:(k + 1) * 128], xb[:, k * 128:(k + 1) * 128], self.ident[:])
            self.copy(xTs[:], pb[:, :], eng="dve")
            self.dma(V(xT_dram.h.rearrange("(k p) s -> p k s", p=128)[:, :, t0:t0 + 128], xT_dram.name),
                     V(xTs.h[:].rearrange("p (k t) -> p k t", k=NKC), xTs.name))
        if defer:
            self.flush_pending()
            self.pending_T = part_b
        else:
            part_b()

    def flush_pending(self):
        if self.pending_T is not None:
            f = self.pending_T
            self.pending_T = None
            f()

    def dma_s(self, out, in_, eng="sp"):
        E = self.P.engs[eng]
        return self.P.add(eng, lambda: E.dma_start(out=out.ap, in_=in_.ap, allow_slow_non_contiguous=True),
                          reads=[in_.b], writes=[out.b], dma=True, n=int(np.prod(out.ap.shape)) * 8)

    def rot(self, name, n, shape, dtype):
        key = "_rot_" + name
        lst = [self.sb(name + str(i), shape, dtype) for i in range(n)]
        self.rots[key] = [lst, 0]
        return key

    def nx(self, key):
        lst, i = self.rots[key]
        self.rots[key][1] = i + 1
        return lst[i % len(lst)]

    def vmax(self, out, in_):
        nc = self.nc
        return self.P.add("dve", lambda: nc.vector.max(out=out.ap, in_=in_.ap), reads=[in_.b], writes=[out.b], n=int(np.prod(in_.ap.shape[1:])))

    def match_replace(self, out, rep, vals, imm):
        nc = self.nc
        return self.P.add("dve", lambda: nc.vector.match_replace(out=out.ap, in_to_replace=rep.ap, in_values=vals.ap, imm_value=imm),
                          reads=[rep.b, vals.b], writes=[out.b], n=int(np.prod(vals.ap.shape[1:])))

    def redabs(self, out, in_):
        nc = self.nc
        return self.P.add("dve", lambda: nc.vector.tensor_reduce(out=out.ap, in_=in_.ap, axis=AX.X, op=ALU.max,
                                                                 apply_absolute_value=True),
                          reads=[in_.b], writes=[out.b], n=int(np.prod(in_.ap.shape[1:])))

    def fm_rows(self, FMS, c0, nchunk, s0, s1):
        return V(FMS.h[c0 * 128:(c0 + nchunk) * 128, s0:s1].rearrange("(c p) s -> p c s", p=128), FMS.name)

    def setup_consts(self, meta, bdm, ovl, NB, NCP):
        S = self.S
        NT = S // 128
        self.NB = NB
        self.NCP = NCP
        self.meta = self.sb("meta", [128, 32], F32)
        self.dma(self.meta[:], meta[:, :])
        self.bdm = self.sb("bdm", [128, 256], F32)
        self.dma(self.bdm[:], bdm[:, :])
        self.ovl = self.sb("ovl", [128, NCP // 128, NB], F32)
        self.dma(self.ovl[:], V(ovl.h.rearrange("(c p) j -> p c j", p=128), ovl.name))
        self.U = self.sb("U", [128, 128], F32)
        self.memset(self.U[:], 1.0)
        self.aselect(self.U[:], self.U[:], [[1, 128]], ALU.is_ge, 0.0, 0, -1)
        self.cneg30 = self.sb("cneg30", [128, 128], F32)
        self.memset(self.cneg30[:], 0.0)
        self.aselect(self.cneg30[:], self.cneg30[:], [[-1, 128]], ALU.is_ge, -1e30, 0, 1)
        self.cneg2k = self.sb("cneg2k", [128, 128], F32)
        self.memset(self.cneg2k[:], 0.0)
        self.aselect(self.cneg2k[:], self.cneg2k[:], [[-1, 128]], ALU.is_ge, -2000.0, 0, 1)
        self.band = self.sb("band", [128, 640], F32)
        self.memset(self.band[:], 0.0)
        self.aselect(self.band[:], self.band[:], [[1, 640]], ALU.is_ge, -2000.0, -1, -1)
        self.aselect(self.band[:], self.band[:], [[-1, 640]], ALU.is_ge, -2000.0, 512, 1)
        self.decayT4 = self.sb("decayT4", [128, 4, 128], F32)
        self.xi = self.sb("xi", [128, 128], F32)
        self.zeta = self.sb("zeta", [128, 128], F32)
        self.cdecay = self.sb("cdecay", [128, 1], F32)
        self.rkc = self.sb("rkc", [128, 20], F32)
        self.sb_base = self.sb_cur
        dji = self.sb("dji", [128, 128], F32)
        self.iota(dji[:], [[1, 128]], 0, -1)
        for h in range(4):
            self.act(self.decayT4[:, h, :], dji[:], AF.Exp, scale=RET_LNG[h])
        self.tt(self.decayT4[:], self.decayT4[:], V(self.U.h[:, :].unsqueeze(1).to_broadcast([128, 4, 128]), self.U.name), ALU.mult)
        ip1 = self.sb("ip1", [128, 128], F32)
        self.iota(ip1[:], [[1, 128]], 1, 0)
        self.act(self.xi[:], ip1[:], AF.Exp, scale=self.meta[:, 12:13])
        jr = self.sb("jr", [128, 128], F32)
        self.iota(jr[:], [[0, 128]], 127, -1)
        for h in range(4):
            self.act(self.zeta[:, 32 * h:32 * h + 32], jr[:, 32 * h:32 * h + 32], AF.Exp, scale=RET_LNG[h])
        c128 = self.sb("c128", [128, 1], F32)
        self.memset(c128[:], 128.0)
        self.act(self.cdecay[:], c128[:], AF.Exp, scale=self.meta[:, 12:13])
        for k in range(20):
            self.memset(self.rkc[:, k:k + 1], 2.0 ** (-k))

    def build_addmask(self):
        NT = self.S // 128
        NB = self.NB
        self.addmask = self.sb("addmask", [128, NT, NB], F32)
        self.memset(self.addmask[:], 0.0)
        for i in range(NT):
            for half in range(2):
                cur = 2 * i + half
                r0 = 64 * half
                v = self.addmask[r0:r0 + 64, i, :]
                self.aselect(v, v, [[-1, NB]], ALU.is_ge, -1e30, cur, 0)
                self.memset(self.addmask[r0:r0 + 64, i, 0:1], 1e30)
                self.memset(self.addmask[r0:r0 + 64, i, cur:cur + 1], 1e30)
                if cur >= 1:
                    self.memset(self.addmask[r0:r0 + 64, i, cur - 1:cur], 1e30)

    def phase_rope(self, ROPE):
        S = self.S
        self.phase_begin()
        pos = self.sb("pos", [128, S], F32)
        self.iota(pos[:], [[1, S]], 0, 0)
        a = self.sb("a", [128, S], F32)
        ki = self.sb("ki", [128, S], I32)
        kf = self.sb("kf", [128, S], F32)
        m = self.sb("m", [128, S], F32)
        r = self.sb("r", [128, S], F32)
        PI = math.pi
        for t in range(4):
            for which in range(2):
                self.ts(a[:], pos[:], self.meta[:, t:t + 1], (PI / 2 if which == 0 else 0.0), ALU.mult, ALU.add)
                self.ts(kf[:], a[:], 1.0 / (2 * PI), None, ALU.mult)
                self.copy(ki[:], kf[:])
                self.copy(kf[:], ki[:])
                self.stt(r[:], kf[:], -2 * PI, a[:], ALU.mult, ALU.add)
                self.ts(m[:], r[:], PI, -2 * PI, ALU.is_gt, ALU.mult)
                self.tt(r[:], r[:], m[:], ALU.add)
                self.ts(m[:], r[:], -PI, 2 * PI, ALU.is_lt, ALU.mult)
                self.tt(r[:], r[:], m[:], ALU.add)
                self.ts(r[:], r[:], PI, -PI, ALU.min, ALU.max)
                self.act(r[:], r[:], AF.Sin)
                col = 4 + 4 * which + t
                self.ts(r[:], r[:], self.meta[:, col:col + 1], None, ALU.mult)
                self.dma(ROPE[t, which], r[:])

    def finish_tile(self, rq, g_bc, b_bc, x_out, xT_out, t0, scr):
        self.layer_norm_tile(rq, g_bc, b_bc, rq, scr)
        self.dma(x_out[t0:t0 + 128, :], rq[:])
        if xT_out is not None:
            self.store_xT(rq, xT_out, t0, scr, defer=True)

    def ln_setup(self, ln_g, ln_b):
        g_bc = self.sb("g_bc", [128, D], F32)
        b_bc = self.sb("b_bc", [128, D], F32)
        self.dma(g_bc[:], V(ln_g.h.partition_broadcast(128), ln_g.name))
        self.dma(b_bc[:], V(ln_b.h.partition_broadcast(128), ln_b.name))
        scr = dict(st=self.sb("st", [128, 8], F32), junk=self.sb("junk", [128, D], BF16),
                   xb=[self.sb("xb0", [128, D], BF16), self.sb("xb1", [128, D], BF16)], xTs=self.sb("xTs", [128, D], BF16))
        return g_bc, b_bc, scr

    def phase_ffn(self, x_in, xT_in, w_gu, w_down, ln_g, ln_b, x_out, xT_out):
        S = self.S
        self.phase_begin()
        wgu_v = w_gu.h.rearrange("(k p) c -> p k c", p=128)
        wgb = []
        for jb in range(NFC // 2):
            blk = self.sb("wgu%d" % jb, [128, NKC, 512], BF16)
            self.dma(blk[:, :, 0:256], V(wgu_v[:, :, jb * 256:(jb + 1) * 256], w_gu.name), eng="pool")
            self.dma(blk[:, :, 256:512], V(wgu_v[:, :, DFF + jb * 256:DFF + (jb + 1) * 256], w_gu.name), eng="pool")
            wgb.append(blk)
        wdn = self.load_w("wdn", lambda k: V(w_down.h[k * 128:(k + 1) * 128, :], w_down.name), NFC, D)
        g_bc, b_bc, scr = self.ln_setup(ln_g, ln_b)
        xt = self.sb("xT", [128, NKC, 512], BF16)
        hT = self.sb("hT", [128, NFC, 512], BF16)
        sg = [self.sb("sg%d" % i, [128, 512], BF16) for i in range(2)]
        xr = [self.sb("xr%d" % i, [128, D], F32) for i in range(2)]
        rqs = [self.sb("r%d" % i, [128, D], F32) for i in range(2)]
        ntile = S // 512

        def load_xt(t_):
            self.dma(xt[:], V(xT_in.h.rearrange("(k p) s -> p k s", p=128)[:, :, t_ * 512:(t_ + 1) * 512], xT_in.name))

        def load_xq(idx):
            self.dma(xr[idx % 2][:], x_in[idx * 128:(idx + 1) * 128, :])
        load_xt(0)
        load_xq(0)
        for t in range(ntile):
            for j in range(NFC):
                pg = self.next_ps()
                pu = self.next_ps()
                wb_ = wgb[j // 2]
                o_ = (j % 2) * 128
                for k in range(NKC):
                    self.mm(pg[:], wb_[:, k, o_:o_ + 128], xt[:, k, :], start=(k == 0), stop=(k == NKC - 1))
                for k in range(NKC):
                    self.mm(pu[:], wb_[:, k, 256 + o_:256 + o_ + 128], xt[:, k, :], start=(k == 0), stop=(k == NKC - 1))
                s = sg[j % 2]
                self.act(s[:], pg[:], AF.Silu)
                self.tt(hT[:, j, :], s[:], pu[:], ALU.mult)
            if t + 1 < ntile:
                load_xt(t + 1)
            for q in range(4):
                t0 = t * 512 + q * 128
                xq = xr[q % 2]
                rq = rqs[q % 2]
                if t * 4 + q + 1 < ntile * 4:
                    load_xq(t * 4 + q + 1)
                for half in range(2):
                    hs = slice(half * 512, (half + 1) * 512)
                    pd = self.next_ps()
                    for j in range(NFC):
                        self.mm(pd[:], hT[:, j, q * 128:(q + 1) * 128], wdn[:, j, hs], start=(j == 0), stop=(j == NFC - 1))
                    self.act(xq[:, hs], xq[:, hs], AF.Copy, scale=ALPHA)
                    self.stt(rq[:, hs], pd[:], 0.5, xq[:, hs], ALU.mult, ALU.add)
                self.finish_tile(rq, g_bc, b_bc, x_out, xT_out, t0, scr)

    def phase_transpose_in(self, x_in, xT_out):
        S = self.S
        self.phase_begin()
        xr = [self.sb("xr%d" % i, [128, D], F32) for i in range(2)]
        scr = dict(xb=self.sb("xb", [128, D], BF16), xTs=self.sb("xTs", [128, D], BF16))
        for i in range(S // 128):
            xq = xr[i % 2]
            self.dma(xq[:], x_in[i * 128:(i + 1) * 128, :])
            self.store_xT(xq, xT_out, i * 128, scr)

    def phase_inproj(self, xT_in, w2, ROPE, FMS, TMB, TMF):
        S = self.S
        self.phase_begin()
        w = self.sb("win", [128, NKC, NCOL2], BF16)
        w2_v = w2.h.rearrange("(k p) c -> p k c", p=128)
        bounds = list(range(0, 4096 + 1, 512)) + [TM0, TM0 + 512, NCOL2]
        for bi in range(len(bounds) - 1):
            a_, b_ = bounds[bi], bounds[bi + 1]
            self.P.add("pool", (lambda a=a_, b=b_: self.nc.gpsimd.dma_start(out=w.h[:, :, a:b], in_=w2_v[:, :, a:b])),
                       reads=[w2.name], writes=["win_b%d" % bi], dma=True, n=128 * NKC * (b_ - a_) * 2)

        def wv(k, a, b):
            for bi in range(len(bounds) - 1):
                if bounds[bi] <= a and b <= bounds[bi + 1]:
                    return V(w.h[:, k, a:b], "win_b%d" % bi)
            raise AssertionError((a, b))
        xts = [self.sb("xT%d" % i_, [128, NKC, 512], BF16) for i_ in range(2)]
        tabs = [self.sb("tab%d" % i_, [128, 4, 2, 512], F32) for i_ in range(2)]
        t1 = self.rot("t1", 3, [128, 512], F32)
        t2 = self.rot("t2", 3, [128, 512], F32)
        ob = self.rot("ob", 4, [128, 512], BF16)
        tmb = self.rot("tmb", 2, [128, 960], BF16)
        tmf = self.rot("tmf", 2, [128, 24], F32)
        ntile = S // 512

        def load_t(t_):
            ss_ = slice(t_ * 512, (t_ + 1) * 512)
            self.dma(xts[t_ % 2][:], V(xT_in.h.rearrange("(k p) s -> p k s", p=128)[:, :, ss_], xT_in.name))
            self.dma(tabs[t_ % 2][:], V(ROPE.h[:, :, :, ss_].rearrange("t w p s -> p t w s"), ROPE.name))
        load_t(0)
        for t in range(ntile):
            ss = slice(t * 512, (t + 1) * 512)
            xt = xts[t % 2]
            tab = tabs[t % 2]
            if t + 1 < ntile:
                load_t(t + 1)
            for ci, tb in enumerate(ROPED_TABLES):
                pA = self.next_ps()
                pB = self.next_ps()
                for k in range(NKC):
                    self.mm(pA[:], wv(k, (2 * ci) * 128, (2 * ci + 1) * 128), xt[:, k, :], start=(k == 0), stop=(k == NKC - 1))
                for k in range(NKC):
                    self.mm(pB[:], wv(k, (2 * ci + 1) * 128, (2 * ci + 2) * 128), xt[:, k, :], start=(k == 0), stop=(k == NKC - 1))
                a1 = self.nx(t1)
                a2 = self.nx(t2)
                o = self.nx(ob)
                self.tt(a1[:], pA[:], tab[:, tb, 0, :], ALU.mult)
                self.tt(a2[:], pB[:], tab[:, tb, 1, :], ALU.mult)
                self.tt(o[:], a1[:], a2[:], ALU.add)
                self.dma(V(FMS.h[ci * 128:(ci + 1) * 128, ss], FMS.name), o[:])
            nr = len(ROPED_TABLES)
            for j in range(7):
                wc = 2 * nr + j
                pA = self.next_ps()
                for k in range(NKC):
                    self.mm(pA[:], wv(k, wc * 128, (wc + 1) * 128), xt[:, k, :], start=(k == 0), stop=(k == NKC - 1))
                o = self.nx(ob)
                self.copy(o[:], pA[:], eng="act")
                self.dma(V(FMS.h[(nr + j) * 128:(nr + j + 1) * 128, ss], FMS.name), o[:])
            for q in range(4):
                t0 = t * 512 + q * 128
                pA = self.next_ps()
                pB = self.next_ps()
                for k in range(NKC):
                    self.mm(pA[:], xt[:, k, q * 128:(q + 1) * 128], wv(k, TM0, TM0 + 512), start=(k == 0), stop=(k == NKC - 1))
                for k in range(NKC):
                    self.mm(pB[:, 0:472], xt[:, k, q * 128:(q + 1) * 128], wv(k, TM0 + 512, TM0 + 984), start=(k == 0), stop=(k == NKC - 1))
                b = self.nx(tmb)
                f = self.nx(tmf)
                self.copy(b[:, 0:512], pA[:], eng="act")
                self.copy(b[:, 512:960], pB[:, 0:448])
                self.copy(f[:], pB[:, 448:472])
                self.dma(TMB[t0:t0 + 128, :], b[:])
                self.dma(TMF[t0:t0 + 128, :], f[:])

    def store_yT(self, y, YT, br, n, yTs_key):
        pb = self.next_psb()
        self.tr(pb[:, 0:128], y[:, 0:128], self.ident[:])
        self.tr(pb[:, 128:256], y[:, 128:256], self.ident[:])
        yTs = self.nx(yTs_key)
        self.copy(yTs[:], pb[:, 0:256])
        self.dma(V(YT.h[br * 256:(br + 1) * 256, n * 128:(n + 1) * 128].rearrange("(c p) t -> p c t", p=128), YT.name),
                 V(yTs.h[:, :].rearrange("p (c t) -> p c t", c=2), yTs.name))

    def phase_ret(self, FMS, TMB, YT):
        S = self.S
        NT = S // 128
        self.phase_begin()
        rq = self.sb("rq", [128, S], BF16)
        rk = self.sb("rk", [128, S], BF16)
        self.dma(rq[:], V(FMS.h[0:128, :], FMS.name))
        self.dma(rk[:], V(FMS.h[128:256, :], FMS.name))
        Sbd = self.sb("Sbd", [128, 256], F32)
        Sbd_bf = self.sb("Sbd_bf", [128, 256], BF16)
        self.memset(Sbd[:], 0.0)
        self.memset(Sbd_bf[:], 0.0)
        vt_k = self.rot("vt", 2, [128, 512], BF16)
        qxi_k = self.rot("qxi", 2, [128, 128], BF16)
        qm_k = self.rot("qm", 2, [128, 4, 128], BF16)
        kz_k = self.rot("kz", 2, [128, 128], BF16)
        PT_k = self.rot("PT", 2, [128, 4, 128], BF16)
        cross_k = self.rot("cross", 2, [128, 256], F32)
        o_k = self.rot("o", 2, [128, 256], F32)
        tmp_k = self.rot("tmp", 2, [128, 256], F32)
        osq_k = self.rot("osq", 2, [128, 256], F32)
        sg_k = self.rot("sg", 2, [128, 256], F32)
        st_k = self.rot("st", 2, [128, 16], F32)
        y_k = self.rot("y", 2, [128, 256], BF16)
        yTs_k = self.rot("yTs", 2, [128, 256], BF16)
        hm = V(self.meta.h[:, 13:17].unsqueeze(2).to_broadcast([128, 4, 128]), self.meta.name)
        for n in range(NT):
            sl = slice(n * 128, (n + 1) * 128)
            vt = self.nx(vt_k)
            self.dma(vt[:], TMB[n * 128:(n + 1) * 128, 0:512])
            qxi = self.nx(qxi_k)
            self.tt(qxi[:], rq[:, sl], self.xi[:], ALU.mult)
            qm = self.nx(qm_k)
            self.tt(qm[:], V(rq.h[:, sl].unsqueeze(1).to_broadcast([128, 4, 128]), rq.name), hm, ALU.mult, eng="pool")
            pb = self.next_psb()
            self.tr(pb[:, 0:128], rk[:, sl], self.ident[:])
            kz = self.nx(kz_k)
            self.tt(kz[:], pb[:, 0:128], self.zeta[:], ALU.mult)
            ps1 = self.next_ps()
            self.mm(ps1[:], rk[:, sl], V(qm.h[:, :, :].rearrange("p h i -> p (h i)"), qm.name))
            PT = self.nx(PT_k)
            self.tt(PT[:], V(ps1.h[:, :].rearrange("p (h i) -> p h i", h=4), ps1.name), self.decayT4[:], ALU.mult)
            ps2 = self.next_ps()
            self.mm(ps2[:, 0:256], qxi[:], Sbd_bf[:])
            cross = self.nx(cross_k)
            self.copy(cross[:], ps2[:, 0:256], eng="act")
            ps3 = self.next_ps()
            for h in range(4):
                self.mm(ps3[:, 64 * h:64 * h + 64], PT[:, h, :], vt[:, 64 * h:64 * h + 64])
            o = self.nx(o_k)
            self.tt(o[:], ps3[:, 0:256], cross[:], ALU.add)
            ps4 = self.next_ps()
            self.mm(ps4[:, 0:256], kz[:], vt[:, 0:256])
            tmp = self.nx(tmp_k)
            self.tt(tmp[:], ps4[:, 0:256], self.bdm[:], ALU.mult)
            self.stt(Sbd[:], Sbd[:], self.cdecay[:, 0:1], tmp[:], ALU.mult, ALU.add)
            self.copy(Sbd_bf[:], Sbd[:], eng="act")
            st = self.nx(st_k)
            o3 = V(o.h[:, :].rearrange("p (h e) -> p h e", h=4), o.name)
            self.red(st[:, 0:4], o3, ALU.add)
            osq = self.nx(osq_k)
            self.tt(osq[:], o[:], o[:], ALU.mult, eng="pool")
            self.red(st[:, 4:8], V(osq.h[:, :].rearrange("p (h e) -> p h e", h=4), osq.name), ALU.add)
            self.ts(st[:, 8:12], st[:, 0:4], 1.0 / 64, None, ALU.mult)
            self.tt(st[:, 12:16], st[:, 8:12], st[:, 8:12], ALU.mult)
            self.stt(st[:, 4:8], st[:, 4:8], 1.0 / 64, st[:, 12:16], ALU.mult, ALU.subtract)
            self.ts(st[:, 4:8], st[:, 4:8], 0.0, LN_EPS, ALU.max, ALU.add)
            self.act(st[:, 4:8], st[:, 4:8], AF.Sqrt)
            self.recip(st[:, 4:8], st[:, 4:8])
            self.tt(o3, o3, V(st.h[:, 8:12].unsqueeze(2).to_broadcast([128, 4, 64]), st.name), ALU.subtract)
            self.tt(o3, o3, V(st.h[:, 4:8].unsqueeze(2).to_broadcast([128, 4, 64]), st.name), ALU.mult)
            sg = self.nx(sg_k)
            self.act(sg[:], vt[:, 256:512], AF.Silu)
            y = self.nx(y_k)
            self.tt(y[:], o[:], sg[:], ALU.mult)
            self.store_yT(y, YT, 0, n, yTs_k)

    def phase_ssd(self, FMS, TMB, TMF, conv_w, conv_b, dt_bias, a_log, d_skip, norm_g, YT):
        S = self.S
        NT = S // 128
        self.phase_begin()
        cw = self.sb("cw", [128, 6, 4], F32)
        for k_ in range(4):
            self.dma_s(cw[:, :, k_], V(conv_w.h[k_].rearrange("(c p) -> p c", p=128), conv_w.name))
        cb = self.sb("cb", [128, 6], F32)
        self.dma_s(cb[:], V(conv_b.h.rearrange("(c p) -> p c", p=128), conv_b.name))
        dtb = self.sb("dtb", [128, 4], F32)
        self.dma(dtb[:], V(dt_bias.h.partition_broadcast(128), dt_bias.name))
        a_bc = self.sb("a_bc", [128, 4], F32)
        self.dma(a_bc[:], V(a_log.h.partition_broadcast(128), a_log.name))
        self.act(a_bc[:], a_bc[:], AF.Exp)
        self.ts(a_bc[:], a_bc[:], -1.0, None, ALU.mult)
        Dbc = self.sb("Dbc", [128, 4], F32)
        self.dma(Dbc[:], V(d_skip.h.partition_broadcast(128), d_skip.name))
        ng_bc = self.sb("ng_bc", [128, 256], F32)
        self.dma(ng_bc[:], V(norm_g.h.partition_broadcast(128), norm_g.name))
        xbcs = self.sb("xbcs", [128, 6, S], BF16)
        raw_k = self.rot("raw", 2, [128, 6, 515], BF16)
        acc_k = self.rot("acc", 2, [128, 512], F32)
        for t in range(S // 512):
            raw = self.nx(raw_k)
            if t == 0:
                self.memset(raw[:, :, 0:3], 0.0)
                self.dma(raw[:, :, 3:515], self.fm_rows(FMS, 14, 6, 0, 512))
            else:
                self.dma(raw[:, :, 0:515], self.fm_rows(FMS, 14, 6, t * 512 - 3, (t + 1) * 512))
            for c in range(6):
                acc = self.nx(acc_k)
                self.ts(acc[:], raw[:, c, 3:515], cw[:, c, 3:4], None, ALU.mult)
                for k in (2, 1, 0):
                    self.stt(acc[:], raw[:, c, k:k + 512], cw[:, c, k:k + 1], acc[:], ALU.mult, ALU.add)
                self.act(xbcs[:, c, t * 512:(t + 1) * 512], acc[:], AF.Silu, bias=cb[:, c:c + 1])
        prev = self.sb("prev", [128, 256], F32)
        prev_bf = self.sb("prev_bf", [128, 256], BF16)
        self.memset(prev[:], 0.0)
        self.memset(prev_bf[:], 0.0)
        xsB_k = self.rot("xsB", 2, [128, 512], BF16)
        tmf_k = self.rot("tmf", 2, [128, 24], F32)
        zt_k = self.rot("zt", 2, [128, 256], BF16)
        st_k = self.rot("st", 2, [128, 32], F32)
        adtb_k = self.rot("adtb", 2, [128, 4, 128], F32)
        seg_k = self.rot("seg", 2, [128, 4, 128], F32)
        MT_k = self.rot("MT", 2, [128, 4, 128], BF16)
        X_k = self.rot("X", 2, [128, 256], BF16)
        Xd_k = self.rot("Xd", 2, [128, 256], BF16)
        yd_k = self.rot("yd", 2, [128, 256], F32)
        y_k = self.rot("y", 2, [128, 256], F32)
        t2_k = self.rot("t2", 2, [128, 256], F32)
        sz_k = self.rot("sz", 2, [128, 256], F32)
        yb_k = self.rot("yb", 2, [128, 256], BF16)
        yTs_k = self.rot("yTs", 2, [128, 256], BF16)
        Ubc = V(self.U.h[:, :].unsqueeze(1).to_broadcast([128, 4, 128]), self.U.name)

        def h4(t_):
            return V(t_.h[:, 0:256].rearrange("p (h e) -> p h e", h=4), t_.name)

        def bc4(v_):
            return V(v_.ap.unsqueeze(2).to_broadcast([128, 4, 64]), v_.b)

        for n in range(NT):
            sl = slice(n * 128, (n + 1) * 128)
            pb = self.next_psb()
            for c in range(4):
                self.tr(pb[:, c * 128:(c + 1) * 128], xbcs[:, c, sl], self.ident[:])
            xsB = self.nx(xsB_k)
            self.copy(xsB[:], pb[:, 0:512])
            tmf = self.nx(tmf_k)
            self.dma(tmf[:], TMF[n * 128:(n + 1) * 128, :])
            zt = self.nx(zt_k)
            self.dma(zt[:], TMB[n * 128:(n + 1) * 128, 704:960])
            st = self.nx(st_k)
            self.tt(st[:, 0:4], tmf[:, 20:24], dtb[:], ALU.add)
            self.act(st[:, 0:4], st[:, 0:4], AF.Exp)
            self.act(st[:, 0:4], st[:, 0:4], AF.Ln, bias=1.0)
            self.tt(st[:, 4:8], st[:, 0:4], a_bc[:], ALU.mult)
            adtb = self.nx(adtb_k)
            self.copy(adtb[:], V(st.h[:, 4:8].unsqueeze(2).to_broadcast([128, 4, 128]), st.name))
            psA = self.next_ps()
            self.mm(psA[:, 0:4], self.U[:], st[:, 4:8])
            self.copy(st[:, 8:12], psA[:, 0:4], eng="act")
            psB = self.next_ps()
            for h in range(4):
                self.mm(psB[:, h * 128:(h + 1) * 128], adtb[:, h, :], self.U[:])
            seg = self.nx(seg_k)
            for h in range(4):
                self.ts(seg[:, h, :], psB[:, h * 128:(h + 1) * 128], st[:, 8 + h:9 + h], 0.0, ALU.subtract, ALU.min)
            self.act(seg[:], seg[:], AF.Exp)
            self.tt(seg[:], seg[:], Ubc, ALU.mult, eng="pool")
            alast = V(psB.h[:, 127:512:128], psB.name)
            self.tt(st[:, 12:16], alast, st[:, 8:12], ALU.subtract)
            self.act(st[:, 12:16], st[:, 12:16], AF.Exp)
            self.act(st[:, 16:20], alast, AF.Exp)
            self.act(st[:, 20:24], st[:, 8:12], AF.Exp)
            psG = self.next_ps()
            for g in range(2):
                self.mm(psG[:, g * 128:(g + 1) * 128], xbcs[:, 2 + g, sl], xbcs[:, 4 + g, sl])
            MT = self.nx(MT_k)
            for g in range(2):
                self.tt(MT[:, 2 * g:2 * g + 2, :], seg[:, 2 * g:2 * g + 2, :],
                        V(psG.h[:, g * 128:(g + 1) * 128].unsqueeze(1).to_broadcast([128, 2, 128]), psG.name), ALU.mult)
            X = self.nx(X_k)
            self.tt(h4(X), h4(xsB), bc4(st[:, 0:4]), ALU.mult)
            psY = self.next_ps()
            for h in range(4):
                self.mm(psY[:, 64 * h:64 * h + 64], MT[:, h, :], X[:, 64 * h:64 * h + 64])
            psO = self.next_ps()
            for g in range(2):
                self.mm(psO[:, 128 * g:128 * g + 128], xbcs[:, 4 + g, sl], prev_bf[:, 128 * g:128 * g + 128])
            yd = self.nx(yd_k)
            self.copy(yd[:], psY[:, 0:256], eng="act")
            y = self.nx(y_k)
            self.tt(h4(y), h4(psO), bc4(st[:, 20:24]), ALU.mult)
            self.tt(y[:], y[:], yd[:], ALU.add)
            t2 = self.nx(t2_k)
            self.tt(h4(t2), h4(xsB), bc4(Dbc[:, 0:4]), ALU.mult, eng="pool")
            self.tt(y[:], y[:], t2[:], ALU.add)
            Xd = self.nx(Xd_k)
            self.tt(h4(Xd), h4(X), bc4(st[:, 12:16]), ALU.mult, eng="pool")
            psS = self.next_ps()
            for g in range(2):
                self.mm(psS[:, 128 * g:128 * g + 128], xsB[:, 256 + 128 * g:256 + 128 * g + 128], Xd[:, 128 * g:128 * g + 128])
            self.tt(h4(prev), h4(prev), bc4(st[:, 16:20]), ALU.mult)
            self.tt(prev[:], prev[:], psS[:, 0:256], ALU.add)
            self.copy(prev_bf[:], prev[:], eng="act")
            sz = self.nx(sz_k)
            self.act(sz[:], zt[:], AF.Silu)
            self.tt(y[:], y[:], sz[:], ALU.mult)
            self.tt(t2[:], y[:], y[:], ALU.mult, eng="pool")
            self.red(st[:, 24:26], V(t2.h[:, :].rearrange("p (g e) -> p g e", g=2), t2.name), ALU.add)
            self.ts(st[:, 24:26], st[:, 24:26], 1.0 / 128, LN_EPS, ALU.mult, ALU.add)
            self.act(st[:, 24:26], st[:, 24:26], AF.Sqrt)
            self.recip(st[:, 24:26], st[:, 24:26])
            y2 = V(y.h[:, :].rearrange("p (g e) -> p g e", g=2), y.name)
            self.tt(y2, y2, V(st.h[:, 24:26].unsqueeze(2).to_broadcast([128, 2, 128]), st.name), ALU.mult)
            yb = self.nx(yb_k)
            self.tt(yb[:], y[:], ng_bc[:], ALU.mult)
            self.store_yT(yb, YT, 3, n, yTs_k)

    def softmax_pv(self, Ssb, nk, Vt, kt0, out, kk, clamp=None, premax=None):
        st = self.nx(kk["st"])
        self.red(st[:, 0:1], (Ssb if premax is None else premax), ALU.max)
        if clamp is not None:
            self.ts(st[:, 0:1], st[:, 0:1], clamp, None, ALU.max)
        self.ts(st[:, 1:2], st[:, 0:1], -1.0, None, ALU.mult)
        Ps = kk["Ps"]
        ng = (nk + 1023) // 1024
        for g in range(ng):
            a = g * 1024
            b = min(nk, a + 1024)
            self.act(Ps[g][:, 0:b - a], V(Ssb.ap[:, a:b], Ssb.b), AF.Exp, bias=st[:, 1:2], accum=st[:, 8 + g:9 + g])
        if ng > 1:
            self.red(st[:, 2:3], st[:, 8:8 + ng], ALU.add)
            ssum = st[:, 2:3]
        else:
            ssum = st[:, 8:9]
        self.ts(st[:, 3:4], ssum, 1e-30, None, ALU.max)
        self.recip(st[:, 4:5], st[:, 3:4])
        po = self.next_ps()
        nkt = nk // 128
        for g0 in range(0, nkt, 8):
            gn = min(8, nkt - g0)
            Pg = Ps[g0 // 8]
            pb = self.next_psb()
            for j in range(gn):
                self.tr(pb[:, j * 128:(j + 1) * 128], Pg[:, j * 128:(j + 1) * 128], self.ident[:])
            PT = self.nx(kk["PT"])
            self.pt_cnt += 1
            self.copy(PT[:, 0:gn * 128], pb[:, 0:gn * 128], eng=("act" if self.pt_cnt % 3 else "dve"))
            for j in range(gn):
                self.mm(po[:, 0:64], PT[:, j * 128:(j + 1) * 128], Vt[:, kt0 + g0 + j, :],
                        start=(g0 + j == 0), stop=(g0 + j == nkt - 1))
        self.ts(out, po[:, 0:64], st[:, 4:5], None, ALU.mult)

    def attn_keys(self, pfx):
        S = self.S
        npc = (S + 1023) // 1024
        return dict(st=self.rot(pfx + "sst", 2, [128, 16], F32),
                    Ps=[self.sb(pfx + "P%d" % g_, [128, 1024], BF16) for g_ in range(npc)],
                    PT=self.rot(pfx + "PTa", 2, [128, 1024], BF16))

    def dsa_setup(self, FMS, TMB, TMF, YT):
        S = self.S
        NT = S // 128
        c = dict(FMS=FMS, YT=YT)
        c["dk"] = self.sb("dk", [128, S], BF16)
        self.dma(c["dk"][:], V(FMS.h[4 * 128:5 * 128, :], FMS.name))
        c["ikr"] = self.sb("ikr", [128, S], BF16)
        self.dma(c["ikr"][:], V(FMS.h[7 * 128:8 * 128, :], FMS.name))
        c["qm"] = self.rot("dqm", 2, [128, 8, 128], BF16)
        c["Vt"] = self.sb("Vt", [128, NT, 64], BF16)
        self.dma(c["Vt"][:], V(TMB.h[:, 512:576].rearrange("(n p) c -> p n c", p=128), TMB.name))
        iw = self.sb("iw", [128, NT, 8], F32)
        self.dma(iw[:], V(TMF.h[:, 0:8].rearrange("(n p) c -> p n c", p=128), TMF.name))
        c["absw"] = self.sb("absw", [128, NT, 8], F32)
        self.act(c["absw"][:], iw[:], AF.Abs, scale=1.0 / 16)
        c["sgn"] = self.sb("sgn", [128, NT, 8], F32)
        self.ts(c["sgn"][:], iw[:], 0.0, 2.0, ALU.is_ge, ALU.mult)
        self.ts(c["sgn"][:], c["sgn"][:], -1.0, None, ALU.add)
        c["I"] = self.rot("I", 1, [128, S], F32)
        c["Ssb"] = self.rot("dSsb", 1, [128, S], F32)
        c["Mb"] = self.rot("dMb", 2, [128, S], BF16)
        c["cm"] = self.rot("dcm", 2, [128, 8], F32)
        c["junk"] = self.sb("djunk", [128, S], BF16)
        c["kk"] = self.attn_keys("d")
        c["q"] = self.rot("dqi", 2, [128, 4, 128], BF16)
        c["tmp"] = self.rot("tmpr", 2, [128, 512], F32)
        c["st"] = self.rot("dst", 2, [128, 16], F32)
        c["Rk"] = self.rot("Rk", 2, [128, 20], F32)
        c["nm"] = self.rot("dnm", 2, [128, 2], F32)
        c["c2"] = self.rot("dc2", 2, [128, 2], F32)
        c["o"] = self.rot("do", 2, [128, 256], F32)
        c["y"] = self.rot("dy", 2, [128, 256], BF16)
        c["yTs"] = self.rot("dyTs", 2, [128, 256], BF16)
        return c

    def dsa_tile(self, c, i):
        FMS = c["FMS"]
        I = self.nx(c["I"])
        Ssb = self.nx(c["Ssb"])
        nk = 128 * (i + 1)
        nkc = (nk + 511) // 512
        q = self.nx(c["q"])
        self.dma(q[:, 0:2, :], self.fm_rows(FMS, 2, 2, i * 128, (i + 1) * 128))
        self.dma(q[:, 2:4, :], self.fm_rows(FMS, 5, 2, i * 128, (i + 1) * 128))
        qm = self.nx(c["qm"])
        hm = V(self.meta.h[:, 13:17].unsqueeze(2).to_broadcast([128, 4, 128]), self.meta.name)
        for cc in range(2):
            self.tt(qm[:, 4 * cc:4 * cc + 4, :], V(q.h[:, 2 + cc, :].unsqueeze(1).to_broadcast([128, 4, 128]), q.name), hm,
                    ALU.mult, eng="pool")
        for kc in range(nkc):
            c0 = kc * 512
            cols = min(512, nk - c0)
            for h in range(8):
                ps = self.next_ps()
                self.mm(ps[:, 0:cols], qm[:, h, :], c["ikr"][:, c0:c0 + cols])
                tmp = self.nx(c["tmp"])
                self.act(tmp[:, 0:cols], ps[:, 0:cols], AF.Relu, scale=c["absw"][:, i, h:h + 1])
                if h == 0:
                    self.ts(I[:, c0:c0 + cols], tmp[:, 0:cols], c["sgn"][:, i, 0:1], None, ALU.mult)
                else:
                    self.stt(I[:, c0:c0 + cols], tmp[:, 0:cols], c["sgn"][:, i, h:h + 1], I[:, c0:c0 + cols], ALU.mult, ALU.add)
        if nk > self.n_keep:
            st = self.nx(c["st"])
            junk = c["junk"]
            self.redabs(st[:, 0:1], I[:, 0:nk])
            self.ts(st[:, 0:1], st[:, 0:1], 1e-20, None, ALU.max)
            self.tt(I[:, nk - 128:nk], I[:, nk - 128:nk], self.cneg30[:], ALU.add)
            Rk = self.nx(c["Rk"])
            self.ts(Rk[:], self.rkc[:], st[:, 0:1], None, ALU.mult)
            self.ts(st[:, 1:2], st[:, 0:1], -1.0, None, ALU.mult)
            n1 = (nk // 2 + 127) // 128 * 128
            n2 = nk - n1
            thr_c = self.n_keep - 0.5 - n2 / 2.0
            for k in range(NBIS):
                nm = self.nx(c["nm"])
                c2 = self.nx(c["c2"])
                self.tt(nm[:, 0:1], st[:, 1:2], Rk[:, k:k + 1], ALU.add)
                self.act(Ssb[:, n1:nk], I[:, n1:nk], AF.Sign, bias=nm[:, 0:1], scale=-1.0, accum=c2[:, 0:1])
                self.ts(junk[:, 0:n1], I[:, 0:n1], nm[:, 0:1], None, ALU.is_ge, ALU.add, accum=st[:, 3:4])
                self.stt(st[:, 4:5], c2[:, 0:1], -0.5, st[:, 3:4], ALU.mult, ALU.add)
                self.ts(st[:, 4:5], st[:, 4:5], thr_c, None, ALU.is_ge)
                self.stt(st[:, 1:2], st[:, 4:5], Rk[:, k:k + 1], st[:, 1:2], ALU.mult, ALU.add)
            Mb = self.nx(c["Mb"])
            self.ts(Mb[:, 0:nk], I[:, 0:nk], st[:, 1:2], 8000.0, ALU.is_ge, ALU.mult)
        else:
            Mb = self.nx(c["Mb"])
            self.memset(Mb[:, 0:nk], 8000.0)
            self.tt(Mb[:, nk - 128:nk], Mb[:, nk - 128:nk], self.cnegb[:], ALU.add)
        o = self.nx(c["o"])
        for h in range(4):
            base = 64 * (h % 2)
            cq = h // 2
            cm = self.nx(c["cm"])
            for kc in range(nkc):
                c0 = kc * 512
                cols = min(512, nk - c0)
                ps = self.next_ps()
                self.mm(ps[:, 0:cols], q[base:base + 64, cq, :], c["dk"][base:base + 64, c0:c0 + cols], start=True, stop=False)
                self.mm(ps[:, 0:cols], self.ident[:], Mb[:, c0:c0 + cols], start=False, stop=True)
                self.ts(Ssb[:, c0:c0 + cols], ps[:, 0:cols], 0.125, None, ALU.mult, ALU.max, accum=cm[:, kc:kc + 1])
            self.softmax_pv(Ssb[:, 0:nk], nk, c["Vt"], 0, o[:, 64 * h:64 * h + 64], c["kk"], premax=cm[:, 0:nkc])
        y = self.nx(c["y"])
        self.copy(y[:], o[:], eng="act")
        self.store_yT(y, c["YT"], 1, i, c["yTs"])

    def nsa_setup(self, FMS, TMB, TMF, cmp_w1, cmp_w2, cmp_pos, YT):
        S = self.S
        NT = S // 128
        NB = self.NB
        NCP = self.NCP
        NC = (S - 32) // 16 + 1
        NCT = NCP // 128
        c = dict(FMS=FMS, YT=YT)
        c["ksT"] = self.sb("ksT", [128, S], BF16)
        self.dma(c["ksT"][:], V(FMS.h[11 * 128:12 * 128, :], FMS.name))
        c["kwT"] = self.sb("kwT", [128, S], BF16)
        self.dma(c["kwT"][:], V(FMS.h[12 * 128:13 * 128, :], FMS.name))
        c["Vs"] = self.sb("Vs", [128, NT, 64], BF16)
        self.dma(c["Vs"][:], V(TMB.h[:, 576:640].rearrange("(n p) c -> p n c", p=128), TMB.name))
        c["Vw"] = self.sb("Vw", [128, NT, 64], BF16)
        self.dma(c["Vw"][:], V(TMB.h[:, 640:704].rearrange("(n p) c -> p n c", p=128), TMB.name))
        c["ngt"] = self.sb("ngt", [128, NT, 12], F32)
        self.dma(c["ngt"][:], V(TMF.h[:, 8:20].rearrange("(n p) c -> p n c", p=128), TMF.name))
        kcmp = self.sb("kcmp", [128, NCP], BF16)
        vcmp = self.sb("vcmp", [128, NCT, 64], BF16)
        c["kcmp"] = kcmp
        c["vcmp"] = vcmp
        c["Ssb"] = self.sb("nSsb", [128, S], F32)
        c["Sw"] = self.sb("Sw", [128, 640], F32)
        c["kk"] = self.attn_keys("n")
        save = self.sb_cur
        srcT = self.sb("srcT", [128, S], BF16)
        w1 = self.sb("w1", [64, 32, 64], BF16)
        w2 = self.sb("w2", [64, 128], BF16)
        posT = self.sb("posT", [64, 32], F32)
        posb = self.sb("posb", [64, 32], BF16)
        cst = self.sb("cst", [64, 1], F32)
        u = self.sb("u", [64, NCP], F32)
        u2 = self.sb("u2", [64, NCP], F32)
        gl = self.sb("gl", [64, NCP], BF16)
        for i in range(2):
            self.dma(srcT[:], V(FMS.h[(10 + 3 * i) * 128:(11 + 3 * i) * 128, :], FMS.name))
            self.dma(w1[:], V(cmp_w1.h[i].rearrange("(l d) f -> d l f", d=64), cmp_w1.name), eng="pool")
            self.dma(w2[:, 0:64], V(cmp_w2.h[i], cmp_w2.name), eng="pool")
            self.dma(w2[:, 64:128], V(cmp_w2.h[i], cmp_w2.name), eng="pool")
            self.dma_s(posT[:], V(cmp_pos.h[i].rearrange("l d -> d l"), cmp_pos.name))
            self.copy(posb[:], posT[:])
            psc = self.next_ps()
            for l in range(32):
                self.mm(psc[0:64, 0:1], w1[:, l, :], posb[:, l:l + 1], start=(l == 0), stop=(l == 31))
            self.copy(cst[:], psc[0:64, 0:1])
            psh = self.next_ps()
            for l in range(32):
                self.mm(psh[0:64, 0:NC], w1[:, l, :], srcT[0:64, l:l + 16 * (NC - 1) + 1:16], start=(l == 0), stop=(l == 31))
            self.memset(u[:], 0.0)
            self.act(u[:, 0:NC], psh[0:64, 0:NC], AF.Identity, bias=cst[:, 0:1])
            self.tt(u2[:], u[:], u[:], ALU.mult)
            self.tt(u2[:], u2[:], u[:], ALU.mult)
            self.stt(u2[:], u2[:], 0.044715, u[:], ALU.mult, ALU.add)
            self.act(u2[:], u2[:], AF.Tanh, scale=0.7978845608028654)
            self.ts(u2[:], u2[:], 1.0, 0.5, ALU.add, ALU.mult)
            self.tt(gl[:], u2[:], u[:], ALU.mult)
            if i == 0:
                pso = self.next_ps()
                self.mm(pso[:, 0:NCP], w2[:, :], gl[:, :])
                self.copy(kcmp[:], pso[:, 0:NCP])
            else:
                for ct in range(NCT):
                    pso = self.next_ps()
                    self.mm(pso[:, 0:64], gl[:, ct * 128:(ct + 1) * 128], w2[:, 0:64])
                    self.copy(vcmp[:, ct, :], pso[:, 0:64])
        self.sb_cur = save
        self.P.barrier()
        c["q"] = self.rot("nqi", 2, [128, 2, 128], BF16)
        c["Mn"] = self.rot("nMn", 1, [128, S], BF16)
        for nm, shp, dt_ in (("vis", [128, NCP], F32), ("pns", [128, NCP], F32), ("pn", [128, NCP], F32), ("Sc", [128, NCP], F32),
                             ("Pc", [128, NCP], F32), ("pnb", [128, NCP], BF16), ("PTc", [128, NCP], BF16), ("pnT", [128, NCP], F32),
                             ("cst2", [128, 8], F32), ("am", [128, NB], F32), ("imp", [128, NB], F32), ("imp2", [128, NB], F32), ("m8", [128, 16], F32),
                             ("selm", [128, NB], F32), ("cm", [128, 8], F32), ("oc", [128, 256], F32), ("os", [128, 256], F32), ("ow", [128, 256], F32),
                             ("gs", [128, 12], F32), ("o", [128, 256], F32), ("y", [128, 256], BF16), ("yTs", [128, 256], BF16)):
            c[nm] = self.rot("n" + nm, (1 if nm in ("pnT", "Pc", "vis", "imp2") else 2), shp, dt_)
        return c

    def nsa_tile(self, c, i):
        NB = self.NB
        NCP = self.NCP
        NCT = NCP // 128
        FMS = c["FMS"]
        Ssb = c["Ssb"]
        Sw = c["Sw"]
        kcmp = c["kcmp"]
        vcmp = c["vcmp"]

        def h4(t_):
            return V(t_.h[:, 0:256].rearrange("p (h e) -> p h e", h=4), t_.name)

        nk = 128 * (i + 1)
        nkc = (nk + 511) // 512
        nq = self.nx(c["q"])
        self.dma(nq[:], self.fm_rows(FMS, 8, 2, i * 128, (i + 1) * 128))
        vis = self.nx(c["vis"])
        self.memset(vis[:], 0.0)
        self.aselect(vis[:], vis[:], [[-16, NCP]], ALU.is_ge, -1000.0, 128 * i - 31, 1)
        pns = self.nx(c["pns"])
        oc = self.nx(c["oc"])
        osl = self.nx(c["os"])
        ow = self.nx(c["ow"])
        for h in range(4):
            base = 64 * (h % 2)
            cq = h // 2
            ps = self.next_ps()
            self.mm(ps[:, 0:NCP], nq[base:base + 64, cq, :], kcmp[base:base + 64, :])
            Sc = self.nx(c["Sc"])
            self.stt(Sc[:], ps[:, 0:NCP], 0.125, vis[:], ALU.mult, ALU.add)
            st = self.nx(c["cst2"])
            self.red(st[:, 0:1], Sc[:], ALU.max)
            self.ts(st[:, 0:1], st[:, 0:1], -500.0, -1.0, ALU.max, ALU.mult)
            Pc = self.nx(c["Pc"])
            self.act(Pc[:], Sc[:], AF.Exp, bias=st[:, 0:1], accum=st[:, 1:2])
            self.ts(st[:, 2:3], st[:, 1:2], 1e-30, None, ALU.max)
            self.recip(st[:, 3:4], st[:, 2:3])
            pn = pns if h == 0 else self.nx(c["pn"])
            self.ts(pn[:], Pc[:], st[:, 3:4], None, ALU.mult)
            pnb = self.nx(c["pnb"])
            self.copy(pnb[:], pn[:], eng="act")
            if h > 0:
                self.tt(pns[:], pns[:], pn[:], ALU.add, eng="pool")
            pb = self.next_psb()
            for ct in range(NCT):
                self.tr(pb[:, ct * 128:(ct + 1) * 128], pnb[:, ct * 128:(ct + 1) * 128], self.ident[:])
            PTc = self.nx(c["PTc"])
            self.copy(PTc[:], pb[:, 0:NCP], eng="act")
            po = self.next_ps()
            for ct in range(NCT):
                self.mm(po[:, 0:64], PTc[:, ct * 128:(ct + 1) * 128], vcmp[:, ct, :], start=(ct == 0), stop=(ct == NCT - 1))
            self.copy(oc[:, 64 * h:64 * h + 64], po[:, 0:64], eng="act")
        selm = self.nx(c["selm"])
        if NB > 16:
            pf = self.next_ps()
            for ct in range(NCT):
                self.tr(pf[:, ct * 128:(ct + 1) * 128], pns[:, ct * 128:(ct + 1) * 128], self.ident_f[:])
            pnT = self.nx(c["pnT"])
            self.copy(pnT[:], pf[:, 0:NCP], eng="act")
            pi = self.next_ps()
            for ct in range(NCT):
                self.mm(pi[:, 0:NB], pnT[:, ct * 128:(ct + 1) * 128], self.ovl[:, ct, :], start=(ct == 0), stop=(ct == NCT - 1))
            am = self.nx(c["am"])
            self.memset(am[:], 0.0)
            for half in range(2):
                cur = 2 * i + half
                r0 = 64 * half
                v_ = am[r0:r0 + 64, :]
                self.aselect(v_, v_, [[-1, NB]], ALU.is_ge, -1e30, cur, 0)
                self.memset(am[r0:r0 + 64, 0:1], 1e30)
                self.memset(am[r0:r0 + 64, cur:cur + 1], 1e30)
                if cur >= 1:
                    self.memset(am[r0:r0 + 64, cur - 1:cur], 1e30)
            imp = self.nx(c["imp"])
            self.tt(imp[:], pi[:, 0:NB], am[:], ALU.add)
            m8 = self.nx(c["m8"])
            self.vmax(m8[:, 0:8], imp[:])
            imp2 = self.nx(c["imp2"])
            self.match_replace(imp2[:], m8[:, 0:8], imp[:], -3.0e38)
            self.vmax(m8[:, 8:16], imp2[:])
            self.ts(selm[:], imp[:], m8[:, 15:16], 8000.0, ALU.is_ge, ALU.mult)
        else:
            self.memset(selm[:], 8000.0)
        Mn = self.nx(c["Mn"])
        nbk = nk // 64
        self.act(V(Mn.h[:, 0:nk].rearrange("p (b e) -> p b e", e=64), Mn.name),
                 V(selm.h[:, 0:nbk].unsqueeze(2).to_broadcast([128, nbk, 64]), selm.name), AF.Copy)
        self.tt(Mn[:, nk - 128:nk], Mn[:, nk - 128:nk], self.cnegb[:], ALU.add)
        for h in range(4):
            base = 64 * (h % 2)
            cq = h // 2
            cm = self.nx(c["cm"])
            for kc in range(nkc):
                c0 = kc * 512
                cols = min(512, nk - c0)
                ps = self.next_ps()
                self.mm(ps[:, 0:cols], nq[base:base + 64, cq, :], c["ksT"][base:base + 64, c0:c0 + cols], start=True, stop=False)
                self.mm(ps[:, 0:cols], self.ident[:], Mn[:, c0:c0 + cols], start=False, stop=True)
                self.ts(Ssb[:, c0:c0 + cols], ps[:, 0:cols], 0.125, None, ALU.mult, ALU.max, accum=cm[:, kc:kc + 1])
            self.softmax_pv(Ssb[:, 0:nk], nk, c["Vs"], 0, osl[:, 64 * h:64 * h + 64], c["kk"], premax=cm[:, 0:nkc])
        k0 = max(0, i * 128 - 512)
        nkw = nk - k0
        boff = 640 - nkw
        for h in range(4):
            base = 64 * (h % 2)
            cq = h // 2
            cm = self.nx(c["cm"])
            nwc = 0
            for c0 in range(0, nkw, 512):
                cols = min(512, nkw - c0)
                ps = self.next_ps()
                self.mm(ps[:, 0:cols], nq[base:base + 64, cq, :], c["kwT"][base:base + 64, k0 + c0:k0 + c0 + cols], start=True, stop=False)
                self.mm(ps[:, 0:cols], self.ident[:], self.bandb[:, boff + c0:boff + c0 + cols], start=False, stop=True)
                self.ts(Sw[:, c0:c0 + cols], ps[:, 0:cols], 0.125, None, ALU.mult, ALU.max, accum=cm[:, nwc:nwc + 1])
                nwc += 1
            self.softmax_pv(Sw[:, 0:nkw], nkw, c["Vw"], k0 // 128, ow[:, 64 * h:64 * h + 64], c["kk"], premax=cm[:, 0:nwc])
        gs = self.nx(c["gs"])
        self.act(gs[:], c["ngt"][:, i, :], AF.Sigmoid)
        o = self.nx(c["o"])

        def gbc(j):
            return V(gs.h[:, j:12:3].unsqueeze(2).to_broadcast([128, 4, 64]), gs.name)
        self.tt(h4(o), h4(oc), gbc(0), ALU.mult)
        self.tt(h4(osl), h4(osl), gbc(1), ALU.mult)
        self.tt(o[:], o[:], osl[:], ALU.add)
        self.tt(h4(ow), h4(ow), gbc(2), ALU.mult)
        self.tt(o[:], o[:], ow[:], ALU.add)
        y = self.nx(c["y"])
        self.copy(y[:], o[:], eng="act")
        self.store_yT(y, c["YT"], 2, i, c["yTs"])

    def phase_dsa_nsa(self, FMS, TMB, TMF, cmp_w1, cmp_w2, cmp_pos, YT):
        S = self.S
        NT = S // 128
        self.phase_begin()
        self.cnegb = self.sb("cnegb", [128, 128], BF16)
        self.ts(self.cnegb[:], self.cneg2k[:], 8.0, None, ALU.mult)
        self.bandb = self.sb("bandb", [128, 640], BF16)
        self.ts(self.bandb[:], self.band[:], 8.0, None, ALU.mult)
        cd = self.dsa_setup(FMS, TMB, TMF, YT)
        cn = self.nsa_setup(FMS, TMB, TMF, cmp_w1, cmp_w2, cmp_pos, YT)
        P = self.P
        for i in range(NT):
            self.ps_set = (0, 3)
            self.psb_set = (0, 1)
            P.capture = []
            self.dsa_tile(cd, i)
            A = P.capture
            self.ps_set = (3, 2)
            self.psb_set = (1, 1)
            P.capture = []
            self.nsa_tile(cn, i)
            B = P.capture
            P.capture = None
            self.ps_set = (0, 5)
            self.psb_set = (0, 2)
            ia = ib = 0
            na, nb = len(A), len(B)
            while ia < na or ib < nb:
                if ib >= nb or (ia < na and ia * nb <= ib * na):
                    P.add(*A[ia][0], **A[ia][1])
                    ia += 1
                else:
                    P.add(*B[ib][0], **B[ib][1])
                    ib += 1

    def phase_merge(self, x_in, xT_in, w_in_l, w_branch, w_out, ln_g, ln_b, YT, x_out, xT_out):
        S = self.S
        self.phase_begin()
        wg = self.load_w("wg", lambda k: V(w_in_l.h[k * 128:(k + 1) * 128, 3128:7224], w_in_l.name), NKC, 4096)
        wb = self.sb("wb", [128, 4, 2, 1024], BF16)
        for n in range(4):
            for kk_ in range(2):
                self.dma(wb[:, n, kk_, :], V(w_branch.h[n, kk_ * 128:(kk_ + 1) * 128, :], w_branch.name), eng="pool")
        wo = self.load_w("wo", lambda k: V(w_out.h[k * 128:(k + 1) * 128, :], w_out.name), NKC, D)
        g_bc, b_bc, scr = self.ln_setup(ln_g, ln_b)
        xt = self.sb("xT", [128, NKC, 512], BF16)
        yt = self.sb("yT", [128, 8, 512], BF16)
        mTs = [self.sb("mT%d" % i_, [128, 8, 512], BF16) for i_ in range(2)]
        acc_k = self.rot("acc", 3, [128, 512], F32)
        sg_k = self.rot("sg", 2, [128, 512], F32)
        tmp_k = self.rot("tmp", 2, [128, 512], F32)
        xr = [self.sb("xr%d" % i, [128, D], F32) for i in range(2)]
        rqs = [self.sb("r%d" % i, [128, D], F32) for i in range(2)]
        ntile = S // 512

        def load_xt(t_):
            ss_ = slice(t_ * 512, (t_ + 1) * 512)
            self.dma(xt[:], V(xT_in.h.rearrange("(k p) s -> p k s", p=128)[:, :, ss_], xT_in.name))
            self.dma(yt[:], V(YT.h[:, ss_].rearrange("(c p) s -> p c s", p=128), YT.name))

        def load_xq(idx):
            self.dma(xr[idx % 2][:], x_in[idx * 128:(idx + 1) * 128, :])
        load_xt(0)
        load_xq(0)
        for t in range(ntile):
            ss = slice(t * 512, (t + 1) * 512)
            mT = mTs[t % 2]
            for dc in range(8):
                acc = self.nx(acc_k)
                for n in range(4):
                    pg = self.next_ps()
                    for k in range(NKC):
                        self.mm(pg[:], wg[:, k, n * 1024 + dc * 128:n * 1024 + (dc + 1) * 128], xt[:, k, :],
                                start=(k == 0), stop=(k == NKC - 1))
                    pp = self.next_ps()
                    for k2 in range(2):
                        self.mm(pp[:], wb[:, n, k2, dc * 128:(dc + 1) * 128], yt[:, 2 * n + k2, :], start=(k2 == 0), stop=(k2 == 1))
                    sg = self.nx(sg_k)
                    self.act(sg[:], pg[:], AF.Sigmoid)
                    if n == 0:
                        self.tt(acc[:], sg[:], pp[:], ALU.mult)
                    else:
                        tmp = self.nx(tmp_k)
                        self.tt(tmp[:], sg[:], pp[:], ALU.mult)
                        self.tt(acc[:], acc[:], tmp[:], ALU.add, eng="pool")
                self.copy(mT[:, dc, :], acc[:], eng="act")
            if t + 1 < ntile:
                load_xt(t + 1)
            for q in range(4):
                t0 = t * 512 + q * 128
                xq = xr[q % 2]
                rq = rqs[q % 2]
                if t * 4 + q + 1 < ntile * 4:
                    load_xq(t * 4 + q + 1)
                for half in range(2):
                    hs = slice(half * 512, (half + 1) * 512)
                    pd = self.next_ps()
                    for dc in range(8):
                        self.mm(pd[:], mT[:, dc, q * 128:(q + 1) * 128], wo[:, dc, hs], start=(dc == 0), stop=(dc == 7))
                    self.act(xq[:, hs], xq[:, hs], AF.Copy, scale=ALPHA)
                    self.stt(rq[:, hs], pd[:], 1.0, xq[:, hs], ALU.mult, ALU.add)
                self.finish_tile(rq, g_bc, b_bc, x_out, xT_out, t0, scr)

    def phase_xattn(self, x_in, xT_in, mem, wq_d, wkv_d, wo_d, ln_g, ln_b, x_out, xT_out):
        S = self.S
        self.phase_begin()
        wkv = self.load_w("wkv", lambda k: V(wkv_d.h[k * 128:(k + 1) * 128, :], wkv_d.name), NKC, 2 * D)
        wq = self.load_w("wq", lambda k: V(wq_d.h[k * 128:(k + 1) * 128, :], wq_d.name), NKC, D)
        wo = self.load_w("wo", lambda k: V(wo_d.h[k * 128:(k + 1) * 128, :], wo_d.name), NKC, D)
        g_bc, b_bc, scr = self.ln_setup(ln_g, ln_b)
        memT = self.sb("memT", [128, 8, 256], BF16)
        mr = self.sb("mr", [128, D], F32)
        mb = self.sb("mb", [128, D], BF16)
        for mt in range(2):
            self.dma(mr[:], mem[mt * 128:(mt + 1) * 128, :])
            self.copy(mb[:], mr[:], eng="act")
            pb = self.next_psb()
            for k in range(8):
                self.tr(pb[:, k * 128:(k + 1) * 128], mb[:, k * 128:(k + 1) * 128], self.ident[:])
            self.copy(memT[:, :, mt * 128:(mt + 1) * 128], V(pb.h[:, :].rearrange("p (k t) -> p k t", k=8), pb.name))
        KT = self.sb("KT", [128, 8, 256], BF16)
        for c in range(8):
            ps = self.next_ps()
            for k in range(NKC):
                self.mm(ps[:, 0:256], wkv[:, k, c * 128:(c + 1) * 128], memT[:, k, :], start=(k == 0), stop=(k == NKC - 1))
            self.copy(KT[:, c, :], ps[:, 0:256], eng=("act" if c % 2 else "dve"))
        Vm = self.sb("Vm", [128, 2, D], BF16)
        for mt in range(2):
            for half in range(2):
                ps = self.next_ps()
                for k in range(NKC):
                    self.mm(ps[:], memT[:, k, mt * 128:(mt + 1) * 128], wkv[:, k, D + half * 512:D + (half + 1) * 512],
                            start=(k == 0), stop=(k == NKC - 1))
                self.copy(Vm[:, mt, half * 512:(half + 1) * 512], ps[:], eng=("act" if half else "dve"))
        xt = self.sb("xT", [128, NKC, 512], BF16)
        qT = self.sb("qT", [128, 8, 512], BF16)
        Pf_k = self.rot("Pf", 2, [128, 4, 256], F32)
        Pb_k = self.rot("Pb", 2, [128, 4, 256], BF16)
        PT_k = self.rot("PTx", 2, [128, 8, 128], BF16)
        oT_k = self.rot("oT", 2, [128, 8, 128], BF16)
        st_k = self.rot("xst", 2, [128, 16], F32)
        xr = [self.sb("xr%d" % i, [128, D], F32) for i in range(2)]
        rqs = [self.sb("r%d" % i, [128, D], F32) for i in range(2)]
        SC = 1.0 / 16
        ntile = S // 512

        def load_xt(t_):
            ss_ = slice(t_ * 512, (t_ + 1) * 512)
            self.dma(xt[:], V(xT_in.h.rearrange("(k p) s -> p k s", p=128)[:, :, ss_], xT_in.name))

        def load_xq(idx):
            self.dma(xr[idx % 2][:], x_in[idx * 128:(idx + 1) * 128, :])
        load_xt(0)
        load_xq(0)
        for t in range(ntile):
            ss = slice(t * 512, (t + 1) * 512)
            for c in range(8):
                ps = self.next_ps()
                for k in range(NKC):
                    self.mm(ps[:], wq[:, k, c * 128:(c + 1) * 128], xt[:, k, :], start=(k == 0), stop=(k == NKC - 1))
                self.copy(qT[:, c, :], ps[:], eng=("act" if c % 2 else "dve"))
            if t + 1 < ntile:
                load_xt(t + 1)
            for q in range(4):
                t0 = t * 512 + q * 128
                tq = slice(q * 128, (q + 1) * 128)
                if t * 4 + q + 1 < ntile * 4:
                    load_xq(t * 4 + q + 1)
                pss = [self.next_ps(), self.next_ps()]
                st = self.nx(st_k)
                Pf = self.nx(Pf_k)
                for h in range(4):
                    pv = pss[h // 2][:, (h % 2) * 256:(h % 2) * 256 + 256]
                    for cc in range(2):
                        self.mm(pv, qT[:, 2 * h + cc, tq], KT[:, 2 * h + cc, :], start=(cc == 0), stop=(cc == 1))
                    self.red(st[:, h:h + 1], pv, ALU.max)
                    self.ts(st[:, 4 + h:5 + h], st[:, h:h + 1], -SC, None, ALU.mult)
                    self.act(Pf[:, h, :], pv, AF.Exp, bias=st[:, 4 + h:5 + h], scale=SC, accum=st[:, 8 + h:9 + h])
                self.recip(st[:, 12:16], st[:, 8:12])
                Pb = self.nx(Pb_k)
                self.tt(Pb[:], Pf[:], V(st.h[:, 12:16].unsqueeze(2).to_broadcast([128, 4, 256]), st.name), ALU.mult)
                pb = self.next_psb()
                for h in range(4):
                    for mc in range(2):
                        j = 2 * h + mc
                        self.tr(pb[:, j * 128:(j + 1) * 128], Pb[:, h, mc * 128:(mc + 1) * 128], self.ident[:])
                PT = self.nx(PT_k)
                self.copy(PT[:], V(pb.h[:, :].rearrange("p (j t) -> p j t", j=8), pb.name))
                oT = self.nx(oT_k)
                pso = [self.next_ps(), self.next_ps()]
                for h in range(4):
                    for dc in range(2):
                        j = 2 * h + dc
                        pv = pso[j // 4][:, (j % 4) * 128:(j % 4) * 128 + 128]
                        for mc in range(2):
                            self.mm(pv, Vm[:, mc, h * 256 + dc * 128:h * 256 + (dc + 1) * 128], PT[:, 2 * h + mc, :],
                                    start=(mc == 0), stop=(mc == 1))
                for j4 in range(2):
                    self.copy(oT[:, 4 * j4:4 * j4 + 4, :], V(pso[j4].h[:, :].rearrange("p (j t) -> p j t", j=4), pso[j4].name),
                              eng=("act" if j4 else "dve"))
                xq = xr[q % 2]
                rq = rqs[q % 2]
                for half in range(2):
                    hs = slice(half * 512, (half + 1) * 512)
                    pd = self.next_ps()
                    for c in range(8):
                        self.mm(pd[:], oT[:, c, :], wo[:, c, hs], start=(c == 0), stop=(c == 7))
                    self.act(xq[:, hs], xq[:, hs], AF.Copy, scale=ALPHA)
                    self.stt(rq[:, hs], pd[:], 1.0, xq[:, hs], ALU.mult, ALU.add)
                self.finish_tile(rq, g_bc, b_bc, x_out, xT_out, t0, scr)


OFF = dict(r_q=0, r_k=128, r_v=256, r_g=512, d_q=768, d_k=1024, d_v=1088, i_q=1152, i_k=1408, i_w=1440,
           n_q=1448, n_kc=1704, n_vc=1768, n_ks=1832, n_vs=1896, n_kw=1960, n_vw=2024, n_g=2088,
           s_z=2100, s_xbc=2356, s_dt=3124, br_g=3128)


def _partner(i, headdim, rot):
    half = rot // 2
    j = i % headdim
    b = i - j
    if j < half:
        return b + j + half
    if j < rot:
        return b + j - half
    return i


def build_colidx():
    cols = []

    def roped(name, width, headdim, rot, lo=0, rep=1):
        loc = []
        for r in range(rep):
            loc += list(range(lo, lo + width))
        assert len(loc) == 128
        a = [OFF[name] + i for i in loc]
        b = [OFF[name] + _partner(i, headdim, rot) for i in loc]
        cols.extend(a)
        cols.extend(b)

    roped("r_q", 128, 32, 32)
    roped("r_k", 128, 32, 32)
    roped("d_q", 128, 64, 16, 0)
    roped("d_q", 128, 64, 16, 128)
    roped("d_k", 64, 64, 16, 0, 2)
    roped("i_q", 128, 32, 8, 0)
    roped("i_q", 128, 32, 8, 128)
    roped("i_k", 32, 32, 8, 0, 4)
    roped("n_q", 128, 64, 16, 0)
    roped("n_q", 128, 64, 16, 128)
    roped("n_kc", 64, 64, 16, 0, 2)
    roped("n_ks", 64, 64, 16, 0, 2)
    roped("n_kw", 64, 64, 16, 0, 2)
    cols.extend([OFF["n_vc"] + i for i in range(64)] * 2)
    cols.extend([OFF["s_xbc"] + i for i in range(768)])
    for name, w in (("r_v", 256), ("r_g", 256), ("d_v", 64), ("n_vs", 64), ("n_vw", 64), ("s_z", 256),
                    ("i_w", 8), ("n_g", 12), ("s_dt", 4)):
        cols.extend([OFF[name] + i for i in range(w)])
    return np.asarray(cols, dtype=np.int64)


ROPED_TABLES = [0, 1, 2, 2, 2, 3, 3, 3, 2, 2, 2, 2, 2]
TM0 = (2 * len(ROPED_TABLES) + 7) * 128
NCOL2 = TM0 + 984
NBIS = 16
RET_LNG = [math.log1p(-2.0 ** (-5 - h)) for h in range(4)]


def host_consts(S):
    meta = np.zeros((128, 32), np.float32)

    def fill(t, headdim, rot, theta, scale):
        half = rot // 2
        inv = np.power(np.float32(theta), (-2.0 * np.arange(half, dtype=np.float32) / np.float32(rot)).astype(np.float32)).astype(np.float32)
        for p in range(128):
            i = p % headdim
            if i < rot:
                meta[p, t] = inv[i % half]
                meta[p, 4 + t] = scale
                meta[p, 8 + t] = -scale if i < half else scale
            else:
                meta[p, t] = 0.0
                meta[p, 4 + t] = 1.0
                meta[p, 8 + t] = 0.0

    fill(0, 32, 32, 10000.0, 1.0)
    fill(1, 32, 32, 10000.0, 32.0 ** -0.5)
    fill(2, 64, 16, 500000.0, 1.0)
    fill(3, 32, 8, 500000.0, 1.0)
    for p in range(128):
        meta[p, 12] = RET_LNG[p // 32]
        meta[p, 13 + p // 32] = 1.0
    bdm = np.zeros((128, 256), np.float32)
    for p in range(128):
        bdm[p, 64 * (p // 32):64 * (p // 32) + 64] = 1.0
    NC = (S - 32) // 16 + 1
    NCP = (NC + 127) // 128 * 128
    NB = S // 64
    ovl = np.zeros((NCP, NB), np.float32)
    for c in range(NC):
        for j in range(NB):
            ovl[c, j] = max(min(16 * c + 32, 64 * j + 64) - max(16 * c, 64 * j), 0) / 32.0
    return meta, bdm, ovl, NB, NCP


STAGES = ["ffn1", "inproj", "ret", "ssd", "dsa", "nsa", "merge", "xattn", "ffn2"]


def build(S, depth=DEPTH, stop_after=None):
    kb = KB(S, depth, stop_after)
    meta_np, bdm_np, ovl_np, NB, NCP = host_consts(S)
    kb.n_keep = min(256, S // 4)
    EI = "ExternalInput"
    x = kb.dram("x", [S, D], F32, kind=EI)
    mem = kb.dram("mem", [N_MEM, D], F32, kind=EI)
    ln_g = kb.dram("ln_g", [DEPTH, 4, D], F32, kind=EI)
    ln_b = kb.dram("ln_b", [DEPTH, 4, D], F32, kind=EI)
    f1gu = kb.dram("ffn1_w_gu", [DEPTH, D, 2 * DFF], F32, kind=EI)
    f1dn = kb.dram("ffn1_w_down", [DEPTH, DFF, D], F32, kind=EI)
    w_in = kb.dram("w_in", [DEPTH, D, 7224], F32, kind=EI)
    w2 = kb.dram("w2", [DEPTH, D, NCOL2], F32, kind=EI)
    cmp_w1 = kb.dram("cmp_w1", [DEPTH, 2, 2048, 64], F32, kind=EI)
    cmp_w2 = kb.dram("cmp_w2", [DEPTH, 2, 64, 64], F32, kind=EI)
    cmp_pos = kb.dram("cmp_pos", [DEPTH, 2, 32, 64], F32, kind=EI)
    conv_w = kb.dram("conv_w", [DEPTH, 4, 768], F32, kind=EI)
    conv_b = kb.dram("conv_b", [DEPTH, 768], F32, kind=EI)
    dt_bias = kb.dram("dt_bias", [DEPTH, 4], F32, kind=EI)
    a_log = kb.dram("a_log", [DEPTH, 4], F32, kind=EI)
    d_skip = kb.dram("d_skip", [DEPTH, 4], F32, kind=EI)
    norm_g = kb.dram("ssm_norm_g", [DEPTH, 256], F32, kind=EI)
    w_branch = kb.dram("w_branch", [DEPTH, 4, 256, D], F32, kind=EI)
    w_out = kb.dram("w_out", [DEPTH, D, D], F32, kind=EI)
    xwq = kb.dram("xattn_wq", [DEPTH, D, D], F32, kind=EI)
    xwkv = kb.dram("xattn_wkv", [DEPTH, D, 2 * D], F32, kind=EI)
    xwo = kb.dram("xattn_wo", [DEPTH, D, D], F32, kind=EI)
    f2gu = kb.dram("ffn2_w_gu", [DEPTH, D, 2 * DFF], F32, kind=EI)
    f2dn = kb.dram("ffn2_w_down", [DEPTH, DFF, D], F32, kind=EI)
    meta = kb.dram("meta", [128, 32], F32, kind=EI)
    bdm = kb.dram("bdm", [128, 256], F32, kind=EI)
    ovl = kb.dram("ovl", [NCP, NB], F32, kind=EI)
    out = kb.dram("out", [S, D], F32)
    xTa = kb.dram("xTa", [D, S], BF16)
    xTb = kb.dram("xTb", [D, S], BF16)
    xa = kb.dram("xa", [S, D], F32)
    xb2 = kb.dram("xb2", [S, D], F32)
    FMS = kb.dram("FMS", [20 * 128, S], BF16)
    TMB = kb.dram("TMB", [S, 960], BF16)
    TMF = kb.dram("TMF", [S, 24], F32)
    YT = kb.dram("YT", [1024, S], BF16)
    ROPE = kb.dram("ROPE", [4, 2, 128, S], F32)
    kb.setup()
    kb.setup_consts(meta, bdm, ovl, NB, NCP)
    kb.phase_rope(ROPE)
    kb.phase_transpose_in(x, xTa)

    def L(t, *idx):
        return T(t.h[idx], t.name, True)

    done = False
    xin = x
    for l in range(depth):
        last = (l == depth - 1)

        def stop(name):
            return stop_after == (l, name)
        kb.phase_ffn(xin, xTa, L(f1gu, l), L(f1dn, l), L(ln_g, l, 0), L(ln_b, l, 0), xa, xTb)
        if stop("ffn1"):
            break
        kb.phase_inproj(xTb, L(w2, l), ROPE, FMS, TMB, TMF)
        if stop("inproj"):
            break
        kb.phase_ret(FMS, TMB, YT)
        if stop("ret"):
            break
        kb.phase_ssd(FMS, TMB, TMF, L(conv_w, l), L(conv_b, l), L(dt_bias, l), L(a_log, l), L(d_skip, l), L(norm_g, l), YT)
        if stop("ssd"):
            break
        kb.phase_dsa_nsa(FMS, TMB, TMF, L(cmp_w1, l), L(cmp_w2, l), L(cmp_pos, l), YT)
        if stop("nsa") or stop("dsa"):
            break
        kb.phase_merge(xa, xTb, L(w_in, l), L(w_branch, l), L(w_out, l), L(ln_g, l, 1), L(ln_b, l, 1), YT, xb2, xTa)
        if stop("merge"):
            break
        kb.phase_xattn(xb2, xTa, mem, L(xwq, l), L(xwkv, l), L(xwo, l), L(ln_g, l, 2), L(ln_b, l, 2), xa, xTb)
        if stop("xattn"):
            break
        kb.phase_ffn(xa, xTb, L(f2gu, l), L(f2dn, l), L(ln_g, l, 3), L(ln_b, l, 3), out if last else xb2, None if last else xTa)
        if stop("ffn2"):
            break
        xin = xb2
    kb.flush_pending()
    st = kb.P.emit()
    kb.stats = st
    return kb


def make_in_maps(inputs, S, ncores):
    meta_np, bdm_np, ovl_np, NB, NCP = host_consts(S)
    colidx = build_colidx()
    w_in = np.asarray(inputs["w_in"], dtype=np.float32)
    w2 = np.ascontiguousarray(w_in[:, :, colidx])
    shared = {k: np.ascontiguousarray(np.asarray(v, dtype=np.float32)) for k, v in inputs.items() if k not in ("x", "mem")}
    shared["w2"] = w2
    shared["meta"] = meta_np
    shared["bdm"] = bdm_np
    shared["ovl"] = ovl_np
    maps = []
    for b in range(ncores):
        m = dict(shared)
        m["x"] = np.ascontiguousarray(np.asarray(inputs["x"][b, :S], dtype=np.float32))
        m["mem"] = np.ascontiguousarray(np.asarray(inputs["mem"][b], dtype=np.float32))
        maps.append(m)
    return maps


def kernel(**inputs):
    S = inputs["x"].shape[1]
    B = inputs["x"].shape[0]
    kb = build(S)
    maps = make_in_maps(inputs, S, B)
    res = run_bass_kernel_spmd(kb.nc, maps, core_ids=list(range(B)))
    out = np.stack([np.asarray(r["out"], dtype=np.float32) for r in res.results], axis=0)
    return out
```
